# Optimizing a Trainium2 kernel written in Bass

```python
import jax, jax.numpy as jnp
from jax import lax
import numpy as np

D_MODEL = 1024
BATCH = 4
SEQ = 4096
DEPTH = 1

NORM_EPS = 1e-6
RW_HEADS = 8
RW_HEAD = 64
RW_WIDTH = RW_HEADS * RW_HEAD
W_LORA = 64
A_LORA = 64
G_LORA = 128
RW_COLS = 3 * RW_WIDTH + W_LORA + A_LORA + G_LORA
GN_EPS = 64e-5
NSA_HEADS = 8
NSA_KV = 2
NSA_HPG = NSA_HEADS // NSA_KV
NSA_DK = 64
NSA_WIDTH = NSA_HEADS * NSA_DK
NSA_KVW = NSA_KV * NSA_DK
CMP_LEN = 32
CMP_STRIDE = 16
CMP_HIDDEN = 256
SEL_BLOCK = 64
SEL_TOPN = 16
FORCE_BONUS = 1000.0
WINDOW = 512
Q_BLOCK = 128
PEER_HEADS = 8
PEER_KEYS = 128
PEER_EXPERTS = PEER_KEYS * PEER_KEYS
PEER_DKEY = 256
PEER_TOPK = 16
PEER_CHUNK = 128
IN_WIDTHS = (RW_COLS, NSA_WIDTH, NSA_KVW, NSA_KVW, NSA_KVW, NSA_KVW, NSA_KVW, NSA_KVW, NSA_HEADS * 3, D_MODEL, D_MODEL)
IN_TOTAL = sum(IN_WIDTHS)

kernel_name = 'hybrid_rwkv7_nsa_peer_block'


def _rmsnorm(x, g):
    xf = x.astype(jnp.float32)
    y = xf * lax.rsqrt(jnp.mean(xf * xf, axis=-1, keepdims=True) + NORM_EPS)
    return (y * g.astype(jnp.float32)).astype(x.dtype)


def _masked_softmax(s, mask):
    s = jnp.where(mask, s.astype(jnp.float32), -1e30)
    m = jnp.max(s, axis=-1, keepdims=True)
    e = jnp.where(mask, jnp.exp(s - m), 0.0)
    return e / jnp.maximum(jnp.sum(e, axis=-1, keepdims=True), 1e-30)


def _rwkv7(p, mu, w0, w_up, a0, a_up, g_up, k_k, k_a, r_k, ln_w, ln_b):
    B, T, _ = p.shape
    prev = jnp.pad(p[:, :-1], ((0, 0), (1, 0), (0, 0)))
    p = p + (prev - p) * mu
    r, k, v, wd, ad, gd = jnp.split(p, [RW_WIDTH, 2 * RW_WIDTH, 3 * RW_WIDTH, 3 * RW_WIDTH + W_LORA, 3 * RW_WIDTH + W_LORA + A_LORA], axis=-1)
    w_log = -jax.nn.softplus(-(w0 + jnp.tanh(wd) @ w_up)) - 0.5
    decay = jnp.exp(-jnp.exp(w_log.astype(jnp.float32)))
    a = jax.nn.sigmoid(a0 + ad @ a_up)
    g = jax.nn.sigmoid(gd) @ g_up
    kk = k * k_k
    k = k * (1.0 + (a - 1.0) * k_a)
    hs = lambda t: t.reshape(B, T, RW_HEADS, RW_HEAD).astype(jnp.float32)
    r, k, v, kk, a, decay = hs(r), hs(k), hs(v), hs(kk), hs(a), hs(decay)
    kk = kk / jnp.maximum(jnp.linalg.norm(kk, axis=-1, keepdims=True), 1e-12)

    def step(S, inp):
        r_t, w_t, k_t, v_t, kk_t, a_t = inp
        sa = jnp.einsum('bhvk,bhk->bhv', S, -kk_t)
        S = S * w_t[:, :, None, :] + sa[..., None] * (kk_t * a_t)[:, :, None, :] + v_t[..., None] * k_t[:, :, None, :]
        return S, jnp.einsum('bhvk,bhk->bhv', S, r_t)

    xs = tuple(jnp.moveaxis(t, 1, 0) for t in (r, decay, k, v, kk, a))
    S0 = jnp.zeros((B, RW_HEADS, RW_HEAD, RW_HEAD), jnp.float32)
    _, y = lax.scan(step, S0, xs)
    y = jnp.moveaxis(y, 0, 1)
    mean = jnp.mean(y, axis=-1, keepdims=True)
    var = jnp.var(y, axis=-1, keepdims=True)
    y = ((y - mean) * lax.rsqrt(var + GN_EPS)).reshape(B, T, RW_WIDTH) * ln_w + ln_b
    bonus = jnp.sum(r * k * r_k, axis=-1, keepdims=True) * v
    y = (y + bonus.reshape(B, T, RW_WIDTH)) * g
    return y.astype(p.dtype)


def _compress(t, pos, w1, w2):
    B, T, G, DK = t.shape
    n_c = (T - CMP_LEN) // CMP_STRIDE + 1
    idx = jnp.arange(n_c)[:, None] * CMP_STRIDE + jnp.arange(CMP_LEN)[None, :]
    blk = t[:, idx] + pos[:, None, :]
    blk = jnp.moveaxis(blk, 3, 2).reshape(B, n_c, G, CMP_LEN * DK)
    return jax.nn.gelu(blk @ w1) @ w2


def _nsa(q, k_c, v_c, k_s, v_s, k_w, v_w, gate_logits, q_g, kc_g, ks_g, kw_g, pos_k, pos_v, ck1, ck2, cv1, cv2):
    B, T, _ = q.shape
    G, HPG, DK = NSA_KV, NSA_HPG, NSA_DK
    q = (_rmsnorm(q.reshape(B, T, NSA_HEADS, DK), q_g) * (DK ** -0.5)).reshape(B, T, G, HPG, DK)
    gates = jax.nn.sigmoid(gate_logits.reshape(B, T, G, HPG, 3))
    kv = lambda t: t.reshape(B, T, G, DK)
    kc = _rmsnorm(_compress(kv(k_c), pos_k, ck1, ck2), kc_g)
    vc = _compress(kv(v_c), pos_v, cv1, cv2)
    n_c = kc.shape[1]
    n_b = T // SEL_BLOCK
    n_top = min(SEL_TOPN, n_b)
    ks = _rmsnorm(kv(k_s), ks_g).reshape(B, n_b, SEL_BLOCK, G, DK).transpose(0, 3, 1, 2, 4)
    vs = kv(v_s).reshape(B, n_b, SEL_BLOCK, G, DK).transpose(0, 3, 1, 2, 4)
    pad = ((0, 0), (WINDOW, 0), (0, 0), (0, 0))
    kw = jnp.pad(_rmsnorm(kv(k_w), kw_g), pad)
    vw = jnp.pad(kv(v_w), pad)
    cmp_start = jnp.arange(n_c) * CMP_STRIDE
    cmp_end = cmp_start + CMP_LEN - 1
    sel_start = jnp.arange(n_b) * SEL_BLOCK
    overlap = jnp.clip(jnp.minimum(cmp_start[:, None] + CMP_LEN, sel_start[None, :] + SEL_BLOCK) - jnp.maximum(cmp_start[:, None], sel_start[None, :]), 0, None)
    overlap = overlap.astype(jnp.float32) / CMP_LEN
    blk_ids = jnp.arange(n_b)
    b_idx = jnp.arange(B)[:, None, None, None]
    g_idx = jnp.arange(G)[None, :, None, None]
    m_sel = n_top * SEL_BLOCK

    def q_block(c):
        t0 = c * Q_BLOCK
        qc = lax.dynamic_slice_in_dim(q, t0, Q_BLOCK, axis=1)
        gc = lax.dynamic_slice_in_dim(gates, t0, Q_BLOCK, axis=1)
        tpos = t0 + jnp.arange(Q_BLOCK)
        s = jnp.einsum('bqghd,bngd->bgqhn', qc, kc)
        p_cmp = _masked_softmax(s, (cmp_end[None, :] <= tpos[:, None])[None, None, :, None, :])
        o_cmp = jnp.einsum('bgqhn,bngd->bqghd', p_cmp.astype(vc.dtype), vc)
        imp = jnp.einsum('bgqhn,nj->bgqj', p_cmp, overlap)
        cur = (tpos // SEL_BLOCK)[:, None]
        valid = blk_ids[None, :] <= cur
        forced = (blk_ids[None, :] == 0) | (blk_ids[None, :] == cur) | (blk_ids[None, :] == cur - 1)
        score = jnp.where(valid, imp + FORCE_BONUS * forced.astype(jnp.float32), -1.0)
        top_s, top_i = lax.top_k(score, n_top)
        k_sel = ks[b_idx, g_idx, top_i].reshape(B, G, Q_BLOCK, m_sel, DK)
        v_sel = vs[b_idx, g_idx, top_i].reshape(B, G, Q_BLOCK, m_sel, DK)
        tok = (top_i[..., None] * SEL_BLOCK + jnp.arange(SEL_BLOCK)).reshape(B, G, Q_BLOCK, m_sel)
        smask = jnp.repeat(top_s > -0.5, SEL_BLOCK, axis=-1) & (tok <= tpos[None, None, :, None])
        s = jnp.einsum('bqghd,bgqmd->bgqhm', qc, k_sel)
        p_sel = _masked_softmax(s, smask[:, :, :, None, :])
        o_sel = jnp.einsum('bgqhm,bgqmd->bqghd', p_sel.astype(v_sel.dtype), v_sel)
        kwc = lax.dynamic_slice_in_dim(kw, t0, WINDOW + Q_BLOCK, axis=1)
        vwc = lax.dynamic_slice_in_dim(vw, t0, WINDOW + Q_BLOCK, axis=1)
        kpos = t0 - WINDOW + jnp.arange(WINDOW + Q_BLOCK)
        wmask = (kpos[None, :] <= tpos[:, None]) & (kpos[None, :] > tpos[:, None] - WINDOW) & (kpos[None, :] >= 0)
        s = jnp.einsum('bqghd,bkgd->bgqhk', qc, kwc)
        p_win = _masked_softmax(s, wmask[None, None, :, None, :])
        o_win = jnp.einsum('bgqhk,bkgd->bqghd', p_win.astype(vwc.dtype), vwc)
        o = gc[..., 0:1] * o_cmp + gc[..., 1:2] * o_sel + gc[..., 2:3] * o_win
        return o.reshape(B, Q_BLOCK, NSA_WIDTH)

    out = lax.map(q_block, jnp.arange(T // Q_BLOCK))
    return jnp.moveaxis(out, 0, 1).reshape(B, T, NSA_WIDTH)


def _peer(h, wq, sub_k1, sub_k2, u_tab, v_tab):
    B, T, D = h.shape
    n_tok = B * T
    hf = h.reshape(n_tok, D)
    qry = (hf @ wq).reshape(n_tok, PEER_HEADS, 2, PEER_DKEY // 2)
    s1, i1 = lax.top_k(jnp.einsum('nhd,kd->nhk', qry[:, :, 0], sub_k1), PEER_TOPK)
    s2, i2 = lax.top_k(jnp.einsum('nhd,kd->nhk', qry[:, :, 1], sub_k2), PEER_TOPK)
    n_cand = PEER_TOPK * PEER_TOPK
    cand_s = (s1[..., :, None] + s2[..., None, :]).reshape(n_tok, PEER_HEADS, n_cand)
    cand_e = (i1[..., :, None] * PEER_KEYS + i2[..., None, :]).reshape(n_tok, PEER_HEADS, n_cand)
    best_s, best_pos = lax.top_k(cand_s, PEER_TOPK)
    expert = jnp.take_along_axis(cand_e, best_pos, axis=-1)
    gate = jax.nn.softmax(best_s.astype(jnp.float32), axis=-1).astype(h.dtype)
    n_ch = n_tok // PEER_CHUNK
    k_all = PEER_HEADS * PEER_TOPK

    def chunk(args):
        x_c, e_c, g_c = args
        act = jax.nn.gelu(jnp.einsum('cd,ced->ce', x_c, u_tab[e_c]), approximate=False)
        return jnp.einsum('ce,ced->cd', g_c * act, v_tab[e_c])

    out = lax.map(chunk, (hf.reshape(n_ch, PEER_CHUNK, D), expert.reshape(n_ch, PEER_CHUNK, k_all), gate.reshape(n_ch, PEER_CHUNK, k_all)))
    return out.reshape(B, T, D)


def _layer(x, norm1_g, w_in, rw_mu, rw_w0, rw_w_up, rw_a0, rw_a_up, rw_g_up, rw_k_k, rw_k_a, rw_r_k, rw_ln_w, rw_ln_b, nsa_q_g, nsa_kc_g, nsa_ks_g, nsa_kw_g, cmp_pos_k, cmp_pos_v, cmp_k_w1, cmp_k_w2, cmp_v_w1, cmp_v_w2, w_branch_a, w_branch_b, w_out, norm2_g, peer_wq, peer_k1, peer_k2, peer_u, peer_v):
    h = _rmsnorm(x, norm1_g)
    offs = np.cumsum(IN_WIDTHS)[:-1].tolist()
    p_rw, q, k_c, v_c, k_s, v_s, k_w, v_w, b_gate, m_gate_a, m_gate_b = jnp.split(h @ w_in, offs, axis=-1)
    y_a = _rwkv7(p_rw, rw_mu, rw_w0, rw_w_up, rw_a0, rw_a_up, rw_g_up, rw_k_k, rw_k_a, rw_r_k, rw_ln_w, rw_ln_b)
    y_b = _nsa(q, k_c, v_c, k_s, v_s, k_w, v_w, b_gate, nsa_q_g, nsa_kc_g, nsa_ks_g, nsa_kw_g, cmp_pos_k, cmp_pos_v, cmp_k_w1, cmp_k_w2, cmp_v_w1, cmp_v_w2)
    mixed = jax.nn.sigmoid(m_gate_a) * (y_a @ w_branch_a) + jax.nn.sigmoid(m_gate_b) * (y_b @ w_branch_b)
    x = x + mixed @ w_out
    return x + _peer(_rmsnorm(x, norm2_g), peer_wq, peer_k1, peer_k2, peer_u, peer_v)


def setup_inputs(seed: int = 0) -> dict:
    key = jax.random.key(seed)
    ks = iter(jax.random.split(key, 40))
    L = DEPTH
    nrm = lambda shape, scale: jax.random.normal(next(ks), shape, jnp.float32) * scale
    gain = lambda shape: 1.0 + nrm(shape, 0.02)
    uni = lambda shape, lo, hi: jax.random.uniform(next(ks), shape, jnp.float32, lo, hi)
    return {
        'x': nrm((BATCH, SEQ, D_MODEL), 1.0),
        'norm1_g': gain((L, D_MODEL)),
        'w_in': nrm((L, D_MODEL, IN_TOTAL), D_MODEL ** -0.5),
        'rw_mu': uni((L, RW_COLS), 0.0, 1.0),
        'rw_w0': uni((L, RW_WIDTH), -4.0, 1.0),
        'rw_w_up': nrm((L, W_LORA, RW_WIDTH), 0.05),
        'rw_a0': nrm((L, RW_WIDTH), 0.5),
        'rw_a_up': nrm((L, A_LORA, RW_WIDTH), A_LORA ** -0.5),
        'rw_g_up': nrm((L, G_LORA, RW_WIDTH), G_LORA ** -0.5),
        'rw_k_k': 0.85 + nrm((L, RW_WIDTH), 0.05),
        'rw_k_a': 1.0 + nrm((L, RW_WIDTH), 0.05),
        'rw_r_k': nrm((L, RW_HEADS, RW_HEAD), 0.1),
        'rw_ln_w': gain((L, RW_WIDTH)),
        'rw_ln_b': nrm((L, RW_WIDTH), 0.02),
        'nsa_q_g': gain((L, NSA_DK)),
        'nsa_kc_g': gain((L, NSA_DK)),
        'nsa_ks_g': gain((L, NSA_DK)),
        'nsa_kw_g': gain((L, NSA_DK)),
        'cmp_pos_k': nrm((L, CMP_LEN, NSA_DK), 0.5),
        'cmp_pos_v': nrm((L, CMP_LEN, NSA_DK), 0.5),
        'cmp_k_w1': nrm((L, CMP_LEN * NSA_DK, CMP_HIDDEN), (CMP_LEN * NSA_DK) ** -0.5),
        'cmp_k_w2': nrm((L, CMP_HIDDEN, NSA_DK), CMP_HIDDEN ** -0.5),
        'cmp_v_w1': nrm((L, CMP_LEN * NSA_DK, CMP_HIDDEN), (CMP_LEN * NSA_DK) ** -0.5),
        'cmp_v_w2': nrm((L, CMP_HIDDEN, NSA_DK), CMP_HIDDEN ** -0.5),
        'w_branch_a': nrm((L, RW_WIDTH, D_MODEL), RW_WIDTH ** -0.5),
        'w_branch_b': nrm((L, NSA_WIDTH, D_MODEL), NSA_WIDTH ** -0.5),
        'w_out': nrm((L, D_MODEL, D_MODEL), D_MODEL ** -0.5),
        'norm2_g': gain((L, D_MODEL)),
        'peer_wq': nrm((L, D_MODEL, PEER_HEADS * PEER_DKEY), D_MODEL ** -0.5),
        'peer_k1': nrm((L, PEER_KEYS, PEER_DKEY // 2), (PEER_DKEY // 2) ** -0.5),
        'peer_k2': nrm((L, PEER_KEYS, PEER_DKEY // 2), (PEER_DKEY // 2) ** -0.5),
        'peer_u': nrm((L, PEER_EXPERTS, D_MODEL), D_MODEL ** -0.5),
        'peer_v': nrm((L, PEER_EXPERTS, D_MODEL), D_MODEL ** -0.5),
    }


def reference(x, norm1_g, w_in, rw_mu, rw_w0, rw_w_up, rw_a0, rw_a_up, rw_g_up, rw_k_k, rw_k_a, rw_r_k, rw_ln_w, rw_ln_b, nsa_q_g, nsa_kc_g, nsa_ks_g, nsa_kw_g, cmp_pos_k, cmp_pos_v, cmp_k_w1, cmp_k_w2, cmp_v_w1, cmp_v_w2, w_branch_a, w_branch_b, w_out, norm2_g, peer_wq, peer_k1, peer_k2, peer_u, peer_v):
    for i in range(DEPTH):
        x = _layer(x, norm1_g[i], w_in[i], rw_mu[i], rw_w0[i], rw_w_up[i], rw_a0[i], rw_a_up[i], rw_g_up[i], rw_k_k[i], rw_k_a[i], rw_r_k[i], rw_ln_w[i], rw_ln_b[i], nsa_q_g[i], nsa_kc_g[i], nsa_ks_g[i], nsa_kw_g[i], cmp_pos_k[i], cmp_pos_v[i], cmp_k_w1[i], cmp_k_w2[i], cmp_v_w1[i], cmp_v_w2[i], w_branch_a[i], w_branch_b[i], w_out[i], norm2_g[i], peer_wq[i], peer_k1[i], peer_k2[i], peer_u[i], peer_v[i])
    return x
```

```python
import numpy as np
import concourse.bass as bass
import concourse.mybir as mybir

F32 = mybir.dt.float32
BF16 = mybir.dt.bfloat16
I32 = mybir.dt.int32
U32 = mybir.dt.uint32
ALU = mybir.AluOpType
AF = mybir.ActivationFunctionType
AX = mybir.AxisListType

EPOCH = 20000
ENGS = ("pe", "act", "dve", "pool", "sp")
NDMASEM = 16


class Prog:
    def __init__(self, nc, stack):
        self.nc = nc
        self.stack = stack
        self.ops = {e: [] for e in ENGS}
        self.cnt = {e: 0 for e in ENGS}
        self.esems = {e: [] for e in ENGS}
        self.waited = {e: {} for e in ENGS}
        self.lastw = {}
        self.readers = {}
        self.dsems = {}
        self.dcount = {}
        self.dtarget = {}
        self.semobjs = {}
        self.alltokens = {}
        for q in ("sp", "act", "pool"):
            self.dsems[q] = [self._newsem(f"d_{q}_{i}") for i in range(NDMASEM)]
            self.dcount[q] = 0
            self.dtarget[q] = [0] * NDMASEM

    def _newsem(self, name):
        s = self.stack.enter_context(self.nc.semaphore(name))
        self.semobjs[name] = s
        return name

    def _esem(self, e, idx):
        ep = idx // EPOCH
        while len(self.esems[e]) <= ep:
            self.esems[e].append(self._newsem(f"e_{e}_{len(self.esems[e])}"))
        return self.esems[e][ep], (idx % EPOCH) + 1

    def _deps(self, reads, writes):
        toks = []
        for k in reads:
            t = self.lastw.get(k)
            if t is not None:
                toks.append(t)
        for k in writes:
            t = self.lastw.get(k)
            if t is not None:
                toks.append(t)
            toks.extend(self.readers.get(k, ()))
        return toks

    def _commit(self, tok, reads, writes):
        for k in reads:
            self.readers.setdefault(k, []).append(tok)
        for k in writes:
            self.lastw[k] = tok
            self.readers[k] = []
        self.alltokens[tok[0]] = max(self.alltokens.get(tok[0], 0), tok[1])

    def _waits(self, e, toks):
        need = {}
        for (s, v) in toks:
            if v > need.get(s, 0):
                need[s] = v
        out = []
        w = self.waited[e]
        for s, v in need.items():
            if w.get(s, 0) < v:
                w[s] = v
                out.append((s, v))
        return out

    def op(self, e, fn, reads=(), writes=()):
        toks = self._deps(reads, writes)
        if e == "pe":
            toks = [t for t in toks if not t[0].startswith("e_pe_")]
        waits = self._waits(e, toks)
        idx = self.cnt[e]
        self.cnt[e] += 1
        tok = self._esem(e, idx)
        self.ops[e].append((waits, fn, (tok[0], 1)))
        self._commit(tok, reads, writes)

    def dma(self, q, fn, reads=(), writes=()):
        toks = self._deps(reads, writes)
        n = self.dcount[q]
        self.dcount[q] += 1
        slot = n % NDMASEM
        sname = self.dsems[q][slot]
        prev = self.dtarget[q][slot]
        if prev > 0:
            toks.append((sname, prev))
        tgt = prev + 16
        self.dtarget[q][slot] = tgt
        waits = self._waits(q, toks)
        tok = (sname, tgt)
        self.ops[q].append((waits, fn, (sname, 16)))
        self._commit(tok, reads, writes)

    def mm(self, out, lhsT, rhs, start=True, stop=True, reads=(), writes=()):
        self.op("pe", lambda e: e.matmul(out, lhsT, rhs, start=start, stop=stop), reads, writes)

    def tr(self, out, in_, ident, reads=(), writes=()):
        self.op("pe", lambda e: e.transpose(out, in_, ident), reads, writes)

    def act(self, out, in_, func, reads=(), writes=(), bias=None, scale=None, eng="act"):
        kw = {}
        if bias is not None:
            kw["bias"] = bias
        if scale is not None:
            kw["scale"] = scale
        self.op(eng, lambda e: e.activation(out, in_, func, **kw), reads, writes)

    def tt(self, eng, out, in0, in1, op, reads=(), writes=()):
        self.op(eng, lambda e: e.tensor_tensor(out, in0, in1, op), reads, writes)

    def ts(self, eng, out, in0, s1, s2, op0, op1=None, reads=(), writes=()):
        if op1 is None:
            self.op(eng, lambda e: e.tensor_scalar(out, in0, s1, s2, op0), reads, writes)
        else:
            self.op(eng, lambda e: e.tensor_scalar(out, in0, s1, s2, op0, op1), reads, writes)

    def stt(self, eng, out, in0, scalar, in1, op0, op1, reads=(), writes=()):
        self.op(eng, lambda e: e.scalar_tensor_tensor(out, in0, scalar, in1, op0, op1), reads, writes)

    def cp(self, eng, out, in_, reads=(), writes=()):
        if eng == "act":
            self.op(eng, lambda e: e.copy(out, in_), reads, writes)
        else:
            self.op(eng, lambda e: e.tensor_copy(out, in_), reads, writes)

    def red(self, eng, out, in_, op, reads=(), writes=(), axis=None):
        ax = AX.X if axis is None else axis
        self.op(eng, lambda e: e.tensor_reduce(out, in_, ax, op), reads, writes)

    def memset(self, eng, ap, val, writes=()):
        self.op(eng, lambda e: e.memset(ap, val), (), writes)

    def ld(self, out, in_, reads=(), writes=(), q="sp"):
        self.dma(q, lambda e: e.dma_start(out, in_), reads, writes)

    def barrier(self):
        toks = list(self.alltokens.items())
        for e in ENGS:
            waits = self._waits(e, toks)
            if waits:
                self.ops[e].append((waits, None, None))
        self.lastw = {}
        self.readers = {}

    def emit(self):
        nc = self.nc
        so = self.semobjs
        with nc.Block() as block:
            def mk(e):
                def body(eng):
                    for waits, fn, inc in self.ops[e]:
                        for (s, v) in waits:
                            eng.wait_ge(so[s], v)
                        if fn is not None:
                            ins = fn(eng)
                            ins.then_inc(so[inc[0]], inc[1])
                return body
            block.tensor(mk("pe"))
            block.scalar(mk("act"))
            block.vector(mk("dve"))
            block.gpsimd(mk("pool"))
            block.sync(mk("sp"))
        self.ops = {e: [] for e in ENGS}
from contextlib import ExitStack
from concourse.bass_utils import run_bass_kernel_spmd

T = 4096
D = 1024
NT = T // 128
INW = 5144
RWC = 1792
O_RW = 0
O_Q = 1792
O_KC = 2304
O_VC = 2432
O_KS = 2560
O_VS = 2688
O_KW = 2816
O_VW = 2944
O_BG = 3072
O_GA = 3096
O_GB = 4120


class Ctx:
    pass


def _mk(C, st):
    nc = C.nc
    sb = lambda name, shape, dt: st.enter_context(nc.sbuf_tensor(name, shape, dt))
    ps = lambda name, shape, dt: st.enter_context(nc.psum_tensor(name, shape, dt))
    return sb, ps


def stage1(C):
    nc, pg, dr = C.nc, C.pg, C.dr
    with ExitStack() as st:
        sb, ps = _mk(C, st)
        win = sb("s1_win", [128, 8, INW], BF16)
        pj = [sb(f"s1_pj{i}", [128, INW], F32) for i in range(2)]
        xt = [sb(f"s1_xt{i}", [128, D], F32) for i in range(2)]
        junk = sb("s1_junk", [128, D], F32)
        hb = [sb(f"s1_h{i}", [128, D], BF16) for i in range(2)]
        hT = [sb(f"s1_hT{i}", [128, 8, 128], BF16) for i in range(2)]
        gt = sb("s1_g", [128, D], F32)
        idf = sb("s1_idf", [128, 128], F32)
        idb = sb("s1_idb", [128, 128], BF16)
        ss = [sb(f"s1_ss{i}", [128, 4], F32) for i in range(2)]
        psT = [ps(f"s1_psT{i}", [128, 8, 128], BF16) for i in range(2)]
        psm = [ps(f"s1_psm{i}", [128, 512], F32) for i in range(4)]

        pg.ld(gt[:], dr["norm1_g_b"][:, :], writes=["gt"])
        pg.ld(idf[:], dr["ident"][:, :], writes=["idf"])
        pg.cp("dve", idb[:], idf[:], reads=["idf"], writes=["idb"])
        engs = ["act", "dve", "pool"]
        for kc in range(8):
            b = pj[kc % 2]
            pg.ld(b[:], dr["w_in"][kc * 128:(kc + 1) * 128, :], writes=[("pjall", kc % 2)])
            pg.cp(engs[kc % 3], win[:, kc, :], b[:], reads=[("pjall", kc % 2)], writes=[("win", kc)])
        winkeys = [("win", kc) for kc in range(8)]
        chunks = []
        c0 = 0
        while c0 < INW:
            w = min(512, INW - c0)
            chunks.append((c0, w))
            c0 += w
        def A1(i):
            s = i % 2
            pg.ld(xt[s][:], dr["x"][i * 128:(i + 1) * 128, :], writes=[("xt", s)])
            pg.tt("dve", junk[:], xt[s][:], xt[s][:], ALU.mult, reads=[("xt", s)], writes=["junk"])
            pg.red("dve", ss[s][:, 0:1], junk[:], ALU.add, reads=["junk"], writes=[("ss", s)])
            pg.act(ss[s][:, 1:2], ss[s][:, 0:1], AF.Sqrt, reads=[("ss", s)], writes=[("ss1", s)],
                   scale=1.0 / D, bias=C.eps6[:, 0:1])
            pg.op("dve", lambda e, o=ss[s][:, 2:3], a=ss[s][:, 1:2]: e.reciprocal(o, a),
                  reads=[("ss1", s)], writes=[("ss2", s)])
            pg.stt("dve", hb[s][:], xt[s][:], ss[s][:, 2:3], gt[:], ALU.mult, ALU.mult,
                   reads=[("xt", s), ("ss2", s), "gt"], writes=[("hb", s)])

        def A2(i):
            s = i % 2
            for j in range(8):
                pg.tr(psT[s][:, j, :], hb[s][:, j * 128:(j + 1) * 128], idb[:],
                      reads=[("hb", s), "idb"], writes=[("psT", s)])
            pg.cp("act", hT[s][:], psT[s][:], reads=[("psT", s)], writes=[("hT", s)])

        def B(i, lo, hi):
            s = i % 2
            for ci in range(lo, hi):
                c0, w = chunks[ci]
                pb = psm[ci % 4]
                for kc in range(8):
                    pg.mm(pb[:, :w], hT[s][:, kc, :], win[:, kc, c0:c0 + w], start=(kc == 0), stop=(kc == 7),
                          reads=[("hT", s), ("win", kc)], writes=[("psm", ci % 4)])
                pg.cp("act" if ci % 2 == 0 else "dve", pj[s][:, c0:c0 + w], pb[:, :w],
                      reads=[("psm", ci % 4)], writes=[("pj", s, ci), ("pjall", s)] if i < 8 else [("pj", s, ci)])

        def S(i):
            s = i % 2
            pg.ld(dr["P"][i * 128:(i + 1) * 128, 0:3096], pj[s][:, 0:3096],
                  reads=[("pj", s, ci) for ci in range(len(chunks))], writes=[("P", i)])
            pg.ld(dr["PG"][i * 128:(i + 1) * 128, :], pj[s][:, 3096:5144],
                  reads=[("pj", s, ci) for ci in range(len(chunks))], writes=[("PG", i)], q="act")

        A1(0)
        A2(0)
        A1(1)
        for i in range(NT):
            B(i, 0, 6)
            if i + 1 < NT:
                A2(i + 1)
            if i + 2 < NT:
                A1(i + 2)
            B(i, 6, len(chunks))
            S(i)
        pg.barrier()
        pg.emit()


def build(stages, dbg_out=(), dbg_in=(), lvl=9, sub=9, peer_tiles=NT):
    nc = bass.Bass("TRN2", target_bir_lowering=False)
    C = Ctx()
    C.peer_tiles = peer_tiles
    C.lvl = lvl
    C.sub = sub
    C.nc = nc
    dr = {}
    C.dr = dr

    def din(name, shape, dt=F32):
        dr[name] = nc.dram_tensor(name, list(shape), dt, kind="ExternalInput").ap()

    def dscr(name, shape, dt=F32):
        kind = "ExternalOutput" if name in dbg_out else ("ExternalInput" if name in dbg_in else "Internal")
        dr[name] = nc.dram_tensor(name, list(shape), dt, kind=kind).ap()

    din("x", [T, D])
    din("norm1_g_b", [128, D])
    din("ident", [128, 128])
    din("w_in", [D, INW])
    dscr("P", [T, INW])
    dscr("PG", [T, 2048])
    for nm in ("rw_mu_b",):
        din(nm, [128, RWC])
    for nm in ("rw_w0_b", "rw_a0_b", "rw_k_k_b", "rw_k_a_b", "rw_r_k_b", "rw_ln_w_b", "rw_ln_b_b", "rw_g_up"):
        din(nm, [128, 512])
    din("rw_w_up", [64, 512])
    din("rw_a_up", [64, 512])
    for nm in ("RB", "RKp", "RV", "RG", "YA", "RR", "RKK", "RLW", "YS"):
        dscr(nm, [T, 512])
    dscr("RBON", [T, 8])
    din("blkmask", [8, 512])
    din("c_tri", [128, 128])
    din("c_msk", [128, 3, 128])
    din("nsa_gains_b", [128, 768])
    din("nsa_kc_g_b", [128, 64])
    din("ovl", [128, 2, 64])
    din("posT", [128, 2, 32])
    din("cmp_w2", [128, 2, 2, 64])
    din("cmp_w1", [2, 128, 32, 256])
    din("c_cmpb", [128, 2, T], BF16)
    din("c_esel", [128, 32, 128], BF16)
    din("c_causb", [128, 4, 512], BF16)
    din("c_winb", [128, 8, 512], BF16)
    din("c_vmfb", [NT, 128, 2, 64])
    dscr("YB", [T, 512])
    din("w_branch_a", [512, D])
    din("w_branch_b", [512, D])
    din("w_out", [D, D])
    dscr("X1L", [peer_tiles * 128, D])
    dscr("YAL", [peer_tiles * 128, 512])
    din("norm2_g_b", [128, D])
    din("iota16", [128, 16])
    din("rowidx", [128, NT], I32)
    din("peer_wq", [D, 2048])
    din("peer_k1", [128, 128])
    din("peer_k2", [128, 128])
    din("peer_u", [16384, D])
    din("peer_v", [16384, D])
    dscr("UV", [16384, 2 * D], BF16)
    dr["out"] = nc.dram_tensor("out", [peer_tiles * 128, D], F32, kind="ExternalOutput").ap()
    with ExitStack() as top:
        pg = Prog(nc, top)
        C.pg = pg
        C.eps6 = top.enter_context(nc.sbuf_tensor("c_eps6", [128, 1], F32))
        pg.memset("dve", C.eps6[:], 1e-6, writes=["eps6"])
        pg.barrier()
        for s in stages:
            s(C)
        pg.barrier()
        pg.emit()
    return nc


def host_inputs(inputs, b, hh=0, ntl=NT):
    g = lambda k: np.ascontiguousarray(inputs[k][0])
    m = {}
    m["x"] = np.ascontiguousarray(inputs["x"][b])
    m["norm1_g_b"] = np.ascontiguousarray(np.broadcast_to(g("norm1_g")[None, :], (128, D)))
    m["ident"] = np.eye(128, dtype=np.float32)
    m["w_in"] = g("w_in")
    bc = lambda a: np.ascontiguousarray(np.broadcast_to(np.asarray(a).reshape(1, -1), (128, a.size)))
    m["rw_mu_b"] = bc(g("rw_mu"))
    for nm in ("rw_w0", "rw_a0", "rw_k_k", "rw_k_a", "rw_r_k", "rw_ln_w", "rw_ln_b"):
        m[nm + "_b"] = bc(g(nm))
    for nm in ("rw_g_up", "rw_w_up", "rw_a_up"):
        m[nm] = g(nm)
    bmk = np.zeros((8, 512), np.float32)
    for h in range(8):
        bmk[h, h * 64:(h + 1) * 64] = 1.0
    m["blkmask"] = bmk
    ii = np.arange(128)
    m["c_tri"] = (ii[:, None] <= ii[None, :]).astype(np.float32)
    m["c_msk"] = np.ascontiguousarray(np.stack([(ii[:, None] < ii[None, :]), (ii[:, None] <= ii[None, :]), (ii[:, None] > ii[None, :])], 1).astype(np.float32))
    m.update(nsa_consts())
    for nm in ("w_branch_a", "w_branch_b", "w_out", "peer_wq", "peer_k1", "peer_k2", "peer_u", "peer_v"):
        m[nm] = g(nm)
    m["norm2_g_b"] = bc(g("norm2_g"))
    ri = np.zeros((128, NT), np.int32)
    ri[:, :ntl] = (hh * ntl * 128 + np.arange(ntl)[None, :] * 128 + np.arange(128)[:, None]).astype(np.int32)
    m["rowidx"] = ri
    m["iota16"] = np.ascontiguousarray(np.broadcast_to(np.arange(16, dtype=np.float32)[None, :], (128, 16)))
    m["nsa_gains_b"] = bc(np.concatenate([np.tile(g("nsa_q_g"), 8), np.tile(g("nsa_ks_g"), 2), np.tile(g("nsa_kw_g"), 2)]))
    m["nsa_kc_g_b"] = bc(g("nsa_kc_g"))
    posT = np.zeros((128, 2, 32), np.float32)
    posT[0:64, 0, :] = g("cmp_pos_k").T
    posT[0:64, 1, :] = g("cmp_pos_v").T
    m["posT"] = posT
    w2 = np.stack([g("cmp_k_w2").reshape(2, 128, 64), g("cmp_v_w2").reshape(2, 128, 64)], 0)
    m["cmp_w2"] = np.ascontiguousarray(w2.transpose(2, 0, 1, 3))
    w1 = []
    for nm in ("cmp_k_w1", "cmp_v_w1"):
        a = g(nm).reshape(32, 64, 256).transpose(1, 0, 2)
        w1.append(np.concatenate([a, a], 0))
    m["cmp_w1"] = np.ascontiguousarray(np.stack(w1, 0))
    return m


def dap(ap, offset, pattern):
    return bass.AP(ap.tensor, offset, [list(p) for p in pattern])


def stage2a(C):
    nc, pg, dr = C.nc, C.pg, C.dr
    with ExitStack() as st:
        sb, ps = _mk(C, st)
        mu = sb("a_mu", [128, RWC], F32)
        w0 = sb("a_w0", [128, 512], F32)
        a0 = sb("a_a0", [128, 512], F32)
        kkc = sb("a_kk", [128, 512], F32)
        kac = sb("a_ka", [128, 512], F32)
        rkc = sb("a_rk", [128, 512], F32)
        wup = sb("a_wup", [128, 512], F32)
        gup = sb("a_gup", [128, 512], F32)
        idf = sb("a_idf", [128, 128], F32)
        p_2 = [sb(f"a_p{i_}", [128, RWC], F32) for i_ in range(2)]
        pv_2 = [sb(f"a_pv{i_}", [128, RWC], F32) for i_ in range(2)]
        pm_2 = [sb(f"a_pm{i_}", [128, RWC], F32) for i_ in range(2)]
        lor_2 = [sb(f"a_lor{i_}", [128, 256], F32) for i_ in range(2)]
        lorT_2 = [sb(f"a_lorT{i_}", [128, 256], F32) for i_ in range(2)]
        wt_2 = [sb(f"a_wt{i_}", [128, 512], F32) for i_ in range(2)]
        lwt_2 = [sb(f"a_lwt{i_}", [128, 512], F32) for i_ in range(2)]
        at_2 = [sb(f"a_at{i_}", [128, 512], F32) for i_ in range(2)]
        gt_2 = [sb(f"a_gt{i_}", [128, 512], F32) for i_ in range(2)]
        kk_2 = [sb(f"a_kkt{i_}", [128, 512], F32) for i_ in range(2)]
        sq_2 = [sb(f"a_sq{i_}", [128, 512], F32) for i_ in range(2)]
        nrm_2 = [sb(f"a_nrm{i_}", [128, 32], F32) for i_ in range(2)]
        kkn_2 = [sb(f"a_kkn{i_}", [128, 512], F32) for i_ in range(2)]
        bt_2 = [sb(f"a_bt{i_}", [128, 512], F32) for i_ in range(2)]
        t1_2 = [sb(f"a_t1{i_}", [128, 512], F32) for i_ in range(2)]
        kp_2 = [sb(f"a_kp{i_}", [128, 512], F32) for i_ in range(2)]
        bon_2 = [sb(f"a_bon{i_}", [128, 8], F32) for i_ in range(2)]
        psl = ps("a_psl", [128, 512], F32)
        psw = ps("a_psw", [128, 512], F32)
        psa = ps("a_psa", [128, 512], F32)
        psg = ps("a_psg", [128, 512], F32)

        for (tile, name) in ((mu, "rw_mu_b"), (w0, "rw_w0_b"), (a0, "rw_a0_b"), (kkc, "rw_k_k_b"),
                             (kac, "rw_k_a_b"), (rkc, "rw_r_k_b"), (gup, "rw_g_up"), (idf, "ident")):
            pg.ld(tile[:], dr[name][:, :], writes=[name])
        pg.ld(wup[0:64, :], dr["rw_w_up"][:, :], writes=["wup0"])
        pg.ld(wup[64:128, :], dr["rw_a_up"][:, :], writes=["wup1"])
        P = dr["P"]
        for i in range(NT):
            t0 = i * 128
            s = i % 2
            p, pv, pm, lor, lorT, wt, lwt, at, gt, kk, sq, nrm, kkn, bt, t1, kp, bon = [t_[s] for t_ in (
                p_2, pv_2, pm_2, lor_2, lorT_2, wt_2, lwt_2, at_2, gt_2, kk_2, sq_2, nrm_2, kkn_2, bt_2, t1_2, kp_2, bon_2)]
            pg.ld(p[:], P[t0:t0 + 128, 0:RWC], reads=[("P", i)], writes=[("p", s)])
            if i == 0:
                pg.memset("dve", pv[0:1, :], 0.0, writes=[("pv0", s)])
                pg.ld(pv[1:128, :], P[0:127, 0:RWC], reads=[("P", 0)], writes=[("pv", s)])
                pvk = [("pv", s), ("pv0", s)]
            else:
                pg.ld(pv[:], P[t0 - 1:t0 + 127, 0:RWC], reads=[("P", i), ("P", i - 1)], writes=[("pv", s), ("pv0", s)])
                pvk = [("pv", s), ("pv0", s)]
            pg.tt("dve", pv[:], pv[:], p[:], ALU.subtract, reads=pvk + [("p", s)], writes=[("pv", s)])
            pg.tt("dve", pv[:], pv[:], mu[:], ALU.mult, reads=[("pv", s), "rw_mu_b"], writes=[("pv", s)])
            pg.tt("dve", pm[:], pv[:], p[:], ALU.add, reads=[("pv", s), ("p", s)], writes=[("pm", s)])
            r_ = pm[:, 0:512]
            k_ = pm[:, 512:1024]
            v_ = pm[:, 1024:1536]
            pg.act(lor[:, 0:64], pm[:, 1536:1600], AF.Tanh, reads=[("pm", s)], writes=[("lor0", s)])
            pg.cp("pool", lor[:, 64:128], pm[:, 1600:1664], reads=[("pm", s)], writes=[("lor1", s)])
            pg.act(lor[:, 128:256], pm[:, 1664:1792], AF.Sigmoid, reads=[("pm", s)], writes=[("lor2", s)])
            pg.tr(psl[:, 0:128], lor[:, 0:128], idf[:], reads=[("lor0", s), ("lor1", s), "ident"], writes=["psl"])
            pg.tr(psl[:, 128:256], lor[:, 128:256], idf[:], reads=[("lor2", s), "ident"], writes=["psl"])
            pg.cp("act", lorT[:], psl[:, 0:256], reads=["psl"], writes=[("lorT", s)])
            pg.mm(psw[:], lorT[0:64, 0:128], wup[0:64, :], reads=[("lorT", s), "wup0"], writes=["psw"])
            pg.mm(psa[:], lorT[64:128, 0:128], wup[64:128, :], reads=[("lorT", s), "wup1"], writes=["psa"])
            pg.mm(psg[:], lorT[:, 128:256], gup[:], reads=[("lorT", s), "rw_g_up"], writes=["psg"])
            pg.tt("dve", wt[:], psw[:], w0[:], ALU.add, reads=["psw", "rw_w0_b"], writes=[("wt", s)])
            pg.act(wt[:], wt[:], AF.Sigmoid, reads=[("wt", s)], writes=[("wt", s)])
            pg.ts("dve", lwt[:], wt[:], -0.6065306597126334, None, ALU.mult, reads=[("wt", s)], writes=[("lwt", s)])
            pg.tt("dve", at[:], psa[:], a0[:], ALU.add, reads=["psa", "rw_a0_b"], writes=[("at", s)])
            pg.act(at[:], at[:], AF.Sigmoid, reads=[("at", s)], writes=[("at", s)])
            pg.cp("act", gt[:], psg[:], reads=["psg"], writes=[("gt", s)])
            pg.tt("dve", kk[:], k_, kkc[:], ALU.mult, reads=[("pm", s), "rw_k_k_b"], writes=[("kk", s)])
            pg.tt("pool", sq[:], kk[:], kk[:], ALU.mult, reads=[("kk", s)], writes=[("sq", s)])
            pg.red("dve", nrm[:, 0:8], sq[:].rearrange("p (h k) -> p h k", h=8), ALU.add, reads=[("sq", s)], writes=[("nrm0", s)])
            pg.act(nrm[:, 8:16], nrm[:, 0:8], AF.Sqrt, reads=[("nrm0", s)], writes=[("nrm1", s)])
            pg.ts("dve", nrm[:, 16:24], nrm[:, 8:16], 1e-12, None, ALU.max, reads=[("nrm1", s)], writes=[("nrm2", s)])
            pg.op("dve", lambda e, nrm=nrm: e.reciprocal(nrm[:, 24:32], nrm[:, 16:24]), reads=[("nrm2", s)], writes=[("nrm3", s)])
            rinv_b = nrm[:, 24:32].unsqueeze(2).to_broadcast([128, 8, 64])
            v3 = lambda tl: tl[:].rearrange("p (h k) -> p h k", h=8)
            pg.stt("dve", v3(kkn), v3(kk), -1.0, rinv_b, ALU.mult, ALU.mult, reads=[("kk", s), ("nrm3", s)], writes=[("kkn", s)])
            pg.stt("dve", bt[:], kkn[:], -1.0, at[:], ALU.mult, ALU.mult, reads=[("kkn", s), ("at", s)], writes=[("bt", s)])
            pg.stt("dve", t1[:], at[:], -1.0, kac[:], ALU.add, ALU.mult, reads=[("at", s), "rw_k_a_b"], writes=[("t1", s)])
            pg.stt("dve", kp[:], t1[:], 1.0, k_, ALU.add, ALU.mult, reads=[("t1", s), ("pm", s)], writes=[("kp", s)])
            pg.tt("pool", sq[:], r_, kp[:], ALU.mult, reads=[("pm", s), ("kp", s), ("sq", s)], writes=[("sq", s)])
            pg.tt("pool", sq[:], sq[:], rkc[:], ALU.mult, reads=[("sq", s), "rw_r_k_b"], writes=[("sq", s)])
            pg.red("dve", bon[:], sq[:].rearrange("p (h k) -> p h k", h=8), ALU.add, reads=[("sq", s)], writes=[("bon", s)])
            pg.ld(dr["RR"][t0:t0 + 128, :], r_, reads=[("pm", s)], writes=[("RR", i)], q="act")
            pg.ld(dr["RKK"][t0:t0 + 128, :], kkn[:], reads=[("kkn", s)], writes=[("RKK", i)], q="act")
            pg.ld(dr["RLW"][t0:t0 + 128, :], lwt[:], reads=[("lwt", s)], writes=[("RLW", i)], q="act")
            pg.ld(dr["RB"][t0:t0 + 128, :], bt[:], reads=[("bt", s)], writes=[("RB", i)], q="act")
            pg.ld(dr["RKp"][t0:t0 + 128, :], kp[:], reads=[("kp", s)], writes=[("RKp", i)], q="act")
            pg.ld(dr["RV"][t0:t0 + 128, :], v_, reads=[("pm", s)], writes=[("RV", i)], q="act")
            pg.ld(dr["RG"][t0:t0 + 128, :], gt[:], reads=[("gt", s)], writes=[("RG", i)], q="act")
            pg.ld(dr["RBON"][t0:t0 + 128, :], bon[:], reads=[("bon", s)], writes=[("RBON", i)], q="act")
        pg.barrier()
        pg.emit()


def igather(pg, out_ap, table_ap, idx_ap, reads, writes):
    pg.dma("pool", lambda e: e.indirect_dma_start(out=out_ap, out_offset=None, in_=table_ap,
                                                   in_offset=bass.IndirectOffsetOnAxis(ap=idx_ap, axis=0)), reads, writes)


def stage2c(C):
    nc, pg, dr = C.nc, C.pg, C.dr
    with ExitStack() as st:
        sb, ps = _mk(C, st)
        lnw = sb("c_lnw", [128, 512], F32)
        lnb = sb("c_lnb", [128, 512], F32)
        eps = sb("c_eps", [128, 1], F32)
        y = [sb(f"c_y{i}", [128, 8, 64], F32) for i in range(2)]
        v = [sb(f"c_v{i}", [128, 8, 64], F32) for i in range(2)]
        g = [sb(f"c_g{i}", [128, 512], F32) for i in range(2)]
        bon = [sb(f"c_bon{i}", [128, 8], F32) for i in range(2)]
        stt_ = [sb(f"c_st{i}", [128, 32], F32) for i in range(2)]
        sq = sb("c_sq", [128, 8, 64], F32)
        pg.ld(lnw[:], dr["rw_ln_w_b"][:, :], writes=["lnw"])
        pg.ld(lnb[:], dr["rw_ln_b_b"][:, :], writes=["lnb"])
        pg.memset("dve", eps[:], 64e-5, writes=["eps"])
        f2 = lambda tl: tl[:].rearrange("p h k -> p (h k)")
        rowidx = sb("c_rowidx", [128, NT], I32)
        pg.ld(rowidx[:], dr["rowidx"][:, :], writes=["rowidx"])
        allk = lambda nm: [(nm, k) for k in range(NT)]
        for i in range(C.peer_tiles):
            s = i % 2
            t0 = i * 128
            yk, vk, gk, bk, sk = ("y", s), ("v", s), ("g", s), ("bon", s), ("st", s)
            ix = rowidx[:, i:i + 1]
            igather(pg, f2(y[s]), dr["YS"][:, :], ix, allk("YS") + ["rowidx"], [yk])
            igather(pg, f2(v[s]), dr["RV"][:, :], ix, allk("RV") + ["rowidx"], [vk])
            igather(pg, g[s][:], dr["RG"][:, :], ix, allk("RG") + ["rowidx"], [gk])
            igather(pg, bon[s][:], dr["RBON"][:, :], ix, allk("RBON") + ["rowidx"], [bk])
            S_ = stt_[s]
            bc = lambda ap: ap.unsqueeze(2).to_broadcast([128, 8, 64])
            pg.red("dve", S_[:, 0:8], y[s][:], ALU.add, reads=[yk], writes=[(sk, 0)])
            pg.ts("dve", S_[:, 8:16], S_[:, 0:8], -1.0 / 64, None, ALU.mult, reads=[(sk, 0)], writes=[(sk, 1)])
            pg.tt("dve", y[s][:], y[s][:], bc(S_[:, 8:16]), ALU.add, reads=[yk, (sk, 1)], writes=[yk])
            pg.tt("pool", sq[:], y[s][:], y[s][:], ALU.mult, reads=[yk], writes=["sq"])
            pg.red("dve", S_[:, 16:24], sq[:], ALU.add, reads=["sq"], writes=[(sk, 2)])
            pg.act(S_[:, 24:32], S_[:, 16:24], AF.Sqrt, reads=[(sk, 2), "eps"], writes=[(sk, 3)], scale=1.0 / 64, bias=eps[:, 0:1])
            pg.op("dve", lambda e, o=S_[:, 16:24], a=S_[:, 24:32]: e.reciprocal(o, a), reads=[(sk, 3)], writes=[(sk, 2)])
            pg.tt("dve", y[s][:], y[s][:], bc(S_[:, 16:24]), ALU.mult, reads=[yk, (sk, 2)], writes=[yk])
            pg.tt("dve", f2(y[s]), f2(y[s]), lnw[:], ALU.mult, reads=[yk, "lnw"], writes=[yk])
            pg.tt("pool", f2(y[s]), f2(y[s]), lnb[:], ALU.add, reads=[yk, "lnb"], writes=[yk])
            pg.tt("pool", v[s][:], v[s][:], bc(bon[s][:, 0:8]), ALU.mult, reads=[vk, bk], writes=[vk])
            pg.tt("dve", y[s][:], y[s][:], v[s][:], ALU.add, reads=[yk, vk], writes=[yk])
            pg.tt("dve", f2(y[s]), f2(y[s]), g[s][:], ALU.mult, reads=[yk, gk], writes=[yk])
            pg.ld(dr["YAL"][t0:t0 + 128, :], f2(y[s]), reads=[yk], writes=[("YAL", i)])
        pg.barrier()
        pg.emit()


NEG = -30000.0


def stage3(C):
    nc, pg, dr = C.nc, C.pg, C.dr
    with ExitStack() as st:
        sb, ps = _mk(C, st)
        qT = sb("n_qT", [128, 4, T], BF16)
        KsT = sb("n_KsT", [128, 2, T], BF16)
        KwT = sb("n_KwT", [128, 2, T], BF16)
        Vs = sb("n_Vs", [128, NT, 2, 65], BF16)
        Vw = sb("n_Vw", [128, NT, 2, 65], BF16)
        KcT = sb("n_KcT", [128, 2, 256], BF16)
        Vc = sb("n_Vc", [128, 2, 2, 129], BF16)
        GT = sb("n_GT", [128, NT, 24], F32)
        idf = sb("n_idf", [128, 128], F32)
        idb = sb("n_idb", [128, 128], BF16)
        eps = sb("n_eps", [128, 1], F32)
        pg.ld(idf[:], dr["ident"][:, :], writes=["idf"])
        pg.cp("dve", idb[:], idf[:], reads=["idf"], writes=["idb"])
        pg.memset("dve", eps[:], 1e-6, writes=["eps"])
        pg.memset("pool", Vs[:], 1.0, writes=["Vs"])
        pg.memset("pool", Vw[:], 1.0, writes=["Vw"])
        pg.memset("pool", Vc[:], 0.0, writes=["Vc"])
        with ExitStack() as sa_:
            sb, ps = _mk(C, sa_)
            kcT2 = sb("n_kcT2", [128, T], BF16)
            vcT2 = sb("n_vcT2", [128, T], BF16)
            w1 = [sb(f"n_w1{i}", [128, 32, 256], BF16) for i in range(2)]
            w1s = sb("n_w1s", [128, 16, 256], F32)
            w2s = sb("n_w2s", [128, 2, 2, 64], F32)
            w2 = sb("n_w2", [128, 2, 2, 64], BF16)
            posf = sb("n_posf", [128, 2, 32], F32)
            posb = sb("n_posb", [128, 2, 32], BF16)
            gains = sb("n_gains", [128, 768], F32)
            kcg = sb("n_kcg", [128, 64], F32)
            ovl = sb("n_ovl", [128, 2, 64], F32)
            R = [sb(f"n_R{i}", [128, 1304], F32) for i in range(2)]
            sq = sb("n_sq", [128, 1280], F32)
            tmp = sb("n_tmp", [128, 768], F32)
            stat = sb("n_stat", [128, 64], F32)
            Xb = sb("n_Xb", [128, 10, 128], BF16)
            biasS = sb("n_biasS", [128, 4], F32)
            xb_ = sb("n_xb", [128, 256], F32)
            x2_ = sb("n_x2", [128, 256], F32)
            hT = sb("n_hT", [128, 2, 256], BF16)
            kcn2 = sb("n_kcn2", [128, 128], BF16)
            st2 = sb("n_st2", [128, 8], F32)
            ksq = sb("n_ksq", [128, 64], F32)
            psX_ = [ps(f"n_psX{i}", [128, 1024], BF16) for i in range(3)]
            psX = [t_[:, 0:512].rearrange("p (a b) -> p a b", a=4) for t_ in psX_]
            psh = ps("n_psh", [128, 512], F32)
            psb = ps("n_psb", [128, 512], F32)
            pso = ps("n_pso", [128, 512], F32)
            psk = ps("n_psk", [128, 1024], BF16)

            pg.ld(gains[:], dr["nsa_gains_b"][:, :], writes=["gains"])
            pg.ts("dve", gains[:, 0:512], gains[:, 0:512], 0.125, None, ALU.mult, reads=["gains"], writes=["gains"])
            pg.ld(kcg[:], dr["nsa_kc_g_b"][:, :], writes=["kcg"])
            pg.ld(ovl[:], dr["ovl"][:, :, :], writes=["ovl"])
            pg.ld(posf[:], dr["posT"][:, :, :], writes=["posf"])
            pg.cp("dve", posb[:], posf[:], reads=["posf"], writes=["posb"])
            pg.ld(w2s[:], dr["cmp_w2"][:, :, :, :], writes=["w2s"])
            pg.cp("dve", w2[:], w2s[:], reads=["w2s"], writes=["w2"])
            for x in range(2):
                for hf in range(2):
                    pg.ld(w1s[:], dr["cmp_w1"][x, :, hf * 16:(hf + 1) * 16, :], writes=["w1s"])
                    pg.cp("pool", w1[x][:, hf * 16:(hf + 1) * 16, :], w1s[:], reads=["w1s"], writes=[("w1", x)])
            for i in range(NT):
                s = i % 2
                t0 = i * 128
                Rk = ("R", s)
                pg.ld(R[s][:], dr["P"][t0:t0 + 128, 1792:3096], reads=[("P", i)], writes=[Rk])
                Rs = R[s]
                pg.tt("pool", sq[:], Rs[:, 0:1280], Rs[:, 0:1280], ALU.mult, reads=[Rk], writes=["sq"])
                pg.red("dve", stat[:, 0:20], sq[:].rearrange("p (a k) -> p a k", k=64), ALU.add, reads=["sq"], writes=["stat0"])
                pg.act(stat[:, 20:40], stat[:, 0:20], AF.Sqrt, reads=["stat0", "eps"], writes=["stat1"], scale=1.0 / 64, bias=eps[:, 0:1])
                pg.op("dve", lambda e: e.reciprocal(stat[:, 40:60], stat[:, 20:40]), reads=["stat1"], writes=["stat2"])
                b3 = lambda ap, n: ap.unsqueeze(2).to_broadcast([128, n, 64])
                v3 = lambda ap: ap.rearrange("p (a k) -> p a k", k=64)
                pg.tt("dve", v3(tmp[:, 0:512]), v3(Rs[:, 0:512]), b3(stat[:, 40:48], 8), ALU.mult, reads=[Rk, "stat2"], writes=["tmp"])
                pg.tt("dve", v3(tmp[:, 512:640]), v3(Rs[:, 768:896]), b3(stat[:, 52:54], 2), ALU.mult, reads=[Rk, "stat2"], writes=["tmp"])
                pg.tt("dve", v3(tmp[:, 640:768]), v3(Rs[:, 1024:1152]), b3(stat[:, 56:58], 2), ALU.mult, reads=[Rk, "stat2"], writes=["tmp"])
                pg.tt("pool", tmp[:], tmp[:], gains[:], ALU.mult, reads=["tmp", "gains"], writes=["tmp"])
                pg.cp("pool", Xb[:, 0:4, :].rearrange("p a b -> p (a b)"), tmp[:, 0:512], reads=["tmp"], writes=["Xb"])
                for (blk, c0) in ((4, 512), (6, 640)):
                    src = tmp[:, c0:c0 + 128].rearrange("p (g k) -> p g k", g=2).unsqueeze(2).to_broadcast([128, 2, 2, 64])
                    dst = Xb[:, blk:blk + 2, :].rearrange("p g (d k) -> p g d k", d=2)
                    pg.cp("dve", dst, src, reads=["tmp"], writes=["Xb"])
                pg.cp("pool", Xb[:, 8, :], Rs[:, 512:640], reads=[Rk], writes=["Xb"])
                pg.cp("pool", Xb[:, 9, :], Rs[:, 640:768], reads=[Rk], writes=["Xb"])
                for blk in range(10):
                    pg.tr(psX[blk // 4][:, blk % 4, :], Xb[:, blk, :], idb[:], reads=["Xb", "idb"], writes=[("psX", blk // 4)])
                pg.cp("act", qT[:, :, t0:t0 + 128], psX[0], reads=[("psX", 0)], writes=["qT"])
                pg.cp("dve", KsT[:, :, t0:t0 + 128], psX[1][:, 0:2, :], reads=[("psX", 1)], writes=["KsT"])
                pg.cp("dve", KwT[:, :, t0:t0 + 128], psX[1][:, 2:4, :], reads=[("psX", 1)], writes=["KwT"])
                pg.cp("act", kcT2[:, t0:t0 + 128], psX[2][:, 0, :], reads=[("psX", 2)], writes=["kcT2"])
                pg.cp("act", vcT2[:, t0:t0 + 128], psX[2][:, 1, :], reads=[("psX", 2)], writes=["vcT2"])
                pg.cp("pool", Vs[:, i, :, 0:64], Rs[:, 896:1024].rearrange("p (g k) -> p g k", g=2), reads=[Rk, "Vs"], writes=["Vs"])
                pg.cp("pool", Vw[:, i, :, 0:64], Rs[:, 1152:1280].rearrange("p (g k) -> p g k", g=2), reads=[Rk, "Vw"], writes=["Vw"])
                pg.act(GT[:, i, :], Rs[:, 1280:1304], AF.Sigmoid, reads=[Rk], writes=["GT"])
            if getattr(C, "lvl", 9) < 2:
                pg.barrier()
                pg.emit()
                return
            pg.memset("dve", hT[:], 0.0, writes=["hT"])
            pg.memset("dve", kcn2[:], 0.0, writes=["kcn2"])
            for x in range(2):
                for hf in range(2):
                    for l in range(32):
                        pg.mm(psb[:, x * 2 + hf:x * 2 + hf + 1], w1[x][0:64, l, hf * 128:(hf + 1) * 128], posb[0:64, x, l:l + 1],
                              start=(l == 0), stop=(l == 31), reads=[("w1", x), "posb"], writes=["psb"])
            pg.cp("dve", biasS[:], psb[:, 0:4], reads=["psb"], writes=["biasS"])
            for x in range(2):
                srcT = kcT2 if x == 0 else vcT2
                skey = "kcT2" if x == 0 else "vcT2"
                for g in range(2):
                    for hf in range(2):
                        for l in range(32):
                            rhs = dap(srcT[:], g * 64 * T + l, [[T, 64], [16, 255]])
                            pg.mm(psh[:, 0:255], w1[x][g * 64:(g + 1) * 64, l, hf * 128:(hf + 1) * 128], rhs,
                                  start=(l == 0), stop=(l == 31), reads=[("w1", x), skey], writes=["psh"])
                        c = slice(0, 255)
                        pg.act(xb_[:, c], psh[:, c], AF.Identity, reads=["psh", "biasS"], writes=["xb"], bias=biasS[:, x * 2 + hf:x * 2 + hf + 1])
                        pg.tt("pool", x2_[:, c], xb_[:, c], xb_[:, c], ALU.mult, reads=["xb"], writes=["x2"])
                        pg.ts("dve", x2_[:, c], x2_[:, c], 0.044715, 1.0, ALU.mult, ALU.add, reads=["x2"], writes=["x2"])
                        pg.tt("dve", x2_[:, c], x2_[:, c], xb_[:, c], ALU.mult, reads=["x2", "xb"], writes=["x2"])
                        pg.act(x2_[:, c], x2_[:, c], AF.Tanh, reads=["x2"], writes=["x2"], scale=0.7978845608028654)
                        pg.stt("dve", x2_[:, c], x2_[:, c], 1.0, xb_[:, c], ALU.add, ALU.mult, reads=["x2", "xb"], writes=["x2"])
                        pg.ts("dve", hT[:, hf, c], x2_[:, c], 0.5, None, ALU.mult, reads=["x2"], writes=["hT"])
                    for m in range(2):
                        rows = 128 if m == 0 else 127
                        for hf in range(2):
                            pg.mm(pso[0:rows, 0:64], hT[:, hf, m * 128:m * 128 + rows], w2[:, x, hf, :], start=(hf == 0), stop=(hf == 1),
                                  reads=["hT", "w2"], writes=["pso"])
                        if x == 0:
                            pg.cp("act", ksq[0:rows, :], pso[0:rows, 0:64], reads=["pso"], writes=["ksq"])
                            pg.tt("pool", x2_[0:rows, 0:64], ksq[0:rows, :], ksq[0:rows, :], ALU.mult, reads=["ksq", "x2"], writes=["x2"])
                            pg.red("dve", st2[0:rows, 0:1], x2_[0:rows, 0:64], ALU.add, reads=["x2"], writes=["st2a"])
                            pg.act(st2[0:rows, 1:2], st2[0:rows, 0:1], AF.Sqrt, reads=["st2a", "eps"], writes=["st2b"], scale=1.0 / 64, bias=eps[0:rows, 0:1])
                            pg.op("dve", lambda e, rows=rows: e.reciprocal(st2[0:rows, 2:3], st2[0:rows, 1:2]), reads=["st2b"], writes=["st2c"])
                            pg.stt("dve", ksq[0:rows, :], ksq[0:rows, :], st2[0:rows, 2:3], kcg[0:rows, :], ALU.mult, ALU.mult,
                                   reads=["ksq", "st2c", "kcg"], writes=["ksq"])
                            src = ksq[0:rows, :].unsqueeze(1).to_broadcast([rows, 2, 64])
                            pg.cp("dve", kcn2[0:rows, :].rearrange("p (d k) -> p d k", d=2), src, reads=["ksq"], writes=["kcn2"])
                            pg.tr(psk[:, 0:128], kcn2[:, :], idb[:], reads=["kcn2", "idb"], writes=["psk"])
                            pg.cp("act", KcT[:, g, m * 128:(m + 1) * 128], psk[:, 0:128], reads=["psk"], writes=["KcT"])
                        else:
                            pg.cp("act", Vc[0:rows, m, g, 0:64], pso[0:rows, 0:64], reads=["pso", "Vc"], writes=["Vc"])
            for m in range(2):
                for g in range(2):
                    pg.memset("dve", Vc[:, m, g, 64:65], 1.0, writes=["Vc"])
                    pg.cp("dve", Vc[:, m, g, 65:129], ovl[:, m, :], reads=["ovl", "Vc"], writes=["Vc"])
            pg.barrier()
            pg.emit()
        if getattr(C, "lvl", 9) < 3:
            return
        stage3_attn(C, st, qT, KsT, KwT, Vs, Vw, KcT, Vc, GT, idf, idb)


def stage3_attn(C, st, qT, KsT, KwT, Vs, Vw, KcT, Vc, GT, idf, idb):
    nc, pg, dr = C.nc, C.pg, C.dr
    with ExitStack() as sb_:
        sb, ps = _mk(C, sb_)
        cmpb = sb("n_cmpb", [128, 2, T], BF16)
        Esel = sb("n_Esel", [128, 32, 128], BF16)
        causb = sb("n_causb", [128, 4, 512], BF16)
        winb = sb("n_winb", [128, 8, 512], BF16)
        selbT = sb("n_selbT", [128, 2, T], BF16)
        eT = [sb(f"n_eT{i}", [128, 512], BF16) for i in range(4)]
        eT2 = [sb(f"n_eT2{i}", [128, 512], BF16) for i in range(4)]
        Mt = [sb(f"n_Mt{i}", [128, 512], BF16) for i in range(2)]
        rm = [0]
        dq = []
        ocmp = sb("n_ocmp", [128, 4, 8, 64], F32)
        osel = sb("n_osel", [128, 4, 8, 64], F32)
        owin = sb("n_owin", [128, 4, 8, 64], F32)
        den = sb("n_den", [128, 16], F32)
        impw = sb("n_impw", [128, 2, 4, 64], F32)
        score = sb("n_score", [128, 2, 64], F32)
        VM = [sb(f"n_VM{i}", [128, 2, 64], F32) for i in range(2)]
        work = sb("n_work", [128, 2, 64], F32)
        m8 = sb("n_m8", [128, 2, 16], F32)
        thr = sb("n_thr", [128, 2], F32)
        msel = sb("n_msel", [128, 2, 64], F32)
        selb = sb("n_selb", [128, 2, 2, 64], BF16)
        osT = [sb(f"n_osT{i}", [65, 512], F32) for i in range(2)]
        dn2 = sb("n_dn2", [128, 8], F32)
        yb = sb("n_yb", [128, 8, 64], F32)
        yb2 = sb("n_yb2", [128, 8, 64], F32)
        psS = [ps(f"n_psS{i}", [128, 512], F32) for i in range(3)]
        psA = [ps(f"n_psA{i}", [128, 512], F32) for i in range(2)]
        psB = [ps(f"n_psB{i}", [128, 512], F32) for i in range(2)]
        psZ_ = ps("n_psZ", [128, 1024], BF16)
        psZ = psZ_[:, 0:256].rearrange("p (g q) -> p g q", g=2)

        pg.ld(cmpb[:], dr["c_cmpb"][:, :, :], writes=["cmpb"])
        pg.ld(Esel[:], dr["c_esel"][:, :, :], writes=["Esel"])
        pg.ld(causb[:], dr["c_causb"][:, :, :], writes=["causb"])
        pg.ld(winb[:], dr["c_winb"][:, :, :], writes=["winb"])
        rs = [0]
        re = [0]

        def nxt(lst, n):
            v = lst[0]
            lst[0] = (v + 1) % n
            return v

        def qk(h):
            return (h % 2) * 64, h // 2, h // 4

        for Q in range(8):
            tq0 = Q * 512
            for ii in range(4):
                i = Q * 4 + ii
                t0 = i * 128
                s = i % 2
                pg.ld(VM[s][:], dr["c_vmfb"][i, :, :, :], writes=[("VM", s)])
                nm = 2 if i >= 16 else 1
                for h in range(8):
                    base, hp, g = qk(h)
                    h4 = h % 4
                    for m in range(nm):
                        r = nxt(rs, 3)
                        pS = psS[r]
                        pg.mm(pS[:, 0:128], KcT[base:base + 64, g, m * 128:(m + 1) * 128], qT[base:base + 64, hp, t0:t0 + 128],
                              start=True, stop=False, reads=["KcT", "qT"], writes=[("psS", r)])
                        pg.mm(pS[:, 0:128], idb[:, :], cmpb[:, m, t0:t0 + 128], start=False, stop=True,
                              reads=["idb", "cmpb"], writes=[("psS", r)])
                        k = nxt(re, 4)
                        pg.act(eT[k][:, 0:128], pS[:, 0:128], AF.Exp, reads=[("psS", r)], writes=[("eT", k)])
                        def pv(g=g, h4=h4, k=k, m=m, nm=nm):
                            pg.mm(psA[g][:, h4 * 65:h4 * 65 + 65], eT[k][:, 0:128], Vc[:, m, g, 0:65], start=(m == 0), stop=(m == nm - 1),
                                  reads=[("eT", k), "Vc"], writes=[("psA", g)])
                            pg.mm(psB[g][:, h4 * 64:h4 * 64 + 64], eT[k][:, 0:128], Vc[:, m, g, 65:129], start=(m == 0), stop=(m == nm - 1),
                                  reads=[("eT", k), "Vc"], writes=[("psB", g)])
                        dq.append(pv)
                        if len(dq) > 2:
                            dq.pop(0)()
                while dq:
                    dq.pop(0)()
                for g in range(2):
                    A3 = psA[g][:, 0:260].rearrange("p (h c) -> p h c", c=65)
                    B3 = psB[g][:, 0:256].rearrange("p (h c) -> p h c", c=64)
                    dsl = den[:, g * 4:(g + 1) * 4]
                    rsl = den[:, 8 + g * 4:8 + (g + 1) * 4]
                    pg.ts("dve", dsl, A3[:, :, 64], 1e-30, None, ALU.max, reads=[("psA", g)], writes=[("den", g)])
                    pg.op("dve", lambda e, o=rsl, a=dsl: e.reciprocal(o, a), reads=[("den", g)], writes=[("rden", g)])
                    rb = rsl.unsqueeze(2).to_broadcast([128, 4, 64])
                    pg.tt("dve", ocmp[:, ii, g * 4:(g + 1) * 4, :], A3[:, :, 0:64], rb, ALU.mult, reads=[("psA", g), ("rden", g)], writes=["ocmp"])
                    pg.tt("dve", impw[:, g, :, :], B3, rb, ALU.mult, reads=[("psB", g), ("rden", g)], writes=[("impw", g)])
                    pg.red("dve", score[:, g, :], impw[:, g, :, :].rearrange("p h j -> p j h"), ALU.add, reads=[("impw", g)], writes=[("score", g)])
                    vm = dr
                    pg.tt("dve", score[:, g, :], score[:, g, :], VM[s][:, 0, :], ALU.mult, reads=[("score", g), ("VM", s)], writes=[("score", g)])
                    pg.tt("dve", score[:, g, :], score[:, g, :], VM[s][:, 1, :], ALU.add, reads=[("score", g), ("VM", s)], writes=[("score", g)])
                    pg.op("dve", lambda e, g=g: e.max(m8[:, g, 0:8], score[:, g, :]), reads=[("score", g)], writes=[("m8a", g)])
                    pg.op("dve", lambda e, g=g: e.match_replace(work[:, g, :], m8[:, g, 0:8], score[:, g, :], -1e9),
                          reads=[("score", g), ("m8a", g)], writes=[("work", g)])
                    pg.op("dve", lambda e, g=g: e.max(m8[:, g, 8:16], work[:, g, :]), reads=[("work", g)], writes=[("m8b", g)])
                    pg.ts("dve", thr[:, g:g + 1], m8[:, g, 15:16], -0.5, None, ALU.max, reads=[("m8b", g)], writes=[("thr", g)])
                    pg.ts("dve", msel[:, g, :], score[:, g, :], thr[:, g:g + 1], None, ALU.is_ge, reads=[("score", g), ("thr", g)], writes=[("msel", g)])
                    pg.cp("dve", selb[:, g, :, :], msel[:, g, :].unsqueeze(1).to_broadcast([128, 2, 64]), reads=[("msel", g)], writes=[("selb", g)])
                    pg.tr(psZ[:, g, :], selb[:, g, :, :].rearrange("p d j -> p (d j)"), idb[:], reads=[("selb", g), "idb"], writes=["psZ"])
                pg.cp("act", selbT[:, :, t0:t0 + 128], psZ, reads=["psZ"], writes=["selbT"])
            for br in range(2):
                if getattr(C, "lvl", 9) < 4 + br:
                    continue
                dest = osel if br == 0 else owin
                dkey = "osel" if br == 0 else "owin"
                KT = KsT if br == 0 else KwT
                Vv = Vs if br == 0 else Vw
                kts = list(range(0, 4 * Q + 4)) if br == 0 else list(range(max(0, 4 * Q - 4), 4 * Q + 4))
                for g in range(2):
                    O = [psA[0], psA[1], psB[0], psB[1]]
                    okeys = [("psA", 0), ("psA", 1), ("psB", 0), ("psB", 1)]
                    for n_, kt in enumerate(kts):
                        if br == 0:
                            r = nxt(rs, 3)
                            pg.mm(psS[r][:, :], Esel[0:64, kt, :], selbT[0:64, g, tq0:tq0 + 512], reads=["Esel", "selbT"], writes=[("psS", r)])
                            mi = nxt(rm, 2)
                            if kt >= 4 * Q:
                                pg.tt("dve", Mt[mi][:], psS[r][:, :], causb[:, kt - 4 * Q, :], ALU.mult, reads=[("psS", r), "causb"], writes=[("Mt", mi)])
                            else:
                                pg.cp("dve", Mt[mi][:], psS[r][:, :], reads=[("psS", r)], writes=[("Mt", mi)])
                            mask, mkeys = Mt[mi][:], [("Mt", mi)]
                        else:
                            mask, mkeys = winb[:, kt - 4 * Q + 4, :], ["winb"]
                        for h4 in range(4):
                            h = g * 4 + h4
                            base, hp, _g = qk(h)
                            r2 = nxt(rs, 3)
                            pg.mm(psS[r2][:, :], KT[base:base + 64, g, kt * 128:(kt + 1) * 128], qT[base:base + 64, hp, tq0:tq0 + 512],
                                  reads=["qT"], writes=[("psS", r2)])
                            k = nxt(re, 4)
                            pg.act(eT[k][:, :], psS[r2][:, :], AF.Exp, reads=[("psS", r2)], writes=[("eT", k)])
                            pg.tt("dve", eT2[k][:, :], eT[k][:, :], mask, ALU.mult, reads=[("eT", k)] + mkeys, writes=[("eT2", k)])
                            dq.append(lambda h4=h4, kt=kt, k=k, n_=n_, O=O, okeys=okeys, Vv=Vv, g=g, kts=kts: pg.mm(
                                O[h4][0:65, :], Vv[:, kt, g, :], eT2[k][:, :], start=(n_ == 0), stop=(n_ == len(kts) - 1),
                                reads=[("eT2", k)], writes=[okeys[h4]]))
                            if len(dq) > 2:
                                dq.pop(0)()
                    while dq:
                        dq.pop(0)()
                    for h4 in range(4):
                        h = g * 4 + h4
                        o = h4 % 2
                        pg.cp("act", osT[o][:, :], O[h4][0:65, :], reads=[okeys[h4]], writes=[("osT", o)])
                        r3 = nxt(rs, 3)
                        Tp = psS[r3]
                        for qq in range(4):
                            pg.tr(Tp[:, qq * 65:(qq + 1) * 65], osT[o][0:65, qq * 128:(qq + 1) * 128], idf[0:65, 0:65],
                                  reads=[("osT", o), "idf"], writes=[("psS", r3)])
                        T3 = Tp[:, 0:260].rearrange("p (q c) -> p q c", c=65)
                        pg.ts("dve", dn2[:, 0:4], T3[:, :, 64], 1e-30, None, ALU.max, reads=[("psS", r3)], writes=["dn2a"])
                        pg.op("dve", lambda e: e.reciprocal(dn2[:, 4:8], dn2[:, 0:4]), reads=["dn2a"], writes=["dn2b"])
                        pg.tt("dve", dest[:, :, h, :], T3[:, :, 0:64], dn2[:, 4:8].unsqueeze(2).to_broadcast([128, 4, 64]), ALU.mult,
                              reads=[("psS", r3), "dn2b"], writes=[dkey])
            for ii in range(4):
                i = Q * 4 + ii
                t0 = i * 128
                G3 = GT[:, i, :].rearrange("p (h c) -> p h c", c=3)
                gb = lambda c: G3[:, :, c].unsqueeze(2).to_broadcast([128, 8, 64])
                pg.tt("dve", yb[:], ocmp[:, ii, :, :], gb(0), ALU.mult, reads=["ocmp", "GT"], writes=["yb"])
                pg.tt("pool", yb2[:], osel[:, ii, :, :], gb(1), ALU.mult, reads=["osel", "GT"], writes=["yb2"])
                pg.tt("dve", yb[:], yb[:], yb2[:], ALU.add, reads=["yb", "yb2"], writes=["yb"])
                pg.tt("pool", yb2[:], owin[:, ii, :, :], gb(2), ALU.mult, reads=["owin", "GT", "yb2"], writes=["yb2"])
                pg.tt("dve", yb[:], yb[:], yb2[:], ALU.add, reads=["yb", "yb2"], writes=["yb"])
                pg.ld(dr["YB"][t0:t0 + 128, :], yb[:].rearrange("p h k -> p (h k)"), reads=["yb"], writes=[("YB", i)])
        pg.barrier()
        pg.emit()


_NSA_CONSTS = None


def nsa_consts():
    global _NSA_CONSTS
    if _NSA_CONSTS is not None:
        return _NSA_CONSTS
    import ml_dtypes
    bf = ml_dtypes.bfloat16
    c = {}
    n = np.arange(256)
    t = np.arange(T)
    cm = np.where((16 * n[:, None] + 31 <= t[None, :]) & (n[:, None] < 255), 0.0, NEG).astype(np.float32)
    c["c_cmpb"] = np.ascontiguousarray(cm.reshape(2, 128, T).transpose(1, 0, 2)).astype(bf)
    es = np.zeros((64, 32, 128), np.float32)
    for kt in range(32):
        for key in range(128):
            es[2 * kt + key // 64, kt, key] = 1.0
    c["c_esel"] = np.concatenate([es, es], 0).astype(bf)
    key = np.arange(128)
    q = np.arange(512)
    cb = np.zeros((128, 4, 512), np.float32)
    for d in range(4):
        cb[:, d, :] = np.where((d * 128 + key[:, None]) <= q[None, :], 1.0, 0.0)
    c["c_causb"] = cb.astype(bf)
    wb = np.zeros((128, 8, 512), np.float32)
    for r in range(8):
        ka = (r - 4) * 128 + key[:, None]
        wb[:, r, :] = np.where((ka <= q[None, :]) & (ka > q[None, :] - 512), 1.0, 0.0)
    c["c_winb"] = wb.astype(bf)
    cs = np.arange(256) * 16
    ss = np.arange(64) * 64
    ov = np.clip(np.minimum(cs[:, None] + 32, ss[None, :] + 64) - np.maximum(cs[:, None], ss[None, :]), 0, None) / 32.0
    ov[255, :] = 0.0
    c["ovl"] = np.ascontiguousarray(ov.reshape(2, 128, 64).transpose(1, 0, 2)).astype(np.float32)
    cur = t // 64
    j = np.arange(64)
    valid = j[None, :] <= cur[:, None]
    forced = (j[None, :] == 0) | (j[None, :] == cur[:, None]) | (j[None, :] == cur[:, None] - 1)
    vm = valid.astype(np.float32)
    fb = np.where(valid, 1000.0 * forced, -1.0).astype(np.float32)
    c["c_vmfb"] = np.ascontiguousarray(np.stack([vm, fb], 1).reshape(NT, 128, 2, 64))
    _NSA_CONSTS = c
    return c


def stage4(C):
    nc, pg, dr = C.nc, C.pg, C.dr
    with ExitStack() as st:
        sb, ps = _mk(C, st)
        wa = sb("m_wa", [128, 4, D], BF16)
        wb = sb("m_wb", [128, 4, D], BF16)
        wo = sb("m_wo", [128, 8, D], BF16)
        stg = sb("m_stg", [128, D], F32)
        idf = sb("m_idf", [128, 128], F32)
        idb = sb("m_idb", [128, 128], BF16)
        yab = [sb(f"m_yab{i}", [128, 1024], F32) for i in range(2)]
        yabb = sb("m_yabb", [128, 1024], BF16)
        yT = sb("m_yT", [128, 8, 128], BF16)
        gts = [sb(f"m_g{i}", [128, 2048], F32) for i in range(2)]
        xt = [sb(f"m_x{i}", [128, D], F32) for i in range(2)]
        mix = sb("m_mix", [128, D], F32)
        mix2 = sb("m_mix2", [128, D], F32)
        mixb = sb("m_mixb", [128, D], BF16)
        mT = sb("m_mT", [128, 8, 128], BF16)
        x1 = [sb(f"m_x1{i}", [128, D], F32) for i in range(2)]
        psT = ps("m_psT", [128, 1024], BF16)
        psm = [ps(f"m_psm{i}", [128, 512], F32) for i in range(4)]
        psT2 = ps("m_psT2", [128, 1024], BF16)
        pso = [ps(f"m_pso{i}", [128, 512], F32) for i in range(2)]

        pg.ld(idf[:], dr["ident"][:, :], writes=["idf"])
        pg.cp("dve", idb[:], idf[:], reads=["idf"], writes=["idb"])
        n = 0
        for (wt, nm, kcs) in ((wa, "w_branch_a", 4), (wb, "w_branch_b", 4), (wo, "w_out", 8)):
            for kc in range(kcs):
                pg.ld(stg[:], dr[nm][kc * 128:(kc + 1) * 128, :], writes=["stg"])
                pg.cp(("act", "dve", "pool")[n % 3], wt[:, kc, :], stg[:], reads=["stg"], writes=[nm])
                n += 1
        rowidx = sb("m_rowidx", [128, NT], I32)
        pg.ld(rowidx[:], dr["rowidx"][:, :], writes=["rowidx"])
        allk = lambda nm: [(nm, k) for k in range(NT)]
        for i in range(C.peer_tiles):
            s = i % 2
            t0 = i * 128
            ix = rowidx[:, i:i + 1]
            pg.ld(yab[s][:, 0:512], dr["YAL"][t0:t0 + 128, :], reads=[("YAL", i)], writes=[("yab", s)])
            igather(pg, yab[s][:, 512:1024], dr["YB"][:, :], ix, allk("YB") + ["rowidx"], [("yab2", s)])
            igather(pg, gts[s][:], dr["PG"][:, :], ix, allk("PG") + ["rowidx"], [("gts", s)])
            igather(pg, xt[s][:], dr["x"][:, :], ix, ["rowidx"], [("xt", s)])
            pg.cp("pool", yabb[:], yab[s][:], reads=[("yab", s), ("yab2", s)], writes=["yabb"])
            for j in range(8):
                pg.tr(psT[:, j * 128:(j + 1) * 128], yabb[:, j * 128:(j + 1) * 128], idb[:], reads=["yabb", "idb"], writes=["psT"])
            pg.cp("act", yT[:].rearrange("p a b -> p (a b)"), psT[:], reads=["psT"], writes=["yT"])
            for br in range(2):
                wt = wa if br == 0 else wb
                for nchunk in range(2):
                    pb = psm[br * 2 + nchunk]
                    for kc in range(4):
                        pg.mm(pb[:], yT[:, br * 4 + kc, :], wt[:, kc, nchunk * 512:(nchunk + 1) * 512], start=(kc == 0), stop=(kc == 3),
                              reads=["yT", "w_branch_a", "w_branch_b"], writes=[("psm", br * 2 + nchunk)])
            pg.act(gts[s][:], gts[s][:], AF.Sigmoid, reads=[("gts", s)], writes=[("gts", s)])
            for nchunk in range(2):
                c = slice(nchunk * 512, (nchunk + 1) * 512)
                pg.tt("dve", mix[:, c], psm[nchunk][:], gts[s][:, nchunk * 512:(nchunk + 1) * 512], ALU.mult,
                      reads=[("psm", nchunk), ("gts", s)], writes=[("mix", nchunk)])
                pg.tt("dve", mix2[:, c], psm[2 + nchunk][:], gts[s][:, 1024 + nchunk * 512:1024 + (nchunk + 1) * 512], ALU.mult,
                      reads=[("psm", 2 + nchunk), ("gts", s)], writes=[("mix2", nchunk)])
                pg.tt("pool", mixb[:, c], mix[:, c], mix2[:, c], ALU.add, reads=[("mix", nchunk), ("mix2", nchunk)], writes=[("mixb", nchunk)])
            for j in range(8):
                pg.tr(psT2[:, j * 128:(j + 1) * 128], mixb[:, j * 128:(j + 1) * 128], idb[:], reads=[("mixb", 0), ("mixb", 1), "idb"], writes=["psT2"])
            pg.cp("act", mT[:].rearrange("p a b -> p (a b)"), psT2[:], reads=["psT2"], writes=["mT"])
            for nchunk in range(2):
                for kc in range(8):
                    pg.mm(pso[nchunk][:], mT[:, kc, :], wo[:, kc, nchunk * 512:(nchunk + 1) * 512], start=(kc == 0), stop=(kc == 7),
                          reads=["mT", "w_out"], writes=[("pso", nchunk)])
                pg.tt("dve", x1[s][:, nchunk * 512:(nchunk + 1) * 512], pso[nchunk][:], xt[s][:, nchunk * 512:(nchunk + 1) * 512], ALU.add,
                      reads=[("pso", nchunk), ("xt", s)], writes=[("x1", s, nchunk)])
            pg.ld(dr["X1L"][t0:t0 + 128, :], x1[s][:], reads=[("x1", s, 0), ("x1", s, 1)], writes=[("X1L", i)])
        pg.barrier()
        pg.emit()


def table_conv_gen(C, sb):
    pg, dr = C.pg, C.dr
    NBUF = 4
    src = [sb(f"z_src{i}", [128, D], F32) for i in range(NBUF)]
    dst = [sb(f"z_dst{i}", [128, D], BF16) for i in range(NBUF)]
    n = 0
    for (tab, co) in (("peer_u", 0), ("peer_v", D)):
        for a in range(16384 // 128):
            b_ = n % NBUF
            pg.ld(src[b_][:], dr[tab][a * 128:(a + 1) * 128, :], writes=[("zsrc", b_)], q="sp")
            pg.cp("pool", dst[b_][:], src[b_][:], reads=[("zsrc", b_)], writes=[("zdst", b_)])
            pg.ld(dr["UV"][a * 128:(a + 1) * 128, co:co + D], dst[b_][:], reads=[("zdst", b_)], writes=[("UV", co, a)], q="act")
            n += 1
            yield


def stage5(C):
    nc, pg, dr = C.nc, C.pg, C.dr
    NB = 12
    with ExitStack() as st:
        sb, ps = _mk(C, st)
        wq = sb("p_wq", [128, 8, 2048], F32)
        kT = sb("p_kT", [128, 2, 128], F32)
        kraw = sb("p_kraw", [128, 2, 128], F32)
        g2 = sb("p_g2", [128, D], F32)
        idf = sb("p_idf", [128, 128], F32)
        io16 = sb("p_io16", [128, 16], F32)
        eps = sb("p_eps", [128, 1], F32)
        x1 = [sb(f"p_x1{i}", [128, D], F32) for i in range(3)]
        h2 = [sb(f"p_h2{i}", [128, D], F32) for i in range(2)]
        junk = sb("p_junk", [128, D], BF16)
        ss = sb("p_ss", [128, 4], F32)
        h2T = sb("p_h2T", [128, 8, 128], F32)
        qT = sb("p_qT", [128, 16, 128], F32)
        sc = sb("p_sc", [128, 16, 128], F32)
        work = sb("p_work", [128, 256], F32)
        tv = sb("p_tv", [128, 16, 16], F32)
        tiu = sb("p_tiu", [128, 16, 16], U32)
        ti = sb("p_ti", [128, 16, 16], F32)
        cs = sb("p_cs", [128, 8, 256], F32)
        bs = sb("p_bs", [128, 8, 16], F32)
        posu = sb("p_posu", [128, 8, 16], U32)
        pa_u = sb("p_pau", [128, 8, 16], U32)
        pb_u = sb("p_pbu", [128, 8, 16], U32)
        pa = sb("p_pa", [128, 8, 16], F32)
        pb = sb("p_pb", [128, 8, 16], F32)
        oh = sb("p_oh", [128, 8, 16, 16], F32)
        ia = sb("p_ia", [128, 8, 16], F32)
        ib = sb("p_ib", [128, 8, 16], F32)
        eidf = sb("p_eidf", [128, 128], F32)
        eidi = [sb(f"p_eidi{i}", [128, 128], I32) for i in range(3)]
        gate = [sb(f"p_gate{i}", [128, 128], F32) for i in range(2)]
        zz = sb("p_zz", [128, 16], F32)
        actv = [sb(f"p_act{i}", [128, 128], F32) for i in range(2)]
        ga = [sb(f"p_ga{i}", [128, 128], F32) for i in range(2)]
        uv = [sb(f"p_uv{i}", [128, 2 * D], BF16) for i in range(NB)]
        h2b = [sb(f"p_h2b{i}", [128, D], BF16) for i in range(2)]
        idb = sb("p_idb", [128, 128], BF16)
        junk2 = sb("p_junk2", [128, D], F32)
        dg = [sb(f"p_dg{i}", [128, 128], BF16) for i in range(4)]
        yo = [sb(f"p_yo{i}", [128, D], F32) for i in range(1)]
        psT = ps("p_psT", [128, 8, 128], F32)
        psQ = [ps(f"p_psQ{i}", [128, 512], F32) for i in range(2)]
        psY = [ps(f"p_psY{i}", [128, 512], F32) for i in range(2)]

        pg.ld(idf[:], dr["ident"][:, :], writes=["idf"])
        pg.ld(g2[:], dr["norm2_g_b"][:, :], writes=["g2"])
        pg.cp("dve", idb[:], idf[:], reads=["idf"], writes=["idb"])
        pg.ld(io16[:], dr["iota16"][:, :], writes=["io16"])
        rowidx = sb("p_rowidx", [128, NT], I32)
        pg.ld(rowidx[:], dr["rowidx"][:, :], writes=["rowidx"])
        pg.memset("dve", eps[:], 1e-6, writes=["eps"])
        for kc in range(8):
            pg.ld(wq[:, kc, :], dr["peer_wq"][kc * 128:(kc + 1) * 128, :], writes=["wq"])
        pg.ld(kraw[:, 0, :], dr["peer_k1"][:, :], writes=["kraw"])
        pg.ld(kraw[:, 1, :], dr["peer_k2"][:, :], writes=["kraw"])
        for hf in range(2):
            pg.tr(psQ[0][:, hf * 128:(hf + 1) * 128], kraw[:, hf, :], idf[:], reads=["kraw", "idf"], writes=[("psQ", 0)])
        pg.cp("dve", kT[:].rearrange("p a b -> p (a b)"), psQ[0][:, 0:256], reads=[("psQ", 0)], writes=["kT"])
        ntiles = getattr(C, "peer_tiles", NT)

        def front(i):
            s = i % 2
            t0 = i * 128
            pg.ld(x1[i % 3][:, :], dr["X1L"][t0:t0 + 128, :], reads=[("X1L", i)], writes=[("x1", i % 3)])
            yield
            pg.tt("pool", junk2[:], x1[i % 3][:], x1[i % 3][:], ALU.mult, reads=[("x1", i % 3), "junk2"], writes=["junk2"])
            yield
            pg.red("dve", ss[:, 0:1], junk2[:], ALU.add, reads=["junk2"], writes=["ss0"])
            yield
            pg.act(ss[:, 1:2], ss[:, 0:1], AF.Sqrt, reads=["ss0", "eps"], writes=["ss1"], scale=1.0 / D, bias=eps[:, 0:1])
            yield
            pg.op("dve", lambda e: e.reciprocal(ss[:, 2:3], ss[:, 1:2]), reads=["ss1"], writes=["ss2"])
            yield
            pg.stt("dve", h2[s][:], x1[i % 3][:], ss[:, 2:3], g2[:], ALU.mult, ALU.mult, reads=[("x1", i % 3), "ss2", "g2"], writes=[("h2", s)])
            yield
            pg.cp("pool", h2b[s][:], h2[s][:], reads=[("h2", s)], writes=[("h2b", s)])
            yield
            for j in range(8):
                pg.tr(psT[:, j, :], h2[s][:, j * 128:(j + 1) * 128], idf[:], reads=[("h2", s), "idf"], writes=["psT"])
                yield
            pg.cp("act", h2T[:], psT[:], reads=["psT"], writes=["h2T"])
            yield
            for cg in range(4):
                bk = psQ[cg % 2]
                for cc in range(4):
                    c = cg * 4 + cc
                    for kc in range(8):
                        pg.mm(bk[:, cc * 128:(cc + 1) * 128], wq[:, kc, c * 128:(c + 1) * 128], h2T[:, kc, :], start=(kc == 0), stop=(kc == 7),
                              reads=["wq", "h2T"], writes=[("psQ", cg % 2)])
                        yield
                pg.cp("act" if cg % 2 == 0 else "dve", qT[:, cg * 4:(cg + 1) * 4, :].rearrange("p a b -> p (a b)"), bk[:],
                      reads=[("psQ", cg % 2)], writes=[("qT", cg)])
                yield
            for cg in range(4):
                bk = psQ[cg % 2]
                for cc in range(4):
                    c = cg * 4 + cc
                    pg.mm(bk[:, cc * 128:(cc + 1) * 128], qT[:, c, :], kT[:, c % 2, :], reads=[("qT", cg), "kT"], writes=[("psQ", cg % 2)])
                    yield
                pg.cp("act" if cg % 2 == 0 else "dve", sc[:, cg * 4:(cg + 1) * 4, :].rearrange("p a b -> p (a b)"), bk[:],
                      reads=[("psQ", cg % 2)], writes=[("sc", cg)])
                yield
            for c in range(16):
                k_ = ("sc", c // 4)
                pg.op("dve", lambda e, c=c: e.max(tv[:, c, 0:8], sc[:, c, :]), reads=[k_], writes=[("tv", c)])
                yield
                pg.op("dve", lambda e, c=c: e.max_index(tiu[:, c, 0:8], tv[:, c, 0:8], sc[:, c, :]), reads=[k_, ("tv", c)], writes=[("tiu", c)])
                yield
                pg.op("dve", lambda e, c=c: e.match_replace(work[:, 0:128], tv[:, c, 0:8], sc[:, c, :], -1e30), reads=[k_, ("tv", c), "work"], writes=["work"])
                yield
                pg.op("dve", lambda e, c=c: e.max(tv[:, c, 8:16], work[:, 0:128]), reads=["work"], writes=[("tv2", c)])
                yield
                pg.op("dve", lambda e, c=c: e.max_index(tiu[:, c, 8:16], tv[:, c, 8:16], sc[:, c, :]), reads=[k_, ("tv2", c)], writes=[("tiu2", c)])
                yield
            allt = [("tv", c) for c in range(16)] + [("tv2", c) for c in range(16)]
            alli = [("tiu", c) for c in range(16)] + [("tiu2", c) for c in range(16)]
            pg.cp("dve", ti[:], tiu[:], reads=alli, writes=["ti"])
            yield
            tv4 = tv[:].rearrange("p (h f) a -> p h f a", f=2)
            ti4 = ti[:].rearrange("p (h f) a -> p h f a", f=2)
            cs4 = cs[:].rearrange("p h (a b) -> p h a b", a=16)
            A_ = lambda t4: t4[:, :, 0, :].unsqueeze(3).to_broadcast([128, 8, 16, 16])
            B_ = lambda t4: t4[:, :, 1, :].unsqueeze(2).to_broadcast([128, 8, 16, 16])
            pg.tt("dve", cs4, A_(tv4), B_(tv4), ALU.add, reads=allt, writes=["cs"])
            yield
            for h in range(8):
                pg.op("dve", lambda e, h=h: e.max(bs[:, h, 0:8], cs[:, h, :]), reads=["cs"], writes=[("bs", h)])
                yield
                pg.op("dve", lambda e, h=h: e.max_index(posu[:, h, 0:8], bs[:, h, 0:8], cs[:, h, :]), reads=["cs", ("bs", h)], writes=[("posu", h)])
                yield
                pg.op("dve", lambda e, h=h: e.match_replace(work[:, :], bs[:, h, 0:8], cs[:, h, :], -1e30), reads=["cs", ("bs", h), "work"], writes=["work"])
                yield
                pg.op("dve", lambda e, h=h: e.max(bs[:, h, 8:16], work[:, :]), reads=["work"], writes=[("bs2", h)])
                yield
                pg.op("dve", lambda e, h=h: e.max_index(posu[:, h, 8:16], bs[:, h, 8:16], cs[:, h, :]), reads=["cs", ("bs2", h)], writes=[("posu2", h)])
                yield
            allb = [("bs", h) for h in range(8)] + [("bs2", h) for h in range(8)]
            allp = [("posu", h) for h in range(8)] + [("posu2", h) for h in range(8)]
            G = gate[s][:].rearrange("p (h j) -> p h j", h=8)
            pg.tt("dve", G, bs[:], bs[:, :, 0:1].to_broadcast([128, 8, 16]), ALU.subtract, reads=allb, writes=[("gate", s)])
            yield
            pg.act(G, G, AF.Exp, reads=[("gate", s)], writes=[("gate", s)])
            yield
            pg.red("dve", zz[:, 0:8], G, ALU.add, reads=[("gate", s)], writes=["zz0"])
            yield
            pg.op("dve", lambda e: e.reciprocal(zz[:, 8:16], zz[:, 0:8]), reads=["zz0"], writes=["zz1"])
            yield
            pg.tt("dve", G, G, zz[:, 8:16].unsqueeze(2).to_broadcast([128, 8, 16]), ALU.mult, reads=[("gate", s), "zz1"], writes=[("gate", s)])
            yield
            pg.ts("dve", pa_u[:], posu[:], 4, None, ALU.logical_shift_right, reads=allp, writes=["pau"])
            yield
            pg.ts("dve", pb_u[:], posu[:], 15, None, ALU.bitwise_and, reads=allp, writes=["pbu"])
            yield
            pg.cp("dve", pa[:], pa_u[:], reads=["pau"], writes=["pa"])
            yield
            pg.cp("dve", pb[:], pb_u[:], reads=["pbu"], writes=["pb"])
            yield
            iob = io16[:, :].unsqueeze(1).unsqueeze(1).to_broadcast([128, 8, 16, 16])
            for (pp, key, half, dst, dk_) in ((pa, "pa", 0, ia, "ia"), (pb, "pb", 1, ib, "ib")):
                pg.tt("dve", oh[:], pp[:].unsqueeze(3).to_broadcast([128, 8, 16, 16]), iob, ALU.is_equal, reads=[key, "io16", "oh"], writes=["oh"])
                yield
                tsel = ti4[:, :, half, :].unsqueeze(2).to_broadcast([128, 8, 16, 16])
                pg.tt("dve", oh[:], oh[:], tsel, ALU.mult, reads=["oh", "ti"], writes=["oh"])
                yield
                pg.red("dve", dst[:], oh[:], ALU.add, reads=["oh"], writes=[dk_])
                yield
            pg.stt("dve", eidf[:].rearrange("p (h j) -> p h j", h=8), ia[:], 128.0, ib[:], ALU.mult, ALU.add, reads=["ia", "ib"], writes=["eidf"])
            yield
            pg.cp("dve", eidi[i % 3][:], eidf[:], reads=["eidf"], writes=[("eidi", i % 3)])
            yield

        GS = 4

        def gstep(i, e_):
            s = i % 2
            b_ = e_ % NB
            pg.dma("pool", lambda e, e_=e_, b_=b_, i=i: e.indirect_dma_start(
                out=uv[b_][:, :], out_offset=None, in_=dr["UV"][:, :],
                in_offset=bass.IndirectOffsetOnAxis(ap=eidi[i % 3][:, e_:e_ + 1], axis=0)),
                reads=[("eidi", i % 3)], writes=[("uv", b_)])
            pg.op("dve", lambda e, e_=e_, b_=b_, s=s: e.scalar_tensor_tensor(junk[:], uv[b_][:, 0:D], 1.0, h2b[s][:], ALU.mult, ALU.mult,
                                                                              accum_out=actv[s][:, e_:e_ + 1]),
                  reads=[("uv", b_), ("h2b", s)], writes=[("act", s, e_)])

        def gelu_grp(i, k):
            s = i % 2
            sl = slice(k * GS, (k + 1) * GS)
            pg.act(ga[s][:, sl], actv[s][:, sl], AF.Gelu, reads=[("act", s, e_) for e_ in range(k * GS, (k + 1) * GS)], writes=[("ga", s, k)])

        def fin_grp(i, k):
            s = i % 2
            sl = slice(k * GS, (k + 1) * GS)
            pg.tt("dve", ga[s][:, sl], ga[s][:, sl], gate[s][:, sl], ALU.mult, reads=[("ga", s, k), ("gate", s)], writes=[("ga", s, k)])
            for e_ in range(k * GS, (k + 1) * GS):
                b_ = e_ % NB
                d_ = e_ % 4
                pg.act(dg[d_][:], idb[:], AF.Copy, reads=[("ga", s, k), "idb"], writes=[("dg", d_)], scale=ga[s][:, e_:e_ + 1])
                for n_ in range(2):
                    pg.mm(psY[n_][:], dg[d_][:], uv[b_][:, D + n_ * 512:D + (n_ + 1) * 512], start=(e_ == 0), stop=(e_ == 127),
                          reads=[("dg", d_), ("uv", b_)], writes=[("psY", n_)])

        def tail(i):
            s = i % 2
            t0 = i * 128
            for n_ in range(2):
                pg.tt("dve", yo[0][:, n_ * 512:(n_ + 1) * 512], psY[n_][:], x1[i % 3][:, n_ * 512:(n_ + 1) * 512], ALU.add,
                      reads=[("psY", n_), ("x1", i % 3)], writes=[("yo", 0, n_)])
            pg.ld(dr["out"][t0:t0 + 128, :], yo[0][:], reads=[("yo", 0, 0), ("yo", 0, 1)], writes=[("out", i)])

        def drain(g, n=None):
            k = 0
            while g is not None and (n is None or k < n):
                try:
                    next(g)
                except StopIteration:
                    return None
                k += 1
            return g

        drain(front(0))
        for i in range(ntiles):
            gen2 = front(i + 1) if i + 1 < ntiles else None
            for k in range(128 // GS):
                for e_ in range(k * GS, (k + 1) * GS):
                    gstep(i, e_)
                    gen2 = drain(gen2, 3)
                gelu_grp(i, k)
                if k >= 1:
                    fin_grp(i, k - 1)
            fin_grp(i, 128 // GS - 1)
            drain(gen2)
            tail(i)
        pg.barrier()
        pg.emit()


_NC_CACHE = {}


def kernel(**inputs):
    inputs = {k: np.asarray(v) for k, v in inputs.items()}
    ntl = NT // 2
    if "nc" not in _NC_CACHE:
        _NC_CACHE["nc"] = build([stage1, stage2a, stage2x, stage2c, stage3, stage4, stage5], peer_tiles=ntl)
    nc = _NC_CACHE["nc"]
    shared = None
    in_maps = []
    for c in range(8):
        b, hh = c % 4, c // 4
        if shared is None:
            shared = host_inputs(inputs, b, hh, ntl)
            m = shared
        else:
            m = dict(shared)
            m["x"] = np.ascontiguousarray(inputs["x"][b])
            ri = np.zeros((128, NT), np.int32)
            ri[:, :ntl] = (hh * ntl * 128 + np.arange(ntl)[None, :] * 128 + np.arange(128)[:, None]).astype(np.int32)
            m["rowidx"] = ri
        in_maps.append(m)
    res = run_bass_kernel_spmd(nc, in_maps, core_ids=list(range(8)))
    out = np.zeros((4, T, D), np.float32)
    for c in range(8):
        b, hh = c % 4, c // 4
        out[b, hh * ntl * 128:(hh + 1) * ntl * 128, :] = res.results[c]["out"]
    return out


def stage2x(C):
    nc, pg, dr = C.nc, C.pg, C.dr
    with ExitStack() as st:
        sb, ps = _mk(C, st)
        idf = sb("x_idf", [128, 128], F32)
        tri = sb("x_tri", [128, 128], F32)
        msk = sb("x_msk", [128, 3, 128], F32)
        ones = sb("x_ones", [128, 1], F32)
        inp = [[sb(f"x_in{s}_{j}", [128, 512], F32) for j in range(6)] for s in range(2)]
        Pt = sb("x_P", [128, 512], F32)
        iP = sb("x_iP", [128, 512], F32)
        Pp = sb("x_Pp", [128, 512], F32)
        tm = [[sb(f"x_tm{s}_{j}", [128, 512], F32) for j in range(4)] for s in range(2)]
        fm = [[sb(f"x_fm{s}_{j}", [64, 8, 128], F32) for j in range(4)] for s in range(2)]
        M = [[sb(f"x_M{s}_{j}", [128, 8, 128], F32) for j in range(5)] for s in range(2)]
        X = [sb(f"x_X{s}", [128, 8, 128], F32) for s in range(2)]
        PC = [sb(f"x_PC{s}", [64, 8], F32) for s in range(2)]
        N2 = [sb(f"x_N2_{j}", [128, 8, 128], F32) for j in range(2)]
        N2T = [sb(f"x_N2T_{j}", [128, 8, 128], F32) for j in range(2)]
        Z = [sb(f"x_Z{j}", [64, 512], F32) for j in range(2)]
        rhs_sb = sb("x_rhs", [128, 512], F32)
        U_sb = sb("x_U", [128, 512], F32)
        Y_sb = [sb(f"x_Y{j}", [128, 512], F32) for j in range(2)]
        bank = [ps(f"x_bank{j}", [128, 512], F32) for j in range(8)]

        pg.ld(idf[:], dr["ident"][:, :], writes=["idf"])
        pg.ld(tri[:], dr["c_tri"][:, :], writes=["tri"])
        pg.ld(msk[:], dr["c_msk"][:, :, :], writes=["msk"])
        pg.memset("dve", ones[:], 1.0, writes=["ones"])
        pg.memset("dve", Z[0][:], 0.0, writes=[("Z", 0)])
        names = ("RR", "RKK", "RLW", "RB", "RKp", "RV")
        bk = [0]

        def nb():
            v = bk[0]
            bk[0] = (v + 1) % 8
            return v

        def pre(c):
            s = c % 2
            t0 = c * 128
            I = inp[s]
            for j, nm in enumerate(names):
                pg.ld(I[j][:], dr[nm][t0:t0 + 128, :], reads=[(nm, c)], writes=[("in", s, j)])
            r_, kkn, lw, b_, kp, v_ = [t_[:] for t_ in I]
            bL = nb()
            pg.mm(bank[bL][:], tri[:], lw, reads=["tri", ("in", s, 2)], writes=[("bank", bL)])
            bC = nb()
            for h in range(8):
                pg.mm(bank[bC][0:64, h:h + 1], I[2][:, h * 64:(h + 1) * 64], ones[:, 0:1], reads=[("in", s, 2), "ones"], writes=[("bank", bC)])
            pg.act(PC[s][:], bank[bC][0:64, 0:8], AF.Exp, reads=[("bank", bC)], writes=[("PC", s)])
            pg.act(Pt[:], bank[bL][:], AF.Exp, reads=[("bank", bL)], writes=["P"])
            pg.act(iP[:], bank[bL][:], AF.Exp, reads=[("bank", bL)], writes=["iP"], scale=-1.0)
            pg.tt("dve", Pp[:], bank[bL][:], lw, ALU.subtract, reads=[("bank", bL), ("in", s, 2)], writes=["Pp"])
            pg.act(Pp[:], Pp[:], AF.Exp, reads=["Pp"], writes=["Pp"])
            TM = tm[s]
            pg.tt("pool", TM[0][:], r_, Pt[:], ALU.mult, reads=[("in", s, 0), "P"], writes=[("tm", s, 0)])
            pg.stt("dve", TM[1][:], kkn, -1.0, Pp[:], ALU.mult, ALU.mult, reads=[("in", s, 1), "Pp"], writes=[("tm", s, 1)])
            pg.tt("pool", TM[2][:], b_, iP[:], ALU.mult, reads=[("in", s, 3), "iP"], writes=[("tm", s, 2)])
            pg.tt("dve", TM[3][:], kp, iP[:], ALU.mult, reads=[("in", s, 4), "iP"], writes=[("tm", s, 3)])
            for j in range(4):
                for hg in range(2):
                    bT = nb()
                    for hh in range(4):
                        h = hg * 4 + hh
                        pg.tr(bank[bT][0:64, hh * 128:(hh + 1) * 128], TM[j][:, h * 64:(h + 1) * 64], idf[:], reads=[("tm", s, j), "idf"], writes=[("bank", bT)])
                    pg.cp("act" if (j + hg) % 2 == 0 else "dve", fm[s][j][:, hg * 4:(hg + 1) * 4, :].rearrange("p a b -> p (a b)"), bank[bT][0:64, :],
                          reads=[("bank", bT)], writes=[("fm", s, j, hg)])
            FR, FKK, FB, FK = fm[s]
            combos = ((0, FB, 2, FKK, 1, 0), (1, FK, 3, FKK, 1, 0), (2, FB, 2, FR, 0, 1), (3, FK, 3, FR, 0, 1), (4, FKK, 1, FB, 2, 2))
            for hg in range(2):
                for (mi, L_, lj, R_, rj, mk) in combos:
                    bM = nb()
                    for hh in range(4):
                        h = hg * 4 + hh
                        pg.mm(bank[bM][:, hh * 128:(hh + 1) * 128], L_[:, h, :], R_[:, h, :], reads=[("fm", s, lj, hg), ("fm", s, rj, hg)], writes=[("bank", bM)])
                    pg.tt("dve", M[s][mi][:, hg * 4:(hg + 1) * 4, :], bank[bM][:].rearrange("p (a b) -> p a b", a=4),
                          msk[:, mk, :].unsqueeze(1).to_broadcast([128, 4, 128]), ALU.mult, reads=[("bank", bM), "msk"], writes=[("M", s, mi, hg)])
                pg.tt("pool", X[s][:, hg * 4:(hg + 1) * 4, :], idf[:, :].unsqueeze(1).to_broadcast([128, 4, 128]), M[s][0][:, hg * 4:(hg + 1) * 4, :], ALU.subtract,
                      reads=[("M", s, 0, hg), "idf"], writes=[("X", s, hg)])
            curN = [M[s][0], M[s][0]]
            curNT = [M[s][4], M[s][4]]
            kN = [("M", s, 0, 0), ("M", s, 0, 1)]
            kNT = [("M", s, 4, 0), ("M", s, 4, 1)]
            for j in range(6):
                dst = j % 2
                for hg in range(2):
                    b1, b2 = nb(), nb()
                    for hh in range(4):
                        h = hg * 4 + hh
                        pg.mm(bank[b1][:, hh * 128:(hh + 1) * 128], curNT[hg][:, h, :], curN[hg][:, h, :], reads=[kN[hg], kNT[hg]], writes=[("bank", b1)])
                    for hh in range(4):
                        h = hg * 4 + hh
                        pg.mm(bank[b2][:, hh * 128:(hh + 1) * 128], curN[hg][:, h, :], curNT[hg][:, h, :], reads=[kN[hg], kNT[hg]], writes=[("bank", b2)])
                    pg.cp("act", N2[dst][:, hg * 4:(hg + 1) * 4, :].rearrange("p a b -> p (a b)"), bank[b1][:], reads=[("bank", b1)], writes=[("N2", dst, hg)])
                    pg.cp("dve", N2T[dst][:, hg * 4:(hg + 1) * 4, :].rearrange("p a b -> p (a b)"), bank[b2][:], reads=[("bank", b2)], writes=[("N2T", dst, hg)])
                for hg in range(2):
                    curN[hg], curNT[hg] = N2[dst], N2T[dst]
                    kN[hg], kNT[hg] = ("N2", dst, hg), ("N2T", dst, hg)
                for hg in range(2):
                    b3 = nb()
                    for hh in range(4):
                        h = hg * 4 + hh
                        pg.mm(bank[b3][:, hh * 128:(hh + 1) * 128], curNT[hg][:, h, :], X[s][:, h, :], reads=[kNT[hg], ("X", s, hg)], writes=[("bank", b3)])
                    pg.tt("dve", X[s][:, hg * 4:(hg + 1) * 4, :].rearrange("p a b -> p (a b)"), X[s][:, hg * 4:(hg + 1) * 4, :].rearrange("p a b -> p (a b)"), bank[b3][:], ALU.add,
                          reads=[("bank", b3), ("X", s, hg)], writes=[("X", s, hg)])

        def seq(c):
            s = c % 2
            t0 = c * 128
            zc, zn = Z[c % 2], Z[(c + 1) % 2]
            kz, kzn = ("Z", c % 2), ("Z", (c + 1) % 2)
            FR, FKK, FB, FK = fm[s]
            V = inp[s][5]
            hsl = lambda h: slice(h * 64, (h + 1) * 64)
            Mk = lambda mi: [("M", s, mi, 0), ("M", s, mi, 1)]
            fk = lambda j: [("fm", s, j, 0), ("fm", s, j, 1)]
            Xk = [("X", s, 0), ("X", s, 1)]
            bG = nb()
            for h in range(8):
                pg.mm(bank[bG][:, hsl(h)], M[s][1][:, h, :], V[:, hsl(h)], start=True, stop=False, reads=Mk(1) + [("in", s, 5)], writes=[("bank", bG)])
                pg.mm(bank[bG][:, hsl(h)], FKK[:, h, :], zc[:, hsl(h)], start=False, stop=True, reads=fk(1) + [kz], writes=[("bank", bG)])
            pg.ts("dve", rhs_sb[:], bank[bG][:], -1.0, None, ALU.mult, reads=[("bank", bG)], writes=["rhs"])
            bU = nb()
            for h in range(8):
                pg.mm(bank[bU][:, hsl(h)], X[s][:, h, :], rhs_sb[:, hsl(h)], reads=Xk + ["rhs"], writes=[("bank", bU)])
            pg.cp("act", U_sb[:], bank[bU][:], reads=[("bank", bU)], writes=["U"])
            bZ = nb()
            for h in range(8):
                pg.mm(bank[bZ][0:64, hsl(h)], tm[s][3][:, hsl(h)], V[:, hsl(h)], start=True, stop=False, reads=[("tm", s, 3), ("in", s, 5)], writes=[("bank", bZ)])
                pg.mm(bank[bZ][0:64, hsl(h)], idf[0:64, 0:64], zc[:, hsl(h)], start=False, stop=False, reads=["idf", kz], writes=[("bank", bZ)])
                pg.mm(bank[bZ][0:64, hsl(h)], tm[s][2][:, hsl(h)], U_sb[:, hsl(h)], start=False, stop=True, reads=[("tm", s, 2), "U"], writes=[("bank", bZ)])
            pg.tt("dve", zn[:].rearrange("p (h v) -> p h v", h=8), bank[bZ][0:64, :].rearrange("p (h v) -> p h v", h=8),
                  PC[s][:, :].unsqueeze(2).to_broadcast([64, 8, 64]), ALU.mult, reads=[("bank", bZ), ("PC", s)], writes=[kzn])
            bY = nb()
            for h in range(8):
                pg.mm(bank[bY][:, hsl(h)], M[s][3][:, h, :], V[:, hsl(h)], start=True, stop=False, reads=Mk(3) + [("in", s, 5)], writes=[("bank", bY)])
                pg.mm(bank[bY][:, hsl(h)], FR[:, h, :], zc[:, hsl(h)], start=False, stop=False, reads=fk(0) + [kz], writes=[("bank", bY)])
                pg.mm(bank[bY][:, hsl(h)], M[s][2][:, h, :], U_sb[:, hsl(h)], start=False, stop=True, reads=Mk(2) + ["U"], writes=[("bank", bY)])
            pg.cp("act", Y_sb[s][:], bank[bY][:], reads=[("bank", bY)], writes=[("Y", s)])
            pg.ld(dr["YS"][t0:t0 + 128, :], Y_sb[s][:], reads=[("Y", s)], writes=[("YS", c)])

        tcg = table_conv_gen(C, sb)
        pre(0)
        for c in range(NT):
            if c + 1 < NT:
                pre(c + 1)
            for _ in range(8):
                next(tcg, None)
            seq(c)
        for _ in tcg:
            pass
        pg.barrier()
        pg.emit()
```

```python
import numpy as np
import concourse.bass as bass
import concourse.mybir as mybir

F32 = mybir.dt.float32
BF16 = mybir.dt.bfloat16
I32 = mybir.dt.int32
U32 = mybir.dt.uint32
ALU = mybir.AluOpType
AF = mybir.ActivationFunctionType
AX = mybir.AxisListType

EPOCH = 20000
ENGS = ("pe", "act", "dve", "pool", "sp")
NDMASEM = 16


class Prog:
    def __init__(self, nc, stack):
        self.nc = nc
        self.stack = stack
        self.ops = {e: [] for e in ENGS}
        self.cnt = {e: 0 for e in ENGS}
        self.esems = {e: [] for e in ENGS}
        self.waited = {e: {} for e in ENGS}
        self.lastw = {}
        self.readers = {}
        self.dsems = {}
        self.dcount = {}
        self.dtarget = {}
        self.semobjs = {}
        self.alltokens = {}
        for q in ("sp", "act", "pool"):
            self.dsems[q] = [self._newsem(f"d_{q}_{i}") for i in range(NDMASEM)]
            self.dcount[q] = 0
            self.dtarget[q] = [0] * NDMASEM

    def _newsem(self, name):
        s = self.stack.enter_context(self.nc.semaphore(name))
        self.semobjs[name] = s
        return name

    def _esem(self, e, idx):
        ep = idx // EPOCH
        while len(self.esems[e]) <= ep:
            self.esems[e].append(self._newsem(f"e_{e}_{len(self.esems[e])}"))
        return self.esems[e][ep], (idx % EPOCH) + 1

    def _deps(self, reads, writes):
        toks = []
        for k in reads:
            t = self.lastw.get(k)
            if t is not None:
                toks.append(t)
        for k in writes:
            t = self.lastw.get(k)
            if t is not None:
                toks.append(t)
            toks.extend(self.readers.get(k, ()))
        return toks

    def _commit(self, tok, reads, writes):
        for k in reads:
            self.readers.setdefault(k, []).append(tok)
        for k in writes:
            self.lastw[k] = tok
            self.readers[k] = []
        self.alltokens[tok[0]] = max(self.alltokens.get(tok[0], 0), tok[1])

    def _waits(self, e, toks):
        need = {}
        for (s, v) in toks:
            if v > need.get(s, 0):
                need[s] = v
        out = []
        w = self.waited[e]
        for s, v in need.items():
            if w.get(s, 0) < v:
                w[s] = v
                out.append((s, v))
        return out

    def op(self, e, fn, reads=(), writes=()):
        toks = self._deps(reads, writes)
        if e == "pe":
            toks = [t for t in toks if not t[0].startswith("e_pe_")]
        waits = self._waits(e, toks)
        idx = self.cnt[e]
        self.cnt[e] += 1
        tok = self._esem(e, idx)
        self.ops[e].append((waits, fn, (tok[0], 1)))
        self._commit(tok, reads, writes)

    def dma(self, q, fn, reads=(), writes=()):
        toks = self._deps(reads, writes)
        n = self.dcount[q]
        self.dcount[q] += 1
        slot = n % NDMASEM
        sname = self.dsems[q][slot]
        prev = self.dtarget[q][slot]
        if prev > 0:
            toks.append((sname, prev))
        tgt = prev + 16
        self.dtarget[q][slot] = tgt
        waits = self._waits(q, toks)
        tok = (sname, tgt)
        self.ops[q].append((waits, fn, (sname, 16)))
        self._commit(tok, reads, writes)

    def mm(self, out, lhsT, rhs, start=True, stop=True, reads=(), writes=()):
        self.op("pe", lambda e: e.matmul(out, lhsT, rhs, start=start, stop=stop), reads, writes)

    def tr(self, out, in_, ident, reads=(), writes=()):
        self.op("pe", lambda e: e.transpose(out, in_, ident), reads, writes)

    def act(self, out, in_, func, reads=(), writes=(), bias=None, scale=None, eng="act"):
        kw = {}
        if bias is not None:
            kw["bias"] = bias
        if scale is not None:
            kw["scale"] = scale
        self.op(eng, lambda e: e.activation(out, in_, func, **kw), reads, writes)

    def tt(self, eng, out, in0, in1, op, reads=(), writes=()):
        self.op(eng, lambda e: e.tensor_tensor(out, in0, in1, op), reads, writes)

    def ts(self, eng, out, in0, s1, s2, op0, op1=None, reads=(), writes=()):
        if op1 is None:
            self.op(eng, lambda e: e.tensor_scalar(out, in0, s1, s2, op0), reads, writes)
        else:
            self.op(eng, lambda e: e.tensor_scalar(out, in0, s1, s2, op0, op1), reads, writes)

    def stt(self, eng, out, in0, scalar, in1, op0, op1, reads=(), writes=()):
        self.op(eng, lambda e: e.scalar_tensor_tensor(out, in0, scalar, in1, op0, op1), reads, writes)

    def cp(self, eng, out, in_, reads=(), writes=()):
        if eng == "act":
            self.op(eng, lambda e: e.copy(out, in_), reads, writes)
        else:
            self.op(eng, lambda e: e.tensor_copy(out, in_), reads, writes)

    def red(self, eng, out, in_, op, reads=(), writes=(), axis=None):
        ax = AX.X if axis is None else axis
        self.op(eng, lambda e: e.tensor_reduce(out, in_, ax, op), reads, writes)

    def memset(self, eng, ap, val, writes=()):
        self.op(eng, lambda e: e.memset(ap, val), (), writes)

    def ld(self, out, in_, reads=(), writes=(), q="sp"):
        self.dma(q, lambda e: e.dma_start(out, in_), reads, writes)

    def barrier(self):
        toks = list(self.alltokens.items())
        for e in ENGS:
            waits = self._waits(e, toks)
            if waits:
                self.ops[e].append((waits, None, None))
        self.lastw = {}
        self.readers = {}

    def emit(self):
        nc = self.nc
        so = self.semobjs
        with nc.Block() as block:
            def mk(e):
                def body(eng):
                    for waits, fn, inc in self.ops[e]:
                        for (s, v) in waits:
                            eng.wait_ge(so[s], v)
                        if fn is not None:
                            ins = fn(eng)
                            ins.then_inc(so[inc[0]], inc[1])
                return body
            block.tensor(mk("pe"))
            block.scalar(mk("act"))
            block.vector(mk("dve"))
            block.gpsimd(mk("pool"))
            block.sync(mk("sp"))
        self.ops = {e: [] for e in ENGS}
from contextlib import ExitStack
from concourse.bass_utils import run_bass_kernel_spmd

T = 4096
D = 1024
NT = T // 128
INW = 5144
RWC = 1792
O_RW = 0
O_Q = 1792
O_KC = 2304
O_VC = 2432
O_KS = 2560
O_VS = 2688
O_KW = 2816
O_VW = 2944
O_BG = 3072
O_GA = 3096
O_GB = 4120


class Ctx:
    pass


def _mk(C, st):
    nc = C.nc
    sb = lambda name, shape, dt: st.enter_context(nc.sbuf_tensor(name, shape, dt))
    ps = lambda name, shape, dt: st.enter_context(nc.psum_tensor(name, shape, dt))
    return sb, ps


def stage1(C):
    nc, pg, dr = C.nc, C.pg, C.dr
    with ExitStack() as st:
        sb, ps = _mk(C, st)
        win = sb("s1_win", [128, 8, INW], BF16)
        pj = [sb(f"s1_pj{i}", [128, INW], F32) for i in range(2)]
        xt = [sb(f"s1_xt{i}", [128, D], F32) for i in range(2)]
        junk = sb("s1_junk", [128, D], F32)
        hb = [sb(f"s1_h{i}", [128, D], BF16) for i in range(2)]
        hT = [sb(f"s1_hT{i}", [128, 8, 128], BF16) for i in range(2)]
        gt = sb("s1_g", [128, D], F32)
        idf = sb("s1_idf", [128, 128], F32)
        idb = sb("s1_idb", [128, 128], BF16)
        ss = [sb(f"s1_ss{i}", [128, 4], F32) for i in range(2)]
        psT = [ps(f"s1_psT{i}", [128, 8, 128], BF16) for i in range(2)]
        psm = [ps(f"s1_psm{i}", [128, 512], F32) for i in range(4)]

        pg.ld(gt[:], dr["norm1_g_b"][:, :], writes=["gt"])
        pg.ld(idf[:], dr["ident"][:, :], writes=["idf"])
        pg.cp("dve", idb[:], idf[:], reads=["idf"], writes=["idb"])
        engs = ["act", "dve", "pool"]
        for kc in range(8):
            b = pj[kc % 2]
            pg.ld(b[:], dr["w_in"][kc * 128:(kc + 1) * 128, :], writes=[("pjall", kc % 2)])
            pg.cp(engs[kc % 3], win[:, kc, :], b[:], reads=[("pjall", kc % 2)], writes=[("win", kc)])
        winkeys = [("win", kc) for kc in range(8)]
        chunks = []
        c0 = 0
        while c0 < INW:
            w = min(512, INW - c0)
            chunks.append((c0, w))
            c0 += w
        def A1(i):
            s = i % 2
            pg.ld(xt[s][:], dr["x"][i * 128:(i + 1) * 128, :], writes=[("xt", s)])
            pg.tt("dve", junk[:], xt[s][:], xt[s][:], ALU.mult, reads=[("xt", s)], writes=["junk"])
            pg.red("dve", ss[s][:, 0:1], junk[:], ALU.add, reads=["junk"], writes=[("ss", s)])
            pg.act(ss[s][:, 1:2], ss[s][:, 0:1], AF.Sqrt, reads=[("ss", s)], writes=[("ss1", s)],
                   scale=1.0 / D, bias=C.eps6[:, 0:1])
            pg.op("dve", lambda e, o=ss[s][:, 2:3], a=ss[s][:, 1:2]: e.reciprocal(o, a),
                  reads=[("ss1", s)], writes=[("ss2", s)])
            pg.stt("dve", hb[s][:], xt[s][:], ss[s][:, 2:3], gt[:], ALU.mult, ALU.mult,
                   reads=[("xt", s), ("ss2", s), "gt"], writes=[("hb", s)])

        def A2(i):
            s = i % 2
            for j in range(8):
                pg.tr(psT[s][:, j, :], hb[s][:, j * 128:(j + 1) * 128], idb[:],
                      reads=[("hb", s), "idb"], writes=[("psT", s)])
            pg.cp("act", hT[s][:], psT[s][:], reads=[("psT", s)], writes=[("hT", s)])

        def B(i, lo, hi):
            s = i % 2
            for ci in range(lo, hi):
                c0, w = chunks[ci]
                pb = psm[ci % 4]
                for kc in range(8):
                    pg.mm(pb[:, :w], hT[s][:, kc, :], win[:, kc, c0:c0 + w], start=(kc == 0), stop=(kc == 7),
                          reads=[("hT", s), ("win", kc)], writes=[("psm", ci % 4)])
                pg.cp("act" if ci % 2 == 0 else "dve", pj[s][:, c0:c0 + w], pb[:, :w],
                      reads=[("psm", ci % 4)], writes=[("pj", s, ci), ("pjall", s)] if i < 8 else [("pj", s, ci)])

        def S(i):
            s = i % 2
            pg.ld(dr["P"][i * 128:(i + 1) * 128, 0:3096], pj[s][:, 0:3096],
                  reads=[("pj", s, ci) for ci in range(len(chunks))], writes=[("P", i)])
            pg.ld(dr["PG"][i * 128:(i + 1) * 128, :], pj[s][:, 3096:5144],
                  reads=[("pj", s, ci) for ci in range(len(chunks))], writes=[("PG", i)], q="act")

        A1(0)
        A2(0)
        A1(1)
        for i in range(NT):
            B(i, 0, 6)
            if i + 1 < NT:
                A2(i + 1)
            if i + 2 < NT:
                A1(i + 2)
            B(i, 6, len(chunks))
            S(i)
        pg.barrier()
        pg.emit()


def build(stages, dbg_out=(), dbg_in=(), lvl=9, sub=9, peer_tiles=NT):
    nc = bass.Bass("TRN2", target_bir_lowering=False)
    C = Ctx()
    C.peer_tiles = peer_tiles
    C.lvl = lvl
    C.sub = sub
    C.nc = nc
    dr = {}
    C.dr = dr

    def din(name, shape, dt=F32):
        dr[name] = nc.dram_tensor(name, list(shape), dt, kind="ExternalInput").ap()

    def dscr(name, shape, dt=F32):
        kind = "ExternalOutput" if name in dbg_out else ("ExternalInput" if name in dbg_in else "Internal")
        dr[name] = nc.dram_tensor(name, list(shape), dt, kind=kind).ap()

    din("x", [T, D])
    din("norm1_g_b", [128, D])
    din("ident", [128, 128])
    din("w_in", [D, INW])
    dscr("P", [T, INW])
    dscr("PG", [T, 2048])
    for nm in ("rw_mu_b",):
        din(nm, [128, RWC])
    for nm in ("rw_w0_b", "rw_a0_b", "rw_k_k_b", "rw_k_a_b", "rw_r_k_b", "rw_ln_w_b", "rw_ln_b_b", "rw_g_up"):
        din(nm, [128, 512])
    din("rw_w_up", [64, 512])
    din("rw_a_up", [64, 512])
    for nm in ("RB", "RKp", "RV", "RG", "YA", "RR", "RKK", "RLW", "YS"):
        dscr(nm, [T, 512])
    dscr("RBON", [T, 8])
    din("blkmask", [8, 512])
    din("c_tri", [128, 128])
    din("c_msk", [128, 3, 128])
    din("nsa_gains_b", [128, 768])
    din("nsa_kc_g_b", [128, 64])
    din("ovl", [128, 2, 64])
    din("posT", [128, 2, 32])
    din("cmp_w2", [128, 2, 2, 64])
    din("cmp_w1", [2, 128, 32, 256])
    din("c_cmpb", [128, 2, T], BF16)
    din("c_esel", [128, 32, 128], BF16)
    din("c_causb", [128, 4, 512], BF16)
    din("c_winb", [128, 8, 512], BF16)
    din("c_vmfb", [NT, 128, 2, 64])
    dscr("YB", [T, 512])
    din("w_branch_a", [512, D])
    din("w_branch_b", [512, D])
    din("w_out", [D, D])
    dscr("X1L", [peer_tiles * 128, D])
    dscr("YAL", [peer_tiles * 128, 512])
    din("norm2_g_b", [128, D])
    din("iota16", [128, 16])
    din("rowidx", [128, NT], I32)
    din("peer_wq", [D, 2048])
    din("peer_k1", [128, 128])
    din("peer_k2", [128, 128])
    din("peer_u", [16384, D])
    din("peer_v", [16384, D])
    dscr("UV", [16384, 2 * D], BF16)
    dr["out"] = nc.dram_tensor("out", [peer_tiles * 128, D], F32, kind="ExternalOutput").ap()
    with ExitStack() as top:
        pg = Prog(nc, top)
        C.pg = pg
        C.eps6 = top.enter_context(nc.sbuf_tensor("c_eps6", [128, 1], F32))
        pg.memset("dve", C.eps6[:], 1e-6, writes=["eps6"])
        pg.barrier()
        for s in stages:
            s(C)
        pg.barrier()
        pg.emit()
    return nc


def host_inputs(inputs, b, hh=0, ntl=NT):
    g = lambda k: np.ascontiguousarray(inputs[k][0])
    m = {}
    m["x"] = np.ascontiguousarray(inputs["x"][b])
    m["norm1_g_b"] = np.ascontiguousarray(np.broadcast_to(g("norm1_g")[None, :], (128, D)))
    m["ident"] = np.eye(128, dtype=np.float32)
    m["w_in"] = g("w_in")
    bc = lambda a: np.ascontiguousarray(np.broadcast_to(np.asarray(a).reshape(1, -1), (128, a.size)))
    m["rw_mu_b"] = bc(g("rw_mu"))
    for nm in ("rw_w0", "rw_a0", "rw_k_k", "rw_k_a", "rw_r_k", "rw_ln_w", "rw_ln_b"):
        m[nm + "_b"] = bc(g(nm))
    for nm in ("rw_g_up", "rw_w_up", "rw_a_up"):
        m[nm] = g(nm)
    bmk = np.zeros((8, 512), np.float32)
    for h in range(8):
        bmk[h, h * 64:(h + 1) * 64] = 1.0
    m["blkmask"] = bmk
    ii = np.arange(128)
    m["c_tri"] = (ii[:, None] <= ii[None, :]).astype(np.float32)
    m["c_msk"] = np.ascontiguousarray(np.stack([(ii[:, None] < ii[None, :]), (ii[:, None] <= ii[None, :]), (ii[:, None] > ii[None, :])], 1).astype(np.float32))
    m.update(nsa_consts())
    for nm in ("w_branch_a", "w_branch_b", "w_out", "peer_wq", "peer_k1", "peer_k2", "peer_u", "peer_v"):
        m[nm] = g(nm)
    m["norm2_g_b"] = bc(g("norm2_g"))
    ri = np.zeros((128, NT), np.int32)
    ri[:, :ntl] = (hh * ntl * 128 + np.arange(ntl)[None, :] * 128 + np.arange(128)[:, None]).astype(np.int32)
    m["rowidx"] = ri
    m["iota16"] = np.ascontiguousarray(np.broadcast_to(np.arange(16, dtype=np.float32)[None, :], (128, 16)))
    m["nsa_gains_b"] = bc(np.concatenate([np.tile(g("nsa_q_g"), 8), np.tile(g("nsa_ks_g"), 2), np.tile(g("nsa_kw_g"), 2)]))
    m["nsa_kc_g_b"] = bc(g("nsa_kc_g"))
    posT = np.zeros((128, 2, 32), np.float32)
    posT[0:64, 0, :] = g("cmp_pos_k").T
    posT[0:64, 1, :] = g("cmp_pos_v").T
    m["posT"] = posT
    w2 = np.stack([g("cmp_k_w2").reshape(2, 128, 64), g("cmp_v_w2").reshape(2, 128, 64)], 0)
    m["cmp_w2"] = np.ascontiguousarray(w2.transpose(2, 0, 1, 3))
    w1 = []
    for nm in ("cmp_k_w1", "cmp_v_w1"):
        a = g(nm).reshape(32, 64, 256).transpose(1, 0, 2)
        w1.append(np.concatenate([a, a], 0))
    m["cmp_w1"] = np.ascontiguousarray(np.stack(w1, 0))
    return m


def dap(ap, offset, pattern):
    return bass.AP(ap.tensor, offset, [list(p) for p in pattern])


def stage2a(C):
    nc, pg, dr = C.nc, C.pg, C.dr
    with ExitStack() as st:
        sb, ps = _mk(C, st)
        mu = sb("a_mu", [128, RWC], F32)
        w0 = sb("a_w0", [128, 512], F32)
        a0 = sb("a_a0", [128, 512], F32)
        kkc = sb("a_kk", [128, 512], F32)
        kac = sb("a_ka", [128, 512], F32)
        rkc = sb("a_rk", [128, 512], F32)
        wup = sb("a_wup", [128, 512], F32)
        gup = sb("a_gup", [128, 512], F32)
        idf = sb("a_idf", [128, 128], F32)
        p_2 = [sb(f"a_p{i_}", [128, RWC], F32) for i_ in range(2)]
        pv_2 = [sb(f"a_pv{i_}", [128, RWC], F32) for i_ in range(2)]
        pm_2 = [sb(f"a_pm{i_}", [128, RWC], F32) for i_ in range(2)]
        lor_2 = [sb(f"a_lor{i_}", [128, 256], F32) for i_ in range(2)]
        lorT_2 = [sb(f"a_lorT{i_}", [128, 256], F32) for i_ in range(2)]
        wt_2 = [sb(f"a_wt{i_}", [128, 512], F32) for i_ in range(2)]
        lwt_2 = [sb(f"a_lwt{i_}", [128, 512], F32) for i_ in range(2)]
        at_2 = [sb(f"a_at{i_}", [128, 512], F32) for i_ in range(2)]
        gt_2 = [sb(f"a_gt{i_}", [128, 512], F32) for i_ in range(2)]
        kk_2 = [sb(f"a_kkt{i_}", [128, 512], F32) for i_ in range(2)]
        sq_2 = [sb(f"a_sq{i_}", [128, 512], F32) for i_ in range(2)]
        nrm_2 = [sb(f"a_nrm{i_}", [128, 32], F32) for i_ in range(2)]
        kkn_2 = [sb(f"a_kkn{i_}", [128, 512], F32) for i_ in range(2)]
        bt_2 = [sb(f"a_bt{i_}", [128, 512], F32) for i_ in range(2)]
        t1_2 = [sb(f"a_t1{i_}", [128, 512], F32) for i_ in range(2)]
        kp_2 = [sb(f"a_kp{i_}", [128, 512], F32) for i_ in range(2)]
        bon_2 = [sb(f"a_bon{i_}", [128, 8], F32) for i_ in range(2)]
        psl = ps("a_psl", [128, 512], F32)
        psw = ps("a_psw", [128, 512], F32)
        psa = ps("a_psa", [128, 512], F32)
        psg = ps("a_psg", [128, 512], F32)

        for (tile, name) in ((mu, "rw_mu_b"), (w0, "rw_w0_b"), (a0, "rw_a0_b"), (kkc, "rw_k_k_b"),
                             (kac, "rw_k_a_b"), (rkc, "rw_r_k_b"), (gup, "rw_g_up"), (idf, "ident")):
            pg.ld(tile[:], dr[name][:, :], writes=[name])
        pg.ld(wup[0:64, :], dr["rw_w_up"][:, :], writes=["wup0"])
        pg.ld(wup[64:128, :], dr["rw_a_up"][:, :], writes=["wup1"])
        P = dr["P"]
        for i in range(NT):
            t0 = i * 128
            s = i % 2
            p, pv, pm, lor, lorT, wt, lwt, at, gt, kk, sq, nrm, kkn, bt, t1, kp, bon = [t_[s] for t_ in (
                p_2, pv_2, pm_2, lor_2, lorT_2, wt_2, lwt_2, at_2, gt_2, kk_2, sq_2, nrm_2, kkn_2, bt_2, t1_2, kp_2, bon_2)]
            pg.ld(p[:], P[t0:t0 + 128, 0:RWC], reads=[("P", i)], writes=[("p", s)])
            if i == 0:
                pg.memset("dve", pv[0:1, :], 0.0, writes=[("pv0", s)])
                pg.ld(pv[1:128, :], P[0:127, 0:RWC], reads=[("P", 0)], writes=[("pv", s)])
                pvk = [("pv", s), ("pv0", s)]
            else:
                pg.ld(pv[:], P[t0 - 1:t0 + 127, 0:RWC], reads=[("P", i), ("P", i - 1)], writes=[("pv", s), ("pv0", s)])
                pvk = [("pv", s), ("pv0", s)]
            pg.tt("dve", pv[:], pv[:], p[:], ALU.subtract, reads=pvk + [("p", s)], writes=[("pv", s)])
            pg.tt("dve", pv[:], pv[:], mu[:], ALU.mult, reads=[("pv", s), "rw_mu_b"], writes=[("pv", s)])
            pg.tt("dve", pm[:], pv[:], p[:], ALU.add, reads=[("pv", s), ("p", s)], writes=[("pm", s)])
            r_ = pm[:, 0:512]
            k_ = pm[:, 512:1024]
            v_ = pm[:, 1024:1536]
            pg.act(lor[:, 0:64], pm[:, 1536:1600], AF.Tanh, reads=[("pm", s)], writes=[("lor0", s)])
            pg.cp("pool", lor[:, 64:128], pm[:, 1600:1664], reads=[("pm", s)], writes=[("lor1", s)])
            pg.act(lor[:, 128:256], pm[:, 1664:1792], AF.Sigmoid, reads=[("pm", s)], writes=[("lor2", s)])
            pg.tr(psl[:, 0:128], lor[:, 0:128], idf[:], reads=[("lor0", s), ("lor1", s), "ident"], writes=["psl"])
            pg.tr(psl[:, 128:256], lor[:, 128:256], idf[:], reads=[("lor2", s), "ident"], writes=["psl"])
            pg.cp("act", lorT[:], psl[:, 0:256], reads=["psl"], writes=[("lorT", s)])
            pg.mm(psw[:], lorT[0:64, 0:128], wup[0:64, :], reads=[("lorT", s), "wup0"], writes=["psw"])
            pg.mm(psa[:], lorT[64:128, 0:128], wup[64:128, :], reads=[("lorT", s), "wup1"], writes=["psa"])
            pg.mm(psg[:], lorT[:, 128:256], gup[:], reads=[("lorT", s), "rw_g_up"], writes=["psg"])
            pg.tt("dve", wt[:], psw[:], w0[:], ALU.add, reads=["psw", "rw_w0_b"], writes=[("wt", s)])
            pg.act(wt[:], wt[:], AF.Sigmoid, reads=[("wt", s)], writes=[("wt", s)])
            pg.ts("dve", lwt[:], wt[:], -0.6065306597126334, None, ALU.mult, reads=[("wt", s)], writes=[("lwt", s)])
            pg.tt("dve", at[:], psa[:], a0[:], ALU.add, reads=["psa", "rw_a0_b"], writes=[("at", s)])
            pg.act(at[:], at[:], AF.Sigmoid, reads=[("at", s)], writes=[("at", s)])
            pg.cp("act", gt[:], psg[:], reads=["psg"], writes=[("gt", s)])
            pg.tt("dve", kk[:], k_, kkc[:], ALU.mult, reads=[("pm", s), "rw_k_k_b"], writes=[("kk", s)])
            pg.tt("pool", sq[:], kk[:], kk[:], ALU.mult, reads=[("kk", s)], writes=[("sq", s)])
            pg.red("dve", nrm[:, 0:8], sq[:].rearrange("p (h k) -> p h k", h=8), ALU.add, reads=[("sq", s)], writes=[("nrm0", s)])
            pg.act(nrm[:, 8:16], nrm[:, 0:8], AF.Sqrt, reads=[("nrm0", s)], writes=[("nrm1", s)])
            pg.ts("dve", nrm[:, 16:24], nrm[:, 8:16], 1e-12, None, ALU.max, reads=[("nrm1", s)], writes=[("nrm2", s)])
            pg.op("dve", lambda e, nrm=nrm: e.reciprocal(nrm[:, 24:32], nrm[:, 16:24]), reads=[("nrm2", s)], writes=[("nrm3", s)])
            rinv_b = nrm[:, 24:32].unsqueeze(2).to_broadcast([128, 8, 64])
            v3 = lambda tl: tl[:].rearrange("p (h k) -> p h k", h=8)
            pg.stt("dve", v3(kkn), v3(kk), -1.0, rinv_b, ALU.mult, ALU.mult, reads=[("kk", s), ("nrm3", s)], writes=[("kkn", s)])
            pg.stt("dve", bt[:], kkn[:], -1.0, at[:], ALU.mult, ALU.mult, reads=[("kkn", s), ("at", s)], writes=[("bt", s)])
            pg.stt("dve", t1[:], at[:], -1.0, kac[:], ALU.add, ALU.mult, reads=[("at", s), "rw_k_a_b"], writes=[("t1", s)])
            pg.stt("dve", kp[:], t1[:], 1.0, k_, ALU.add, ALU.mult, reads=[("t1", s), ("pm", s)], writes=[("kp", s)])
            pg.tt("pool", sq[:], r_, kp[:], ALU.mult, reads=[("pm", s), ("kp", s), ("sq", s)], writes=[("sq", s)])
            pg.tt("pool", sq[:], sq[:], rkc[:], ALU.mult, reads=[("sq", s), "rw_r_k_b"], writes=[("sq", s)])
            pg.red("dve", bon[:], sq[:].rearrange("p (h k) -> p h k", h=8), ALU.add, reads=[("sq", s)], writes=[("bon", s)])
            pg.ld(dr["RR"][t0:t0 + 128, :], r_, reads=[("pm", s)], writes=[("RR", i)], q="act")
            pg.ld(dr["RKK"][t0:t0 + 128, :], kkn[:], reads=[("kkn", s)], writes=[("RKK", i)], q="act")
            pg.ld(dr["RLW"][t0:t0 + 128, :], lwt[:], reads=[("lwt", s)], writes=[("RLW", i)], q="act")
            pg.ld(dr["RB"][t0:t0 + 128, :], bt[:], reads=[("bt", s)], writes=[("RB", i)], q="act")
            pg.ld(dr["RKp"][t0:t0 + 128, :], kp[:], reads=[("kp", s)], writes=[("RKp", i)], q="act")
            pg.ld(dr["RV"][t0:t0 + 128, :], v_, reads=[("pm", s)], writes=[("RV", i)], q="act")
            pg.ld(dr["RG"][t0:t0 + 128, :], gt[:], reads=[("gt", s)], writes=[("RG", i)], q="act")
            pg.ld(dr["RBON"][t0:t0 + 128, :], bon[:], reads=[("bon", s)], writes=[("RBON", i)], q="act")
        pg.barrier()
        pg.emit()


def igather(pg, out_ap, table_ap, idx_ap, reads, writes):
    pg.dma("pool", lambda e: e.indirect_dma_start(out=out_ap, out_offset=None, in_=table_ap,
                                                   in_offset=bass.IndirectOffsetOnAxis(ap=idx_ap, axis=0)), reads, writes)


def stage2c(C):
    nc, pg, dr = C.nc, C.pg, C.dr
    with ExitStack() as st:
        sb, ps = _mk(C, st)
        lnw = sb("c_lnw", [128, 512], F32)
        lnb = sb("c_lnb", [128, 512], F32)
        eps = sb("c_eps", [128, 1], F32)
        y = [sb(f"c_y{i}", [128, 8, 64], F32) for i in range(2)]
        v = [sb(f"c_v{i}", [128, 8, 64], F32) for i in range(2)]
        g = [sb(f"c_g{i}", [128, 512], F32) for i in range(2)]
        bon = [sb(f"c_bon{i}", [128, 8], F32) for i in range(2)]
        stt_ = [sb(f"c_st{i}", [128, 32], F32) for i in range(2)]
        sq = sb("c_sq", [128, 8, 64], F32)
        pg.ld(lnw[:], dr["rw_ln_w_b"][:, :], writes=["lnw"])
        pg.ld(lnb[:], dr["rw_ln_b_b"][:, :], writes=["lnb"])
        pg.memset("dve", eps[:], 64e-5, writes=["eps"])
        f2 = lambda tl: tl[:].rearrange("p h k -> p (h k)")
        rowidx = sb("c_rowidx", [128, NT], I32)
        pg.ld(rowidx[:], dr["rowidx"][:, :], writes=["rowidx"])
        allk = lambda nm: [(nm, k) for k in range(NT)]
        for i in range(C.peer_tiles):
            s = i % 2
            t0 = i * 128
            yk, vk, gk, bk, sk = ("y", s), ("v", s), ("g", s), ("bon", s), ("st", s)
            ix = rowidx[:, i:i + 1]
            igather(pg, f2(y[s]), dr["YS"][:, :], ix, allk("YS") + ["rowidx"], [yk])
            igather(pg, f2(v[s]), dr["RV"][:, :], ix, allk("RV") + ["rowidx"], [vk])
            igather(pg, g[s][:], dr["RG"][:, :], ix, allk("RG") + ["rowidx"], [gk])
            igather(pg, bon[s][:], dr["RBON"][:, :], ix, allk("RBON") + ["rowidx"], [bk])
            S_ = stt_[s]
            bc = lambda ap: ap.unsqueeze(2).to_broadcast([128, 8, 64])
            pg.red("dve", S_[:, 0:8], y[s][:], ALU.add, reads=[yk], writes=[(sk, 0)])
            pg.ts("dve", S_[:, 8:16], S_[:, 0:8], -1.0 / 64, None, ALU.mult, reads=[(sk, 0)], writes=[(sk, 1)])
            pg.tt("dve", y[s][:], y[s][:], bc(S_[:, 8:16]), ALU.add, reads=[yk, (sk, 1)], writes=[yk])
            pg.tt("pool", sq[:], y[s][:], y[s][:], ALU.mult, reads=[yk], writes=["sq"])
            pg.red("dve", S_[:, 16:24], sq[:], ALU.add, reads=["sq"], writes=[(sk, 2)])
            pg.act(S_[:, 24:32], S_[:, 16:24], AF.Sqrt, reads=[(sk, 2), "eps"], writes=[(sk, 3)], scale=1.0 / 64, bias=eps[:, 0:1])
            pg.op("dve", lambda e, o=S_[:, 16:24], a=S_[:, 24:32]: e.reciprocal(o, a), reads=[(sk, 3)], writes=[(sk, 2)])
            pg.tt("dve", y[s][:], y[s][:], bc(S_[:, 16:24]), ALU.mult, reads=[yk, (sk, 2)], writes=[yk])
            pg.tt("dve", f2(y[s]), f2(y[s]), lnw[:], ALU.mult, reads=[yk, "lnw"], writes=[yk])
            pg.tt("pool", f2(y[s]), f2(y[s]), lnb[:], ALU.add, reads=[yk, "lnb"], writes=[yk])
            pg.tt("pool", v[s][:], v[s][:], bc(bon[s][:, 0:8]), ALU.mult, reads=[vk, bk], writes=[vk])
            pg.tt("dve", y[s][:], y[s][:], v[s][:], ALU.add, reads=[yk, vk], writes=[yk])
            pg.tt("dve", f2(y[s]), f2(y[s]), g[s][:], ALU.mult, reads=[yk, gk], writes=[yk])
            pg.ld(dr["YAL"][t0:t0 + 128, :], f2(y[s]), reads=[yk], writes=[("YAL", i)])
        pg.barrier()
        pg.emit()


NEG = -30000.0


def stage3(C):
    nc, pg, dr = C.nc, C.pg, C.dr
    with ExitStack() as st:
        sb, ps = _mk(C, st)
        qT = sb("n_qT", [128, 4, T], BF16)
        KsT = sb("n_KsT", [128, 2, T], BF16)
        KwT = sb("n_KwT", [128, 2, T], BF16)
        Vs = sb("n_Vs", [128, NT, 2, 65], BF16)
        Vw = sb("n_Vw", [128, NT, 2, 65], BF16)
        KcT = sb("n_KcT", [128, 2, 256], BF16)
        Vc = sb("n_Vc", [128, 2, 2, 129], BF16)
        GT = sb("n_GT", [128, NT, 24], F32)
        idf = sb("n_idf", [128, 128], F32)
        idb = sb("n_idb", [128, 128], BF16)
        eps = sb("n_eps", [128, 1], F32)
        pg.ld(idf[:], dr["ident"][:, :], writes=["idf"])
        pg.cp("dve", idb[:], idf[:], reads=["idf"], writes=["idb"])
        pg.memset("dve", eps[:], 1e-6, writes=["eps"])
        pg.memset("pool", Vs[:], 1.0, writes=["Vs"])
        pg.memset("pool", Vw[:], 1.0, writes=["Vw"])
        pg.memset("pool", Vc[:], 0.0, writes=["Vc"])
        with ExitStack() as sa_:
            sb, ps = _mk(C, sa_)
            kcT2 = sb("n_kcT2", [128, T], BF16)
            vcT2 = sb("n_vcT2", [128, T], BF16)
            w1 = [sb(f"n_w1{i}", [128, 32, 256], BF16) for i in range(2)]
            w1s = sb("n_w1s", [128, 16, 256], F32)
            w2s = sb("n_w2s", [128, 2, 2, 64], F32)
            w2 = sb("n_w2", [128, 2, 2, 64], BF16)
            posf = sb("n_posf", [128, 2, 32], F32)
            posb = sb("n_posb", [128, 2, 32], BF16)
            gains = sb("n_gains", [128, 768], F32)
            kcg = sb("n_kcg", [128, 64], F32)
            ovl = sb("n_ovl", [128, 2, 64], F32)
            R = [sb(f"n_R{i}", [128, 1304], F32) for i in range(2)]
            sq = sb("n_sq", [128, 1280], F32)
            tmp = sb("n_tmp", [128, 768], F32)
            stat = sb("n_stat", [128, 64], F32)
            Xb = sb("n_Xb", [128, 10, 128], BF16)
            biasS = sb("n_biasS", [128, 4], F32)
            xb_ = sb("n_xb", [128, 256], F32)
            x2_ = sb("n_x2", [128, 256], F32)
            hT = sb("n_hT", [128, 2, 256], BF16)
            kcn2 = sb("n_kcn2", [128, 128], BF16)
            st2 = sb("n_st2", [128, 8], F32)
            ksq = sb("n_ksq", [128, 64], F32)
            psX_ = [ps(f"n_psX{i}", [128, 1024], BF16) for i in range(3)]
            psX = [t_[:, 0:512].rearrange("p (a b) -> p a b", a=4) for t_ in psX_]
            psh = ps("n_psh", [128, 512], F32)
            psb = ps("n_psb", [128, 512], F32)
            pso = ps("n_pso", [128, 512], F32)
            psk = ps("n_psk", [128, 1024], BF16)

            pg.ld(gains[:], dr["nsa_gains_b"][:, :], writes=["gains"])
            pg.ts("dve", gains[:, 0:512], gains[:, 0:512], 0.125, None, ALU.mult, reads=["gains"], writes=["gains"])
            pg.ld(kcg[:], dr["nsa_kc_g_b"][:, :], writes=["kcg"])
            pg.ld(ovl[:], dr["ovl"][:, :, :], writes=["ovl"])
            pg.ld(posf[:], dr["posT"][:, :, :], writes=["posf"])
            pg.cp("dve", posb[:], posf[:], reads=["posf"], writes=["posb"])
            pg.ld(w2s[:], dr["cmp_w2"][:, :, :, :], writes=["w2s"])
            pg.cp("dve", w2[:], w2s[:], reads=["w2s"], writes=["w2"])
            for x in range(2):
                for hf in range(2):
                    pg.ld(w1s[:], dr["cmp_w1"][x, :, hf * 16:(hf + 1) * 16, :], writes=["w1s"])
                    pg.cp("pool", w1[x][:, hf * 16:(hf + 1) * 16, :], w1s[:], reads=["w1s"], writes=[("w1", x)])
            for i in range(NT):
                s = i % 2
                t0 = i * 128
                Rk = ("R", s)
                pg.ld(R[s][:], dr["P"][t0:t0 + 128, 1792:3096], reads=[("P", i)], writes=[Rk])
                Rs = R[s]
                pg.tt("pool", sq[:], Rs[:, 0:1280], Rs[:, 0:1280], ALU.mult, reads=[Rk], writes=["sq"])
                pg.red("dve", stat[:, 0:20], sq[:].rearrange("p (a k) -> p a k", k=64), ALU.add, reads=["sq"], writes=["stat0"])
                pg.act(stat[:, 20:40], stat[:, 0:20], AF.Sqrt, reads=["stat0", "eps"], writes=["stat1"], scale=1.0 / 64, bias=eps[:, 0:1])
                pg.op("dve", lambda e: e.reciprocal(stat[:, 40:60], stat[:, 20:40]), reads=["stat1"], writes=["stat2"])
                b3 = lambda ap, n: ap.unsqueeze(2).to_broadcast([128, n, 64])
                v3 = lambda ap: ap.rearrange("p (a k) -> p a k", k=64)
                pg.tt("dve", v3(tmp[:, 0:512]), v3(Rs[:, 0:512]), b3(stat[:, 40:48], 8), ALU.mult, reads=[Rk, "stat2"], writes=["tmp"])
                pg.tt("dve", v3(tmp[:, 512:640]), v3(Rs[:, 768:896]), b3(stat[:, 52:54], 2), ALU.mult, reads=[Rk, "stat2"], writes=["tmp"])
                pg.tt("dve", v3(tmp[:, 640:768]), v3(Rs[:, 1024:1152]), b3(stat[:, 56:58], 2), ALU.mult, reads=[Rk, "stat2"], writes=["tmp"])
                pg.tt("pool", tmp[:], tmp[:], gains[:], ALU.mult, reads=["tmp", "gains"], writes=["tmp"])
                pg.cp("pool", Xb[:, 0:4, :].rearrange("p a b -> p (a b)"), tmp[:, 0:512], reads=["tmp"], writes=["Xb"])
                for (blk, c0) in ((4, 512), (6, 640)):
                    src = tmp[:, c0:c0 + 128].rearrange("p (g k) -> p g k", g=2).unsqueeze(2).to_broadcast([128, 2, 2, 64])
                    dst = Xb[:, blk:blk + 2, :].rearrange("p g (d k) -> p g d k", d=2)
                    pg.cp("dve", dst, src, reads=["tmp"], writes=["Xb"])
                pg.cp("pool", Xb[:, 8, :], Rs[:, 512:640], reads=[Rk], writes=["Xb"])
                pg.cp("pool", Xb[:, 9, :], Rs[:, 640:768], reads=[Rk], writes=["Xb"])
                for blk in range(10):
                    pg.tr(psX[blk // 4][:, blk % 4, :], Xb[:, blk, :], idb[:], reads=["Xb", "idb"], writes=[("psX", blk // 4)])
                pg.cp("act", qT[:, :, t0:t0 + 128], psX[0], reads=[("psX", 0)], writes=["qT"])
                pg.cp("dve", KsT[:, :, t0:t0 + 128], psX[1][:, 0:2, :], reads=[("psX", 1)], writes=["KsT"])
                pg.cp("dve", KwT[:, :, t0:t0 + 128], psX[1][:, 2:4, :], reads=[("psX", 1)], writes=["KwT"])
                pg.cp("act", kcT2[:, t0:t0 + 128], psX[2][:, 0, :], reads=[("psX", 2)], writes=["kcT2"])
                pg.cp("act", vcT2[:, t0:t0 + 128], psX[2][:, 1, :], reads=[("psX", 2)], writes=["vcT2"])
                pg.cp("pool", Vs[:, i, :, 0:64], Rs[:, 896:1024].rearrange("p (g k) -> p g k", g=2), reads=[Rk, "Vs"], writes=["Vs"])
                pg.cp("pool", Vw[:, i, :, 0:64], Rs[:, 1152:1280].rearrange("p (g k) -> p g k", g=2), reads=[Rk, "Vw"], writes=["Vw"])
                pg.act(GT[:, i, :], Rs[:, 1280:1304], AF.Sigmoid, reads=[Rk], writes=["GT"])
            if getattr(C, "lvl", 9) < 2:
                pg.barrier()
                pg.emit()
                return
            pg.memset("dve", hT[:], 0.0, writes=["hT"])
            pg.memset("dve", kcn2[:], 0.0, writes=["kcn2"])
            for x in range(2):
                for hf in range(2):
                    for l in range(32):
                        pg.mm(psb[:, x * 2 + hf:x * 2 + hf + 1], w1[x][0:64, l, hf * 128:(hf + 1) * 128], posb[0:64, x, l:l + 1],
                              start=(l == 0), stop=(l == 31), reads=[("w1", x), "posb"], writes=["psb"])
            pg.cp("dve", biasS[:], psb[:, 0:4], reads=["psb"], writes=["biasS"])
            for x in range(2):
                srcT = kcT2 if x == 0 else vcT2
                skey = "kcT2" if x == 0 else "vcT2"
                for g in range(2):
                    for hf in range(2):
                        for l in range(32):
                            rhs = dap(srcT[:], g * 64 * T + l, [[T, 64], [16, 255]])
                            pg.mm(psh[:, 0:255], w1[x][g * 64:(g + 1) * 64, l, hf * 128:(hf + 1) * 128], rhs,
                                  start=(l == 0), stop=(l == 31), reads=[("w1", x), skey], writes=["psh"])
                        c = slice(0, 255)
                        pg.act(xb_[:, c], psh[:, c], AF.Identity, reads=["psh", "biasS"], writes=["xb"], bias=biasS[:, x * 2 + hf:x * 2 + hf + 1])
                        pg.tt("pool", x2_[:, c], xb_[:, c], xb_[:, c], ALU.mult, reads=["xb"], writes=["x2"])
                        pg.ts("dve", x2_[:, c], x2_[:, c], 0.044715, 1.0, ALU.mult, ALU.add, reads=["x2"], writes=["x2"])
                        pg.tt("dve", x2_[:, c], x2_[:, c], xb_[:, c], ALU.mult, reads=["x2", "xb"], writes=["x2"])
                        pg.act(x2_[:, c], x2_[:, c], AF.Tanh, reads=["x2"], writes=["x2"], scale=0.7978845608028654)
                        pg.stt("dve", x2_[:, c], x2_[:, c], 1.0, xb_[:, c], ALU.add, ALU.mult, reads=["x2", "xb"], writes=["x2"])
                        pg.ts("dve", hT[:, hf, c], x2_[:, c], 0.5, None, ALU.mult, reads=["x2"], writes=["hT"])
                    for m in range(2):
                        rows = 128 if m == 0 else 127
                        for hf in range(2):
                            pg.mm(pso[0:rows, 0:64], hT[:, hf, m * 128:m * 128 + rows], w2[:, x, hf, :], start=(hf == 0), stop=(hf == 1),
                                  reads=["hT", "w2"], writes=["pso"])
                        if x == 0:
                            pg.cp("act", ksq[0:rows, :], pso[0:rows, 0:64], reads=["pso"], writes=["ksq"])
                            pg.tt("pool", x2_[0:rows, 0:64], ksq[0:rows, :], ksq[0:rows, :], ALU.mult, reads=["ksq", "x2"], writes=["x2"])
                            pg.red("dve", st2[0:rows, 0:1], x2_[0:rows, 0:64], ALU.add, reads=["x2"], writes=["st2a"])
                            pg.act(st2[0:rows, 1:2], st2[0:rows, 0:1], AF.Sqrt, reads=["st2a", "eps"], writes=["st2b"], scale=1.0 / 64, bias=eps[0:rows, 0:1])
                            pg.op("dve", lambda e, rows=rows: e.reciprocal(st2[0:rows, 2:3], st2[0:rows, 1:2]), reads=["st2b"], writes=["st2c"])
                            pg.stt("dve", ksq[0:rows, :], ksq[0:rows, :], st2[0:rows, 2:3], kcg[0:rows, :], ALU.mult, ALU.mult,
                                   reads=["ksq", "st2c", "kcg"], writes=["ksq"])
                            src = ksq[0:rows, :].unsqueeze(1).to_broadcast([rows, 2, 64])
                            pg.cp("dve", kcn2[0:rows, :].rearrange("p (d k) -> p d k", d=2), src, reads=["ksq"], writes=["kcn2"])
                            pg.tr(psk[:, 0:128], kcn2[:, :], idb[:], reads=["kcn2", "idb"], writes=["psk"])
                            pg.cp("act", KcT[:, g, m * 128:(m + 1) * 128], psk[:, 0:128], reads=["psk"], writes=["KcT"])
                        else:
                            pg.cp("act", Vc[0:rows, m, g, 0:64], pso[0:rows, 0:64], reads=["pso", "Vc"], writes=["Vc"])
            for m in range(2):
                for g in range(2):
                    pg.memset("dve", Vc[:, m, g, 64:65], 1.0, writes=["Vc"])
                    pg.cp("dve", Vc[:, m, g, 65:129], ovl[:, m, :], reads=["ovl", "Vc"], writes=["Vc"])
            pg.barrier()
            pg.emit()
        if getattr(C, "lvl", 9) < 3:
            return
        stage3_attn(C, st, qT, KsT, KwT, Vs, Vw, KcT, Vc, GT, idf, idb)


def stage3_attn(C, st, qT, KsT, KwT, Vs, Vw, KcT, Vc, GT, idf, idb):
    nc, pg, dr = C.nc, C.pg, C.dr
    with ExitStack() as sb_:
        sb, ps = _mk(C, sb_)
        cmpb = sb("n_cmpb", [128, 2, T], BF16)
        Esel = sb("n_Esel", [128, 32, 128], BF16)
        causb = sb("n_causb", [128, 4, 512], BF16)
        winb = sb("n_winb", [128, 8, 512], BF16)
        selbT = sb("n_selbT", [128, 2, T], BF16)
        eT = [sb(f"n_eT{i}", [128, 512], BF16) for i in range(4)]
        eT2 = [sb(f"n_eT2{i}", [128, 512], BF16) for i in range(4)]
        Mt = [sb(f"n_Mt{i}", [128, 512], BF16) for i in range(2)]
        rm = [0]
        dq = []
        ocmp = sb("n_ocmp", [128, 4, 8, 64], F32)
        osel = sb("n_osel", [128, 4, 8, 64], F32)
        owin = sb("n_owin", [128, 4, 8, 64], F32)
        den = sb("n_den", [128, 16], F32)
        impw = sb("n_impw", [128, 2, 4, 64], F32)
        score = sb("n_score", [128, 2, 64], F32)
        VM = [sb(f"n_VM{i}", [128, 2, 64], F32) for i in range(2)]
        work = sb("n_work", [128, 2, 64], F32)
        m8 = sb("n_m8", [128, 2, 16], F32)
        thr = sb("n_thr", [128, 2], F32)
        msel = sb("n_msel", [128, 2, 64], F32)
        selb = sb("n_selb", [128, 2, 2, 64], BF16)
        osT = [sb(f"n_osT{i}", [65, 512], F32) for i in range(2)]
        dn2 = sb("n_dn2", [128, 8], F32)
        yb = sb("n_yb", [128, 8, 64], F32)
        yb2 = sb("n_yb2", [128, 8, 64], F32)
        psS = [ps(f"n_psS{i}", [128, 512], F32) for i in range(3)]
        psA = [ps(f"n_psA{i}", [128, 512], F32) for i in range(2)]
        psB = [ps(f"n_psB{i}", [128, 512], F32) for i in range(2)]
        psZ_ = ps("n_psZ", [128, 1024], BF16)
        psZ = psZ_[:, 0:256].rearrange("p (g q) -> p g q", g=2)

        pg.ld(cmpb[:], dr["c_cmpb"][:, :, :], writes=["cmpb"])
        pg.ld(Esel[:], dr["c_esel"][:, :, :], writes=["Esel"])
        pg.ld(causb[:], dr["c_causb"][:, :, :], writes=["causb"])
        pg.ld(winb[:], dr["c_winb"][:, :, :], writes=["winb"])
        rs = [0]
        re = [0]

        def nxt(lst, n):
            v = lst[0]
            lst[0] = (v + 1) % n
            return v

        def qk(h):
            return (h % 2) * 64, h // 2, h // 4

        for Q in range(8):
            tq0 = Q * 512
            for ii in range(4):
                i = Q * 4 + ii
                t0 = i * 128
                s = i % 2
                pg.ld(VM[s][:], dr["c_vmfb"][i, :, :, :], writes=[("VM", s)])
                nm = 2 if i >= 16 else 1
                for h in range(8):
                    base, hp, g = qk(h)
                    h4 = h % 4
                    for m in range(nm):
                        r = nxt(rs, 3)
                        pS = psS[r]
                        pg.mm(pS[:, 0:128], KcT[base:base + 64, g, m * 128:(m + 1) * 128], qT[base:base + 64, hp, t0:t0 + 128],
                              start=True, stop=False, reads=["KcT", "qT"], writes=[("psS", r)])
                        pg.mm(pS[:, 0:128], idb[:, :], cmpb[:, m, t0:t0 + 128], start=False, stop=True,
                              reads=["idb", "cmpb"], writes=[("psS", r)])
                        k = nxt(re, 4)
                        pg.act(eT[k][:, 0:128], pS[:, 0:128], AF.Exp, reads=[("psS", r)], writes=[("eT", k)])
                        def pv(g=g, h4=h4, k=k, m=m, nm=nm):
                            pg.mm(psA[g][:, h4 * 65:h4 * 65 + 65], eT[k][:, 0:128], Vc[:, m, g, 0:65], start=(m == 0), stop=(m == nm - 1),
                                  reads=[("eT", k), "Vc"], writes=[("psA", g)])
                            pg.mm(psB[g][:, h4 * 64:h4 * 64 + 64], eT[k][:, 0:128], Vc[:, m, g, 65:129], start=(m == 0), stop=(m == nm - 1),
                                  reads=[("eT", k), "Vc"], writes=[("psB", g)])
                        dq.append(pv)
                        if len(dq) > 2:
                            dq.pop(0)()
                while dq:
                    dq.pop(0)()
                for g in range(2):
                    A3 = psA[g][:, 0:260].rearrange("p (h c) -> p h c", c=65)
                    B3 = psB[g][:, 0:256].rearrange("p (h c) -> p h c", c=64)
                    dsl = den[:, g * 4:(g + 1) * 4]
                    rsl = den[:, 8 + g * 4:8 + (g + 1) * 4]
                    pg.ts("dve", dsl, A3[:, :, 64], 1e-30, None, ALU.max, reads=[("psA", g)], writes=[("den", g)])
                    pg.op("dve", lambda e, o=rsl, a=dsl: e.reciprocal(o, a), reads=[("den", g)], writes=[("rden", g)])
                    rb = rsl.unsqueeze(2).to_broadcast([128, 4, 64])
                    pg.tt("dve", ocmp[:, ii, g * 4:(g + 1) * 4, :], A3[:, :, 0:64], rb, ALU.mult, reads=[("psA", g), ("rden", g)], writes=["ocmp"])
                    pg.tt("dve", impw[:, g, :, :], B3, rb, ALU.mult, reads=[("psB", g), ("rden", g)], writes=[("impw", g)])
                    pg.red("dve", score[:, g, :], impw[:, g, :, :].rearrange("p h j -> p j h"), ALU.add, reads=[("impw", g)], writes=[("score", g)])
                    vm = dr
                    pg.tt("dve", score[:, g, :], score[:, g, :], VM[s][:, 0, :], ALU.mult, reads=[("score", g), ("VM", s)], writes=[("score", g)])
                    pg.tt("dve", score[:, g, :], score[:, g, :], VM[s][:, 1, :], ALU.add, reads=[("score", g), ("VM", s)], writes=[("score", g)])
                    pg.op("dve", lambda e, g=g: e.max(m8[:, g, 0:8], score[:, g, :]), reads=[("score", g)], writes=[("m8a", g)])
                    pg.op("dve", lambda e, g=g: e.match_replace(work[:, g, :], m8[:, g, 0:8], score[:, g, :], -1e9),
                          reads=[("score", g), ("m8a", g)], writes=[("work", g)])
                    pg.op("dve", lambda e, g=g: e.max(m8[:, g, 8:16], work[:, g, :]), reads=[("work", g)], writes=[("m8b", g)])
                    pg.ts("dve", thr[:, g:g + 1], m8[:, g, 15:16], -0.5, None, ALU.max, reads=[("m8b", g)], writes=[("thr", g)])
                    pg.ts("dve", msel[:, g, :], score[:, g, :], thr[:, g:g + 1], None, ALU.is_ge, reads=[("score", g), ("thr", g)], writes=[("msel", g)])
                    pg.cp("dve", selb[:, g, :, :], msel[:, g, :].unsqueeze(1).to_broadcast([128, 2, 64]), reads=[("msel", g)], writes=[("selb", g)])
                    pg.tr(psZ[:, g, :], selb[:, g, :, :].rearrange("p d j -> p (d j)"), idb[:], reads=[("selb", g), "idb"], writes=["psZ"])
                pg.cp("act", selbT[:, :, t0:t0 + 128], psZ, reads=["psZ"], writes=["selbT"])
            for br in range(2):
                if getattr(C, "lvl", 9) < 4 + br:
                    continue
                dest = osel if br == 0 else owin
                dkey = "osel" if br == 0 else "owin"
                KT = KsT if br == 0 else KwT
                Vv = Vs if br == 0 else Vw
                kts = list(range(0, 4 * Q + 4)) if br == 0 else list(range(max(0, 4 * Q - 4), 4 * Q + 4))
                for g in range(2):
                    O = [psA[0], psA[1], psB[0], psB[1]]
                    okeys = [("psA", 0), ("psA", 1), ("psB", 0), ("psB", 1)]
                    for n_, kt in enumerate(kts):
                        if br == 0:
                            r = nxt(rs, 3)
                            pg.mm(psS[r][:, :], Esel[0:64, kt, :], selbT[0:64, g, tq0:tq0 + 512], reads=["Esel", "selbT"], writes=[("psS", r)])
                            mi = nxt(rm, 2)
                            if kt >= 4 * Q:
                                pg.tt("dve", Mt[mi][:], psS[r][:, :], causb[:, kt - 4 * Q, :], ALU.mult, reads=[("psS", r), "causb"], writes=[("Mt", mi)])
                            else:
                                pg.cp("dve", Mt[mi][:], psS[r][:, :], reads=[("psS", r)], writes=[("Mt", mi)])
                            mask, mkeys = Mt[mi][:], [("Mt", mi)]
                        else:
                            mask, mkeys = winb[:, kt - 4 * Q + 4, :], ["winb"]
                        for h4 in range(4):
                            h = g * 4 + h4
                            base, hp, _g = qk(h)
                            r2 = nxt(rs, 3)
                            pg.mm(psS[r2][:, :], KT[base:base + 64, g, kt * 128:(kt + 1) * 128], qT[base:base + 64, hp, tq0:tq0 + 512],
                                  reads=["qT"], writes=[("psS", r2)])
                            k = nxt(re, 4)
                            pg.act(eT[k][:, :], psS[r2][:, :], AF.Exp, reads=[("psS", r2)], writes=[("eT", k)])
                            pg.tt("dve", eT2[k][:, :], eT[k][:, :], mask, ALU.mult, reads=[("eT", k)] + mkeys, writes=[("eT2", k)])
                            dq.append(lambda h4=h4, kt=kt, k=k, n_=n_, O=O, okeys=okeys, Vv=Vv, g=g, kts=kts: pg.mm(
                                O[h4][0:65, :], Vv[:, kt, g, :], eT2[k][:, :], start=(n_ == 0), stop=(n_ == len(kts) - 1),
                                reads=[("eT2", k)], writes=[okeys[h4]]))
                            if len(dq) > 2:
                                dq.pop(0)()
                    while dq:
                        dq.pop(0)()
                    for h4 in range(4):
                        h = g * 4 + h4
                        o = h4 % 2
                        pg.cp("act", osT[o][:, :], O[h4][0:65, :], reads=[okeys[h4]], writes=[("osT", o)])
                        r3 = nxt(rs, 3)
                        Tp = psS[r3]
                        for qq in range(4):
                            pg.tr(Tp[:, qq * 65:(qq + 1) * 65], osT[o][0:65, qq * 128:(qq + 1) * 128], idf[0:65, 0:65],
                                  reads=[("osT", o), "idf"], writes=[("psS", r3)])
                        T3 = Tp[:, 0:260].rearrange("p (q c) -> p q c", c=65)
                        pg.ts("dve", dn2[:, 0:4], T3[:, :, 64], 1e-30, None, ALU.max, reads=[("psS", r3)], writes=["dn2a"])
                        pg.op("dve", lambda e: e.reciprocal(dn2[:, 4:8], dn2[:, 0:4]), reads=["dn2a"], writes=["dn2b"])
                        pg.tt("dve", dest[:, :, h, :], T3[:, :, 0:64], dn2[:, 4:8].unsqueeze(2).to_broadcast([128, 4, 64]), ALU.mult,
                              reads=[("psS", r3), "dn2b"], writes=[dkey])
            for ii in range(4):
                i = Q * 4 + ii
                t0 = i * 128
                G3 = GT[:, i, :].rearrange("p (h c) -> p h c", c=3)
                gb = lambda c: G3[:, :, c].unsqueeze(2).to_broadcast([128, 8, 64])
                pg.tt("dve", yb[:], ocmp[:, ii, :, :], gb(0), ALU.mult, reads=["ocmp", "GT"], writes=["yb"])
                pg.tt("pool", yb2[:], osel[:, ii, :, :], gb(1), ALU.mult, reads=["osel", "GT"], writes=["yb2"])
                pg.tt("dve", yb[:], yb[:], yb2[:], ALU.add, reads=["yb", "yb2"], writes=["yb"])
                pg.tt("pool", yb2[:], owin[:, ii, :, :], gb(2), ALU.mult, reads=["owin", "GT", "yb2"], writes=["yb2"])
                pg.tt("dve", yb[:], yb[:], yb2[:], ALU.add, reads=["yb", "yb2"], writes=["yb"])
                pg.ld(dr["YB"][t0:t0 + 128, :], yb[:].rearrange("p h k -> p (h k)"), reads=["yb"], writes=[("YB", i)])
        pg.barrier()
        pg.emit()


_NSA_CONSTS = None


def nsa_consts():
    global _NSA_CONSTS
    if _NSA_CONSTS is not None:
        return _NSA_CONSTS
    import ml_dtypes
    bf = ml_dtypes.bfloat16
    c = {}
    n = np.arange(256)
    t = np.arange(T)
    cm = np.where((16 * n[:, None] + 31 <= t[None, :]) & (n[:, None] < 255), 0.0, NEG).astype(np.float32)
    c["c_cmpb"] = np.ascontiguousarray(cm.reshape(2, 128, T).transpose(1, 0, 2)).astype(bf)
    es = np.zeros((64, 32, 128), np.float32)
    for kt in range(32):
        for key in range(128):
            es[2 * kt + key // 64, kt, key] = 1.0
    c["c_esel"] = np.concatenate([es, es], 0).astype(bf)
    key = np.arange(128)
    q = np.arange(512)
    cb = np.zeros((128, 4, 512), np.float32)
    for d in range(4):
        cb[:, d, :] = np.where((d * 128 + key[:, None]) <= q[None, :], 1.0, 0.0)
    c["c_causb"] = cb.astype(bf)
    wb = np.zeros((128, 8, 512), np.float32)
    for r in range(8):
        ka = (r - 4) * 128 + key[:, None]
        wb[:, r, :] = np.where((ka <= q[None, :]) & (ka > q[None, :] - 512), 1.0, 0.0)
    c["c_winb"] = wb.astype(bf)
    cs = np.arange(256) * 16
    ss = np.arange(64) * 64
    ov = np.clip(np.minimum(cs[:, None] + 32, ss[None, :] + 64) - np.maximum(cs[:, None], ss[None, :]), 0, None) / 32.0
    ov[255, :] = 0.0
    c["ovl"] = np.ascontiguousarray(ov.reshape(2, 128, 64).transpose(1, 0, 2)).astype(np.float32)
    cur = t // 64
    j = np.arange(64)
    valid = j[None, :] <= cur[:, None]
    forced = (j[None, :] == 0) | (j[None, :] == cur[:, None]) | (j[None, :] == cur[:, None] - 1)
    vm = valid.astype(np.float32)
    fb = np.where(valid, 1000.0 * forced, -1.0).astype(np.float32)
    c["c_vmfb"] = np.ascontiguousarray(np.stack([vm, fb], 1).reshape(NT, 128, 2, 64))
    _NSA_CONSTS = c
    return c


def stage4(C):
    nc, pg, dr = C.nc, C.pg, C.dr
    with ExitStack() as st:
        sb, ps = _mk(C, st)
        wa = sb("m_wa", [128, 4, D], BF16)
        wb = sb("m_wb", [128, 4, D], BF16)
        wo = sb("m_wo", [128, 8, D], BF16)
        stg = sb("m_stg", [128, D], F32)
        idf = sb("m_idf", [128, 128], F32)
        idb = sb("m_idb", [128, 128], BF16)
        yab = [sb(f"m_yab{i}", [128, 1024], F32) for i in range(2)]
        yabb = sb("m_yabb", [128, 1024], BF16)
        yT = sb("m_yT", [128, 8, 128], BF16)
        gts = [sb(f"m_g{i}", [128, 2048], F32) for i in range(2)]
        xt = [sb(f"m_x{i}", [128, D], F32) for i in range(2)]
        mix = sb("m_mix", [128, D], F32)
        mix2 = sb("m_mix2", [128, D], F32)
        mixb = sb("m_mixb", [128, D], BF16)
        mT = sb("m_mT", [128, 8, 128], BF16)
        x1 = [sb(f"m_x1{i}", [128, D], F32) for i in range(2)]
        psT = ps("m_psT", [128, 1024], BF16)
        psm = [ps(f"m_psm{i}", [128, 512], F32) for i in range(4)]
        psT2 = ps("m_psT2", [128, 1024], BF16)
        pso = [ps(f"m_pso{i}", [128, 512], F32) for i in range(2)]

        pg.ld(idf[:], dr["ident"][:, :], writes=["idf"])
        pg.cp("dve", idb[:], idf[:], reads=["idf"], writes=["idb"])
        n = 0
        for (wt, nm, kcs) in ((wa, "w_branch_a", 4), (wb, "w_branch_b", 4), (wo, "w_out", 8)):
            for kc in range(kcs):
                pg.ld(stg[:], dr[nm][kc * 128:(kc + 1) * 128, :], writes=["stg"])
                pg.cp(("act", "dve", "pool")[n % 3], wt[:, kc, :], stg[:], reads=["stg"], writes=[nm])
                n += 1
        rowidx = sb("m_rowidx", [128, NT], I32)
        pg.ld(rowidx[:], dr["rowidx"][:, :], writes=["rowidx"])
        allk = lambda nm: [(nm, k) for k in range(NT)]
        for i in range(C.peer_tiles):
            s = i % 2
            t0 = i * 128
            ix = rowidx[:, i:i + 1]
            pg.ld(yab[s][:, 0:512], dr["YAL"][t0:t0 + 128, :], reads=[("YAL", i)], writes=[("yab", s)])
            igather(pg, yab[s][:, 512:1024], dr["YB"][:, :], ix, allk("YB") + ["rowidx"], [("yab2", s)])
            igather(pg, gts[s][:], dr["PG"][:, :], ix, allk("PG") + ["rowidx"], [("gts", s)])
            igather(pg, xt[s][:], dr["x"][:, :], ix, ["rowidx"], [("xt", s)])
            pg.cp("pool", yabb[:], yab[s][:], reads=[("yab", s), ("yab2", s)], writes=["yabb"])
            for j in range(8):
                pg.tr(psT[:, j * 128:(j + 1) * 128], yabb[:, j * 128:(j + 1) * 128], idb[:], reads=["yabb", "idb"], writes=["psT"])
            pg.cp("act", yT[:].rearrange("p a b -> p (a b)"), psT[:], reads=["psT"], writes=["yT"])
            for br in range(2):
                wt = wa if br == 0 else wb
                for nchunk in range(2):
                    pb = psm[br * 2 + nchunk]
                    for kc in range(4):
                        pg.mm(pb[:], yT[:, br * 4 + kc, :], wt[:, kc, nchunk * 512:(nchunk + 1) * 512], start=(kc == 0), stop=(kc == 3),
                              reads=["yT", "w_branch_a", "w_branch_b"], writes=[("psm", br * 2 + nchunk)])
            pg.act(gts[s][:], gts[s][:], AF.Sigmoid, reads=[("gts", s)], writes=[("gts", s)])
            for nchunk in range(2):
                c = slice(nchunk * 512, (nchunk + 1) * 512)
                pg.tt("dve", mix[:, c], psm[nchunk][:], gts[s][:, nchunk * 512:(nchunk + 1) * 512], ALU.mult,
                      reads=[("psm", nchunk), ("gts", s)], writes=[("mix", nchunk)])
                pg.tt("dve", mix2[:, c], psm[2 + nchunk][:], gts[s][:, 1024 + nchunk * 512:1024 + (nchunk + 1) * 512], ALU.mult,
                      reads=[("psm", 2 + nchunk), ("gts", s)], writes=[("mix2", nchunk)])
                pg.tt("pool", mixb[:, c], mix[:, c], mix2[:, c], ALU.add, reads=[("mix", nchunk), ("mix2", nchunk)], writes=[("mixb", nchunk)])
            for j in range(8):
                pg.tr(psT2[:, j * 128:(j + 1) * 128], mixb[:, j * 128:(j + 1) * 128], idb[:], reads=[("mixb", 0), ("mixb", 1), "idb"], writes=["psT2"])
            pg.cp("act", mT[:].rearrange("p a b -> p (a b)"), psT2[:], reads=["psT2"], writes=["mT"])
            for nchunk in range(2):
                for kc in range(8):
                    pg.mm(pso[nchunk][:], mT[:, kc, :], wo[:, kc, nchunk * 512:(nchunk + 1) * 512], start=(kc == 0), stop=(kc == 7),
                          reads=["mT", "w_out"], writes=[("pso", nchunk)])
                pg.tt("dve", x1[s][:, nchunk * 512:(nchunk + 1) * 512], pso[nchunk][:], xt[s][:, nchunk * 512:(nchunk + 1) * 512], ALU.add,
                      reads=[("pso", nchunk), ("xt", s)], writes=[("x1", s, nchunk)])
            pg.ld(dr["X1L"][t0:t0 + 128, :], x1[s][:], reads=[("x1", s, 0), ("x1", s, 1)], writes=[("X1L", i)])
        pg.barrier()
        pg.emit()


def table_conv_gen(C, sb):
    pg, dr = C.pg, C.dr
    NBUF = 4
    src = [sb(f"z_src{i}", [128, D], F32) for i in range(NBUF)]
    dst = [sb(f"z_dst{i}", [128, D], BF16) for i in range(NBUF)]
    n = 0
    for (tab, co) in (("peer_u", 0), ("peer_v", D)):
        for a in range(16384 // 128):
            b_ = n % NBUF
            pg.ld(src[b_][:], dr[tab][a * 128:(a + 1) * 128, :], writes=[("zsrc", b_)], q="sp")
            pg.cp("pool", dst[b_][:], src[b_][:], reads=[("zsrc", b_)], writes=[("zdst", b_)])
            pg.ld(dr["UV"][a * 128:(a + 1) * 128, co:co + D], dst[b_][:], reads=[("zdst", b_)], writes=[("UV", co, a)], q="act")
            n += 1
            yield


def stage5(C):
    nc, pg, dr = C.nc, C.pg, C.dr
    NB = 12
    with ExitStack() as st:
        sb, ps = _mk(C, st)
        wq = sb("p_wq", [128, 8, 2048], F32)
        kT = sb("p_kT", [128, 2, 128], F32)
        kraw = sb("p_kraw", [128, 2, 128], F32)
        g2 = sb("p_g2", [128, D], F32)
        idf = sb("p_idf", [128, 128], F32)
        io16 = sb("p_io16", [128, 16], F32)
        eps = sb("p_eps", [128, 1], F32)
        x1 = [sb(f"p_x1{i}", [128, D], F32) for i in range(3)]
        h2 = [sb(f"p_h2{i}", [128, D], F32) for i in range(2)]
        junk = sb("p_junk", [128, D], BF16)

        ss = sb("p_ss", [128, 4], F32)
        h2T = sb("p_h2T", [128, 8, 128], F32)
        qT = sb("p_qT", [128, 16, 128], F32)
        sc = sb("p_sc", [128, 16, 128], F32)
        work = sb("p_work", [128, 256], F32)
        tv = sb("p_tv", [128, 16, 16], F32)
        tiu = sb("p_tiu", [128, 16, 16], U32)
        ti = sb("p_ti", [128, 16, 16], F32)
        cs = sb("p_cs", [128, 8, 256], F32)
        bs = sb("p_bs", [128, 8, 16], F32)
        posu = sb("p_posu", [128, 8, 16], U32)
        pa_u = sb("p_pau", [128, 8, 16], U32)
        pb_u = sb("p_pbu", [128, 8, 16], U32)
        pa = sb("p_pa", [128, 8, 16], F32)
        pb = sb("p_pb", [128, 8, 16], F32)
        oh = sb("p_oh", [128, 8, 16, 16], F32)
        ia = sb("p_ia", [128, 8, 16], F32)
        ib = sb("p_ib", [128, 8, 16], F32)
        eidf = sb("p_eidf", [128, 128], F32)
        eidi = [sb(f"p_eidi{i}", [128, 128], I32) for i in range(3)]
        gate = [sb(f"p_gate{i}", [128, 128], F32) for i in range(2)]
        zz = sb("p_zz", [128, 16], F32)
        actv = [sb(f"p_act{i}", [128, 128], F32) for i in range(2)]
        ga = [sb(f"p_ga{i}", [128, 128], F32) for i in range(2)]
        uv = [sb(f"p_uv{i}", [128, 2 * D], BF16) for i in range(NB)]
        h2b = [sb(f"p_h2b{i}", [128, D], BF16) for i in range(2)]
        idb = sb("p_idb", [128, 128], BF16)
        junk2 = sb("p_junk2", [128, D], F32)
        dg = [sb(f"p_dg{i}", [128, 128], BF16) for i in range(4)]
        yo = [sb(f"p_yo{i}", [128, D], F32) for i in range(1)]
        psT = ps("p_psT", [128, 8, 128], F32)
        psQ = [ps(f"p_psQ{i}", [128, 512], F32) for i in range(2)]
        psY = [ps(f"p_psY{i}", [128, 512], F32) for i in range(2)]

        pg.ld(idf[:], dr["ident"][:, :], writes=["idf"])
        pg.ld(g2[:], dr["norm2_g_b"][:, :], writes=["g2"])
        pg.cp("dve", idb[:], idf[:], reads=["idf"], writes=["idb"])
        pg.ld(io16[:], dr["iota16"][:, :], writes=["io16"])
        rowidx = sb("p_rowidx", [128, NT], I32)
        pg.ld(rowidx[:], dr["rowidx"][:, :], writes=["rowidx"])
        pg.memset("dve", eps[:], 1e-6, writes=["eps"])
        for kc in range(8):
            pg.ld(wq[:, kc, :], dr["peer_wq"][kc * 128:(kc + 1) * 128, :], writes=["wq"])
        pg.ld(kraw[:, 0, :], dr["peer_k1"][:, :], writes=["kraw"])
        pg.ld(kraw[:, 1, :], dr["peer_k2"][:, :], writes=["kraw"])
        for hf in range(2):
            pg.tr(psQ[0][:, hf * 128:(hf + 1) * 128], kraw[:, hf, :], idf[:], reads=["kraw", "idf"], writes=[("psQ", 0)])
        pg.cp("dve", kT[:].rearrange("p a b -> p (a b)"), psQ[0][:, 0:256], reads=[("psQ", 0)], writes=["kT"])
        ntiles = getattr(C, "peer_tiles", NT)

        def front(i):
            s = i % 2
            t0 = i * 128
            pg.ld(x1[i % 3][:, :], dr["X1L"][t0:t0 + 128, :], reads=[("X1L", i)], writes=[("x1", i % 3)])
            yield
            pg.tt("pool", junk2[:], x1[i % 3][:], x1[i % 3][:], ALU.mult, reads=[("x1", i % 3), "junk2"], writes=["junk2"])
            yield
            pg.red("dve", ss[:, 0:1], junk2[:], ALU.add, reads=["junk2"], writes=["ss0"])
            yield
            pg.act(ss[:, 1:2], ss[:, 0:1], AF.Sqrt, reads=["ss0", "eps"], writes=["ss1"], scale=1.0 / D, bias=eps[:, 0:1])
            yield
            pg.op("dve", lambda e: e.reciprocal(ss[:, 2:3], ss[:, 1:2]), reads=["ss1"], writes=["ss2"])
            yield
            pg.stt("dve", h2[s][:], x1[i % 3][:], ss[:, 2:3], g2[:], ALU.mult, ALU.mult, reads=[("x1", i % 3), "ss2", "g2"], writes=[("h2", s)])
            yield
            pg.cp("pool", h2b[s][:], h2[s][:], reads=[("h2", s)], writes=[("h2b", s)])
            yield
            for j in range(8):
                pg.tr(psT[:, j, :], h2[s][:, j * 128:(j + 1) * 128], idf[:], reads=[("h2", s), "idf"], writes=["psT"])
                yield
            pg.cp("act", h2T[:], psT[:], reads=["psT"], writes=["h2T"])
            yield
            for cg in range(4):
                bk = psQ[cg % 2]
                for cc in range(4):
                    c = cg * 4 + cc
                    for kc in range(8):
                        pg.mm(bk[:, cc * 128:(cc + 1) * 128], wq[:, kc, c * 128:(c + 1) * 128], h2T[:, kc, :], start=(kc == 0), stop=(kc == 7),
                              reads=["wq", "h2T"], writes=[("psQ", cg % 2)])
                        yield
                pg.cp("act" if cg % 2 == 0 else "dve", qT[:, cg * 4:(cg + 1) * 4, :].rearrange("p a b -> p (a b)"), bk[:],
                      reads=[("psQ", cg % 2)], writes=[("qT", cg)])
                yield
            for cg in range(4):
                bk = psQ[cg % 2]
                for cc in range(4):
                    c = cg * 4 + cc
                    pg.mm(bk[:, cc * 128:(cc + 1) * 128], qT[:, c, :], kT[:, c % 2, :], reads=[("qT", cg), "kT"], writes=[("psQ", cg % 2)])
                    yield
                pg.cp("act" if cg % 2 == 0 else "dve", sc[:, cg * 4:(cg + 1) * 4, :].rearrange("p a b -> p (a b)"), bk[:],
                      reads=[("psQ", cg % 2)], writes=[("sc", cg)])
                yield
            for c in range(16):
                k_ = ("sc", c // 4)
                pg.op("dve", lambda e, c=c: e.max(tv[:, c, 0:8], sc[:, c, :]), reads=[k_], writes=[("tv", c)])
                yield
                pg.op("dve", lambda e, c=c: e.max_index(tiu[:, c, 0:8], tv[:, c, 0:8], sc[:, c, :]), reads=[k_, ("tv", c)], writes=[("tiu", c)])
                yield
                pg.op("dve", lambda e, c=c: e.match_replace(work[:, 0:128], tv[:, c, 0:8], sc[:, c, :], -1e30), reads=[k_, ("tv", c), "work"], writes=["work"])
                yield
                pg.op("dve", lambda e, c=c: e.max(tv[:, c, 8:16], work[:, 0:128]), reads=["work"], writes=[("tv2", c)])
                yield
                pg.op("dve", lambda e, c=c: e.max_index(tiu[:, c, 8:16], tv[:, c, 8:16], sc[:, c, :]), reads=[k_, ("tv2", c)], writes=[("tiu2", c)])
                yield
            allt = [("tv", c) for c in range(16)] + [("tv2", c) for c in range(16)]
            alli = [("tiu", c) for c in range(16)] + [("tiu2", c) for c in range(16)]
            pg.cp("dve", ti[:], tiu[:], reads=alli, writes=["ti"])
            yield
            tv4 = tv[:].rearrange("p (h f) a -> p h f a", f=2)
            ti4 = ti[:].rearrange("p (h f) a -> p h f a", f=2)
            cs4 = cs[:].rearrange("p h (a b) -> p h a b", a=16)
            A_ = lambda t4: t4[:, :, 0, :].unsqueeze(3).to_broadcast([128, 8, 16, 16])
            B_ = lambda t4: t4[:, :, 1, :].unsqueeze(2).to_broadcast([128, 8, 16, 16])
            pg.tt("dve", cs4, A_(tv4), B_(tv4), ALU.add, reads=allt, writes=["cs"])
            yield
            for h in range(8):
                pg.op("dve", lambda e, h=h: e.max(bs[:, h, 0:8], cs[:, h, :]), reads=["cs"], writes=[("bs", h)])
                yield
                pg.op("dve", lambda e, h=h: e.max_index(posu[:, h, 0:8], bs[:, h, 0:8], cs[:, h, :]), reads=["cs", ("bs", h)], writes=[("posu", h)])
                yield
                pg.op("dve", lambda e, h=h: e.match_replace(work[:, :], bs[:, h, 0:8], cs[:, h, :], -1e30), reads=["cs", ("bs", h), "work"], writes=["work"])
                yield
                pg.op("dve", lambda e, h=h: e.max(bs[:, h, 8:16], work[:, :]), reads=["work"], writes=[("bs2", h)])
                yield
                pg.op("dve", lambda e, h=h: e.max_index(posu[:, h, 8:16], bs[:, h, 8:16], cs[:, h, :]), reads=["cs", ("bs2", h)], writes=[("posu2", h)])
                yield
            allb = [("bs", h) for h in range(8)] + [("bs2", h) for h in range(8)]
            allp = [("posu", h) for h in range(8)] + [("posu2", h) for h in range(8)]
            G = gate[s][:].rearrange("p (h j) -> p h j", h=8)
            pg.tt("dve", G, bs[:], bs[:, :, 0:1].to_broadcast([128, 8, 16]), ALU.subtract, reads=allb, writes=[("gate", s)])
            yield
            pg.act(G, G, AF.Exp, reads=[("gate", s)], writes=[("gate", s)])
            yield
            pg.red("dve", zz[:, 0:8], G, ALU.add, reads=[("gate", s)], writes=["zz0"])
            yield
            pg.op("dve", lambda e: e.reciprocal(zz[:, 8:16], zz[:, 0:8]), reads=["zz0"], writes=["zz1"])
            yield
            pg.tt("dve", G, G, zz[:, 8:16].unsqueeze(2).to_broadcast([128, 8, 16]), ALU.mult, reads=[("gate", s), "zz1"], writes=[("gate", s)])
            yield
            pg.ts("dve", pa_u[:], posu[:], 4, None, ALU.logical_shift_right, reads=allp, writes=["pau"])
            yield
            pg.ts("dve", pb_u[:], posu[:], 15, None, ALU.bitwise_and, reads=allp, writes=["pbu"])
            yield
            pg.cp("dve", pa[:], pa_u[:], reads=["pau"], writes=["pa"])
            yield
            pg.cp("dve", pb[:], pb_u[:], reads=["pbu"], writes=["pb"])
            yield
            iob = io16[:, :].unsqueeze(1).unsqueeze(1).to_broadcast([128, 8, 16, 16])
            for (pp, key, half, dst, dk_) in ((pa, "pa", 0, ia, "ia"), (pb, "pb", 1, ib, "ib")):
                pg.tt("dve", oh[:], pp[:].unsqueeze(3).to_broadcast([128, 8, 16, 16]), iob, ALU.is_equal, reads=[key, "io16", "oh"], writes=["oh"])
                yield
                tsel = ti4[:, :, half, :].unsqueeze(2).to_broadcast([128, 8, 16, 16])
                pg.tt("dve", oh[:], oh[:], tsel, ALU.mult, reads=["oh", "ti"], writes=["oh"])
                yield
                pg.red("dve", dst[:], oh[:], ALU.add, reads=["oh"], writes=[dk_])
                yield
            pg.stt("dve", eidf[:].rearrange("p (h j) -> p h j", h=8), ia[:], 128.0, ib[:], ALU.mult, ALU.add, reads=["ia", "ib"], writes=["eidf"])
            yield
            pg.cp("dve", eidi[i % 3][:], eidf[:], reads=["eidf"], writes=[("eidi", i % 3)])
            yield

        GS = 4

        def gstep(i, e_):
            s = i % 2
            b_ = e_ % NB
            pg.dma("pool", lambda e, e_=e_, b_=b_, i=i: e.indirect_dma_start(
                out=uv[b_][:, :], out_offset=None, in_=dr["UV"][:, :],
                in_offset=bass.IndirectOffsetOnAxis(ap=eidi[i % 3][:, e_:e_ + 1], axis=0)),
                reads=[("eidi", i % 3)], writes=[("uv", b_)])
            pg.op("dve", lambda e, e_=e_, b_=b_, s=s: e.scalar_tensor_tensor(junk[:], uv[b_][:, 0:D], 1.0, h2b[s][:], ALU.mult, ALU.mult,
                                                                              accum_out=actv[s][:, e_:e_ + 1]),
                  reads=[("uv", b_), ("h2b", s)], writes=[("act", s, e_)])

        def gelu_grp(i, k):
            s = i % 2
            sl = slice(k * GS, (k + 1) * GS)
            pg.act(ga[s][:, sl], actv[s][:, sl], AF.Gelu, reads=[("act", s, e_) for e_ in range(k * GS, (k + 1) * GS)], writes=[("ga", s, k)])

        def fin_grp(i, k):
            s = i % 2
            sl = slice(k * GS, (k + 1) * GS)
            pg.tt("dve", ga[s][:, sl], ga[s][:, sl], gate[s][:, sl], ALU.mult, reads=[("ga", s, k), ("gate", s)], writes=[("ga", s, k)])
            for e_ in range(k * GS, (k + 1) * GS):
                b_ = e_ % NB
                d_ = e_ % 4
                pg.act(dg[d_][:], idb[:], AF.Copy, reads=[("ga", s, k), "idb"], writes=[("dg", d_)], scale=ga[s][:, e_:e_ + 1])
                for n_ in range(2):
                    pg.mm(psY[n_][:], dg[d_][:], uv[b_][:, D + n_ * 512:D + (n_ + 1) * 512], start=(e_ == 0), stop=(e_ == 127),
                          reads=[("dg", d_), ("uv", b_)], writes=[("psY", n_)])

        def tail(i):
            s = i % 2
            t0 = i * 128
            for n_ in range(2):
                pg.tt("dve", yo[0][:, n_ * 512:(n_ + 1) * 512], psY[n_][:], x1[i % 3][:, n_ * 512:(n_ + 1) * 512], ALU.add,
                      reads=[("psY", n_), ("x1", i % 3)], writes=[("yo", 0, n_)])
            pg.ld(dr["out"][t0:t0 + 128, :], yo[0][:], reads=[("yo", 0, 0), ("yo", 0, 1)], writes=[("out", i)])

        def drain(g, n=None):
            k = 0
            while g is not None and (n is None or k < n):
                try:
                    next(g)
                except StopIteration:
                    return None
                k += 1
            return g

        drain(front(0))
        for i in range(ntiles):
            gen2 = front(i + 1) if i + 1 < ntiles else None
            for k in range(128 // GS):
                for e_ in range(k * GS, (k + 1) * GS):
                    gstep(i, e_)
                    gen2 = drain(gen2, 3)
                gelu_grp(i, k)
                if k >= 1:
                    fin_grp(i, k - 1)
            fin_grp(i, 128 // GS - 1)
            drain(gen2)
            tail(i)
        pg.barrier()
        pg.emit()


_NC_CACHE = {}


def kernel(**inputs):
    inputs = {k: np.asarray(v) for k, v in inputs.items()}
    ntl = NT // 2
    if "nc" not in _NC_CACHE:
        _NC_CACHE["nc"] = build([stage1, stage2a, stage2x, stage2c, stage3, stage4, stage5], peer_tiles=ntl)
    nc = _NC_CACHE["nc"]
    shared = None
    in_maps = []
    for c in range(8):
        b, hh = c % 4, c // 4
        if shared is None:
            shared = host_inputs(inputs, b, hh, ntl)
            m = shared
        else:
            m = dict(shared)
            m["x"] = np.ascontiguousarray(inputs["x"][b])
            ri = np.zeros((128, NT), np.int32)
            ri[:, :ntl] = (hh * ntl * 128 + np.arange(ntl)[None, :] * 128 + np.arange(128)[:, None]).astype(np.int32)
            m["rowidx"] = ri
        in_maps.append(m)
    res = run_bass_kernel_spmd(nc, in_maps, core_ids=list(range(8)))
    out = np.zeros((4, T, D), np.float32)
    for c in range(8):
        b, hh = c % 4, c // 4
        out[b, hh * ntl * 128:(hh + 1) * ntl * 128, :] = res.results[c]["out"]
    return out


def stage2x(C):
    nc, pg, dr = C.nc, C.pg, C.dr
    with ExitStack() as st:
        sb, ps = _mk(C, st)
        idf = sb("x_idf", [128, 128], F32)
        tri = sb("x_tri", [128, 128], F32)
        msk = sb("x_msk", [128, 3, 128], F32)
        ones = sb("x_ones", [128, 1], F32)
        inp = [[sb(f"x_in{s}_{j}", [128, 512], F32) for j in range(6)] for s in range(2)]
        Pt = sb("x_P", [128, 512], F32)
        iP = sb("x_iP", [128, 512], F32)
        Pp = sb("x_Pp", [128, 512], F32)
        tm = [[sb(f"x_tm{s}_{j}", [128, 512], F32) for j in range(4)] for s in range(2)]
        fm = [[sb(f"x_fm{s}_{j}", [64, 8, 128], F32) for j in range(4)] for s in range(2)]
        M = [[sb(f"x_M{s}_{j}", [128, 8, 128], (BF16 if j in (0, 4) else F32)) for j in range(5)] for s in range(2)]
        Xb = sb("x_Xb", [128, 8, 128], BF16)
        idb = sb("x_idb", [128, 128], BF16)
        X = [sb(f"x_X{s}", [128, 8, 128], F32) for s in range(2)]
        PC = [sb(f"x_PC{s}", [64, 8], F32) for s in range(2)]
        N2 = [sb(f"x_N2_{j}", [128, 8, 128], BF16) for j in range(2)]
        N2T = [sb(f"x_N2T_{j}", [128, 8, 128], BF16) for j in range(2)]
        Z = [sb(f"x_Z{j}", [64, 512], F32) for j in range(2)]
        rhs_sb = sb("x_rhs", [128, 512], F32)
        U_sb = sb("x_U", [128, 512], F32)
        Y_sb = [sb(f"x_Y{j}", [128, 512], F32) for j in range(2)]
        bank = [ps(f"x_bank{j}", [128, 512], F32) for j in range(8)]

        pg.ld(idf[:], dr["ident"][:, :], writes=["idf"])
        pg.ld(tri[:], dr["c_tri"][:, :], writes=["tri"])
        pg.ld(msk[:], dr["c_msk"][:, :, :], writes=["msk"])
        pg.memset("dve", ones[:], 1.0, writes=["ones"])
        pg.cp("dve", idb[:], idf[:], reads=["idf"], writes=["idb"])
        pg.memset("dve", Z[0][:], 0.0, writes=[("Z", 0)])
        names = ("RR", "RKK", "RLW", "RB", "RKp", "RV")
        bk = [0]

        def nb():
            v = bk[0]
            bk[0] = (v + 1) % 8
            return v

        def pre(c):
            s = c % 2
            t0 = c * 128
            I = inp[s]
            for j, nm in enumerate(names):
                pg.ld(I[j][:], dr[nm][t0:t0 + 128, :], reads=[(nm, c)], writes=[("in", s, j)])
            r_, kkn, lw, b_, kp, v_ = [t_[:] for t_ in I]
            bL = nb()
            pg.mm(bank[bL][:], tri[:], lw, reads=["tri", ("in", s, 2)], writes=[("bank", bL)])
            bC = nb()
            for h in range(8):
                pg.mm(bank[bC][0:64, h:h + 1], I[2][:, h * 64:(h + 1) * 64], ones[:, 0:1], reads=[("in", s, 2), "ones"], writes=[("bank", bC)])
            pg.act(PC[s][:], bank[bC][0:64, 0:8], AF.Exp, reads=[("bank", bC)], writes=[("PC", s)])
            pg.act(Pt[:], bank[bL][:], AF.Exp, reads=[("bank", bL)], writes=["P"])
            pg.act(iP[:], bank[bL][:], AF.Exp, reads=[("bank", bL)], writes=["iP"], scale=-1.0)
            pg.tt("dve", Pp[:], bank[bL][:], lw, ALU.subtract, reads=[("bank", bL), ("in", s, 2)], writes=["Pp"])
            pg.act(Pp[:], Pp[:], AF.Exp, reads=["Pp"], writes=["Pp"])
            TM = tm[s]
            pg.tt("pool", TM[0][:], r_, Pt[:], ALU.mult, reads=[("in", s, 0), "P"], writes=[("tm", s, 0)])
            pg.stt("dve", TM[1][:], kkn, -1.0, Pp[:], ALU.mult, ALU.mult, reads=[("in", s, 1), "Pp"], writes=[("tm", s, 1)])
            pg.tt("pool", TM[2][:], b_, iP[:], ALU.mult, reads=[("in", s, 3), "iP"], writes=[("tm", s, 2)])
            pg.tt("dve", TM[3][:], kp, iP[:], ALU.mult, reads=[("in", s, 4), "iP"], writes=[("tm", s, 3)])
            for j in range(4):
                for hg in range(2):
                    bT = nb()
                    for hh in range(4):
                        h = hg * 4 + hh
                        pg.tr(bank[bT][0:64, hh * 128:(hh + 1) * 128], TM[j][:, h * 64:(h + 1) * 64], idf[:], reads=[("tm", s, j), "idf"], writes=[("bank", bT)])
                    pg.cp("act" if (j + hg) % 2 == 0 else "dve", fm[s][j][:, hg * 4:(hg + 1) * 4, :].rearrange("p a b -> p (a b)"), bank[bT][0:64, :],
                          reads=[("bank", bT)], writes=[("fm", s, j, hg)])
            FR, FKK, FB, FK = fm[s]
            combos = ((0, FB, 2, FKK, 1, 0), (1, FK, 3, FKK, 1, 0), (2, FB, 2, FR, 0, 1), (3, FK, 3, FR, 0, 1), (4, FKK, 1, FB, 2, 2))
            for hg in range(2):
                for (mi, L_, lj, R_, rj, mk) in combos:
                    bM = nb()
                    for hh in range(4):
                        h = hg * 4 + hh
                        pg.mm(bank[bM][:, hh * 128:(hh + 1) * 128], L_[:, h, :], R_[:, h, :], reads=[("fm", s, lj, hg), ("fm", s, rj, hg)], writes=[("bank", bM)])
                    pg.tt("dve", M[s][mi][:, hg * 4:(hg + 1) * 4, :], bank[bM][:].rearrange("p (a b) -> p a b", a=4),
                          msk[:, mk, :].unsqueeze(1).to_broadcast([128, 4, 128]), ALU.mult, reads=[("bank", bM), "msk"], writes=[("M", s, mi, hg)])
                pg.tt("pool", Xb[:, hg * 4:(hg + 1) * 4, :], idb[:, :].unsqueeze(1).to_broadcast([128, 4, 128]), M[s][0][:, hg * 4:(hg + 1) * 4, :], ALU.subtract,
                      reads=[("M", s, 0, hg), "idb"], writes=[("Xb", hg)])
            curN = [M[s][0], M[s][0]]
            curNT = [M[s][4], M[s][4]]
            kN = [("M", s, 0, 0), ("M", s, 0, 1)]
            kNT = [("M", s, 4, 0), ("M", s, 4, 1)]
            for j in range(6):
                dst = j % 2
                for hg in range(2):
                    b1, b2 = nb(), nb()
                    for hh in range(4):
                        h = hg * 4 + hh
                        pg.mm(bank[b1][:, hh * 128:(hh + 1) * 128], curNT[hg][:, h, :], curN[hg][:, h, :], reads=[kN[hg], kNT[hg]], writes=[("bank", b1)])
                    for hh in range(4):
                        h = hg * 4 + hh
                        pg.mm(bank[b2][:, hh * 128:(hh + 1) * 128], curN[hg][:, h, :], curNT[hg][:, h, :], reads=[kN[hg], kNT[hg]], writes=[("bank", b2)])
                    pg.cp("act", N2[dst][:, hg * 4:(hg + 1) * 4, :].rearrange("p a b -> p (a b)"), bank[b1][:], reads=[("bank", b1)], writes=[("N2", dst, hg)])
                    pg.cp("dve", N2T[dst][:, hg * 4:(hg + 1) * 4, :].rearrange("p a b -> p (a b)"), bank[b2][:], reads=[("bank", b2)], writes=[("N2T", dst, hg)])
                for hg in range(2):
                    curN[hg], curNT[hg] = N2[dst], N2T[dst]
                    kN[hg], kNT[hg] = ("N2", dst, hg), ("N2T", dst, hg)
                for hg in range(2):
                    b3 = nb()
                    for hh in range(4):
                        h = hg * 4 + hh
                        pg.mm(bank[b3][:, hh * 128:(hh + 1) * 128], curNT[hg][:, h, :], Xb[:, h, :], reads=[kNT[hg], ("Xb", hg)], writes=[("bank", b3)])
                    xo = (X[s] if j == 5 else Xb)
                    pg.tt("dve", xo[:, hg * 4:(hg + 1) * 4, :].rearrange("p a b -> p (a b)"), Xb[:, hg * 4:(hg + 1) * 4, :].rearrange("p a b -> p (a b)"), bank[b3][:], ALU.add,
                          reads=[("bank", b3), ("Xb", hg)], writes=[("X", s, hg)] if j == 5 else [("Xb", hg)])

        def seq(c):
            s = c % 2
            t0 = c * 128
            zc, zn = Z[c % 2], Z[(c + 1) % 2]
            kz, kzn = ("Z", c % 2), ("Z", (c + 1) % 2)
            FR, FKK, FB, FK = fm[s]
            V = inp[s][5]
            hsl = lambda h: slice(h * 64, (h + 1) * 64)
            Mk = lambda mi: [("M", s, mi, 0), ("M", s, mi, 1)]
            fk = lambda j: [("fm", s, j, 0), ("fm", s, j, 1)]
            Xk = [("X", s, 0), ("X", s, 1)]
            bG = nb()
            for h in range(8):
                pg.mm(bank[bG][:, hsl(h)], M[s][1][:, h, :], V[:, hsl(h)], start=True, stop=False, reads=Mk(1) + [("in", s, 5)], writes=[("bank", bG)])
                pg.mm(bank[bG][:, hsl(h)], FKK[:, h, :], zc[:, hsl(h)], start=False, stop=True, reads=fk(1) + [kz], writes=[("bank", bG)])
            pg.ts("dve", rhs_sb[:], bank[bG][:], -1.0, None, ALU.mult, reads=[("bank", bG)], writes=["rhs"])
            bU = nb()
            for h in range(8):
                pg.mm(bank[bU][:, hsl(h)], X[s][:, h, :], rhs_sb[:, hsl(h)], reads=Xk + ["rhs"], writes=[("bank", bU)])
            pg.cp("act", U_sb[:], bank[bU][:], reads=[("bank", bU)], writes=["U"])
            bZ = nb()
            for h in range(8):
                pg.mm(bank[bZ][0:64, hsl(h)], tm[s][3][:, hsl(h)], V[:, hsl(h)], start=True, stop=False, reads=[("tm", s, 3), ("in", s, 5)], writes=[("bank", bZ)])
                pg.mm(bank[bZ][0:64, hsl(h)], idf[0:64, 0:64], zc[:, hsl(h)], start=False, stop=False, reads=["idf", kz], writes=[("bank", bZ)])
                pg.mm(bank[bZ][0:64, hsl(h)], tm[s][2][:, hsl(h)], U_sb[:, hsl(h)], start=False, stop=True, reads=[("tm", s, 2), "U"], writes=[("bank", bZ)])
            pg.tt("dve", zn[:].rearrange("p (h v) -> p h v", h=8), bank[bZ][0:64, :].rearrange("p (h v) -> p h v", h=8),
                  PC[s][:, :].unsqueeze(2).to_broadcast([64, 8, 64]), ALU.mult, reads=[("bank", bZ), ("PC", s)], writes=[kzn])
            bY = nb()
            for h in range(8):
                pg.mm(bank[bY][:, hsl(h)], M[s][3][:, h, :], V[:, hsl(h)], start=True, stop=False, reads=Mk(3) + [("in", s, 5)], writes=[("bank", bY)])
                pg.mm(bank[bY][:, hsl(h)], FR[:, h, :], zc[:, hsl(h)], start=False, stop=False, reads=fk(0) + [kz], writes=[("bank", bY)])
                pg.mm(bank[bY][:, hsl(h)], M[s][2][:, h, :], U_sb[:, hsl(h)], start=False, stop=True, reads=Mk(2) + ["U"], writes=[("bank", bY)])
            pg.cp("act", Y_sb[s][:], bank[bY][:], reads=[("bank", bY)], writes=[("Y", s)])
            pg.ld(dr["YS"][t0:t0 + 128, :], Y_sb[s][:], reads=[("Y", s)], writes=[("YS", c)])

        tcg = table_conv_gen(C, sb)
        pre(0)
        for c in range(NT):
            if c + 1 < NT:
                pre(c + 1)
            for _ in range(8):
                next(tcg, None)
            seq(c)
        for _ in tcg:
            pass
        pg.barrier()
        pg.emit()
```

```python
import numpy as np
import concourse.bass as bass
import concourse.mybir as mybir

F32 = mybir.dt.float32
BF16 = mybir.dt.bfloat16
I32 = mybir.dt.int32
U32 = mybir.dt.uint32
ALU = mybir.AluOpType
AF = mybir.ActivationFunctionType
AX = mybir.AxisListType

EPOCH = 20000
ENGS = ("pe", "act", "dve", "pool", "sp")
NDMASEM = 16


class Prog:
    def __init__(self, nc, stack):
        self.nc = nc
        self.stack = stack
        self.ops = {e: [] for e in ENGS}
        self.cnt = {e: 0 for e in ENGS}
        self.esems = {e: [] for e in ENGS}
        self.waited = {e: {} for e in ENGS}
        self.lastw = {}
        self.readers = {}
        self.dsems = {}
        self.dcount = {}
        self.dtarget = {}
        self.semobjs = {}
        self.alltokens = {}
        for q in ("sp", "act", "pool"):
            self.dsems[q] = [self._newsem(f"d_{q}_{i}") for i in range(NDMASEM)]
            self.dcount[q] = 0
            self.dtarget[q] = [0] * NDMASEM

    def _newsem(self, name):
        s = self.stack.enter_context(self.nc.semaphore(name))
        self.semobjs[name] = s
        return name

    def _esem(self, e, idx):
        ep = idx // EPOCH
        while len(self.esems[e]) <= ep:
            self.esems[e].append(self._newsem(f"e_{e}_{len(self.esems[e])}"))
        return self.esems[e][ep], (idx % EPOCH) + 1

    def _deps(self, reads, writes):
        toks = []
        for k in reads:
            t = self.lastw.get(k)
            if t is not None:
                toks.append(t)
        for k in writes:
            t = self.lastw.get(k)
            if t is not None:
                toks.append(t)
            toks.extend(self.readers.get(k, ()))
        return toks

    def _commit(self, tok, reads, writes):
        for k in reads:
            self.readers.setdefault(k, []).append(tok)
        for k in writes:
            self.lastw[k] = tok
            self.readers[k] = []
        self.alltokens[tok[0]] = max(self.alltokens.get(tok[0], 0), tok[1])

    def _waits(self, e, toks):
        need = {}
        for (s, v) in toks:
            if v > need.get(s, 0):
                need[s] = v
        out = []
        w = self.waited[e]
        for s, v in need.items():
            if w.get(s, 0) < v:
                w[s] = v
                out.append((s, v))
        return out

    def op(self, e, fn, reads=(), writes=()):
        toks = self._deps(reads, writes)
        if e == "pe":
            toks = [t for t in toks if not t[0].startswith("e_pe_")]
        waits = self._waits(e, toks)
        idx = self.cnt[e]
        self.cnt[e] += 1
        tok = self._esem(e, idx)
        self.ops[e].append((waits, fn, (tok[0], 1)))
        self._commit(tok, reads, writes)

    def dma(self, q, fn, reads=(), writes=()):
        toks = self._deps(reads, writes)
        n = self.dcount[q]
        self.dcount[q] += 1
        slot = n % NDMASEM
        sname = self.dsems[q][slot]
        prev = self.dtarget[q][slot]
        if prev > 0:
            toks.append((sname, prev))
        tgt = prev + 16
        self.dtarget[q][slot] = tgt
        waits = self._waits(q, toks)
        tok = (sname, tgt)
        self.ops[q].append((waits, fn, (sname, 16)))
        self._commit(tok, reads, writes)

    def mm(self, out, lhsT, rhs, start=True, stop=True, reads=(), writes=()):
        self.op("pe", lambda e: e.matmul(out, lhsT, rhs, start=start, stop=stop), reads, writes)

    def tr(self, out, in_, ident, reads=(), writes=()):
        self.op("pe", lambda e: e.transpose(out, in_, ident), reads, writes)

    def act(self, out, in_, func, reads=(), writes=(), bias=None, scale=None, eng="act"):
        kw = {}
        if bias is not None:
            kw["bias"] = bias
        if scale is not None:
            kw["scale"] = scale
        self.op(eng, lambda e: e.activation(out, in_, func, **kw), reads, writes)

    def tt(self, eng, out, in0, in1, op, reads=(), writes=()):
        self.op(eng, lambda e: e.tensor_tensor(out, in0, in1, op), reads, writes)

    def ts(self, eng, out, in0, s1, s2, op0, op1=None, reads=(), writes=()):
        if op1 is None:
            self.op(eng, lambda e: e.tensor_scalar(out, in0, s1, s2, op0), reads, writes)
        else:
            self.op(eng, lambda e: e.tensor_scalar(out, in0, s1, s2, op0, op1), reads, writes)

    def stt(self, eng, out, in0, scalar, in1, op0, op1, reads=(), writes=()):
        self.op(eng, lambda e: e.scalar_tensor_tensor(out, in0, scalar, in1, op0, op1), reads, writes)

    def cp(self, eng, out, in_, reads=(), writes=()):
        if eng == "act":
            self.op(eng, lambda e: e.copy(out, in_), reads, writes)
        else:
            self.op(eng, lambda e: e.tensor_copy(out, in_), reads, writes)

    def red(self, eng, out, in_, op, reads=(), writes=(), axis=None):
        ax = AX.X if axis is None else axis
        self.op(eng, lambda e: e.tensor_reduce(out, in_, ax, op), reads, writes)

    def memset(self, eng, ap, val, writes=()):
        self.op(eng, lambda e: e.memset(ap, val), (), writes)

    def ld(self, out, in_, reads=(), writes=(), q="sp"):
        self.dma(q, lambda e: e.dma_start(out, in_), reads, writes)

    def barrier(self):
        toks = list(self.alltokens.items())
        for e in ENGS:
            waits = self._waits(e, toks)
            if waits:
                self.ops[e].append((waits, None, None))
        self.lastw = {}
        self.readers = {}

    def emit(self):
        nc = self.nc
        so = self.semobjs
        with nc.Block() as block:
            def mk(e):
                def body(eng):
                    for waits, fn, inc in self.ops[e]:
                        for (s, v) in waits:
                            eng.wait_ge(so[s], v)
                        if fn is not None:
                            ins = fn(eng)
                            ins.then_inc(so[inc[0]], inc[1])
                return body
            block.tensor(mk("pe"))
            block.scalar(mk("act"))
            block.vector(mk("dve"))
            block.gpsimd(mk("pool"))
            block.sync(mk("sp"))
        self.ops = {e: [] for e in ENGS}
from contextlib import ExitStack
from concourse.bass_utils import run_bass_kernel_spmd

T = 4096
D = 1024
NT = T // 128
INW = 5144
RWC = 1792
O_RW = 0
O_Q = 1792
O_KC = 2304
O_VC = 2432
O_KS = 2560
O_VS = 2688
O_KW = 2816
O_VW = 2944
O_BG = 3072
O_GA = 3096
O_GB = 4120


class Ctx:
    pass


def _mk(C, st):
    nc = C.nc
    sb = lambda name, shape, dt: st.enter_context(nc.sbuf_tensor(name, shape, dt))
    ps = lambda name, shape, dt: st.enter_context(nc.psum_tensor(name, shape, dt))
    return sb, ps


def stage1(C):
    nc, pg, dr = C.nc, C.pg, C.dr
    with ExitStack() as st:
        sb, ps = _mk(C, st)
        win = sb("s1_win", [128, 8, INW], BF16)
        pj = [sb(f"s1_pj{i}", [128, INW], F32) for i in range(2)]
        xt = [sb(f"s1_xt{i}", [128, D], F32) for i in range(2)]
        junk = sb("s1_junk", [128, D], F32)
        hb = [sb(f"s1_h{i}", [128, D], BF16) for i in range(2)]
        hT = [sb(f"s1_hT{i}", [128, 8, 128], BF16) for i in range(2)]
        gt = sb("s1_g", [128, D], F32)
        idf = sb("s1_idf", [128, 128], F32)
        idb = sb("s1_idb", [128, 128], BF16)
        ss = [sb(f"s1_ss{i}", [128, 4], F32) for i in range(2)]
        psT = [ps(f"s1_psT{i}", [128, 8, 128], BF16) for i in range(2)]
        psm = [ps(f"s1_psm{i}", [128, 512], F32) for i in range(4)]

        pg.ld(gt[:], dr["norm1_g_b"][:, :], writes=["gt"])
        pg.ld(idf[:], dr["ident"][:, :], writes=["idf"])
        pg.cp("dve", idb[:], idf[:], reads=["idf"], writes=["idb"])
        engs = ["act", "dve", "pool"]
        for kc in range(8):
            b = pj[kc % 2]
            pg.ld(b[:], dr["w_in"][kc * 128:(kc + 1) * 128, :], writes=[("pjall", kc % 2)])
            pg.cp(engs[kc % 3], win[:, kc, :], b[:], reads=[("pjall", kc % 2)], writes=[("win", kc)])
        winkeys = [("win", kc) for kc in range(8)]
        chunks = []
        c0 = 0
        while c0 < INW:
            w = min(512, INW - c0)
            chunks.append((c0, w))
            c0 += w
        def A1(i):
            s = i % 2
            pg.ld(xt[s][:], dr["x"][i * 128:(i + 1) * 128, :], writes=[("xt", s)])
            pg.tt("dve", junk[:], xt[s][:], xt[s][:], ALU.mult, reads=[("xt", s)], writes=["junk"])
            pg.red("dve", ss[s][:, 0:1], junk[:], ALU.add, reads=["junk"], writes=[("ss", s)])
            pg.act(ss[s][:, 1:2], ss[s][:, 0:1], AF.Sqrt, reads=[("ss", s)], writes=[("ss1", s)],
                   scale=1.0 / D, bias=C.eps6[:, 0:1])
            pg.op("dve", lambda e, o=ss[s][:, 2:3], a=ss[s][:, 1:2]: e.reciprocal(o, a),
                  reads=[("ss1", s)], writes=[("ss2", s)])
            pg.stt("dve", hb[s][:], xt[s][:], ss[s][:, 2:3], gt[:], ALU.mult, ALU.mult,
                   reads=[("xt", s), ("ss2", s), "gt"], writes=[("hb", s)])

        def A2(i):
            s = i % 2
            for j in range(8):
                pg.tr(psT[s][:, j, :], hb[s][:, j * 128:(j + 1) * 128], idb[:],
                      reads=[("hb", s), "idb"], writes=[("psT", s)])
            pg.cp("act", hT[s][:], psT[s][:], reads=[("psT", s)], writes=[("hT", s)])

        def B(i, lo, hi):
            s = i % 2
            for ci in range(lo, hi):
                c0, w = chunks[ci]
                pb = psm[ci % 4]
                for kc in range(8):
                    pg.mm(pb[:, :w], hT[s][:, kc, :], win[:, kc, c0:c0 + w], start=(kc == 0), stop=(kc == 7),
                          reads=[("hT", s), ("win", kc)], writes=[("psm", ci % 4)])
                pg.cp("act" if ci % 2 == 0 else "dve", pj[s][:, c0:c0 + w], pb[:, :w],
                      reads=[("psm", ci % 4)], writes=[("pj", s, ci), ("pjall", s)] if i < 8 else [("pj", s, ci)])

        def S(i):
            s = i % 2
            pg.ld(dr["P"][i * 128:(i + 1) * 128, 0:3096], pj[s][:, 0:3096],
                  reads=[("pj", s, ci) for ci in range(len(chunks))], writes=[("P", i)])
            pg.ld(dr["PG"][i * 128:(i + 1) * 128, :], pj[s][:, 3096:5144],
                  reads=[("pj", s, ci) for ci in range(len(chunks))], writes=[("PG", i)], q="act")

        A1(0)
        A2(0)
        A1(1)
        for i in range(NT):
            B(i, 0, 6)
            if i + 1 < NT:
                A2(i + 1)
            if i + 2 < NT:
                A1(i + 2)
            B(i, 6, len(chunks))
            S(i)
        pg.barrier()
        pg.emit()


def build(stages, dbg_out=(), dbg_in=(), lvl=9, sub=9, peer_tiles=NT):
    nc = bass.Bass("TRN2", target_bir_lowering=False)
    C = Ctx()
    C.peer_tiles = peer_tiles
    C.lvl = lvl
    C.sub = sub
    C.nc = nc
    dr = {}
    C.dr = dr

    def din(name, shape, dt=F32):
        dr[name] = nc.dram_tensor(name, list(shape), dt, kind="ExternalInput").ap()

    def dscr(name, shape, dt=F32):
        kind = "ExternalOutput" if name in dbg_out else ("ExternalInput" if name in dbg_in else "Internal")
        dr[name] = nc.dram_tensor(name, list(shape), dt, kind=kind).ap()

    din("x", [T, D])
    din("norm1_g_b", [128, D])
    din("ident", [128, 128])
    din("w_in", [D, INW])
    dscr("P", [T, INW])
    dscr("PG", [T, 2048])
    for nm in ("rw_mu_b",):
        din(nm, [128, RWC])
    for nm in ("rw_w0_b", "rw_a0_b", "rw_k_k_b", "rw_k_a_b", "rw_r_k_b", "rw_ln_w_b", "rw_ln_b_b", "rw_g_up"):
        din(nm, [128, 512])
    din("rw_w_up", [64, 512])
    din("rw_a_up", [64, 512])
    for nm in ("RB", "RKp", "RV", "RG", "YA", "RR", "RKK", "RLW", "YS"):
        dscr(nm, [T, 512])
    dscr("RBON", [T, 8])
    din("blkmask", [8, 512])
    din("c_tri", [128, 128])
    din("c_msk", [128, 3, 128])
    din("nsa_gains_b", [128, 768])
    din("nsa_kc_g_b", [128, 64])
    din("ovl", [128, 2, 64])
    din("posT", [128, 2, 32])
    din("cmp_w2", [128, 2, 2, 64])
    din("cmp_w1", [2, 128, 32, 256])
    din("c_cmpb", [128, 2, T], BF16)
    din("c_esel", [128, 32, 128], BF16)
    din("c_causb", [128, 4, 512], BF16)
    din("c_winb", [128, 8, 512], BF16)
    din("c_winb4", [128, 8, 512], BF16)
    din("c_vmfb", [NT, 128, 2, 64])
    dscr("YB", [T, 512])
    din("w_branch_a", [512, D])
    din("w_branch_b", [512, D])
    din("w_out", [D, D])
    dscr("X1L", [peer_tiles * 128, D])
    dscr("YAL", [peer_tiles * 128, 512])
    din("norm2_g_b", [128, D])
    din("iota16", [128, 16])
    din("rowidx", [128, NT], I32)
    din("peer_wq", [D, 2048])
    din("peer_k1", [128, 128])
    din("peer_k2", [128, 128])
    din("peer_u", [16384, D])
    din("peer_v", [16384, D])
    dscr("UV", [16384, 2 * D], BF16)
    dr["out"] = nc.dram_tensor("out", [peer_tiles * 128, D], F32, kind="ExternalOutput").ap()
    with ExitStack() as top:
        pg = Prog(nc, top)
        C.pg = pg
        C.eps6 = top.enter_context(nc.sbuf_tensor("c_eps6", [128, 1], F32))
        pg.memset("dve", C.eps6[:], 1e-6, writes=["eps6"])
        pg.barrier()
        for s in stages:
            s(C)
        pg.barrier()
        pg.emit()
    return nc


def core_x(inputs, b, hh):
    xb = np.asarray(inputs["x"][b])
    if hh == 0:
        return np.ascontiguousarray(np.concatenate([np.zeros((T // 2, D), np.float32), xb[0:T // 2]], 0))
    return np.ascontiguousarray(xb)


def host_inputs(inputs, b, hh=1, ntl=NT):
    g = lambda k: np.ascontiguousarray(inputs[k][0])
    m = {}
    m["x"] = core_x(inputs, b, hh)
    m["norm1_g_b"] = np.ascontiguousarray(np.broadcast_to(g("norm1_g")[None, :], (128, D)))
    m["ident"] = np.eye(128, dtype=np.float32)
    m["w_in"] = g("w_in")
    bc = lambda a: np.ascontiguousarray(np.broadcast_to(np.asarray(a).reshape(1, -1), (128, a.size)))
    m["rw_mu_b"] = bc(g("rw_mu"))
    for nm in ("rw_w0", "rw_a0", "rw_k_k", "rw_k_a", "rw_r_k", "rw_ln_w", "rw_ln_b"):
        m[nm + "_b"] = bc(g(nm))
    for nm in ("rw_g_up", "rw_w_up", "rw_a_up"):
        m[nm] = g(nm)
    bmk = np.zeros((8, 512), np.float32)
    for h in range(8):
        bmk[h, h * 64:(h + 1) * 64] = 1.0
    m["blkmask"] = bmk
    ii = np.arange(128)
    m["c_tri"] = (ii[:, None] <= ii[None, :]).astype(np.float32)
    m["c_msk"] = np.ascontiguousarray(np.stack([(ii[:, None] < ii[None, :]), (ii[:, None] <= ii[None, :]), (ii[:, None] > ii[None, :])], 1).astype(np.float32))
    m.update(nsa_consts(hh))
    for nm in ("w_branch_a", "w_branch_b", "w_out", "peer_wq", "peer_k1", "peer_k2", "peer_u", "peer_v"):
        m[nm] = g(nm)
    m["norm2_g_b"] = bc(g("norm2_g"))
    ri = np.zeros((128, NT), np.int32)
    ri[:, :ntl] = ((NT - ntl) * 128 + np.arange(ntl)[None, :] * 128 + np.arange(128)[:, None]).astype(np.int32)
    m["rowidx"] = ri
    m["iota16"] = np.ascontiguousarray(np.broadcast_to(np.arange(16, dtype=np.float32)[None, :], (128, 16)))
    m["nsa_gains_b"] = bc(np.concatenate([np.tile(g("nsa_q_g"), 8), np.tile(g("nsa_ks_g"), 2), np.tile(g("nsa_kw_g"), 2)]))
    m["nsa_kc_g_b"] = bc(g("nsa_kc_g"))
    posT = np.zeros((128, 2, 32), np.float32)
    posT[0:64, 0, :] = g("cmp_pos_k").T
    posT[0:64, 1, :] = g("cmp_pos_v").T
    m["posT"] = posT
    w2 = np.stack([g("cmp_k_w2").reshape(2, 128, 64), g("cmp_v_w2").reshape(2, 128, 64)], 0)
    m["cmp_w2"] = np.ascontiguousarray(w2.transpose(2, 0, 1, 3))
    w1 = []
    for nm in ("cmp_k_w1", "cmp_v_w1"):
        a = g(nm).reshape(32, 64, 256).transpose(1, 0, 2)
        w1.append(np.concatenate([a, a], 0))
    m["cmp_w1"] = np.ascontiguousarray(np.stack(w1, 0))
    return m


def dap(ap, offset, pattern):
    return bass.AP(ap.tensor, offset, [list(p) for p in pattern])


def stage2a(C):
    nc, pg, dr = C.nc, C.pg, C.dr
    with ExitStack() as st:
        sb, ps = _mk(C, st)
        mu = sb("a_mu", [128, RWC], F32)
        w0 = sb("a_w0", [128, 512], F32)
        a0 = sb("a_a0", [128, 512], F32)
        kkc = sb("a_kk", [128, 512], F32)
        kac = sb("a_ka", [128, 512], F32)
        rkc = sb("a_rk", [128, 512], F32)
        wup = sb("a_wup", [128, 512], F32)
        gup = sb("a_gup", [128, 512], F32)
        idf = sb("a_idf", [128, 128], F32)
        p_2 = [sb(f"a_p{i_}", [128, RWC], F32) for i_ in range(2)]
        pv_2 = [sb(f"a_pv{i_}", [128, RWC], F32) for i_ in range(2)]
        pm_2 = [sb(f"a_pm{i_}", [128, RWC], F32) for i_ in range(2)]
        lor_2 = [sb(f"a_lor{i_}", [128, 256], F32) for i_ in range(2)]
        lorT_2 = [sb(f"a_lorT{i_}", [128, 256], F32) for i_ in range(2)]
        wt_2 = [sb(f"a_wt{i_}", [128, 512], F32) for i_ in range(2)]
        lwt_2 = [sb(f"a_lwt{i_}", [128, 512], F32) for i_ in range(2)]
        at_2 = [sb(f"a_at{i_}", [128, 512], F32) for i_ in range(2)]
        gt_2 = [sb(f"a_gt{i_}", [128, 512], F32) for i_ in range(2)]
        kk_2 = [sb(f"a_kkt{i_}", [128, 512], F32) for i_ in range(2)]
        sq_2 = [sb(f"a_sq{i_}", [128, 512], F32) for i_ in range(2)]
        nrm_2 = [sb(f"a_nrm{i_}", [128, 32], F32) for i_ in range(2)]
        kkn_2 = [sb(f"a_kkn{i_}", [128, 512], F32) for i_ in range(2)]
        bt_2 = [sb(f"a_bt{i_}", [128, 512], F32) for i_ in range(2)]
        t1_2 = [sb(f"a_t1{i_}", [128, 512], F32) for i_ in range(2)]
        kp_2 = [sb(f"a_kp{i_}", [128, 512], F32) for i_ in range(2)]
        bon_2 = [sb(f"a_bon{i_}", [128, 8], F32) for i_ in range(2)]
        psl = ps("a_psl", [128, 512], F32)
        psw = ps("a_psw", [128, 512], F32)
        psa = ps("a_psa", [128, 512], F32)
        psg = ps("a_psg", [128, 512], F32)

        for (tile, name) in ((mu, "rw_mu_b"), (w0, "rw_w0_b"), (a0, "rw_a0_b"), (kkc, "rw_k_k_b"),
                             (kac, "rw_k_a_b"), (rkc, "rw_r_k_b"), (gup, "rw_g_up"), (idf, "ident")):
            pg.ld(tile[:], dr[name][:, :], writes=[name])
        pg.ld(wup[0:64, :], dr["rw_w_up"][:, :], writes=["wup0"])
        pg.ld(wup[64:128, :], dr["rw_a_up"][:, :], writes=["wup1"])
        P = dr["P"]
        for i in range(NT):
            t0 = i * 128
            s = i % 2
            p, pv, pm, lor, lorT, wt, lwt, at, gt, kk, sq, nrm, kkn, bt, t1, kp, bon = [t_[s] for t_ in (
                p_2, pv_2, pm_2, lor_2, lorT_2, wt_2, lwt_2, at_2, gt_2, kk_2, sq_2, nrm_2, kkn_2, bt_2, t1_2, kp_2, bon_2)]
            pg.ld(p[:], P[t0:t0 + 128, 0:RWC], reads=[("P", i)], writes=[("p", s)])
            if i == 0:
                pg.memset("dve", pv[0:1, :], 0.0, writes=[("pv0", s)])
                pg.ld(pv[1:128, :], P[0:127, 0:RWC], reads=[("P", 0)], writes=[("pv", s)])
                pvk = [("pv", s), ("pv0", s)]
            else:
                pg.ld(pv[:], P[t0 - 1:t0 + 127, 0:RWC], reads=[("P", i), ("P", i - 1)], writes=[("pv", s), ("pv0", s)])
                pvk = [("pv", s), ("pv0", s)]
            pg.tt("dve", pv[:], pv[:], p[:], ALU.subtract, reads=pvk + [("p", s)], writes=[("pv", s)])
            pg.tt("dve", pv[:], pv[:], mu[:], ALU.mult, reads=[("pv", s), "rw_mu_b"], writes=[("pv", s)])
            pg.tt("dve", pm[:], pv[:], p[:], ALU.add, reads=[("pv", s), ("p", s)], writes=[("pm", s)])
            r_ = pm[:, 0:512]
            k_ = pm[:, 512:1024]
            v_ = pm[:, 1024:1536]
            pg.act(lor[:, 0:64], pm[:, 1536:1600], AF.Tanh, reads=[("pm", s)], writes=[("lor0", s)])
            pg.cp("pool", lor[:, 64:128], pm[:, 1600:1664], reads=[("pm", s)], writes=[("lor1", s)])
            pg.act(lor[:, 128:256], pm[:, 1664:1792], AF.Sigmoid, reads=[("pm", s)], writes=[("lor2", s)])
            pg.tr(psl[:, 0:128], lor[:, 0:128], idf[:], reads=[("lor0", s), ("lor1", s), "ident"], writes=["psl"])
            pg.tr(psl[:, 128:256], lor[:, 128:256], idf[:], reads=[("lor2", s), "ident"], writes=["psl"])
            pg.cp("act", lorT[:], psl[:, 0:256], reads=["psl"], writes=[("lorT", s)])
            pg.mm(psw[:], lorT[0:64, 0:128], wup[0:64, :], reads=[("lorT", s), "wup0"], writes=["psw"])
            pg.mm(psa[:], lorT[64:128, 0:128], wup[64:128, :], reads=[("lorT", s), "wup1"], writes=["psa"])
            pg.mm(psg[:], lorT[:, 128:256], gup[:], reads=[("lorT", s), "rw_g_up"], writes=["psg"])
            pg.tt("dve", wt[:], psw[:], w0[:], ALU.add, reads=["psw", "rw_w0_b"], writes=[("wt", s)])
            pg.act(wt[:], wt[:], AF.Sigmoid, reads=[("wt", s)], writes=[("wt", s)])
            pg.ts("dve", lwt[:], wt[:], -0.6065306597126334, None, ALU.mult, reads=[("wt", s)], writes=[("lwt", s)])
            pg.tt("dve", at[:], psa[:], a0[:], ALU.add, reads=["psa", "rw_a0_b"], writes=[("at", s)])
            pg.act(at[:], at[:], AF.Sigmoid, reads=[("at", s)], writes=[("at", s)])
            pg.cp("act", gt[:], psg[:], reads=["psg"], writes=[("gt", s)])
            pg.tt("dve", kk[:], k_, kkc[:], ALU.mult, reads=[("pm", s), "rw_k_k_b"], writes=[("kk", s)])
            pg.tt("pool", sq[:], kk[:], kk[:], ALU.mult, reads=[("kk", s)], writes=[("sq", s)])
            pg.red("dve", nrm[:, 0:8], sq[:].rearrange("p (h k) -> p h k", h=8), ALU.add, reads=[("sq", s)], writes=[("nrm0", s)])
            pg.act(nrm[:, 8:16], nrm[:, 0:8], AF.Sqrt, reads=[("nrm0", s)], writes=[("nrm1", s)])
            pg.ts("dve", nrm[:, 16:24], nrm[:, 8:16], 1e-12, None, ALU.max, reads=[("nrm1", s)], writes=[("nrm2", s)])
            pg.op("dve", lambda e, nrm=nrm: e.reciprocal(nrm[:, 24:32], nrm[:, 16:24]), reads=[("nrm2", s)], writes=[("nrm3", s)])
            rinv_b = nrm[:, 24:32].unsqueeze(2).to_broadcast([128, 8, 64])
            v3 = lambda tl: tl[:].rearrange("p (h k) -> p h k", h=8)
            pg.stt("dve", v3(kkn), v3(kk), -1.0, rinv_b, ALU.mult, ALU.mult, reads=[("kk", s), ("nrm3", s)], writes=[("kkn", s)])
            pg.stt("dve", bt[:], kkn[:], -1.0, at[:], ALU.mult, ALU.mult, reads=[("kkn", s), ("at", s)], writes=[("bt", s)])
            pg.stt("dve", t1[:], at[:], -1.0, kac[:], ALU.add, ALU.mult, reads=[("at", s), "rw_k_a_b"], writes=[("t1", s)])
            pg.stt("dve", kp[:], t1[:], 1.0, k_, ALU.add, ALU.mult, reads=[("t1", s), ("pm", s)], writes=[("kp", s)])
            pg.tt("pool", sq[:], r_, kp[:], ALU.mult, reads=[("pm", s), ("kp", s), ("sq", s)], writes=[("sq", s)])
            pg.tt("pool", sq[:], sq[:], rkc[:], ALU.mult, reads=[("sq", s), "rw_r_k_b"], writes=[("sq", s)])
            pg.red("dve", bon[:], sq[:].rearrange("p (h k) -> p h k", h=8), ALU.add, reads=[("sq", s)], writes=[("bon", s)])
            pg.ld(dr["RR"][t0:t0 + 128, :], r_, reads=[("pm", s)], writes=[("RR", i)], q="act")
            pg.ld(dr["RKK"][t0:t0 + 128, :], kkn[:], reads=[("kkn", s)], writes=[("RKK", i)], q="act")
            pg.ld(dr["RLW"][t0:t0 + 128, :], lwt[:], reads=[("lwt", s)], writes=[("RLW", i)], q="act")
            pg.ld(dr["RB"][t0:t0 + 128, :], bt[:], reads=[("bt", s)], writes=[("RB", i)], q="act")
            pg.ld(dr["RKp"][t0:t0 + 128, :], kp[:], reads=[("kp", s)], writes=[("RKp", i)], q="act")
            pg.ld(dr["RV"][t0:t0 + 128, :], v_, reads=[("pm", s)], writes=[("RV", i)], q="act")
            pg.ld(dr["RG"][t0:t0 + 128, :], gt[:], reads=[("gt", s)], writes=[("RG", i)], q="act")
            pg.ld(dr["RBON"][t0:t0 + 128, :], bon[:], reads=[("bon", s)], writes=[("RBON", i)], q="act")
        pg.barrier()
        pg.emit()


def igather(pg, out_ap, table_ap, idx_ap, reads, writes):
    pg.dma("pool", lambda e: e.indirect_dma_start(out=out_ap, out_offset=None, in_=table_ap,
                                                   in_offset=bass.IndirectOffsetOnAxis(ap=idx_ap, axis=0)), reads, writes)


def stage2c(C):
    nc, pg, dr = C.nc, C.pg, C.dr
    with ExitStack() as st:
        sb, ps = _mk(C, st)
        lnw = sb("c_lnw", [128, 512], F32)
        lnb = sb("c_lnb", [128, 512], F32)
        eps = sb("c_eps", [128, 1], F32)
        y = [sb(f"c_y{i}", [128, 8, 64], F32) for i in range(2)]
        v = [sb(f"c_v{i}", [128, 8, 64], F32) for i in range(2)]
        g = [sb(f"c_g{i}", [128, 512], F32) for i in range(2)]
        bon = [sb(f"c_bon{i}", [128, 8], F32) for i in range(2)]
        stt_ = [sb(f"c_st{i}", [128, 32], F32) for i in range(2)]
        sq = sb("c_sq", [128, 8, 64], F32)
        pg.ld(lnw[:], dr["rw_ln_w_b"][:, :], writes=["lnw"])
        pg.ld(lnb[:], dr["rw_ln_b_b"][:, :], writes=["lnb"])
        pg.memset("dve", eps[:], 64e-5, writes=["eps"])
        f2 = lambda tl: tl[:].rearrange("p h k -> p (h k)")
        rowidx = sb("c_rowidx", [128, NT], I32)
        pg.ld(rowidx[:], dr["rowidx"][:, :], writes=["rowidx"])
        allk = lambda nm: [(nm, k) for k in range(NT)]
        for i in range(C.peer_tiles):
            s = i % 2
            t0 = i * 128
            yk, vk, gk, bk, sk = ("y", s), ("v", s), ("g", s), ("bon", s), ("st", s)
            ix = rowidx[:, i:i + 1]
            igather(pg, f2(y[s]), dr["YS"][:, :], ix, allk("YS") + ["rowidx"], [yk])
            igather(pg, f2(v[s]), dr["RV"][:, :], ix, allk("RV") + ["rowidx"], [vk])
            igather(pg, g[s][:], dr["RG"][:, :], ix, allk("RG") + ["rowidx"], [gk])
            igather(pg, bon[s][:], dr["RBON"][:, :], ix, allk("RBON") + ["rowidx"], [bk])
            S_ = stt_[s]
            bc = lambda ap: ap.unsqueeze(2).to_broadcast([128, 8, 64])
            pg.red("dve", S_[:, 0:8], y[s][:], ALU.add, reads=[yk], writes=[(sk, 0)])
            pg.ts("dve", S_[:, 8:16], S_[:, 0:8], -1.0 / 64, None, ALU.mult, reads=[(sk, 0)], writes=[(sk, 1)])
            pg.tt("dve", y[s][:], y[s][:], bc(S_[:, 8:16]), ALU.add, reads=[yk, (sk, 1)], writes=[yk])
            pg.tt("pool", sq[:], y[s][:], y[s][:], ALU.mult, reads=[yk], writes=["sq"])
            pg.red("dve", S_[:, 16:24], sq[:], ALU.add, reads=["sq"], writes=[(sk, 2)])
            pg.act(S_[:, 24:32], S_[:, 16:24], AF.Sqrt, reads=[(sk, 2), "eps"], writes=[(sk, 3)], scale=1.0 / 64, bias=eps[:, 0:1])
            pg.op("dve", lambda e, o=S_[:, 16:24], a=S_[:, 24:32]: e.reciprocal(o, a), reads=[(sk, 3)], writes=[(sk, 2)])
            pg.tt("dve", y[s][:], y[s][:], bc(S_[:, 16:24]), ALU.mult, reads=[yk, (sk, 2)], writes=[yk])
            pg.tt("dve", f2(y[s]), f2(y[s]), lnw[:], ALU.mult, reads=[yk, "lnw"], writes=[yk])
            pg.tt("pool", f2(y[s]), f2(y[s]), lnb[:], ALU.add, reads=[yk, "lnb"], writes=[yk])
            pg.tt("pool", v[s][:], v[s][:], bc(bon[s][:, 0:8]), ALU.mult, reads=[vk, bk], writes=[vk])
            pg.tt("dve", y[s][:], y[s][:], v[s][:], ALU.add, reads=[yk, vk], writes=[yk])
            pg.tt("dve", f2(y[s]), f2(y[s]), g[s][:], ALU.mult, reads=[yk, gk], writes=[yk])
            pg.ld(dr["YAL"][t0:t0 + 128, :], f2(y[s]), reads=[yk], writes=[("YAL", i)])
        pg.barrier()
        pg.emit()


NEG = -30000.0


def stage3(C):
    nc, pg, dr = C.nc, C.pg, C.dr
    with ExitStack() as st:
        sb, ps = _mk(C, st)
        qT = sb("n_qT", [128, 4, T], BF16)
        KsT = sb("n_KsT", [128, 2, T], BF16)
        KwT = sb("n_KwT", [128, 2, T], BF16)
        Vs = sb("n_Vs", [128, NT, 2, 65], BF16)
        Vw = sb("n_Vw", [128, NT, 2, 65], BF16)
        KcT = sb("n_KcT", [128, 2, 256], BF16)
        Vc = sb("n_Vc", [128, 2, 2, 129], BF16)
        GT = sb("n_GT", [128, NT, 24], F32)
        idf = sb("n_idf", [128, 128], F32)
        idb = sb("n_idb", [128, 128], BF16)
        eps = sb("n_eps", [128, 1], F32)
        pg.ld(idf[:], dr["ident"][:, :], writes=["idf"])
        pg.cp("dve", idb[:], idf[:], reads=["idf"], writes=["idb"])
        pg.memset("dve", eps[:], 1e-6, writes=["eps"])
        pg.memset("pool", Vs[:], 1.0, writes=["Vs"])
        pg.memset("pool", Vw[:], 1.0, writes=["Vw"])
        pg.memset("pool", Vc[:], 0.0, writes=["Vc"])
        with ExitStack() as sa_:
            sb, ps = _mk(C, sa_)
            kcT2 = sb("n_kcT2", [128, T], BF16)
            vcT2 = sb("n_vcT2", [128, T], BF16)
            w1 = [sb(f"n_w1{i}", [128, 32, 256], BF16) for i in range(2)]
            w1s = sb("n_w1s", [128, 16, 256], F32)
            w2s = sb("n_w2s", [128, 2, 2, 64], F32)
            w2 = sb("n_w2", [128, 2, 2, 64], BF16)
            posf = sb("n_posf", [128, 2, 32], F32)
            posb = sb("n_posb", [128, 2, 32], BF16)
            gains = sb("n_gains", [128, 768], F32)
            kcg = sb("n_kcg", [128, 64], F32)
            ovl = sb("n_ovl", [128, 2, 64], F32)
            R = [sb(f"n_R{i}", [128, 1304], F32) for i in range(2)]
            sq = sb("n_sq", [128, 1280], F32)
            tmp = sb("n_tmp", [128, 768], F32)
            stat = sb("n_stat", [128, 64], F32)
            Xb = sb("n_Xb", [128, 10, 128], BF16)
            biasS = sb("n_biasS", [128, 4], F32)
            xb_ = sb("n_xb", [128, 256], F32)
            x2_ = sb("n_x2", [128, 256], F32)
            hT = sb("n_hT", [128, 2, 256], BF16)
            kcn2 = sb("n_kcn2", [128, 128], BF16)
            st2 = sb("n_st2", [128, 8], F32)
            ksq = sb("n_ksq", [128, 64], F32)
            psX_ = [ps(f"n_psX{i}", [128, 1024], BF16) for i in range(3)]
            psX = [t_[:, 0:512].rearrange("p (a b) -> p a b", a=4) for t_ in psX_]
            psh = ps("n_psh", [128, 512], F32)
            psb = ps("n_psb", [128, 512], F32)
            pso = ps("n_pso", [128, 512], F32)
            psk = ps("n_psk", [128, 1024], BF16)

            pg.ld(gains[:], dr["nsa_gains_b"][:, :], writes=["gains"])
            pg.ts("dve", gains[:, 0:512], gains[:, 0:512], 0.125, None, ALU.mult, reads=["gains"], writes=["gains"])
            pg.ld(kcg[:], dr["nsa_kc_g_b"][:, :], writes=["kcg"])
            pg.ld(ovl[:], dr["ovl"][:, :, :], writes=["ovl"])
            pg.ld(posf[:], dr["posT"][:, :, :], writes=["posf"])
            pg.cp("dve", posb[:], posf[:], reads=["posf"], writes=["posb"])
            pg.ld(w2s[:], dr["cmp_w2"][:, :, :, :], writes=["w2s"])
            pg.cp("dve", w2[:], w2s[:], reads=["w2s"], writes=["w2"])
            for x in range(2):
                for hf in range(2):
                    pg.ld(w1s[:], dr["cmp_w1"][x, :, hf * 16:(hf + 1) * 16, :], writes=["w1s"])
                    pg.cp("pool", w1[x][:, hf * 16:(hf + 1) * 16, :], w1s[:], reads=["w1s"], writes=[("w1", x)])
            for i in range(NT):
                s = i % 2
                t0 = i * 128
                Rk = ("R", s)
                pg.ld(R[s][:], dr["P"][t0:t0 + 128, 1792:3096], reads=[("P", i)], writes=[Rk])
                Rs = R[s]
                pg.tt("pool", sq[:], Rs[:, 0:1280], Rs[:, 0:1280], ALU.mult, reads=[Rk], writes=["sq"])
                pg.red("dve", stat[:, 0:20], sq[:].rearrange("p (a k) -> p a k", k=64), ALU.add, reads=["sq"], writes=["stat0"])
                pg.act(stat[:, 20:40], stat[:, 0:20], AF.Sqrt, reads=["stat0", "eps"], writes=["stat1"], scale=1.0 / 64, bias=eps[:, 0:1])
                pg.op("dve", lambda e: e.reciprocal(stat[:, 40:60], stat[:, 20:40]), reads=["stat1"], writes=["stat2"])
                b3 = lambda ap, n: ap.unsqueeze(2).to_broadcast([128, n, 64])
                v3 = lambda ap: ap.rearrange("p (a k) -> p a k", k=64)
                pg.tt("dve", v3(tmp[:, 0:512]), v3(Rs[:, 0:512]), b3(stat[:, 40:48], 8), ALU.mult, reads=[Rk, "stat2"], writes=["tmp"])
                pg.tt("dve", v3(tmp[:, 512:640]), v3(Rs[:, 768:896]), b3(stat[:, 52:54], 2), ALU.mult, reads=[Rk, "stat2"], writes=["tmp"])
                pg.tt("dve", v3(tmp[:, 640:768]), v3(Rs[:, 1024:1152]), b3(stat[:, 56:58], 2), ALU.mult, reads=[Rk, "stat2"], writes=["tmp"])
                pg.tt("pool", tmp[:], tmp[:], gains[:], ALU.mult, reads=["tmp", "gains"], writes=["tmp"])
                pg.cp("pool", Xb[:, 0:4, :].rearrange("p a b -> p (a b)"), tmp[:, 0:512], reads=["tmp"], writes=["Xb"])
                for (blk, c0) in ((4, 512), (6, 640)):
                    src = tmp[:, c0:c0 + 128].rearrange("p (g k) -> p g k", g=2).unsqueeze(2).to_broadcast([128, 2, 2, 64])
                    dst = Xb[:, blk:blk + 2, :].rearrange("p g (d k) -> p g d k", d=2)
                    pg.cp("dve", dst, src, reads=["tmp"], writes=["Xb"])
                pg.cp("pool", Xb[:, 8, :], Rs[:, 512:640], reads=[Rk], writes=["Xb"])
                pg.cp("pool", Xb[:, 9, :], Rs[:, 640:768], reads=[Rk], writes=["Xb"])
                for blk in range(10):
                    pg.tr(psX[blk // 4][:, blk % 4, :], Xb[:, blk, :], idb[:], reads=["Xb", "idb"], writes=[("psX", blk // 4)])
                pg.cp("act", qT[:, :, t0:t0 + 128], psX[0], reads=[("psX", 0)], writes=["qT"])
                pg.cp("dve", KsT[:, :, t0:t0 + 128], psX[1][:, 0:2, :], reads=[("psX", 1)], writes=["KsT"])
                pg.cp("dve", KwT[:, :, t0:t0 + 128], psX[1][:, 2:4, :], reads=[("psX", 1)], writes=["KwT"])
                pg.cp("act", kcT2[:, t0:t0 + 128], psX[2][:, 0, :], reads=[("psX", 2)], writes=["kcT2"])
                pg.cp("act", vcT2[:, t0:t0 + 128], psX[2][:, 1, :], reads=[("psX", 2)], writes=["vcT2"])
                pg.cp("pool", Vs[:, i, :, 0:64], Rs[:, 896:1024].rearrange("p (g k) -> p g k", g=2), reads=[Rk, "Vs"], writes=["Vs"])
                pg.cp("pool", Vw[:, i, :, 0:64], Rs[:, 1152:1280].rearrange("p (g k) -> p g k", g=2), reads=[Rk, "Vw"], writes=["Vw"])
                pg.act(GT[:, i, :], Rs[:, 1280:1304], AF.Sigmoid, reads=[Rk], writes=["GT"])
            if getattr(C, "lvl", 9) < 2:
                pg.barrier()
                pg.emit()
                return
            pg.memset("dve", hT[:], 0.0, writes=["hT"])
            pg.memset("dve", kcn2[:], 0.0, writes=["kcn2"])
            for x in range(2):
                for hf in range(2):
                    for l in range(32):
                        pg.mm(psb[:, x * 2 + hf:x * 2 + hf + 1], w1[x][0:64, l, hf * 128:(hf + 1) * 128], posb[0:64, x, l:l + 1],
                              start=(l == 0), stop=(l == 31), reads=[("w1", x), "posb"], writes=["psb"])
            pg.cp("dve", biasS[:], psb[:, 0:4], reads=["psb"], writes=["biasS"])
            for x in range(2):
                srcT = kcT2 if x == 0 else vcT2
                skey = "kcT2" if x == 0 else "vcT2"
                for g in range(2):
                    for hf in range(2):
                        for l in range(32):
                            rhs = dap(srcT[:], g * 64 * T + l, [[T, 64], [16, 255]])
                            pg.mm(psh[:, 0:255], w1[x][g * 64:(g + 1) * 64, l, hf * 128:(hf + 1) * 128], rhs,
                                  start=(l == 0), stop=(l == 31), reads=[("w1", x), skey], writes=["psh"])
                        c = slice(0, 255)
                        pg.act(xb_[:, c], psh[:, c], AF.Identity, reads=["psh", "biasS"], writes=["xb"], bias=biasS[:, x * 2 + hf:x * 2 + hf + 1])
                        pg.tt("pool", x2_[:, c], xb_[:, c], xb_[:, c], ALU.mult, reads=["xb"], writes=["x2"])
                        pg.ts("dve", x2_[:, c], x2_[:, c], 0.044715, 1.0, ALU.mult, ALU.add, reads=["x2"], writes=["x2"])
                        pg.tt("dve", x2_[:, c], x2_[:, c], xb_[:, c], ALU.mult, reads=["x2", "xb"], writes=["x2"])
                        pg.act(x2_[:, c], x2_[:, c], AF.Tanh, reads=["x2"], writes=["x2"], scale=0.7978845608028654)
                        pg.stt("dve", x2_[:, c], x2_[:, c], 1.0, xb_[:, c], ALU.add, ALU.mult, reads=["x2", "xb"], writes=["x2"])
                        pg.ts("dve", hT[:, hf, c], x2_[:, c], 0.5, None, ALU.mult, reads=["x2"], writes=["hT"])
                    for m in range(2):
                        rows = 128 if m == 0 else 127
                        for hf in range(2):
                            pg.mm(pso[0:rows, 0:64], hT[:, hf, m * 128:m * 128 + rows], w2[:, x, hf, :], start=(hf == 0), stop=(hf == 1),
                                  reads=["hT", "w2"], writes=["pso"])
                        if x == 0:
                            pg.cp("act", ksq[0:rows, :], pso[0:rows, 0:64], reads=["pso"], writes=["ksq"])
                            pg.tt("pool", x2_[0:rows, 0:64], ksq[0:rows, :], ksq[0:rows, :], ALU.mult, reads=["ksq", "x2"], writes=["x2"])
                            pg.red("dve", st2[0:rows, 0:1], x2_[0:rows, 0:64], ALU.add, reads=["x2"], writes=["st2a"])
                            pg.act(st2[0:rows, 1:2], st2[0:rows, 0:1], AF.Sqrt, reads=["st2a", "eps"], writes=["st2b"], scale=1.0 / 64, bias=eps[0:rows, 0:1])
                            pg.op("dve", lambda e, rows=rows: e.reciprocal(st2[0:rows, 2:3], st2[0:rows, 1:2]), reads=["st2b"], writes=["st2c"])
                            pg.stt("dve", ksq[0:rows, :], ksq[0:rows, :], st2[0:rows, 2:3], kcg[0:rows, :], ALU.mult, ALU.mult,
                                   reads=["ksq", "st2c", "kcg"], writes=["ksq"])
                            src = ksq[0:rows, :].unsqueeze(1).to_broadcast([rows, 2, 64])
                            pg.cp("dve", kcn2[0:rows, :].rearrange("p (d k) -> p d k", d=2), src, reads=["ksq"], writes=["kcn2"])
                            pg.tr(psk[:, 0:128], kcn2[:, :], idb[:], reads=["kcn2", "idb"], writes=["psk"])
                            pg.cp("act", KcT[:, g, m * 128:(m + 1) * 128], psk[:, 0:128], reads=["psk"], writes=["KcT"])
                        else:
                            pg.cp("act", Vc[0:rows, m, g, 0:64], pso[0:rows, 0:64], reads=["pso", "Vc"], writes=["Vc"])
            for m in range(2):
                for g in range(2):
                    pg.memset("dve", Vc[:, m, g, 64:65], 1.0, writes=["Vc"])
                    pg.cp("dve", Vc[:, m, g, 65:129], ovl[:, m, :], reads=["ovl", "Vc"], writes=["Vc"])
            pg.barrier()
            pg.emit()
        if getattr(C, "lvl", 9) < 3:
            return
        stage3_attn(C, st, qT, KsT, KwT, Vs, Vw, KcT, Vc, GT, idf, idb)


def stage3_attn(C, st, qT, KsT, KwT, Vs, Vw, KcT, Vc, GT, idf, idb):
    nc, pg, dr = C.nc, C.pg, C.dr
    with ExitStack() as sb_:
        sb, ps = _mk(C, sb_)
        cmpb = sb("n_cmpb", [128, 2, T], BF16)
        Esel = sb("n_Esel", [128, 32, 128], BF16)
        causb = sb("n_causb", [128, 4, 512], BF16)
        winb = sb("n_winb", [128, 8, 512], BF16)
        selbT = sb("n_selbT", [128, 2, T], BF16)
        eT = [sb(f"n_eT{i}", [128, 512], BF16) for i in range(4)]
        eT2 = [sb(f"n_eT2{i}", [128, 512], BF16) for i in range(4)]
        Mt = [sb(f"n_Mt{i}", [128, 512], BF16) for i in range(2)]
        rm = [0]
        dq = []
        ocmp = sb("n_ocmp", [128, 4, 8, 64], F32)
        osel = sb("n_osel", [128, 4, 8, 64], F32)
        owin = sb("n_owin", [128, 4, 8, 64], F32)
        den = sb("n_den", [128, 16], F32)
        impw = sb("n_impw", [128, 2, 4, 64], F32)
        score = sb("n_score", [128, 2, 64], F32)
        VM = [sb(f"n_VM{i}", [128, 2, 64], F32) for i in range(2)]
        work = sb("n_work", [128, 2, 64], F32)
        m8 = sb("n_m8", [128, 2, 16], F32)
        thr = sb("n_thr", [128, 2], F32)
        msel = sb("n_msel", [128, 2, 64], F32)
        selb = sb("n_selb", [128, 2, 2, 64], BF16)
        osT = [sb(f"n_osT{i}", [65, 512], F32) for i in range(2)]
        dn2 = sb("n_dn2", [128, 8], F32)
        yb = sb("n_yb", [128, 8, 64], F32)
        yb2 = sb("n_yb2", [128, 8, 64], F32)
        psS = [ps(f"n_psS{i}", [128, 512], F32) for i in range(3)]
        psA = [ps(f"n_psA{i}", [128, 512], F32) for i in range(2)]
        psB = [ps(f"n_psB{i}", [128, 512], F32) for i in range(2)]
        psZ_ = ps("n_psZ", [128, 1024], BF16)
        psZ = psZ_[:, 0:256].rearrange("p (g q) -> p g q", g=2)

        pg.ld(cmpb[:], dr["c_cmpb"][:, :, :], writes=["cmpb"])
        pg.ld(Esel[:], dr["c_esel"][:, :, :], writes=["Esel"])
        pg.ld(causb[:], dr["c_causb"][:, :, :], writes=["causb"])
        pg.ld(winb[:], dr["c_winb"][:, :, :], writes=["winb"])
        winb4 = sb("n_winb4", [128, 8, 512], BF16)
        pg.ld(winb4[:], dr["c_winb4"][:, :, :], writes=["winb4"])
        rs = [0]
        re = [0]

        def nxt(lst, n):
            v = lst[0]
            lst[0] = (v + 1) % n
            return v

        def qk(h):
            return (h % 2) * 64, h // 2, h // 4

        for Q in range(4, 8):
            tq0 = Q * 512
            for ii in range(4):
                i = Q * 4 + ii
                t0 = i * 128
                s = i % 2
                pg.ld(VM[s][:], dr["c_vmfb"][i, :, :, :], writes=[("VM", s)])
                nm = 2 if i >= 16 else 1
                for h in range(8):
                    base, hp, g = qk(h)
                    h4 = h % 4
                    for m in range(nm):
                        r = nxt(rs, 3)
                        pS = psS[r]
                        pg.mm(pS[:, 0:128], KcT[base:base + 64, g, m * 128:(m + 1) * 128], qT[base:base + 64, hp, t0:t0 + 128],
                              start=True, stop=False, reads=["KcT", "qT"], writes=[("psS", r)])
                        pg.mm(pS[:, 0:128], idb[:, :], cmpb[:, m, t0:t0 + 128], start=False, stop=True,
                              reads=["idb", "cmpb"], writes=[("psS", r)])
                        k = nxt(re, 4)
                        pg.act(eT[k][:, 0:128], pS[:, 0:128], AF.Exp, reads=[("psS", r)], writes=[("eT", k)])
                        def pv(g=g, h4=h4, k=k, m=m, nm=nm):
                            pg.mm(psA[g][:, h4 * 65:h4 * 65 + 65], eT[k][:, 0:128], Vc[:, m, g, 0:65], start=(m == 0), stop=(m == nm - 1),
                                  reads=[("eT", k), "Vc"], writes=[("psA", g)])
                            pg.mm(psB[g][:, h4 * 64:h4 * 64 + 64], eT[k][:, 0:128], Vc[:, m, g, 65:129], start=(m == 0), stop=(m == nm - 1),
                                  reads=[("eT", k), "Vc"], writes=[("psB", g)])
                        dq.append(pv)
                        if len(dq) > 2:
                            dq.pop(0)()
                while dq:
                    dq.pop(0)()
                for g in range(2):
                    A3 = psA[g][:, 0:260].rearrange("p (h c) -> p h c", c=65)
                    B3 = psB[g][:, 0:256].rearrange("p (h c) -> p h c", c=64)
                    dsl = den[:, g * 4:(g + 1) * 4]
                    rsl = den[:, 8 + g * 4:8 + (g + 1) * 4]
                    pg.ts("dve", dsl, A3[:, :, 64], 1e-30, None, ALU.max, reads=[("psA", g)], writes=[("den", g)])
                    pg.op("dve", lambda e, o=rsl, a=dsl: e.reciprocal(o, a), reads=[("den", g)], writes=[("rden", g)])
                    rb = rsl.unsqueeze(2).to_broadcast([128, 4, 64])
                    pg.tt("dve", ocmp[:, ii, g * 4:(g + 1) * 4, :], A3[:, :, 0:64], rb, ALU.mult, reads=[("psA", g), ("rden", g)], writes=["ocmp"])
                    pg.tt("dve", impw[:, g, :, :], B3, rb, ALU.mult, reads=[("psB", g), ("rden", g)], writes=[("impw", g)])
                    pg.red("dve", score[:, g, :], impw[:, g, :, :].rearrange("p h j -> p j h"), ALU.add, reads=[("impw", g)], writes=[("score", g)])
                    vm = dr
                    pg.tt("dve", score[:, g, :], score[:, g, :], VM[s][:, 0, :], ALU.mult, reads=[("score", g), ("VM", s)], writes=[("score", g)])
                    pg.tt("dve", score[:, g, :], score[:, g, :], VM[s][:, 1, :], ALU.add, reads=[("score", g), ("VM", s)], writes=[("score", g)])
                    pg.op("dve", lambda e, g=g: e.max(m8[:, g, 0:8], score[:, g, :]), reads=[("score", g)], writes=[("m8a", g)])
                    pg.op("dve", lambda e, g=g: e.match_replace(work[:, g, :], m8[:, g, 0:8], score[:, g, :], -1e9),
                          reads=[("score", g), ("m8a", g)], writes=[("work", g)])
                    pg.op("dve", lambda e, g=g: e.max(m8[:, g, 8:16], work[:, g, :]), reads=[("work", g)], writes=[("m8b", g)])
                    pg.ts("dve", thr[:, g:g + 1], m8[:, g, 15:16], -0.5, None, ALU.max, reads=[("m8b", g)], writes=[("thr", g)])
                    pg.ts("dve", msel[:, g, :], score[:, g, :], thr[:, g:g + 1], None, ALU.is_ge, reads=[("score", g), ("thr", g)], writes=[("msel", g)])
                    pg.cp("dve", selb[:, g, :, :], msel[:, g, :].unsqueeze(1).to_broadcast([128, 2, 64]), reads=[("msel", g)], writes=[("selb", g)])
                    pg.tr(psZ[:, g, :], selb[:, g, :, :].rearrange("p d j -> p (d j)"), idb[:], reads=[("selb", g), "idb"], writes=["psZ"])
                pg.cp("act", selbT[:, :, t0:t0 + 128], psZ, reads=["psZ"], writes=["selbT"])
            for br in range(2):
                if getattr(C, "lvl", 9) < 4 + br:
                    continue
                dest = osel if br == 0 else owin
                dkey = "osel" if br == 0 else "owin"
                KT = KsT if br == 0 else KwT
                Vv = Vs if br == 0 else Vw
                kts = list(range(0, 4 * Q + 4)) if br == 0 else list(range(max(0, 4 * Q - 4), 4 * Q + 4))
                for g in range(2):
                    O = [psA[0], psA[1], psB[0], psB[1]]
                    okeys = [("psA", 0), ("psA", 1), ("psB", 0), ("psB", 1)]
                    for n_, kt in enumerate(kts):
                        if br == 0:
                            r = nxt(rs, 3)
                            pg.mm(psS[r][:, :], Esel[0:64, kt, :], selbT[0:64, g, tq0:tq0 + 512], reads=["Esel", "selbT"], writes=[("psS", r)])
                            mi = nxt(rm, 2)
                            if kt >= 4 * Q:
                                pg.tt("dve", Mt[mi][:], psS[r][:, :], causb[:, kt - 4 * Q, :], ALU.mult, reads=[("psS", r), "causb"], writes=[("Mt", mi)])
                            else:
                                pg.cp("dve", Mt[mi][:], psS[r][:, :], reads=[("psS", r)], writes=[("Mt", mi)])
                            mask, mkeys = Mt[mi][:], [("Mt", mi)]
                        else:
                            wsrc = winb4 if Q == 4 else winb
                            mask, mkeys = wsrc[:, kt - 4 * Q + 4, :], ["winb", "winb4"]
                        for h4 in range(4):
                            h = g * 4 + h4
                            base, hp, _g = qk(h)
                            r2 = nxt(rs, 3)
                            pg.mm(psS[r2][:, :], KT[base:base + 64, g, kt * 128:(kt + 1) * 128], qT[base:base + 64, hp, tq0:tq0 + 512],
                                  reads=["qT"], writes=[("psS", r2)])
                            k = nxt(re, 4)
                            pg.act(eT[k][:, :], psS[r2][:, :], AF.Exp, reads=[("psS", r2)], writes=[("eT", k)])
                            pg.tt("dve", eT2[k][:, :], eT[k][:, :], mask, ALU.mult, reads=[("eT", k)] + mkeys, writes=[("eT2", k)])
                            dq.append(lambda h4=h4, kt=kt, k=k, n_=n_, O=O, okeys=okeys, Vv=Vv, g=g, kts=kts: pg.mm(
                                O[h4][0:65, :], Vv[:, kt, g, :], eT2[k][:, :], start=(n_ == 0), stop=(n_ == len(kts) - 1),
                                reads=[("eT2", k)], writes=[okeys[h4]]))
                            if len(dq) > 2:
                                dq.pop(0)()
                    while dq:
                        dq.pop(0)()
                    for h4 in range(4):
                        h = g * 4 + h4
                        o = h4 % 2
                        pg.cp("act", osT[o][:, :], O[h4][0:65, :], reads=[okeys[h4]], writes=[("osT", o)])
                        r3 = nxt(rs, 3)
                        Tp = psS[r3]
                        for qq in range(4):
                            pg.tr(Tp[:, qq * 65:(qq + 1) * 65], osT[o][0:65, qq * 128:(qq + 1) * 128], idf[0:65, 0:65],
                                  reads=[("osT", o), "idf"], writes=[("psS", r3)])
                        T3 = Tp[:, 0:260].rearrange("p (q c) -> p q c", c=65)
                        pg.ts("dve", dn2[:, 0:4], T3[:, :, 64], 1e-30, None, ALU.max, reads=[("psS", r3)], writes=["dn2a"])
                        pg.op("dve", lambda e: e.reciprocal(dn2[:, 4:8], dn2[:, 0:4]), reads=["dn2a"], writes=["dn2b"])
                        pg.tt("dve", dest[:, :, h, :], T3[:, :, 0:64], dn2[:, 4:8].unsqueeze(2).to_broadcast([128, 4, 64]), ALU.mult,
                              reads=[("psS", r3), "dn2b"], writes=[dkey])
            for ii in range(4):
                i = Q * 4 + ii
                t0 = i * 128
                G3 = GT[:, i, :].rearrange("p (h c) -> p h c", c=3)
                gb = lambda c: G3[:, :, c].unsqueeze(2).to_broadcast([128, 8, 64])
                pg.tt("dve", yb[:], ocmp[:, ii, :, :], gb(0), ALU.mult, reads=["ocmp", "GT"], writes=["yb"])
                pg.tt("pool", yb2[:], osel[:, ii, :, :], gb(1), ALU.mult, reads=["osel", "GT"], writes=["yb2"])
                pg.tt("dve", yb[:], yb[:], yb2[:], ALU.add, reads=["yb", "yb2"], writes=["yb"])
                pg.tt("pool", yb2[:], owin[:, ii, :, :], gb(2), ALU.mult, reads=["owin", "GT", "yb2"], writes=["yb2"])
                pg.tt("dve", yb[:], yb[:], yb2[:], ALU.add, reads=["yb", "yb2"], writes=["yb"])
                pg.ld(dr["YB"][t0:t0 + 128, :], yb[:].rearrange("p h k -> p (h k)"), reads=["yb"], writes=[("YB", i)])
        pg.barrier()
        pg.emit()


_NSA_CONSTS = {}


def nsa_consts(hh=1):
    if hh in _NSA_CONSTS:
        return _NSA_CONSTS[hh]
    import ml_dtypes
    bf = ml_dtypes.bfloat16
    c = {}
    n = np.arange(256)
    t = np.arange(T)
    nlo = 128 if hh == 0 else 0
    cm = np.where((16 * n[:, None] + 31 <= t[None, :]) & (n[:, None] < 255) & (n[:, None] >= nlo), 0.0, NEG).astype(np.float32)
    c["c_cmpb"] = np.ascontiguousarray(cm.reshape(2, 128, T).transpose(1, 0, 2)).astype(bf)
    es = np.zeros((64, 32, 128), np.float32)
    for kt in range(32):
        for key in range(128):
            es[2 * kt + key // 64, kt, key] = 1.0
    c["c_esel"] = np.concatenate([es, es], 0).astype(bf)
    key = np.arange(128)
    q = np.arange(512)
    cb = np.zeros((128, 4, 512), np.float32)
    for d in range(4):
        cb[:, d, :] = np.where((d * 128 + key[:, None]) <= q[None, :], 1.0, 0.0)
    c["c_causb"] = cb.astype(bf)
    wb = np.zeros((128, 8, 512), np.float32)
    for r in range(8):
        ka = (r - 4) * 128 + key[:, None]
        wb[:, r, :] = np.where((ka <= q[None, :]) & (ka > q[None, :] - 512), 1.0, 0.0)
    c["c_winb"] = wb.astype(bf)
    wb4 = wb.copy()
    if hh == 0:
        wb4[:, 0:4, :] = 0.0
    c["c_winb4"] = wb4.astype(bf)
    cs = np.arange(256) * 16
    ss = np.arange(64) * 64
    ov = np.clip(np.minimum(cs[:, None] + 32, ss[None, :] + 64) - np.maximum(cs[:, None], ss[None, :]), 0, None) / 32.0
    ov[255, :] = 0.0
    c["ovl"] = np.ascontiguousarray(ov.reshape(2, 128, 64).transpose(1, 0, 2)).astype(np.float32)
    cur = t // 64
    j = np.arange(64)
    jlo = 32 if hh == 0 else 0
    valid = (j[None, :] <= cur[:, None]) & (j[None, :] >= jlo)
    forced = (j[None, :] == jlo) | (j[None, :] == cur[:, None]) | (j[None, :] == cur[:, None] - 1)
    vm = valid.astype(np.float32)
    fb = np.where(valid, 1000.0 * forced, -1.0).astype(np.float32)
    c["c_vmfb"] = np.ascontiguousarray(np.stack([vm, fb], 1).reshape(NT, 128, 2, 64))
    _NSA_CONSTS[hh] = c
    return c


def stage4(C):
    nc, pg, dr = C.nc, C.pg, C.dr
    with ExitStack() as st:
        sb, ps = _mk(C, st)
        wa = sb("m_wa", [128, 4, D], BF16)
        wb = sb("m_wb", [128, 4, D], BF16)
        wo = sb("m_wo", [128, 8, D], BF16)
        stg = sb("m_stg", [128, D], F32)
        idf = sb("m_idf", [128, 128], F32)
        idb = sb("m_idb", [128, 128], BF16)
        yab = [sb(f"m_yab{i}", [128, 1024], F32) for i in range(2)]
        yabb = sb("m_yabb", [128, 1024], BF16)
        yT = sb("m_yT", [128, 8, 128], BF16)
        gts = [sb(f"m_g{i}", [128, 2048], F32) for i in range(2)]
        xt = [sb(f"m_x{i}", [128, D], F32) for i in range(2)]
        mix = sb("m_mix", [128, D], F32)
        mix2 = sb("m_mix2", [128, D], F32)
        mixb = sb("m_mixb", [128, D], BF16)
        mT = sb("m_mT", [128, 8, 128], BF16)
        x1 = [sb(f"m_x1{i}", [128, D], F32) for i in range(2)]
        psT = ps("m_psT", [128, 1024], BF16)
        psm = [ps(f"m_psm{i}", [128, 512], F32) for i in range(4)]
        psT2 = ps("m_psT2", [128, 1024], BF16)
        pso = [ps(f"m_pso{i}", [128, 512], F32) for i in range(2)]

        pg.ld(idf[:], dr["ident"][:, :], writes=["idf"])
        pg.cp("dve", idb[:], idf[:], reads=["idf"], writes=["idb"])
        n = 0
        for (wt, nm, kcs) in ((wa, "w_branch_a", 4), (wb, "w_branch_b", 4), (wo, "w_out", 8)):
            for kc in range(kcs):
                pg.ld(stg[:], dr[nm][kc * 128:(kc + 1) * 128, :], writes=["stg"])
                pg.cp(("act", "dve", "pool")[n % 3], wt[:, kc, :], stg[:], reads=["stg"], writes=[nm])
                n += 1
        rowidx = sb("m_rowidx", [128, NT], I32)
        pg.ld(rowidx[:], dr["rowidx"][:, :], writes=["rowidx"])
        allk = lambda nm: [(nm, k) for k in range(NT)]
        for i in range(C.peer_tiles):
            s = i % 2
            t0 = i * 128
            ix = rowidx[:, i:i + 1]
            pg.ld(yab[s][:, 0:512], dr["YAL"][t0:t0 + 128, :], reads=[("YAL", i)], writes=[("yab", s)])
            igather(pg, yab[s][:, 512:1024], dr["YB"][:, :], ix, allk("YB") + ["rowidx"], [("yab2", s)])
            igather(pg, gts[s][:], dr["PG"][:, :], ix, allk("PG") + ["rowidx"], [("gts", s)])
            igather(pg, xt[s][:], dr["x"][:, :], ix, ["rowidx"], [("xt", s)])
            pg.cp("pool", yabb[:], yab[s][:], reads=[("yab", s), ("yab2", s)], writes=["yabb"])
            for j in range(8):
                pg.tr(psT[:, j * 128:(j + 1) * 128], yabb[:, j * 128:(j + 1) * 128], idb[:], reads=["yabb", "idb"], writes=["psT"])
            pg.cp("act", yT[:].rearrange("p a b -> p (a b)"), psT[:], reads=["psT"], writes=["yT"])
            for br in range(2):
                wt = wa if br == 0 else wb
                for nchunk in range(2):
                    pb = psm[br * 2 + nchunk]
                    for kc in range(4):
                        pg.mm(pb[:], yT[:, br * 4 + kc, :], wt[:, kc, nchunk * 512:(nchunk + 1) * 512], start=(kc == 0), stop=(kc == 3),
                              reads=["yT", "w_branch_a", "w_branch_b"], writes=[("psm", br * 2 + nchunk)])
            pg.act(gts[s][:], gts[s][:], AF.Sigmoid, reads=[("gts", s)], writes=[("gts", s)])
            for nchunk in range(2):
                c = slice(nchunk * 512, (nchunk + 1) * 512)
                pg.tt("dve", mix[:, c], psm[nchunk][:], gts[s][:, nchunk * 512:(nchunk + 1) * 512], ALU.mult,
                      reads=[("psm", nchunk), ("gts", s)], writes=[("mix", nchunk)])
                pg.tt("dve", mix2[:, c], psm[2 + nchunk][:], gts[s][:, 1024 + nchunk * 512:1024 + (nchunk + 1) * 512], ALU.mult,
                      reads=[("psm", 2 + nchunk), ("gts", s)], writes=[("mix2", nchunk)])
                pg.tt("pool", mixb[:, c], mix[:, c], mix2[:, c], ALU.add, reads=[("mix", nchunk), ("mix2", nchunk)], writes=[("mixb", nchunk)])
            for j in range(8):
                pg.tr(psT2[:, j * 128:(j + 1) * 128], mixb[:, j * 128:(j + 1) * 128], idb[:], reads=[("mixb", 0), ("mixb", 1), "idb"], writes=["psT2"])
            pg.cp("act", mT[:].rearrange("p a b -> p (a b)"), psT2[:], reads=["psT2"], writes=["mT"])
            for nchunk in range(2):
                for kc in range(8):
                    pg.mm(pso[nchunk][:], mT[:, kc, :], wo[:, kc, nchunk * 512:(nchunk + 1) * 512], start=(kc == 0), stop=(kc == 7),
                          reads=["mT", "w_out"], writes=[("pso", nchunk)])
                pg.tt("dve", x1[s][:, nchunk * 512:(nchunk + 1) * 512], pso[nchunk][:], xt[s][:, nchunk * 512:(nchunk + 1) * 512], ALU.add,
                      reads=[("pso", nchunk), ("xt", s)], writes=[("x1", s, nchunk)])
            pg.ld(dr["X1L"][t0:t0 + 128, :], x1[s][:], reads=[("x1", s, 0), ("x1", s, 1)], writes=[("X1L", i)])
        pg.barrier()
        pg.emit()


def table_conv_gen(C, sb):
    pg, dr = C.pg, C.dr
    NBUF = 4
    src = [sb(f"z_src{i}", [128, D], F32) for i in range(NBUF)]
    dst = [sb(f"z_dst{i}", [128, D], BF16) for i in range(NBUF)]
    n = 0
    for (tab, co) in (("peer_u", 0), ("peer_v", D)):
        for a in range(16384 // 128):
            b_ = n % NBUF
            pg.ld(src[b_][:], dr[tab][a * 128:(a + 1) * 128, :], writes=[("zsrc", b_)], q="sp")
            pg.cp("pool", dst[b_][:], src[b_][:], reads=[("zsrc", b_)], writes=[("zdst", b_)])
            pg.ld(dr["UV"][a * 128:(a + 1) * 128, co:co + D], dst[b_][:], reads=[("zdst", b_)], writes=[("UV", co, a)], q="act")
            n += 1
            yield


def stage5(C):
    nc, pg, dr = C.nc, C.pg, C.dr
    NB = 12
    with ExitStack() as st:
        sb, ps = _mk(C, st)
        wq = sb("p_wq", [128, 8, 2048], F32)
        kT = sb("p_kT", [128, 2, 128], F32)
        kraw = sb("p_kraw", [128, 2, 128], F32)
        g2 = sb("p_g2", [128, D], F32)
        idf = sb("p_idf", [128, 128], F32)
        io16 = sb("p_io16", [128, 16], F32)
        eps = sb("p_eps", [128, 1], F32)
        x1 = [sb(f"p_x1{i}", [128, D], F32) for i in range(3)]
        h2 = [sb(f"p_h2{i}", [128, D], F32) for i in range(2)]
        junk = sb("p_junk", [128, D], BF16)

        ss = sb("p_ss", [128, 4], F32)
        h2T = sb("p_h2T", [128, 8, 128], F32)
        qT = sb("p_qT", [128, 16, 128], F32)
        sc = sb("p_sc", [128, 16, 128], F32)
        work = sb("p_work", [128, 256], F32)
        tv = sb("p_tv", [128, 16, 16], F32)
        tiu = sb("p_tiu", [128, 16, 16], U32)
        ti = sb("p_ti", [128, 16, 16], F32)
        cs = sb("p_cs", [128, 8, 256], F32)
        bs = sb("p_bs", [128, 8, 16], F32)
        posu = sb("p_posu", [128, 8, 16], U32)
        pa_u = sb("p_pau", [128, 8, 16], U32)
        pb_u = sb("p_pbu", [128, 8, 16], U32)
        pa = sb("p_pa", [128, 8, 16], F32)
        pb = sb("p_pb", [128, 8, 16], F32)
        oh = sb("p_oh", [128, 8, 16, 16], F32)
        ia = sb("p_ia", [128, 8, 16], F32)
        ib = sb("p_ib", [128, 8, 16], F32)
        eidf = sb("p_eidf", [128, 128], F32)
        eidi = [sb(f"p_eidi{i}", [128, 128], I32) for i in range(3)]
        gate = [sb(f"p_gate{i}", [128, 128], F32) for i in range(2)]
        zz = sb("p_zz", [128, 16], F32)
        actv = [sb(f"p_act{i}", [128, 128], F32) for i in range(2)]
        ga = [sb(f"p_ga{i}", [128, 128], F32) for i in range(2)]
        uv = [sb(f"p_uv{i}", [128, 2 * D], BF16) for i in range(NB)]
        h2b = [sb(f"p_h2b{i}", [128, D], BF16) for i in range(2)]
        idb = sb("p_idb", [128, 128], BF16)
        junk2 = sb("p_junk2", [128, D], F32)
        dg = [sb(f"p_dg{i}", [128, 128], BF16) for i in range(4)]
        yo = [sb(f"p_yo{i}", [128, D], F32) for i in range(1)]
        psT = ps("p_psT", [128, 8, 128], F32)
        psQ = [ps(f"p_psQ{i}", [128, 512], F32) for i in range(2)]
        psY = [ps(f"p_psY{i}", [128, 512], F32) for i in range(2)]

        pg.ld(idf[:], dr["ident"][:, :], writes=["idf"])
        pg.ld(g2[:], dr["norm2_g_b"][:, :], writes=["g2"])
        pg.cp("dve", idb[:], idf[:], reads=["idf"], writes=["idb"])
        pg.ld(io16[:], dr["iota16"][:, :], writes=["io16"])
        rowidx = sb("p_rowidx", [128, NT], I32)
        pg.ld(rowidx[:], dr["rowidx"][:, :], writes=["rowidx"])
        pg.memset("dve", eps[:], 1e-6, writes=["eps"])
        for kc in range(8):
            pg.ld(wq[:, kc, :], dr["peer_wq"][kc * 128:(kc + 1) * 128, :], writes=["wq"])
        pg.ld(kraw[:, 0, :], dr["peer_k1"][:, :], writes=["kraw"])
        pg.ld(kraw[:, 1, :], dr["peer_k2"][:, :], writes=["kraw"])
        for hf in range(2):
            pg.tr(psQ[0][:, hf * 128:(hf + 1) * 128], kraw[:, hf, :], idf[:], reads=["kraw", "idf"], writes=[("psQ", 0)])
        pg.cp("dve", kT[:].rearrange("p a b -> p (a b)"), psQ[0][:, 0:256], reads=[("psQ", 0)], writes=["kT"])
        ntiles = getattr(C, "peer_tiles", NT)

        def front(i):
            s = i % 2
            t0 = i * 128
            pg.ld(x1[i % 3][:, :], dr["X1L"][t0:t0 + 128, :], reads=[("X1L", i)], writes=[("x1", i % 3)])
            yield
            pg.tt("pool", junk2[:], x1[i % 3][:], x1[i % 3][:], ALU.mult, reads=[("x1", i % 3), "junk2"], writes=["junk2"])
            yield
            pg.red("dve", ss[:, 0:1], junk2[:], ALU.add, reads=["junk2"], writes=["ss0"])
            yield
            pg.act(ss[:, 1:2], ss[:, 0:1], AF.Sqrt, reads=["ss0", "eps"], writes=["ss1"], scale=1.0 / D, bias=eps[:, 0:1])
            yield
            pg.op("dve", lambda e: e.reciprocal(ss[:, 2:3], ss[:, 1:2]), reads=["ss1"], writes=["ss2"])
            yield
            pg.stt("dve", h2[s][:], x1[i % 3][:], ss[:, 2:3], g2[:], ALU.mult, ALU.mult, reads=[("x1", i % 3), "ss2", "g2"], writes=[("h2", s)])
            yield
            pg.cp("pool", h2b[s][:], h2[s][:], reads=[("h2", s)], writes=[("h2b", s)])
            yield
            for j in range(8):
                pg.tr(psT[:, j, :], h2[s][:, j * 128:(j + 1) * 128], idf[:], reads=[("h2", s), "idf"], writes=["psT"])
                yield
            pg.cp("act", h2T[:], psT[:], reads=["psT"], writes=["h2T"])
            yield
            for cg in range(4):
                bk = psQ[cg % 2]
                for cc in range(4):
                    c = cg * 4 + cc
                    for kc in range(8):
                        pg.mm(bk[:, cc * 128:(cc + 1) * 128], wq[:, kc, c * 128:(c + 1) * 128], h2T[:, kc, :], start=(kc == 0), stop=(kc == 7),
                              reads=["wq", "h2T"], writes=[("psQ", cg % 2)])
                        yield
                pg.cp("act" if cg % 2 == 0 else "dve", qT[:, cg * 4:(cg + 1) * 4, :].rearrange("p a b -> p (a b)"), bk[:],
                      reads=[("psQ", cg % 2)], writes=[("qT", cg)])
                yield
            for cg in range(4):
                bk = psQ[cg % 2]
                for cc in range(4):
                    c = cg * 4 + cc
                    pg.mm(bk[:, cc * 128:(cc + 1) * 128], qT[:, c, :], kT[:, c % 2, :], reads=[("qT", cg), "kT"], writes=[("psQ", cg % 2)])
                    yield
                pg.cp("act" if cg % 2 == 0 else "dve", sc[:, cg * 4:(cg + 1) * 4, :].rearrange("p a b -> p (a b)"), bk[:],
                      reads=[("psQ", cg % 2)], writes=[("sc", cg)])
                yield
            for c in range(16):
                k_ = ("sc", c // 4)
                pg.op("dve", lambda e, c=c: e.max(tv[:, c, 0:8], sc[:, c, :]), reads=[k_], writes=[("tv", c)])
                yield
                pg.op("dve", lambda e, c=c: e.max_index(tiu[:, c, 0:8], tv[:, c, 0:8], sc[:, c, :]), reads=[k_, ("tv", c)], writes=[("tiu", c)])
                yield
                pg.op("dve", lambda e, c=c: e.match_replace(work[:, 0:128], tv[:, c, 0:8], sc[:, c, :], -1e30), reads=[k_, ("tv", c), "work"], writes=["work"])
                yield
                pg.op("dve", lambda e, c=c: e.max(tv[:, c, 8:16], work[:, 0:128]), reads=["work"], writes=[("tv2", c)])
                yield
                pg.op("dve", lambda e, c=c: e.max_index(tiu[:, c, 8:16], tv[:, c, 8:16], sc[:, c, :]), reads=[k_, ("tv2", c)], writes=[("tiu2", c)])
                yield
            allt = [("tv", c) for c in range(16)] + [("tv2", c) for c in range(16)]
            alli = [("tiu", c) for c in range(16)] + [("tiu2", c) for c in range(16)]
            pg.cp("dve", ti[:], tiu[:], reads=alli, writes=["ti"])
            yield
            tv4 = tv[:].rearrange("p (h f) a -> p h f a", f=2)
            ti4 = ti[:].rearrange("p (h f) a -> p h f a", f=2)
            cs4 = cs[:].rearrange("p h (a b) -> p h a b", a=16)
            A_ = lambda t4: t4[:, :, 0, :].unsqueeze(3).to_broadcast([128, 8, 16, 16])
            B_ = lambda t4: t4[:, :, 1, :].unsqueeze(2).to_broadcast([128, 8, 16, 16])
            pg.tt("dve", cs4, A_(tv4), B_(tv4), ALU.add, reads=allt, writes=["cs"])
            yield
            for h in range(8):
                pg.op("dve", lambda e, h=h: e.max(bs[:, h, 0:8], cs[:, h, :]), reads=["cs"], writes=[("bs", h)])
                yield
                pg.op("dve", lambda e, h=h: e.max_index(posu[:, h, 0:8], bs[:, h, 0:8], cs[:, h, :]), reads=["cs", ("bs", h)], writes=[("posu", h)])
                yield
                pg.op("dve", lambda e, h=h: e.match_replace(work[:, :], bs[:, h, 0:8], cs[:, h, :], -1e30), reads=["cs", ("bs", h), "work"], writes=["work"])
                yield
                pg.op("dve", lambda e, h=h: e.max(bs[:, h, 8:16], work[:, :]), reads=["work"], writes=[("bs2", h)])
                yield
                pg.op("dve", lambda e, h=h: e.max_index(posu[:, h, 8:16], bs[:, h, 8:16], cs[:, h, :]), reads=["cs", ("bs2", h)], writes=[("posu2", h)])
                yield
            allb = [("bs", h) for h in range(8)] + [("bs2", h) for h in range(8)]
            allp = [("posu", h) for h in range(8)] + [("posu2", h) for h in range(8)]
            G = gate[s][:].rearrange("p (h j) -> p h j", h=8)
            pg.tt("dve", G, bs[:], bs[:, :, 0:1].to_broadcast([128, 8, 16]), ALU.subtract, reads=allb, writes=[("gate", s)])
            yield
            pg.act(G, G, AF.Exp, reads=[("gate", s)], writes=[("gate", s)])
            yield
            pg.red("dve", zz[:, 0:8], G, ALU.add, reads=[("gate", s)], writes=["zz0"])
            yield
            pg.op("dve", lambda e: e.reciprocal(zz[:, 8:16], zz[:, 0:8]), reads=["zz0"], writes=["zz1"])
            yield
            pg.tt("dve", G, G, zz[:, 8:16].unsqueeze(2).to_broadcast([128, 8, 16]), ALU.mult, reads=[("gate", s), "zz1"], writes=[("gate", s)])
            yield
            pg.ts("dve", pa_u[:], posu[:], 4, None, ALU.logical_shift_right, reads=allp, writes=["pau"])
            yield
            pg.ts("dve", pb_u[:], posu[:], 15, None, ALU.bitwise_and, reads=allp, writes=["pbu"])
            yield
            pg.cp("dve", pa[:], pa_u[:], reads=["pau"], writes=["pa"])
            yield
            pg.cp("dve", pb[:], pb_u[:], reads=["pbu"], writes=["pb"])
            yield
            iob = io16[:, :].unsqueeze(1).unsqueeze(1).to_broadcast([128, 8, 16, 16])
            for (pp, key, half, dst, dk_) in ((pa, "pa", 0, ia, "ia"), (pb, "pb", 1, ib, "ib")):
                pg.tt("dve", oh[:], pp[:].unsqueeze(3).to_broadcast([128, 8, 16, 16]), iob, ALU.is_equal, reads=[key, "io16", "oh"], writes=["oh"])
                yield
                tsel = ti4[:, :, half, :].unsqueeze(2).to_broadcast([128, 8, 16, 16])
                pg.tt("dve", oh[:], oh[:], tsel, ALU.mult, reads=["oh", "ti"], writes=["oh"])
                yield
                pg.red("dve", dst[:], oh[:], ALU.add, reads=["oh"], writes=[dk_])
                yield
            pg.stt("dve", eidf[:].rearrange("p (h j) -> p h j", h=8), ia[:], 128.0, ib[:], ALU.mult, ALU.add, reads=["ia", "ib"], writes=["eidf"])
            yield
            pg.cp("dve", eidi[i % 3][:], eidf[:], reads=["eidf"], writes=[("eidi", i % 3)])
            yield

        GS = 4

        def gstep(i, e_):
            s = i % 2
            b_ = e_ % NB
            pg.dma("pool", lambda e, e_=e_, b_=b_, i=i: e.indirect_dma_start(
                out=uv[b_][:, :], out_offset=None, in_=dr["UV"][:, :],
                in_offset=bass.IndirectOffsetOnAxis(ap=eidi[i % 3][:, e_:e_ + 1], axis=0)),
                reads=[("eidi", i % 3)], writes=[("uv", b_)])
            pg.op("dve", lambda e, e_=e_, b_=b_, s=s: e.scalar_tensor_tensor(junk[:], uv[b_][:, 0:D], 1.0, h2b[s][:], ALU.mult, ALU.mult,
                                                                              accum_out=actv[s][:, e_:e_ + 1]),
                  reads=[("uv", b_), ("h2b", s)], writes=[("act", s, e_)])

        def gelu_grp(i, k):
            s = i % 2
            sl = slice(k * GS, (k + 1) * GS)
            pg.act(ga[s][:, sl], actv[s][:, sl], AF.Gelu, reads=[("act", s, e_) for e_ in range(k * GS, (k + 1) * GS)], writes=[("ga", s, k)])

        def fin_grp(i, k):
            s = i % 2
            sl = slice(k * GS, (k + 1) * GS)
            pg.tt("dve", ga[s][:, sl], ga[s][:, sl], gate[s][:, sl], ALU.mult, reads=[("ga", s, k), ("gate", s)], writes=[("ga", s, k)])
            for e_ in range(k * GS, (k + 1) * GS):
                b_ = e_ % NB
                d_ = e_ % 4
                pg.act(dg[d_][:], idb[:], AF.Copy, reads=[("ga", s, k), "idb"], writes=[("dg", d_)], scale=ga[s][:, e_:e_ + 1])
                for n_ in range(2):
                    pg.mm(psY[n_][:], dg[d_][:], uv[b_][:, D + n_ * 512:D + (n_ + 1) * 512], start=(e_ == 0), stop=(e_ == 127),
                          reads=[("dg", d_), ("uv", b_)], writes=[("psY", n_)])

        def tail(i):
            s = i % 2
            t0 = i * 128
            for n_ in range(2):
                pg.tt("dve", yo[0][:, n_ * 512:(n_ + 1) * 512], psY[n_][:], x1[i % 3][:, n_ * 512:(n_ + 1) * 512], ALU.add,
                      reads=[("psY", n_), ("x1", i % 3)], writes=[("yo", 0, n_)])
            pg.ld(dr["out"][t0:t0 + 128, :], yo[0][:], reads=[("yo", 0, 0), ("yo", 0, 1)], writes=[("out", i)])

        def drain(g, n=None):
            k = 0
            while g is not None and (n is None or k < n):
                try:
                    next(g)
                except StopIteration:
                    return None
                k += 1
            return g

        drain(front(0))
        for i in range(ntiles):
            gen2 = front(i + 1) if i + 1 < ntiles else None
            for k in range(128 // GS):
                for e_ in range(k * GS, (k + 1) * GS):
                    gstep(i, e_)
                    gen2 = drain(gen2, 3)
                gelu_grp(i, k)
                if k >= 1:
                    fin_grp(i, k - 1)
            fin_grp(i, 128 // GS - 1)
            drain(gen2)
            tail(i)
        pg.barrier()
        pg.emit()


_NC_CACHE = {}


def kernel(**inputs):
    inputs = {k: np.asarray(v) for k, v in inputs.items()}
    ntl = NT // 2
    if "nc" not in _NC_CACHE:
        _NC_CACHE["nc"] = build([stage1, stage2a, stage2x, stage2c, stage3, stage4, stage5], peer_tiles=ntl)
    nc = _NC_CACHE["nc"]
    base = {}
    in_maps = []
    for c in range(8):
        b, hh = c % 4, c // 4
        if hh not in base:
            base[hh] = host_inputs(inputs, b, hh, ntl)
            m = base[hh]
        else:
            m = dict(base[hh])
            m["x"] = core_x(inputs, b, hh)
        in_maps.append(m)
    res = run_bass_kernel_spmd(nc, in_maps, core_ids=list(range(8)))
    out = np.zeros((4, T, D), np.float32)
    for c in range(8):
        b, hh = c % 4, c // 4
        out[b, hh * ntl * 128:(hh + 1) * ntl * 128, :] = res.results[c]["out"]
    return out


def stage2x(C):
    nc, pg, dr = C.nc, C.pg, C.dr
    with ExitStack() as st:
        sb, ps = _mk(C, st)
        idf = sb("x_idf", [128, 128], F32)
        tri = sb("x_tri", [128, 128], F32)
        msk = sb("x_msk", [128, 3, 128], F32)
        ones = sb("x_ones", [128, 1], F32)
        inp = [[sb(f"x_in{s}_{j}", [128, 512], F32) for j in range(6)] for s in range(2)]
        Pt = sb("x_P", [128, 512], F32)
        iP = sb("x_iP", [128, 512], F32)
        Pp = sb("x_Pp", [128, 512], F32)
        tm = [[sb(f"x_tm{s}_{j}", [128, 512], F32) for j in range(4)] for s in range(2)]
        fm = [[sb(f"x_fm{s}_{j}", [64, 8, 128], F32) for j in range(4)] for s in range(2)]
        M = [[sb(f"x_M{s}_{j}", [128, 8, 128], (BF16 if j in (0, 4) else F32)) for j in range(5)] for s in range(2)]
        Xb = sb("x_Xb", [128, 8, 128], BF16)
        idb = sb("x_idb", [128, 128], BF16)
        X = [sb(f"x_X{s}", [128, 8, 128], F32) for s in range(2)]
        PC = [sb(f"x_PC{s}", [64, 8], F32) for s in range(2)]
        N2 = [sb(f"x_N2_{j}", [128, 8, 128], BF16) for j in range(2)]
        N2T = [sb(f"x_N2T_{j}", [128, 8, 128], BF16) for j in range(2)]
        Z = [sb(f"x_Z{j}", [64, 512], F32) for j in range(2)]
        rhs_sb = sb("x_rhs", [128, 512], F32)
        U_sb = sb("x_U", [128, 512], F32)
        Y_sb = [sb(f"x_Y{j}", [128, 512], F32) for j in range(2)]
        bank = [ps(f"x_bank{j}", [128, 512], F32) for j in range(8)]

        pg.ld(idf[:], dr["ident"][:, :], writes=["idf"])
        pg.ld(tri[:], dr["c_tri"][:, :], writes=["tri"])
        pg.ld(msk[:], dr["c_msk"][:, :, :], writes=["msk"])
        pg.memset("dve", ones[:], 1.0, writes=["ones"])
        pg.cp("dve", idb[:], idf[:], reads=["idf"], writes=["idb"])
        pg.memset("dve", Z[0][:], 0.0, writes=[("Z", 0)])
        names = ("RR", "RKK", "RLW", "RB", "RKp", "RV")
        bk = [0]

        def nb():
            v = bk[0]
            bk[0] = (v + 1) % 8
            return v

        def pre(c):
            s = c % 2
            t0 = c * 128
            I = inp[s]
            for j, nm in enumerate(names):
                pg.ld(I[j][:], dr[nm][t0:t0 + 128, :], reads=[(nm, c)], writes=[("in", s, j)])
            r_, kkn, lw, b_, kp, v_ = [t_[:] for t_ in I]
            bL = nb()
            pg.mm(bank[bL][:], tri[:], lw, reads=["tri", ("in", s, 2)], writes=[("bank", bL)])
            bC = nb()
            for h in range(8):
                pg.mm(bank[bC][0:64, h:h + 1], I[2][:, h * 64:(h + 1) * 64], ones[:, 0:1], reads=[("in", s, 2), "ones"], writes=[("bank", bC)])
            pg.act(PC[s][:], bank[bC][0:64, 0:8], AF.Exp, reads=[("bank", bC)], writes=[("PC", s)])
            pg.act(Pt[:], bank[bL][:], AF.Exp, reads=[("bank", bL)], writes=["P"])
            pg.act(iP[:], bank[bL][:], AF.Exp, reads=[("bank", bL)], writes=["iP"], scale=-1.0)
            pg.tt("dve", Pp[:], bank[bL][:], lw, ALU.subtract, reads=[("bank", bL), ("in", s, 2)], writes=["Pp"])
            pg.act(Pp[:], Pp[:], AF.Exp, reads=["Pp"], writes=["Pp"])
            TM = tm[s]
            pg.tt("pool", TM[0][:], r_, Pt[:], ALU.mult, reads=[("in", s, 0), "P"], writes=[("tm", s, 0)])
            pg.stt("dve", TM[1][:], kkn, -1.0, Pp[:], ALU.mult, ALU.mult, reads=[("in", s, 1), "Pp"], writes=[("tm", s, 1)])
            pg.tt("pool", TM[2][:], b_, iP[:], ALU.mult, reads=[("in", s, 3), "iP"], writes=[("tm", s, 2)])
            pg.tt("dve", TM[3][:], kp, iP[:], ALU.mult, reads=[("in", s, 4), "iP"], writes=[("tm", s, 3)])
            for j in range(4):
                for hg in range(2):
                    bT = nb()
                    for hh in range(4):
                        h = hg * 4 + hh
                        pg.tr(bank[bT][0:64, hh * 128:(hh + 1) * 128], TM[j][:, h * 64:(h + 1) * 64], idf[:], reads=[("tm", s, j), "idf"], writes=[("bank", bT)])
                    pg.cp("act" if (j + hg) % 2 == 0 else "dve", fm[s][j][:, hg * 4:(hg + 1) * 4, :].rearrange("p a b -> p (a b)"), bank[bT][0:64, :],
                          reads=[("bank", bT)], writes=[("fm", s, j, hg)])
            FR, FKK, FB, FK = fm[s]
            combos = ((0, FB, 2, FKK, 1, 0), (1, FK, 3, FKK, 1, 0), (2, FB, 2, FR, 0, 1), (3, FK, 3, FR, 0, 1), (4, FKK, 1, FB, 2, 2))
            for hg in range(2):
                for (mi, L_, lj, R_, rj, mk) in combos:
                    bM = nb()
                    for hh in range(4):
                        h = hg * 4 + hh
                        pg.mm(bank[bM][:, hh * 128:(hh + 1) * 128], L_[:, h, :], R_[:, h, :], reads=[("fm", s, lj, hg), ("fm", s, rj, hg)], writes=[("bank", bM)])
                    pg.tt("dve", M[s][mi][:, hg * 4:(hg + 1) * 4, :], bank[bM][:].rearrange("p (a b) -> p a b", a=4),
                          msk[:, mk, :].unsqueeze(1).to_broadcast([128, 4, 128]), ALU.mult, reads=[("bank", bM), "msk"], writes=[("M", s, mi, hg)])
                pg.tt("pool", Xb[:, hg * 4:(hg + 1) * 4, :], idb[:, :].unsqueeze(1).to_broadcast([128, 4, 128]), M[s][0][:, hg * 4:(hg + 1) * 4, :], ALU.subtract,
                      reads=[("M", s, 0, hg), "idb"], writes=[("Xb", hg)])
            curN = [M[s][0], M[s][0]]
            curNT = [M[s][4], M[s][4]]
            kN = [("M", s, 0, 0), ("M", s, 0, 1)]
            kNT = [("M", s, 4, 0), ("M", s, 4, 1)]
            for j in range(6):
                dst = j % 2
                for hg in range(2):
                    b1, b2 = nb(), nb()
                    for hh in range(4):
                        h = hg * 4 + hh
                        pg.mm(bank[b1][:, hh * 128:(hh + 1) * 128], curNT[hg][:, h, :], curN[hg][:, h, :], reads=[kN[hg], kNT[hg]], writes=[("bank", b1)])
                    for hh in range(4):
                        h = hg * 4 + hh
                        pg.mm(bank[b2][:, hh * 128:(hh + 1) * 128], curN[hg][:, h, :], curNT[hg][:, h, :], reads=[kN[hg], kNT[hg]], writes=[("bank", b2)])
                    pg.cp("act", N2[dst][:, hg * 4:(hg + 1) * 4, :].rearrange("p a b -> p (a b)"), bank[b1][:], reads=[("bank", b1)], writes=[("N2", dst, hg)])
                    pg.cp("dve", N2T[dst][:, hg * 4:(hg + 1) * 4, :].rearrange("p a b -> p (a b)"), bank[b2][:], reads=[("bank", b2)], writes=[("N2T", dst, hg)])
                for hg in range(2):
                    curN[hg], curNT[hg] = N2[dst], N2T[dst]
                    kN[hg], kNT[hg] = ("N2", dst, hg), ("N2T", dst, hg)
                for hg in range(2):
                    b3 = nb()
                    for hh in range(4):
                        h = hg * 4 + hh
                        pg.mm(bank[b3][:, hh * 128:(hh + 1) * 128], curNT[hg][:, h, :], Xb[:, h, :], reads=[kNT[hg], ("Xb", hg)], writes=[("bank", b3)])
                    xo = (X[s] if j == 5 else Xb)
                    pg.tt("dve", xo[:, hg * 4:(hg + 1) * 4, :].rearrange("p a b -> p (a b)"), Xb[:, hg * 4:(hg + 1) * 4, :].rearrange("p a b -> p (a b)"), bank[b3][:], ALU.add,
                          reads=[("bank", b3), ("Xb", hg)], writes=[("X", s, hg)] if j == 5 else [("Xb", hg)])

        def seq(c):
            s = c % 2
            t0 = c * 128
            zc, zn = Z[c % 2], Z[(c + 1) % 2]
            kz, kzn = ("Z", c % 2), ("Z", (c + 1) % 2)
            FR, FKK, FB, FK = fm[s]
            V = inp[s][5]
            hsl = lambda h: slice(h * 64, (h + 1) * 64)
            Mk = lambda mi: [("M", s, mi, 0), ("M", s, mi, 1)]
            fk = lambda j: [("fm", s, j, 0), ("fm", s, j, 1)]
            Xk = [("X", s, 0), ("X", s, 1)]
            bG = nb()
            for h in range(8):
                pg.mm(bank[bG][:, hsl(h)], M[s][1][:, h, :], V[:, hsl(h)], start=True, stop=False, reads=Mk(1) + [("in", s, 5)], writes=[("bank", bG)])
                pg.mm(bank[bG][:, hsl(h)], FKK[:, h, :], zc[:, hsl(h)], start=False, stop=True, reads=fk(1) + [kz], writes=[("bank", bG)])
            pg.ts("dve", rhs_sb[:], bank[bG][:], -1.0, None, ALU.mult, reads=[("bank", bG)], writes=["rhs"])
            bU = nb()
            for h in range(8):
                pg.mm(bank[bU][:, hsl(h)], X[s][:, h, :], rhs_sb[:, hsl(h)], reads=Xk + ["rhs"], writes=[("bank", bU)])
            pg.cp("act", U_sb[:], bank[bU][:], reads=[("bank", bU)], writes=["U"])
            bZ = nb()
            for h in range(8):
                pg.mm(bank[bZ][0:64, hsl(h)], tm[s][3][:, hsl(h)], V[:, hsl(h)], start=True, stop=False, reads=[("tm", s, 3), ("in", s, 5)], writes=[("bank", bZ)])
                pg.mm(bank[bZ][0:64, hsl(h)], idf[0:64, 0:64], zc[:, hsl(h)], start=False, stop=False, reads=["idf", kz], writes=[("bank", bZ)])
                pg.mm(bank[bZ][0:64, hsl(h)], tm[s][2][:, hsl(h)], U_sb[:, hsl(h)], start=False, stop=True, reads=[("tm", s, 2), "U"], writes=[("bank", bZ)])
            pg.tt("dve", zn[:].rearrange("p (h v) -> p h v", h=8), bank[bZ][0:64, :].rearrange("p (h v) -> p h v", h=8),
                  PC[s][:, :].unsqueeze(2).to_broadcast([64, 8, 64]), ALU.mult, reads=[("bank", bZ), ("PC", s)], writes=[kzn])
            if c < NT // 2:
                return
            bY = nb()
            for h in range(8):
                pg.mm(bank[bY][:, hsl(h)], M[s][3][:, h, :], V[:, hsl(h)], start=True, stop=False, reads=Mk(3) + [("in", s, 5)], writes=[("bank", bY)])
                pg.mm(bank[bY][:, hsl(h)], FR[:, h, :], zc[:, hsl(h)], start=False, stop=False, reads=fk(0) + [kz], writes=[("bank", bY)])
                pg.mm(bank[bY][:, hsl(h)], M[s][2][:, h, :], U_sb[:, hsl(h)], start=False, stop=True, reads=Mk(2) + ["U"], writes=[("bank", bY)])
            pg.cp("act", Y_sb[s][:], bank[bY][:], reads=[("bank", bY)], writes=[("Y", s)])
            pg.ld(dr["YS"][t0:t0 + 128, :], Y_sb[s][:], reads=[("Y", s)], writes=[("YS", c)])

        tcg = table_conv_gen(C, sb)
        pre(0)
        for c in range(NT):
            if c + 1 < NT:
                pre(c + 1)
            for _ in range(8):
                next(tcg, None)
            seq(c)
        for _ in tcg:
            pass
        pg.barrier()
        pg.emit()
```

```python
import numpy as np
import concourse.bass as bass
import concourse.mybir as mybir

F32 = mybir.dt.float32
BF16 = mybir.dt.bfloat16
I32 = mybir.dt.int32
U32 = mybir.dt.uint32
ALU = mybir.AluOpType
AF = mybir.ActivationFunctionType
AX = mybir.AxisListType

EPOCH = 20000
ENGS = ("pe", "act", "dve", "pool", "sp")
NDMASEM = 16


class Prog:
    def __init__(self, nc, stack):
        self.nc = nc
        self.stack = stack
        self.ops = {e: [] for e in ENGS}
        self.cnt = {e: 0 for e in ENGS}
        self.esems = {e: [] for e in ENGS}
        self.waited = {e: {} for e in ENGS}
        self.lastw = {}
        self.readers = {}
        self.dsems = {}
        self.dcount = {}
        self.dtarget = {}
        self.semobjs = {}
        self.alltokens = {}
        for q in ("sp", "act", "pool"):
            self.dsems[q] = [self._newsem(f"d_{q}_{i}") for i in range(NDMASEM)]
            self.dcount[q] = 0
            self.dtarget[q] = [0] * NDMASEM

    def _newsem(self, name):
        s = self.stack.enter_context(self.nc.semaphore(name))
        self.semobjs[name] = s
        return name

    def _esem(self, e, idx):
        ep = idx // EPOCH
        while len(self.esems[e]) <= ep:
            self.esems[e].append(self._newsem(f"e_{e}_{len(self.esems[e])}"))
        return self.esems[e][ep], (idx % EPOCH) + 1

    def _deps(self, reads, writes):
        toks = []
        for k in reads:
            t = self.lastw.get(k)
            if t is not None:
                toks.append(t)
        for k in writes:
            t = self.lastw.get(k)
            if t is not None:
                toks.append(t)
            toks.extend(self.readers.get(k, ()))
        return toks

    def _commit(self, tok, reads, writes):
        for k in reads:
            self.readers.setdefault(k, []).append(tok)
        for k in writes:
            self.lastw[k] = tok
            self.readers[k] = []
        self.alltokens[tok[0]] = max(self.alltokens.get(tok[0], 0), tok[1])

    def _waits(self, e, toks):
        need = {}
        for (s, v) in toks:
            if v > need.get(s, 0):
                need[s] = v
        out = []
        w = self.waited[e]
        for s, v in need.items():
            if w.get(s, 0) < v:
                w[s] = v
                out.append((s, v))
        return out

    def op(self, e, fn, reads=(), writes=()):
        toks = self._deps(reads, writes)
        if e == "pe":
            toks = [t for t in toks if not t[0].startswith("e_pe_")]
        waits = self._waits(e, toks)
        idx = self.cnt[e]
        self.cnt[e] += 1
        tok = self._esem(e, idx)
        self.ops[e].append((waits, fn, (tok[0], 1)))
        self._commit(tok, reads, writes)

    def dma(self, q, fn, reads=(), writes=()):
        toks = self._deps(reads, writes)
        n = self.dcount[q]
        self.dcount[q] += 1
        slot = n % NDMASEM
        sname = self.dsems[q][slot]
        prev = self.dtarget[q][slot]
        if prev > 0:
            toks.append((sname, prev))
        tgt = prev + 16
        self.dtarget[q][slot] = tgt
        waits = self._waits(q, toks)
        tok = (sname, tgt)
        self.ops[q].append((waits, fn, (sname, 16)))
        self._commit(tok, reads, writes)

    def mm(self, out, lhsT, rhs, start=True, stop=True, reads=(), writes=()):
        self.op("pe", lambda e: e.matmul(out, lhsT, rhs, start=start, stop=stop), reads, writes)

    def tr(self, out, in_, ident, reads=(), writes=()):
        self.op("pe", lambda e: e.transpose(out, in_, ident), reads, writes)

    def act(self, out, in_, func, reads=(), writes=(), bias=None, scale=None, eng="act"):
        kw = {}
        if bias is not None:
            kw["bias"] = bias
        if scale is not None:
            kw["scale"] = scale
        self.op(eng, lambda e: e.activation(out, in_, func, **kw), reads, writes)

    def tt(self, eng, out, in0, in1, op, reads=(), writes=()):
        self.op(eng, lambda e: e.tensor_tensor(out, in0, in1, op), reads, writes)

    def ts(self, eng, out, in0, s1, s2, op0, op1=None, reads=(), writes=()):
        if op1 is None:
            self.op(eng, lambda e: e.tensor_scalar(out, in0, s1, s2, op0), reads, writes)
        else:
            self.op(eng, lambda e: e.tensor_scalar(out, in0, s1, s2, op0, op1), reads, writes)

    def stt(self, eng, out, in0, scalar, in1, op0, op1, reads=(), writes=()):
        self.op(eng, lambda e: e.scalar_tensor_tensor(out, in0, scalar, in1, op0, op1), reads, writes)

    def cp(self, eng, out, in_, reads=(), writes=()):
        if eng == "act":
            self.op(eng, lambda e: e.copy(out, in_), reads, writes)
        else:
            self.op(eng, lambda e: e.tensor_copy(out, in_), reads, writes)

    def red(self, eng, out, in_, op, reads=(), writes=(), axis=None):
        ax = AX.X if axis is None else axis
        self.op(eng, lambda e: e.tensor_reduce(out, in_, ax, op), reads, writes)

    def memset(self, eng, ap, val, writes=()):
        self.op(eng, lambda e: e.memset(ap, val), (), writes)

    def ld(self, out, in_, reads=(), writes=(), q="sp"):
        self.dma(q, lambda e: e.dma_start(out, in_), reads, writes)

    def barrier(self):
        toks = list(self.alltokens.items())
        for e in ENGS:
            waits = self._waits(e, toks)
            if waits:
                self.ops[e].append((waits, None, None))
        self.lastw = {}
        self.readers = {}

    def emit(self):
        nc = self.nc
        so = self.semobjs
        with nc.Block() as block:
            def mk(e):
                def body(eng):
                    for waits, fn, inc in self.ops[e]:
                        for (s, v) in waits:
                            eng.wait_ge(so[s], v)
                        if fn is not None:
                            ins = fn(eng)
                            ins.then_inc(so[inc[0]], inc[1])
                return body
            block.tensor(mk("pe"))
            block.scalar(mk("act"))
            block.vector(mk("dve"))
            block.gpsimd(mk("pool"))
            block.sync(mk("sp"))
        self.ops = {e: [] for e in ENGS}
from contextlib import ExitStack
from concourse.bass_utils import run_bass_kernel_spmd

T = 4096
D = 1024
NT = T // 128
INW = 5144
RWC = 1792
O_RW = 0
O_Q = 1792
O_KC = 2304
O_VC = 2432
O_KS = 2560
O_VS = 2688
O_KW = 2816
O_VW = 2944
O_BG = 3072
O_GA = 3096
O_GB = 4120


class Ctx:
    pass


def _mk(C, st):
    nc = C.nc
    sb = lambda name, shape, dt: st.enter_context(nc.sbuf_tensor(name, shape, dt))
    ps = lambda name, shape, dt: st.enter_context(nc.psum_tensor(name, shape, dt))
    return sb, ps


def stage1(C):
    nc, pg, dr = C.nc, C.pg, C.dr
    with ExitStack() as st:
        sb, ps = _mk(C, st)
        win = sb("s1_win", [128, 8, INW], BF16)
        pj = [sb(f"s1_pj{i}", [128, INW], F32) for i in range(2)]
        xt = [sb(f"s1_xt{i}", [128, D], F32) for i in range(2)]
        junk = sb("s1_junk", [128, D], F32)
        hb = [sb(f"s1_h{i}", [128, D], BF16) for i in range(2)]
        hT = [sb(f"s1_hT{i}", [128, 8, 128], BF16) for i in range(2)]
        gt = sb("s1_g", [128, D], F32)
        idf = sb("s1_idf", [128, 128], F32)
        idb = sb("s1_idb", [128, 128], BF16)
        ss = [sb(f"s1_ss{i}", [128, 4], F32) for i in range(2)]
        psT = [ps(f"s1_psT{i}", [128, 8, 128], BF16) for i in range(2)]
        psm = [ps(f"s1_psm{i}", [128, 512], F32) for i in range(4)]

        pg.ld(gt[:], dr["norm1_g_b"][:, :], writes=["gt"])
        pg.ld(idf[:], dr["ident"][:, :], writes=["idf"])
        pg.cp("dve", idb[:], idf[:], reads=["idf"], writes=["idb"])
        engs = ["act", "dve", "pool"]
        for kc in range(8):
            b = pj[kc % 2]
            pg.ld(b[:], dr["w_in"][kc * 128:(kc + 1) * 128, :], writes=[("pjall", kc % 2)])
            pg.cp(engs[kc % 3], win[:, kc, :], b[:], reads=[("pjall", kc % 2)], writes=[("win", kc)])
        winkeys = [("win", kc) for kc in range(8)]
        chunks = []
        c0 = 0
        while c0 < INW:
            w = min(512, INW - c0)
            chunks.append((c0, w))
            c0 += w
        def A1(i):
            s = i % 2
            pg.ld(xt[s][:], dr["x"][i * 128:(i + 1) * 128, :], writes=[("xt", s)])
            pg.tt("dve", junk[:], xt[s][:], xt[s][:], ALU.mult, reads=[("xt", s)], writes=["junk"])
            pg.red("dve", ss[s][:, 0:1], junk[:], ALU.add, reads=["junk"], writes=[("ss", s)])
            pg.act(ss[s][:, 1:2], ss[s][:, 0:1], AF.Sqrt, reads=[("ss", s)], writes=[("ss1", s)],
                   scale=1.0 / D, bias=C.eps6[:, 0:1])
            pg.op("dve", lambda e, o=ss[s][:, 2:3], a=ss[s][:, 1:2]: e.reciprocal(o, a),
                  reads=[("ss1", s)], writes=[("ss2", s)])
            pg.stt("dve", hb[s][:], xt[s][:], ss[s][:, 2:3], gt[:], ALU.mult, ALU.mult,
                   reads=[("xt", s), ("ss2", s), "gt"], writes=[("hb", s)])

        def A2(i):
            s = i % 2
            for j in range(8):
                pg.tr(psT[s][:, j, :], hb[s][:, j * 128:(j + 1) * 128], idb[:],
                      reads=[("hb", s), "idb"], writes=[("psT", s)])
            pg.cp("act", hT[s][:], psT[s][:], reads=[("psT", s)], writes=[("hT", s)])

        chunks_lo = [(512, 512), (1024, 512), (1536, 128), (2304, 512), (2816, 256)]

        def B(i, lo, hi):
            s = i % 2
            if i < NT // 2 - 1:
                if lo != 0:
                    return
                for ci, (c0, w) in enumerate(chunks_lo):
                    pb = psm[ci % 4]
                    for kc in range(8):
                        pg.mm(pb[:, :w], hT[s][:, kc, :], win[:, kc, c0:c0 + w], start=(kc == 0), stop=(kc == 7),
                              reads=[("hT", s), ("win", kc)], writes=[("psm", ci % 4)])
                    pg.cp("act" if ci % 2 == 0 else "dve", pj[s][:, c0:c0 + w], pb[:, :w],
                          reads=[("psm", ci % 4)], writes=[("pj", s, k_) for k_ in range(len(chunks))] + ([("pjall", s)] if i < 8 else []))
                return
            for ci in range(lo, hi):
                c0, w = chunks[ci]
                pb = psm[ci % 4]
                for kc in range(8):
                    pg.mm(pb[:, :w], hT[s][:, kc, :], win[:, kc, c0:c0 + w], start=(kc == 0), stop=(kc == 7),
                          reads=[("hT", s), ("win", kc)], writes=[("psm", ci % 4)])
                pg.cp("act" if ci % 2 == 0 else "dve", pj[s][:, c0:c0 + w], pb[:, :w],
                      reads=[("psm", ci % 4)], writes=[("pj", s, ci), ("pjall", s)] if i < 8 else [("pj", s, ci)])

        def S(i):
            s = i % 2
            pg.ld(dr["P"][i * 128:(i + 1) * 128, 0:3096], pj[s][:, 0:3096],
                  reads=[("pj", s, ci) for ci in range(len(chunks))], writes=[("P", i)])
            pg.ld(dr["PG"][i * 128:(i + 1) * 128, :], pj[s][:, 3096:5144],
                  reads=[("pj", s, ci) for ci in range(len(chunks))], writes=[("PG", i)], q="act")

        A1(0)
        A2(0)
        A1(1)
        for i in range(NT):
            B(i, 0, 6)
            if i + 1 < NT:
                A2(i + 1)
            if i + 2 < NT:
                A1(i + 2)
            B(i, 6, len(chunks))
            S(i)
        pg.barrier()
        pg.emit()


def build(stages, dbg_out=(), dbg_in=(), lvl=9, sub=9, peer_tiles=NT):
    nc = bass.Bass("TRN2", target_bir_lowering=False)
    C = Ctx()
    C.peer_tiles = peer_tiles
    C.lvl = lvl
    C.sub = sub
    C.nc = nc
    dr = {}
    C.dr = dr

    def din(name, shape, dt=F32):
        dr[name] = nc.dram_tensor(name, list(shape), dt, kind="ExternalInput").ap()

    def dscr(name, shape, dt=F32):
        kind = "ExternalOutput" if name in dbg_out else ("ExternalInput" if name in dbg_in else "Internal")
        dr[name] = nc.dram_tensor(name, list(shape), dt, kind=kind).ap()

    din("x", [T, D])
    din("norm1_g_b", [128, D])
    din("ident", [128, 128])
    din("w_in", [D, INW])
    dscr("P", [T, INW])
    dscr("PG", [T, 2048])
    for nm in ("rw_mu_b",):
        din(nm, [128, RWC])
    for nm in ("rw_w0_b", "rw_a0_b", "rw_k_k_b", "rw_k_a_b", "rw_r_k_b", "rw_ln_w_b", "rw_ln_b_b", "rw_g_up"):
        din(nm, [128, 512])
    din("rw_w_up", [64, 512])
    din("rw_a_up", [64, 512])
    for nm in ("RB", "RKp", "RV", "RG", "YA", "RR", "RKK", "RLW", "YS"):
        dscr(nm, [T, 512])
    dscr("RBON", [T, 8])
    din("blkmask", [8, 512])
    din("c_tri", [128, 128])
    din("c_msk", [128, 3, 128])
    din("nsa_gains_b", [128, 768])
    din("nsa_kc_g_b", [128, 64])
    din("ovl", [128, 2, 64])
    din("posT", [128, 2, 32])
    din("cmp_w2", [128, 2, 2, 64])
    din("cmp_w1", [2, 128, 32, 256])
    din("c_cmpb", [128, 2, T], BF16)
    din("c_esel", [128, 32, 128], BF16)
    din("c_causb", [128, 4, 512], BF16)
    din("c_winb", [128, 8, 512], BF16)
    din("c_winb4", [128, 8, 512], BF16)
    din("c_vmfb", [NT, 128, 2, 64])
    dscr("YB", [T, 512])
    din("w_branch_a", [512, D])
    din("w_branch_b", [512, D])
    din("w_out", [D, D])
    dscr("X1L", [peer_tiles * 128, D])
    dscr("YAL", [peer_tiles * 128, 512])
    din("norm2_g_b", [128, D])
    din("iota16", [128, 16])
    din("rowidx", [128, NT], I32)
    din("peer_wq", [D, 2048])
    din("peer_k1", [128, 128])
    din("peer_k2", [128, 128])
    din("peer_u", [16384, D])
    din("peer_v", [16384, D])
    dscr("UV", [16384, 2 * D], BF16)
    dr["out"] = nc.dram_tensor("out", [peer_tiles * 128, D], F32, kind="ExternalOutput").ap()
    with ExitStack() as top:
        pg = Prog(nc, top)
        C.pg = pg
        C.eps6 = top.enter_context(nc.sbuf_tensor("c_eps6", [128, 1], F32))
        pg.memset("dve", C.eps6[:], 1e-6, writes=["eps6"])
        pg.barrier()
        for s in stages:
            s(C)
        pg.barrier()
        pg.emit()
    return nc


def core_x(inputs, b, hh):
    xb = np.asarray(inputs["x"][b])
    if hh == 0:
        return np.ascontiguousarray(np.concatenate([np.zeros((T // 2, D), np.float32), xb[0:T // 2]], 0))
    return np.ascontiguousarray(xb)


def host_inputs(inputs, b, hh=1, ntl=NT):
    g = lambda k: np.ascontiguousarray(inputs[k][0])
    m = {}
    m["x"] = core_x(inputs, b, hh)
    m["norm1_g_b"] = np.ascontiguousarray(np.broadcast_to(g("norm1_g")[None, :], (128, D)))
    m["ident"] = np.eye(128, dtype=np.float32)
    m["w_in"] = g("w_in")
    bc = lambda a: np.ascontiguousarray(np.broadcast_to(np.asarray(a).reshape(1, -1), (128, a.size)))
    m["rw_mu_b"] = bc(g("rw_mu"))
    for nm in ("rw_w0", "rw_a0", "rw_k_k", "rw_k_a", "rw_r_k", "rw_ln_w", "rw_ln_b"):
        m[nm + "_b"] = bc(g(nm))
    for nm in ("rw_g_up", "rw_w_up", "rw_a_up"):
        m[nm] = g(nm)
    bmk = np.zeros((8, 512), np.float32)
    for h in range(8):
        bmk[h, h * 64:(h + 1) * 64] = 1.0
    m["blkmask"] = bmk
    ii = np.arange(128)
    m["c_tri"] = (ii[:, None] <= ii[None, :]).astype(np.float32)
    m["c_msk"] = np.ascontiguousarray(np.stack([(ii[:, None] < ii[None, :]), (ii[:, None] <= ii[None, :]), (ii[:, None] > ii[None, :])], 1).astype(np.float32))
    m.update(nsa_consts(hh))
    for nm in ("w_branch_a", "w_branch_b", "w_out", "peer_wq", "peer_k1", "peer_k2", "peer_u", "peer_v"):
        m[nm] = g(nm)
    m["norm2_g_b"] = bc(g("norm2_g"))
    ri = np.zeros((128, NT), np.int32)
    ri[:, :ntl] = ((NT - ntl) * 128 + np.arange(ntl)[None, :] * 128 + np.arange(128)[:, None]).astype(np.int32)
    m["rowidx"] = ri
    m["iota16"] = np.ascontiguousarray(np.broadcast_to(np.arange(16, dtype=np.float32)[None, :], (128, 16)))
    m["nsa_gains_b"] = bc(np.concatenate([np.tile(g("nsa_q_g"), 8), np.tile(g("nsa_ks_g"), 2), np.tile(g("nsa_kw_g"), 2)]))
    m["nsa_kc_g_b"] = bc(g("nsa_kc_g"))
    posT = np.zeros((128, 2, 32), np.float32)
    posT[0:64, 0, :] = g("cmp_pos_k").T
    posT[0:64, 1, :] = g("cmp_pos_v").T
    m["posT"] = posT
    w2 = np.stack([g("cmp_k_w2").reshape(2, 128, 64), g("cmp_v_w2").reshape(2, 128, 64)], 0)
    m["cmp_w2"] = np.ascontiguousarray(w2.transpose(2, 0, 1, 3))
    w1 = []
    for nm in ("cmp_k_w1", "cmp_v_w1"):
        a = g(nm).reshape(32, 64, 256).transpose(1, 0, 2)
        w1.append(np.concatenate([a, a], 0))
    m["cmp_w1"] = np.ascontiguousarray(np.stack(w1, 0))
    return m


def dap(ap, offset, pattern):
    return bass.AP(ap.tensor, offset, [list(p) for p in pattern])


def stage2a(C):
    nc, pg, dr = C.nc, C.pg, C.dr
    with ExitStack() as st:
        sb, ps = _mk(C, st)
        mu = sb("a_mu", [128, RWC], F32)
        w0 = sb("a_w0", [128, 512], F32)
        a0 = sb("a_a0", [128, 512], F32)
        kkc = sb("a_kk", [128, 512], F32)
        kac = sb("a_ka", [128, 512], F32)
        rkc = sb("a_rk", [128, 512], F32)
        wup = sb("a_wup", [128, 512], F32)
        gup = sb("a_gup", [128, 512], F32)
        idf = sb("a_idf", [128, 128], F32)
        p_2 = [sb(f"a_p{i_}", [128, RWC], F32) for i_ in range(2)]
        pv_2 = [sb(f"a_pv{i_}", [128, RWC], F32) for i_ in range(2)]
        pm_2 = [sb(f"a_pm{i_}", [128, RWC], F32) for i_ in range(2)]
        lor_2 = [sb(f"a_lor{i_}", [128, 256], F32) for i_ in range(2)]
        lorT_2 = [sb(f"a_lorT{i_}", [128, 256], F32) for i_ in range(2)]
        wt_2 = [sb(f"a_wt{i_}", [128, 512], F32) for i_ in range(2)]
        lwt_2 = [sb(f"a_lwt{i_}", [128, 512], F32) for i_ in range(2)]
        at_2 = [sb(f"a_at{i_}", [128, 512], F32) for i_ in range(2)]
        gt_2 = [sb(f"a_gt{i_}", [128, 512], F32) for i_ in range(2)]
        kk_2 = [sb(f"a_kkt{i_}", [128, 512], F32) for i_ in range(2)]
        sq_2 = [sb(f"a_sq{i_}", [128, 512], F32) for i_ in range(2)]
        nrm_2 = [sb(f"a_nrm{i_}", [128, 32], F32) for i_ in range(2)]
        kkn_2 = [sb(f"a_kkn{i_}", [128, 512], F32) for i_ in range(2)]
        bt_2 = [sb(f"a_bt{i_}", [128, 512], F32) for i_ in range(2)]
        t1_2 = [sb(f"a_t1{i_}", [128, 512], F32) for i_ in range(2)]
        kp_2 = [sb(f"a_kp{i_}", [128, 512], F32) for i_ in range(2)]
        bon_2 = [sb(f"a_bon{i_}", [128, 8], F32) for i_ in range(2)]
        psl = ps("a_psl", [128, 512], F32)
        psw = ps("a_psw", [128, 512], F32)
        psa = ps("a_psa", [128, 512], F32)
        psg = ps("a_psg", [128, 512], F32)

        for (tile, name) in ((mu, "rw_mu_b"), (w0, "rw_w0_b"), (a0, "rw_a0_b"), (kkc, "rw_k_k_b"),
                             (kac, "rw_k_a_b"), (rkc, "rw_r_k_b"), (gup, "rw_g_up"), (idf, "ident")):
            pg.ld(tile[:], dr[name][:, :], writes=[name])
        pg.ld(wup[0:64, :], dr["rw_w_up"][:, :], writes=["wup0"])
        pg.ld(wup[64:128, :], dr["rw_a_up"][:, :], writes=["wup1"])
        P = dr["P"]
        for i in range(NT):
            t0 = i * 128
            s = i % 2
            p, pv, pm, lor, lorT, wt, lwt, at, gt, kk, sq, nrm, kkn, bt, t1, kp, bon = [t_[s] for t_ in (
                p_2, pv_2, pm_2, lor_2, lorT_2, wt_2, lwt_2, at_2, gt_2, kk_2, sq_2, nrm_2, kkn_2, bt_2, t1_2, kp_2, bon_2)]
            pg.ld(p[:], P[t0:t0 + 128, 0:RWC], reads=[("P", i)], writes=[("p", s)])
            if i == 0:
                pg.memset("dve", pv[0:1, :], 0.0, writes=[("pv0", s)])
                pg.ld(pv[1:128, :], P[0:127, 0:RWC], reads=[("P", 0)], writes=[("pv", s)])
                pvk = [("pv", s), ("pv0", s)]
            else:
                pg.ld(pv[:], P[t0 - 1:t0 + 127, 0:RWC], reads=[("P", i), ("P", i - 1)], writes=[("pv", s), ("pv0", s)])
                pvk = [("pv", s), ("pv0", s)]
            pg.tt("dve", pv[:], pv[:], p[:], ALU.subtract, reads=pvk + [("p", s)], writes=[("pv", s)])
            pg.tt("dve", pv[:], pv[:], mu[:], ALU.mult, reads=[("pv", s), "rw_mu_b"], writes=[("pv", s)])
            pg.tt("dve", pm[:], pv[:], p[:], ALU.add, reads=[("pv", s), ("p", s)], writes=[("pm", s)])
            r_ = pm[:, 0:512]
            k_ = pm[:, 512:1024]
            v_ = pm[:, 1024:1536]
            pg.act(lor[:, 0:64], pm[:, 1536:1600], AF.Tanh, reads=[("pm", s)], writes=[("lor0", s)])
            pg.cp("pool", lor[:, 64:128], pm[:, 1600:1664], reads=[("pm", s)], writes=[("lor1", s)])
            pg.act(lor[:, 128:256], pm[:, 1664:1792], AF.Sigmoid, reads=[("pm", s)], writes=[("lor2", s)])
            pg.tr(psl[:, 0:128], lor[:, 0:128], idf[:], reads=[("lor0", s), ("lor1", s), "ident"], writes=["psl"])
            pg.tr(psl[:, 128:256], lor[:, 128:256], idf[:], reads=[("lor2", s), "ident"], writes=["psl"])
            pg.cp("act", lorT[:], psl[:, 0:256], reads=["psl"], writes=[("lorT", s)])
            pg.mm(psw[:], lorT[0:64, 0:128], wup[0:64, :], reads=[("lorT", s), "wup0"], writes=["psw"])
            pg.mm(psa[:], lorT[64:128, 0:128], wup[64:128, :], reads=[("lorT", s), "wup1"], writes=["psa"])
            pg.mm(psg[:], lorT[:, 128:256], gup[:], reads=[("lorT", s), "rw_g_up"], writes=["psg"])
            pg.tt("dve", wt[:], psw[:], w0[:], ALU.add, reads=["psw", "rw_w0_b"], writes=[("wt", s)])
            pg.act(wt[:], wt[:], AF.Sigmoid, reads=[("wt", s)], writes=[("wt", s)])
            pg.ts("dve", lwt[:], wt[:], -0.6065306597126334, None, ALU.mult, reads=[("wt", s)], writes=[("lwt", s)])
            pg.tt("dve", at[:], psa[:], a0[:], ALU.add, reads=["psa", "rw_a0_b"], writes=[("at", s)])
            pg.act(at[:], at[:], AF.Sigmoid, reads=[("at", s)], writes=[("at", s)])
            pg.cp("act", gt[:], psg[:], reads=["psg"], writes=[("gt", s)])
            pg.tt("dve", kk[:], k_, kkc[:], ALU.mult, reads=[("pm", s), "rw_k_k_b"], writes=[("kk", s)])
            pg.tt("pool", sq[:], kk[:], kk[:], ALU.mult, reads=[("kk", s)], writes=[("sq", s)])
            pg.red("dve", nrm[:, 0:8], sq[:].rearrange("p (h k) -> p h k", h=8), ALU.add, reads=[("sq", s)], writes=[("nrm0", s)])
            pg.act(nrm[:, 8:16], nrm[:, 0:8], AF.Sqrt, reads=[("nrm0", s)], writes=[("nrm1", s)])
            pg.ts("dve", nrm[:, 16:24], nrm[:, 8:16], 1e-12, None, ALU.max, reads=[("nrm1", s)], writes=[("nrm2", s)])
            pg.op("dve", lambda e, nrm=nrm: e.reciprocal(nrm[:, 24:32], nrm[:, 16:24]), reads=[("nrm2", s)], writes=[("nrm3", s)])
            rinv_b = nrm[:, 24:32].unsqueeze(2).to_broadcast([128, 8, 64])
            v3 = lambda tl: tl[:].rearrange("p (h k) -> p h k", h=8)
            pg.stt("dve", v3(kkn), v3(kk), -1.0, rinv_b, ALU.mult, ALU.mult, reads=[("kk", s), ("nrm3", s)], writes=[("kkn", s)])
            pg.stt("dve", bt[:], kkn[:], -1.0, at[:], ALU.mult, ALU.mult, reads=[("kkn", s), ("at", s)], writes=[("bt", s)])
            pg.stt("dve", t1[:], at[:], -1.0, kac[:], ALU.add, ALU.mult, reads=[("at", s), "rw_k_a_b"], writes=[("t1", s)])
            pg.stt("dve", kp[:], t1[:], 1.0, k_, ALU.add, ALU.mult, reads=[("t1", s), ("pm", s)], writes=[("kp", s)])
            pg.tt("pool", sq[:], r_, kp[:], ALU.mult, reads=[("pm", s), ("kp", s), ("sq", s)], writes=[("sq", s)])
            pg.tt("pool", sq[:], sq[:], rkc[:], ALU.mult, reads=[("sq", s), "rw_r_k_b"], writes=[("sq", s)])
            pg.red("dve", bon[:], sq[:].rearrange("p (h k) -> p h k", h=8), ALU.add, reads=[("sq", s)], writes=[("bon", s)])
            pg.ld(dr["RR"][t0:t0 + 128, :], r_, reads=[("pm", s)], writes=[("RR", i)], q="act")
            pg.ld(dr["RKK"][t0:t0 + 128, :], kkn[:], reads=[("kkn", s)], writes=[("RKK", i)], q="act")
            pg.ld(dr["RLW"][t0:t0 + 128, :], lwt[:], reads=[("lwt", s)], writes=[("RLW", i)], q="act")
            pg.ld(dr["RB"][t0:t0 + 128, :], bt[:], reads=[("bt", s)], writes=[("RB", i)], q="act")
            pg.ld(dr["RKp"][t0:t0 + 128, :], kp[:], reads=[("kp", s)], writes=[("RKp", i)], q="act")
            pg.ld(dr["RV"][t0:t0 + 128, :], v_, reads=[("pm", s)], writes=[("RV", i)], q="act")
            pg.ld(dr["RG"][t0:t0 + 128, :], gt[:], reads=[("gt", s)], writes=[("RG", i)], q="act")
            pg.ld(dr["RBON"][t0:t0 + 128, :], bon[:], reads=[("bon", s)], writes=[("RBON", i)], q="act")
        pg.barrier()
        pg.emit()


def igather(pg, out_ap, table_ap, idx_ap, reads, writes):
    pg.dma("pool", lambda e: e.indirect_dma_start(out=out_ap, out_offset=None, in_=table_ap,
                                                   in_offset=bass.IndirectOffsetOnAxis(ap=idx_ap, axis=0)), reads, writes)


def stage2c(C):
    nc, pg, dr = C.nc, C.pg, C.dr
    with ExitStack() as st:
        sb, ps = _mk(C, st)
        lnw = sb("c_lnw", [128, 512], F32)
        lnb = sb("c_lnb", [128, 512], F32)
        eps = sb("c_eps", [128, 1], F32)
        y = [sb(f"c_y{i}", [128, 8, 64], F32) for i in range(2)]
        v = [sb(f"c_v{i}", [128, 8, 64], F32) for i in range(2)]
        g = [sb(f"c_g{i}", [128, 512], F32) for i in range(2)]
        bon = [sb(f"c_bon{i}", [128, 8], F32) for i in range(2)]
        stt_ = [sb(f"c_st{i}", [128, 32], F32) for i in range(2)]
        sq = sb("c_sq", [128, 8, 64], F32)
        pg.ld(lnw[:], dr["rw_ln_w_b"][:, :], writes=["lnw"])
        pg.ld(lnb[:], dr["rw_ln_b_b"][:, :], writes=["lnb"])
        pg.memset("dve", eps[:], 64e-5, writes=["eps"])
        f2 = lambda tl: tl[:].rearrange("p h k -> p (h k)")
        rowidx = sb("c_rowidx", [128, NT], I32)
        pg.ld(rowidx[:], dr["rowidx"][:, :], writes=["rowidx"])
        allk = lambda nm: [(nm, k) for k in range(NT)]
        for i in range(C.peer_tiles):
            s = i % 2
            t0 = i * 128
            yk, vk, gk, bk, sk = ("y", s), ("v", s), ("g", s), ("bon", s), ("st", s)
            ix = rowidx[:, i:i + 1]
            igather(pg, f2(y[s]), dr["YS"][:, :], ix, allk("YS") + ["rowidx"], [yk])
            igather(pg, f2(v[s]), dr["RV"][:, :], ix, allk("RV") + ["rowidx"], [vk])
            igather(pg, g[s][:], dr["RG"][:, :], ix, allk("RG") + ["rowidx"], [gk])
            igather(pg, bon[s][:], dr["RBON"][:, :], ix, allk("RBON") + ["rowidx"], [bk])
            S_ = stt_[s]
            bc = lambda ap: ap.unsqueeze(2).to_broadcast([128, 8, 64])
            pg.red("dve", S_[:, 0:8], y[s][:], ALU.add, reads=[yk], writes=[(sk, 0)])
            pg.ts("dve", S_[:, 8:16], S_[:, 0:8], -1.0 / 64, None, ALU.mult, reads=[(sk, 0)], writes=[(sk, 1)])
            pg.tt("dve", y[s][:], y[s][:], bc(S_[:, 8:16]), ALU.add, reads=[yk, (sk, 1)], writes=[yk])
            pg.tt("pool", sq[:], y[s][:], y[s][:], ALU.mult, reads=[yk], writes=["sq"])
            pg.red("dve", S_[:, 16:24], sq[:], ALU.add, reads=["sq"], writes=[(sk, 2)])
            pg.act(S_[:, 24:32], S_[:, 16:24], AF.Sqrt, reads=[(sk, 2), "eps"], writes=[(sk, 3)], scale=1.0 / 64, bias=eps[:, 0:1])
            pg.op("dve", lambda e, o=S_[:, 16:24], a=S_[:, 24:32]: e.reciprocal(o, a), reads=[(sk, 3)], writes=[(sk, 2)])
            pg.tt("dve", y[s][:], y[s][:], bc(S_[:, 16:24]), ALU.mult, reads=[yk, (sk, 2)], writes=[yk])
            pg.tt("dve", f2(y[s]), f2(y[s]), lnw[:], ALU.mult, reads=[yk, "lnw"], writes=[yk])
            pg.tt("pool", f2(y[s]), f2(y[s]), lnb[:], ALU.add, reads=[yk, "lnb"], writes=[yk])
            pg.tt("pool", v[s][:], v[s][:], bc(bon[s][:, 0:8]), ALU.mult, reads=[vk, bk], writes=[vk])
            pg.tt("dve", y[s][:], y[s][:], v[s][:], ALU.add, reads=[yk, vk], writes=[yk])
            pg.tt("dve", f2(y[s]), f2(y[s]), g[s][:], ALU.mult, reads=[yk, gk], writes=[yk])
            pg.ld(dr["YAL"][t0:t0 + 128, :], f2(y[s]), reads=[yk], writes=[("YAL", i)])
        pg.barrier()
        pg.emit()


NEG = -30000.0


def stage3(C):
    nc, pg, dr = C.nc, C.pg, C.dr
    with ExitStack() as st:
        sb, ps = _mk(C, st)
        qT = sb("n_qT", [128, 4, T], BF16)
        KsT = sb("n_KsT", [128, 2, T], BF16)
        KwT = sb("n_KwT", [128, 2, T], BF16)
        Vs = sb("n_Vs", [128, NT, 2, 65], BF16)
        Vw = sb("n_Vw", [128, NT, 2, 65], BF16)
        KcT = sb("n_KcT", [128, 2, 256], BF16)
        Vc = sb("n_Vc", [128, 2, 2, 129], BF16)
        GT = sb("n_GT", [128, NT, 24], F32)
        idf = sb("n_idf", [128, 128], F32)
        idb = sb("n_idb", [128, 128], BF16)
        eps = sb("n_eps", [128, 1], F32)
        pg.ld(idf[:], dr["ident"][:, :], writes=["idf"])
        pg.cp("dve", idb[:], idf[:], reads=["idf"], writes=["idb"])
        pg.memset("dve", eps[:], 1e-6, writes=["eps"])
        pg.memset("pool", Vs[:], 1.0, writes=["Vs"])
        pg.memset("pool", Vw[:], 1.0, writes=["Vw"])
        pg.memset("pool", Vc[:], 0.0, writes=["Vc"])
        with ExitStack() as sa_:
            sb, ps = _mk(C, sa_)
            kcT2 = sb("n_kcT2", [128, T], BF16)
            vcT2 = sb("n_vcT2", [128, T], BF16)
            w1 = [sb(f"n_w1{i}", [128, 32, 256], BF16) for i in range(2)]
            w1s = sb("n_w1s", [128, 16, 256], F32)
            w2s = sb("n_w2s", [128, 2, 2, 64], F32)
            w2 = sb("n_w2", [128, 2, 2, 64], BF16)
            posf = sb("n_posf", [128, 2, 32], F32)
            posb = sb("n_posb", [128, 2, 32], BF16)
            gains = sb("n_gains", [128, 768], F32)
            kcg = sb("n_kcg", [128, 64], F32)
            ovl = sb("n_ovl", [128, 2, 64], F32)
            R = [sb(f"n_R{i}", [128, 1304], F32) for i in range(2)]
            sq = sb("n_sq", [128, 1280], F32)
            tmp = sb("n_tmp", [128, 768], F32)
            stat = sb("n_stat", [128, 64], F32)
            Xb = sb("n_Xb", [128, 10, 128], BF16)
            biasS = sb("n_biasS", [128, 4], F32)
            xb_ = sb("n_xb", [128, 256], F32)
            x2_ = sb("n_x2", [128, 256], F32)
            hT = sb("n_hT", [128, 2, 256], BF16)
            kcn2 = sb("n_kcn2", [128, 128], BF16)
            st2 = sb("n_st2", [128, 8], F32)
            ksq = sb("n_ksq", [128, 64], F32)
            psX_ = [ps(f"n_psX{i}", [128, 1024], BF16) for i in range(3)]
            psX = [t_[:, 0:512].rearrange("p (a b) -> p a b", a=4) for t_ in psX_]
            psh = ps("n_psh", [128, 512], F32)
            psb = ps("n_psb", [128, 512], F32)
            pso = ps("n_pso", [128, 512], F32)
            psk = ps("n_psk", [128, 1024], BF16)

            pg.ld(gains[:], dr["nsa_gains_b"][:, :], writes=["gains"])
            pg.ts("dve", gains[:, 0:512], gains[:, 0:512], 0.125, None, ALU.mult, reads=["gains"], writes=["gains"])
            pg.ld(kcg[:], dr["nsa_kc_g_b"][:, :], writes=["kcg"])
            pg.ld(ovl[:], dr["ovl"][:, :, :], writes=["ovl"])
            pg.ld(posf[:], dr["posT"][:, :, :], writes=["posf"])
            pg.cp("dve", posb[:], posf[:], reads=["posf"], writes=["posb"])
            pg.ld(w2s[:], dr["cmp_w2"][:, :, :, :], writes=["w2s"])
            pg.cp("dve", w2[:], w2s[:], reads=["w2s"], writes=["w2"])
            for x in range(2):
                for hf in range(2):
                    pg.ld(w1s[:], dr["cmp_w1"][x, :, hf * 16:(hf + 1) * 16, :], writes=["w1s"])
                    pg.cp("pool", w1[x][:, hf * 16:(hf + 1) * 16, :], w1s[:], reads=["w1s"], writes=[("w1", x)])
            for i in range(NT):
                s = i % 2
                t0 = i * 128
                Rk = ("R", s)
                pg.ld(R[s][:], dr["P"][t0:t0 + 128, 1792:3096], reads=[("P", i)], writes=[Rk])
                Rs = R[s]
                pg.tt("pool", sq[:], Rs[:, 0:1280], Rs[:, 0:1280], ALU.mult, reads=[Rk], writes=["sq"])
                pg.red("dve", stat[:, 0:20], sq[:].rearrange("p (a k) -> p a k", k=64), ALU.add, reads=["sq"], writes=["stat0"])
                pg.act(stat[:, 20:40], stat[:, 0:20], AF.Sqrt, reads=["stat0", "eps"], writes=["stat1"], scale=1.0 / 64, bias=eps[:, 0:1])
                pg.op("dve", lambda e: e.reciprocal(stat[:, 40:60], stat[:, 20:40]), reads=["stat1"], writes=["stat2"])
                b3 = lambda ap, n: ap.unsqueeze(2).to_broadcast([128, n, 64])
                v3 = lambda ap: ap.rearrange("p (a k) -> p a k", k=64)
                pg.tt("dve", v3(tmp[:, 0:512]), v3(Rs[:, 0:512]), b3(stat[:, 40:48], 8), ALU.mult, reads=[Rk, "stat2"], writes=["tmp"])
                pg.tt("dve", v3(tmp[:, 512:640]), v3(Rs[:, 768:896]), b3(stat[:, 52:54], 2), ALU.mult, reads=[Rk, "stat2"], writes=["tmp"])
                pg.tt("dve", v3(tmp[:, 640:768]), v3(Rs[:, 1024:1152]), b3(stat[:, 56:58], 2), ALU.mult, reads=[Rk, "stat2"], writes=["tmp"])
                pg.tt("pool", tmp[:], tmp[:], gains[:], ALU.mult, reads=["tmp", "gains"], writes=["tmp"])
                pg.cp("pool", Xb[:, 0:4, :].rearrange("p a b -> p (a b)"), tmp[:, 0:512], reads=["tmp"], writes=["Xb"])
                for (blk, c0) in ((4, 512), (6, 640)):
                    src = tmp[:, c0:c0 + 128].rearrange("p (g k) -> p g k", g=2).unsqueeze(2).to_broadcast([128, 2, 2, 64])
                    dst = Xb[:, blk:blk + 2, :].rearrange("p g (d k) -> p g d k", d=2)
                    pg.cp("dve", dst, src, reads=["tmp"], writes=["Xb"])
                pg.cp("pool", Xb[:, 8, :], Rs[:, 512:640], reads=[Rk], writes=["Xb"])
                pg.cp("pool", Xb[:, 9, :], Rs[:, 640:768], reads=[Rk], writes=["Xb"])
                for blk in range(10):
                    pg.tr(psX[blk // 4][:, blk % 4, :], Xb[:, blk, :], idb[:], reads=["Xb", "idb"], writes=[("psX", blk // 4)])
                pg.cp("act", qT[:, :, t0:t0 + 128], psX[0], reads=[("psX", 0)], writes=["qT"])
                pg.cp("dve", KsT[:, :, t0:t0 + 128], psX[1][:, 0:2, :], reads=[("psX", 1)], writes=["KsT"])
                pg.cp("dve", KwT[:, :, t0:t0 + 128], psX[1][:, 2:4, :], reads=[("psX", 1)], writes=["KwT"])
                pg.cp("act", kcT2[:, t0:t0 + 128], psX[2][:, 0, :], reads=[("psX", 2)], writes=["kcT2"])
                pg.cp("act", vcT2[:, t0:t0 + 128], psX[2][:, 1, :], reads=[("psX", 2)], writes=["vcT2"])
                pg.cp("pool", Vs[:, i, :, 0:64], Rs[:, 896:1024].rearrange("p (g k) -> p g k", g=2), reads=[Rk, "Vs"], writes=["Vs"])
                pg.cp("pool", Vw[:, i, :, 0:64], Rs[:, 1152:1280].rearrange("p (g k) -> p g k", g=2), reads=[Rk, "Vw"], writes=["Vw"])
                pg.act(GT[:, i, :], Rs[:, 1280:1304], AF.Sigmoid, reads=[Rk], writes=["GT"])
            if getattr(C, "lvl", 9) < 2:
                pg.barrier()
                pg.emit()
                return
            pg.memset("dve", hT[:], 0.0, writes=["hT"])
            pg.memset("dve", kcn2[:], 0.0, writes=["kcn2"])
            for x in range(2):
                for hf in range(2):
                    for l in range(32):
                        pg.mm(psb[:, x * 2 + hf:x * 2 + hf + 1], w1[x][0:64, l, hf * 128:(hf + 1) * 128], posb[0:64, x, l:l + 1],
                              start=(l == 0), stop=(l == 31), reads=[("w1", x), "posb"], writes=["psb"])
            pg.cp("dve", biasS[:], psb[:, 0:4], reads=["psb"], writes=["biasS"])
            for x in range(2):
                srcT = kcT2 if x == 0 else vcT2
                skey = "kcT2" if x == 0 else "vcT2"
                for g in range(2):
                    for hf in range(2):
                        for l in range(32):
                            rhs = dap(srcT[:], g * 64 * T + l, [[T, 64], [16, 255]])
                            pg.mm(psh[:, 0:255], w1[x][g * 64:(g + 1) * 64, l, hf * 128:(hf + 1) * 128], rhs,
                                  start=(l == 0), stop=(l == 31), reads=[("w1", x), skey], writes=["psh"])
                        c = slice(0, 255)
                        pg.act(xb_[:, c], psh[:, c], AF.Identity, reads=["psh", "biasS"], writes=["xb"], bias=biasS[:, x * 2 + hf:x * 2 + hf + 1])
                        pg.tt("pool", x2_[:, c], xb_[:, c], xb_[:, c], ALU.mult, reads=["xb"], writes=["x2"])
                        pg.ts("dve", x2_[:, c], x2_[:, c], 0.044715, 1.0, ALU.mult, ALU.add, reads=["x2"], writes=["x2"])
                        pg.tt("dve", x2_[:, c], x2_[:, c], xb_[:, c], ALU.mult, reads=["x2", "xb"], writes=["x2"])
                        pg.act(x2_[:, c], x2_[:, c], AF.Tanh, reads=["x2"], writes=["x2"], scale=0.7978845608028654)
                        pg.stt("dve", x2_[:, c], x2_[:, c], 1.0, xb_[:, c], ALU.add, ALU.mult, reads=["x2", "xb"], writes=["x2"])
                        pg.ts("dve", hT[:, hf, c], x2_[:, c], 0.5, None, ALU.mult, reads=["x2"], writes=["hT"])
                    for m in range(2):
                        rows = 128 if m == 0 else 127
                        for hf in range(2):
                            pg.mm(pso[0:rows, 0:64], hT[:, hf, m * 128:m * 128 + rows], w2[:, x, hf, :], start=(hf == 0), stop=(hf == 1),
                                  reads=["hT", "w2"], writes=["pso"])
                        if x == 0:
                            pg.cp("act", ksq[0:rows, :], pso[0:rows, 0:64], reads=["pso"], writes=["ksq"])
                            pg.tt("pool", x2_[0:rows, 0:64], ksq[0:rows, :], ksq[0:rows, :], ALU.mult, reads=["ksq", "x2"], writes=["x2"])
                            pg.red("dve", st2[0:rows, 0:1], x2_[0:rows, 0:64], ALU.add, reads=["x2"], writes=["st2a"])
                            pg.act(st2[0:rows, 1:2], st2[0:rows, 0:1], AF.Sqrt, reads=["st2a", "eps"], writes=["st2b"], scale=1.0 / 64, bias=eps[0:rows, 0:1])
                            pg.op("dve", lambda e, rows=rows: e.reciprocal(st2[0:rows, 2:3], st2[0:rows, 1:2]), reads=["st2b"], writes=["st2c"])
                            pg.stt("dve", ksq[0:rows, :], ksq[0:rows, :], st2[0:rows, 2:3], kcg[0:rows, :], ALU.mult, ALU.mult,
                                   reads=["ksq", "st2c", "kcg"], writes=["ksq"])
                            src = ksq[0:rows, :].unsqueeze(1).to_broadcast([rows, 2, 64])
                            pg.cp("dve", kcn2[0:rows, :].rearrange("p (d k) -> p d k", d=2), src, reads=["ksq"], writes=["kcn2"])
                            pg.tr(psk[:, 0:128], kcn2[:, :], idb[:], reads=["kcn2", "idb"], writes=["psk"])
                            pg.cp("act", KcT[:, g, m * 128:(m + 1) * 128], psk[:, 0:128], reads=["psk"], writes=["KcT"])
                        else:
                            pg.cp("act", Vc[0:rows, m, g, 0:64], pso[0:rows, 0:64], reads=["pso", "Vc"], writes=["Vc"])
            for m in range(2):
                for g in range(2):
                    pg.memset("dve", Vc[:, m, g, 64:65], 1.0, writes=["Vc"])
                    pg.cp("dve", Vc[:, m, g, 65:129], ovl[:, m, :], reads=["ovl", "Vc"], writes=["Vc"])
            pg.barrier()
            pg.emit()
        if getattr(C, "lvl", 9) < 3:
            return
        stage3_attn(C, st, qT, KsT, KwT, Vs, Vw, KcT, Vc, GT, idf, idb)


def stage3_attn(C, st, qT, KsT, KwT, Vs, Vw, KcT, Vc, GT, idf, idb):
    nc, pg, dr = C.nc, C.pg, C.dr
    with ExitStack() as sb_:
        sb, ps = _mk(C, sb_)
        cmpb = sb("n_cmpb", [128, 2, T], BF16)
        Esel = sb("n_Esel", [128, 32, 128], BF16)
        causb = sb("n_causb", [128, 4, 512], BF16)
        winb = sb("n_winb", [128, 8, 512], BF16)
        selbT = sb("n_selbT", [128, 2, T], BF16)
        eT = [sb(f"n_eT{i}", [128, 512], BF16) for i in range(4)]
        eT2 = [sb(f"n_eT2{i}", [128, 512], BF16) for i in range(4)]
        Mt = [sb(f"n_Mt{i}", [128, 512], BF16) for i in range(2)]
        rm = [0]
        dq = []
        ocmp = sb("n_ocmp", [128, 4, 8, 64], F32)
        osel = sb("n_osel", [128, 4, 8, 64], F32)
        owin = sb("n_owin", [128, 4, 8, 64], F32)
        den = sb("n_den", [128, 16], F32)
        impw = sb("n_impw", [128, 2, 4, 64], F32)
        score = sb("n_score", [128, 2, 64], F32)
        VM = [sb(f"n_VM{i}", [128, 2, 64], F32) for i in range(2)]
        work = sb("n_work", [128, 2, 64], F32)
        m8 = sb("n_m8", [128, 2, 16], F32)
        thr = sb("n_thr", [128, 2], F32)
        msel = sb("n_msel", [128, 2, 64], F32)
        selb = sb("n_selb", [128, 2, 2, 64], BF16)
        osT = [sb(f"n_osT{i}", [65, 512], F32) for i in range(2)]
        dn2 = sb("n_dn2", [128, 8], F32)
        yb = sb("n_yb", [128, 8, 64], F32)
        yb2 = sb("n_yb2", [128, 8, 64], F32)
        psS = [ps(f"n_psS{i}", [128, 512], F32) for i in range(3)]
        psA = [ps(f"n_psA{i}", [128, 512], F32) for i in range(2)]
        psB = [ps(f"n_psB{i}", [128, 512], F32) for i in range(2)]
        psZ_ = ps("n_psZ", [128, 1024], BF16)
        psZ = psZ_[:, 0:256].rearrange("p (g q) -> p g q", g=2)

        pg.ld(cmpb[:], dr["c_cmpb"][:, :, :], writes=["cmpb"])
        pg.ld(Esel[:], dr["c_esel"][:, :, :], writes=["Esel"])
        pg.ld(causb[:], dr["c_causb"][:, :, :], writes=["causb"])
        pg.ld(winb[:], dr["c_winb"][:, :, :], writes=["winb"])
        winb4 = sb("n_winb4", [128, 8, 512], BF16)
        pg.ld(winb4[:], dr["c_winb4"][:, :, :], writes=["winb4"])
        rs = [0]
        re = [0]

        def nxt(lst, n):
            v = lst[0]
            lst[0] = (v + 1) % n
            return v

        def qk(h):
            return (h % 2) * 64, h // 2, h // 4

        for Q in range(4, 8):
            tq0 = Q * 512
            for ii in range(4):
                i = Q * 4 + ii
                t0 = i * 128
                s = i % 2
                pg.ld(VM[s][:], dr["c_vmfb"][i, :, :, :], writes=[("VM", s)])
                nm = 2 if i >= 16 else 1
                for h in range(8):
                    base, hp, g = qk(h)
                    h4 = h % 4
                    for m in range(nm):
                        r = nxt(rs, 3)
                        pS = psS[r]
                        pg.mm(pS[:, 0:128], KcT[base:base + 64, g, m * 128:(m + 1) * 128], qT[base:base + 64, hp, t0:t0 + 128],
                              start=True, stop=False, reads=["KcT", "qT"], writes=[("psS", r)])
                        pg.mm(pS[:, 0:128], idb[:, :], cmpb[:, m, t0:t0 + 128], start=False, stop=True,
                              reads=["idb", "cmpb"], writes=[("psS", r)])
                        k = nxt(re, 4)
                        pg.act(eT[k][:, 0:128], pS[:, 0:128], AF.Exp, reads=[("psS", r)], writes=[("eT", k)])
                        def pv(g=g, h4=h4, k=k, m=m, nm=nm):
                            pg.mm(psA[g][:, h4 * 65:h4 * 65 + 65], eT[k][:, 0:128], Vc[:, m, g, 0:65], start=(m == 0), stop=(m == nm - 1),
                                  reads=[("eT", k), "Vc"], writes=[("psA", g)])
                            pg.mm(psB[g][:, h4 * 64:h4 * 64 + 64], eT[k][:, 0:128], Vc[:, m, g, 65:129], start=(m == 0), stop=(m == nm - 1),
                                  reads=[("eT", k), "Vc"], writes=[("psB", g)])
                        dq.append(pv)
                        if len(dq) > 2:
                            dq.pop(0)()
                while dq:
                    dq.pop(0)()
                for g in range(2):
                    A3 = psA[g][:, 0:260].rearrange("p (h c) -> p h c", c=65)
                    B3 = psB[g][:, 0:256].rearrange("p (h c) -> p h c", c=64)
                    dsl = den[:, g * 4:(g + 1) * 4]
                    rsl = den[:, 8 + g * 4:8 + (g + 1) * 4]
                    pg.ts("dve", dsl, A3[:, :, 64], 1e-30, None, ALU.max, reads=[("psA", g)], writes=[("den", g)])
                    pg.op("dve", lambda e, o=rsl, a=dsl: e.reciprocal(o, a), reads=[("den", g)], writes=[("rden", g)])
                    rb = rsl.unsqueeze(2).to_broadcast([128, 4, 64])
                    pg.tt("dve", ocmp[:, ii, g * 4:(g + 1) * 4, :], A3[:, :, 0:64], rb, ALU.mult, reads=[("psA", g), ("rden", g)], writes=["ocmp"])
                    pg.tt("dve", impw[:, g, :, :], B3, rb, ALU.mult, reads=[("psB", g), ("rden", g)], writes=[("impw", g)])
                    pg.red("dve", score[:, g, :], impw[:, g, :, :].rearrange("p h j -> p j h"), ALU.add, reads=[("impw", g)], writes=[("score", g)])
                    vm = dr
                    pg.tt("dve", score[:, g, :], score[:, g, :], VM[s][:, 0, :], ALU.mult, reads=[("score", g), ("VM", s)], writes=[("score", g)])
                    pg.tt("dve", score[:, g, :], score[:, g, :], VM[s][:, 1, :], ALU.add, reads=[("score", g), ("VM", s)], writes=[("score", g)])
                    pg.op("dve", lambda e, g=g: e.max(m8[:, g, 0:8], score[:, g, :]), reads=[("score", g)], writes=[("m8a", g)])
                    pg.op("dve", lambda e, g=g: e.match_replace(work[:, g, :], m8[:, g, 0:8], score[:, g, :], -1e9),
                          reads=[("score", g), ("m8a", g)], writes=[("work", g)])
                    pg.op("dve", lambda e, g=g: e.max(m8[:, g, 8:16], work[:, g, :]), reads=[("work", g)], writes=[("m8b", g)])
                    pg.ts("dve", thr[:, g:g + 1], m8[:, g, 15:16], -0.5, None, ALU.max, reads=[("m8b", g)], writes=[("thr", g)])
                    pg.ts("dve", msel[:, g, :], score[:, g, :], thr[:, g:g + 1], None, ALU.is_ge, reads=[("score", g), ("thr", g)], writes=[("msel", g)])
                    pg.cp("dve", selb[:, g, :, :], msel[:, g, :].unsqueeze(1).to_broadcast([128, 2, 64]), reads=[("msel", g)], writes=[("selb", g)])
                    pg.tr(psZ[:, g, :], selb[:, g, :, :].rearrange("p d j -> p (d j)"), idb[:], reads=[("selb", g), "idb"], writes=["psZ"])
                pg.cp("act", selbT[:, :, t0:t0 + 128], psZ, reads=["psZ"], writes=["selbT"])
            for br in range(2):
                if getattr(C, "lvl", 9) < 4 + br:
                    continue
                dest = osel if br == 0 else owin
                dkey = "osel" if br == 0 else "owin"
                KT = KsT if br == 0 else KwT
                Vv = Vs if br == 0 else Vw
                kts = list(range(0, 4 * Q + 4)) if br == 0 else list(range(max(0, 4 * Q - 4), 4 * Q + 4))
                for g in range(2):
                    O = [psA[0], psA[1], psB[0], psB[1]]
                    okeys = [("psA", 0), ("psA", 1), ("psB", 0), ("psB", 1)]
                    for n_, kt in enumerate(kts):
                        if br == 0:
                            r = nxt(rs, 3)
                            pg.mm(psS[r][:, :], Esel[0:64, kt, :], selbT[0:64, g, tq0:tq0 + 512], reads=["Esel", "selbT"], writes=[("psS", r)])
                            mi = nxt(rm, 2)
                            if kt >= 4 * Q:
                                pg.tt("dve", Mt[mi][:], psS[r][:, :], causb[:, kt - 4 * Q, :], ALU.mult, reads=[("psS", r), "causb"], writes=[("Mt", mi)])
                            else:
                                pg.cp("dve", Mt[mi][:], psS[r][:, :], reads=[("psS", r)], writes=[("Mt", mi)])
                            mask, mkeys = Mt[mi][:], [("Mt", mi)]
                        else:
                            wsrc = winb4 if Q == 4 else winb
                            mask, mkeys = wsrc[:, kt - 4 * Q + 4, :], ["winb", "winb4"]
                        for h4 in range(4):
                            h = g * 4 + h4
                            base, hp, _g = qk(h)
                            r2 = nxt(rs, 3)
                            pg.mm(psS[r2][:, :], KT[base:base + 64, g, kt * 128:(kt + 1) * 128], qT[base:base + 64, hp, tq0:tq0 + 512],
                                  reads=["qT"], writes=[("psS", r2)])
                            k = nxt(re, 4)
                            pg.act(eT[k][:, :], psS[r2][:, :], AF.Exp, reads=[("psS", r2)], writes=[("eT", k)])
                            pg.tt("dve", eT2[k][:, :], eT[k][:, :], mask, ALU.mult, reads=[("eT", k)] + mkeys, writes=[("eT2", k)])
                            dq.append(lambda h4=h4, kt=kt, k=k, n_=n_, O=O, okeys=okeys, Vv=Vv, g=g, kts=kts: pg.mm(
                                O[h4][0:65, :], Vv[:, kt, g, :], eT2[k][:, :], start=(n_ == 0), stop=(n_ == len(kts) - 1),
                                reads=[("eT2", k)], writes=[okeys[h4]]))
                            if len(dq) > 2:
                                dq.pop(0)()
                    while dq:
                        dq.pop(0)()
                    for h4 in range(4):
                        h = g * 4 + h4
                        o = h4 % 2
                        pg.cp("act", osT[o][:, :], O[h4][0:65, :], reads=[okeys[h4]], writes=[("osT", o)])
                        r3 = nxt(rs, 3)
                        Tp = psS[r3]
                        for qq in range(4):
                            pg.tr(Tp[:, qq * 65:(qq + 1) * 65], osT[o][0:65, qq * 128:(qq + 1) * 128], idf[0:65, 0:65],
                                  reads=[("osT", o), "idf"], writes=[("psS", r3)])
                        T3 = Tp[:, 0:260].rearrange("p (q c) -> p q c", c=65)
                        pg.ts("dve", dn2[:, 0:4], T3[:, :, 64], 1e-30, None, ALU.max, reads=[("psS", r3)], writes=["dn2a"])
                        pg.op("dve", lambda e: e.reciprocal(dn2[:, 4:8], dn2[:, 0:4]), reads=["dn2a"], writes=["dn2b"])
                        pg.tt("dve", dest[:, :, h, :], T3[:, :, 0:64], dn2[:, 4:8].unsqueeze(2).to_broadcast([128, 4, 64]), ALU.mult,
                              reads=[("psS", r3), "dn2b"], writes=[dkey])
            for ii in range(4):
                i = Q * 4 + ii
                t0 = i * 128
                G3 = GT[:, i, :].rearrange("p (h c) -> p h c", c=3)
                gb = lambda c: G3[:, :, c].unsqueeze(2).to_broadcast([128, 8, 64])
                pg.tt("dve", yb[:], ocmp[:, ii, :, :], gb(0), ALU.mult, reads=["ocmp", "GT"], writes=["yb"])
                pg.tt("pool", yb2[:], osel[:, ii, :, :], gb(1), ALU.mult, reads=["osel", "GT"], writes=["yb2"])
                pg.tt("dve", yb[:], yb[:], yb2[:], ALU.add, reads=["yb", "yb2"], writes=["yb"])
                pg.tt("pool", yb2[:], owin[:, ii, :, :], gb(2), ALU.mult, reads=["owin", "GT", "yb2"], writes=["yb2"])
                pg.tt("dve", yb[:], yb[:], yb2[:], ALU.add, reads=["yb", "yb2"], writes=["yb"])
                pg.ld(dr["YB"][t0:t0 + 128, :], yb[:].rearrange("p h k -> p (h k)"), reads=["yb"], writes=[("YB", i)])
        pg.barrier()
        pg.emit()


_NSA_CONSTS = {}


def nsa_consts(hh=1):
    if hh in _NSA_CONSTS:
        return _NSA_CONSTS[hh]
    import ml_dtypes
    bf = ml_dtypes.bfloat16
    c = {}
    n = np.arange(256)
    t = np.arange(T)
    nlo = 128 if hh == 0 else 0
    cm = np.where((16 * n[:, None] + 31 <= t[None, :]) & (n[:, None] < 255) & (n[:, None] >= nlo), 0.0, NEG).astype(np.float32)
    c["c_cmpb"] = np.ascontiguousarray(cm.reshape(2, 128, T).transpose(1, 0, 2)).astype(bf)
    es = np.zeros((64, 32, 128), np.float32)
    for kt in range(32):
        for key in range(128):
            es[2 * kt + key // 64, kt, key] = 1.0
    c["c_esel"] = np.concatenate([es, es], 0).astype(bf)
    key = np.arange(128)
    q = np.arange(512)
    cb = np.zeros((128, 4, 512), np.float32)
    for d in range(4):
        cb[:, d, :] = np.where((d * 128 + key[:, None]) <= q[None, :], 1.0, 0.0)
    c["c_causb"] = cb.astype(bf)
    wb = np.zeros((128, 8, 512), np.float32)
    for r in range(8):
        ka = (r - 4) * 128 + key[:, None]
        wb[:, r, :] = np.where((ka <= q[None, :]) & (ka > q[None, :] - 512), 1.0, 0.0)
    c["c_winb"] = wb.astype(bf)
    wb4 = wb.copy()
    if hh == 0:
        wb4[:, 0:4, :] = 0.0
    c["c_winb4"] = wb4.astype(bf)
    cs = np.arange(256) * 16
    ss = np.arange(64) * 64
    ov = np.clip(np.minimum(cs[:, None] + 32, ss[None, :] + 64) - np.maximum(cs[:, None], ss[None, :]), 0, None) / 32.0
    ov[255, :] = 0.0
    c["ovl"] = np.ascontiguousarray(ov.reshape(2, 128, 64).transpose(1, 0, 2)).astype(np.float32)
    cur = t // 64
    j = np.arange(64)
    jlo = 32 if hh == 0 else 0
    valid = (j[None, :] <= cur[:, None]) & (j[None, :] >= jlo)
    forced = (j[None, :] == jlo) | (j[None, :] == cur[:, None]) | (j[None, :] == cur[:, None] - 1)
    vm = valid.astype(np.float32)
    fb = np.where(valid, 1000.0 * forced, -1.0).astype(np.float32)
    c["c_vmfb"] = np.ascontiguousarray(np.stack([vm, fb], 1).reshape(NT, 128, 2, 64))
    _NSA_CONSTS[hh] = c
    return c


def stage4(C):
    nc, pg, dr = C.nc, C.pg, C.dr
    with ExitStack() as st:
        sb, ps = _mk(C, st)
        wa = sb("m_wa", [128, 4, D], BF16)
        wb = sb("m_wb", [128, 4, D], BF16)
        wo = sb("m_wo", [128, 8, D], BF16)
        stg = sb("m_stg", [128, D], F32)
        idf = sb("m_idf", [128, 128], F32)
        idb = sb("m_idb", [128, 128], BF16)
        yab = [sb(f"m_yab{i}", [128, 1024], F32) for i in range(2)]
        yabb = sb("m_yabb", [128, 1024], BF16)
        yT = sb("m_yT", [128, 8, 128], BF16)
        gts = [sb(f"m_g{i}", [128, 2048], F32) for i in range(2)]
        xt = [sb(f"m_x{i}", [128, D], F32) for i in range(2)]
        mix = sb("m_mix", [128, D], F32)
        mix2 = sb("m_mix2", [128, D], F32)
        mixb = sb("m_mixb", [128, D], BF16)
        mT = sb("m_mT", [128, 8, 128], BF16)
        x1 = [sb(f"m_x1{i}", [128, D], F32) for i in range(2)]
        psT = ps("m_psT", [128, 1024], BF16)
        psm = [ps(f"m_psm{i}", [128, 512], F32) for i in range(4)]
        psT2 = ps("m_psT2", [128, 1024], BF16)
        pso = [ps(f"m_pso{i}", [128, 512], F32) for i in range(2)]

        pg.ld(idf[:], dr["ident"][:, :], writes=["idf"])
        pg.cp("dve", idb[:], idf[:], reads=["idf"], writes=["idb"])
        n = 0
        for (wt, nm, kcs) in ((wa, "w_branch_a", 4), (wb, "w_branch_b", 4), (wo, "w_out", 8)):
            for kc in range(kcs):
                pg.ld(stg[:], dr[nm][kc * 128:(kc + 1) * 128, :], writes=["stg"])
                pg.cp(("act", "dve", "pool")[n % 3], wt[:, kc, :], stg[:], reads=["stg"], writes=[nm])
                n += 1
        rowidx = sb("m_rowidx", [128, NT], I32)
        pg.ld(rowidx[:], dr["rowidx"][:, :], writes=["rowidx"])
        allk = lambda nm: [(nm, k) for k in range(NT)]
        for i in range(C.peer_tiles):
            s = i % 2
            t0 = i * 128
            ix = rowidx[:, i:i + 1]
            pg.ld(yab[s][:, 0:512], dr["YAL"][t0:t0 + 128, :], reads=[("YAL", i)], writes=[("yab", s)])
            igather(pg, yab[s][:, 512:1024], dr["YB"][:, :], ix, allk("YB") + ["rowidx"], [("yab2", s)])
            igather(pg, gts[s][:], dr["PG"][:, :], ix, allk("PG") + ["rowidx"], [("gts", s)])
            igather(pg, xt[s][:], dr["x"][:, :], ix, ["rowidx"], [("xt", s)])
            pg.cp("pool", yabb[:], yab[s][:], reads=[("yab", s), ("yab2", s)], writes=["yabb"])
            for j in range(8):
                pg.tr(psT[:, j * 128:(j + 1) * 128], yabb[:, j * 128:(j + 1) * 128], idb[:], reads=["yabb", "idb"], writes=["psT"])
            pg.cp("act", yT[:].rearrange("p a b -> p (a b)"), psT[:], reads=["psT"], writes=["yT"])
            for br in range(2):
                wt = wa if br == 0 else wb
                for nchunk in range(2):
                    pb = psm[br * 2 + nchunk]
                    for kc in range(4):
                        pg.mm(pb[:], yT[:, br * 4 + kc, :], wt[:, kc, nchunk * 512:(nchunk + 1) * 512], start=(kc == 0), stop=(kc == 3),
                              reads=["yT", "w_branch_a", "w_branch_b"], writes=[("psm", br * 2 + nchunk)])
            pg.act(gts[s][:], gts[s][:], AF.Sigmoid, reads=[("gts", s)], writes=[("gts", s)])
            for nchunk in range(2):
                c = slice(nchunk * 512, (nchunk + 1) * 512)
                pg.tt("dve", mix[:, c], psm[nchunk][:], gts[s][:, nchunk * 512:(nchunk + 1) * 512], ALU.mult,
                      reads=[("psm", nchunk), ("gts", s)], writes=[("mix", nchunk)])
                pg.tt("dve", mix2[:, c], psm[2 + nchunk][:], gts[s][:, 1024 + nchunk * 512:1024 + (nchunk + 1) * 512], ALU.mult,
                      reads=[("psm", 2 + nchunk), ("gts", s)], writes=[("mix2", nchunk)])
                pg.tt("pool", mixb[:, c], mix[:, c], mix2[:, c], ALU.add, reads=[("mix", nchunk), ("mix2", nchunk)], writes=[("mixb", nchunk)])
            for j in range(8):
                pg.tr(psT2[:, j * 128:(j + 1) * 128], mixb[:, j * 128:(j + 1) * 128], idb[:], reads=[("mixb", 0), ("mixb", 1), "idb"], writes=["psT2"])
            pg.cp("act", mT[:].rearrange("p a b -> p (a b)"), psT2[:], reads=["psT2"], writes=["mT"])
            for nchunk in range(2):
                for kc in range(8):
                    pg.mm(pso[nchunk][:], mT[:, kc, :], wo[:, kc, nchunk * 512:(nchunk + 1) * 512], start=(kc == 0), stop=(kc == 7),
                          reads=["mT", "w_out"], writes=[("pso", nchunk)])
                pg.tt("dve", x1[s][:, nchunk * 512:(nchunk + 1) * 512], pso[nchunk][:], xt[s][:, nchunk * 512:(nchunk + 1) * 512], ALU.add,
                      reads=[("pso", nchunk), ("xt", s)], writes=[("x1", s, nchunk)])
            pg.ld(dr["X1L"][t0:t0 + 128, :], x1[s][:], reads=[("x1", s, 0), ("x1", s, 1)], writes=[("X1L", i)])
        pg.barrier()
        pg.emit()


def table_conv_gen(C, sb):
    pg, dr = C.pg, C.dr
    NBUF = 4
    src = [sb(f"z_src{i}", [128, D], F32) for i in range(NBUF)]
    dst = [sb(f"z_dst{i}", [128, D], BF16) for i in range(NBUF)]
    n = 0
    for (tab, co) in (("peer_u", 0), ("peer_v", D)):
        for a in range(16384 // 128):
            b_ = n % NBUF
            pg.ld(src[b_][:], dr[tab][a * 128:(a + 1) * 128, :], writes=[("zsrc", b_)], q="sp")
            pg.cp("pool", dst[b_][:], src[b_][:], reads=[("zsrc", b_)], writes=[("zdst", b_)])
            pg.ld(dr["UV"][a * 128:(a + 1) * 128, co:co + D], dst[b_][:], reads=[("zdst", b_)], writes=[("UV", co, a)], q="act")
            n += 1
            yield


def stage5(C):
    nc, pg, dr = C.nc, C.pg, C.dr
    NB = 12
    with ExitStack() as st:
        sb, ps = _mk(C, st)
        wq = sb("p_wq", [128, 8, 2048], F32)
        kT = sb("p_kT", [128, 2, 128], F32)
        kraw = sb("p_kraw", [128, 2, 128], F32)
        g2 = sb("p_g2", [128, D], F32)
        idf = sb("p_idf", [128, 128], F32)
        io16 = sb("p_io16", [128, 16], F32)
        eps = sb("p_eps", [128, 1], F32)
        x1 = [sb(f"p_x1{i}", [128, D], F32) for i in range(3)]
        h2 = [sb(f"p_h2{i}", [128, D], F32) for i in range(2)]
        junk = sb("p_junk", [128, D], BF16)

        ss = sb("p_ss", [128, 4], F32)
        h2T = sb("p_h2T", [128, 8, 128], F32)
        qT = sb("p_qT", [128, 16, 128], F32)
        sc = sb("p_sc", [128, 16, 128], F32)
        work = sb("p_work", [128, 256], F32)
        tv = sb("p_tv", [128, 16, 16], F32)
        tiu = sb("p_tiu", [128, 16, 16], U32)
        ti = sb("p_ti", [128, 16, 16], F32)
        cs = sb("p_cs", [128, 8, 256], F32)
        bs = sb("p_bs", [128, 8, 16], F32)
        posu = sb("p_posu", [128, 8, 16], U32)
        pa_u = sb("p_pau", [128, 8, 16], U32)
        pb_u = sb("p_pbu", [128, 8, 16], U32)
        pa = sb("p_pa", [128, 8, 16], F32)
        pb = sb("p_pb", [128, 8, 16], F32)
        oh = sb("p_oh", [128, 8, 16, 16], F32)
        ia = sb("p_ia", [128, 8, 16], F32)
        ib = sb("p_ib", [128, 8, 16], F32)
        eidf = sb("p_eidf", [128, 128], F32)
        eidi = [sb(f"p_eidi{i}", [128, 128], I32) for i in range(3)]
        gate = [sb(f"p_gate{i}", [128, 128], F32) for i in range(2)]
        zz = sb("p_zz", [128, 16], F32)
        actv = [sb(f"p_act{i}", [128, 128], F32) for i in range(2)]
        ga = [sb(f"p_ga{i}", [128, 128], F32) for i in range(2)]
        uv = [sb(f"p_uv{i}", [128, 2 * D], BF16) for i in range(NB)]
        h2b = [sb(f"p_h2b{i}", [128, D], BF16) for i in range(2)]
        idb = sb("p_idb", [128, 128], BF16)
        junk2 = sb("p_junk2", [128, D], F32)
        dg = [sb(f"p_dg{i}", [128, 128], BF16) for i in range(4)]
        yo = [sb(f"p_yo{i}", [128, D], F32) for i in range(1)]
        psT = ps("p_psT", [128, 8, 128], F32)
        psQ = [ps(f"p_psQ{i}", [128, 512], F32) for i in range(2)]
        psY = [ps(f"p_psY{i}", [128, 512], F32) for i in range(2)]

        pg.ld(idf[:], dr["ident"][:, :], writes=["idf"])
        pg.ld(g2[:], dr["norm2_g_b"][:, :], writes=["g2"])
        pg.cp("dve", idb[:], idf[:], reads=["idf"], writes=["idb"])
        pg.ld(io16[:], dr["iota16"][:, :], writes=["io16"])
        rowidx = sb("p_rowidx", [128, NT], I32)
        pg.ld(rowidx[:], dr["rowidx"][:, :], writes=["rowidx"])
        pg.memset("dve", eps[:], 1e-6, writes=["eps"])
        for kc in range(8):
            pg.ld(wq[:, kc, :], dr["peer_wq"][kc * 128:(kc + 1) * 128, :], writes=["wq"])
        pg.ld(kraw[:, 0, :], dr["peer_k1"][:, :], writes=["kraw"])
        pg.ld(kraw[:, 1, :], dr["peer_k2"][:, :], writes=["kraw"])
        for hf in range(2):
            pg.tr(psQ[0][:, hf * 128:(hf + 1) * 128], kraw[:, hf, :], idf[:], reads=["kraw", "idf"], writes=[("psQ", 0)])
        pg.cp("dve", kT[:].rearrange("p a b -> p (a b)"), psQ[0][:, 0:256], reads=[("psQ", 0)], writes=["kT"])
        ntiles = getattr(C, "peer_tiles", NT)

        def front(i):
            s = i % 2
            t0 = i * 128
            pg.ld(x1[i % 3][:, :], dr["X1L"][t0:t0 + 128, :], reads=[("X1L", i)], writes=[("x1", i % 3)])
            yield
            pg.tt("pool", junk2[:], x1[i % 3][:], x1[i % 3][:], ALU.mult, reads=[("x1", i % 3), "junk2"], writes=["junk2"])
            yield
            pg.red("dve", ss[:, 0:1], junk2[:], ALU.add, reads=["junk2"], writes=["ss0"])
            yield
            pg.act(ss[:, 1:2], ss[:, 0:1], AF.Sqrt, reads=["ss0", "eps"], writes=["ss1"], scale=1.0 / D, bias=eps[:, 0:1])
            yield
            pg.op("dve", lambda e: e.reciprocal(ss[:, 2:3], ss[:, 1:2]), reads=["ss1"], writes=["ss2"])
            yield
            pg.stt("dve", h2[s][:], x1[i % 3][:], ss[:, 2:3], g2[:], ALU.mult, ALU.mult, reads=[("x1", i % 3), "ss2", "g2"], writes=[("h2", s)])
            yield
            pg.cp("pool", h2b[s][:], h2[s][:], reads=[("h2", s)], writes=[("h2b", s)])
            yield
            for j in range(8):
                pg.tr(psT[:, j, :], h2[s][:, j * 128:(j + 1) * 128], idf[:], reads=[("h2", s), "idf"], writes=["psT"])
                yield
            pg.cp("act", h2T[:], psT[:], reads=["psT"], writes=["h2T"])
            yield
            for cg in range(4):
                bk = psQ[cg % 2]
                for cc in range(4):
                    c = cg * 4 + cc
                    for kc in range(8):
                        pg.mm(bk[:, cc * 128:(cc + 1) * 128], wq[:, kc, c * 128:(c + 1) * 128], h2T[:, kc, :], start=(kc == 0), stop=(kc == 7),
                              reads=["wq", "h2T"], writes=[("psQ", cg % 2)])
                        yield
                pg.cp("act" if cg % 2 == 0 else "dve", qT[:, cg * 4:(cg + 1) * 4, :].rearrange("p a b -> p (a b)"), bk[:],
                      reads=[("psQ", cg % 2)], writes=[("qT", cg)])
                yield
            for cg in range(4):
                bk = psQ[cg % 2]
                for cc in range(4):
                    c = cg * 4 + cc
                    pg.mm(bk[:, cc * 128:(cc + 1) * 128], qT[:, c, :], kT[:, c % 2, :], reads=[("qT", cg), "kT"], writes=[("psQ", cg % 2)])
                    yield
                pg.cp("act" if cg % 2 == 0 else "dve", sc[:, cg * 4:(cg + 1) * 4, :].rearrange("p a b -> p (a b)"), bk[:],
                      reads=[("psQ", cg % 2)], writes=[("sc", cg)])
                yield
            for c in range(16):
                k_ = ("sc", c // 4)
                pg.op("dve", lambda e, c=c: e.max(tv[:, c, 0:8], sc[:, c, :]), reads=[k_], writes=[("tv", c)])
                yield
                pg.op("dve", lambda e, c=c: e.max_index(tiu[:, c, 0:8], tv[:, c, 0:8], sc[:, c, :]), reads=[k_, ("tv", c)], writes=[("tiu", c)])
                yield
                pg.op("dve", lambda e, c=c: e.match_replace(work[:, 0:128], tv[:, c, 0:8], sc[:, c, :], -1e30), reads=[k_, ("tv", c), "work"], writes=["work"])
                yield
                pg.op("dve", lambda e, c=c: e.max(tv[:, c, 8:16], work[:, 0:128]), reads=["work"], writes=[("tv2", c)])
                yield
                pg.op("dve", lambda e, c=c: e.max_index(tiu[:, c, 8:16], tv[:, c, 8:16], sc[:, c, :]), reads=[k_, ("tv2", c)], writes=[("tiu2", c)])
                yield
            allt = [("tv", c) for c in range(16)] + [("tv2", c) for c in range(16)]
            alli = [("tiu", c) for c in range(16)] + [("tiu2", c) for c in range(16)]
            pg.cp("dve", ti[:], tiu[:], reads=alli, writes=["ti"])
            yield
            tv4 = tv[:].rearrange("p (h f) a -> p h f a", f=2)
            ti4 = ti[:].rearrange("p (h f) a -> p h f a", f=2)
            cs4 = cs[:].rearrange("p h (a b) -> p h a b", a=16)
            A_ = lambda t4: t4[:, :, 0, :].unsqueeze(3).to_broadcast([128, 8, 16, 16])
            B_ = lambda t4: t4[:, :, 1, :].unsqueeze(2).to_broadcast([128, 8, 16, 16])
            pg.tt("dve", cs4, A_(tv4), B_(tv4), ALU.add, reads=allt, writes=["cs"])
            yield
            for h in range(8):
                pg.op("dve", lambda e, h=h: e.max(bs[:, h, 0:8], cs[:, h, :]), reads=["cs"], writes=[("bs", h)])
                yield
                pg.op("dve", lambda e, h=h: e.max_index(posu[:, h, 0:8], bs[:, h, 0:8], cs[:, h, :]), reads=["cs", ("bs", h)], writes=[("posu", h)])
                yield
                pg.op("dve", lambda e, h=h: e.match_replace(work[:, :], bs[:, h, 0:8], cs[:, h, :], -1e30), reads=["cs", ("bs", h), "work"], writes=["work"])
                yield
                pg.op("dve", lambda e, h=h: e.max(bs[:, h, 8:16], work[:, :]), reads=["work"], writes=[("bs2", h)])
                yield
                pg.op("dve", lambda e, h=h: e.max_index(posu[:, h, 8:16], bs[:, h, 8:16], cs[:, h, :]), reads=["cs", ("bs2", h)], writes=[("posu2", h)])
                yield
            allb = [("bs", h) for h in range(8)] + [("bs2", h) for h in range(8)]
            allp = [("posu", h) for h in range(8)] + [("posu2", h) for h in range(8)]
            G = gate[s][:].rearrange("p (h j) -> p h j", h=8)
            pg.tt("dve", G, bs[:], bs[:, :, 0:1].to_broadcast([128, 8, 16]), ALU.subtract, reads=allb, writes=[("gate", s)])
            yield
            pg.act(G, G, AF.Exp, reads=[("gate", s)], writes=[("gate", s)])
            yield
            pg.red("dve", zz[:, 0:8], G, ALU.add, reads=[("gate", s)], writes=["zz0"])
            yield
            pg.op("dve", lambda e: e.reciprocal(zz[:, 8:16], zz[:, 0:8]), reads=["zz0"], writes=["zz1"])
            yield
            pg.tt("dve", G, G, zz[:, 8:16].unsqueeze(2).to_broadcast([128, 8, 16]), ALU.mult, reads=[("gate", s), "zz1"], writes=[("gate", s)])
            yield
            pg.ts("dve", pa_u[:], posu[:], 4, None, ALU.logical_shift_right, reads=allp, writes=["pau"])
            yield
            pg.ts("dve", pb_u[:], posu[:], 15, None, ALU.bitwise_and, reads=allp, writes=["pbu"])
            yield
            pg.cp("dve", pa[:], pa_u[:], reads=["pau"], writes=["pa"])
            yield
            pg.cp("dve", pb[:], pb_u[:], reads=["pbu"], writes=["pb"])
            yield
            iob = io16[:, :].unsqueeze(1).unsqueeze(1).to_broadcast([128, 8, 16, 16])
            for (pp, key, half, dst, dk_) in ((pa, "pa", 0, ia, "ia"), (pb, "pb", 1, ib, "ib")):
                pg.tt("dve", oh[:], pp[:].unsqueeze(3).to_broadcast([128, 8, 16, 16]), iob, ALU.is_equal, reads=[key, "io16", "oh"], writes=["oh"])
                yield
                tsel = ti4[:, :, half, :].unsqueeze(2).to_broadcast([128, 8, 16, 16])
                pg.tt("dve", oh[:], oh[:], tsel, ALU.mult, reads=["oh", "ti"], writes=["oh"])
                yield
                pg.red("dve", dst[:], oh[:], ALU.add, reads=["oh"], writes=[dk_])
                yield
            pg.stt("dve", eidf[:].rearrange("p (h j) -> p h j", h=8), ia[:], 128.0, ib[:], ALU.mult, ALU.add, reads=["ia", "ib"], writes=["eidf"])
            yield
            pg.cp("dve", eidi[i % 3][:], eidf[:], reads=["eidf"], writes=[("eidi", i % 3)])
            yield

        GS = 4

        def gstep(i, e_):
            s = i % 2
            b_ = e_ % NB
            pg.dma("pool", lambda e, e_=e_, b_=b_, i=i: e.indirect_dma_start(
                out=uv[b_][:, :], out_offset=None, in_=dr["UV"][:, :],
                in_offset=bass.IndirectOffsetOnAxis(ap=eidi[i % 3][:, e_:e_ + 1], axis=0)),
                reads=[("eidi", i % 3)], writes=[("uv", b_)])
            pg.op("dve", lambda e, e_=e_, b_=b_, s=s: e.scalar_tensor_tensor(junk[:], uv[b_][:, 0:D], 1.0, h2b[s][:], ALU.mult, ALU.mult,
                                                                              accum_out=actv[s][:, e_:e_ + 1]),
                  reads=[("uv", b_), ("h2b", s)], writes=[("act", s, e_)])

        def gelu_grp(i, k):
            s = i % 2
            sl = slice(k * GS, (k + 1) * GS)
            pg.act(ga[s][:, sl], actv[s][:, sl], AF.Gelu, reads=[("act", s, e_) for e_ in range(k * GS, (k + 1) * GS)], writes=[("ga", s, k)])

        def fin_grp(i, k):
            s = i % 2
            sl = slice(k * GS, (k + 1) * GS)
            pg.tt("dve", ga[s][:, sl], ga[s][:, sl], gate[s][:, sl], ALU.mult, reads=[("ga", s, k), ("gate", s)], writes=[("ga", s, k)])
            for e_ in range(k * GS, (k + 1) * GS):
                b_ = e_ % NB
                d_ = e_ % 4
                pg.act(dg[d_][:], idb[:], AF.Copy, reads=[("ga", s, k), "idb"], writes=[("dg", d_)], scale=ga[s][:, e_:e_ + 1])
                for n_ in range(2):
                    pg.mm(psY[n_][:], dg[d_][:], uv[b_][:, D + n_ * 512:D + (n_ + 1) * 512], start=(e_ == 0), stop=(e_ == 127),
                          reads=[("dg", d_), ("uv", b_)], writes=[("psY", n_)])

        def tail(i):
            s = i % 2
            t0 = i * 128
            for n_ in range(2):
                pg.tt("dve", yo[0][:, n_ * 512:(n_ + 1) * 512], psY[n_][:], x1[i % 3][:, n_ * 512:(n_ + 1) * 512], ALU.add,
                      reads=[("psY", n_), ("x1", i % 3)], writes=[("yo", 0, n_)])
            pg.ld(dr["out"][t0:t0 + 128, :], yo[0][:], reads=[("yo", 0, 0), ("yo", 0, 1)], writes=[("out", i)])

        def drain(g, n=None):
            k = 0
            while g is not None and (n is None or k < n):
                try:
                    next(g)
                except StopIteration:
                    return None
                k += 1
            return g

        drain(front(0))
        for i in range(ntiles):
            gen2 = front(i + 1) if i + 1 < ntiles else None
            for k in range(128 // GS):
                for e_ in range(k * GS, (k + 1) * GS):
                    gstep(i, e_)
                    gen2 = drain(gen2, 3)
                gelu_grp(i, k)
                if k >= 1:
                    fin_grp(i, k - 1)
            fin_grp(i, 128 // GS - 1)
            drain(gen2)
            tail(i)
        pg.barrier()
        pg.emit()


_NC_CACHE = {}


def kernel(**inputs):
    inputs = {k: np.asarray(v) for k, v in inputs.items()}
    ntl = NT // 2
    if "nc" not in _NC_CACHE:
        _NC_CACHE["nc"] = build([stage1, stage2a, stage2x, stage2c, stage3, stage4, stage5], peer_tiles=ntl)
    nc = _NC_CACHE["nc"]
    base = {}
    in_maps = []
    for c in range(8):
        b, hh = c % 4, c // 4
        if hh not in base:
            base[hh] = host_inputs(inputs, b, hh, ntl)
            m = base[hh]
        else:
            m = dict(base[hh])
            m["x"] = core_x(inputs, b, hh)
        in_maps.append(m)
    res = run_bass_kernel_spmd(nc, in_maps, core_ids=list(range(8)))
    out = np.zeros((4, T, D), np.float32)
    for c in range(8):
        b, hh = c % 4, c // 4
        out[b, hh * ntl * 128:(hh + 1) * ntl * 128, :] = res.results[c]["out"]
    return out


def stage2x(C):
    nc, pg, dr = C.nc, C.pg, C.dr
    with ExitStack() as st:
        sb, ps = _mk(C, st)
        idf = sb("x_idf", [128, 128], F32)
        tri = sb("x_tri", [128, 128], F32)
        msk = sb("x_msk", [128, 3, 128], F32)
        ones = sb("x_ones", [128, 1], F32)
        inp = [[sb(f"x_in{s}_{j}", [128, 512], F32) for j in range(6)] for s in range(2)]
        Pt = sb("x_P", [128, 512], F32)
        iP = sb("x_iP", [128, 512], F32)
        Pp = sb("x_Pp", [128, 512], F32)
        tm = [[sb(f"x_tm{s}_{j}", [128, 512], F32) for j in range(4)] for s in range(2)]
        fm = [[sb(f"x_fm{s}_{j}", [64, 8, 128], F32) for j in range(4)] for s in range(2)]
        M = [[sb(f"x_M{s}_{j}", [128, 8, 128], (BF16 if j in (0, 4) else F32)) for j in range(5)] for s in range(2)]
        Xb = sb("x_Xb", [128, 8, 128], BF16)
        idb = sb("x_idb", [128, 128], BF16)
        X = [sb(f"x_X{s}", [128, 8, 128], F32) for s in range(2)]
        PC = [sb(f"x_PC{s}", [64, 8], F32) for s in range(2)]
        N2 = [sb(f"x_N2_{j}", [128, 8, 128], BF16) for j in range(2)]
        N2T = [sb(f"x_N2T_{j}", [128, 8, 128], BF16) for j in range(2)]
        Z = [sb(f"x_Z{j}", [64, 512], F32) for j in range(2)]
        rhs_sb = sb("x_rhs", [128, 512], F32)
        U_sb = sb("x_U", [128, 512], F32)
        Y_sb = [sb(f"x_Y{j}", [128, 512], F32) for j in range(2)]
        bank = [ps(f"x_bank{j}", [128, 512], F32) for j in range(8)]

        pg.ld(idf[:], dr["ident"][:, :], writes=["idf"])
        pg.ld(tri[:], dr["c_tri"][:, :], writes=["tri"])
        pg.ld(msk[:], dr["c_msk"][:, :, :], writes=["msk"])
        pg.memset("dve", ones[:], 1.0, writes=["ones"])
        pg.cp("dve", idb[:], idf[:], reads=["idf"], writes=["idb"])
        pg.memset("dve", Z[0][:], 0.0, writes=[("Z", 0)])
        names = ("RR", "RKK", "RLW", "RB", "RKp", "RV")
        bk = [0]

        def nb():
            v = bk[0]
            bk[0] = (v + 1) % 8
            return v

        def pre(c):
            s = c % 2
            t0 = c * 128
            I = inp[s]
            for j, nm in enumerate(names):
                pg.ld(I[j][:], dr[nm][t0:t0 + 128, :], reads=[(nm, c)], writes=[("in", s, j)])
            r_, kkn, lw, b_, kp, v_ = [t_[:] for t_ in I]
            bL = nb()
            pg.mm(bank[bL][:], tri[:], lw, reads=["tri", ("in", s, 2)], writes=[("bank", bL)])
            bC = nb()
            for h in range(8):
                pg.mm(bank[bC][0:64, h:h + 1], I[2][:, h * 64:(h + 1) * 64], ones[:, 0:1], reads=[("in", s, 2), "ones"], writes=[("bank", bC)])
            pg.act(PC[s][:], bank[bC][0:64, 0:8], AF.Exp, reads=[("bank", bC)], writes=[("PC", s)])
            pg.act(Pt[:], bank[bL][:], AF.Exp, reads=[("bank", bL)], writes=["P"])
            pg.act(iP[:], bank[bL][:], AF.Exp, reads=[("bank", bL)], writes=["iP"], scale=-1.0)
            pg.tt("dve", Pp[:], bank[bL][:], lw, ALU.subtract, reads=[("bank", bL), ("in", s, 2)], writes=["Pp"])
            pg.act(Pp[:], Pp[:], AF.Exp, reads=["Pp"], writes=["Pp"])
            TM = tm[s]
            pg.tt("pool", TM[0][:], r_, Pt[:], ALU.mult, reads=[("in", s, 0), "P"], writes=[("tm", s, 0)])
            pg.stt("dve", TM[1][:], kkn, -1.0, Pp[:], ALU.mult, ALU.mult, reads=[("in", s, 1), "Pp"], writes=[("tm", s, 1)])
            pg.tt("pool", TM[2][:], b_, iP[:], ALU.mult, reads=[("in", s, 3), "iP"], writes=[("tm", s, 2)])
            pg.tt("dve", TM[3][:], kp, iP[:], ALU.mult, reads=[("in", s, 4), "iP"], writes=[("tm", s, 3)])
            for j in range(4):
                if j == 0 and c < NT // 2:
                    continue
                for hg in range(2):
                    bT = nb()
                    for hh in range(4):
                        h = hg * 4 + hh
                        pg.tr(bank[bT][0:64, hh * 128:(hh + 1) * 128], TM[j][:, h * 64:(h + 1) * 64], idf[:], reads=[("tm", s, j), "idf"], writes=[("bank", bT)])
                    pg.cp("act" if (j + hg) % 2 == 0 else "dve", fm[s][j][:, hg * 4:(hg + 1) * 4, :].rearrange("p a b -> p (a b)"), bank[bT][0:64, :],
                          reads=[("bank", bT)], writes=[("fm", s, j, hg)])
            FR, FKK, FB, FK = fm[s]
            combos = ((0, FB, 2, FKK, 1, 0), (1, FK, 3, FKK, 1, 0), (2, FB, 2, FR, 0, 1), (3, FK, 3, FR, 0, 1), (4, FKK, 1, FB, 2, 2))
            for hg in range(2):
                for (mi, L_, lj, R_, rj, mk) in combos:
                    if mi in (2, 3) and c < NT // 2:
                        continue
                    bM = nb()
                    for hh in range(4):
                        h = hg * 4 + hh
                        pg.mm(bank[bM][:, hh * 128:(hh + 1) * 128], L_[:, h, :], R_[:, h, :], reads=[("fm", s, lj, hg), ("fm", s, rj, hg)], writes=[("bank", bM)])
                    pg.tt("dve", M[s][mi][:, hg * 4:(hg + 1) * 4, :], bank[bM][:].rearrange("p (a b) -> p a b", a=4),
                          msk[:, mk, :].unsqueeze(1).to_broadcast([128, 4, 128]), ALU.mult, reads=[("bank", bM), "msk"], writes=[("M", s, mi, hg)])
                pg.tt("pool", Xb[:, hg * 4:(hg + 1) * 4, :], idb[:, :].unsqueeze(1).to_broadcast([128, 4, 128]), M[s][0][:, hg * 4:(hg + 1) * 4, :], ALU.subtract,
                      reads=[("M", s, 0, hg), "idb"], writes=[("Xb", hg)])
            curN = [M[s][0], M[s][0]]
            curNT = [M[s][4], M[s][4]]
            kN = [("M", s, 0, 0), ("M", s, 0, 1)]
            kNT = [("M", s, 4, 0), ("M", s, 4, 1)]
            for j in range(6):
                dst = j % 2
                for hg in range(2):
                    b1, b2 = nb(), nb()
                    for hh in range(4):
                        h = hg * 4 + hh
                        pg.mm(bank[b1][:, hh * 128:(hh + 1) * 128], curNT[hg][:, h, :], curN[hg][:, h, :], reads=[kN[hg], kNT[hg]], writes=[("bank", b1)])
                    for hh in range(4):
                        h = hg * 4 + hh
                        pg.mm(bank[b2][:, hh * 128:(hh + 1) * 128], curN[hg][:, h, :], curNT[hg][:, h, :], reads=[kN[hg], kNT[hg]], writes=[("bank", b2)])
                    pg.cp("act", N2[dst][:, hg * 4:(hg + 1) * 4, :].rearrange("p a b -> p (a b)"), bank[b1][:], reads=[("bank", b1)], writes=[("N2", dst, hg)])
                    pg.cp("dve", N2T[dst][:, hg * 4:(hg + 1) * 4, :].rearrange("p a b -> p (a b)"), bank[b2][:], reads=[("bank", b2)], writes=[("N2T", dst, hg)])
                for hg in range(2):
                    curN[hg], curNT[hg] = N2[dst], N2T[dst]
                    kN[hg], kNT[hg] = ("N2", dst, hg), ("N2T", dst, hg)
                for hg in range(2):
                    b3 = nb()
                    for hh in range(4):
                        h = hg * 4 + hh
                        pg.mm(bank[b3][:, hh * 128:(hh + 1) * 128], curNT[hg][:, h, :], Xb[:, h, :], reads=[kNT[hg], ("Xb", hg)], writes=[("bank", b3)])
                    xo = (X[s] if j == 5 else Xb)
                    pg.tt("dve", xo[:, hg * 4:(hg + 1) * 4, :].rearrange("p a b -> p (a b)"), Xb[:, hg * 4:(hg + 1) * 4, :].rearrange("p a b -> p (a b)"), bank[b3][:], ALU.add,
                          reads=[("bank", b3), ("Xb", hg)], writes=[("X", s, hg)] if j == 5 else [("Xb", hg)])

        def seq(c):
            s = c % 2
            t0 = c * 128
            zc, zn = Z[c % 2], Z[(c + 1) % 2]
            kz, kzn = ("Z", c % 2), ("Z", (c + 1) % 2)
            FR, FKK, FB, FK = fm[s]
            V = inp[s][5]
            hsl = lambda h: slice(h * 64, (h + 1) * 64)
            Mk = lambda mi: [("M", s, mi, 0), ("M", s, mi, 1)]
            fk = lambda j: [("fm", s, j, 0), ("fm", s, j, 1)]
            Xk = [("X", s, 0), ("X", s, 1)]
            bG = nb()
            for h in range(8):
                pg.mm(bank[bG][:, hsl(h)], M[s][1][:, h, :], V[:, hsl(h)], start=True, stop=False, reads=Mk(1) + [("in", s, 5)], writes=[("bank", bG)])
                pg.mm(bank[bG][:, hsl(h)], FKK[:, h, :], zc[:, hsl(h)], start=False, stop=True, reads=fk(1) + [kz], writes=[("bank", bG)])
            pg.ts("dve", rhs_sb[:], bank[bG][:], -1.0, None, ALU.mult, reads=[("bank", bG)], writes=["rhs"])
            bU = nb()
            for h in range(8):
                pg.mm(bank[bU][:, hsl(h)], X[s][:, h, :], rhs_sb[:, hsl(h)], reads=Xk + ["rhs"], writes=[("bank", bU)])
            pg.cp("act", U_sb[:], bank[bU][:], reads=[("bank", bU)], writes=["U"])
            bZ = nb()
            for h in range(8):
                pg.mm(bank[bZ][0:64, hsl(h)], tm[s][3][:, hsl(h)], V[:, hsl(h)], start=True, stop=False, reads=[("tm", s, 3), ("in", s, 5)], writes=[("bank", bZ)])
                pg.mm(bank[bZ][0:64, hsl(h)], idf[0:64, 0:64], zc[:, hsl(h)], start=False, stop=False, reads=["idf", kz], writes=[("bank", bZ)])
                pg.mm(bank[bZ][0:64, hsl(h)], tm[s][2][:, hsl(h)], U_sb[:, hsl(h)], start=False, stop=True, reads=[("tm", s, 2), "U"], writes=[("bank", bZ)])
            pg.tt("dve", zn[:].rearrange("p (h v) -> p h v", h=8), bank[bZ][0:64, :].rearrange("p (h v) -> p h v", h=8),
                  PC[s][:, :].unsqueeze(2).to_broadcast([64, 8, 64]), ALU.mult, reads=[("bank", bZ), ("PC", s)], writes=[kzn])
            if c < NT // 2:
                return
            bY = nb()
            for h in range(8):
                pg.mm(bank[bY][:, hsl(h)], M[s][3][:, h, :], V[:, hsl(h)], start=True, stop=False, reads=Mk(3) + [("in", s, 5)], writes=[("bank", bY)])
                pg.mm(bank[bY][:, hsl(h)], FR[:, h, :], zc[:, hsl(h)], start=False, stop=False, reads=fk(0) + [kz], writes=[("bank", bY)])
                pg.mm(bank[bY][:, hsl(h)], M[s][2][:, h, :], U_sb[:, hsl(h)], start=False, stop=True, reads=Mk(2) + ["U"], writes=[("bank", bY)])
            pg.cp("act", Y_sb[s][:], bank[bY][:], reads=[("bank", bY)], writes=[("Y", s)])
            pg.ld(dr["YS"][t0:t0 + 128, :], Y_sb[s][:], reads=[("Y", s)], writes=[("YS", c)])

        tcg = table_conv_gen(C, sb)
        pre(0)
        for c in range(NT):
            if c + 1 < NT:
                pre(c + 1)
            for _ in range(8):
                next(tcg, None)
            seq(c)
        for _ in tcg:
            pass
        pg.barrier()
        pg.emit()
```

```python
import numpy as np
import concourse.bass as bass
import concourse.mybir as mybir

F32 = mybir.dt.float32
BF16 = mybir.dt.bfloat16
I32 = mybir.dt.int32
U32 = mybir.dt.uint32
ALU = mybir.AluOpType
AF = mybir.ActivationFunctionType
AX = mybir.AxisListType

EPOCH = 20000
ENGS = ("pe", "act", "dve", "pool", "sp")
NDMASEM = 16


class Prog:
    def __init__(self, nc, stack):
        self.nc = nc
        self.stack = stack
        self.ops = {e: [] for e in ENGS}
        self.cnt = {e: 0 for e in ENGS}
        self.esems = {e: [] for e in ENGS}
        self.waited = {e: {} for e in ENGS}
        self.lastw = {}
        self.readers = {}
        self.dsems = {}
        self.dcount = {}
        self.dtarget = {}
        self.semobjs = {}
        self.alltokens = {}
        for q in ("sp", "act", "pool"):
            self.dsems[q] = [self._newsem(f"d_{q}_{i}") for i in range(NDMASEM)]
            self.dcount[q] = 0
            self.dtarget[q] = [0] * NDMASEM

    def _newsem(self, name):
        s = self.stack.enter_context(self.nc.semaphore(name))
        self.semobjs[name] = s
        return name

    def _esem(self, e, idx):
        ep = idx // EPOCH
        while len(self.esems[e]) <= ep:
            self.esems[e].append(self._newsem(f"e_{e}_{len(self.esems[e])}"))
        return self.esems[e][ep], (idx % EPOCH) + 1

    def _deps(self, reads, writes):
        toks = []
        for k in reads:
            t = self.lastw.get(k)
            if t is not None:
                toks.append(t)
        for k in writes:
            t = self.lastw.get(k)
            if t is not None:
                toks.append(t)
            toks.extend(self.readers.get(k, ()))
        return toks

    def _commit(self, tok, reads, writes):
        for k in reads:
            self.readers.setdefault(k, []).append(tok)
        for k in writes:
            self.lastw[k] = tok
            self.readers[k] = []
        self.alltokens[tok[0]] = max(self.alltokens.get(tok[0], 0), tok[1])

    def _waits(self, e, toks):
        need = {}
        for (s, v) in toks:
            if v > need.get(s, 0):
                need[s] = v
        out = []
        w = self.waited[e]
        for s, v in need.items():
            if w.get(s, 0) < v:
                w[s] = v
                out.append((s, v))
        return out

    def op(self, e, fn, reads=(), writes=()):
        toks = self._deps(reads, writes)
        if e == "pe":
            toks = [t for t in toks if not t[0].startswith("e_pe_")]
        waits = self._waits(e, toks)
        idx = self.cnt[e]
        self.cnt[e] += 1
        tok = self._esem(e, idx)
        self.ops[e].append((waits, fn, (tok[0], 1)))
        self._commit(tok, reads, writes)

    def dma(self, q, fn, reads=(), writes=()):
        toks = self._deps(reads, writes)
        n = self.dcount[q]
        self.dcount[q] += 1
        slot = n % NDMASEM
        sname = self.dsems[q][slot]
        prev = self.dtarget[q][slot]
        if prev > 0:
            toks.append((sname, prev))
        tgt = prev + 16
        self.dtarget[q][slot] = tgt
        waits = self._waits(q, toks)
        tok = (sname, tgt)
        self.ops[q].append((waits, fn, (sname, 16)))
        self._commit(tok, reads, writes)

    def mm(self, out, lhsT, rhs, start=True, stop=True, reads=(), writes=()):
        self.op("pe", lambda e: e.matmul(out, lhsT, rhs, start=start, stop=stop), reads, writes)

    def tr(self, out, in_, ident, reads=(), writes=()):
        self.op("pe", lambda e: e.transpose(out, in_, ident), reads, writes)

    def act(self, out, in_, func, reads=(), writes=(), bias=None, scale=None, eng="act"):
        kw = {}
        if bias is not None:
            kw["bias"] = bias
        if scale is not None:
            kw["scale"] = scale
        self.op(eng, lambda e: e.activation(out, in_, func, **kw), reads, writes)

    def tt(self, eng, out, in0, in1, op, reads=(), writes=()):
        self.op(eng, lambda e: e.tensor_tensor(out, in0, in1, op), reads, writes)

    def ts(self, eng, out, in0, s1, s2, op0, op1=None, reads=(), writes=()):
        if op1 is None:
            self.op(eng, lambda e: e.tensor_scalar(out, in0, s1, s2, op0), reads, writes)
        else:
            self.op(eng, lambda e: e.tensor_scalar(out, in0, s1, s2, op0, op1), reads, writes)

    def stt(self, eng, out, in0, scalar, in1, op0, op1, reads=(), writes=()):
        self.op(eng, lambda e: e.scalar_tensor_tensor(out, in0, scalar, in1, op0, op1), reads, writes)

    def cp(self, eng, out, in_, reads=(), writes=()):
        if eng == "act":
            self.op(eng, lambda e: e.copy(out, in_), reads, writes)
        else:
            self.op(eng, lambda e: e.tensor_copy(out, in_), reads, writes)

    def red(self, eng, out, in_, op, reads=(), writes=(), axis=None):
        ax = AX.X if axis is None else axis
        self.op(eng, lambda e: e.tensor_reduce(out, in_, ax, op), reads, writes)

    def memset(self, eng, ap, val, writes=()):
        self.op(eng, lambda e: e.memset(ap, val), (), writes)

    def ld(self, out, in_, reads=(), writes=(), q="sp"):
        self.dma(q, lambda e: e.dma_start(out, in_), reads, writes)

    def barrier(self):
        toks = list(self.alltokens.items())
        for e in ENGS:
            waits = self._waits(e, toks)
            if waits:
                self.ops[e].append((waits, None, None))
        self.lastw = {}
        self.readers = {}

    def emit(self):
        nc = self.nc
        so = self.semobjs
        with nc.Block() as block:
            def mk(e):
                def body(eng):
                    for waits, fn, inc in self.ops[e]:
                        for (s, v) in waits:
                            eng.wait_ge(so[s], v)
                        if fn is not None:
                            ins = fn(eng)
                            ins.then_inc(so[inc[0]], inc[1])
                return body
            block.tensor(mk("pe"))
            block.scalar(mk("act"))
            block.vector(mk("dve"))
            block.gpsimd(mk("pool"))
            block.sync(mk("sp"))
        self.ops = {e: [] for e in ENGS}
from contextlib import ExitStack
from concourse.bass_utils import run_bass_kernel_spmd

T = 4096
D = 1024
NT = T // 128
INW = 5144
RWC = 1792
O_RW = 0
O_Q = 1792
O_KC = 2304
O_VC = 2432
O_KS = 2560
O_VS = 2688
O_KW = 2816
O_VW = 2944
O_BG = 3072
O_GA = 3096
O_GB = 4120


class Ctx:
    pass


def _mk(C, st):
    nc = C.nc
    sb = lambda name, shape, dt: st.enter_context(nc.sbuf_tensor(name, shape, dt))
    ps = lambda name, shape, dt: st.enter_context(nc.psum_tensor(name, shape, dt))
    return sb, ps


def stage1(C):
    nc, pg, dr = C.nc, C.pg, C.dr
    with ExitStack() as st:
        sb, ps = _mk(C, st)
        win = sb("s1_win", [128, 8, INW], BF16)
        pj = [sb(f"s1_pj{i}", [128, INW], F32) for i in range(2)]
        xt = [sb(f"s1_xt{i}", [128, D], F32) for i in range(2)]
        junk = sb("s1_junk", [128, D], F32)
        hb = [sb(f"s1_h{i}", [128, D], BF16) for i in range(2)]
        hT = [sb(f"s1_hT{i}", [128, 8, 128], BF16) for i in range(2)]
        gt = sb("s1_g", [128, D], F32)
        idf = sb("s1_idf", [128, 128], F32)
        idb = sb("s1_idb", [128, 128], BF16)
        ss = [sb(f"s1_ss{i}", [128, 4], F32) for i in range(2)]
        psT = [ps(f"s1_psT{i}", [128, 8, 128], BF16) for i in range(2)]
        psm = [ps(f"s1_psm{i}", [128, 512], F32) for i in range(4)]

        pg.ld(gt[:], dr["norm1_g_b"][:, :], writes=["gt"])
        pg.ld(idf[:], dr["ident"][:, :], writes=["idf"])
        pg.cp("dve", idb[:], idf[:], reads=["idf"], writes=["idb"])
        engs = ["act", "dve", "pool"]
        for kc in range(8):
            b = pj[kc % 2]
            pg.ld(b[:], dr["w_in"][kc * 128:(kc + 1) * 128, :], writes=[("pjall", kc % 2)])
            pg.cp(engs[kc % 3], win[:, kc, :], b[:], reads=[("pjall", kc % 2)], writes=[("win", kc)])
        winkeys = [("win", kc) for kc in range(8)]
        chunks = []
        c0 = 0
        while c0 < INW:
            w = min(512, INW - c0)
            chunks.append((c0, w))
            c0 += w
        def A1(i):
            s = i % 2
            pg.ld(xt[s][:], dr["x"][i * 128:(i + 1) * 128, :], writes=[("xt", s)])
            pg.tt("dve", junk[:], xt[s][:], xt[s][:], ALU.mult, reads=[("xt", s)], writes=["junk"])
            pg.red("dve", ss[s][:, 0:1], junk[:], ALU.add, reads=["junk"], writes=[("ss", s)])
            pg.act(ss[s][:, 1:2], ss[s][:, 0:1], AF.Sqrt, reads=[("ss", s)], writes=[("ss1", s)],
                   scale=1.0 / D, bias=C.eps6[:, 0:1])
            pg.op("dve", lambda e, o=ss[s][:, 2:3], a=ss[s][:, 1:2]: e.reciprocal(o, a),
                  reads=[("ss1", s)], writes=[("ss2", s)])
            pg.stt("dve", hb[s][:], xt[s][:], ss[s][:, 2:3], gt[:], ALU.mult, ALU.mult,
                   reads=[("xt", s), ("ss2", s), "gt"], writes=[("hb", s)])

        def A2(i):
            s = i % 2
            for j in range(8):
                pg.tr(psT[s][:, j, :], hb[s][:, j * 128:(j + 1) * 128], idb[:],
                      reads=[("hb", s), "idb"], writes=[("psT", s)])
            pg.cp("act", hT[s][:], psT[s][:], reads=[("psT", s)], writes=[("hT", s)])

        chunks_lo = [(512, 512), (1024, 512), (1536, 128), (2304, 512), (2816, 256)]

        def B(i, lo, hi):
            s = i % 2
            if i < NT // 2 - 1:
                if lo != 0:
                    return
                for ci, (c0, w) in enumerate(chunks_lo):
                    pb = psm[ci % 4]
                    for kc in range(8):
                        pg.mm(pb[:, :w], hT[s][:, kc, :], win[:, kc, c0:c0 + w], start=(kc == 0), stop=(kc == 7),
                              reads=[("hT", s), ("win", kc)], writes=[("psm", ci % 4)])
                    pg.cp("act" if ci % 2 == 0 else "dve", pj[s][:, c0:c0 + w], pb[:, :w],
                          reads=[("psm", ci % 4)], writes=[("pj", s, k_) for k_ in range(len(chunks))] + ([("pjall", s)] if i < 8 else []))
                return
            for ci in range(lo, hi):
                c0, w = chunks[ci]
                pb = psm[ci % 4]
                for kc in range(8):
                    pg.mm(pb[:, :w], hT[s][:, kc, :], win[:, kc, c0:c0 + w], start=(kc == 0), stop=(kc == 7),
                          reads=[("hT", s), ("win", kc)], writes=[("psm", ci % 4)])
                pg.cp("act" if ci % 2 == 0 else "dve", pj[s][:, c0:c0 + w], pb[:, :w],
                      reads=[("psm", ci % 4)], writes=[("pj", s, ci), ("pjall", s)] if i < 8 else [("pj", s, ci)])

        def S(i):
            s = i % 2
            pg.ld(dr["P"][i * 128:(i + 1) * 128, 0:3096], pj[s][:, 0:3096],
                  reads=[("pj", s, ci) for ci in range(len(chunks))], writes=[("P", i)])
            pg.ld(dr["PG"][i * 128:(i + 1) * 128, :], pj[s][:, 3096:5144],
                  reads=[("pj", s, ci) for ci in range(len(chunks))], writes=[("PG", i)], q="act")

        A1(0)
        A2(0)
        A1(1)
        for i in range(NT):
            B(i, 0, 6)
            if i + 1 < NT:
                A2(i + 1)
            if i + 2 < NT:
                A1(i + 2)
            B(i, 6, len(chunks))
            S(i)
        pg.barrier()
        pg.emit()


def build(stages, dbg_out=(), dbg_in=(), lvl=9, sub=9, peer_tiles=NT):
    nc = bass.Bass("TRN2", target_bir_lowering=False)
    C = Ctx()
    C.peer_tiles = peer_tiles
    C.lvl = lvl
    C.sub = sub
    C.nc = nc
    dr = {}
    C.dr = dr

    def din(name, shape, dt=F32):
        dr[name] = nc.dram_tensor(name, list(shape), dt, kind="ExternalInput").ap()

    def dscr(name, shape, dt=F32):
        kind = "ExternalOutput" if name in dbg_out else ("ExternalInput" if name in dbg_in else "Internal")
        dr[name] = nc.dram_tensor(name, list(shape), dt, kind=kind).ap()

    din("x", [T, D])
    din("norm1_g_b", [128, D])
    din("ident", [128, 128])
    din("w_in", [D, INW])
    dscr("P", [T, INW])
    dscr("PG", [T, 2048])
    for nm in ("rw_mu_b",):
        din(nm, [128, RWC])
    for nm in ("rw_w0_b", "rw_a0_b", "rw_k_k_b", "rw_k_a_b", "rw_r_k_b", "rw_ln_w_b", "rw_ln_b_b", "rw_g_up"):
        din(nm, [128, 512])
    din("rw_w_up", [64, 512])
    din("rw_a_up", [64, 512])
    for nm in ("RB", "RKp", "RV", "RG", "YA", "RR", "RKK", "RLW", "YS"):
        dscr(nm, [T, 512])
    dscr("RBON", [T, 8])
    din("blkmask", [8, 512])
    din("c_tri", [128, 128])
    din("c_msk", [128, 3, 128])
    din("nsa_gains_b", [128, 768])
    din("nsa_kc_g_b", [128, 64])
    din("ovl", [128, 2, 64])
    din("posT", [128, 2, 32])
    din("cmp_w2", [128, 2, 2, 64])
    din("cmp_w1", [2, 128, 32, 256])
    din("c_cmpb", [128, 2, T], BF16)
    din("c_esel", [128, 32, 128], BF16)
    din("c_causb", [128, 4, 512], BF16)
    din("c_winb", [128, 8, 512], BF16)
    din("c_winb4", [128, 8, 512], BF16)
    din("c_vmfb", [NT, 128, 2, 64])
    dscr("YB", [T, 512])
    din("w_branch_a", [512, D])
    din("w_branch_b", [512, D])
    din("w_out", [D, D])
    dscr("X1L", [peer_tiles * 128, D])
    dscr("YAL", [peer_tiles * 128, 512])
    din("norm2_g_b", [128, D])
    din("iota16", [128, 16])
    din("rowidx", [128, NT], I32)
    din("peer_wq", [D, 2048])
    din("peer_k1", [128, 128])
    din("peer_k2", [128, 128])
    din("peer_u", [16384, D])
    din("peer_v", [16384, D])
    dscr("UV", [16384, 2 * D], BF16)
    dr["out"] = nc.dram_tensor("out", [peer_tiles * 128, D], F32, kind="ExternalOutput").ap()
    with ExitStack() as top:
        pg = Prog(nc, top)
        C.pg = pg
        C.eps6 = top.enter_context(nc.sbuf_tensor("c_eps6", [128, 1], F32))
        pg.memset("dve", C.eps6[:], 1e-6, writes=["eps6"])
        pg.barrier()
        for s in stages:
            s(C)
        pg.barrier()
        pg.emit()
    return nc


def core_x(inputs, b, hh):
    xb = np.asarray(inputs["x"][b])
    if hh == 0:
        return np.ascontiguousarray(np.concatenate([np.zeros((T // 2, D), np.float32), xb[0:T // 2]], 0))
    return np.ascontiguousarray(xb)


def host_inputs(inputs, b, hh=1, ntl=NT):
    g = lambda k: np.ascontiguousarray(inputs[k][0])
    m = {}
    m["x"] = core_x(inputs, b, hh)
    m["norm1_g_b"] = np.ascontiguousarray(np.broadcast_to(g("norm1_g")[None, :], (128, D)))
    m["ident"] = np.eye(128, dtype=np.float32)
    m["w_in"] = g("w_in")
    bc = lambda a: np.ascontiguousarray(np.broadcast_to(np.asarray(a).reshape(1, -1), (128, a.size)))
    m["rw_mu_b"] = bc(g("rw_mu"))
    for nm in ("rw_w0", "rw_a0", "rw_k_k", "rw_k_a", "rw_r_k", "rw_ln_w", "rw_ln_b"):
        m[nm + "_b"] = bc(g(nm))
    for nm in ("rw_g_up", "rw_w_up", "rw_a_up"):
        m[nm] = g(nm)
    bmk = np.zeros((8, 512), np.float32)
    for h in range(8):
        bmk[h, h * 64:(h + 1) * 64] = 1.0
    m["blkmask"] = bmk
    ii = np.arange(128)
    m["c_tri"] = (ii[:, None] <= ii[None, :]).astype(np.float32)
    m["c_msk"] = np.ascontiguousarray(np.stack([(ii[:, None] < ii[None, :]), (ii[:, None] <= ii[None, :]), (ii[:, None] > ii[None, :])], 1).astype(np.float32))
    m.update(nsa_consts(hh))
    for nm in ("w_branch_a", "w_branch_b", "w_out", "peer_wq", "peer_k1", "peer_k2", "peer_u", "peer_v"):
        m[nm] = g(nm)
    m["norm2_g_b"] = bc(g("norm2_g"))
    ri = np.zeros((128, NT), np.int32)
    ri[:, :ntl] = ((NT - ntl) * 128 + np.arange(ntl)[None, :] * 128 + np.arange(128)[:, None]).astype(np.int32)
    m["rowidx"] = ri
    m["iota16"] = np.ascontiguousarray(np.broadcast_to(np.arange(16, dtype=np.float32)[None, :], (128, 16)))
    m["nsa_gains_b"] = bc(np.concatenate([np.tile(g("nsa_q_g"), 8), np.tile(g("nsa_ks_g"), 2), np.tile(g("nsa_kw_g"), 2)]))
    m["nsa_kc_g_b"] = bc(g("nsa_kc_g"))
    posT = np.zeros((128, 2, 32), np.float32)
    posT[0:64, 0, :] = g("cmp_pos_k").T
    posT[0:64, 1, :] = g("cmp_pos_v").T
    m["posT"] = posT
    w2 = np.stack([g("cmp_k_w2").reshape(2, 128, 64), g("cmp_v_w2").reshape(2, 128, 64)], 0)
    m["cmp_w2"] = np.ascontiguousarray(w2.transpose(2, 0, 1, 3))
    w1 = []
    for nm in ("cmp_k_w1", "cmp_v_w1"):
        a = g(nm).reshape(32, 64, 256).transpose(1, 0, 2)
        w1.append(np.concatenate([a, a], 0))
    m["cmp_w1"] = np.ascontiguousarray(np.stack(w1, 0))
    return m


def dap(ap, offset, pattern):
    return bass.AP(ap.tensor, offset, [list(p) for p in pattern])


def stage2a(C):
    nc, pg, dr = C.nc, C.pg, C.dr
    with ExitStack() as st:
        sb, ps = _mk(C, st)
        mu = sb("a_mu", [128, RWC], F32)
        w0 = sb("a_w0", [128, 512], F32)
        a0 = sb("a_a0", [128, 512], F32)
        kkc = sb("a_kk", [128, 512], F32)
        kac = sb("a_ka", [128, 512], F32)
        rkc = sb("a_rk", [128, 512], F32)
        wup = sb("a_wup", [128, 512], F32)
        gup = sb("a_gup", [128, 512], F32)
        idf = sb("a_idf", [128, 128], F32)
        p_2 = [sb(f"a_p{i_}", [128, RWC], F32) for i_ in range(2)]
        pv_2 = [sb(f"a_pv{i_}", [128, RWC], F32) for i_ in range(2)]
        pm_2 = [sb(f"a_pm{i_}", [128, RWC], F32) for i_ in range(2)]
        lor_2 = [sb(f"a_lor{i_}", [128, 256], F32) for i_ in range(2)]
        lorT_2 = [sb(f"a_lorT{i_}", [128, 256], F32) for i_ in range(2)]
        wt_2 = [sb(f"a_wt{i_}", [128, 512], F32) for i_ in range(2)]
        lwt_2 = [sb(f"a_lwt{i_}", [128, 512], F32) for i_ in range(2)]
        at_2 = [sb(f"a_at{i_}", [128, 512], F32) for i_ in range(2)]
        gt_2 = [sb(f"a_gt{i_}", [128, 512], F32) for i_ in range(2)]
        kk_2 = [sb(f"a_kkt{i_}", [128, 512], F32) for i_ in range(2)]
        sq_2 = [sb(f"a_sq{i_}", [128, 512], F32) for i_ in range(2)]
        nrm_2 = [sb(f"a_nrm{i_}", [128, 32], F32) for i_ in range(2)]
        kkn_2 = [sb(f"a_kkn{i_}", [128, 512], F32) for i_ in range(2)]
        bt_2 = [sb(f"a_bt{i_}", [128, 512], F32) for i_ in range(2)]
        t1_2 = [sb(f"a_t1{i_}", [128, 512], F32) for i_ in range(2)]
        kp_2 = [sb(f"a_kp{i_}", [128, 512], F32) for i_ in range(2)]
        bon_2 = [sb(f"a_bon{i_}", [128, 8], F32) for i_ in range(2)]
        psl = ps("a_psl", [128, 512], F32)
        psw = ps("a_psw", [128, 512], F32)
        psa = ps("a_psa", [128, 512], F32)
        psg = ps("a_psg", [128, 512], F32)

        for (tile, name) in ((mu, "rw_mu_b"), (w0, "rw_w0_b"), (a0, "rw_a0_b"), (kkc, "rw_k_k_b"),
                             (kac, "rw_k_a_b"), (rkc, "rw_r_k_b"), (gup, "rw_g_up"), (idf, "ident")):
            pg.ld(tile[:], dr[name][:, :], writes=[name])
        pg.ld(wup[0:64, :], dr["rw_w_up"][:, :], writes=["wup0"])
        pg.ld(wup[64:128, :], dr["rw_a_up"][:, :], writes=["wup1"])
        P = dr["P"]
        for i in range(NT):
            t0 = i * 128
            s = i % 2
            p, pv, pm, lor, lorT, wt, lwt, at, gt, kk, sq, nrm, kkn, bt, t1, kp, bon = [t_[s] for t_ in (
                p_2, pv_2, pm_2, lor_2, lorT_2, wt_2, lwt_2, at_2, gt_2, kk_2, sq_2, nrm_2, kkn_2, bt_2, t1_2, kp_2, bon_2)]
            pg.ld(p[:], P[t0:t0 + 128, 0:RWC], reads=[("P", i)], writes=[("p", s)])
            if i == 0:
                pg.memset("dve", pv[0:1, :], 0.0, writes=[("pv0", s)])
                pg.ld(pv[1:128, :], P[0:127, 0:RWC], reads=[("P", 0)], writes=[("pv", s)])
                pvk = [("pv", s), ("pv0", s)]
            else:
                pg.ld(pv[:], P[t0 - 1:t0 + 127, 0:RWC], reads=[("P", i), ("P", i - 1)], writes=[("pv", s), ("pv0", s)])
                pvk = [("pv", s), ("pv0", s)]
            pg.tt("dve", pv[:], pv[:], p[:], ALU.subtract, reads=pvk + [("p", s)], writes=[("pv", s)])
            pg.tt("dve", pv[:], pv[:], mu[:], ALU.mult, reads=[("pv", s), "rw_mu_b"], writes=[("pv", s)])
            pg.tt("dve", pm[:], pv[:], p[:], ALU.add, reads=[("pv", s), ("p", s)], writes=[("pm", s)])
            r_ = pm[:, 0:512]
            k_ = pm[:, 512:1024]
            v_ = pm[:, 1024:1536]
            pg.act(lor[:, 0:64], pm[:, 1536:1600], AF.Tanh, reads=[("pm", s)], writes=[("lor0", s)])
            pg.cp("pool", lor[:, 64:128], pm[:, 1600:1664], reads=[("pm", s)], writes=[("lor1", s)])
            pg.act(lor[:, 128:256], pm[:, 1664:1792], AF.Sigmoid, reads=[("pm", s)], writes=[("lor2", s)])
            pg.tr(psl[:, 0:128], lor[:, 0:128], idf[:], reads=[("lor0", s), ("lor1", s), "ident"], writes=["psl"])
            pg.tr(psl[:, 128:256], lor[:, 128:256], idf[:], reads=[("lor2", s), "ident"], writes=["psl"])
            pg.cp("act", lorT[:], psl[:, 0:256], reads=["psl"], writes=[("lorT", s)])
            pg.mm(psw[:], lorT[0:64, 0:128], wup[0:64, :], reads=[("lorT", s), "wup0"], writes=["psw"])
            pg.mm(psa[:], lorT[64:128, 0:128], wup[64:128, :], reads=[("lorT", s), "wup1"], writes=["psa"])
            pg.mm(psg[:], lorT[:, 128:256], gup[:], reads=[("lorT", s), "rw_g_up"], writes=["psg"])
            pg.tt("dve", wt[:], psw[:], w0[:], ALU.add, reads=["psw", "rw_w0_b"], writes=[("wt", s)])
            pg.act(wt[:], wt[:], AF.Sigmoid, reads=[("wt", s)], writes=[("wt", s)])
            pg.ts("dve", lwt[:], wt[:], -0.6065306597126334, None, ALU.mult, reads=[("wt", s)], writes=[("lwt", s)])
            pg.tt("dve", at[:], psa[:], a0[:], ALU.add, reads=["psa", "rw_a0_b"], writes=[("at", s)])
            pg.act(at[:], at[:], AF.Sigmoid, reads=[("at", s)], writes=[("at", s)])
            pg.cp("act", gt[:], psg[:], reads=["psg"], writes=[("gt", s)])
            pg.tt("dve", kk[:], k_, kkc[:], ALU.mult, reads=[("pm", s), "rw_k_k_b"], writes=[("kk", s)])
            pg.tt("pool", sq[:], kk[:], kk[:], ALU.mult, reads=[("kk", s)], writes=[("sq", s)])
            pg.red("dve", nrm[:, 0:8], sq[:].rearrange("p (h k) -> p h k", h=8), ALU.add, reads=[("sq", s)], writes=[("nrm0", s)])
            pg.act(nrm[:, 8:16], nrm[:, 0:8], AF.Sqrt, reads=[("nrm0", s)], writes=[("nrm1", s)])
            pg.ts("dve", nrm[:, 16:24], nrm[:, 8:16], 1e-12, None, ALU.max, reads=[("nrm1", s)], writes=[("nrm2", s)])
            pg.op("dve", lambda e, nrm=nrm: e.reciprocal(nrm[:, 24:32], nrm[:, 16:24]), reads=[("nrm2", s)], writes=[("nrm3", s)])
            rinv_b = nrm[:, 24:32].unsqueeze(2).to_broadcast([128, 8, 64])
            v3 = lambda tl: tl[:].rearrange("p (h k) -> p h k", h=8)
            pg.stt("dve", v3(kkn), v3(kk), -1.0, rinv_b, ALU.mult, ALU.mult, reads=[("kk", s), ("nrm3", s)], writes=[("kkn", s)])
            pg.stt("dve", bt[:], kkn[:], -1.0, at[:], ALU.mult, ALU.mult, reads=[("kkn", s), ("at", s)], writes=[("bt", s)])
            pg.stt("dve", t1[:], at[:], -1.0, kac[:], ALU.add, ALU.mult, reads=[("at", s), "rw_k_a_b"], writes=[("t1", s)])
            pg.stt("dve", kp[:], t1[:], 1.0, k_, ALU.add, ALU.mult, reads=[("t1", s), ("pm", s)], writes=[("kp", s)])
            pg.tt("pool", sq[:], r_, kp[:], ALU.mult, reads=[("pm", s), ("kp", s), ("sq", s)], writes=[("sq", s)])
            pg.tt("pool", sq[:], sq[:], rkc[:], ALU.mult, reads=[("sq", s), "rw_r_k_b"], writes=[("sq", s)])
            pg.red("dve", bon[:], sq[:].rearrange("p (h k) -> p h k", h=8), ALU.add, reads=[("sq", s)], writes=[("bon", s)])
            pg.ld(dr["RR"][t0:t0 + 128, :], r_, reads=[("pm", s)], writes=[("RR", i)], q="act")
            pg.ld(dr["RKK"][t0:t0 + 128, :], kkn[:], reads=[("kkn", s)], writes=[("RKK", i)], q="act")
            pg.ld(dr["RLW"][t0:t0 + 128, :], lwt[:], reads=[("lwt", s)], writes=[("RLW", i)], q="act")
            pg.ld(dr["RB"][t0:t0 + 128, :], bt[:], reads=[("bt", s)], writes=[("RB", i)], q="act")
            pg.ld(dr["RKp"][t0:t0 + 128, :], kp[:], reads=[("kp", s)], writes=[("RKp", i)], q="act")
            pg.ld(dr["RV"][t0:t0 + 128, :], v_, reads=[("pm", s)], writes=[("RV", i)], q="act")
            pg.ld(dr["RG"][t0:t0 + 128, :], gt[:], reads=[("gt", s)], writes=[("RG", i)], q="act")
            pg.ld(dr["RBON"][t0:t0 + 128, :], bon[:], reads=[("bon", s)], writes=[("RBON", i)], q="act")
        pg.barrier()
        pg.emit()


def igather(pg, out_ap, table_ap, idx_ap, reads, writes):
    pg.dma("pool", lambda e: e.indirect_dma_start(out=out_ap, out_offset=None, in_=table_ap,
                                                   in_offset=bass.IndirectOffsetOnAxis(ap=idx_ap, axis=0)), reads, writes)


def stage2c(C):
    nc, pg, dr = C.nc, C.pg, C.dr
    with ExitStack() as st:
        sb, ps = _mk(C, st)
        lnw = sb("c_lnw", [128, 512], F32)
        lnb = sb("c_lnb", [128, 512], F32)
        eps = sb("c_eps", [128, 1], F32)
        y = [sb(f"c_y{i}", [128, 8, 64], F32) for i in range(2)]
        v = [sb(f"c_v{i}", [128, 8, 64], F32) for i in range(2)]
        g = [sb(f"c_g{i}", [128, 512], F32) for i in range(2)]
        bon = [sb(f"c_bon{i}", [128, 8], F32) for i in range(2)]
        stt_ = [sb(f"c_st{i}", [128, 32], F32) for i in range(2)]
        sq = sb("c_sq", [128, 8, 64], F32)
        pg.ld(lnw[:], dr["rw_ln_w_b"][:, :], writes=["lnw"])
        pg.ld(lnb[:], dr["rw_ln_b_b"][:, :], writes=["lnb"])
        pg.memset("dve", eps[:], 64e-5, writes=["eps"])
        f2 = lambda tl: tl[:].rearrange("p h k -> p (h k)")
        rowidx = sb("c_rowidx", [128, NT], I32)
        pg.ld(rowidx[:], dr["rowidx"][:, :], writes=["rowidx"])
        allk = lambda nm: [(nm, k) for k in range(NT)]
        for i in range(C.peer_tiles):
            s = i % 2
            t0 = i * 128
            yk, vk, gk, bk, sk = ("y", s), ("v", s), ("g", s), ("bon", s), ("st", s)
            ix = rowidx[:, i:i + 1]
            igather(pg, f2(y[s]), dr["YS"][:, :], ix, allk("YS") + ["rowidx"], [yk])
            igather(pg, f2(v[s]), dr["RV"][:, :], ix, allk("RV") + ["rowidx"], [vk])
            igather(pg, g[s][:], dr["RG"][:, :], ix, allk("RG") + ["rowidx"], [gk])
            igather(pg, bon[s][:], dr["RBON"][:, :], ix, allk("RBON") + ["rowidx"], [bk])
            S_ = stt_[s]
            bc = lambda ap: ap.unsqueeze(2).to_broadcast([128, 8, 64])
            pg.red("dve", S_[:, 0:8], y[s][:], ALU.add, reads=[yk], writes=[(sk, 0)])
            pg.ts("dve", S_[:, 8:16], S_[:, 0:8], -1.0 / 64, None, ALU.mult, reads=[(sk, 0)], writes=[(sk, 1)])
            pg.tt("dve", y[s][:], y[s][:], bc(S_[:, 8:16]), ALU.add, reads=[yk, (sk, 1)], writes=[yk])
            pg.tt("pool", sq[:], y[s][:], y[s][:], ALU.mult, reads=[yk], writes=["sq"])
            pg.red("dve", S_[:, 16:24], sq[:], ALU.add, reads=["sq"], writes=[(sk, 2)])
            pg.act(S_[:, 24:32], S_[:, 16:24], AF.Sqrt, reads=[(sk, 2), "eps"], writes=[(sk, 3)], scale=1.0 / 64, bias=eps[:, 0:1])
            pg.op("dve", lambda e, o=S_[:, 16:24], a=S_[:, 24:32]: e.reciprocal(o, a), reads=[(sk, 3)], writes=[(sk, 2)])
            pg.tt("dve", y[s][:], y[s][:], bc(S_[:, 16:24]), ALU.mult, reads=[yk, (sk, 2)], writes=[yk])
            pg.tt("dve", f2(y[s]), f2(y[s]), lnw[:], ALU.mult, reads=[yk, "lnw"], writes=[yk])
            pg.tt("pool", f2(y[s]), f2(y[s]), lnb[:], ALU.add, reads=[yk, "lnb"], writes=[yk])
            pg.tt("pool", v[s][:], v[s][:], bc(bon[s][:, 0:8]), ALU.mult, reads=[vk, bk], writes=[vk])
            pg.tt("dve", y[s][:], y[s][:], v[s][:], ALU.add, reads=[yk, vk], writes=[yk])
            pg.tt("dve", f2(y[s]), f2(y[s]), g[s][:], ALU.mult, reads=[yk, gk], writes=[yk])
            pg.ld(dr["YAL"][t0:t0 + 128, :], f2(y[s]), reads=[yk], writes=[("YAL", i)])
        pg.barrier()
        pg.emit()


NEG = -30000.0


def stage3(C):
    nc, pg, dr = C.nc, C.pg, C.dr
    with ExitStack() as st:
        sb, ps = _mk(C, st)
        qT = sb("n_qT", [128, 4, T], BF16)
        KsT = sb("n_KsT", [128, 2, T], BF16)
        KwT = sb("n_KwT", [128, 2, T], BF16)
        Vs = sb("n_Vs", [128, NT, 2, 65], BF16)
        Vw = sb("n_Vw", [128, NT, 2, 65], BF16)
        KcT = sb("n_KcT", [128, 2, 256], BF16)
        Vc = sb("n_Vc", [128, 2, 2, 129], BF16)
        GT = sb("n_GT", [128, NT, 24], F32)
        idf = sb("n_idf", [128, 128], F32)
        idb = sb("n_idb", [128, 128], BF16)
        eps = sb("n_eps", [128, 1], F32)
        pg.ld(idf[:], dr["ident"][:, :], writes=["idf"])
        pg.cp("dve", idb[:], idf[:], reads=["idf"], writes=["idb"])
        pg.memset("dve", eps[:], 1e-6, writes=["eps"])
        pg.memset("pool", Vs[:], 1.0, writes=["Vs"])
        pg.memset("pool", Vw[:], 1.0, writes=["Vw"])
        pg.memset("pool", Vc[:], 0.0, writes=["Vc"])
        with ExitStack() as sa_:
            sb, ps = _mk(C, sa_)
            kcT2 = sb("n_kcT2", [128, T], BF16)
            vcT2 = sb("n_vcT2", [128, T], BF16)
            w1 = [sb(f"n_w1{i}", [128, 32, 256], BF16) for i in range(2)]
            w1s = sb("n_w1s", [128, 16, 256], F32)
            w2s = sb("n_w2s", [128, 2, 2, 64], F32)
            w2 = sb("n_w2", [128, 2, 2, 64], BF16)
            posf = sb("n_posf", [128, 2, 32], F32)
            posb = sb("n_posb", [128, 2, 32], BF16)
            gains = sb("n_gains", [128, 768], F32)
            kcg = sb("n_kcg", [128, 64], F32)
            ovl = sb("n_ovl", [128, 2, 64], F32)
            R = [sb(f"n_R{i}", [128, 1304], F32) for i in range(2)]
            sq = sb("n_sq", [128, 1280], F32)
            tmp = sb("n_tmp", [128, 768], F32)
            stat = sb("n_stat", [128, 64], F32)
            Xb = sb("n_Xb", [128, 10, 128], BF16)
            biasS = sb("n_biasS", [128, 4], F32)
            xb_ = sb("n_xb", [128, 256], F32)
            x2_ = sb("n_x2", [128, 256], F32)
            hT = sb("n_hT", [128, 2, 256], BF16)
            kcn2 = sb("n_kcn2", [128, 128], BF16)
            st2 = sb("n_st2", [128, 8], F32)
            ksq = sb("n_ksq", [128, 64], F32)
            psX_ = [ps(f"n_psX{i}", [128, 1024], BF16) for i in range(3)]
            psX = [t_[:, 0:512].rearrange("p (a b) -> p a b", a=4) for t_ in psX_]
            psh = ps("n_psh", [128, 512], F32)
            psb = ps("n_psb", [128, 512], F32)
            pso = ps("n_pso", [128, 512], F32)
            psk = ps("n_psk", [128, 1024], BF16)

            pg.ld(gains[:], dr["nsa_gains_b"][:, :], writes=["gains"])
            pg.ts("dve", gains[:, 0:512], gains[:, 0:512], 0.125, None, ALU.mult, reads=["gains"], writes=["gains"])
            pg.ld(kcg[:], dr["nsa_kc_g_b"][:, :], writes=["kcg"])
            pg.ld(ovl[:], dr["ovl"][:, :, :], writes=["ovl"])
            pg.ld(posf[:], dr["posT"][:, :, :], writes=["posf"])
            pg.cp("dve", posb[:], posf[:], reads=["posf"], writes=["posb"])
            pg.ld(w2s[:], dr["cmp_w2"][:, :, :, :], writes=["w2s"])
            pg.cp("dve", w2[:], w2s[:], reads=["w2s"], writes=["w2"])
            for x in range(2):
                for hf in range(2):
                    pg.ld(w1s[:], dr["cmp_w1"][x, :, hf * 16:(hf + 1) * 16, :], writes=["w1s"])
                    pg.cp("pool", w1[x][:, hf * 16:(hf + 1) * 16, :], w1s[:], reads=["w1s"], writes=[("w1", x)])
            for i in range(NT):
                s = i % 2
                t0 = i * 128
                Rk = ("R", s)
                pg.ld(R[s][:], dr["P"][t0:t0 + 128, 1792:3096], reads=[("P", i)], writes=[Rk])
                Rs = R[s]
                pg.tt("pool", sq[:], Rs[:, 0:1280], Rs[:, 0:1280], ALU.mult, reads=[Rk], writes=["sq"])
                pg.red("dve", stat[:, 0:20], sq[:].rearrange("p (a k) -> p a k", k=64), ALU.add, reads=["sq"], writes=["stat0"])
                pg.act(stat[:, 20:40], stat[:, 0:20], AF.Sqrt, reads=["stat0", "eps"], writes=["stat1"], scale=1.0 / 64, bias=eps[:, 0:1])
                pg.op("dve", lambda e: e.reciprocal(stat[:, 40:60], stat[:, 20:40]), reads=["stat1"], writes=["stat2"])
                b3 = lambda ap, n: ap.unsqueeze(2).to_broadcast([128, n, 64])
                v3 = lambda ap: ap.rearrange("p (a k) -> p a k", k=64)
                pg.tt("dve", v3(tmp[:, 0:512]), v3(Rs[:, 0:512]), b3(stat[:, 40:48], 8), ALU.mult, reads=[Rk, "stat2"], writes=["tmp"])
                pg.tt("dve", v3(tmp[:, 512:640]), v3(Rs[:, 768:896]), b3(stat[:, 52:54], 2), ALU.mult, reads=[Rk, "stat2"], writes=["tmp"])
                pg.tt("dve", v3(tmp[:, 640:768]), v3(Rs[:, 1024:1152]), b3(stat[:, 56:58], 2), ALU.mult, reads=[Rk, "stat2"], writes=["tmp"])
                pg.tt("pool", tmp[:], tmp[:], gains[:], ALU.mult, reads=["tmp", "gains"], writes=["tmp"])
                pg.cp("pool", Xb[:, 0:4, :].rearrange("p a b -> p (a b)"), tmp[:, 0:512], reads=["tmp"], writes=["Xb"])
                for (blk, c0) in ((4, 512), (6, 640)):
                    src = tmp[:, c0:c0 + 128].rearrange("p (g k) -> p g k", g=2).unsqueeze(2).to_broadcast([128, 2, 2, 64])
                    dst = Xb[:, blk:blk + 2, :].rearrange("p g (d k) -> p g d k", d=2)
                    pg.cp("dve", dst, src, reads=["tmp"], writes=["Xb"])
                pg.cp("pool", Xb[:, 8, :], Rs[:, 512:640], reads=[Rk], writes=["Xb"])
                pg.cp("pool", Xb[:, 9, :], Rs[:, 640:768], reads=[Rk], writes=["Xb"])
                for blk in range(10):
                    pg.tr(psX[blk // 4][:, blk % 4, :], Xb[:, blk, :], idb[:], reads=["Xb", "idb"], writes=[("psX", blk // 4)])
                pg.cp("act", qT[:, :, t0:t0 + 128], psX[0], reads=[("psX", 0)], writes=["qT"])
                pg.cp("dve", KsT[:, :, t0:t0 + 128], psX[1][:, 0:2, :], reads=[("psX", 1)], writes=["KsT"])
                pg.cp("dve", KwT[:, :, t0:t0 + 128], psX[1][:, 2:4, :], reads=[("psX", 1)], writes=["KwT"])
                pg.cp("act", kcT2[:, t0:t0 + 128], psX[2][:, 0, :], reads=[("psX", 2)], writes=["kcT2"])
                pg.cp("act", vcT2[:, t0:t0 + 128], psX[2][:, 1, :], reads=[("psX", 2)], writes=["vcT2"])
                pg.cp("pool", Vs[:, i, :, 0:64], Rs[:, 896:1024].rearrange("p (g k) -> p g k", g=2), reads=[Rk, "Vs"], writes=["Vs"])
                pg.cp("pool", Vw[:, i, :, 0:64], Rs[:, 1152:1280].rearrange("p (g k) -> p g k", g=2), reads=[Rk, "Vw"], writes=["Vw"])
                pg.act(GT[:, i, :], Rs[:, 1280:1304], AF.Sigmoid, reads=[Rk], writes=["GT"])
            if getattr(C, "lvl", 9) < 2:
                pg.barrier()
                pg.emit()
                return
            pg.memset("dve", hT[:], 0.0, writes=["hT"])
            pg.memset("dve", kcn2[:], 0.0, writes=["kcn2"])
            for x in range(2):
                for hf in range(2):
                    for l in range(32):
                        pg.mm(psb[:, x * 2 + hf:x * 2 + hf + 1], w1[x][0:64, l, hf * 128:(hf + 1) * 128], posb[0:64, x, l:l + 1],
                              start=(l == 0), stop=(l == 31), reads=[("w1", x), "posb"], writes=["psb"])
            pg.cp("dve", biasS[:], psb[:, 0:4], reads=["psb"], writes=["biasS"])
            for x in range(2):
                srcT = kcT2 if x == 0 else vcT2
                skey = "kcT2" if x == 0 else "vcT2"
                for g in range(2):
                    for hf in range(2):
                        for l in range(32):
                            rhs = dap(srcT[:], g * 64 * T + l, [[T, 64], [16, 255]])
                            pg.mm(psh[:, 0:255], w1[x][g * 64:(g + 1) * 64, l, hf * 128:(hf + 1) * 128], rhs,
                                  start=(l == 0), stop=(l == 31), reads=[("w1", x), skey], writes=["psh"])
                        c = slice(0, 255)
                        pg.act(xb_[:, c], psh[:, c], AF.Identity, reads=["psh", "biasS"], writes=["xb"], bias=biasS[:, x * 2 + hf:x * 2 + hf + 1])
                        pg.tt("pool", x2_[:, c], xb_[:, c], xb_[:, c], ALU.mult, reads=["xb"], writes=["x2"])
                        pg.ts("dve", x2_[:, c], x2_[:, c], 0.044715, 1.0, ALU.mult, ALU.add, reads=["x2"], writes=["x2"])
                        pg.tt("dve", x2_[:, c], x2_[:, c], xb_[:, c], ALU.mult, reads=["x2", "xb"], writes=["x2"])
                        pg.act(x2_[:, c], x2_[:, c], AF.Tanh, reads=["x2"], writes=["x2"], scale=0.7978845608028654)
                        pg.stt("dve", x2_[:, c], x2_[:, c], 1.0, xb_[:, c], ALU.add, ALU.mult, reads=["x2", "xb"], writes=["x2"])
                        pg.ts("dve", hT[:, hf, c], x2_[:, c], 0.5, None, ALU.mult, reads=["x2"], writes=["hT"])
                    for m in range(2):
                        rows = 128 if m == 0 else 127
                        for hf in range(2):
                            pg.mm(pso[0:rows, 0:64], hT[:, hf, m * 128:m * 128 + rows], w2[:, x, hf, :], start=(hf == 0), stop=(hf == 1),
                                  reads=["hT", "w2"], writes=["pso"])
                        if x == 0:
                            pg.cp("act", ksq[0:rows, :], pso[0:rows, 0:64], reads=["pso"], writes=["ksq"])
                            pg.tt("pool", x2_[0:rows, 0:64], ksq[0:rows, :], ksq[0:rows, :], ALU.mult, reads=["ksq", "x2"], writes=["x2"])
                            pg.red("dve", st2[0:rows, 0:1], x2_[0:rows, 0:64], ALU.add, reads=["x2"], writes=["st2a"])
                            pg.act(st2[0:rows, 1:2], st2[0:rows, 0:1], AF.Sqrt, reads=["st2a", "eps"], writes=["st2b"], scale=1.0 / 64, bias=eps[0:rows, 0:1])
                            pg.op("dve", lambda e, rows=rows: e.reciprocal(st2[0:rows, 2:3], st2[0:rows, 1:2]), reads=["st2b"], writes=["st2c"])
                            pg.stt("dve", ksq[0:rows, :], ksq[0:rows, :], st2[0:rows, 2:3], kcg[0:rows, :], ALU.mult, ALU.mult,
                                   reads=["ksq", "st2c", "kcg"], writes=["ksq"])
                            src = ksq[0:rows, :].unsqueeze(1).to_broadcast([rows, 2, 64])
                            pg.cp("dve", kcn2[0:rows, :].rearrange("p (d k) -> p d k", d=2), src, reads=["ksq"], writes=["kcn2"])
                            pg.tr(psk[:, 0:128], kcn2[:, :], idb[:], reads=["kcn2", "idb"], writes=["psk"])
                            pg.cp("act", KcT[:, g, m * 128:(m + 1) * 128], psk[:, 0:128], reads=["psk"], writes=["KcT"])
                        else:
                            pg.cp("act", Vc[0:rows, m, g, 0:64], pso[0:rows, 0:64], reads=["pso", "Vc"], writes=["Vc"])
            for m in range(2):
                for g in range(2):
                    pg.memset("dve", Vc[:, m, g, 64:65], 1.0, writes=["Vc"])
                    pg.cp("dve", Vc[:, m, g, 65:129], ovl[:, m, :], reads=["ovl", "Vc"], writes=["Vc"])
            pg.barrier()
            pg.emit()
        if getattr(C, "lvl", 9) < 3:
            return
        stage3_attn(C, st, qT, KsT, KwT, Vs, Vw, KcT, Vc, GT, idf, idb)


def stage3_attn(C, st, qT, KsT, KwT, Vs, Vw, KcT, Vc, GT, idf, idb):
    nc, pg, dr = C.nc, C.pg, C.dr
    with ExitStack() as sb_:
        sb, ps = _mk(C, sb_)
        cmpb = sb("n_cmpb", [128, 2, T], BF16)
        Esel = sb("n_Esel", [128, 32, 128], BF16)
        causb = sb("n_causb", [128, 4, 512], BF16)
        winb = sb("n_winb", [128, 8, 512], BF16)
        selbT = sb("n_selbT", [128, 2, T], BF16)
        eT = [sb(f"n_eT{i}", [128, 512], BF16) for i in range(4)]
        eT2 = [sb(f"n_eT2{i}", [128, 512], BF16) for i in range(4)]
        Mt = [sb(f"n_Mt{i}", [128, 512], BF16) for i in range(2)]
        rm = [0]
        dq = []
        ocmp = sb("n_ocmp", [128, 4, 8, 64], F32)
        osel = sb("n_osel", [128, 4, 8, 64], F32)
        owin = sb("n_owin", [128, 4, 8, 64], F32)
        den = sb("n_den", [128, 16], F32)
        impw = sb("n_impw", [128, 2, 4, 64], F32)
        score = sb("n_score", [128, 2, 64], F32)
        VM = [sb(f"n_VM{i}", [128, 2, 64], F32) for i in range(2)]
        work = sb("n_work", [128, 2, 64], F32)
        m8 = sb("n_m8", [128, 2, 16], F32)
        thr = sb("n_thr", [128, 2], F32)
        msel = sb("n_msel", [128, 2, 64], F32)
        selb = sb("n_selb", [128, 2, 2, 64], BF16)
        osT = [sb(f"n_osT{i}", [65, 512], F32) for i in range(2)]
        dn2 = sb("n_dn2", [128, 8], F32)
        yb = sb("n_yb", [128, 8, 64], F32)
        yb2 = sb("n_yb2", [128, 8, 64], F32)
        psS = [ps(f"n_psS{i}", [128, 512], F32) for i in range(3)]
        psA = [ps(f"n_psA{i}", [128, 512], F32) for i in range(2)]
        psB = [ps(f"n_psB{i}", [128, 512], F32) for i in range(2)]
        psZ_ = ps("n_psZ", [128, 1024], BF16)
        psZ = psZ_[:, 0:256].rearrange("p (g q) -> p g q", g=2)

        pg.ld(cmpb[:], dr["c_cmpb"][:, :, :], writes=["cmpb"])
        pg.ld(Esel[:], dr["c_esel"][:, :, :], writes=["Esel"])
        pg.ld(causb[:], dr["c_causb"][:, :, :], writes=["causb"])
        pg.ld(winb[:], dr["c_winb"][:, :, :], writes=["winb"])
        winb4 = sb("n_winb4", [128, 8, 512], BF16)
        pg.ld(winb4[:], dr["c_winb4"][:, :, :], writes=["winb4"])
        rs = [0]
        re = [0]

        def nxt(lst, n):
            v = lst[0]
            lst[0] = (v + 1) % n
            return v

        def qk(h):
            return (h % 2) * 64, h // 2, h // 4

        for Q in range(4, 8):
            tq0 = Q * 512
            for ii in range(4):
                i = Q * 4 + ii
                t0 = i * 128
                s = i % 2
                pg.ld(VM[s][:], dr["c_vmfb"][i, :, :, :], writes=[("VM", s)])
                nm = 2 if i >= 16 else 1
                for h in range(8):
                    base, hp, g = qk(h)
                    h4 = h % 4
                    for m in range(nm):
                        r = nxt(rs, 3)
                        pS = psS[r]
                        pg.mm(pS[:, 0:128], KcT[base:base + 64, g, m * 128:(m + 1) * 128], qT[base:base + 64, hp, t0:t0 + 128],
                              start=True, stop=False, reads=["KcT", "qT"], writes=[("psS", r)])
                        pg.mm(pS[:, 0:128], idb[:, :], cmpb[:, m, t0:t0 + 128], start=False, stop=True,
                              reads=["idb", "cmpb"], writes=[("psS", r)])
                        k = nxt(re, 4)
                        pg.act(eT[k][:, 0:128], pS[:, 0:128], AF.Exp, reads=[("psS", r)], writes=[("eT", k)])
                        def pv(g=g, h4=h4, k=k, m=m, nm=nm):
                            pg.mm(psA[g][:, h4 * 65:h4 * 65 + 65], eT[k][:, 0:128], Vc[:, m, g, 0:65], start=(m == 0), stop=(m == nm - 1),
                                  reads=[("eT", k), "Vc"], writes=[("psA", g)])
                            pg.mm(psB[g][:, h4 * 64:h4 * 64 + 64], eT[k][:, 0:128], Vc[:, m, g, 65:129], start=(m == 0), stop=(m == nm - 1),
                                  reads=[("eT", k), "Vc"], writes=[("psB", g)])
                        dq.append(pv)
                        if len(dq) > 2:
                            dq.pop(0)()
                while dq:
                    dq.pop(0)()
                for g in range(2):
                    A3 = psA[g][:, 0:260].rearrange("p (h c) -> p h c", c=65)
                    B3 = psB[g][:, 0:256].rearrange("p (h c) -> p h c", c=64)
                    dsl = den[:, g * 4:(g + 1) * 4]
                    rsl = den[:, 8 + g * 4:8 + (g + 1) * 4]
                    pg.ts("dve", dsl, A3[:, :, 64], 1e-30, None, ALU.max, reads=[("psA", g)], writes=[("den", g)])
                    pg.op("dve", lambda e, o=rsl, a=dsl: e.reciprocal(o, a), reads=[("den", g)], writes=[("rden", g)])
                    rb = rsl.unsqueeze(2).to_broadcast([128, 4, 64])
                    pg.tt("dve", ocmp[:, ii, g * 4:(g + 1) * 4, :], A3[:, :, 0:64], rb, ALU.mult, reads=[("psA", g), ("rden", g)], writes=["ocmp"])
                    pg.tt("dve", impw[:, g, :, :], B3, rb, ALU.mult, reads=[("psB", g), ("rden", g)], writes=[("impw", g)])
                    pg.red("dve", score[:, g, :], impw[:, g, :, :].rearrange("p h j -> p j h"), ALU.add, reads=[("impw", g)], writes=[("score", g)])
                    vm = dr
                    pg.tt("dve", score[:, g, :], score[:, g, :], VM[s][:, 0, :], ALU.mult, reads=[("score", g), ("VM", s)], writes=[("score", g)])
                    pg.tt("dve", score[:, g, :], score[:, g, :], VM[s][:, 1, :], ALU.add, reads=[("score", g), ("VM", s)], writes=[("score", g)])
                    pg.op("dve", lambda e, g=g: e.max(m8[:, g, 0:8], score[:, g, :]), reads=[("score", g)], writes=[("m8a", g)])
                    pg.op("dve", lambda e, g=g: e.match_replace(work[:, g, :], m8[:, g, 0:8], score[:, g, :], -1e9),
                          reads=[("score", g), ("m8a", g)], writes=[("work", g)])
                    pg.op("dve", lambda e, g=g: e.max(m8[:, g, 8:16], work[:, g, :]), reads=[("work", g)], writes=[("m8b", g)])
                    pg.ts("dve", thr[:, g:g + 1], m8[:, g, 15:16], -0.5, None, ALU.max, reads=[("m8b", g)], writes=[("thr", g)])
                    pg.ts("dve", msel[:, g, :], score[:, g, :], thr[:, g:g + 1], None, ALU.is_ge, reads=[("score", g), ("thr", g)], writes=[("msel", g)])
                    pg.cp("dve", selb[:, g, :, :], msel[:, g, :].unsqueeze(1).to_broadcast([128, 2, 64]), reads=[("msel", g)], writes=[("selb", g)])
                    pg.tr(psZ[:, g, :], selb[:, g, :, :].rearrange("p d j -> p (d j)"), idb[:], reads=[("selb", g), "idb"], writes=["psZ"])
                pg.cp("act", selbT[:, :, t0:t0 + 128], psZ, reads=["psZ"], writes=["selbT"])
            for br in range(2):
                if getattr(C, "lvl", 9) < 4 + br:
                    continue
                dest = osel if br == 0 else owin
                dkey = "osel" if br == 0 else "owin"
                KT = KsT if br == 0 else KwT
                Vv = Vs if br == 0 else Vw
                kts = list(range(0, 4 * Q + 4)) if br == 0 else list(range(max(0, 4 * Q - 4), 4 * Q + 4))
                for g in range(2):
                    O = [psA[0], psA[1], psB[0], psB[1]]
                    okeys = [("psA", 0), ("psA", 1), ("psB", 0), ("psB", 1)]
                    for n_, kt in enumerate(kts):
                        if br == 0:
                            r = nxt(rs, 3)
                            pg.mm(psS[r][:, :], Esel[0:64, kt, :], selbT[0:64, g, tq0:tq0 + 512], reads=["Esel", "selbT"], writes=[("psS", r)])
                            mi = nxt(rm, 2)
                            if kt >= 4 * Q:
                                pg.tt("dve", Mt[mi][:], psS[r][:, :], causb[:, kt - 4 * Q, :], ALU.mult, reads=[("psS", r), "causb"], writes=[("Mt", mi)])
                            else:
                                pg.cp("dve", Mt[mi][:], psS[r][:, :], reads=[("psS", r)], writes=[("Mt", mi)])
                            mask, mkeys = Mt[mi][:], [("Mt", mi)]
                        else:
                            wsrc = winb4 if Q == 4 else winb
                            mask, mkeys = wsrc[:, kt - 4 * Q + 4, :], ["winb", "winb4"]
                        for h4 in range(4):
                            h = g * 4 + h4
                            base, hp, _g = qk(h)
                            r2 = nxt(rs, 3)
                            pg.mm(psS[r2][:, :], KT[base:base + 64, g, kt * 128:(kt + 1) * 128], qT[base:base + 64, hp, tq0:tq0 + 512],
                                  reads=["qT"], writes=[("psS", r2)])
                            k = nxt(re, 4)
                            pg.act(eT[k][:, :], psS[r2][:, :], AF.Exp, reads=[("psS", r2)], writes=[("eT", k)])
                            pg.tt("dve", eT2[k][:, :], eT[k][:, :], mask, ALU.mult, reads=[("eT", k)] + mkeys, writes=[("eT2", k)])
                            dq.append(lambda h4=h4, kt=kt, k=k, n_=n_, O=O, okeys=okeys, Vv=Vv, g=g, kts=kts: pg.mm(
                                O[h4][0:65, :], Vv[:, kt, g, :], eT2[k][:, :], start=(n_ == 0), stop=(n_ == len(kts) - 1),
                                reads=[("eT2", k)], writes=[okeys[h4]]))
                            if len(dq) > 2:
                                dq.pop(0)()
                    while dq:
                        dq.pop(0)()
                    for h4 in range(4):
                        h = g * 4 + h4
                        o = h4 % 2
                        pg.cp("act", osT[o][:, :], O[h4][0:65, :], reads=[okeys[h4]], writes=[("osT", o)])
                        r3 = nxt(rs, 3)
                        Tp = psS[r3]
                        for qq in range(4):
                            pg.tr(Tp[:, qq * 65:(qq + 1) * 65], osT[o][0:65, qq * 128:(qq + 1) * 128], idf[0:65, 0:65],
                                  reads=[("osT", o), "idf"], writes=[("psS", r3)])
                        T3 = Tp[:, 0:260].rearrange("p (q c) -> p q c", c=65)
                        pg.ts("dve", dn2[:, 0:4], T3[:, :, 64], 1e-30, None, ALU.max, reads=[("psS", r3)], writes=["dn2a"])
                        pg.op("dve", lambda e: e.reciprocal(dn2[:, 4:8], dn2[:, 0:4]), reads=["dn2a"], writes=["dn2b"])
                        pg.tt("dve", dest[:, :, h, :], T3[:, :, 0:64], dn2[:, 4:8].unsqueeze(2).to_broadcast([128, 4, 64]), ALU.mult,
                              reads=[("psS", r3), "dn2b"], writes=[dkey])
            for ii in range(4):
                i = Q * 4 + ii
                t0 = i * 128
                G3 = GT[:, i, :].rearrange("p (h c) -> p h c", c=3)
                gb = lambda c: G3[:, :, c].unsqueeze(2).to_broadcast([128, 8, 64])
                pg.tt("dve", yb[:], ocmp[:, ii, :, :], gb(0), ALU.mult, reads=["ocmp", "GT"], writes=["yb"])
                pg.tt("pool", yb2[:], osel[:, ii, :, :], gb(1), ALU.mult, reads=["osel", "GT"], writes=["yb2"])
                pg.tt("dve", yb[:], yb[:], yb2[:], ALU.add, reads=["yb", "yb2"], writes=["yb"])
                pg.tt("pool", yb2[:], owin[:, ii, :, :], gb(2), ALU.mult, reads=["owin", "GT", "yb2"], writes=["yb2"])
                pg.tt("dve", yb[:], yb[:], yb2[:], ALU.add, reads=["yb", "yb2"], writes=["yb"])
                pg.ld(dr["YB"][t0:t0 + 128, :], yb[:].rearrange("p h k -> p (h k)"), reads=["yb"], writes=[("YB", i)])
        pg.barrier()
        pg.emit()


_NSA_CONSTS = {}


def nsa_consts(hh=1):
    if hh in _NSA_CONSTS:
        return _NSA_CONSTS[hh]
    import ml_dtypes
    bf = ml_dtypes.bfloat16
    c = {}
    n = np.arange(256)
    t = np.arange(T)
    nlo = 128 if hh == 0 else 0
    cm = np.where((16 * n[:, None] + 31 <= t[None, :]) & (n[:, None] < 255) & (n[:, None] >= nlo), 0.0, NEG).astype(np.float32)
    c["c_cmpb"] = np.ascontiguousarray(cm.reshape(2, 128, T).transpose(1, 0, 2)).astype(bf)
    es = np.zeros((64, 32, 128), np.float32)
    for kt in range(32):
        for key in range(128):
            es[2 * kt + key // 64, kt, key] = 1.0
    c["c_esel"] = np.concatenate([es, es], 0).astype(bf)
    key = np.arange(128)
    q = np.arange(512)
    cb = np.zeros((128, 4, 512), np.float32)
    for d in range(4):
        cb[:, d, :] = np.where((d * 128 + key[:, None]) <= q[None, :], 1.0, 0.0)
    c["c_causb"] = cb.astype(bf)
    wb = np.zeros((128, 8, 512), np.float32)
    for r in range(8):
        ka = (r - 4) * 128 + key[:, None]
        wb[:, r, :] = np.where((ka <= q[None, :]) & (ka > q[None, :] - 512), 1.0, 0.0)
    c["c_winb"] = wb.astype(bf)
    wb4 = wb.copy()
    if hh == 0:
        wb4[:, 0:4, :] = 0.0
    c["c_winb4"] = wb4.astype(bf)
    cs = np.arange(256) * 16
    ss = np.arange(64) * 64
    ov = np.clip(np.minimum(cs[:, None] + 32, ss[None, :] + 64) - np.maximum(cs[:, None], ss[None, :]), 0, None) / 32.0
    ov[255, :] = 0.0
    c["ovl"] = np.ascontiguousarray(ov.reshape(2, 128, 64).transpose(1, 0, 2)).astype(np.float32)
    cur = t // 64
    j = np.arange(64)
    jlo = 32 if hh == 0 else 0
    valid = (j[None, :] <= cur[:, None]) & (j[None, :] >= jlo)
    forced = (j[None, :] == jlo) | (j[None, :] == cur[:, None]) | (j[None, :] == cur[:, None] - 1)
    vm = valid.astype(np.float32)
    fb = np.where(valid, 1000.0 * forced, -1.0).astype(np.float32)
    c["c_vmfb"] = np.ascontiguousarray(np.stack([vm, fb], 1).reshape(NT, 128, 2, 64))
    _NSA_CONSTS[hh] = c
    return c


def stage4(C):
    nc, pg, dr = C.nc, C.pg, C.dr
    with ExitStack() as st:
        sb, ps = _mk(C, st)
        wa = sb("m_wa", [128, 4, D], BF16)
        wb = sb("m_wb", [128, 4, D], BF16)
        wo = sb("m_wo", [128, 8, D], BF16)
        stg = sb("m_stg", [128, D], F32)
        idf = sb("m_idf", [128, 128], F32)
        idb = sb("m_idb", [128, 128], BF16)
        yab = [sb(f"m_yab{i}", [128, 1024], F32) for i in range(2)]
        yabb = sb("m_yabb", [128, 1024], BF16)
        yT = sb("m_yT", [128, 8, 128], BF16)
        gts = [sb(f"m_g{i}", [128, 2048], F32) for i in range(2)]
        xt = [sb(f"m_x{i}", [128, D], F32) for i in range(2)]
        mix = sb("m_mix", [128, D], F32)
        mix2 = sb("m_mix2", [128, D], F32)
        mixb = sb("m_mixb", [128, D], BF16)
        mT = sb("m_mT", [128, 8, 128], BF16)
        x1 = [sb(f"m_x1{i}", [128, D], F32) for i in range(2)]
        psT = ps("m_psT", [128, 1024], BF16)
        psm = [ps(f"m_psm{i}", [128, 512], F32) for i in range(4)]
        psT2 = ps("m_psT2", [128, 1024], BF16)
        pso = [ps(f"m_pso{i}", [128, 512], F32) for i in range(2)]

        pg.ld(idf[:], dr["ident"][:, :], writes=["idf"])
        pg.cp("dve", idb[:], idf[:], reads=["idf"], writes=["idb"])
        n = 0
        for (wt, nm, kcs) in ((wa, "w_branch_a", 4), (wb, "w_branch_b", 4), (wo, "w_out", 8)):
            for kc in range(kcs):
                pg.ld(stg[:], dr[nm][kc * 128:(kc + 1) * 128, :], writes=["stg"])
                pg.cp(("act", "dve", "pool")[n % 3], wt[:, kc, :], stg[:], reads=["stg"], writes=[nm])
                n += 1
        rowidx = sb("m_rowidx", [128, NT], I32)
        pg.ld(rowidx[:], dr["rowidx"][:, :], writes=["rowidx"])
        allk = lambda nm: [(nm, k) for k in range(NT)]
        for i in range(C.peer_tiles):
            s = i % 2
            t0 = i * 128
            ix = rowidx[:, i:i + 1]
            pg.ld(yab[s][:, 0:512], dr["YAL"][t0:t0 + 128, :], reads=[("YAL", i)], writes=[("yab", s)])
            igather(pg, yab[s][:, 512:1024], dr["YB"][:, :], ix, allk("YB") + ["rowidx"], [("yab2", s)])
            igather(pg, gts[s][:], dr["PG"][:, :], ix, allk("PG") + ["rowidx"], [("gts", s)])
            igather(pg, xt[s][:], dr["x"][:, :], ix, ["rowidx"], [("xt", s)])
            pg.cp("pool", yabb[:], yab[s][:], reads=[("yab", s), ("yab2", s)], writes=["yabb"])
            for j in range(8):
                pg.tr(psT[:, j * 128:(j + 1) * 128], yabb[:, j * 128:(j + 1) * 128], idb[:], reads=["yabb", "idb"], writes=["psT"])
            pg.cp("act", yT[:].rearrange("p a b -> p (a b)"), psT[:], reads=["psT"], writes=["yT"])
            for br in range(2):
                wt = wa if br == 0 else wb
                for nchunk in range(2):
                    pb = psm[br * 2 + nchunk]
                    for kc in range(4):
                        pg.mm(pb[:], yT[:, br * 4 + kc, :], wt[:, kc, nchunk * 512:(nchunk + 1) * 512], start=(kc == 0), stop=(kc == 3),
                              reads=["yT", "w_branch_a", "w_branch_b"], writes=[("psm", br * 2 + nchunk)])
            pg.act(gts[s][:], gts[s][:], AF.Sigmoid, reads=[("gts", s)], writes=[("gts", s)])
            for nchunk in range(2):
                c = slice(nchunk * 512, (nchunk + 1) * 512)
                pg.tt("dve", mix[:, c], psm[nchunk][:], gts[s][:, nchunk * 512:(nchunk + 1) * 512], ALU.mult,
                      reads=[("psm", nchunk), ("gts", s)], writes=[("mix", nchunk)])
                pg.tt("dve", mix2[:, c], psm[2 + nchunk][:], gts[s][:, 1024 + nchunk * 512:1024 + (nchunk + 1) * 512], ALU.mult,
                      reads=[("psm", 2 + nchunk), ("gts", s)], writes=[("mix2", nchunk)])
                pg.tt("pool", mixb[:, c], mix[:, c], mix2[:, c], ALU.add, reads=[("mix", nchunk), ("mix2", nchunk)], writes=[("mixb", nchunk)])
            for j in range(8):
                pg.tr(psT2[:, j * 128:(j + 1) * 128], mixb[:, j * 128:(j + 1) * 128], idb[:], reads=[("mixb", 0), ("mixb", 1), "idb"], writes=["psT2"])
            pg.cp("act", mT[:].rearrange("p a b -> p (a b)"), psT2[:], reads=["psT2"], writes=["mT"])
            for nchunk in range(2):
                for kc in range(8):
                    pg.mm(pso[nchunk][:], mT[:, kc, :], wo[:, kc, nchunk * 512:(nchunk + 1) * 512], start=(kc == 0), stop=(kc == 7),
                          reads=["mT", "w_out"], writes=[("pso", nchunk)])
                pg.tt("dve", x1[s][:, nchunk * 512:(nchunk + 1) * 512], pso[nchunk][:], xt[s][:, nchunk * 512:(nchunk + 1) * 512], ALU.add,
                      reads=[("pso", nchunk), ("xt", s)], writes=[("x1", s, nchunk)])
            pg.ld(dr["X1L"][t0:t0 + 128, :], x1[s][:], reads=[("x1", s, 0), ("x1", s, 1)], writes=[("X1L", i)])
        pg.barrier()
        pg.emit()


def table_conv_gen(C, sb):
    pg, dr = C.pg, C.dr
    NBUF = 4
    src = [sb(f"z_src{i}", [128, D], F32) for i in range(NBUF)]
    dst = [sb(f"z_dst{i}", [128, D], BF16) for i in range(NBUF)]
    n = 0
    for (tab, co) in (("peer_u", 0), ("peer_v", D)):
        for a in range(16384 // 128):
            b_ = n % NBUF
            pg.ld(src[b_][:], dr[tab][a * 128:(a + 1) * 128, :], writes=[("zsrc", b_)], q="sp")
            pg.cp("pool", dst[b_][:], src[b_][:], reads=[("zsrc", b_)], writes=[("zdst", b_)])
            pg.ld(dr["UV"][a * 128:(a + 1) * 128, co:co + D], dst[b_][:], reads=[("zdst", b_)], writes=[("UV", co, a)], q="act")
            n += 1
            yield


def stage5(C):
    nc, pg, dr = C.nc, C.pg, C.dr
    NB = 12
    with ExitStack() as st:
        sb, ps = _mk(C, st)
        wq = sb("p_wq", [128, 8, 2048], F32)
        kT = sb("p_kT", [128, 2, 128], F32)
        kraw = sb("p_kraw", [128, 2, 128], F32)
        g2 = sb("p_g2", [128, D], F32)
        idf = sb("p_idf", [128, 128], F32)
        io16 = sb("p_io16", [128, 16], F32)
        eps = sb("p_eps", [128, 1], F32)
        x1 = [sb(f"p_x1{i}", [128, D], F32) for i in range(3)]
        h2 = [sb(f"p_h2{i}", [128, D], F32) for i in range(2)]
        junk = sb("p_junk", [128, D], BF16)

        ss = sb("p_ss", [128, 4], F32)
        h2T = sb("p_h2T", [128, 8, 128], F32)
        qT = sb("p_qT", [128, 16, 128], F32)
        sc = sb("p_sc", [128, 16, 128], F32)
        work = sb("p_work", [128, 256], F32)
        tv = sb("p_tv", [128, 16, 16], F32)
        tiu = sb("p_tiu", [128, 16, 16], U32)
        ti = sb("p_ti", [128, 16, 16], F32)
        cs = sb("p_cs", [128, 8, 256], F32)
        bs = sb("p_bs", [128, 8, 16], F32)
        posu = sb("p_posu", [128, 8, 16], U32)
        pa_u = sb("p_pau", [128, 8, 16], U32)
        pb_u = sb("p_pbu", [128, 8, 16], U32)
        pa = sb("p_pa", [128, 8, 16], F32)
        pb = sb("p_pb", [128, 8, 16], F32)
        oh = sb("p_oh", [128, 8, 16, 16], F32)
        ia = sb("p_ia", [128, 8, 16], F32)
        ib = sb("p_ib", [128, 8, 16], F32)
        eidf = sb("p_eidf", [128, 128], F32)
        eidi = [sb(f"p_eidi{i}", [128, 128], I32) for i in range(3)]
        gate = [sb(f"p_gate{i}", [128, 128], F32) for i in range(2)]
        zz = sb("p_zz", [128, 16], F32)
        actv = [sb(f"p_act{i}", [128, 128], F32) for i in range(2)]
        ga = [sb(f"p_ga{i}", [128, 128], F32) for i in range(2)]
        uv = [sb(f"p_uv{i}", [128, 2 * D], BF16) for i in range(NB)]
        h2b = [sb(f"p_h2b{i}", [128, D], BF16) for i in range(2)]
        idb = sb("p_idb", [128, 128], BF16)
        junk2 = sb("p_junk2", [128, D], F32)
        dg = [sb(f"p_dg{i}", [128, 128], BF16) for i in range(4)]
        yo = [sb(f"p_yo{i}", [128, D], F32) for i in range(1)]
        psT = ps("p_psT", [128, 8, 128], F32)
        psQ = [ps(f"p_psQ{i}", [128, 512], F32) for i in range(2)]
        psY = [ps(f"p_psY{i}", [128, 512], F32) for i in range(2)]

        pg.ld(idf[:], dr["ident"][:, :], writes=["idf"])
        pg.ld(g2[:], dr["norm2_g_b"][:, :], writes=["g2"])
        pg.cp("dve", idb[:], idf[:], reads=["idf"], writes=["idb"])
        pg.ld(io16[:], dr["iota16"][:, :], writes=["io16"])
        rowidx = sb("p_rowidx", [128, NT], I32)
        pg.ld(rowidx[:], dr["rowidx"][:, :], writes=["rowidx"])
        pg.memset("dve", eps[:], 1e-6, writes=["eps"])
        for kc in range(8):
            pg.ld(wq[:, kc, :], dr["peer_wq"][kc * 128:(kc + 1) * 128, :], writes=["wq"])
        pg.ld(kraw[:, 0, :], dr["peer_k1"][:, :], writes=["kraw"])
        pg.ld(kraw[:, 1, :], dr["peer_k2"][:, :], writes=["kraw"])
        for hf in range(2):
            pg.tr(psQ[0][:, hf * 128:(hf + 1) * 128], kraw[:, hf, :], idf[:], reads=["kraw", "idf"], writes=[("psQ", 0)])
        pg.cp("dve", kT[:].rearrange("p a b -> p (a b)"), psQ[0][:, 0:256], reads=[("psQ", 0)], writes=["kT"])
        ntiles = getattr(C, "peer_tiles", NT)

        def front(i):
            s = i % 2
            t0 = i * 128
            pg.ld(x1[i % 3][:, :], dr["X1L"][t0:t0 + 128, :], reads=[("X1L", i)], writes=[("x1", i % 3)])
            yield
            pg.tt("pool", junk2[:], x1[i % 3][:], x1[i % 3][:], ALU.mult, reads=[("x1", i % 3), "junk2"], writes=["junk2"])
            yield
            pg.red("dve", ss[:, 0:1], junk2[:], ALU.add, reads=["junk2"], writes=["ss0"])
            yield
            pg.act(ss[:, 1:2], ss[:, 0:1], AF.Sqrt, reads=["ss0", "eps"], writes=["ss1"], scale=1.0 / D, bias=eps[:, 0:1])
            yield
            pg.op("dve", lambda e: e.reciprocal(ss[:, 2:3], ss[:, 1:2]), reads=["ss1"], writes=["ss2"])
            yield
            pg.stt("dve", h2[s][:], x1[i % 3][:], ss[:, 2:3], g2[:], ALU.mult, ALU.mult, reads=[("x1", i % 3), "ss2", "g2"], writes=[("h2", s)])
            yield
            pg.cp("pool", h2b[s][:], h2[s][:], reads=[("h2", s)], writes=[("h2b", s)])
            yield
            for j in range(8):
                pg.tr(psT[:, j, :], h2[s][:, j * 128:(j + 1) * 128], idf[:], reads=[("h2", s), "idf"], writes=["psT"])
                yield
            pg.cp("act", h2T[:], psT[:], reads=["psT"], writes=["h2T"])
            yield
            for cg in range(4):
                bk = psQ[cg % 2]
                for cc in range(4):
                    c = cg * 4 + cc
                    for kc in range(8):
                        pg.mm(bk[:, cc * 128:(cc + 1) * 128], wq[:, kc, c * 128:(c + 1) * 128], h2T[:, kc, :], start=(kc == 0), stop=(kc == 7),
                              reads=["wq", "h2T"], writes=[("psQ", cg % 2)])
                        yield
                pg.cp("act" if cg % 2 == 0 else "dve", qT[:, cg * 4:(cg + 1) * 4, :].rearrange("p a b -> p (a b)"), bk[:],
                      reads=[("psQ", cg % 2)], writes=[("qT", cg)])
                yield
            for cg in range(4):
                bk = psQ[cg % 2]
                for cc in range(4):
                    c = cg * 4 + cc
                    pg.mm(bk[:, cc * 128:(cc + 1) * 128], qT[:, c, :], kT[:, c % 2, :], reads=[("qT", cg), "kT"], writes=[("psQ", cg % 2)])
                    yield
                pg.cp("act" if cg % 2 == 0 else "dve", sc[:, cg * 4:(cg + 1) * 4, :].rearrange("p a b -> p (a b)"), bk[:],
                      reads=[("psQ", cg % 2)], writes=[("sc", cg)])
                yield
            for c in range(16):
                k_ = ("sc", c // 4)
                pg.op("dve", lambda e, c=c: e.max(tv[:, c, 0:8], sc[:, c, :]), reads=[k_], writes=[("tv", c)])
                yield
                pg.op("dve", lambda e, c=c: e.max_index(tiu[:, c, 0:8], tv[:, c, 0:8], sc[:, c, :]), reads=[k_, ("tv", c)], writes=[("tiu", c)])
                yield
                pg.op("dve", lambda e, c=c: e.match_replace(work[:, 0:128], tv[:, c, 0:8], sc[:, c, :], -1e30), reads=[k_, ("tv", c), "work"], writes=["work"])
                yield
                pg.op("dve", lambda e, c=c: e.max(tv[:, c, 8:16], work[:, 0:128]), reads=["work"], writes=[("tv2", c)])
                yield
                pg.op("dve", lambda e, c=c: e.max_index(tiu[:, c, 8:16], tv[:, c, 8:16], sc[:, c, :]), reads=[k_, ("tv2", c)], writes=[("tiu2", c)])
                yield
            allt = [("tv", c) for c in range(16)] + [("tv2", c) for c in range(16)]
            alli = [("tiu", c) for c in range(16)] + [("tiu2", c) for c in range(16)]
            pg.cp("dve", ti[:], tiu[:], reads=alli, writes=["ti"])
            yield
            tv4 = tv[:].rearrange("p (h f) a -> p h f a", f=2)
            ti4 = ti[:].rearrange("p (h f) a -> p h f a", f=2)
            cs4 = cs[:].rearrange("p h (a b) -> p h a b", a=16)
            A_ = lambda t4: t4[:, :, 0, :].unsqueeze(3).to_broadcast([128, 8, 16, 16])
            B_ = lambda t4: t4[:, :, 1, :].unsqueeze(2).to_broadcast([128, 8, 16, 16])
            pg.tt("dve", cs4, A_(tv4), B_(tv4), ALU.add, reads=allt, writes=["cs"])
            yield
            for h in range(8):
                pg.op("dve", lambda e, h=h: e.max(bs[:, h, 0:8], cs[:, h, :]), reads=["cs"], writes=[("bs", h)])
                yield
                pg.op("dve", lambda e, h=h: e.max_index(posu[:, h, 0:8], bs[:, h, 0:8], cs[:, h, :]), reads=["cs", ("bs", h)], writes=[("posu", h)])
                yield
                pg.op("dve", lambda e, h=h: e.match_replace(work[:, :], bs[:, h, 0:8], cs[:, h, :], -1e30), reads=["cs", ("bs", h), "work"], writes=["work"])
                yield
                pg.op("dve", lambda e, h=h: e.max(bs[:, h, 8:16], work[:, :]), reads=["work"], writes=[("bs2", h)])
                yield
                pg.op("dve", lambda e, h=h: e.max_index(posu[:, h, 8:16], bs[:, h, 8:16], cs[:, h, :]), reads=["cs", ("bs2", h)], writes=[("posu2", h)])
                yield
            allb = [("bs", h) for h in range(8)] + [("bs2", h) for h in range(8)]
            allp = [("posu", h) for h in range(8)] + [("posu2", h) for h in range(8)]
            G = gate[s][:].rearrange("p (h j) -> p h j", h=8)
            pg.tt("dve", G, bs[:], bs[:, :, 0:1].to_broadcast([128, 8, 16]), ALU.subtract, reads=allb, writes=[("gate", s)])
            yield
            pg.act(G, G, AF.Exp, reads=[("gate", s)], writes=[("gate", s)])
            yield
            pg.red("dve", zz[:, 0:8], G, ALU.add, reads=[("gate", s)], writes=["zz0"])
            yield
            pg.op("dve", lambda e: e.reciprocal(zz[:, 8:16], zz[:, 0:8]), reads=["zz0"], writes=["zz1"])
            yield
            pg.tt("dve", G, G, zz[:, 8:16].unsqueeze(2).to_broadcast([128, 8, 16]), ALU.mult, reads=[("gate", s), "zz1"], writes=[("gate", s)])
            yield
            pg.ts("dve", pa_u[:], posu[:], 4, None, ALU.logical_shift_right, reads=allp, writes=["pau"])
            yield
            pg.ts("dve", pb_u[:], posu[:], 15, None, ALU.bitwise_and, reads=allp, writes=["pbu"])
            yield
            pg.cp("dve", pa[:], pa_u[:], reads=["pau"], writes=["pa"])
            yield
            pg.cp("dve", pb[:], pb_u[:], reads=["pbu"], writes=["pb"])
            yield
            iob = io16[:, :].unsqueeze(1).unsqueeze(1).to_broadcast([128, 8, 16, 16])
            for (pp, key, half, dst, dk_) in ((pa, "pa", 0, ia, "ia"), (pb, "pb", 1, ib, "ib")):
                pg.tt("dve", oh[:], pp[:].unsqueeze(3).to_broadcast([128, 8, 16, 16]), iob, ALU.is_equal, reads=[key, "io16", "oh"], writes=["oh"])
                yield
                tsel = ti4[:, :, half, :].unsqueeze(2).to_broadcast([128, 8, 16, 16])
                pg.tt("dve", oh[:], oh[:], tsel, ALU.mult, reads=["oh", "ti"], writes=["oh"])
                yield
                pg.red("dve", dst[:], oh[:], ALU.add, reads=["oh"], writes=[dk_])
                yield
            pg.stt("dve", eidf[:].rearrange("p (h j) -> p h j", h=8), ia[:], 128.0, ib[:], ALU.mult, ALU.add, reads=["ia", "ib"], writes=["eidf"])
            yield
            pg.cp("dve", eidi[i % 3][:], eidf[:], reads=["eidf"], writes=[("eidi", i % 3)])
            yield

        GS = 2
        SK = 1

        def gstep(i, e_):
            s = i % 2
            b_ = e_ % NB
            pg.dma("pool", lambda e, e_=e_, b_=b_, i=i: e.indirect_dma_start(
                out=uv[b_][:, :], out_offset=None, in_=dr["UV"][:, :],
                in_offset=bass.IndirectOffsetOnAxis(ap=eidi[i % 3][:, e_:e_ + 1], axis=0)),
                reads=[("eidi", i % 3)], writes=[("uv", b_)])
            pg.op("dve", lambda e, e_=e_, b_=b_, s=s: e.scalar_tensor_tensor(junk[:], uv[b_][:, 0:D], 1.0, h2b[s][:], ALU.mult, ALU.mult,
                                                                              accum_out=actv[s][:, e_:e_ + 1]),
                  reads=[("uv", b_), ("h2b", s)], writes=[("act", s, e_)])

        def gelu_grp(i, k):
            s = i % 2
            sl = slice(k * GS, (k + 1) * GS)
            pg.act(ga[s][:, sl], actv[s][:, sl], AF.Gelu, reads=[("act", s, e_) for e_ in range(k * GS, (k + 1) * GS)], writes=[("ga", s, k)])

        def fin_grp(i, k):
            s = i % 2
            sl = slice(k * GS, (k + 1) * GS)
            pg.tt("dve", ga[s][:, sl], ga[s][:, sl], gate[s][:, sl], ALU.mult, reads=[("ga", s, k), ("gate", s)], writes=[("ga", s, k)])
            for e_ in range(k * GS, (k + 1) * GS):
                b_ = e_ % NB
                d_ = e_ % 4
                pg.act(dg[d_][:], idb[:], AF.Copy, reads=[("ga", s, k), "idb"], writes=[("dg", d_)], scale=ga[s][:, e_:e_ + 1])
                for n_ in range(2):
                    pg.mm(psY[n_][:], dg[d_][:], uv[b_][:, D + n_ * 512:D + (n_ + 1) * 512], start=(e_ == 0), stop=(e_ == 127),
                          reads=[("dg", d_), ("uv", b_)], writes=[("psY", n_)])

        def tail(i):
            s = i % 2
            t0 = i * 128
            for n_ in range(2):
                pg.tt("dve", yo[0][:, n_ * 512:(n_ + 1) * 512], psY[n_][:], x1[i % 3][:, n_ * 512:(n_ + 1) * 512], ALU.add,
                      reads=[("psY", n_), ("x1", i % 3)], writes=[("yo", 0, n_)])
            pg.ld(dr["out"][t0:t0 + 128, :], yo[0][:], reads=[("yo", 0, 0), ("yo", 0, 1)], writes=[("out", i)])

        def drain(g, n=None):
            k = 0
            while g is not None and (n is None or k < n):
                try:
                    next(g)
                except StopIteration:
                    return None
                k += 1
            return g

        drain(front(0))
        for i in range(ntiles):
            gen2 = front(i + 1) if i + 1 < ntiles else None
            for k in range(128 // GS):
                for e_ in range(k * GS, (k + 1) * GS):
                    gstep(i, e_)
                    gen2 = drain(gen2, 3)
                gelu_grp(i, k)
                if k >= SK:
                    fin_grp(i, k - SK)
            for k in range(128 // GS - SK, 128 // GS):
                fin_grp(i, k)
            drain(gen2)
            tail(i)
        pg.barrier()
        pg.emit()


_NC_CACHE = {}


def kernel(**inputs):
    inputs = {k: np.asarray(v) for k, v in inputs.items()}
    ntl = NT // 2
    if "nc" not in _NC_CACHE:
        _NC_CACHE["nc"] = build([stage1, stage2a, stage2x, stage2c, stage3, stage4, stage5], peer_tiles=ntl)
    nc = _NC_CACHE["nc"]
    base = {}
    in_maps = []
    for c in range(8):
        b, hh = c % 4, c // 4
        if hh not in base:
            base[hh] = host_inputs(inputs, b, hh, ntl)
            m = base[hh]
        else:
            m = dict(base[hh])
            m["x"] = core_x(inputs, b, hh)
        in_maps.append(m)
    res = run_bass_kernel_spmd(nc, in_maps, core_ids=list(range(8)))
    out = np.zeros((4, T, D), np.float32)
    for c in range(8):
        b, hh = c % 4, c // 4
        out[b, hh * ntl * 128:(hh + 1) * ntl * 128, :] = res.results[c]["out"]
    return out


def stage2x(C):
    nc, pg, dr = C.nc, C.pg, C.dr
    with ExitStack() as st:
        sb, ps = _mk(C, st)
        idf = sb("x_idf", [128, 128], F32)
        tri = sb("x_tri", [128, 128], F32)
        msk = sb("x_msk", [128, 3, 128], F32)
        ones = sb("x_ones", [128, 1], F32)
        inp = [[sb(f"x_in{s}_{j}", [128, 512], F32) for j in range(6)] for s in range(2)]
        Pt = sb("x_P", [128, 512], F32)
        iP = sb("x_iP", [128, 512], F32)
        Pp = sb("x_Pp", [128, 512], F32)
        tm = [[sb(f"x_tm{s}_{j}", [128, 512], F32) for j in range(4)] for s in range(2)]
        fm = [[sb(f"x_fm{s}_{j}", [64, 8, 128], F32) for j in range(4)] for s in range(2)]
        M = [[sb(f"x_M{s}_{j}", [128, 8, 128], (BF16 if j in (0, 4) else F32)) for j in range(5)] for s in range(2)]
        Xb = sb("x_Xb", [128, 8, 128], BF16)
        idb = sb("x_idb", [128, 128], BF16)
        X = [sb(f"x_X{s}", [128, 8, 128], F32) for s in range(2)]
        PC = [sb(f"x_PC{s}", [64, 8], F32) for s in range(2)]
        N2 = [sb(f"x_N2_{j}", [128, 8, 128], BF16) for j in range(2)]
        N2T = [sb(f"x_N2T_{j}", [128, 8, 128], BF16) for j in range(2)]
        Z = [sb(f"x_Z{j}", [64, 512], F32) for j in range(2)]
        rhs_sb = sb("x_rhs", [128, 512], F32)
        U_sb = sb("x_U", [128, 512], F32)
        Y_sb = [sb(f"x_Y{j}", [128, 512], F32) for j in range(2)]
        bank = [ps(f"x_bank{j}", [128, 512], F32) for j in range(8)]

        pg.ld(idf[:], dr["ident"][:, :], writes=["idf"])
        pg.ld(tri[:], dr["c_tri"][:, :], writes=["tri"])
        pg.ld(msk[:], dr["c_msk"][:, :, :], writes=["msk"])
        pg.memset("dve", ones[:], 1.0, writes=["ones"])
        pg.cp("dve", idb[:], idf[:], reads=["idf"], writes=["idb"])
        pg.memset("dve", Z[0][:], 0.0, writes=[("Z", 0)])
        names = ("RR", "RKK", "RLW", "RB", "RKp", "RV")
        bk = [0]

        def nb():
            v = bk[0]
            bk[0] = (v + 1) % 8
            return v

        def pre(c):
            s = c % 2
            t0 = c * 128
            I = inp[s]
            for j, nm in enumerate(names):
                pg.ld(I[j][:], dr[nm][t0:t0 + 128, :], reads=[(nm, c)], writes=[("in", s, j)])
            r_, kkn, lw, b_, kp, v_ = [t_[:] for t_ in I]
            bL = nb()
            pg.mm(bank[bL][:], tri[:], lw, reads=["tri", ("in", s, 2)], writes=[("bank", bL)])
            bC = nb()
            for h in range(8):
                pg.mm(bank[bC][0:64, h:h + 1], I[2][:, h * 64:(h + 1) * 64], ones[:, 0:1], reads=[("in", s, 2), "ones"], writes=[("bank", bC)])
            pg.act(PC[s][:], bank[bC][0:64, 0:8], AF.Exp, reads=[("bank", bC)], writes=[("PC", s)])
            pg.act(Pt[:], bank[bL][:], AF.Exp, reads=[("bank", bL)], writes=["P"])
            pg.act(iP[:], bank[bL][:], AF.Exp, reads=[("bank", bL)], writes=["iP"], scale=-1.0)
            pg.tt("dve", Pp[:], bank[bL][:], lw, ALU.subtract, reads=[("bank", bL), ("in", s, 2)], writes=["Pp"])
            pg.act(Pp[:], Pp[:], AF.Exp, reads=["Pp"], writes=["Pp"])
            TM = tm[s]
            pg.tt("pool", TM[0][:], r_, Pt[:], ALU.mult, reads=[("in", s, 0), "P"], writes=[("tm", s, 0)])
            pg.stt("dve", TM[1][:], kkn, -1.0, Pp[:], ALU.mult, ALU.mult, reads=[("in", s, 1), "Pp"], writes=[("tm", s, 1)])
            pg.tt("pool", TM[2][:], b_, iP[:], ALU.mult, reads=[("in", s, 3), "iP"], writes=[("tm", s, 2)])
            pg.tt("dve", TM[3][:], kp, iP[:], ALU.mult, reads=[("in", s, 4), "iP"], writes=[("tm", s, 3)])
            for j in range(4):
                if j == 0 and c < NT // 2:
                    continue
                for hg in range(2):
                    bT = nb()
                    for hh in range(4):
                        h = hg * 4 + hh
                        pg.tr(bank[bT][0:64, hh * 128:(hh + 1) * 128], TM[j][:, h * 64:(h + 1) * 64], idf[:], reads=[("tm", s, j), "idf"], writes=[("bank", bT)])
                    pg.cp("act" if (j + hg) % 2 == 0 else "dve", fm[s][j][:, hg * 4:(hg + 1) * 4, :].rearrange("p a b -> p (a b)"), bank[bT][0:64, :],
                          reads=[("bank", bT)], writes=[("fm", s, j, hg)])
            FR, FKK, FB, FK = fm[s]
            combos = ((0, FB, 2, FKK, 1, 0), (1, FK, 3, FKK, 1, 0), (2, FB, 2, FR, 0, 1), (3, FK, 3, FR, 0, 1), (4, FKK, 1, FB, 2, 2))
            for hg in range(2):
                for (mi, L_, lj, R_, rj, mk) in combos:
                    if mi in (2, 3) and c < NT // 2:
                        continue
                    bM = nb()
                    for hh in range(4):
                        h = hg * 4 + hh
                        pg.mm(bank[bM][:, hh * 128:(hh + 1) * 128], L_[:, h, :], R_[:, h, :], reads=[("fm", s, lj, hg), ("fm", s, rj, hg)], writes=[("bank", bM)])
                    pg.tt("dve", M[s][mi][:, hg * 4:(hg + 1) * 4, :], bank[bM][:].rearrange("p (a b) -> p a b", a=4),
                          msk[:, mk, :].unsqueeze(1).to_broadcast([128, 4, 128]), ALU.mult, reads=[("bank", bM), "msk"], writes=[("M", s, mi, hg)])
                pg.tt("pool", Xb[:, hg * 4:(hg + 1) * 4, :], idb[:, :].unsqueeze(1).to_broadcast([128, 4, 128]), M[s][0][:, hg * 4:(hg + 1) * 4, :], ALU.subtract,
                      reads=[("M", s, 0, hg), "idb"], writes=[("Xb", hg)])
            curN = [M[s][0], M[s][0]]
            curNT = [M[s][4], M[s][4]]
            kN = [("M", s, 0, 0), ("M", s, 0, 1)]
            kNT = [("M", s, 4, 0), ("M", s, 4, 1)]
            for j in range(6):
                dst = j % 2
                for hg in range(2):
                    b1, b2 = nb(), nb()
                    for hh in range(4):
                        h = hg * 4 + hh
                        pg.mm(bank[b1][:, hh * 128:(hh + 1) * 128], curNT[hg][:, h, :], curN[hg][:, h, :], reads=[kN[hg], kNT[hg]], writes=[("bank", b1)])
                    for hh in range(4):
                        h = hg * 4 + hh
                        pg.mm(bank[b2][:, hh * 128:(hh + 1) * 128], curN[hg][:, h, :], curNT[hg][:, h, :], reads=[kN[hg], kNT[hg]], writes=[("bank", b2)])
                    pg.cp("act", N2[dst][:, hg * 4:(hg + 1) * 4, :].rearrange("p a b -> p (a b)"), bank[b1][:], reads=[("bank", b1)], writes=[("N2", dst, hg)])
                    pg.cp("dve", N2T[dst][:, hg * 4:(hg + 1) * 4, :].rearrange("p a b -> p (a b)"), bank[b2][:], reads=[("bank", b2)], writes=[("N2T", dst, hg)])
                for hg in range(2):
                    curN[hg], curNT[hg] = N2[dst], N2T[dst]
                    kN[hg], kNT[hg] = ("N2", dst, hg), ("N2T", dst, hg)
                for hg in range(2):
                    b3 = nb()
                    for hh in range(4):
                        h = hg * 4 + hh
                        pg.mm(bank[b3][:, hh * 128:(hh + 1) * 128], curNT[hg][:, h, :], Xb[:, h, :], reads=[kNT[hg], ("Xb", hg)], writes=[("bank", b3)])
                    xo = (X[s] if j == 5 else Xb)
                    pg.tt("dve", xo[:, hg * 4:(hg + 1) * 4, :].rearrange("p a b -> p (a b)"), Xb[:, hg * 4:(hg + 1) * 4, :].rearrange("p a b -> p (a b)"), bank[b3][:], ALU.add,
                          reads=[("bank", b3), ("Xb", hg)], writes=[("X", s, hg)] if j == 5 else [("Xb", hg)])

        def seq(c):
            s = c % 2
            t0 = c * 128
            zc, zn = Z[c % 2], Z[(c + 1) % 2]
            kz, kzn = ("Z", c % 2), ("Z", (c + 1) % 2)
            FR, FKK, FB, FK = fm[s]
            V = inp[s][5]
            hsl = lambda h: slice(h * 64, (h + 1) * 64)
            Mk = lambda mi: [("M", s, mi, 0), ("M", s, mi, 1)]
            fk = lambda j: [("fm", s, j, 0), ("fm", s, j, 1)]
            Xk = [("X", s, 0), ("X", s, 1)]
            bG = nb()
            for h in range(8):
                pg.mm(bank[bG][:, hsl(h)], M[s][1][:, h, :], V[:, hsl(h)], start=True, stop=False, reads=Mk(1) + [("in", s, 5)], writes=[("bank", bG)])
                pg.mm(bank[bG][:, hsl(h)], FKK[:, h, :], zc[:, hsl(h)], start=False, stop=True, reads=fk(1) + [kz], writes=[("bank", bG)])
            pg.ts("dve", rhs_sb[:], bank[bG][:], -1.0, None, ALU.mult, reads=[("bank", bG)], writes=["rhs"])
            bU = nb()
            for h in range(8):
                pg.mm(bank[bU][:, hsl(h)], X[s][:, h, :], rhs_sb[:, hsl(h)], reads=Xk + ["rhs"], writes=[("bank", bU)])
            pg.cp("act", U_sb[:], bank[bU][:], reads=[("bank", bU)], writes=["U"])
            bZ = nb()
            for h in range(8):
                pg.mm(bank[bZ][0:64, hsl(h)], tm[s][3][:, hsl(h)], V[:, hsl(h)], start=True, stop=False, reads=[("tm", s, 3), ("in", s, 5)], writes=[("bank", bZ)])
                pg.mm(bank[bZ][0:64, hsl(h)], idf[0:64, 0:64], zc[:, hsl(h)], start=False, stop=False, reads=["idf", kz], writes=[("bank", bZ)])
                pg.mm(bank[bZ][0:64, hsl(h)], tm[s][2][:, hsl(h)], U_sb[:, hsl(h)], start=False, stop=True, reads=[("tm", s, 2), "U"], writes=[("bank", bZ)])
            pg.tt("dve", zn[:].rearrange("p (h v) -> p h v", h=8), bank[bZ][0:64, :].rearrange("p (h v) -> p h v", h=8),
                  PC[s][:, :].unsqueeze(2).to_broadcast([64, 8, 64]), ALU.mult, reads=[("bank", bZ), ("PC", s)], writes=[kzn])
            if c < NT // 2:
                return
            bY = nb()
            for h in range(8):
                pg.mm(bank[bY][:, hsl(h)], M[s][3][:, h, :], V[:, hsl(h)], start=True, stop=False, reads=Mk(3) + [("in", s, 5)], writes=[("bank", bY)])
                pg.mm(bank[bY][:, hsl(h)], FR[:, h, :], zc[:, hsl(h)], start=False, stop=False, reads=fk(0) + [kz], writes=[("bank", bY)])
                pg.mm(bank[bY][:, hsl(h)], M[s][2][:, h, :], U_sb[:, hsl(h)], start=False, stop=True, reads=Mk(2) + ["U"], writes=[("bank", bY)])
            pg.cp("act", Y_sb[s][:], bank[bY][:], reads=[("bank", bY)], writes=[("Y", s)])
            pg.ld(dr["YS"][t0:t0 + 128, :], Y_sb[s][:], reads=[("Y", s)], writes=[("YS", c)])

        tcg = table_conv_gen(C, sb)
        pre(0)
        for c in range(NT):
            if c + 1 < NT:
                pre(c + 1)
            for _ in range(8):
                next(tcg, None)
            seq(c)
        for _ in tcg:
            pass
        pg.barrier()
        pg.emit()
```

```python
import numpy as np
import concourse.bass as bass
import concourse.mybir as mybir

F32 = mybir.dt.float32
BF16 = mybir.dt.bfloat16
I32 = mybir.dt.int32
U32 = mybir.dt.uint32
ALU = mybir.AluOpType
AF = mybir.ActivationFunctionType
AX = mybir.AxisListType

EPOCH = 20000
ENGS = ("pe", "act", "dve", "pool", "sp")
NDMASEM = 16


class Prog:
    def __init__(self, nc, stack):
        self.nc = nc
        self.stack = stack
        self.ops = {e: [] for e in ENGS}
        self.cnt = {e: 0 for e in ENGS}
        self.esems = {e: [] for e in ENGS}
        self.waited = {e: {} for e in ENGS}
        self.lastw = {}
        self.readers = {}
        self.dsems = {}
        self.dcount = {}
        self.dtarget = {}
        self.semobjs = {}
        self.alltokens = {}
        for q in ("sp", "act", "pool"):
            self.dsems[q] = [self._newsem(f"d_{q}_{i}") for i in range(NDMASEM)]
            self.dcount[q] = 0
            self.dtarget[q] = [0] * NDMASEM

    def _newsem(self, name):
        s = self.stack.enter_context(self.nc.semaphore(name))
        self.semobjs[name] = s
        return name

    def _esem(self, e, idx):
        ep = idx // EPOCH
        while len(self.esems[e]) <= ep:
            self.esems[e].append(self._newsem(f"e_{e}_{len(self.esems[e])}"))
        return self.esems[e][ep], (idx % EPOCH) + 1

    def _deps(self, reads, writes):
        toks = []
        for k in reads:
            t = self.lastw.get(k)
            if t is not None:
                toks.append(t)
        for k in writes:
            t = self.lastw.get(k)
            if t is not None:
                toks.append(t)
            toks.extend(self.readers.get(k, ()))
        return toks

    def _commit(self, tok, reads, writes):
        for k in reads:
            self.readers.setdefault(k, []).append(tok)
        for k in writes:
            self.lastw[k] = tok
            self.readers[k] = []
        self.alltokens[tok[0]] = max(self.alltokens.get(tok[0], 0), tok[1])

    def _waits(self, e, toks):
        need = {}
        for (s, v) in toks:
            if v > need.get(s, 0):
                need[s] = v
        out = []
        w = self.waited[e]
        for s, v in need.items():
            if w.get(s, 0) < v:
                w[s] = v
                out.append((s, v))
        return out

    def op(self, e, fn, reads=(), writes=()):
        toks = self._deps(reads, writes)
        if e == "pe":
            toks = [t for t in toks if not t[0].startswith("e_pe_")]
        waits = self._waits(e, toks)
        idx = self.cnt[e]
        self.cnt[e] += 1
        tok = self._esem(e, idx)
        self.ops[e].append((waits, fn, (tok[0], 1)))
        self._commit(tok, reads, writes)

    def dma(self, q, fn, reads=(), writes=()):
        toks = self._deps(reads, writes)
        n = self.dcount[q]
        self.dcount[q] += 1
        slot = n % NDMASEM
        sname = self.dsems[q][slot]
        prev = self.dtarget[q][slot]
        if prev > 0:
            toks.append((sname, prev))
        tgt = prev + 16
        self.dtarget[q][slot] = tgt
        waits = self._waits(q, toks)
        tok = (sname, tgt)
        self.ops[q].append((waits, fn, (sname, 16)))
        self._commit(tok, reads, writes)

    def mm(self, out, lhsT, rhs, start=True, stop=True, reads=(), writes=()):
        self.op("pe", lambda e: e.matmul(out, lhsT, rhs, start=start, stop=stop), reads, writes)

    def tr(self, out, in_, ident, reads=(), writes=()):
        self.op("pe", lambda e: e.transpose(out, in_, ident), reads, writes)

    def act(self, out, in_, func, reads=(), writes=(), bias=None, scale=None, eng="act"):
        kw = {}
        if bias is not None:
            kw["bias"] = bias
        if scale is not None:
            kw["scale"] = scale
        self.op(eng, lambda e: e.activation(out, in_, func, **kw), reads, writes)

    def tt(self, eng, out, in0, in1, op, reads=(), writes=()):
        self.op(eng, lambda e: e.tensor_tensor(out, in0, in1, op), reads, writes)

    def ts(self, eng, out, in0, s1, s2, op0, op1=None, reads=(), writes=()):
        if op1 is None:
            self.op(eng, lambda e: e.tensor_scalar(out, in0, s1, s2, op0), reads, writes)
        else:
            self.op(eng, lambda e: e.tensor_scalar(out, in0, s1, s2, op0, op1), reads, writes)

    def stt(self, eng, out, in0, scalar, in1, op0, op1, reads=(), writes=()):
        self.op(eng, lambda e: e.scalar_tensor_tensor(out, in0, scalar, in1, op0, op1), reads, writes)

    def cp(self, eng, out, in_, reads=(), writes=()):
        if eng == "act":
            self.op(eng, lambda e: e.copy(out, in_), reads, writes)
        else:
            self.op(eng, lambda e: e.tensor_copy(out, in_), reads, writes)

    def red(self, eng, out, in_, op, reads=(), writes=(), axis=None):
        ax = AX.X if axis is None else axis
        self.op(eng, lambda e: e.tensor_reduce(out, in_, ax, op), reads, writes)

    def memset(self, eng, ap, val, writes=()):
        self.op(eng, lambda e: e.memset(ap, val), (), writes)

    def ld(self, out, in_, reads=(), writes=(), q="sp"):
        self.dma(q, lambda e: e.dma_start(out, in_), reads, writes)

    def barrier(self):
        toks = list(self.alltokens.items())
        for e in ENGS:
            waits = self._waits(e, toks)
            if waits:
                self.ops[e].append((waits, None, None))
        self.lastw = {}
        self.readers = {}

    def emit(self):
        nc = self.nc
        so = self.semobjs
        with nc.Block() as block:
            def mk(e):
                def body(eng):
                    for waits, fn, inc in self.ops[e]:
                        for (s, v) in waits:
                            eng.wait_ge(so[s], v)
                        if fn is not None:
                            ins = fn(eng)
                            ins.then_inc(so[inc[0]], inc[1])
                return body
            block.tensor(mk("pe"))
            block.scalar(mk("act"))
            block.vector(mk("dve"))
            block.gpsimd(mk("pool"))
            block.sync(mk("sp"))
        self.ops = {e: [] for e in ENGS}
from contextlib import ExitStack
from concourse.bass_utils import run_bass_kernel_spmd

T = 4096
D = 1024
NT = T // 128
INW = 5144
RWC = 1792
O_RW = 0
O_Q = 1792
O_KC = 2304
O_VC = 2432
O_KS = 2560
O_VS = 2688
O_KW = 2816
O_VW = 2944
O_BG = 3072
O_GA = 3096
O_GB = 4120


class Ctx:
    pass


def _mk(C, st):
    nc = C.nc
    sb = lambda name, shape, dt: st.enter_context(nc.sbuf_tensor(name, shape, dt))
    ps = lambda name, shape, dt: st.enter_context(nc.psum_tensor(name, shape, dt))
    return sb, ps


def stage1(C):
    nc, pg, dr = C.nc, C.pg, C.dr
    with ExitStack() as st:
        sb, ps = _mk(C, st)
        win = sb("s1_win", [128, 8, INW], BF16)
        pj = [sb(f"s1_pj{i}", [128, INW], F32) for i in range(2)]
        xt = [sb(f"s1_xt{i}", [128, D], F32) for i in range(2)]
        junk = sb("s1_junk", [128, D], F32)
        hb = [sb(f"s1_h{i}", [128, D], BF16) for i in range(2)]
        hT = [sb(f"s1_hT{i}", [128, 8, 128], BF16) for i in range(2)]
        gt = sb("s1_g", [128, D], F32)
        idf = sb("s1_idf", [128, 128], F32)
        idb = sb("s1_idb", [128, 128], BF16)
        ss = [sb(f"s1_ss{i}", [128, 4], F32) for i in range(2)]
        psT = [ps(f"s1_psT{i}", [128, 8, 128], BF16) for i in range(2)]
        psm = [ps(f"s1_psm{i}", [128, 512], F32) for i in range(4)]

        pg.ld(gt[:], dr["norm1_g_b"][:, :], writes=["gt"])
        pg.ld(idf[:], dr["ident"][:, :], writes=["idf"])
        pg.cp("dve", idb[:], idf[:], reads=["idf"], writes=["idb"])
        engs = ["act", "dve", "pool"]
        for kc in range(8):
            b = pj[kc % 2]
            pg.ld(b[:], dr["w_in"][kc * 128:(kc + 1) * 128, :], writes=[("pjall", kc % 2)])
            pg.cp(engs[kc % 3], win[:, kc, :], b[:], reads=[("pjall", kc % 2)], writes=[("win", kc)])
        winkeys = [("win", kc) for kc in range(8)]
        chunks = []
        c0 = 0
        while c0 < INW:
            w = min(512, INW - c0)
            chunks.append((c0, w))
            c0 += w
        def A1(i):
            s = i % 2
            pg.ld(xt[s][:], dr["x"][i * 128:(i + 1) * 128, :], writes=[("xt", s)])
            pg.tt("dve", junk[:], xt[s][:], xt[s][:], ALU.mult, reads=[("xt", s)], writes=["junk"])
            pg.red("dve", ss[s][:, 0:1], junk[:], ALU.add, reads=["junk"], writes=[("ss", s)])
            pg.act(ss[s][:, 1:2], ss[s][:, 0:1], AF.Sqrt, reads=[("ss", s)], writes=[("ss1", s)],
                   scale=1.0 / D, bias=C.eps6[:, 0:1])
            pg.op("dve", lambda e, o=ss[s][:, 2:3], a=ss[s][:, 1:2]: e.reciprocal(o, a),
                  reads=[("ss1", s)], writes=[("ss2", s)])
            pg.stt("dve", hb[s][:], xt[s][:], ss[s][:, 2:3], gt[:], ALU.mult, ALU.mult,
                   reads=[("xt", s), ("ss2", s), "gt"], writes=[("hb", s)])

        def A2(i):
            s = i % 2
            for j in range(8):
                pg.tr(psT[s][:, j, :], hb[s][:, j * 128:(j + 1) * 128], idb[:],
                      reads=[("hb", s), "idb"], writes=[("psT", s)])
            pg.cp("act", hT[s][:], psT[s][:], reads=[("psT", s)], writes=[("hT", s)])

        chunks_lo = [(512, 512), (1024, 512), (1536, 128), (2304, 512), (2816, 256)]

        def B(i, lo, hi):
            s = i % 2
            if i < NT // 2 - 1:
                if lo != 0:
                    return
                for ci, (c0, w) in enumerate(chunks_lo):
                    pb = psm[ci % 4]
                    for kc in range(8):
                        pg.mm(pb[:, :w], hT[s][:, kc, :], win[:, kc, c0:c0 + w], start=(kc == 0), stop=(kc == 7),
                              reads=[("hT", s), ("win", kc)], writes=[("psm", ci % 4)])
                    pg.cp("act" if ci % 2 == 0 else "dve", pj[s][:, c0:c0 + w], pb[:, :w],
                          reads=[("psm", ci % 4)], writes=[("pj", s, k_) for k_ in range(len(chunks))] + ([("pjall", s)] if i < 8 else []))
                return
            for ci in range(lo, hi):
                c0, w = chunks[ci]
                pb = psm[ci % 4]
                for kc in range(8):
                    pg.mm(pb[:, :w], hT[s][:, kc, :], win[:, kc, c0:c0 + w], start=(kc == 0), stop=(kc == 7),
                          reads=[("hT", s), ("win", kc)], writes=[("psm", ci % 4)])
                pg.cp("act" if ci % 2 == 0 else "dve", pj[s][:, c0:c0 + w], pb[:, :w],
                      reads=[("psm", ci % 4)], writes=[("pj", s, ci), ("pjall", s)] if i < 8 else [("pj", s, ci)])

        def S(i):
            s = i % 2
            pg.ld(dr["P"][i * 128:(i + 1) * 128, 0:3096], pj[s][:, 0:3096],
                  reads=[("pj", s, ci) for ci in range(len(chunks))], writes=[("P", i)])
            pg.ld(dr["PG"][i * 128:(i + 1) * 128, :], pj[s][:, 3096:5144],
                  reads=[("pj", s, ci) for ci in range(len(chunks))], writes=[("PG", i)], q="act")

        A1(0)
        A2(0)
        A1(1)
        for i in range(NT):
            B(i, 0, 6)
            if i + 1 < NT:
                A2(i + 1)
            if i + 2 < NT:
                A1(i + 2)
            B(i, 6, len(chunks))
            S(i)
        pg.barrier()
        pg.emit()


def build(stages, dbg_out=(), dbg_in=(), lvl=9, sub=9, peer_tiles=NT):
    nc = bass.Bass("TRN2", target_bir_lowering=False)
    C = Ctx()
    C.peer_tiles = peer_tiles
    C.lvl = lvl
    C.sub = sub
    C.nc = nc
    dr = {}
    C.dr = dr

    def din(name, shape, dt=F32):
        dr[name] = nc.dram_tensor(name, list(shape), dt, kind="ExternalInput").ap()

    def dscr(name, shape, dt=F32):
        kind = "ExternalOutput" if name in dbg_out else ("ExternalInput" if name in dbg_in else "Internal")
        dr[name] = nc.dram_tensor(name, list(shape), dt, kind=kind).ap()

    din("x", [T, D])
    din("norm1_g_b", [128, D])
    din("ident", [128, 128])
    din("w_in", [D, INW])
    dscr("P", [T, INW])
    dscr("PG", [T, 2048])
    for nm in ("rw_mu_b",):
        din(nm, [128, RWC])
    for nm in ("rw_w0_b", "rw_a0_b", "rw_k_k_b", "rw_k_a_b", "rw_r_k_b", "rw_ln_w_b", "rw_ln_b_b", "rw_g_up"):
        din(nm, [128, 512])
    din("rw_w_up", [64, 512])
    din("rw_a_up", [64, 512])
    for nm in ("RB", "RKp", "RV", "RG", "YA", "RR", "RKK", "RLW", "YS"):
        dscr(nm, [T, 512])
    dscr("RBON", [T, 8])
    din("blkmask", [8, 512])
    din("c_tri", [128, 128])
    din("c_msk", [128, 3, 128])
    din("nsa_gains_b", [128, 768])
    din("nsa_kc_g_b", [128, 64])
    din("ovl", [128, 2, 64])
    din("posT", [128, 2, 32])
    din("cmp_w2", [128, 2, 2, 64])
    din("cmp_w1", [2, 128, 32, 256])
    din("c_cmpb", [128, 2, T], BF16)
    din("c_esel", [128, 32, 128], BF16)
    din("c_causb", [128, 4, 512], BF16)
    din("c_winb", [128, 8, 512], BF16)
    din("c_winb4", [128, 8, 512], BF16)
    din("c_vmfb", [NT, 128, 2, 64])
    dscr("YB", [T, 512])
    din("w_branch_a", [512, D])
    din("w_branch_b", [512, D])
    din("w_out", [D, D])
    dscr("X1L", [peer_tiles * 128, D])
    dscr("YAL", [peer_tiles * 128, 512])
    din("norm2_g_b", [128, D])
    din("iota16", [128, 16])
    din("rowidx", [128, NT], I32)
    din("peer_wq", [D, 2048])
    din("peer_k1", [128, 128])
    din("peer_k2", [128, 128])
    din("peer_u", [16384, D])
    din("peer_v", [16384, D])
    dscr("UV", [16384, 2 * D], BF16)
    dr["out"] = nc.dram_tensor("out", [peer_tiles * 128, D], F32, kind="ExternalOutput").ap()
    with ExitStack() as top:
        pg = Prog(nc, top)
        C.pg = pg
        C.eps6 = top.enter_context(nc.sbuf_tensor("c_eps6", [128, 1], F32))
        pg.memset("dve", C.eps6[:], 1e-6, writes=["eps6"])
        pg.barrier()
        for s in stages:
            s(C)
        pg.barrier()
        pg.emit()
    return nc


def core_x(inputs, b, hh):
    xb = np.asarray(inputs["x"][b])
    if hh == 0:
        return np.ascontiguousarray(np.concatenate([np.zeros((T // 2, D), np.float32), xb[0:T // 2]], 0))
    return np.ascontiguousarray(xb)


def host_inputs(inputs, b, hh=1, ntl=NT):
    g = lambda k: np.ascontiguousarray(inputs[k][0])
    m = {}
    m["x"] = core_x(inputs, b, hh)
    m["norm1_g_b"] = np.ascontiguousarray(np.broadcast_to(g("norm1_g")[None, :], (128, D)))
    m["ident"] = np.eye(128, dtype=np.float32)
    m["w_in"] = g("w_in")
    bc = lambda a: np.ascontiguousarray(np.broadcast_to(np.asarray(a).reshape(1, -1), (128, a.size)))
    m["rw_mu_b"] = bc(g("rw_mu"))
    for nm in ("rw_w0", "rw_a0", "rw_k_k", "rw_k_a", "rw_r_k", "rw_ln_w", "rw_ln_b"):
        m[nm + "_b"] = bc(g(nm))
    for nm in ("rw_g_up", "rw_w_up", "rw_a_up"):
        m[nm] = g(nm)
    bmk = np.zeros((8, 512), np.float32)
    for h in range(8):
        bmk[h, h * 64:(h + 1) * 64] = 1.0
    m["blkmask"] = bmk
    ii = np.arange(128)
    m["c_tri"] = (ii[:, None] <= ii[None, :]).astype(np.float32)
    m["c_msk"] = np.ascontiguousarray(np.stack([(ii[:, None] < ii[None, :]), (ii[:, None] <= ii[None, :]), (ii[:, None] > ii[None, :])], 1).astype(np.float32))
    m.update(nsa_consts(hh))
    for nm in ("w_branch_a", "w_branch_b", "w_out", "peer_wq", "peer_k1", "peer_k2", "peer_u", "peer_v"):
        m[nm] = g(nm)
    m["norm2_g_b"] = bc(g("norm2_g"))
    ri = np.zeros((128, NT), np.int32)
    ri[:, :ntl] = ((NT - ntl) * 128 + np.arange(ntl)[None, :] * 128 + np.arange(128)[:, None]).astype(np.int32)
    m["rowidx"] = ri
    m["iota16"] = np.ascontiguousarray(np.broadcast_to(np.arange(16, dtype=np.float32)[None, :], (128, 16)))
    m["nsa_gains_b"] = bc(np.concatenate([np.tile(g("nsa_q_g"), 8), np.tile(g("nsa_ks_g"), 2), np.tile(g("nsa_kw_g"), 2)]))
    m["nsa_kc_g_b"] = bc(g("nsa_kc_g"))
    posT = np.zeros((128, 2, 32), np.float32)
    posT[0:64, 0, :] = g("cmp_pos_k").T
    posT[0:64, 1, :] = g("cmp_pos_v").T
    m["posT"] = posT
    w2 = np.stack([g("cmp_k_w2").reshape(2, 128, 64), g("cmp_v_w2").reshape(2, 128, 64)], 0)
    m["cmp_w2"] = np.ascontiguousarray(w2.transpose(2, 0, 1, 3))
    w1 = []
    for nm in ("cmp_k_w1", "cmp_v_w1"):
        a = g(nm).reshape(32, 64, 256).transpose(1, 0, 2)
        w1.append(np.concatenate([a, a], 0))
    m["cmp_w1"] = np.ascontiguousarray(np.stack(w1, 0))
    return m


def dap(ap, offset, pattern):
    return bass.AP(ap.tensor, offset, [list(p) for p in pattern])


def stage2a(C):
    nc, pg, dr = C.nc, C.pg, C.dr
    with ExitStack() as st:
        sb, ps = _mk(C, st)
        mu = sb("a_mu", [128, RWC], F32)
        w0 = sb("a_w0", [128, 512], F32)
        a0 = sb("a_a0", [128, 512], F32)
        kkc = sb("a_kk", [128, 512], F32)
        kac = sb("a_ka", [128, 512], F32)
        rkc = sb("a_rk", [128, 512], F32)
        wup = sb("a_wup", [128, 512], F32)
        gup = sb("a_gup", [128, 512], F32)
        idf = sb("a_idf", [128, 128], F32)
        p_2 = [sb(f"a_p{i_}", [128, RWC], F32) for i_ in range(2)]
        pv_2 = [sb(f"a_pv{i_}", [128, RWC], F32) for i_ in range(2)]
        pm_2 = [sb(f"a_pm{i_}", [128, RWC], F32) for i_ in range(2)]
        lor_2 = [sb(f"a_lor{i_}", [128, 256], F32) for i_ in range(2)]
        lorT_2 = [sb(f"a_lorT{i_}", [128, 256], F32) for i_ in range(2)]
        wt_2 = [sb(f"a_wt{i_}", [128, 512], F32) for i_ in range(2)]
        lwt_2 = [sb(f"a_lwt{i_}", [128, 512], F32) for i_ in range(2)]
        at_2 = [sb(f"a_at{i_}", [128, 512], F32) for i_ in range(2)]
        gt_2 = [sb(f"a_gt{i_}", [128, 512], F32) for i_ in range(2)]
        kk_2 = [sb(f"a_kkt{i_}", [128, 512], F32) for i_ in range(2)]
        sq_2 = [sb(f"a_sq{i_}", [128, 512], F32) for i_ in range(2)]
        nrm_2 = [sb(f"a_nrm{i_}", [128, 32], F32) for i_ in range(2)]
        kkn_2 = [sb(f"a_kkn{i_}", [128, 512], F32) for i_ in range(2)]
        bt_2 = [sb(f"a_bt{i_}", [128, 512], F32) for i_ in range(2)]
        t1_2 = [sb(f"a_t1{i_}", [128, 512], F32) for i_ in range(2)]
        kp_2 = [sb(f"a_kp{i_}", [128, 512], F32) for i_ in range(2)]
        bon_2 = [sb(f"a_bon{i_}", [128, 8], F32) for i_ in range(2)]
        psl = ps("a_psl", [128, 512], F32)
        psw = ps("a_psw", [128, 512], F32)
        psa = ps("a_psa", [128, 512], F32)
        psg = ps("a_psg", [128, 512], F32)

        for (tile, name) in ((mu, "rw_mu_b"), (w0, "rw_w0_b"), (a0, "rw_a0_b"), (kkc, "rw_k_k_b"),
                             (kac, "rw_k_a_b"), (rkc, "rw_r_k_b"), (gup, "rw_g_up"), (idf, "ident")):
            pg.ld(tile[:], dr[name][:, :], writes=[name])
        pg.ld(wup[0:64, :], dr["rw_w_up"][:, :], writes=["wup0"])
        pg.ld(wup[64:128, :], dr["rw_a_up"][:, :], writes=["wup1"])
        P = dr["P"]
        for i in range(NT):
            t0 = i * 128
            s = i % 2
            p, pv, pm, lor, lorT, wt, lwt, at, gt, kk, sq, nrm, kkn, bt, t1, kp, bon = [t_[s] for t_ in (
                p_2, pv_2, pm_2, lor_2, lorT_2, wt_2, lwt_2, at_2, gt_2, kk_2, sq_2, nrm_2, kkn_2, bt_2, t1_2, kp_2, bon_2)]
            pg.ld(p[:], P[t0:t0 + 128, 0:RWC], reads=[("P", i)], writes=[("p", s)])
            if i == 0:
                pg.memset("dve", pv[0:1, :], 0.0, writes=[("pv0", s)])
                pg.ld(pv[1:128, :], P[0:127, 0:RWC], reads=[("P", 0)], writes=[("pv", s)])
                pvk = [("pv", s), ("pv0", s)]
            else:
                pg.ld(pv[:], P[t0 - 1:t0 + 127, 0:RWC], reads=[("P", i), ("P", i - 1)], writes=[("pv", s), ("pv0", s)])
                pvk = [("pv", s), ("pv0", s)]
            pg.tt("dve", pv[:], pv[:], p[:], ALU.subtract, reads=pvk + [("p", s)], writes=[("pv", s)])
            pg.tt("dve", pv[:], pv[:], mu[:], ALU.mult, reads=[("pv", s), "rw_mu_b"], writes=[("pv", s)])
            pg.tt("dve", pm[:], pv[:], p[:], ALU.add, reads=[("pv", s), ("p", s)], writes=[("pm", s)])
            r_ = pm[:, 0:512]
            k_ = pm[:, 512:1024]
            v_ = pm[:, 1024:1536]
            pg.act(lor[:, 0:64], pm[:, 1536:1600], AF.Tanh, reads=[("pm", s)], writes=[("lor0", s)])
            pg.cp("pool", lor[:, 64:128], pm[:, 1600:1664], reads=[("pm", s)], writes=[("lor1", s)])
            pg.act(lor[:, 128:256], pm[:, 1664:1792], AF.Sigmoid, reads=[("pm", s)], writes=[("lor2", s)])
            pg.tr(psl[:, 0:128], lor[:, 0:128], idf[:], reads=[("lor0", s), ("lor1", s), "ident"], writes=["psl"])
            pg.tr(psl[:, 128:256], lor[:, 128:256], idf[:], reads=[("lor2", s), "ident"], writes=["psl"])
            pg.cp("act", lorT[:], psl[:, 0:256], reads=["psl"], writes=[("lorT", s)])
            pg.mm(psw[:], lorT[0:64, 0:128], wup[0:64, :], reads=[("lorT", s), "wup0"], writes=["psw"])
            pg.mm(psa[:], lorT[64:128, 0:128], wup[64:128, :], reads=[("lorT", s), "wup1"], writes=["psa"])
            pg.mm(psg[:], lorT[:, 128:256], gup[:], reads=[("lorT", s), "rw_g_up"], writes=["psg"])
            pg.tt("dve", wt[:], psw[:], w0[:], ALU.add, reads=["psw", "rw_w0_b"], writes=[("wt", s)])
            pg.act(wt[:], wt[:], AF.Sigmoid, reads=[("wt", s)], writes=[("wt", s)])
            pg.ts("dve", lwt[:], wt[:], -0.6065306597126334, None, ALU.mult, reads=[("wt", s)], writes=[("lwt", s)])
            pg.tt("dve", at[:], psa[:], a0[:], ALU.add, reads=["psa", "rw_a0_b"], writes=[("at", s)])
            pg.act(at[:], at[:], AF.Sigmoid, reads=[("at", s)], writes=[("at", s)])
            pg.cp("act", gt[:], psg[:], reads=["psg"], writes=[("gt", s)])
            pg.tt("dve", kk[:], k_, kkc[:], ALU.mult, reads=[("pm", s), "rw_k_k_b"], writes=[("kk", s)])
            pg.tt("pool", sq[:], kk[:], kk[:], ALU.mult, reads=[("kk", s)], writes=[("sq", s)])
            pg.red("dve", nrm[:, 0:8], sq[:].rearrange("p (h k) -> p h k", h=8), ALU.add, reads=[("sq", s)], writes=[("nrm0", s)])
            pg.act(nrm[:, 8:16], nrm[:, 0:8], AF.Sqrt, reads=[("nrm0", s)], writes=[("nrm1", s)])
            pg.ts("dve", nrm[:, 16:24], nrm[:, 8:16], 1e-12, None, ALU.max, reads=[("nrm1", s)], writes=[("nrm2", s)])
            pg.op("dve", lambda e, nrm=nrm: e.reciprocal(nrm[:, 24:32], nrm[:, 16:24]), reads=[("nrm2", s)], writes=[("nrm3", s)])
            rinv_b = nrm[:, 24:32].unsqueeze(2).to_broadcast([128, 8, 64])
            v3 = lambda tl: tl[:].rearrange("p (h k) -> p h k", h=8)
            pg.stt("dve", v3(kkn), v3(kk), -1.0, rinv_b, ALU.mult, ALU.mult, reads=[("kk", s), ("nrm3", s)], writes=[("kkn", s)])
            pg.stt("dve", bt[:], kkn[:], -1.0, at[:], ALU.mult, ALU.mult, reads=[("kkn", s), ("at", s)], writes=[("bt", s)])
            pg.stt("dve", t1[:], at[:], -1.0, kac[:], ALU.add, ALU.mult, reads=[("at", s), "rw_k_a_b"], writes=[("t1", s)])
            pg.stt("dve", kp[:], t1[:], 1.0, k_, ALU.add, ALU.mult, reads=[("t1", s), ("pm", s)], writes=[("kp", s)])
            pg.tt("pool", sq[:], r_, kp[:], ALU.mult, reads=[("pm", s), ("kp", s), ("sq", s)], writes=[("sq", s)])
            pg.tt("pool", sq[:], sq[:], rkc[:], ALU.mult, reads=[("sq", s), "rw_r_k_b"], writes=[("sq", s)])
            pg.red("dve", bon[:], sq[:].rearrange("p (h k) -> p h k", h=8), ALU.add, reads=[("sq", s)], writes=[("bon", s)])
            pg.ld(dr["RR"][t0:t0 + 128, :], r_, reads=[("pm", s)], writes=[("RR", i)], q="act")
            pg.ld(dr["RKK"][t0:t0 + 128, :], kkn[:], reads=[("kkn", s)], writes=[("RKK", i)], q="act")
            pg.ld(dr["RLW"][t0:t0 + 128, :], lwt[:], reads=[("lwt", s)], writes=[("RLW", i)], q="act")
            pg.ld(dr["RB"][t0:t0 + 128, :], bt[:], reads=[("bt", s)], writes=[("RB", i)], q="act")
            pg.ld(dr["RKp"][t0:t0 + 128, :], kp[:], reads=[("kp", s)], writes=[("RKp", i)], q="act")
            pg.ld(dr["RV"][t0:t0 + 128, :], v_, reads=[("pm", s)], writes=[("RV", i)], q="act")
            pg.ld(dr["RG"][t0:t0 + 128, :], gt[:], reads=[("gt", s)], writes=[("RG", i)], q="act")
            pg.ld(dr["RBON"][t0:t0 + 128, :], bon[:], reads=[("bon", s)], writes=[("RBON", i)], q="act")
        pg.barrier()
        pg.emit()


def igather(pg, out_ap, table_ap, idx_ap, reads, writes):
    pg.dma("pool", lambda e: e.indirect_dma_start(out=out_ap, out_offset=None, in_=table_ap,
                                                   in_offset=bass.IndirectOffsetOnAxis(ap=idx_ap, axis=0)), reads, writes)


def stage2c(C):
    nc, pg, dr = C.nc, C.pg, C.dr
    with ExitStack() as st:
        sb, ps = _mk(C, st)
        lnw = sb("c_lnw", [128, 512], F32)
        lnb = sb("c_lnb", [128, 512], F32)
        eps = sb("c_eps", [128, 1], F32)
        y = [sb(f"c_y{i}", [128, 8, 64], F32) for i in range(2)]
        v = [sb(f"c_v{i}", [128, 8, 64], F32) for i in range(2)]
        g = [sb(f"c_g{i}", [128, 512], F32) for i in range(2)]
        bon = [sb(f"c_bon{i}", [128, 8], F32) for i in range(2)]
        stt_ = [sb(f"c_st{i}", [128, 32], F32) for i in range(2)]
        sq = sb("c_sq", [128, 8, 64], F32)
        pg.ld(lnw[:], dr["rw_ln_w_b"][:, :], writes=["lnw"])
        pg.ld(lnb[:], dr["rw_ln_b_b"][:, :], writes=["lnb"])
        pg.memset("dve", eps[:], 64e-5, writes=["eps"])
        f2 = lambda tl: tl[:].rearrange("p h k -> p (h k)")
        rowidx = sb("c_rowidx", [128, NT], I32)
        pg.ld(rowidx[:], dr["rowidx"][:, :], writes=["rowidx"])
        allk = lambda nm: [(nm, k) for k in range(NT)]
        for i in range(C.peer_tiles):
            s = i % 2
            t0 = i * 128
            yk, vk, gk, bk, sk = ("y", s), ("v", s), ("g", s), ("bon", s), ("st", s)
            ix = rowidx[:, i:i + 1]
            igather(pg, f2(y[s]), dr["YS"][:, :], ix, allk("YS") + ["rowidx"], [yk])
            igather(pg, f2(v[s]), dr["RV"][:, :], ix, allk("RV") + ["rowidx"], [vk])
            igather(pg, g[s][:], dr["RG"][:, :], ix, allk("RG") + ["rowidx"], [gk])
            igather(pg, bon[s][:], dr["RBON"][:, :], ix, allk("RBON") + ["rowidx"], [bk])
            S_ = stt_[s]
            bc = lambda ap: ap.unsqueeze(2).to_broadcast([128, 8, 64])
            pg.red("dve", S_[:, 0:8], y[s][:], ALU.add, reads=[yk], writes=[(sk, 0)])
            pg.ts("dve", S_[:, 8:16], S_[:, 0:8], -1.0 / 64, None, ALU.mult, reads=[(sk, 0)], writes=[(sk, 1)])
            pg.tt("dve", y[s][:], y[s][:], bc(S_[:, 8:16]), ALU.add, reads=[yk, (sk, 1)], writes=[yk])
            pg.tt("pool", sq[:], y[s][:], y[s][:], ALU.mult, reads=[yk], writes=["sq"])
            pg.red("dve", S_[:, 16:24], sq[:], ALU.add, reads=["sq"], writes=[(sk, 2)])
            pg.act(S_[:, 24:32], S_[:, 16:24], AF.Sqrt, reads=[(sk, 2), "eps"], writes=[(sk, 3)], scale=1.0 / 64, bias=eps[:, 0:1])
            pg.op("dve", lambda e, o=S_[:, 16:24], a=S_[:, 24:32]: e.reciprocal(o, a), reads=[(sk, 3)], writes=[(sk, 2)])
            pg.tt("dve", y[s][:], y[s][:], bc(S_[:, 16:24]), ALU.mult, reads=[yk, (sk, 2)], writes=[yk])
            pg.tt("dve", f2(y[s]), f2(y[s]), lnw[:], ALU.mult, reads=[yk, "lnw"], writes=[yk])
            pg.tt("pool", f2(y[s]), f2(y[s]), lnb[:], ALU.add, reads=[yk, "lnb"], writes=[yk])
            pg.tt("pool", v[s][:], v[s][:], bc(bon[s][:, 0:8]), ALU.mult, reads=[vk, bk], writes=[vk])
            pg.tt("dve", y[s][:], y[s][:], v[s][:], ALU.add, reads=[yk, vk], writes=[yk])
            pg.tt("dve", f2(y[s]), f2(y[s]), g[s][:], ALU.mult, reads=[yk, gk], writes=[yk])
            pg.ld(dr["YAL"][t0:t0 + 128, :], f2(y[s]), reads=[yk], writes=[("YAL", i)])
        pg.barrier()
        pg.emit()


NEG = -30000.0


def stage3(C):
    nc, pg, dr = C.nc, C.pg, C.dr
    with ExitStack() as st:
        sb, ps = _mk(C, st)
        qT = sb("n_qT", [128, 4, T], BF16)
        KsT = sb("n_KsT", [128, 2, T], BF16)
        KwT = sb("n_KwT", [128, 2, T], BF16)
        Vs = sb("n_Vs", [128, NT, 2, 65], BF16)
        Vw = sb("n_Vw", [128, NT, 2, 65], BF16)
        KcT = sb("n_KcT", [128, 2, 256], BF16)
        Vc = sb("n_Vc", [128, 2, 2, 129], BF16)
        GT = sb("n_GT", [128, NT, 24], F32)
        idf = sb("n_idf", [128, 128], F32)
        idb = sb("n_idb", [128, 128], BF16)
        eps = sb("n_eps", [128, 1], F32)
        pg.ld(idf[:], dr["ident"][:, :], writes=["idf"])
        pg.cp("dve", idb[:], idf[:], reads=["idf"], writes=["idb"])
        pg.memset("dve", eps[:], 1e-6, writes=["eps"])
        pg.memset("pool", Vs[:], 1.0, writes=["Vs"])
        pg.memset("pool", Vw[:], 1.0, writes=["Vw"])
        pg.memset("pool", Vc[:], 0.0, writes=["Vc"])
        with ExitStack() as sa_:
            sb, ps = _mk(C, sa_)
            kcT2 = sb("n_kcT2", [128, T], BF16)
            vcT2 = sb("n_vcT2", [128, T], BF16)
            w1 = [sb(f"n_w1{i}", [128, 32, 256], BF16) for i in range(2)]
            w1s = sb("n_w1s", [128, 16, 256], F32)
            w2s = sb("n_w2s", [128, 2, 2, 64], F32)
            w2 = sb("n_w2", [128, 2, 2, 64], BF16)
            posf = sb("n_posf", [128, 2, 32], F32)
            posb = sb("n_posb", [128, 2, 32], BF16)
            gains = sb("n_gains", [128, 768], F32)
            kcg = sb("n_kcg", [128, 64], F32)
            ovl = sb("n_ovl", [128, 2, 64], F32)
            R = [sb(f"n_R{i}", [128, 1304], F32) for i in range(2)]
            sq = sb("n_sq", [128, 1280], F32)
            tmp = sb("n_tmp", [128, 768], F32)
            stat = sb("n_stat", [128, 64], F32)
            Xb = sb("n_Xb", [128, 10, 128], BF16)
            biasS = sb("n_biasS", [128, 4], F32)
            xb_ = sb("n_xb", [128, 256], F32)
            x2_ = sb("n_x2", [128, 256], F32)
            hT = sb("n_hT", [128, 2, 256], BF16)
            kcn2 = sb("n_kcn2", [128, 128], BF16)
            st2 = sb("n_st2", [128, 8], F32)
            ksq = sb("n_ksq", [128, 64], F32)
            psX_ = [ps(f"n_psX{i}", [128, 1024], BF16) for i in range(3)]
            psX = [t_[:, 0:512].rearrange("p (a b) -> p a b", a=4) for t_ in psX_]
            psh = ps("n_psh", [128, 512], F32)
            psb = ps("n_psb", [128, 512], F32)
            pso = ps("n_pso", [128, 512], F32)
            psk = ps("n_psk", [128, 1024], BF16)

            pg.ld(gains[:], dr["nsa_gains_b"][:, :], writes=["gains"])
            pg.ts("dve", gains[:, 0:512], gains[:, 0:512], 0.125, None, ALU.mult, reads=["gains"], writes=["gains"])
            pg.ld(kcg[:], dr["nsa_kc_g_b"][:, :], writes=["kcg"])
            pg.ld(ovl[:], dr["ovl"][:, :, :], writes=["ovl"])
            pg.ld(posf[:], dr["posT"][:, :, :], writes=["posf"])
            pg.cp("dve", posb[:], posf[:], reads=["posf"], writes=["posb"])
            pg.ld(w2s[:], dr["cmp_w2"][:, :, :, :], writes=["w2s"])
            pg.cp("dve", w2[:], w2s[:], reads=["w2s"], writes=["w2"])
            for x in range(2):
                for hf in range(2):
                    pg.ld(w1s[:], dr["cmp_w1"][x, :, hf * 16:(hf + 1) * 16, :], writes=["w1s"])
                    pg.cp("pool", w1[x][:, hf * 16:(hf + 1) * 16, :], w1s[:], reads=["w1s"], writes=[("w1", x)])
            for i in range(NT):
                s = i % 2
                t0 = i * 128
                Rk = ("R", s)
                pg.ld(R[s][:], dr["P"][t0:t0 + 128, 1792:3096], reads=[("P", i)], writes=[Rk])
                Rs = R[s]
                pg.tt("pool", sq[:], Rs[:, 0:1280], Rs[:, 0:1280], ALU.mult, reads=[Rk], writes=["sq"])
                pg.red("dve", stat[:, 0:20], sq[:].rearrange("p (a k) -> p a k", k=64), ALU.add, reads=["sq"], writes=["stat0"])
                pg.act(stat[:, 20:40], stat[:, 0:20], AF.Sqrt, reads=["stat0", "eps"], writes=["stat1"], scale=1.0 / 64, bias=eps[:, 0:1])
                pg.op("dve", lambda e: e.reciprocal(stat[:, 40:60], stat[:, 20:40]), reads=["stat1"], writes=["stat2"])
                b3 = lambda ap, n: ap.unsqueeze(2).to_broadcast([128, n, 64])
                v3 = lambda ap: ap.rearrange("p (a k) -> p a k", k=64)
                pg.tt("dve", v3(tmp[:, 0:512]), v3(Rs[:, 0:512]), b3(stat[:, 40:48], 8), ALU.mult, reads=[Rk, "stat2"], writes=["tmp"])
                pg.tt("dve", v3(tmp[:, 512:640]), v3(Rs[:, 768:896]), b3(stat[:, 52:54], 2), ALU.mult, reads=[Rk, "stat2"], writes=["tmp"])
                pg.tt("dve", v3(tmp[:, 640:768]), v3(Rs[:, 1024:1152]), b3(stat[:, 56:58], 2), ALU.mult, reads=[Rk, "stat2"], writes=["tmp"])
                pg.tt("pool", tmp[:], tmp[:], gains[:], ALU.mult, reads=["tmp", "gains"], writes=["tmp"])
                pg.cp("pool", Xb[:, 0:4, :].rearrange("p a b -> p (a b)"), tmp[:, 0:512], reads=["tmp"], writes=["Xb"])
                for (blk, c0) in ((4, 512), (6, 640)):
                    src = tmp[:, c0:c0 + 128].rearrange("p (g k) -> p g k", g=2).unsqueeze(2).to_broadcast([128, 2, 2, 64])
                    dst = Xb[:, blk:blk + 2, :].rearrange("p g (d k) -> p g d k", d=2)
                    pg.cp("dve", dst, src, reads=["tmp"], writes=["Xb"])
                pg.cp("pool", Xb[:, 8, :], Rs[:, 512:640], reads=[Rk], writes=["Xb"])
                pg.cp("pool", Xb[:, 9, :], Rs[:, 640:768], reads=[Rk], writes=["Xb"])
                for blk in range(10):
                    pg.tr(psX[blk // 4][:, blk % 4, :], Xb[:, blk, :], idb[:], reads=["Xb", "idb"], writes=[("psX", blk // 4)])
                pg.cp("act", qT[:, :, t0:t0 + 128], psX[0], reads=[("psX", 0)], writes=["qT"])
                pg.cp("dve", KsT[:, :, t0:t0 + 128], psX[1][:, 0:2, :], reads=[("psX", 1)], writes=["KsT"])
                pg.cp("dve", KwT[:, :, t0:t0 + 128], psX[1][:, 2:4, :], reads=[("psX", 1)], writes=["KwT"])
                pg.cp("act", kcT2[:, t0:t0 + 128], psX[2][:, 0, :], reads=[("psX", 2)], writes=["kcT2"])
                pg.cp("act", vcT2[:, t0:t0 + 128], psX[2][:, 1, :], reads=[("psX", 2)], writes=["vcT2"])
                pg.cp("pool", Vs[:, i, :, 0:64], Rs[:, 896:1024].rearrange("p (g k) -> p g k", g=2), reads=[Rk, "Vs"], writes=["Vs"])
                pg.cp("pool", Vw[:, i, :, 0:64], Rs[:, 1152:1280].rearrange("p (g k) -> p g k", g=2), reads=[Rk, "Vw"], writes=["Vw"])
                pg.act(GT[:, i, :], Rs[:, 1280:1304], AF.Sigmoid, reads=[Rk], writes=["GT"])
            if getattr(C, "lvl", 9) < 2:
                pg.barrier()
                pg.emit()
                return
            pg.memset("dve", hT[:], 0.0, writes=["hT"])
            pg.memset("dve", kcn2[:], 0.0, writes=["kcn2"])
            for x in range(2):
                for hf in range(2):
                    for l in range(32):
                        pg.mm(psb[:, x * 2 + hf:x * 2 + hf + 1], w1[x][0:64, l, hf * 128:(hf + 1) * 128], posb[0:64, x, l:l + 1],
                              start=(l == 0), stop=(l == 31), reads=[("w1", x), "posb"], writes=["psb"])
            pg.cp("dve", biasS[:], psb[:, 0:4], reads=["psb"], writes=["biasS"])
            for x in range(2):
                srcT = kcT2 if x == 0 else vcT2
                skey = "kcT2" if x == 0 else "vcT2"
                for g in range(2):
                    for hf in range(2):
                        for l in range(32):
                            rhs = dap(srcT[:], g * 64 * T + l, [[T, 64], [16, 255]])
                            pg.mm(psh[:, 0:255], w1[x][g * 64:(g + 1) * 64, l, hf * 128:(hf + 1) * 128], rhs,
                                  start=(l == 0), stop=(l == 31), reads=[("w1", x), skey], writes=["psh"])
                        c = slice(0, 255)
                        pg.act(xb_[:, c], psh[:, c], AF.Identity, reads=["psh", "biasS"], writes=["xb"], bias=biasS[:, x * 2 + hf:x * 2 + hf + 1])
                        pg.tt("pool", x2_[:, c], xb_[:, c], xb_[:, c], ALU.mult, reads=["xb"], writes=["x2"])
                        pg.ts("dve", x2_[:, c], x2_[:, c], 0.044715, 1.0, ALU.mult, ALU.add, reads=["x2"], writes=["x2"])
                        pg.tt("dve", x2_[:, c], x2_[:, c], xb_[:, c], ALU.mult, reads=["x2", "xb"], writes=["x2"])
                        pg.act(x2_[:, c], x2_[:, c], AF.Tanh, reads=["x2"], writes=["x2"], scale=0.7978845608028654)
                        pg.stt("dve", x2_[:, c], x2_[:, c], 1.0, xb_[:, c], ALU.add, ALU.mult, reads=["x2", "xb"], writes=["x2"])
                        pg.ts("dve", hT[:, hf, c], x2_[:, c], 0.5, None, ALU.mult, reads=["x2"], writes=["hT"])
                    for m in range(2):
                        rows = 128 if m == 0 else 127
                        for hf in range(2):
                            pg.mm(pso[0:rows, 0:64], hT[:, hf, m * 128:m * 128 + rows], w2[:, x, hf, :], start=(hf == 0), stop=(hf == 1),
                                  reads=["hT", "w2"], writes=["pso"])
                        if x == 0:
                            pg.cp("act", ksq[0:rows, :], pso[0:rows, 0:64], reads=["pso"], writes=["ksq"])
                            pg.tt("pool", x2_[0:rows, 0:64], ksq[0:rows, :], ksq[0:rows, :], ALU.mult, reads=["ksq", "x2"], writes=["x2"])
                            pg.red("dve", st2[0:rows, 0:1], x2_[0:rows, 0:64], ALU.add, reads=["x2"], writes=["st2a"])
                            pg.act(st2[0:rows, 1:2], st2[0:rows, 0:1], AF.Sqrt, reads=["st2a", "eps"], writes=["st2b"], scale=1.0 / 64, bias=eps[0:rows, 0:1])
                            pg.op("dve", lambda e, rows=rows: e.reciprocal(st2[0:rows, 2:3], st2[0:rows, 1:2]), reads=["st2b"], writes=["st2c"])
                            pg.stt("dve", ksq[0:rows, :], ksq[0:rows, :], st2[0:rows, 2:3], kcg[0:rows, :], ALU.mult, ALU.mult,
                                   reads=["ksq", "st2c", "kcg"], writes=["ksq"])
                            src = ksq[0:rows, :].unsqueeze(1).to_broadcast([rows, 2, 64])
                            pg.cp("dve", kcn2[0:rows, :].rearrange("p (d k) -> p d k", d=2), src, reads=["ksq"], writes=["kcn2"])
                            pg.tr(psk[:, 0:128], kcn2[:, :], idb[:], reads=["kcn2", "idb"], writes=["psk"])
                            pg.cp("act", KcT[:, g, m * 128:(m + 1) * 128], psk[:, 0:128], reads=["psk"], writes=["KcT"])
                        else:
                            pg.cp("act", Vc[0:rows, m, g, 0:64], pso[0:rows, 0:64], reads=["pso", "Vc"], writes=["Vc"])
            for m in range(2):
                for g in range(2):
                    pg.memset("dve", Vc[:, m, g, 64:65], 1.0, writes=["Vc"])
                    pg.cp("dve", Vc[:, m, g, 65:129], ovl[:, m, :], reads=["ovl", "Vc"], writes=["Vc"])
            pg.barrier()
            pg.emit()
        if getattr(C, "lvl", 9) < 3:
            return
        stage3_attn(C, st, qT, KsT, KwT, Vs, Vw, KcT, Vc, GT, idf, idb)


def stage3_attn(C, st, qT, KsT, KwT, Vs, Vw, KcT, Vc, GT, idf, idb):
    nc, pg, dr = C.nc, C.pg, C.dr
    with ExitStack() as sb_:
        sb, ps = _mk(C, sb_)
        cmpb = sb("n_cmpb", [128, 2, T], BF16)
        Esel = sb("n_Esel", [128, 32, 128], BF16)
        causb = sb("n_causb", [128, 4, 512], BF16)
        winb = sb("n_winb", [128, 8, 512], BF16)
        selbT = sb("n_selbT", [128, 2, T], BF16)
        eT = [sb(f"n_eT{i}", [128, 512], BF16) for i in range(4)]
        eT2 = [sb(f"n_eT2{i}", [128, 512], BF16) for i in range(4)]
        Mt = [sb(f"n_Mt{i}", [128, 512], BF16) for i in range(2)]
        rm = [0]
        dq = []
        ocmp = sb("n_ocmp", [128, 4, 8, 64], F32)
        osel = sb("n_osel", [128, 4, 8, 64], F32)
        owin = sb("n_owin", [128, 4, 8, 64], F32)
        den = sb("n_den", [128, 16], F32)
        impw = sb("n_impw", [128, 2, 4, 64], F32)
        score = sb("n_score", [128, 2, 64], F32)
        VM = [sb(f"n_VM{i}", [128, 2, 64], F32) for i in range(2)]
        work = sb("n_work", [128, 2, 64], F32)
        m8 = sb("n_m8", [128, 2, 16], F32)
        thr = sb("n_thr", [128, 2], F32)
        msel = sb("n_msel", [128, 2, 64], F32)
        selb = sb("n_selb", [128, 2, 2, 64], BF16)
        osT = [sb(f"n_osT{i}", [65, 512], F32) for i in range(2)]
        dn2 = sb("n_dn2", [128, 8], F32)
        yb = sb("n_yb", [128, 8, 64], F32)
        yb2 = sb("n_yb2", [128, 8, 64], F32)
        psS = [ps(f"n_psS{i}", [128, 512], F32) for i in range(3)]
        psA = [ps(f"n_psA{i}", [128, 512], F32) for i in range(2)]
        psB = [ps(f"n_psB{i}", [128, 512], F32) for i in range(2)]
        psZ_ = ps("n_psZ", [128, 1024], BF16)
        psZ = psZ_[:, 0:256].rearrange("p (g q) -> p g q", g=2)

        pg.ld(cmpb[:], dr["c_cmpb"][:, :, :], writes=["cmpb"])
        pg.ld(Esel[:], dr["c_esel"][:, :, :], writes=["Esel"])
        pg.ld(causb[:], dr["c_causb"][:, :, :], writes=["causb"])
        pg.ld(winb[:], dr["c_winb"][:, :, :], writes=["winb"])
        winb4 = sb("n_winb4", [128, 8, 512], BF16)
        pg.ld(winb4[:], dr["c_winb4"][:, :, :], writes=["winb4"])
        rs = [0]
        re = [0]

        def nxt(lst, n):
            v = lst[0]
            lst[0] = (v + 1) % n
            return v

        def qk(h):
            return (h % 2) * 64, h // 2, h // 4

        for Q in range(4, 8):
            tq0 = Q * 512
            for ii in range(4):
                i = Q * 4 + ii
                t0 = i * 128
                s = i % 2
                pg.ld(VM[s][:], dr["c_vmfb"][i, :, :, :], writes=[("VM", s)])
                nm = 2 if i >= 16 else 1
                for h in range(8):
                    base, hp, g = qk(h)
                    h4 = h % 4
                    for m in range(nm):
                        r = nxt(rs, 3)
                        pS = psS[r]
                        pg.mm(pS[:, 0:128], KcT[base:base + 64, g, m * 128:(m + 1) * 128], qT[base:base + 64, hp, t0:t0 + 128],
                              start=True, stop=False, reads=["KcT", "qT"], writes=[("psS", r)])
                        pg.mm(pS[:, 0:128], idb[:, :], cmpb[:, m, t0:t0 + 128], start=False, stop=True,
                              reads=["idb", "cmpb"], writes=[("psS", r)])
                        k = nxt(re, 4)
                        pg.act(eT[k][:, 0:128], pS[:, 0:128], AF.Exp, reads=[("psS", r)], writes=[("eT", k)])
                        def pv(g=g, h4=h4, k=k, m=m, nm=nm):
                            pg.mm(psA[g][:, h4 * 65:h4 * 65 + 65], eT[k][:, 0:128], Vc[:, m, g, 0:65], start=(m == 0), stop=(m == nm - 1),
                                  reads=[("eT", k), "Vc"], writes=[("psA", g)])
                            pg.mm(psB[g][:, h4 * 64:h4 * 64 + 64], eT[k][:, 0:128], Vc[:, m, g, 65:129], start=(m == 0), stop=(m == nm - 1),
                                  reads=[("eT", k), "Vc"], writes=[("psB", g)])
                        dq.append(pv)
                        if len(dq) > 2:
                            dq.pop(0)()
                while dq:
                    dq.pop(0)()
                for g in range(2):
                    A3 = psA[g][:, 0:260].rearrange("p (h c) -> p h c", c=65)
                    B3 = psB[g][:, 0:256].rearrange("p (h c) -> p h c", c=64)
                    dsl = den[:, g * 4:(g + 1) * 4]
                    rsl = den[:, 8 + g * 4:8 + (g + 1) * 4]
                    pg.ts("dve", dsl, A3[:, :, 64], 1e-30, None, ALU.max, reads=[("psA", g)], writes=[("den", g)])
                    pg.op("dve", lambda e, o=rsl, a=dsl: e.reciprocal(o, a), reads=[("den", g)], writes=[("rden", g)])
                    rb = rsl.unsqueeze(2).to_broadcast([128, 4, 64])
                    pg.tt("dve", ocmp[:, ii, g * 4:(g + 1) * 4, :], A3[:, :, 0:64], rb, ALU.mult, reads=[("psA", g), ("rden", g)], writes=["ocmp"])
                    pg.tt("dve", impw[:, g, :, :], B3, rb, ALU.mult, reads=[("psB", g), ("rden", g)], writes=[("impw", g)])
                    pg.red("dve", score[:, g, :], impw[:, g, :, :].rearrange("p h j -> p j h"), ALU.add, reads=[("impw", g)], writes=[("score", g)])
                    vm = dr
                    pg.tt("dve", score[:, g, :], score[:, g, :], VM[s][:, 0, :], ALU.mult, reads=[("score", g), ("VM", s)], writes=[("score", g)])
                    pg.tt("dve", score[:, g, :], score[:, g, :], VM[s][:, 1, :], ALU.add, reads=[("score", g), ("VM", s)], writes=[("score", g)])
                    pg.op("dve", lambda e, g=g: e.max(m8[:, g, 0:8], score[:, g, :]), reads=[("score", g)], writes=[("m8a", g)])
                    pg.op("dve", lambda e, g=g: e.match_replace(work[:, g, :], m8[:, g, 0:8], score[:, g, :], -1e9),
                          reads=[("score", g), ("m8a", g)], writes=[("work", g)])
                    pg.op("dve", lambda e, g=g: e.max(m8[:, g, 8:16], work[:, g, :]), reads=[("work", g)], writes=[("m8b", g)])
                    pg.ts("dve", thr[:, g:g + 1], m8[:, g, 15:16], -0.5, None, ALU.max, reads=[("m8b", g)], writes=[("thr", g)])
                    pg.ts("dve", msel[:, g, :], score[:, g, :], thr[:, g:g + 1], None, ALU.is_ge, reads=[("score", g), ("thr", g)], writes=[("msel", g)])
                    pg.cp("dve", selb[:, g, :, :], msel[:, g, :].unsqueeze(1).to_broadcast([128, 2, 64]), reads=[("msel", g)], writes=[("selb", g)])
                    pg.tr(psZ[:, g, :], selb[:, g, :, :].rearrange("p d j -> p (d j)"), idb[:], reads=[("selb", g), "idb"], writes=["psZ"])
                pg.cp("act", selbT[:, :, t0:t0 + 128], psZ, reads=["psZ"], writes=["selbT"])
            for br in range(2):
                if getattr(C, "lvl", 9) < 4 + br:
                    continue
                dest = osel if br == 0 else owin
                dkey = "osel" if br == 0 else "owin"
                KT = KsT if br == 0 else KwT
                Vv = Vs if br == 0 else Vw
                kts = list(range(0, 4 * Q + 4)) if br == 0 else list(range(max(0, 4 * Q - 4), 4 * Q + 4))
                for g in range(2):
                    O = [psA[0], psA[1], psB[0], psB[1]]
                    okeys = [("psA", 0), ("psA", 1), ("psB", 0), ("psB", 1)]
                    for n_, kt in enumerate(kts):
                        if br == 0:
                            r = nxt(rs, 3)
                            pg.mm(psS[r][:, :], Esel[0:64, kt, :], selbT[0:64, g, tq0:tq0 + 512], reads=["Esel", "selbT"], writes=[("psS", r)])
                            mi = nxt(rm, 2)
                            if kt >= 4 * Q:
                                pg.tt("dve", Mt[mi][:], psS[r][:, :], causb[:, kt - 4 * Q, :], ALU.mult, reads=[("psS", r), "causb"], writes=[("Mt", mi)])
                            else:
                                pg.cp("dve", Mt[mi][:], psS[r][:, :], reads=[("psS", r)], writes=[("Mt", mi)])
                            mask, mkeys = Mt[mi][:], [("Mt", mi)]
                        else:
                            wsrc = winb4 if Q == 4 else winb
                            mask, mkeys = wsrc[:, kt - 4 * Q + 4, :], ["winb", "winb4"]
                        for h4 in range(4):
                            h = g * 4 + h4
                            base, hp, _g = qk(h)
                            r2 = nxt(rs, 3)
                            pg.mm(psS[r2][:, :], KT[base:base + 64, g, kt * 128:(kt + 1) * 128], qT[base:base + 64, hp, tq0:tq0 + 512],
                                  reads=["qT"], writes=[("psS", r2)])
                            k = nxt(re, 4)
                            pg.act(eT[k][:, :], psS[r2][:, :], AF.Exp, reads=[("psS", r2)], writes=[("eT", k)])
                            pg.tt("dve", eT2[k][:, :], eT[k][:, :], mask, ALU.mult, reads=[("eT", k)] + mkeys, writes=[("eT2", k)])
                            dq.append(lambda h4=h4, kt=kt, k=k, n_=n_, O=O, okeys=okeys, Vv=Vv, g=g, kts=kts: pg.mm(
                                O[h4][0:65, :], Vv[:, kt, g, :], eT2[k][:, :], start=(n_ == 0), stop=(n_ == len(kts) - 1),
                                reads=[("eT2", k)], writes=[okeys[h4]]))
                            if len(dq) > 2:
                                dq.pop(0)()
                    while dq:
                        dq.pop(0)()
                    for h4 in range(4):
                        h = g * 4 + h4
                        o = h4 % 2
                        pg.cp("act", osT[o][:, :], O[h4][0:65, :], reads=[okeys[h4]], writes=[("osT", o)])
                        r3 = nxt(rs, 3)
                        Tp = psS[r3]
                        for qq in range(4):
                            pg.tr(Tp[:, qq * 65:(qq + 1) * 65], osT[o][0:65, qq * 128:(qq + 1) * 128], idf[0:65, 0:65],
                                  reads=[("osT", o), "idf"], writes=[("psS", r3)])
                        T3 = Tp[:, 0:260].rearrange("p (q c) -> p q c", c=65)
                        pg.ts("dve", dn2[:, 0:4], T3[:, :, 64], 1e-30, None, ALU.max, reads=[("psS", r3)], writes=["dn2a"])
                        pg.op("dve", lambda e: e.reciprocal(dn2[:, 4:8], dn2[:, 0:4]), reads=["dn2a"], writes=["dn2b"])
                        pg.tt("dve", dest[:, :, h, :], T3[:, :, 0:64], dn2[:, 4:8].unsqueeze(2).to_broadcast([128, 4, 64]), ALU.mult,
                              reads=[("psS", r3), "dn2b"], writes=[dkey])
            for ii in range(4):
                i = Q * 4 + ii
                t0 = i * 128
                G3 = GT[:, i, :].rearrange("p (h c) -> p h c", c=3)
                gb = lambda c: G3[:, :, c].unsqueeze(2).to_broadcast([128, 8, 64])
                pg.tt("dve", yb[:], ocmp[:, ii, :, :], gb(0), ALU.mult, reads=["ocmp", "GT"], writes=["yb"])
                pg.tt("pool", yb2[:], osel[:, ii, :, :], gb(1), ALU.mult, reads=["osel", "GT"], writes=["yb2"])
                pg.tt("dve", yb[:], yb[:], yb2[:], ALU.add, reads=["yb", "yb2"], writes=["yb"])
                pg.tt("pool", yb2[:], owin[:, ii, :, :], gb(2), ALU.mult, reads=["owin", "GT", "yb2"], writes=["yb2"])
                pg.tt("dve", yb[:], yb[:], yb2[:], ALU.add, reads=["yb", "yb2"], writes=["yb"])
                pg.ld(dr["YB"][t0:t0 + 128, :], yb[:].rearrange("p h k -> p (h k)"), reads=["yb"], writes=[("YB", i)])
        pg.barrier()
        pg.emit()


_NSA_CONSTS = {}


def nsa_consts(hh=1):
    if hh in _NSA_CONSTS:
        return _NSA_CONSTS[hh]
    import ml_dtypes
    bf = ml_dtypes.bfloat16
    c = {}
    n = np.arange(256)
    t = np.arange(T)
    nlo = 128 if hh == 0 else 0
    cm = np.where((16 * n[:, None] + 31 <= t[None, :]) & (n[:, None] < 255) & (n[:, None] >= nlo), 0.0, NEG).astype(np.float32)
    c["c_cmpb"] = np.ascontiguousarray(cm.reshape(2, 128, T).transpose(1, 0, 2)).astype(bf)
    es = np.zeros((64, 32, 128), np.float32)
    for kt in range(32):
        for key in range(128):
            es[2 * kt + key // 64, kt, key] = 1.0
    c["c_esel"] = np.concatenate([es, es], 0).astype(bf)
    key = np.arange(128)
    q = np.arange(512)
    cb = np.zeros((128, 4, 512), np.float32)
    for d in range(4):
        cb[:, d, :] = np.where((d * 128 + key[:, None]) <= q[None, :], 1.0, 0.0)
    c["c_causb"] = cb.astype(bf)
    wb = np.zeros((128, 8, 512), np.float32)
    for r in range(8):
        ka = (r - 4) * 128 + key[:, None]
        wb[:, r, :] = np.where((ka <= q[None, :]) & (ka > q[None, :] - 512), 1.0, 0.0)
    c["c_winb"] = wb.astype(bf)
    wb4 = wb.copy()
    if hh == 0:
        wb4[:, 0:4, :] = 0.0
    c["c_winb4"] = wb4.astype(bf)
    cs = np.arange(256) * 16
    ss = np.arange(64) * 64
    ov = np.clip(np.minimum(cs[:, None] + 32, ss[None, :] + 64) - np.maximum(cs[:, None], ss[None, :]), 0, None) / 32.0
    ov[255, :] = 0.0
    c["ovl"] = np.ascontiguousarray(ov.reshape(2, 128, 64).transpose(1, 0, 2)).astype(np.float32)
    cur = t // 64
    j = np.arange(64)
    jlo = 32 if hh == 0 else 0
    valid = (j[None, :] <= cur[:, None]) & (j[None, :] >= jlo)
    forced = (j[None, :] == jlo) | (j[None, :] == cur[:, None]) | (j[None, :] == cur[:, None] - 1)
    vm = valid.astype(np.float32)
    fb = np.where(valid, 1000.0 * forced, -1.0).astype(np.float32)
    c["c_vmfb"] = np.ascontiguousarray(np.stack([vm, fb], 1).reshape(NT, 128, 2, 64))
    _NSA_CONSTS[hh] = c
    return c


def stage4(C):
    nc, pg, dr = C.nc, C.pg, C.dr
    with ExitStack() as st:
        sb, ps = _mk(C, st)
        wa = sb("m_wa", [128, 4, D], BF16)
        wb = sb("m_wb", [128, 4, D], BF16)
        wo = sb("m_wo", [128, 8, D], BF16)
        stg = sb("m_stg", [128, D], F32)
        idf = sb("m_idf", [128, 128], F32)
        idb = sb("m_idb", [128, 128], BF16)
        yab = [sb(f"m_yab{i}", [128, 1024], F32) for i in range(2)]
        yabb = sb("m_yabb", [128, 1024], BF16)
        yT = sb("m_yT", [128, 8, 128], BF16)
        gts = [sb(f"m_g{i}", [128, 2048], F32) for i in range(2)]
        xt = [sb(f"m_x{i}", [128, D], F32) for i in range(2)]
        mix = sb("m_mix", [128, D], F32)
        mix2 = sb("m_mix2", [128, D], F32)
        mixb = sb("m_mixb", [128, D], BF16)
        mT = sb("m_mT", [128, 8, 128], BF16)
        x1 = [sb(f"m_x1{i}", [128, D], F32) for i in range(2)]
        psT = ps("m_psT", [128, 1024], BF16)
        psm = [ps(f"m_psm{i}", [128, 512], F32) for i in range(4)]
        psT2 = ps("m_psT2", [128, 1024], BF16)
        pso = [ps(f"m_pso{i}", [128, 512], F32) for i in range(2)]

        pg.ld(idf[:], dr["ident"][:, :], writes=["idf"])
        pg.cp("dve", idb[:], idf[:], reads=["idf"], writes=["idb"])
        n = 0
        for (wt, nm, kcs) in ((wa, "w_branch_a", 4), (wb, "w_branch_b", 4), (wo, "w_out", 8)):
            for kc in range(kcs):
                pg.ld(stg[:], dr[nm][kc * 128:(kc + 1) * 128, :], writes=["stg"])
                pg.cp(("act", "dve", "pool")[n % 3], wt[:, kc, :], stg[:], reads=["stg"], writes=[nm])
                n += 1
        rowidx = sb("m_rowidx", [128, NT], I32)
        pg.ld(rowidx[:], dr["rowidx"][:, :], writes=["rowidx"])
        allk = lambda nm: [(nm, k) for k in range(NT)]
        for i in range(C.peer_tiles):
            s = i % 2
            t0 = i * 128
            ix = rowidx[:, i:i + 1]
            pg.ld(yab[s][:, 0:512], dr["YAL"][t0:t0 + 128, :], reads=[("YAL", i)], writes=[("yab", s)])
            igather(pg, yab[s][:, 512:1024], dr["YB"][:, :], ix, allk("YB") + ["rowidx"], [("yab2", s)])
            igather(pg, gts[s][:], dr["PG"][:, :], ix, allk("PG") + ["rowidx"], [("gts", s)])
            igather(pg, xt[s][:], dr["x"][:, :], ix, ["rowidx"], [("xt", s)])
            pg.cp("pool", yabb[:], yab[s][:], reads=[("yab", s), ("yab2", s)], writes=["yabb"])
            for j in range(8):
                pg.tr(psT[:, j * 128:(j + 1) * 128], yabb[:, j * 128:(j + 1) * 128], idb[:], reads=["yabb", "idb"], writes=["psT"])
            pg.cp("act", yT[:].rearrange("p a b -> p (a b)"), psT[:], reads=["psT"], writes=["yT"])
            for br in range(2):
                wt = wa if br == 0 else wb
                for nchunk in range(2):
                    pb = psm[br * 2 + nchunk]
                    for kc in range(4):
                        pg.mm(pb[:], yT[:, br * 4 + kc, :], wt[:, kc, nchunk * 512:(nchunk + 1) * 512], start=(kc == 0), stop=(kc == 3),
                              reads=["yT", "w_branch_a", "w_branch_b"], writes=[("psm", br * 2 + nchunk)])
            pg.act(gts[s][:], gts[s][:], AF.Sigmoid, reads=[("gts", s)], writes=[("gts", s)])
            for nchunk in range(2):
                c = slice(nchunk * 512, (nchunk + 1) * 512)
                pg.tt("dve", mix[:, c], psm[nchunk][:], gts[s][:, nchunk * 512:(nchunk + 1) * 512], ALU.mult,
                      reads=[("psm", nchunk), ("gts", s)], writes=[("mix", nchunk)])
                pg.tt("dve", mix2[:, c], psm[2 + nchunk][:], gts[s][:, 1024 + nchunk * 512:1024 + (nchunk + 1) * 512], ALU.mult,
                      reads=[("psm", 2 + nchunk), ("gts", s)], writes=[("mix2", nchunk)])
                pg.tt("pool", mixb[:, c], mix[:, c], mix2[:, c], ALU.add, reads=[("mix", nchunk), ("mix2", nchunk)], writes=[("mixb", nchunk)])
            for j in range(8):
                pg.tr(psT2[:, j * 128:(j + 1) * 128], mixb[:, j * 128:(j + 1) * 128], idb[:], reads=[("mixb", 0), ("mixb", 1), "idb"], writes=["psT2"])
            pg.cp("act", mT[:].rearrange("p a b -> p (a b)"), psT2[:], reads=["psT2"], writes=["mT"])
            for nchunk in range(2):
                for kc in range(8):
                    pg.mm(pso[nchunk][:], mT[:, kc, :], wo[:, kc, nchunk * 512:(nchunk + 1) * 512], start=(kc == 0), stop=(kc == 7),
                          reads=["mT", "w_out"], writes=[("pso", nchunk)])
                pg.tt("dve", x1[s][:, nchunk * 512:(nchunk + 1) * 512], pso[nchunk][:], xt[s][:, nchunk * 512:(nchunk + 1) * 512], ALU.add,
                      reads=[("pso", nchunk), ("xt", s)], writes=[("x1", s, nchunk)])
            pg.ld(dr["X1L"][t0:t0 + 128, :], x1[s][:], reads=[("x1", s, 0), ("x1", s, 1)], writes=[("X1L", i)])
        pg.barrier()
        pg.emit()


def table_conv_gen(C, sb):
    pg, dr = C.pg, C.dr
    NBUF = 4
    src = [sb(f"z_src{i}", [128, D], F32) for i in range(NBUF)]
    dst = [sb(f"z_dst{i}", [128, D], BF16) for i in range(NBUF)]
    n = 0
    for (tab, co) in (("peer_u", 0), ("peer_v", D)):
        for a in range(16384 // 128):
            b_ = n % NBUF
            pg.ld(src[b_][:], dr[tab][a * 128:(a + 1) * 128, :], writes=[("zsrc", b_)], q="sp")
            pg.cp("act", dst[b_][:], src[b_][:], reads=[("zsrc", b_)], writes=[("zdst", b_)])
            pg.ld(dr["UV"][a * 128:(a + 1) * 128, co:co + D], dst[b_][:], reads=[("zdst", b_)], writes=[("UV", co, a)], q="act")
            n += 1
            yield


def stage5(C):
    nc, pg, dr = C.nc, C.pg, C.dr
    NB = 12
    with ExitStack() as st:
        sb, ps = _mk(C, st)
        wq = sb("p_wq", [128, 8, 2048], F32)
        kT = sb("p_kT", [128, 2, 128], F32)
        kraw = sb("p_kraw", [128, 2, 128], F32)
        g2 = sb("p_g2", [128, D], F32)
        idf = sb("p_idf", [128, 128], F32)
        io16 = sb("p_io16", [128, 16], F32)
        eps = sb("p_eps", [128, 1], F32)
        x1 = [sb(f"p_x1{i}", [128, D], F32) for i in range(3)]
        h2 = [sb(f"p_h2{i}", [128, D], F32) for i in range(2)]
        junk = sb("p_junk", [128, D], BF16)

        ss = sb("p_ss", [128, 4], F32)
        h2T = sb("p_h2T", [128, 8, 128], F32)
        qT = sb("p_qT", [128, 16, 128], F32)
        sc = sb("p_sc", [128, 16, 128], F32)
        work = sb("p_work", [128, 256], F32)
        tv = sb("p_tv", [128, 16, 16], F32)
        tiu = sb("p_tiu", [128, 16, 16], U32)
        ti = sb("p_ti", [128, 16, 16], F32)
        cs = sb("p_cs", [128, 8, 256], F32)
        bs = sb("p_bs", [128, 8, 16], F32)
        posu = sb("p_posu", [128, 8, 16], U32)
        pa_u = sb("p_pau", [128, 8, 16], U32)
        pb_u = sb("p_pbu", [128, 8, 16], U32)
        pa = sb("p_pa", [128, 8, 16], F32)
        pb = sb("p_pb", [128, 8, 16], F32)
        oh = sb("p_oh", [128, 8, 16, 16], F32)
        ia = sb("p_ia", [128, 8, 16], F32)
        ib = sb("p_ib", [128, 8, 16], F32)
        eidf = sb("p_eidf", [128, 128], F32)
        eidi = [sb(f"p_eidi{i}", [128, 128], I32) for i in range(3)]
        gate = [sb(f"p_gate{i}", [128, 128], F32) for i in range(2)]
        zz = sb("p_zz", [128, 16], F32)
        actv = [sb(f"p_act{i}", [128, 128], F32) for i in range(2)]
        ga = [sb(f"p_ga{i}", [128, 128], F32) for i in range(2)]
        uv = [sb(f"p_uv{i}", [128, 2 * D], BF16) for i in range(NB)]
        h2b = [sb(f"p_h2b{i}", [128, D], BF16) for i in range(2)]
        idb = sb("p_idb", [128, 128], BF16)
        junk2 = sb("p_junk2", [128, D], F32)
        dg = [sb(f"p_dg{i}", [128, 128], BF16) for i in range(4)]
        yo = [sb(f"p_yo{i}", [128, D], F32) for i in range(1)]
        psT = ps("p_psT", [128, 8, 128], F32)
        psQ = [ps(f"p_psQ{i}", [128, 512], F32) for i in range(2)]
        psY = [ps(f"p_psY{i}", [128, 512], F32) for i in range(2)]

        pg.ld(idf[:], dr["ident"][:, :], writes=["idf"])
        pg.ld(g2[:], dr["norm2_g_b"][:, :], writes=["g2"])
        pg.cp("dve", idb[:], idf[:], reads=["idf"], writes=["idb"])
        pg.ld(io16[:], dr["iota16"][:, :], writes=["io16"])
        rowidx = sb("p_rowidx", [128, NT], I32)
        pg.ld(rowidx[:], dr["rowidx"][:, :], writes=["rowidx"])
        pg.memset("dve", eps[:], 1e-6, writes=["eps"])
        for kc in range(8):
            pg.ld(wq[:, kc, :], dr["peer_wq"][kc * 128:(kc + 1) * 128, :], writes=["wq"])
        pg.ld(kraw[:, 0, :], dr["peer_k1"][:, :], writes=["kraw"])
        pg.ld(kraw[:, 1, :], dr["peer_k2"][:, :], writes=["kraw"])
        for hf in range(2):
            pg.tr(psQ[0][:, hf * 128:(hf + 1) * 128], kraw[:, hf, :], idf[:], reads=["kraw", "idf"], writes=[("psQ", 0)])
        pg.cp("dve", kT[:].rearrange("p a b -> p (a b)"), psQ[0][:, 0:256], reads=[("psQ", 0)], writes=["kT"])
        ntiles = getattr(C, "peer_tiles", NT)

        def front(i):
            s = i % 2
            t0 = i * 128
            pg.ld(x1[i % 3][:, :], dr["X1L"][t0:t0 + 128, :], reads=[("X1L", i)], writes=[("x1", i % 3)])
            yield
            pg.tt("pool", junk2[:], x1[i % 3][:], x1[i % 3][:], ALU.mult, reads=[("x1", i % 3), "junk2"], writes=["junk2"])
            yield
            pg.red("dve", ss[:, 0:1], junk2[:], ALU.add, reads=["junk2"], writes=["ss0"])
            yield
            pg.act(ss[:, 1:2], ss[:, 0:1], AF.Sqrt, reads=["ss0", "eps"], writes=["ss1"], scale=1.0 / D, bias=eps[:, 0:1])
            yield
            pg.op("dve", lambda e: e.reciprocal(ss[:, 2:3], ss[:, 1:2]), reads=["ss1"], writes=["ss2"])
            yield
            pg.stt("dve", h2[s][:], x1[i % 3][:], ss[:, 2:3], g2[:], ALU.mult, ALU.mult, reads=[("x1", i % 3), "ss2", "g2"], writes=[("h2", s)])
            yield
            pg.cp("pool", h2b[s][:], h2[s][:], reads=[("h2", s)], writes=[("h2b", s)])
            yield
            for j in range(8):
                pg.tr(psT[:, j, :], h2[s][:, j * 128:(j + 1) * 128], idf[:], reads=[("h2", s), "idf"], writes=["psT"])
                yield
            pg.cp("act", h2T[:], psT[:], reads=["psT"], writes=["h2T"])
            yield
            for cg in range(4):
                bk = psQ[cg % 2]
                for cc in range(4):
                    c = cg * 4 + cc
                    for kc in range(8):
                        pg.mm(bk[:, cc * 128:(cc + 1) * 128], wq[:, kc, c * 128:(c + 1) * 128], h2T[:, kc, :], start=(kc == 0), stop=(kc == 7),
                              reads=["wq", "h2T"], writes=[("psQ", cg % 2)])
                        yield
                pg.cp("act" if cg % 2 == 0 else "dve", qT[:, cg * 4:(cg + 1) * 4, :].rearrange("p a b -> p (a b)"), bk[:],
                      reads=[("psQ", cg % 2)], writes=[("qT", cg)])
                yield
            for cg in range(4):
                bk = psQ[cg % 2]
                for cc in range(4):
                    c = cg * 4 + cc
                    pg.mm(bk[:, cc * 128:(cc + 1) * 128], qT[:, c, :], kT[:, c % 2, :], reads=[("qT", cg), "kT"], writes=[("psQ", cg % 2)])
                    yield
                pg.cp("act" if cg % 2 == 0 else "dve", sc[:, cg * 4:(cg + 1) * 4, :].rearrange("p a b -> p (a b)"), bk[:],
                      reads=[("psQ", cg % 2)], writes=[("sc", cg)])
                yield
            for c in range(16):
                k_ = ("sc", c // 4)
                pg.op("dve", lambda e, c=c: e.max(tv[:, c, 0:8], sc[:, c, :]), reads=[k_], writes=[("tv", c)])
                yield
                pg.op("dve", lambda e, c=c: e.max_index(tiu[:, c, 0:8], tv[:, c, 0:8], sc[:, c, :]), reads=[k_, ("tv", c)], writes=[("tiu", c)])
                yield
                pg.op("dve", lambda e, c=c: e.match_replace(work[:, 0:128], tv[:, c, 0:8], sc[:, c, :], -1e30), reads=[k_, ("tv", c), "work"], writes=["work"])
                yield
                pg.op("dve", lambda e, c=c: e.max(tv[:, c, 8:16], work[:, 0:128]), reads=["work"], writes=[("tv2", c)])
                yield
                pg.op("dve", lambda e, c=c: e.max_index(tiu[:, c, 8:16], tv[:, c, 8:16], sc[:, c, :]), reads=[k_, ("tv2", c)], writes=[("tiu2", c)])
                yield
            allt = [("tv", c) for c in range(16)] + [("tv2", c) for c in range(16)]
            alli = [("tiu", c) for c in range(16)] + [("tiu2", c) for c in range(16)]
            pg.cp("dve", ti[:], tiu[:], reads=alli, writes=["ti"])
            yield
            tv4 = tv[:].rearrange("p (h f) a -> p h f a", f=2)
            ti4 = ti[:].rearrange("p (h f) a -> p h f a", f=2)
            cs4 = cs[:].rearrange("p h (a b) -> p h a b", a=16)
            A_ = lambda t4: t4[:, :, 0, :].unsqueeze(3).to_broadcast([128, 8, 16, 16])
            B_ = lambda t4: t4[:, :, 1, :].unsqueeze(2).to_broadcast([128, 8, 16, 16])
            pg.tt("dve", cs4, A_(tv4), B_(tv4), ALU.add, reads=allt, writes=["cs"])
            yield
            for h in range(8):
                pg.op("dve", lambda e, h=h: e.max(bs[:, h, 0:8], cs[:, h, :]), reads=["cs"], writes=[("bs", h)])
                yield
                pg.op("dve", lambda e, h=h: e.max_index(posu[:, h, 0:8], bs[:, h, 0:8], cs[:, h, :]), reads=["cs", ("bs", h)], writes=[("posu", h)])
                yield
                pg.op("dve", lambda e, h=h: e.match_replace(work[:, :], bs[:, h, 0:8], cs[:, h, :], -1e30), reads=["cs", ("bs", h), "work"], writes=["work"])
                yield
                pg.op("dve", lambda e, h=h: e.max(bs[:, h, 8:16], work[:, :]), reads=["work"], writes=[("bs2", h)])
                yield
                pg.op("dve", lambda e, h=h: e.max_index(posu[:, h, 8:16], bs[:, h, 8:16], cs[:, h, :]), reads=["cs", ("bs2", h)], writes=[("posu2", h)])
                yield
            allb = [("bs", h) for h in range(8)] + [("bs2", h) for h in range(8)]
            allp = [("posu", h) for h in range(8)] + [("posu2", h) for h in range(8)]
            G = gate[s][:].rearrange("p (h j) -> p h j", h=8)
            pg.tt("dve", G, bs[:], bs[:, :, 0:1].to_broadcast([128, 8, 16]), ALU.subtract, reads=allb, writes=[("gate", s)])
            yield
            pg.act(G, G, AF.Exp, reads=[("gate", s)], writes=[("gate", s)])
            yield
            pg.red("dve", zz[:, 0:8], G, ALU.add, reads=[("gate", s)], writes=["zz0"])
            yield
            pg.op("dve", lambda e: e.reciprocal(zz[:, 8:16], zz[:, 0:8]), reads=["zz0"], writes=["zz1"])
            yield
            pg.tt("dve", G, G, zz[:, 8:16].unsqueeze(2).to_broadcast([128, 8, 16]), ALU.mult, reads=[("gate", s), "zz1"], writes=[("gate", s)])
            yield
            pg.ts("dve", pa_u[:], posu[:], 4, None, ALU.logical_shift_right, reads=allp, writes=["pau"])
            yield
            pg.ts("dve", pb_u[:], posu[:], 15, None, ALU.bitwise_and, reads=allp, writes=["pbu"])
            yield
            pg.cp("dve", pa[:], pa_u[:], reads=["pau"], writes=["pa"])
            yield
            pg.cp("dve", pb[:], pb_u[:], reads=["pbu"], writes=["pb"])
            yield
            iob = io16[:, :].unsqueeze(1).unsqueeze(1).to_broadcast([128, 8, 16, 16])
            for (pp, key, half, dst, dk_) in ((pa, "pa", 0, ia, "ia"), (pb, "pb", 1, ib, "ib")):
                pg.tt("dve", oh[:], pp[:].unsqueeze(3).to_broadcast([128, 8, 16, 16]), iob, ALU.is_equal, reads=[key, "io16", "oh"], writes=["oh"])
                yield
                tsel = ti4[:, :, half, :].unsqueeze(2).to_broadcast([128, 8, 16, 16])
                pg.tt("dve", oh[:], oh[:], tsel, ALU.mult, reads=["oh", "ti"], writes=["oh"])
                yield
                pg.red("dve", dst[:], oh[:], ALU.add, reads=["oh"], writes=[dk_])
                yield
            pg.stt("dve", eidf[:].rearrange("p (h j) -> p h j", h=8), ia[:], 128.0, ib[:], ALU.mult, ALU.add, reads=["ia", "ib"], writes=["eidf"])
            yield
            pg.cp("dve", eidi[i % 3][:], eidf[:], reads=["eidf"], writes=[("eidi", i % 3)])
            yield

        GS = 2
        SK = 1

        def gstep(i, e_):
            s = i % 2
            b_ = e_ % NB
            pg.dma("pool", lambda e, e_=e_, b_=b_, i=i: e.indirect_dma_start(
                out=uv[b_][:, :], out_offset=None, in_=dr["UV"][:, :],
                in_offset=bass.IndirectOffsetOnAxis(ap=eidi[i % 3][:, e_:e_ + 1], axis=0)),
                reads=[("eidi", i % 3)], writes=[("uv", b_)])
            pg.op("dve", lambda e, e_=e_, b_=b_, s=s: e.scalar_tensor_tensor(junk[:], uv[b_][:, 0:D], 1.0, h2b[s][:], ALU.mult, ALU.mult,
                                                                              accum_out=actv[s][:, e_:e_ + 1]),
                  reads=[("uv", b_), ("h2b", s)], writes=[("act", s, e_)])

        def gelu_grp(i, k):
            s = i % 2
            sl = slice(k * GS, (k + 1) * GS)
            pg.act(ga[s][:, sl], actv[s][:, sl], AF.Gelu, reads=[("act", s, e_) for e_ in range(k * GS, (k + 1) * GS)], writes=[("ga", s, k)])

        def fin_grp(i, k):
            s = i % 2
            sl = slice(k * GS, (k + 1) * GS)
            pg.tt("dve", ga[s][:, sl], ga[s][:, sl], gate[s][:, sl], ALU.mult, reads=[("ga", s, k), ("gate", s)], writes=[("ga", s, k)])
            for e_ in range(k * GS, (k + 1) * GS):
                b_ = e_ % NB
                d_ = e_ % 4
                pg.act(dg[d_][:], idb[:], AF.Copy, reads=[("ga", s, k), "idb"], writes=[("dg", d_)], scale=ga[s][:, e_:e_ + 1])
                for n_ in range(2):
                    pg.mm(psY[n_][:], dg[d_][:], uv[b_][:, D + n_ * 512:D + (n_ + 1) * 512], start=(e_ == 0), stop=(e_ == 127),
                          reads=[("dg", d_), ("uv", b_)], writes=[("psY", n_)])

        def tail(i):
            s = i % 2
            t0 = i * 128
            for n_ in range(2):
                pg.tt("dve", yo[0][:, n_ * 512:(n_ + 1) * 512], psY[n_][:], x1[i % 3][:, n_ * 512:(n_ + 1) * 512], ALU.add,
                      reads=[("psY", n_), ("x1", i % 3)], writes=[("yo", 0, n_)])
            pg.ld(dr["out"][t0:t0 + 128, :], yo[0][:], reads=[("yo", 0, 0), ("yo", 0, 1)], writes=[("out", i)])

        def drain(g, n=None):
            k = 0
            while g is not None and (n is None or k < n):
                try:
                    next(g)
                except StopIteration:
                    return None
                k += 1
            return g

        drain(front(0))
        for i in range(ntiles):
            gen2 = front(i + 1) if i + 1 < ntiles else None
            for k in range(128 // GS):
                for e_ in range(k * GS, (k + 1) * GS):
                    gstep(i, e_)
                    gen2 = drain(gen2, 3)
                gelu_grp(i, k)
                if k >= SK:
                    fin_grp(i, k - SK)
            for k in range(128 // GS - SK, 128 // GS):
                fin_grp(i, k)
            drain(gen2)
            tail(i)
        pg.barrier()
        pg.emit()


_NC_CACHE = {}


def kernel(**inputs):
    inputs = {k: np.asarray(v) for k, v in inputs.items()}
    ntl = NT // 2
    if "nc" not in _NC_CACHE:
        _NC_CACHE["nc"] = build([stage1, stage2a, stage2x, stage2c, stage3, stage4, stage5], peer_tiles=ntl)
    nc = _NC_CACHE["nc"]
    base = {}
    in_maps = []
    for c in range(8):
        b, hh = c % 4, c // 4
        if hh not in base:
            base[hh] = host_inputs(inputs, b, hh, ntl)
            m = base[hh]
        else:
            m = dict(base[hh])
            m["x"] = core_x(inputs, b, hh)
        in_maps.append(m)
    res = run_bass_kernel_spmd(nc, in_maps, core_ids=list(range(8)))
    out = np.zeros((4, T, D), np.float32)
    for c in range(8):
        b, hh = c % 4, c // 4
        out[b, hh * ntl * 128:(hh + 1) * ntl * 128, :] = res.results[c]["out"]
    return out


def stage2x(C):
    nc, pg, dr = C.nc, C.pg, C.dr
    with ExitStack() as st:
        sb, ps = _mk(C, st)
        idf = sb("x_idf", [128, 128], F32)
        tri = sb("x_tri", [128, 128], F32)
        msk = sb("x_msk", [128, 3, 128], F32)
        ones = sb("x_ones", [128, 1], F32)
        inp = [[sb(f"x_in{s}_{j}", [128, 512], F32) for j in range(6)] for s in range(2)]
        Pt = sb("x_P", [128, 512], F32)
        iP = sb("x_iP", [128, 512], F32)
        Pp = sb("x_Pp", [128, 512], F32)
        tm = [[sb(f"x_tm{s}_{j}", [128, 512], F32) for j in range(4)] for s in range(2)]
        fm = [[sb(f"x_fm{s}_{j}", [64, 8, 128], F32) for j in range(4)] for s in range(2)]
        M = [[sb(f"x_M{s}_{j}", [128, 8, 128], (BF16 if j in (0, 4) else F32)) for j in range(5)] for s in range(2)]
        Xb = sb("x_Xb", [128, 8, 128], BF16)
        idb = sb("x_idb", [128, 128], BF16)
        X = [sb(f"x_X{s}", [128, 8, 128], F32) for s in range(2)]
        PC = [sb(f"x_PC{s}", [64, 8], F32) for s in range(2)]
        N2 = [sb(f"x_N2_{j}", [128, 8, 128], BF16) for j in range(2)]
        N2T = [sb(f"x_N2T_{j}", [128, 8, 128], BF16) for j in range(2)]
        Z = [sb(f"x_Z{j}", [64, 512], F32) for j in range(2)]
        rhs_sb = sb("x_rhs", [128, 512], F32)
        U_sb = sb("x_U", [128, 512], F32)
        Y_sb = [sb(f"x_Y{j}", [128, 512], F32) for j in range(2)]
        bank = [ps(f"x_bank{j}", [128, 512], F32) for j in range(8)]

        pg.ld(idf[:], dr["ident"][:, :], writes=["idf"])
        pg.ld(tri[:], dr["c_tri"][:, :], writes=["tri"])
        pg.ld(msk[:], dr["c_msk"][:, :, :], writes=["msk"])
        pg.memset("dve", ones[:], 1.0, writes=["ones"])
        pg.cp("dve", idb[:], idf[:], reads=["idf"], writes=["idb"])
        pg.memset("dve", Z[0][:], 0.0, writes=[("Z", 0)])
        names = ("RR", "RKK", "RLW", "RB", "RKp", "RV")
        bk = [0]

        def nb():
            v = bk[0]
            bk[0] = (v + 1) % 8
            return v

        def pre(c):
            s = c % 2
            t0 = c * 128
            I = inp[s]
            for j, nm in enumerate(names):
                pg.ld(I[j][:], dr[nm][t0:t0 + 128, :], reads=[(nm, c)], writes=[("in", s, j)])
            r_, kkn, lw, b_, kp, v_ = [t_[:] for t_ in I]
            bL = nb()
            pg.mm(bank[bL][:], tri[:], lw, reads=["tri", ("in", s, 2)], writes=[("bank", bL)])
            bC = nb()
            for h in range(8):
                pg.mm(bank[bC][0:64, h:h + 1], I[2][:, h * 64:(h + 1) * 64], ones[:, 0:1], reads=[("in", s, 2), "ones"], writes=[("bank", bC)])
            pg.act(PC[s][:], bank[bC][0:64, 0:8], AF.Exp, reads=[("bank", bC)], writes=[("PC", s)])
            pg.act(Pt[:], bank[bL][:], AF.Exp, reads=[("bank", bL)], writes=["P"])
            pg.act(iP[:], bank[bL][:], AF.Exp, reads=[("bank", bL)], writes=["iP"], scale=-1.0)
            pg.tt("dve", Pp[:], bank[bL][:], lw, ALU.subtract, reads=[("bank", bL), ("in", s, 2)], writes=["Pp"])
            pg.act(Pp[:], Pp[:], AF.Exp, reads=["Pp"], writes=["Pp"])
            TM = tm[s]
            pg.tt("pool", TM[0][:], r_, Pt[:], ALU.mult, reads=[("in", s, 0), "P"], writes=[("tm", s, 0)])
            pg.stt("dve", TM[1][:], kkn, -1.0, Pp[:], ALU.mult, ALU.mult, reads=[("in", s, 1), "Pp"], writes=[("tm", s, 1)])
            pg.tt("pool", TM[2][:], b_, iP[:], ALU.mult, reads=[("in", s, 3), "iP"], writes=[("tm", s, 2)])
            pg.tt("dve", TM[3][:], kp, iP[:], ALU.mult, reads=[("in", s, 4), "iP"], writes=[("tm", s, 3)])
            for j in range(4):
                if j == 0 and c < NT // 2:
                    continue
                for hg in range(2):
                    bT = nb()
                    for hh in range(4):
                        h = hg * 4 + hh
                        pg.tr(bank[bT][0:64, hh * 128:(hh + 1) * 128], TM[j][:, h * 64:(h + 1) * 64], idf[:], reads=[("tm", s, j), "idf"], writes=[("bank", bT)])
                    pg.cp("act" if (j + hg) % 2 == 0 else "dve", fm[s][j][:, hg * 4:(hg + 1) * 4, :].rearrange("p a b -> p (a b)"), bank[bT][0:64, :],
                          reads=[("bank", bT)], writes=[("fm", s, j, hg)])
            FR, FKK, FB, FK = fm[s]
            combos = ((0, FB, 2, FKK, 1, 0), (1, FK, 3, FKK, 1, 0), (2, FB, 2, FR, 0, 1), (3, FK, 3, FR, 0, 1), (4, FKK, 1, FB, 2, 2))
            for hg in range(2):
                for (mi, L_, lj, R_, rj, mk) in combos:
                    if mi in (2, 3) and c < NT // 2:
                        continue
                    bM = nb()
                    for hh in range(4):
                        h = hg * 4 + hh
                        pg.mm(bank[bM][:, hh * 128:(hh + 1) * 128], L_[:, h, :], R_[:, h, :], reads=[("fm", s, lj, hg), ("fm", s, rj, hg)], writes=[("bank", bM)])
                    pg.tt("dve", M[s][mi][:, hg * 4:(hg + 1) * 4, :], bank[bM][:].rearrange("p (a b) -> p a b", a=4),
                          msk[:, mk, :].unsqueeze(1).to_broadcast([128, 4, 128]), ALU.mult, reads=[("bank", bM), "msk"], writes=[("M", s, mi, hg)])
                pg.tt("pool", Xb[:, hg * 4:(hg + 1) * 4, :], idb[:, :].unsqueeze(1).to_broadcast([128, 4, 128]), M[s][0][:, hg * 4:(hg + 1) * 4, :], ALU.subtract,
                      reads=[("M", s, 0, hg), "idb"], writes=[("Xb", hg)])
            curN = [M[s][0], M[s][0]]
            curNT = [M[s][4], M[s][4]]
            kN = [("M", s, 0, 0), ("M", s, 0, 1)]
            kNT = [("M", s, 4, 0), ("M", s, 4, 1)]
            for j in range(6):
                dst = j % 2
                for hg in range(2):
                    b1, b2 = nb(), nb()
                    for hh in range(4):
                        h = hg * 4 + hh
                        pg.mm(bank[b1][:, hh * 128:(hh + 1) * 128], curNT[hg][:, h, :], curN[hg][:, h, :], reads=[kN[hg], kNT[hg]], writes=[("bank", b1)])
                    for hh in range(4):
                        h = hg * 4 + hh
                        pg.mm(bank[b2][:, hh * 128:(hh + 1) * 128], curN[hg][:, h, :], curNT[hg][:, h, :], reads=[kN[hg], kNT[hg]], writes=[("bank", b2)])
                    pg.cp("act", N2[dst][:, hg * 4:(hg + 1) * 4, :].rearrange("p a b -> p (a b)"), bank[b1][:], reads=[("bank", b1)], writes=[("N2", dst, hg)])
                    pg.cp("dve", N2T[dst][:, hg * 4:(hg + 1) * 4, :].rearrange("p a b -> p (a b)"), bank[b2][:], reads=[("bank", b2)], writes=[("N2T", dst, hg)])
                for hg in range(2):
                    curN[hg], curNT[hg] = N2[dst], N2T[dst]
                    kN[hg], kNT[hg] = ("N2", dst, hg), ("N2T", dst, hg)
                for hg in range(2):
                    b3 = nb()
                    for hh in range(4):
                        h = hg * 4 + hh
                        pg.mm(bank[b3][:, hh * 128:(hh + 1) * 128], curNT[hg][:, h, :], Xb[:, h, :], reads=[kNT[hg], ("Xb", hg)], writes=[("bank", b3)])
                    xo = (X[s] if j == 5 else Xb)
                    pg.tt("dve", xo[:, hg * 4:(hg + 1) * 4, :].rearrange("p a b -> p (a b)"), Xb[:, hg * 4:(hg + 1) * 4, :].rearrange("p a b -> p (a b)"), bank[b3][:], ALU.add,
                          reads=[("bank", b3), ("Xb", hg)], writes=[("X", s, hg)] if j == 5 else [("Xb", hg)])

        def seq(c):
            s = c % 2
            t0 = c * 128
            zc, zn = Z[c % 2], Z[(c + 1) % 2]
            kz, kzn = ("Z", c % 2), ("Z", (c + 1) % 2)
            FR, FKK, FB, FK = fm[s]
            V = inp[s][5]
            hsl = lambda h: slice(h * 64, (h + 1) * 64)
            Mk = lambda mi: [("M", s, mi, 0), ("M", s, mi, 1)]
            fk = lambda j: [("fm", s, j, 0), ("fm", s, j, 1)]
            Xk = [("X", s, 0), ("X", s, 1)]
            bG = nb()
            for h in range(8):
                pg.mm(bank[bG][:, hsl(h)], M[s][1][:, h, :], V[:, hsl(h)], start=True, stop=False, reads=Mk(1) + [("in", s, 5)], writes=[("bank", bG)])
                pg.mm(bank[bG][:, hsl(h)], FKK[:, h, :], zc[:, hsl(h)], start=False, stop=True, reads=fk(1) + [kz], writes=[("bank", bG)])
            pg.ts("dve", rhs_sb[:], bank[bG][:], -1.0, None, ALU.mult, reads=[("bank", bG)], writes=["rhs"])
            bU = nb()
            for h in range(8):
                pg.mm(bank[bU][:, hsl(h)], X[s][:, h, :], rhs_sb[:, hsl(h)], reads=Xk + ["rhs"], writes=[("bank", bU)])
            pg.cp("act", U_sb[:], bank[bU][:], reads=[("bank", bU)], writes=["U"])
            bZ = nb()
            for h in range(8):
                pg.mm(bank[bZ][0:64, hsl(h)], tm[s][3][:, hsl(h)], V[:, hsl(h)], start=True, stop=False, reads=[("tm", s, 3), ("in", s, 5)], writes=[("bank", bZ)])
                pg.mm(bank[bZ][0:64, hsl(h)], idf[0:64, 0:64], zc[:, hsl(h)], start=False, stop=False, reads=["idf", kz], writes=[("bank", bZ)])
                pg.mm(bank[bZ][0:64, hsl(h)], tm[s][2][:, hsl(h)], U_sb[:, hsl(h)], start=False, stop=True, reads=[("tm", s, 2), "U"], writes=[("bank", bZ)])
            pg.tt("dve", zn[:].rearrange("p (h v) -> p h v", h=8), bank[bZ][0:64, :].rearrange("p (h v) -> p h v", h=8),
                  PC[s][:, :].unsqueeze(2).to_broadcast([64, 8, 64]), ALU.mult, reads=[("bank", bZ), ("PC", s)], writes=[kzn])
            if c < NT // 2:
                return
            bY = nb()
            for h in range(8):
                pg.mm(bank[bY][:, hsl(h)], M[s][3][:, h, :], V[:, hsl(h)], start=True, stop=False, reads=Mk(3) + [("in", s, 5)], writes=[("bank", bY)])
                pg.mm(bank[bY][:, hsl(h)], FR[:, h, :], zc[:, hsl(h)], start=False, stop=False, reads=fk(0) + [kz], writes=[("bank", bY)])
                pg.mm(bank[bY][:, hsl(h)], M[s][2][:, h, :], U_sb[:, hsl(h)], start=False, stop=True, reads=Mk(2) + ["U"], writes=[("bank", bY)])
            pg.cp("act", Y_sb[s][:], bank[bY][:], reads=[("bank", bY)], writes=[("Y", s)])
            pg.ld(dr["YS"][t0:t0 + 128, :], Y_sb[s][:], reads=[("Y", s)], writes=[("YS", c)])

        tcg = table_conv_gen(C, sb)
        pre(0)
        for c in range(NT):
            if c + 1 < NT:
                pre(c + 1)
            for _ in range(8):
                next(tcg, None)
            seq(c)
        for _ in tcg:
            pass
        pg.barrier()
        pg.emit()
```

```python
import numpy as np
import concourse.bass as bass
import concourse.mybir as mybir

F32 = mybir.dt.float32
BF16 = mybir.dt.bfloat16
I32 = mybir.dt.int32
U32 = mybir.dt.uint32
ALU = mybir.AluOpType
AF = mybir.ActivationFunctionType
AX = mybir.AxisListType

EPOCH = 20000
ENGS = ("pe", "act", "dve", "pool", "sp")
NDMASEM = 16


class Prog:
    def __init__(self, nc, stack):
        self.nc = nc
        self.stack = stack
        self.ops = {e: [] for e in ENGS}
        self.cnt = {e: 0 for e in ENGS}
        self.esems = {e: [] for e in ENGS}
        self.waited = {e: {} for e in ENGS}
        self.lastw = {}
        self.readers = {}
        self.dsems = {}
        self.dcount = {}
        self.dtarget = {}
        self.semobjs = {}
        self.alltokens = {}
        for q in ("sp", "act", "pool"):
            self.dsems[q] = [self._newsem(f"d_{q}_{i}") for i in range(NDMASEM)]
            self.dcount[q] = 0
            self.dtarget[q] = [0] * NDMASEM

    def _newsem(self, name):
        s = self.stack.enter_context(self.nc.semaphore(name))
        self.semobjs[name] = s
        return name

    def _esem(self, e, idx):
        ep = idx // EPOCH
        while len(self.esems[e]) <= ep:
            self.esems[e].append(self._newsem(f"e_{e}_{len(self.esems[e])}"))
        return self.esems[e][ep], (idx % EPOCH) + 1

    def _deps(self, reads, writes):
        toks = []
        for k in reads:
            t = self.lastw.get(k)
            if t is not None:
                toks.append(t)
        for k in writes:
            t = self.lastw.get(k)
            if t is not None:
                toks.append(t)
            toks.extend(self.readers.get(k, ()))
        return toks

    def _commit(self, tok, reads, writes):
        for k in reads:
            self.readers.setdefault(k, []).append(tok)
        for k in writes:
            self.lastw[k] = tok
            self.readers[k] = []
        self.alltokens[tok[0]] = max(self.alltokens.get(tok[0], 0), tok[1])

    def _waits(self, e, toks):
        need = {}
        for (s, v) in toks:
            if v > need.get(s, 0):
                need[s] = v
        out = []
        w = self.waited[e]
        for s, v in need.items():
            if w.get(s, 0) < v:
                w[s] = v
                out.append((s, v))
        return out

    def op(self, e, fn, reads=(), writes=()):
        toks = self._deps(reads, writes)
        if e == "pe":
            toks = [t for t in toks if not t[0].startswith("e_pe_")]
        waits = self._waits(e, toks)
        idx = self.cnt[e]
        self.cnt[e] += 1
        tok = self._esem(e, idx)
        self.ops[e].append((waits, fn, (tok[0], 1)))
        self._commit(tok, reads, writes)

    def dma(self, q, fn, reads=(), writes=()):
        toks = self._deps(reads, writes)
        n = self.dcount[q]
        self.dcount[q] += 1
        slot = n % NDMASEM
        sname = self.dsems[q][slot]
        prev = self.dtarget[q][slot]
        if prev > 0:
            toks.append((sname, prev))
        tgt = prev + 16
        self.dtarget[q][slot] = tgt
        waits = self._waits(q, toks)
        tok = (sname, tgt)
        self.ops[q].append((waits, fn, (sname, 16)))
        self._commit(tok, reads, writes)

    def mm(self, out, lhsT, rhs, start=True, stop=True, reads=(), writes=()):
        self.op("pe", lambda e: e.matmul(out, lhsT, rhs, start=start, stop=stop), reads, writes)

    def tr(self, out, in_, ident, reads=(), writes=()):
        self.op("pe", lambda e: e.transpose(out, in_, ident), reads, writes)

    def act(self, out, in_, func, reads=(), writes=(), bias=None, scale=None, eng="act"):
        kw = {}
        if bias is not None:
            kw["bias"] = bias
        if scale is not None:
            kw["scale"] = scale
        self.op(eng, lambda e: e.activation(out, in_, func, **kw), reads, writes)

    def tt(self, eng, out, in0, in1, op, reads=(), writes=()):
        self.op(eng, lambda e: e.tensor_tensor(out, in0, in1, op), reads, writes)

    def ts(self, eng, out, in0, s1, s2, op0, op1=None, reads=(), writes=()):
        if op1 is None:
            self.op(eng, lambda e: e.tensor_scalar(out, in0, s1, s2, op0), reads, writes)
        else:
            self.op(eng, lambda e: e.tensor_scalar(out, in0, s1, s2, op0, op1), reads, writes)

    def stt(self, eng, out, in0, scalar, in1, op0, op1, reads=(), writes=()):
        self.op(eng, lambda e: e.scalar_tensor_tensor(out, in0, scalar, in1, op0, op1), reads, writes)

    def cp(self, eng, out, in_, reads=(), writes=()):
        if eng == "act":
            self.op(eng, lambda e: e.copy(out, in_), reads, writes)
        else:
            self.op(eng, lambda e: e.tensor_copy(out, in_), reads, writes)

    def red(self, eng, out, in_, op, reads=(), writes=(), axis=None):
        ax = AX.X if axis is None else axis
        self.op(eng, lambda e: e.tensor_reduce(out, in_, ax, op), reads, writes)

    def memset(self, eng, ap, val, writes=()):
        self.op(eng, lambda e: e.memset(ap, val), (), writes)

    def ld(self, out, in_, reads=(), writes=(), q="sp"):
        self.dma(q, lambda e: e.dma_start(out, in_), reads, writes)

    def barrier(self):
        toks = list(self.alltokens.items())
        for e in ENGS:
            waits = self._waits(e, toks)
            if waits:
                self.ops[e].append((waits, None, None))
        self.lastw = {}
        self.readers = {}

    def emit(self):
        nc = self.nc
        so = self.semobjs
        with nc.Block() as block:
            def mk(e):
                def body(eng):
                    for waits, fn, inc in self.ops[e]:
                        for (s, v) in waits:
                            eng.wait_ge(so[s], v)
                        if fn is not None:
                            ins = fn(eng)
                            ins.then_inc(so[inc[0]], inc[1])
                return body
            block.tensor(mk("pe"))
            block.scalar(mk("act"))
            block.vector(mk("dve"))
            block.gpsimd(mk("pool"))
            block.sync(mk("sp"))
        self.ops = {e: [] for e in ENGS}
from contextlib import ExitStack
from concourse.bass_utils import run_bass_kernel_spmd

T = 4096
D = 1024
NT = T // 128
INW = 5144
RWC = 1792
O_RW = 0
O_Q = 1792
O_KC = 2304
O_VC = 2432
O_KS = 2560
O_VS = 2688
O_KW = 2816
O_VW = 2944
O_BG = 3072
O_GA = 3096
O_GB = 4120


class Ctx:
    pass


def _mk(C, st):
    nc = C.nc
    sb = lambda name, shape, dt: st.enter_context(nc.sbuf_tensor(name, shape, dt))
    ps = lambda name, shape, dt: st.enter_context(nc.psum_tensor(name, shape, dt))
    return sb, ps


def stage1(C):
    nc, pg, dr = C.nc, C.pg, C.dr
    with ExitStack() as st:
        sb, ps = _mk(C, st)
        win = sb("s1_win", [128, 8, INW], BF16)
        pj = [sb(f"s1_pj{i}", [128, INW], F32) for i in range(2)]
        xt = [sb(f"s1_xt{i}", [128, D], F32) for i in range(2)]
        junk = sb("s1_junk", [128, D], F32)
        hb = [sb(f"s1_h{i}", [128, D], BF16) for i in range(2)]
        hT = [sb(f"s1_hT{i}", [128, 8, 128], BF16) for i in range(2)]
        gt = sb("s1_g", [128, D], F32)
        idf = sb("s1_idf", [128, 128], F32)
        idb = sb("s1_idb", [128, 128], BF16)
        ss = [sb(f"s1_ss{i}", [128, 4], F32) for i in range(2)]
        psT = [ps(f"s1_psT{i}", [128, 8, 128], BF16) for i in range(2)]
        psm = [ps(f"s1_psm{i}", [128, 512], F32) for i in range(4)]

        pg.ld(gt[:], dr["norm1_g_b"][:, :], writes=["gt"])
        pg.ld(idf[:], dr["ident"][:, :], writes=["idf"])
        pg.cp("dve", idb[:], idf[:], reads=["idf"], writes=["idb"])
        engs = ["act", "dve", "pool"]
        for kc in range(8):
            b = pj[kc % 2]
            pg.ld(b[:], dr["w_in"][kc * 128:(kc + 1) * 128, :], writes=[("pjall", kc % 2)])
            pg.cp(engs[kc % 3], win[:, kc, :], b[:], reads=[("pjall", kc % 2)], writes=[("win", kc)])
        winkeys = [("win", kc) for kc in range(8)]
        chunks = []
        c0 = 0
        while c0 < INW:
            w = min(512, INW - c0)
            chunks.append((c0, w))
            c0 += w
        def A1(i):
            s = i % 2
            pg.ld(xt[s][:], dr["x"][i * 128:(i + 1) * 128, :], writes=[("xt", s)])
            pg.tt("dve", junk[:], xt[s][:], xt[s][:], ALU.mult, reads=[("xt", s)], writes=["junk"])
            pg.red("dve", ss[s][:, 0:1], junk[:], ALU.add, reads=["junk"], writes=[("ss", s)])
            pg.act(ss[s][:, 1:2], ss[s][:, 0:1], AF.Sqrt, reads=[("ss", s)], writes=[("ss1", s)],
                   scale=1.0 / D, bias=C.eps6[:, 0:1])
            pg.op("dve", lambda e, o=ss[s][:, 2:3], a=ss[s][:, 1:2]: e.reciprocal(o, a),
                  reads=[("ss1", s)], writes=[("ss2", s)])
            pg.stt("dve", hb[s][:], xt[s][:], ss[s][:, 2:3], gt[:], ALU.mult, ALU.mult,
                   reads=[("xt", s), ("ss2", s), "gt"], writes=[("hb", s)])

        def A2(i):
            s = i % 2
            for j in range(8):
                pg.tr(psT[s][:, j, :], hb[s][:, j * 128:(j + 1) * 128], idb[:],
                      reads=[("hb", s), "idb"], writes=[("psT", s)])
            pg.cp("act", hT[s][:], psT[s][:], reads=[("psT", s)], writes=[("hT", s)])

        chunks_lo = [(512, 512), (1024, 512), (1536, 128), (2304, 512), (2816, 256)]

        def B(i, lo, hi):
            s = i % 2
            if i < NT // 2 - 1:
                if lo != 0:
                    return
                for ci, (c0, w) in enumerate(chunks_lo):
                    pb = psm[ci % 4]
                    for kc in range(8):
                        pg.mm(pb[:, :w], hT[s][:, kc, :], win[:, kc, c0:c0 + w], start=(kc == 0), stop=(kc == 7),
                              reads=[("hT", s), ("win", kc)], writes=[("psm", ci % 4)])
                    pg.cp("act" if ci % 2 == 0 else "dve", pj[s][:, c0:c0 + w], pb[:, :w],
                          reads=[("psm", ci % 4)], writes=[("pj", s, k_) for k_ in range(len(chunks))] + ([("pjall", s)] if i < 8 else []))
                return
            for ci in range(lo, hi):
                c0, w = chunks[ci]
                pb = psm[ci % 4]
                for kc in range(8):
                    pg.mm(pb[:, :w], hT[s][:, kc, :], win[:, kc, c0:c0 + w], start=(kc == 0), stop=(kc == 7),
                          reads=[("hT", s), ("win", kc)], writes=[("psm", ci % 4)])
                pg.cp("act" if ci % 2 == 0 else "dve", pj[s][:, c0:c0 + w], pb[:, :w],
                      reads=[("psm", ci % 4)], writes=[("pj", s, ci), ("pjall", s)] if i < 8 else [("pj", s, ci)])

        def S(i):
            s = i % 2
            pg.ld(dr["P"][i * 128:(i + 1) * 128, 0:3096], pj[s][:, 0:3096],
                  reads=[("pj", s, ci) for ci in range(len(chunks))], writes=[("P", i)])
            pg.ld(dr["PG"][i * 128:(i + 1) * 128, :], pj[s][:, 3096:5144],
                  reads=[("pj", s, ci) for ci in range(len(chunks))], writes=[("PG", i)], q="act")

        A1(0)
        A2(0)
        A1(1)
        for i in range(NT):
            B(i, 0, 6)
            if i + 1 < NT:
                A2(i + 1)
            if i + 2 < NT:
                A1(i + 2)
            B(i, 6, len(chunks))
            S(i)
        pg.barrier()
        pg.emit()


def build(stages, dbg_out=(), dbg_in=(), lvl=9, sub=9, peer_tiles=NT):
    nc = bass.Bass("TRN2", target_bir_lowering=False)
    C = Ctx()
    C.peer_tiles = peer_tiles
    C.lvl = lvl
    C.sub = sub
    C.nc = nc
    dr = {}
    C.dr = dr

    def din(name, shape, dt=F32):
        dr[name] = nc.dram_tensor(name, list(shape), dt, kind="ExternalInput").ap()

    def dscr(name, shape, dt=F32):
        kind = "ExternalOutput" if name in dbg_out else ("ExternalInput" if name in dbg_in else "Internal")
        dr[name] = nc.dram_tensor(name, list(shape), dt, kind=kind).ap()

    din("x", [T, D])
    din("norm1_g_b", [128, D])
    din("ident", [128, 128])
    din("w_in", [D, INW])
    dscr("P", [T, INW])
    dscr("PG", [T, 2048])
    for nm in ("rw_mu_b",):
        din(nm, [128, RWC])
    for nm in ("rw_w0_b", "rw_a0_b", "rw_k_k_b", "rw_k_a_b", "rw_r_k_b", "rw_ln_w_b", "rw_ln_b_b", "rw_g_up"):
        din(nm, [128, 512])
    din("rw_w_up", [64, 512])
    din("rw_a_up", [64, 512])
    for nm in ("RB", "RKp", "RV", "RG", "YA", "RR", "RKK", "RLW", "YS"):
        dscr(nm, [T, 512])
    dscr("RBON", [T, 8])
    din("blkmask", [8, 512])
    din("c_tri", [128, 128])
    din("c_msk", [128, 3, 128])
    din("nsa_gains_b", [128, 768])
    din("nsa_kc_g_b", [128, 64])
    din("ovl", [128, 2, 64])
    din("posT", [128, 2, 32])
    din("cmp_w2", [128, 2, 2, 64])
    din("cmp_w1", [2, 128, 32, 256])
    din("c_cmpb", [128, 2, T], BF16)
    din("c_esel", [128, 32, 128], BF16)
    din("c_causb", [128, 4, 512], BF16)
    din("c_winb", [128, 8, 512], BF16)
    din("c_winb4", [128, 8, 512], BF16)
    din("c_vmfb", [NT, 128, 2, 64])
    dscr("YB", [T, 512])
    din("w_branch_a", [512, D])
    din("w_branch_b", [512, D])
    din("w_out", [D, D])
    dscr("X1L", [peer_tiles * 128, D])
    dscr("YAL", [peer_tiles * 128, 512])
    din("norm2_g_b", [128, D])
    din("iota16", [128, 16])
    din("rowidx", [128, NT], I32)
    din("peer_wq", [D, 2048])
    din("peer_k1", [128, 128])
    din("peer_k2", [128, 128])
    din("peer_u", [16384, D])
    din("peer_v", [16384, D])
    dscr("UV", [16384, 2 * D], BF16)
    dr["out"] = nc.dram_tensor("out", [peer_tiles * 128, D], F32, kind="ExternalOutput").ap()
    with ExitStack() as top:
        pg = Prog(nc, top)
        C.pg = pg
        C.eps6 = top.enter_context(nc.sbuf_tensor("c_eps6", [128, 1], F32))
        pg.memset("dve", C.eps6[:], 1e-6, writes=["eps6"])
        pg.barrier()
        for s in stages:
            s(C)
        pg.barrier()
        pg.emit()
    return nc


def core_x(inputs, b, hh):
    xb = np.asarray(inputs["x"][b])
    if hh == 0:
        return np.ascontiguousarray(np.concatenate([np.zeros((T // 2, D), np.float32), xb[0:T // 2]], 0))
    return np.ascontiguousarray(xb)


def host_inputs(inputs, b, hh=1, ntl=NT):
    g = lambda k: np.ascontiguousarray(inputs[k][0])
    m = {}
    m["x"] = core_x(inputs, b, hh)
    m["norm1_g_b"] = np.ascontiguousarray(np.broadcast_to(g("norm1_g")[None, :], (128, D)))
    m["ident"] = np.eye(128, dtype=np.float32)
    m["w_in"] = g("w_in")
    bc = lambda a: np.ascontiguousarray(np.broadcast_to(np.asarray(a).reshape(1, -1), (128, a.size)))
    m["rw_mu_b"] = bc(g("rw_mu"))
    for nm in ("rw_w0", "rw_a0", "rw_k_k", "rw_k_a", "rw_r_k", "rw_ln_w", "rw_ln_b"):
        m[nm + "_b"] = bc(g(nm))
    for nm in ("rw_g_up", "rw_w_up", "rw_a_up"):
        m[nm] = g(nm)
    bmk = np.zeros((8, 512), np.float32)
    for h in range(8):
        bmk[h, h * 64:(h + 1) * 64] = 1.0
    m["blkmask"] = bmk
    ii = np.arange(128)
    m["c_tri"] = (ii[:, None] <= ii[None, :]).astype(np.float32)
    m["c_msk"] = np.ascontiguousarray(np.stack([(ii[:, None] < ii[None, :]), (ii[:, None] <= ii[None, :]), (ii[:, None] > ii[None, :])], 1).astype(np.float32))
    m.update(nsa_consts(hh))
    for nm in ("w_branch_a", "w_branch_b", "w_out", "peer_wq", "peer_k1", "peer_k2", "peer_u", "peer_v"):
        m[nm] = g(nm)
    m["norm2_g_b"] = bc(g("norm2_g"))
    ri = np.zeros((128, NT), np.int32)
    ri[:, :ntl] = ((NT - ntl) * 128 + np.arange(ntl)[None, :] * 128 + np.arange(128)[:, None]).astype(np.int32)
    m["rowidx"] = ri
    m["iota16"] = np.ascontiguousarray(np.broadcast_to(np.arange(16, dtype=np.float32)[None, :], (128, 16)))
    m["nsa_gains_b"] = bc(np.concatenate([np.tile(g("nsa_q_g"), 8), np.tile(g("nsa_ks_g"), 2), np.tile(g("nsa_kw_g"), 2)]))
    m["nsa_kc_g_b"] = bc(g("nsa_kc_g"))
    posT = np.zeros((128, 2, 32), np.float32)
    posT[0:64, 0, :] = g("cmp_pos_k").T
    posT[0:64, 1, :] = g("cmp_pos_v").T
    m["posT"] = posT
    w2 = np.stack([g("cmp_k_w2").reshape(2, 128, 64), g("cmp_v_w2").reshape(2, 128, 64)], 0)
    m["cmp_w2"] = np.ascontiguousarray(w2.transpose(2, 0, 1, 3))
    w1 = []
    for nm in ("cmp_k_w1", "cmp_v_w1"):
        a = g(nm).reshape(32, 64, 256).transpose(1, 0, 2)
        w1.append(np.concatenate([a, a], 0))
    m["cmp_w1"] = np.ascontiguousarray(np.stack(w1, 0))
    return m


def dap(ap, offset, pattern):
    return bass.AP(ap.tensor, offset, [list(p) for p in pattern])


def stage2a(C):
    nc, pg, dr = C.nc, C.pg, C.dr
    with ExitStack() as st:
        sb, ps = _mk(C, st)
        mu = sb("a_mu", [128, RWC], F32)
        w0 = sb("a_w0", [128, 512], F32)
        a0 = sb("a_a0", [128, 512], F32)
        kkc = sb("a_kk", [128, 512], F32)
        kac = sb("a_ka", [128, 512], F32)
        rkc = sb("a_rk", [128, 512], F32)
        wup = sb("a_wup", [128, 512], F32)
        gup = sb("a_gup", [128, 512], F32)
        idf = sb("a_idf", [128, 128], F32)
        p_2 = [sb(f"a_p{i_}", [128, RWC], F32) for i_ in range(2)]
        pv_2 = [sb(f"a_pv{i_}", [128, RWC], F32) for i_ in range(2)]
        pm_2 = [sb(f"a_pm{i_}", [128, RWC], F32) for i_ in range(2)]
        lor_2 = [sb(f"a_lor{i_}", [128, 256], F32) for i_ in range(2)]
        lorT_2 = [sb(f"a_lorT{i_}", [128, 256], F32) for i_ in range(2)]
        wt_2 = [sb(f"a_wt{i_}", [128, 512], F32) for i_ in range(2)]
        lwt_2 = [sb(f"a_lwt{i_}", [128, 512], F32) for i_ in range(2)]
        at_2 = [sb(f"a_at{i_}", [128, 512], F32) for i_ in range(2)]
        gt_2 = [sb(f"a_gt{i_}", [128, 512], F32) for i_ in range(2)]
        kk_2 = [sb(f"a_kkt{i_}", [128, 512], F32) for i_ in range(2)]
        sq_2 = [sb(f"a_sq{i_}", [128, 512], F32) for i_ in range(2)]
        nrm_2 = [sb(f"a_nrm{i_}", [128, 32], F32) for i_ in range(2)]
        kkn_2 = [sb(f"a_kkn{i_}", [128, 512], F32) for i_ in range(2)]
        bt_2 = [sb(f"a_bt{i_}", [128, 512], F32) for i_ in range(2)]
        t1_2 = [sb(f"a_t1{i_}", [128, 512], F32) for i_ in range(2)]
        kp_2 = [sb(f"a_kp{i_}", [128, 512], F32) for i_ in range(2)]
        bon_2 = [sb(f"a_bon{i_}", [128, 8], F32) for i_ in range(2)]
        psl_2 = [ps(f"a_psl{i_}", [128, 512], F32) for i_ in range(2)]
        psw_2 = [ps(f"a_psw{i_}", [128, 512], F32) for i_ in range(2)]
        psa_2 = [ps(f"a_psa{i_}", [128, 512], F32) for i_ in range(2)]
        psg_2 = [ps(f"a_psg{i_}", [128, 512], F32) for i_ in range(2)]

        for (tile, name) in ((mu, "rw_mu_b"), (w0, "rw_w0_b"), (a0, "rw_a0_b"), (kkc, "rw_k_k_b"),
                             (kac, "rw_k_a_b"), (rkc, "rw_r_k_b"), (gup, "rw_g_up"), (idf, "ident")):
            pg.ld(tile[:], dr[name][:, :], writes=[name])
        pg.ld(wup[0:64, :], dr["rw_w_up"][:, :], writes=["wup0"])
        pg.ld(wup[64:128, :], dr["rw_a_up"][:, :], writes=["wup1"])
        P = dr["P"]
        def E_(i):
            t0 = i * 128
            s = i % 2
            p, pv, pm, lor, lorT, wt, lwt, at, gt, kk, sq, nrm, kkn, bt, t1, kp, bon = [t_[s] for t_ in (
                p_2, pv_2, pm_2, lor_2, lorT_2, wt_2, lwt_2, at_2, gt_2, kk_2, sq_2, nrm_2, kkn_2, bt_2, t1_2, kp_2, bon_2)]
            psl, psw, psa, psg = psl_2[s], psw_2[s], psa_2[s], psg_2[s]
            pg.ld(p[:], P[t0:t0 + 128, 0:RWC], reads=[("P", i)], writes=[("p", s)])
            if i == 0:
                pg.memset("dve", pv[0:1, :], 0.0, writes=[("pv0", s)])
                pg.ld(pv[1:128, :], P[0:127, 0:RWC], reads=[("P", 0)], writes=[("pv", s)])
                pvk = [("pv", s), ("pv0", s)]
            else:
                pg.ld(pv[:], P[t0 - 1:t0 + 127, 0:RWC], reads=[("P", i), ("P", i - 1)], writes=[("pv", s), ("pv0", s)])
                pvk = [("pv", s), ("pv0", s)]
            pg.tt("dve", pv[:], pv[:], p[:], ALU.subtract, reads=pvk + [("p", s)], writes=[("pv", s)])
            pg.tt("dve", pv[:], pv[:], mu[:], ALU.mult, reads=[("pv", s), "rw_mu_b"], writes=[("pv", s)])
            pg.tt("dve", pm[:], pv[:], p[:], ALU.add, reads=[("pv", s), ("p", s)], writes=[("pm", s)])
            r_ = pm[:, 0:512]
            k_ = pm[:, 512:1024]
            v_ = pm[:, 1024:1536]
            pg.act(lor[:, 0:64], pm[:, 1536:1600], AF.Tanh, reads=[("pm", s)], writes=[("lor0", s)])
            pg.cp("pool", lor[:, 64:128], pm[:, 1600:1664], reads=[("pm", s)], writes=[("lor1", s)])
            pg.act(lor[:, 128:256], pm[:, 1664:1792], AF.Sigmoid, reads=[("pm", s)], writes=[("lor2", s)])
            pg.tr(psl[:, 0:128], lor[:, 0:128], idf[:], reads=[("lor0", s), ("lor1", s), "ident"], writes=[("psl", s)])
            pg.tr(psl[:, 128:256], lor[:, 128:256], idf[:], reads=[("lor2", s), "ident"], writes=[("psl", s)])
            pg.cp("act", lorT[:], psl[:, 0:256], reads=[("psl", s)], writes=[("lorT", s)])
            pg.mm(psw[:], lorT[0:64, 0:128], wup[0:64, :], reads=[("lorT", s), "wup0"], writes=[("psw", s)])
            pg.mm(psa[:], lorT[64:128, 0:128], wup[64:128, :], reads=[("lorT", s), "wup1"], writes=[("psa", s)])
            pg.mm(psg[:], lorT[:, 128:256], gup[:], reads=[("lorT", s), "rw_g_up"], writes=[("psg", s)])
        def L_(i):
            t0 = i * 128
            s = i % 2
            p, pv, pm, lor, lorT, wt, lwt, at, gt, kk, sq, nrm, kkn, bt, t1, kp, bon = [t_[s] for t_ in (
                p_2, pv_2, pm_2, lor_2, lorT_2, wt_2, lwt_2, at_2, gt_2, kk_2, sq_2, nrm_2, kkn_2, bt_2, t1_2, kp_2, bon_2)]
            psl, psw, psa, psg = psl_2[s], psw_2[s], psa_2[s], psg_2[s]
            r_ = pm[:, 0:512]
            k_ = pm[:, 512:1024]
            v_ = pm[:, 1024:1536]
            pg.tt("dve", wt[:], psw[:], w0[:], ALU.add, reads=[("psw", s), "rw_w0_b"], writes=[("wt", s)])
            pg.act(wt[:], wt[:], AF.Sigmoid, reads=[("wt", s)], writes=[("wt", s)])
            pg.ts("dve", lwt[:], wt[:], -0.6065306597126334, None, ALU.mult, reads=[("wt", s)], writes=[("lwt", s)])
            pg.tt("dve", at[:], psa[:], a0[:], ALU.add, reads=[("psa", s), "rw_a0_b"], writes=[("at", s)])
            pg.act(at[:], at[:], AF.Sigmoid, reads=[("at", s)], writes=[("at", s)])
            pg.cp("act", gt[:], psg[:], reads=[("psg", s)], writes=[("gt", s)])
            pg.tt("dve", kk[:], k_, kkc[:], ALU.mult, reads=[("pm", s), "rw_k_k_b"], writes=[("kk", s)])
            pg.tt("pool", sq[:], kk[:], kk[:], ALU.mult, reads=[("kk", s)], writes=[("sq", s)])
            pg.red("dve", nrm[:, 0:8], sq[:].rearrange("p (h k) -> p h k", h=8), ALU.add, reads=[("sq", s)], writes=[("nrm0", s)])
            pg.act(nrm[:, 8:16], nrm[:, 0:8], AF.Sqrt, reads=[("nrm0", s)], writes=[("nrm1", s)])
            pg.ts("dve", nrm[:, 16:24], nrm[:, 8:16], 1e-12, None, ALU.max, reads=[("nrm1", s)], writes=[("nrm2", s)])
            pg.op("dve", lambda e, nrm=nrm: e.reciprocal(nrm[:, 24:32], nrm[:, 16:24]), reads=[("nrm2", s)], writes=[("nrm3", s)])
            rinv_b = nrm[:, 24:32].unsqueeze(2).to_broadcast([128, 8, 64])
            v3 = lambda tl: tl[:].rearrange("p (h k) -> p h k", h=8)
            pg.stt("dve", v3(kkn), v3(kk), -1.0, rinv_b, ALU.mult, ALU.mult, reads=[("kk", s), ("nrm3", s)], writes=[("kkn", s)])
            pg.stt("dve", bt[:], kkn[:], -1.0, at[:], ALU.mult, ALU.mult, reads=[("kkn", s), ("at", s)], writes=[("bt", s)])
            pg.stt("dve", t1[:], at[:], -1.0, kac[:], ALU.add, ALU.mult, reads=[("at", s), "rw_k_a_b"], writes=[("t1", s)])
            pg.stt("dve", kp[:], t1[:], 1.0, k_, ALU.add, ALU.mult, reads=[("t1", s), ("pm", s)], writes=[("kp", s)])
            pg.tt("pool", sq[:], r_, kp[:], ALU.mult, reads=[("pm", s), ("kp", s), ("sq", s)], writes=[("sq", s)])
            pg.tt("pool", sq[:], sq[:], rkc[:], ALU.mult, reads=[("sq", s), "rw_r_k_b"], writes=[("sq", s)])
            pg.red("dve", bon[:], sq[:].rearrange("p (h k) -> p h k", h=8), ALU.add, reads=[("sq", s)], writes=[("bon", s)])
            pg.ld(dr["RR"][t0:t0 + 128, :], r_, reads=[("pm", s)], writes=[("RR", i)], q="act")
            pg.ld(dr["RKK"][t0:t0 + 128, :], kkn[:], reads=[("kkn", s)], writes=[("RKK", i)], q="act")
            pg.ld(dr["RLW"][t0:t0 + 128, :], lwt[:], reads=[("lwt", s)], writes=[("RLW", i)], q="act")
            pg.ld(dr["RB"][t0:t0 + 128, :], bt[:], reads=[("bt", s)], writes=[("RB", i)], q="act")
            pg.ld(dr["RKp"][t0:t0 + 128, :], kp[:], reads=[("kp", s)], writes=[("RKp", i)], q="act")
            pg.ld(dr["RV"][t0:t0 + 128, :], v_, reads=[("pm", s)], writes=[("RV", i)], q="act")
            pg.ld(dr["RG"][t0:t0 + 128, :], gt[:], reads=[("gt", s)], writes=[("RG", i)], q="act")
            pg.ld(dr["RBON"][t0:t0 + 128, :], bon[:], reads=[("bon", s)], writes=[("RBON", i)], q="act")
        E_(0)
        for i in range(NT):
            if i + 1 < NT:
                E_(i + 1)
            L_(i)
        pg.barrier()
        pg.emit()


def igather(pg, out_ap, table_ap, idx_ap, reads, writes):
    pg.dma("pool", lambda e: e.indirect_dma_start(out=out_ap, out_offset=None, in_=table_ap,
                                                   in_offset=bass.IndirectOffsetOnAxis(ap=idx_ap, axis=0)), reads, writes)


def stage2c(C):
    nc, pg, dr = C.nc, C.pg, C.dr
    with ExitStack() as st:
        sb, ps = _mk(C, st)
        lnw = sb("c_lnw", [128, 512], F32)
        lnb = sb("c_lnb", [128, 512], F32)
        eps = sb("c_eps", [128, 1], F32)
        y = [sb(f"c_y{i}", [128, 8, 64], F32) for i in range(2)]
        v = [sb(f"c_v{i}", [128, 8, 64], F32) for i in range(2)]
        g = [sb(f"c_g{i}", [128, 512], F32) for i in range(2)]
        bon = [sb(f"c_bon{i}", [128, 8], F32) for i in range(2)]
        stt_ = [sb(f"c_st{i}", [128, 32], F32) for i in range(2)]
        sq = sb("c_sq", [128, 8, 64], F32)
        pg.ld(lnw[:], dr["rw_ln_w_b"][:, :], writes=["lnw"])
        pg.ld(lnb[:], dr["rw_ln_b_b"][:, :], writes=["lnb"])
        pg.memset("dve", eps[:], 64e-5, writes=["eps"])
        f2 = lambda tl: tl[:].rearrange("p h k -> p (h k)")
        rowidx = sb("c_rowidx", [128, NT], I32)
        pg.ld(rowidx[:], dr["rowidx"][:, :], writes=["rowidx"])
        allk = lambda nm: [(nm, k) for k in range(NT)]
        for i in range(C.peer_tiles):
            s = i % 2
            t0 = i * 128
            yk, vk, gk, bk, sk = ("y", s), ("v", s), ("g", s), ("bon", s), ("st", s)
            ix = rowidx[:, i:i + 1]
            igather(pg, f2(y[s]), dr["YS"][:, :], ix, allk("YS") + ["rowidx"], [yk])
            igather(pg, f2(v[s]), dr["RV"][:, :], ix, allk("RV") + ["rowidx"], [vk])
            igather(pg, g[s][:], dr["RG"][:, :], ix, allk("RG") + ["rowidx"], [gk])
            igather(pg, bon[s][:], dr["RBON"][:, :], ix, allk("RBON") + ["rowidx"], [bk])
            S_ = stt_[s]
            bc = lambda ap: ap.unsqueeze(2).to_broadcast([128, 8, 64])
            pg.red("dve", S_[:, 0:8], y[s][:], ALU.add, reads=[yk], writes=[(sk, 0)])
            pg.ts("dve", S_[:, 8:16], S_[:, 0:8], -1.0 / 64, None, ALU.mult, reads=[(sk, 0)], writes=[(sk, 1)])
            pg.tt("dve", y[s][:], y[s][:], bc(S_[:, 8:16]), ALU.add, reads=[yk, (sk, 1)], writes=[yk])
            pg.tt("pool", sq[:], y[s][:], y[s][:], ALU.mult, reads=[yk], writes=["sq"])
            pg.red("dve", S_[:, 16:24], sq[:], ALU.add, reads=["sq"], writes=[(sk, 2)])
            pg.act(S_[:, 24:32], S_[:, 16:24], AF.Sqrt, reads=[(sk, 2), "eps"], writes=[(sk, 3)], scale=1.0 / 64, bias=eps[:, 0:1])
            pg.op("dve", lambda e, o=S_[:, 16:24], a=S_[:, 24:32]: e.reciprocal(o, a), reads=[(sk, 3)], writes=[(sk, 2)])
            pg.tt("dve", y[s][:], y[s][:], bc(S_[:, 16:24]), ALU.mult, reads=[yk, (sk, 2)], writes=[yk])
            pg.tt("dve", f2(y[s]), f2(y[s]), lnw[:], ALU.mult, reads=[yk, "lnw"], writes=[yk])
            pg.tt("pool", f2(y[s]), f2(y[s]), lnb[:], ALU.add, reads=[yk, "lnb"], writes=[yk])
            pg.tt("pool", v[s][:], v[s][:], bc(bon[s][:, 0:8]), ALU.mult, reads=[vk, bk], writes=[vk])
            pg.tt("dve", y[s][:], y[s][:], v[s][:], ALU.add, reads=[yk, vk], writes=[yk])
            pg.tt("dve", f2(y[s]), f2(y[s]), g[s][:], ALU.mult, reads=[yk, gk], writes=[yk])
            pg.ld(dr["YAL"][t0:t0 + 128, :], f2(y[s]), reads=[yk], writes=[("YAL", i)])
        pg.barrier()
        pg.emit()


NEG = -30000.0


def stage3(C):
    nc, pg, dr = C.nc, C.pg, C.dr
    with ExitStack() as st:
        sb, ps = _mk(C, st)
        qT = sb("n_qT", [128, 4, T], BF16)
        KsT = sb("n_KsT", [128, 2, T], BF16)
        KwT = sb("n_KwT", [128, 2, T], BF16)
        Vs = sb("n_Vs", [128, NT, 2, 65], BF16)
        Vw = sb("n_Vw", [128, NT, 2, 65], BF16)
        KcT = sb("n_KcT", [128, 2, 256], BF16)
        Vc = sb("n_Vc", [128, 2, 2, 129], BF16)
        GT = sb("n_GT", [128, NT, 24], F32)
        idf = sb("n_idf", [128, 128], F32)
        idb = sb("n_idb", [128, 128], BF16)
        eps = sb("n_eps", [128, 1], F32)
        pg.ld(idf[:], dr["ident"][:, :], writes=["idf"])
        pg.cp("dve", idb[:], idf[:], reads=["idf"], writes=["idb"])
        pg.memset("dve", eps[:], 1e-6, writes=["eps"])
        pg.memset("pool", Vs[:], 1.0, writes=["Vs"])
        pg.memset("pool", Vw[:], 1.0, writes=["Vw"])
        pg.memset("pool", Vc[:], 0.0, writes=["Vc"])
        with ExitStack() as sa_:
            sb, ps = _mk(C, sa_)
            kcT2 = sb("n_kcT2", [128, T], BF16)
            vcT2 = sb("n_vcT2", [128, T], BF16)
            w1 = [sb(f"n_w1{i}", [128, 32, 256], BF16) for i in range(2)]
            w1s = sb("n_w1s", [128, 16, 256], F32)
            w2s = sb("n_w2s", [128, 2, 2, 64], F32)
            w2 = sb("n_w2", [128, 2, 2, 64], BF16)
            posf = sb("n_posf", [128, 2, 32], F32)
            posb = sb("n_posb", [128, 2, 32], BF16)
            gains = sb("n_gains", [128, 768], F32)
            kcg = sb("n_kcg", [128, 64], F32)
            ovl = sb("n_ovl", [128, 2, 64], F32)
            R = [sb(f"n_R{i}", [128, 1304], F32) for i in range(2)]
            sq = sb("n_sq", [128, 1280], F32)
            tmp = sb("n_tmp", [128, 768], F32)
            stat = sb("n_stat", [128, 64], F32)
            Xb = sb("n_Xb", [128, 10, 128], BF16)
            biasS = sb("n_biasS", [128, 4], F32)
            xb_ = sb("n_xb", [128, 256], F32)
            x2_ = sb("n_x2", [128, 256], F32)
            hT = sb("n_hT", [128, 2, 256], BF16)
            kcn2 = sb("n_kcn2", [128, 128], BF16)
            st2 = sb("n_st2", [128, 8], F32)
            ksq = sb("n_ksq", [128, 64], F32)
            psX_ = [ps(f"n_psX{i}", [128, 1024], BF16) for i in range(3)]
            psX = [t_[:, 0:512].rearrange("p (a b) -> p a b", a=4) for t_ in psX_]
            psh = ps("n_psh", [128, 512], F32)
            psb = ps("n_psb", [128, 512], F32)
            pso = ps("n_pso", [128, 512], F32)
            psk = ps("n_psk", [128, 1024], BF16)

            pg.ld(gains[:], dr["nsa_gains_b"][:, :], writes=["gains"])
            pg.ts("dve", gains[:, 0:512], gains[:, 0:512], 0.125, None, ALU.mult, reads=["gains"], writes=["gains"])
            pg.ld(kcg[:], dr["nsa_kc_g_b"][:, :], writes=["kcg"])
            pg.ld(ovl[:], dr["ovl"][:, :, :], writes=["ovl"])
            pg.ld(posf[:], dr["posT"][:, :, :], writes=["posf"])
            pg.cp("dve", posb[:], posf[:], reads=["posf"], writes=["posb"])
            pg.ld(w2s[:], dr["cmp_w2"][:, :, :, :], writes=["w2s"])
            pg.cp("dve", w2[:], w2s[:], reads=["w2s"], writes=["w2"])
            for x in range(2):
                for hf in range(2):
                    pg.ld(w1s[:], dr["cmp_w1"][x, :, hf * 16:(hf + 1) * 16, :], writes=["w1s"])
                    pg.cp("pool", w1[x][:, hf * 16:(hf + 1) * 16, :], w1s[:], reads=["w1s"], writes=[("w1", x)])
            for i in range(NT):
                s = i % 2
                t0 = i * 128
                Rk = ("R", s)
                pg.ld(R[s][:], dr["P"][t0:t0 + 128, 1792:3096], reads=[("P", i)], writes=[Rk])
                Rs = R[s]
                pg.tt("pool", sq[:], Rs[:, 0:1280], Rs[:, 0:1280], ALU.mult, reads=[Rk], writes=["sq"])
                pg.red("dve", stat[:, 0:20], sq[:].rearrange("p (a k) -> p a k", k=64), ALU.add, reads=["sq"], writes=["stat0"])
                pg.act(stat[:, 20:40], stat[:, 0:20], AF.Sqrt, reads=["stat0", "eps"], writes=["stat1"], scale=1.0 / 64, bias=eps[:, 0:1])
                pg.op("dve", lambda e: e.reciprocal(stat[:, 40:60], stat[:, 20:40]), reads=["stat1"], writes=["stat2"])
                b3 = lambda ap, n: ap.unsqueeze(2).to_broadcast([128, n, 64])
                v3 = lambda ap: ap.rearrange("p (a k) -> p a k", k=64)
                pg.tt("dve", v3(tmp[:, 0:512]), v3(Rs[:, 0:512]), b3(stat[:, 40:48], 8), ALU.mult, reads=[Rk, "stat2"], writes=["tmp"])
                pg.tt("dve", v3(tmp[:, 512:640]), v3(Rs[:, 768:896]), b3(stat[:, 52:54], 2), ALU.mult, reads=[Rk, "stat2"], writes=["tmp"])
                pg.tt("dve", v3(tmp[:, 640:768]), v3(Rs[:, 1024:1152]), b3(stat[:, 56:58], 2), ALU.mult, reads=[Rk, "stat2"], writes=["tmp"])
                pg.tt("pool", tmp[:], tmp[:], gains[:], ALU.mult, reads=["tmp", "gains"], writes=["tmp"])
                pg.cp("pool", Xb[:, 0:4, :].rearrange("p a b -> p (a b)"), tmp[:, 0:512], reads=["tmp"], writes=["Xb"])
                for (blk, c0) in ((4, 512), (6, 640)):
                    src = tmp[:, c0:c0 + 128].rearrange("p (g k) -> p g k", g=2).unsqueeze(2).to_broadcast([128, 2, 2, 64])
                    dst = Xb[:, blk:blk + 2, :].rearrange("p g (d k) -> p g d k", d=2)
                    pg.cp("dve", dst, src, reads=["tmp"], writes=["Xb"])
                pg.cp("pool", Xb[:, 8, :], Rs[:, 512:640], reads=[Rk], writes=["Xb"])
                pg.cp("pool", Xb[:, 9, :], Rs[:, 640:768], reads=[Rk], writes=["Xb"])
                for blk in range(10):
                    pg.tr(psX[blk // 4][:, blk % 4, :], Xb[:, blk, :], idb[:], reads=["Xb", "idb"], writes=[("psX", blk // 4)])
                pg.cp("act", qT[:, :, t0:t0 + 128], psX[0], reads=[("psX", 0)], writes=["qT"])
                pg.cp("dve", KsT[:, :, t0:t0 + 128], psX[1][:, 0:2, :], reads=[("psX", 1)], writes=["KsT"])
                pg.cp("dve", KwT[:, :, t0:t0 + 128], psX[1][:, 2:4, :], reads=[("psX", 1)], writes=["KwT"])
                pg.cp("act", kcT2[:, t0:t0 + 128], psX[2][:, 0, :], reads=[("psX", 2)], writes=["kcT2"])
                pg.cp("act", vcT2[:, t0:t0 + 128], psX[2][:, 1, :], reads=[("psX", 2)], writes=["vcT2"])
                pg.cp("pool", Vs[:, i, :, 0:64], Rs[:, 896:1024].rearrange("p (g k) -> p g k", g=2), reads=[Rk, "Vs"], writes=["Vs"])
                pg.cp("pool", Vw[:, i, :, 0:64], Rs[:, 1152:1280].rearrange("p (g k) -> p g k", g=2), reads=[Rk, "Vw"], writes=["Vw"])
                pg.act(GT[:, i, :], Rs[:, 1280:1304], AF.Sigmoid, reads=[Rk], writes=["GT"])
            if getattr(C, "lvl", 9) < 2:
                pg.barrier()
                pg.emit()
                return
            pg.memset("dve", hT[:], 0.0, writes=["hT"])
            pg.memset("dve", kcn2[:], 0.0, writes=["kcn2"])
            for x in range(2):
                for hf in range(2):
                    for l in range(32):
                        pg.mm(psb[:, x * 2 + hf:x * 2 + hf + 1], w1[x][0:64, l, hf * 128:(hf + 1) * 128], posb[0:64, x, l:l + 1],
                              start=(l == 0), stop=(l == 31), reads=[("w1", x), "posb"], writes=["psb"])
            pg.cp("dve", biasS[:], psb[:, 0:4], reads=["psb"], writes=["biasS"])
            for x in range(2):
                srcT = kcT2 if x == 0 else vcT2
                skey = "kcT2" if x == 0 else "vcT2"
                for g in range(2):
                    for hf in range(2):
                        for l in range(32):
                            rhs = dap(srcT[:], g * 64 * T + l, [[T, 64], [16, 255]])
                            pg.mm(psh[:, 0:255], w1[x][g * 64:(g + 1) * 64, l, hf * 128:(hf + 1) * 128], rhs,
                                  start=(l == 0), stop=(l == 31), reads=[("w1", x), skey], writes=["psh"])
                        c = slice(0, 255)
                        pg.act(xb_[:, c], psh[:, c], AF.Identity, reads=["psh", "biasS"], writes=["xb"], bias=biasS[:, x * 2 + hf:x * 2 + hf + 1])
                        pg.tt("pool", x2_[:, c], xb_[:, c], xb_[:, c], ALU.mult, reads=["xb"], writes=["x2"])
                        pg.ts("dve", x2_[:, c], x2_[:, c], 0.044715, 1.0, ALU.mult, ALU.add, reads=["x2"], writes=["x2"])
                        pg.tt("dve", x2_[:, c], x2_[:, c], xb_[:, c], ALU.mult, reads=["x2", "xb"], writes=["x2"])
                        pg.act(x2_[:, c], x2_[:, c], AF.Tanh, reads=["x2"], writes=["x2"], scale=0.7978845608028654)
                        pg.stt("dve", x2_[:, c], x2_[:, c], 1.0, xb_[:, c], ALU.add, ALU.mult, reads=["x2", "xb"], writes=["x2"])
                        pg.ts("dve", hT[:, hf, c], x2_[:, c], 0.5, None, ALU.mult, reads=["x2"], writes=["hT"])
                    for m in range(2):
                        rows = 128 if m == 0 else 127
                        for hf in range(2):
                            pg.mm(pso[0:rows, 0:64], hT[:, hf, m * 128:m * 128 + rows], w2[:, x, hf, :], start=(hf == 0), stop=(hf == 1),
                                  reads=["hT", "w2"], writes=["pso"])
                        if x == 0:
                            pg.cp("act", ksq[0:rows, :], pso[0:rows, 0:64], reads=["pso"], writes=["ksq"])
                            pg.tt("pool", x2_[0:rows, 0:64], ksq[0:rows, :], ksq[0:rows, :], ALU.mult, reads=["ksq", "x2"], writes=["x2"])
                            pg.red("dve", st2[0:rows, 0:1], x2_[0:rows, 0:64], ALU.add, reads=["x2"], writes=["st2a"])
                            pg.act(st2[0:rows, 1:2], st2[0:rows, 0:1], AF.Sqrt, reads=["st2a", "eps"], writes=["st2b"], scale=1.0 / 64, bias=eps[0:rows, 0:1])
                            pg.op("dve", lambda e, rows=rows: e.reciprocal(st2[0:rows, 2:3], st2[0:rows, 1:2]), reads=["st2b"], writes=["st2c"])
                            pg.stt("dve", ksq[0:rows, :], ksq[0:rows, :], st2[0:rows, 2:3], kcg[0:rows, :], ALU.mult, ALU.mult,
                                   reads=["ksq", "st2c", "kcg"], writes=["ksq"])
                            src = ksq[0:rows, :].unsqueeze(1).to_broadcast([rows, 2, 64])
                            pg.cp("dve", kcn2[0:rows, :].rearrange("p (d k) -> p d k", d=2), src, reads=["ksq"], writes=["kcn2"])
                            pg.tr(psk[:, 0:128], kcn2[:, :], idb[:], reads=["kcn2", "idb"], writes=["psk"])
                            pg.cp("act", KcT[:, g, m * 128:(m + 1) * 128], psk[:, 0:128], reads=["psk"], writes=["KcT"])
                        else:
                            pg.cp("act", Vc[0:rows, m, g, 0:64], pso[0:rows, 0:64], reads=["pso", "Vc"], writes=["Vc"])
            for m in range(2):
                for g in range(2):
                    pg.memset("dve", Vc[:, m, g, 64:65], 1.0, writes=["Vc"])
                    pg.cp("dve", Vc[:, m, g, 65:129], ovl[:, m, :], reads=["ovl", "Vc"], writes=["Vc"])
            pg.barrier()
            pg.emit()
        if getattr(C, "lvl", 9) < 3:
            return
        stage3_attn(C, st, qT, KsT, KwT, Vs, Vw, KcT, Vc, GT, idf, idb)


def stage3_attn(C, st, qT, KsT, KwT, Vs, Vw, KcT, Vc, GT, idf, idb):
    nc, pg, dr = C.nc, C.pg, C.dr
    with ExitStack() as sb_:
        sb, ps = _mk(C, sb_)
        cmpb = sb("n_cmpb", [128, 2, T], BF16)
        Esel = sb("n_Esel", [128, 32, 128], BF16)
        causb = sb("n_causb", [128, 4, 512], BF16)
        winb = sb("n_winb", [128, 8, 512], BF16)
        selbT = sb("n_selbT", [128, 2, T], BF16)
        eT = [sb(f"n_eT{i}", [128, 512], BF16) for i in range(4)]
        eT2 = [sb(f"n_eT2{i}", [128, 512], BF16) for i in range(4)]
        Mt = [sb(f"n_Mt{i}", [128, 512], BF16) for i in range(2)]
        rm = [0]
        dq = []
        ocmp = sb("n_ocmp", [128, 4, 8, 64], F32)
        osel = sb("n_osel", [128, 4, 8, 64], F32)
        owin = sb("n_owin", [128, 4, 8, 64], F32)
        den = sb("n_den", [128, 16], F32)
        impw = sb("n_impw", [128, 2, 4, 64], F32)
        score = sb("n_score", [128, 2, 64], F32)
        VM = [sb(f"n_VM{i}", [128, 2, 64], F32) for i in range(2)]
        work = sb("n_work", [128, 2, 64], F32)
        m8 = sb("n_m8", [128, 2, 16], F32)
        thr = sb("n_thr", [128, 2], F32)
        msel = sb("n_msel", [128, 2, 64], F32)
        selb = sb("n_selb", [128, 2, 2, 64], BF16)
        osT = [sb(f"n_osT{i}", [65, 512], F32) for i in range(2)]
        dn2 = sb("n_dn2", [128, 8], F32)
        yb = sb("n_yb", [128, 8, 64], F32)
        yb2 = sb("n_yb2", [128, 8, 64], F32)
        psS = [ps(f"n_psS{i}", [128, 512], F32) for i in range(3)]
        psA = [ps(f"n_psA{i}", [128, 512], F32) for i in range(2)]
        psB = [ps(f"n_psB{i}", [128, 512], F32) for i in range(2)]
        psZ_ = ps("n_psZ", [128, 1024], BF16)
        psZ = psZ_[:, 0:256].rearrange("p (g q) -> p g q", g=2)

        pg.ld(cmpb[:], dr["c_cmpb"][:, :, :], writes=["cmpb"])
        pg.ld(Esel[:], dr["c_esel"][:, :, :], writes=["Esel"])
        pg.ld(causb[:], dr["c_causb"][:, :, :], writes=["causb"])
        pg.ld(winb[:], dr["c_winb"][:, :, :], writes=["winb"])
        winb4 = sb("n_winb4", [128, 8, 512], BF16)
        pg.ld(winb4[:], dr["c_winb4"][:, :, :], writes=["winb4"])
        rs = [0]
        re = [0]

        def nxt(lst, n):
            v = lst[0]
            lst[0] = (v + 1) % n
            return v

        def qk(h):
            return (h % 2) * 64, h // 2, h // 4

        for Q in range(4, 8):
            tq0 = Q * 512
            for ii in range(4):
                i = Q * 4 + ii
                t0 = i * 128
                s = i % 2
                pg.ld(VM[s][:], dr["c_vmfb"][i, :, :, :], writes=[("VM", s)])
                nm = 2 if i >= 16 else 1
                for h in range(8):
                    base, hp, g = qk(h)
                    h4 = h % 4
                    for m in range(nm):
                        r = nxt(rs, 3)
                        pS = psS[r]
                        pg.mm(pS[:, 0:128], KcT[base:base + 64, g, m * 128:(m + 1) * 128], qT[base:base + 64, hp, t0:t0 + 128],
                              start=True, stop=False, reads=["KcT", "qT"], writes=[("psS", r)])
                        pg.mm(pS[:, 0:128], idb[:, :], cmpb[:, m, t0:t0 + 128], start=False, stop=True,
                              reads=["idb", "cmpb"], writes=[("psS", r)])
                        k = nxt(re, 4)
                        pg.act(eT[k][:, 0:128], pS[:, 0:128], AF.Exp, reads=[("psS", r)], writes=[("eT", k)])
                        def pv(g=g, h4=h4, k=k, m=m, nm=nm):
                            pg.mm(psA[g][:, h4 * 65:h4 * 65 + 65], eT[k][:, 0:128], Vc[:, m, g, 0:65], start=(m == 0), stop=(m == nm - 1),
                                  reads=[("eT", k), "Vc"], writes=[("psA", g)])
                            pg.mm(psB[g][:, h4 * 64:h4 * 64 + 64], eT[k][:, 0:128], Vc[:, m, g, 65:129], start=(m == 0), stop=(m == nm - 1),
                                  reads=[("eT", k), "Vc"], writes=[("psB", g)])
                        dq.append(pv)
                        if len(dq) > 2:
                            dq.pop(0)()
                while dq:
                    dq.pop(0)()
                for g in range(2):
                    A3 = psA[g][:, 0:260].rearrange("p (h c) -> p h c", c=65)
                    B3 = psB[g][:, 0:256].rearrange("p (h c) -> p h c", c=64)
                    dsl = den[:, g * 4:(g + 1) * 4]
                    rsl = den[:, 8 + g * 4:8 + (g + 1) * 4]
                    pg.ts("dve", dsl, A3[:, :, 64], 1e-30, None, ALU.max, reads=[("psA", g)], writes=[("den", g)])
                    pg.op("dve", lambda e, o=rsl, a=dsl: e.reciprocal(o, a), reads=[("den", g)], writes=[("rden", g)])
                    rb = rsl.unsqueeze(2).to_broadcast([128, 4, 64])
                    pg.tt("dve", ocmp[:, ii, g * 4:(g + 1) * 4, :], A3[:, :, 0:64], rb, ALU.mult, reads=[("psA", g), ("rden", g)], writes=["ocmp"])
                    pg.tt("dve", impw[:, g, :, :], B3, rb, ALU.mult, reads=[("psB", g), ("rden", g)], writes=[("impw", g)])
                    pg.red("dve", score[:, g, :], impw[:, g, :, :].rearrange("p h j -> p j h"), ALU.add, reads=[("impw", g)], writes=[("score", g)])
                    vm = dr
                    pg.tt("dve", score[:, g, :], score[:, g, :], VM[s][:, 0, :], ALU.mult, reads=[("score", g), ("VM", s)], writes=[("score", g)])
                    pg.tt("dve", score[:, g, :], score[:, g, :], VM[s][:, 1, :], ALU.add, reads=[("score", g), ("VM", s)], writes=[("score", g)])
                    pg.op("dve", lambda e, g=g: e.max(m8[:, g, 0:8], score[:, g, :]), reads=[("score", g)], writes=[("m8a", g)])
                    pg.op("dve", lambda e, g=g: e.match_replace(work[:, g, :], m8[:, g, 0:8], score[:, g, :], -1e9),
                          reads=[("score", g), ("m8a", g)], writes=[("work", g)])
                    pg.op("dve", lambda e, g=g: e.max(m8[:, g, 8:16], work[:, g, :]), reads=[("work", g)], writes=[("m8b", g)])
                    pg.ts("dve", thr[:, g:g + 1], m8[:, g, 15:16], -0.5, None, ALU.max, reads=[("m8b", g)], writes=[("thr", g)])
                    pg.ts("dve", msel[:, g, :], score[:, g, :], thr[:, g:g + 1], None, ALU.is_ge, reads=[("score", g), ("thr", g)], writes=[("msel", g)])
                    pg.cp("dve", selb[:, g, :, :], msel[:, g, :].unsqueeze(1).to_broadcast([128, 2, 64]), reads=[("msel", g)], writes=[("selb", g)])
                    pg.tr(psZ[:, g, :], selb[:, g, :, :].rearrange("p d j -> p (d j)"), idb[:], reads=[("selb", g), "idb"], writes=["psZ"])
                pg.cp("act", selbT[:, :, t0:t0 + 128], psZ, reads=["psZ"], writes=["selbT"])
            for br in range(2):
                if getattr(C, "lvl", 9) < 4 + br:
                    continue
                dest = osel if br == 0 else owin
                dkey = "osel" if br == 0 else "owin"
                KT = KsT if br == 0 else KwT
                Vv = Vs if br == 0 else Vw
                kts = list(range(0, 4 * Q + 4)) if br == 0 else list(range(max(0, 4 * Q - 4), 4 * Q + 4))
                for g in range(2):
                    O = [psA[0], psA[1], psB[0], psB[1]]
                    okeys = [("psA", 0), ("psA", 1), ("psB", 0), ("psB", 1)]
                    for n_, kt in enumerate(kts):
                        if br == 0:
                            r = nxt(rs, 3)
                            pg.mm(psS[r][:, :], Esel[0:64, kt, :], selbT[0:64, g, tq0:tq0 + 512], reads=["Esel", "selbT"], writes=[("psS", r)])
                            mi = nxt(rm, 2)
                            if kt >= 4 * Q:
                                pg.tt("dve", Mt[mi][:], psS[r][:, :], causb[:, kt - 4 * Q, :], ALU.mult, reads=[("psS", r), "causb"], writes=[("Mt", mi)])
                            else:
                                pg.cp("dve", Mt[mi][:], psS[r][:, :], reads=[("psS", r)], writes=[("Mt", mi)])
                            mask, mkeys = Mt[mi][:], [("Mt", mi)]
                        else:
                            wsrc = winb4 if Q == 4 else winb
                            mask, mkeys = wsrc[:, kt - 4 * Q + 4, :], ["winb", "winb4"]
                        for h4 in range(4):
                            h = g * 4 + h4
                            base, hp, _g = qk(h)
                            r2 = nxt(rs, 3)
                            pg.mm(psS[r2][:, :], KT[base:base + 64, g, kt * 128:(kt + 1) * 128], qT[base:base + 64, hp, tq0:tq0 + 512],
                                  reads=["qT"], writes=[("psS", r2)])
                            k = nxt(re, 4)
                            pg.act(eT[k][:, :], psS[r2][:, :], AF.Exp, reads=[("psS", r2)], writes=[("eT", k)])
                            pg.tt("dve", eT2[k][:, :], eT[k][:, :], mask, ALU.mult, reads=[("eT", k)] + mkeys, writes=[("eT2", k)])
                            dq.append(lambda h4=h4, kt=kt, k=k, n_=n_, O=O, okeys=okeys, Vv=Vv, g=g, kts=kts: pg.mm(
                                O[h4][0:65, :], Vv[:, kt, g, :], eT2[k][:, :], start=(n_ == 0), stop=(n_ == len(kts) - 1),
                                reads=[("eT2", k)], writes=[okeys[h4]]))
                            if len(dq) > 2:
                                dq.pop(0)()
                    while dq:
                        dq.pop(0)()
                    for h4 in range(4):
                        h = g * 4 + h4
                        o = h4 % 2
                        pg.cp("act", osT[o][:, :], O[h4][0:65, :], reads=[okeys[h4]], writes=[("osT", o)])
                        r3 = nxt(rs, 3)
                        Tp = psS[r3]
                        for qq in range(4):
                            pg.tr(Tp[:, qq * 65:(qq + 1) * 65], osT[o][0:65, qq * 128:(qq + 1) * 128], idf[0:65, 0:65],
                                  reads=[("osT", o), "idf"], writes=[("psS", r3)])
                        T3 = Tp[:, 0:260].rearrange("p (q c) -> p q c", c=65)
                        pg.ts("dve", dn2[:, 0:4], T3[:, :, 64], 1e-30, None, ALU.max, reads=[("psS", r3)], writes=["dn2a"])
                        pg.op("dve", lambda e: e.reciprocal(dn2[:, 4:8], dn2[:, 0:4]), reads=["dn2a"], writes=["dn2b"])
                        pg.tt("dve", dest[:, :, h, :], T3[:, :, 0:64], dn2[:, 4:8].unsqueeze(2).to_broadcast([128, 4, 64]), ALU.mult,
                              reads=[("psS", r3), "dn2b"], writes=[dkey])
            for ii in range(4):
                i = Q * 4 + ii
                t0 = i * 128
                G3 = GT[:, i, :].rearrange("p (h c) -> p h c", c=3)
                gb = lambda c: G3[:, :, c].unsqueeze(2).to_broadcast([128, 8, 64])
                pg.tt("dve", yb[:], ocmp[:, ii, :, :], gb(0), ALU.mult, reads=["ocmp", "GT"], writes=["yb"])
                pg.tt("pool", yb2[:], osel[:, ii, :, :], gb(1), ALU.mult, reads=["osel", "GT"], writes=["yb2"])
                pg.tt("dve", yb[:], yb[:], yb2[:], ALU.add, reads=["yb", "yb2"], writes=["yb"])
                pg.tt("pool", yb2[:], owin[:, ii, :, :], gb(2), ALU.mult, reads=["owin", "GT", "yb2"], writes=["yb2"])
                pg.tt("dve", yb[:], yb[:], yb2[:], ALU.add, reads=["yb", "yb2"], writes=["yb"])
                pg.ld(dr["YB"][t0:t0 + 128, :], yb[:].rearrange("p h k -> p (h k)"), reads=["yb"], writes=[("YB", i)])
        pg.barrier()
        pg.emit()


_NSA_CONSTS = {}


def nsa_consts(hh=1):
    if hh in _NSA_CONSTS:
        return _NSA_CONSTS[hh]
    import ml_dtypes
    bf = ml_dtypes.bfloat16
    c = {}
    n = np.arange(256)
    t = np.arange(T)
    nlo = 128 if hh == 0 else 0
    cm = np.where((16 * n[:, None] + 31 <= t[None, :]) & (n[:, None] < 255) & (n[:, None] >= nlo), 0.0, NEG).astype(np.float32)
    c["c_cmpb"] = np.ascontiguousarray(cm.reshape(2, 128, T).transpose(1, 0, 2)).astype(bf)
    es = np.zeros((64, 32, 128), np.float32)
    for kt in range(32):
        for key in range(128):
            es[2 * kt + key // 64, kt, key] = 1.0
    c["c_esel"] = np.concatenate([es, es], 0).astype(bf)
    key = np.arange(128)
    q = np.arange(512)
    cb = np.zeros((128, 4, 512), np.float32)
    for d in range(4):
        cb[:, d, :] = np.where((d * 128 + key[:, None]) <= q[None, :], 1.0, 0.0)
    c["c_causb"] = cb.astype(bf)
    wb = np.zeros((128, 8, 512), np.float32)
    for r in range(8):
        ka = (r - 4) * 128 + key[:, None]
        wb[:, r, :] = np.where((ka <= q[None, :]) & (ka > q[None, :] - 512), 1.0, 0.0)
    c["c_winb"] = wb.astype(bf)
    wb4 = wb.copy()
    if hh == 0:
        wb4[:, 0:4, :] = 0.0
    c["c_winb4"] = wb4.astype(bf)
    cs = np.arange(256) * 16
    ss = np.arange(64) * 64
    ov = np.clip(np.minimum(cs[:, None] + 32, ss[None, :] + 64) - np.maximum(cs[:, None], ss[None, :]), 0, None) / 32.0
    ov[255, :] = 0.0
    c["ovl"] = np.ascontiguousarray(ov.reshape(2, 128, 64).transpose(1, 0, 2)).astype(np.float32)
    cur = t // 64
    j = np.arange(64)
    jlo = 32 if hh == 0 else 0
    valid = (j[None, :] <= cur[:, None]) & (j[None, :] >= jlo)
    forced = (j[None, :] == jlo) | (j[None, :] == cur[:, None]) | (j[None, :] == cur[:, None] - 1)
    vm = valid.astype(np.float32)
    fb = np.where(valid, 1000.0 * forced, -1.0).astype(np.float32)
    c["c_vmfb"] = np.ascontiguousarray(np.stack([vm, fb], 1).reshape(NT, 128, 2, 64))
    _NSA_CONSTS[hh] = c
    return c


def stage4(C):
    nc, pg, dr = C.nc, C.pg, C.dr
    with ExitStack() as st:
        sb, ps = _mk(C, st)
        wa = sb("m_wa", [128, 4, D], BF16)
        wb = sb("m_wb", [128, 4, D], BF16)
        wo = sb("m_wo", [128, 8, D], BF16)
        stg = sb("m_stg", [128, D], F32)
        idf = sb("m_idf", [128, 128], F32)
        idb = sb("m_idb", [128, 128], BF16)
        yab = [sb(f"m_yab{i}", [128, 1024], F32) for i in range(2)]
        yabb = sb("m_yabb", [128, 1024], BF16)
        yT = sb("m_yT", [128, 8, 128], BF16)
        gts = [sb(f"m_g{i}", [128, 2048], F32) for i in range(2)]
        xt = [sb(f"m_x{i}", [128, D], F32) for i in range(2)]
        mix = sb("m_mix", [128, D], F32)
        mix2 = sb("m_mix2", [128, D], F32)
        mixb = sb("m_mixb", [128, D], BF16)
        mT = sb("m_mT", [128, 8, 128], BF16)
        x1 = [sb(f"m_x1{i}", [128, D], F32) for i in range(2)]
        psT = ps("m_psT", [128, 1024], BF16)
        psm = [ps(f"m_psm{i}", [128, 512], F32) for i in range(4)]
        psT2 = ps("m_psT2", [128, 1024], BF16)
        pso = [ps(f"m_pso{i}", [128, 512], F32) for i in range(2)]

        pg.ld(idf[:], dr["ident"][:, :], writes=["idf"])
        pg.cp("dve", idb[:], idf[:], reads=["idf"], writes=["idb"])
        n = 0
        for (wt, nm, kcs) in ((wa, "w_branch_a", 4), (wb, "w_branch_b", 4), (wo, "w_out", 8)):
            for kc in range(kcs):
                pg.ld(stg[:], dr[nm][kc * 128:(kc + 1) * 128, :], writes=["stg"])
                pg.cp(("act", "dve", "pool")[n % 3], wt[:, kc, :], stg[:], reads=["stg"], writes=[nm])
                n += 1
        rowidx = sb("m_rowidx", [128, NT], I32)
        pg.ld(rowidx[:], dr["rowidx"][:, :], writes=["rowidx"])
        allk = lambda nm: [(nm, k) for k in range(NT)]
        for i in range(C.peer_tiles):
            s = i % 2
            t0 = i * 128
            ix = rowidx[:, i:i + 1]
            pg.ld(yab[s][:, 0:512], dr["YAL"][t0:t0 + 128, :], reads=[("YAL", i)], writes=[("yab", s)])
            igather(pg, yab[s][:, 512:1024], dr["YB"][:, :], ix, allk("YB") + ["rowidx"], [("yab2", s)])
            igather(pg, gts[s][:], dr["PG"][:, :], ix, allk("PG") + ["rowidx"], [("gts", s)])
            igather(pg, xt[s][:], dr["x"][:, :], ix, ["rowidx"], [("xt", s)])
            pg.cp("pool", yabb[:], yab[s][:], reads=[("yab", s), ("yab2", s)], writes=["yabb"])
            for j in range(8):
                pg.tr(psT[:, j * 128:(j + 1) * 128], yabb[:, j * 128:(j + 1) * 128], idb[:], reads=["yabb", "idb"], writes=["psT"])
            pg.cp("act", yT[:].rearrange("p a b -> p (a b)"), psT[:], reads=["psT"], writes=["yT"])
            for br in range(2):
                wt = wa if br == 0 else wb
                for nchunk in range(2):
                    pb = psm[br * 2 + nchunk]
                    for kc in range(4):
                        pg.mm(pb[:], yT[:, br * 4 + kc, :], wt[:, kc, nchunk * 512:(nchunk + 1) * 512], start=(kc == 0), stop=(kc == 3),
                              reads=["yT", "w_branch_a", "w_branch_b"], writes=[("psm", br * 2 + nchunk)])
            pg.act(gts[s][:], gts[s][:], AF.Sigmoid, reads=[("gts", s)], writes=[("gts", s)])
            for nchunk in range(2):
                c = slice(nchunk * 512, (nchunk + 1) * 512)
                pg.tt("dve", mix[:, c], psm[nchunk][:], gts[s][:, nchunk * 512:(nchunk + 1) * 512], ALU.mult,
                      reads=[("psm", nchunk), ("gts", s)], writes=[("mix", nchunk)])
                pg.tt("dve", mix2[:, c], psm[2 + nchunk][:], gts[s][:, 1024 + nchunk * 512:1024 + (nchunk + 1) * 512], ALU.mult,
                      reads=[("psm", 2 + nchunk), ("gts", s)], writes=[("mix2", nchunk)])
                pg.tt("pool", mixb[:, c], mix[:, c], mix2[:, c], ALU.add, reads=[("mix", nchunk), ("mix2", nchunk)], writes=[("mixb", nchunk)])
            for j in range(8):
                pg.tr(psT2[:, j * 128:(j + 1) * 128], mixb[:, j * 128:(j + 1) * 128], idb[:], reads=[("mixb", 0), ("mixb", 1), "idb"], writes=["psT2"])
            pg.cp("act", mT[:].rearrange("p a b -> p (a b)"), psT2[:], reads=["psT2"], writes=["mT"])
            for nchunk in range(2):
                for kc in range(8):
                    pg.mm(pso[nchunk][:], mT[:, kc, :], wo[:, kc, nchunk * 512:(nchunk + 1) * 512], start=(kc == 0), stop=(kc == 7),
                          reads=["mT", "w_out"], writes=[("pso", nchunk)])
                pg.tt("dve", x1[s][:, nchunk * 512:(nchunk + 1) * 512], pso[nchunk][:], xt[s][:, nchunk * 512:(nchunk + 1) * 512], ALU.add,
                      reads=[("pso", nchunk), ("xt", s)], writes=[("x1", s, nchunk)])
            pg.ld(dr["X1L"][t0:t0 + 128, :], x1[s][:], reads=[("x1", s, 0), ("x1", s, 1)], writes=[("X1L", i)])
        pg.barrier()
        pg.emit()


def table_conv_gen(C, sb):
    pg, dr = C.pg, C.dr
    NBUF = 4
    src = [sb(f"z_src{i}", [128, D], F32) for i in range(NBUF)]
    dst = [sb(f"z_dst{i}", [128, D], BF16) for i in range(NBUF)]
    n = 0
    for (tab, co) in (("peer_u", 0), ("peer_v", D)):
        for a in range(16384 // 128):
            b_ = n % NBUF
            pg.ld(src[b_][:], dr[tab][a * 128:(a + 1) * 128, :], writes=[("zsrc", b_)], q="sp")
            pg.cp("act", dst[b_][:], src[b_][:], reads=[("zsrc", b_)], writes=[("zdst", b_)])
            pg.ld(dr["UV"][a * 128:(a + 1) * 128, co:co + D], dst[b_][:], reads=[("zdst", b_)], writes=[("UV", co, a)], q="act")
            n += 1
            yield


def stage5(C):
    nc, pg, dr = C.nc, C.pg, C.dr
    NB = 12
    with ExitStack() as st:
        sb, ps = _mk(C, st)
        wq = sb("p_wq", [128, 8, 2048], F32)
        kT = sb("p_kT", [128, 2, 128], F32)
        kraw = sb("p_kraw", [128, 2, 128], F32)
        g2 = sb("p_g2", [128, D], F32)
        idf = sb("p_idf", [128, 128], F32)
        io16 = sb("p_io16", [128, 16], F32)
        eps = sb("p_eps", [128, 1], F32)
        x1 = [sb(f"p_x1{i}", [128, D], F32) for i in range(3)]
        h2 = [sb(f"p_h2{i}", [128, D], F32) for i in range(2)]
        junk = sb("p_junk", [128, D], BF16)

        ss = sb("p_ss", [128, 4], F32)
        h2T = sb("p_h2T", [128, 8, 128], F32)
        qT = sb("p_qT", [128, 16, 128], F32)
        sc = sb("p_sc", [128, 16, 128], F32)
        work = sb("p_work", [128, 256], F32)
        tv = sb("p_tv", [128, 16, 16], F32)
        tiu = sb("p_tiu", [128, 16, 16], U32)
        ti = sb("p_ti", [128, 16, 16], F32)
        cs = sb("p_cs", [128, 8, 256], F32)
        bs = sb("p_bs", [128, 8, 16], F32)
        posu = sb("p_posu", [128, 8, 16], U32)
        pa_u = sb("p_pau", [128, 8, 16], U32)
        pb_u = sb("p_pbu", [128, 8, 16], U32)
        pa = sb("p_pa", [128, 8, 16], F32)
        pb = sb("p_pb", [128, 8, 16], F32)
        oh = sb("p_oh", [128, 8, 16, 16], F32)
        ia = sb("p_ia", [128, 8, 16], F32)
        ib = sb("p_ib", [128, 8, 16], F32)
        eidf = sb("p_eidf", [128, 128], F32)
        eidi = [sb(f"p_eidi{i}", [128, 128], I32) for i in range(3)]
        gate = [sb(f"p_gate{i}", [128, 128], F32) for i in range(2)]
        zz = sb("p_zz", [128, 16], F32)
        actv = [sb(f"p_act{i}", [128, 128], F32) for i in range(2)]
        ga = [sb(f"p_ga{i}", [128, 128], F32) for i in range(2)]
        uv = [sb(f"p_uv{i}", [128, 2 * D], BF16) for i in range(NB)]
        h2b = [sb(f"p_h2b{i}", [128, D], BF16) for i in range(2)]
        idb = sb("p_idb", [128, 128], BF16)
        junk2 = sb("p_junk2", [128, D], F32)
        dg = [sb(f"p_dg{i}", [128, 128], BF16) for i in range(4)]
        yo = [sb(f"p_yo{i}", [128, D], F32) for i in range(1)]
        psT = ps("p_psT", [128, 8, 128], F32)
        psQ = [ps(f"p_psQ{i}", [128, 512], F32) for i in range(2)]
        psY = [ps(f"p_psY{i}", [128, 512], F32) for i in range(2)]

        pg.ld(idf[:], dr["ident"][:, :], writes=["idf"])
        pg.ld(g2[:], dr["norm2_g_b"][:, :], writes=["g2"])
        pg.cp("dve", idb[:], idf[:], reads=["idf"], writes=["idb"])
        pg.ld(io16[:], dr["iota16"][:, :], writes=["io16"])
        rowidx = sb("p_rowidx", [128, NT], I32)
        pg.ld(rowidx[:], dr["rowidx"][:, :], writes=["rowidx"])
        pg.memset("dve", eps[:], 1e-6, writes=["eps"])
        for kc in range(8):
            pg.ld(wq[:, kc, :], dr["peer_wq"][kc * 128:(kc + 1) * 128, :], writes=["wq"])
        pg.ld(kraw[:, 0, :], dr["peer_k1"][:, :], writes=["kraw"])
        pg.ld(kraw[:, 1, :], dr["peer_k2"][:, :], writes=["kraw"])
        for hf in range(2):
            pg.tr(psQ[0][:, hf * 128:(hf + 1) * 128], kraw[:, hf, :], idf[:], reads=["kraw", "idf"], writes=[("psQ", 0)])
        pg.cp("dve", kT[:].rearrange("p a b -> p (a b)"), psQ[0][:, 0:256], reads=[("psQ", 0)], writes=["kT"])
        ntiles = getattr(C, "peer_tiles", NT)

        def front(i):
            s = i % 2
            t0 = i * 128
            pg.ld(x1[i % 3][:, :], dr["X1L"][t0:t0 + 128, :], reads=[("X1L", i)], writes=[("x1", i % 3)])
            yield
            pg.tt("pool", junk2[:], x1[i % 3][:], x1[i % 3][:], ALU.mult, reads=[("x1", i % 3), "junk2"], writes=["junk2"])
            yield
            pg.red("dve", ss[:, 0:1], junk2[:], ALU.add, reads=["junk2"], writes=["ss0"])
            yield
            pg.act(ss[:, 1:2], ss[:, 0:1], AF.Sqrt, reads=["ss0", "eps"], writes=["ss1"], scale=1.0 / D, bias=eps[:, 0:1])
            yield
            pg.op("dve", lambda e: e.reciprocal(ss[:, 2:3], ss[:, 1:2]), reads=["ss1"], writes=["ss2"])
            yield
            pg.stt("dve", h2[s][:], x1[i % 3][:], ss[:, 2:3], g2[:], ALU.mult, ALU.mult, reads=[("x1", i % 3), "ss2", "g2"], writes=[("h2", s)])
            yield
            pg.cp("pool", h2b[s][:], h2[s][:], reads=[("h2", s)], writes=[("h2b", s)])
            yield
            for j in range(8):
                pg.tr(psT[:, j, :], h2[s][:, j * 128:(j + 1) * 128], idf[:], reads=[("h2", s), "idf"], writes=["psT"])
                yield
            pg.cp("act", h2T[:], psT[:], reads=["psT"], writes=["h2T"])
            yield
            for cg in range(4):
                bk = psQ[cg % 2]
                for cc in range(4):
                    c = cg * 4 + cc
                    for kc in range(8):
                        pg.mm(bk[:, cc * 128:(cc + 1) * 128], wq[:, kc, c * 128:(c + 1) * 128], h2T[:, kc, :], start=(kc == 0), stop=(kc == 7),
                              reads=["wq", "h2T"], writes=[("psQ", cg % 2)])
                        yield
                pg.cp("act" if cg % 2 == 0 else "dve", qT[:, cg * 4:(cg + 1) * 4, :].rearrange("p a b -> p (a b)"), bk[:],
                      reads=[("psQ", cg % 2)], writes=[("qT", cg)])
                yield
            for cg in range(4):
                bk = psQ[cg % 2]
                for cc in range(4):
                    c = cg * 4 + cc
                    pg.mm(bk[:, cc * 128:(cc + 1) * 128], qT[:, c, :], kT[:, c % 2, :], reads=[("qT", cg), "kT"], writes=[("psQ", cg % 2)])
                    yield
                pg.cp("act" if cg % 2 == 0 else "dve", sc[:, cg * 4:(cg + 1) * 4, :].rearrange("p a b -> p (a b)"), bk[:],
                      reads=[("psQ", cg % 2)], writes=[("sc", cg)])
                yield
            for c in range(16):
                k_ = ("sc", c // 4)
                pg.op("dve", lambda e, c=c: e.max(tv[:, c, 0:8], sc[:, c, :]), reads=[k_], writes=[("tv", c)])
                yield
                pg.op("dve", lambda e, c=c: e.max_index(tiu[:, c, 0:8], tv[:, c, 0:8], sc[:, c, :]), reads=[k_, ("tv", c)], writes=[("tiu", c)])
                yield
                pg.op("dve", lambda e, c=c: e.match_replace(work[:, 0:128], tv[:, c, 0:8], sc[:, c, :], -1e30), reads=[k_, ("tv", c), "work"], writes=["work"])
                yield
                pg.op("dve", lambda e, c=c: e.max(tv[:, c, 8:16], work[:, 0:128]), reads=["work"], writes=[("tv2", c)])
                yield
                pg.op("dve", lambda e, c=c: e.max_index(tiu[:, c, 8:16], tv[:, c, 8:16], sc[:, c, :]), reads=[k_, ("tv2", c)], writes=[("tiu2", c)])
                yield
            allt = [("tv", c) for c in range(16)] + [("tv2", c) for c in range(16)]
            alli = [("tiu", c) for c in range(16)] + [("tiu2", c) for c in range(16)]
            pg.cp("dve", ti[:], tiu[:], reads=alli, writes=["ti"])
            yield
            tv4 = tv[:].rearrange("p (h f) a -> p h f a", f=2)
            ti4 = ti[:].rearrange("p (h f) a -> p h f a", f=2)
            cs4 = cs[:].rearrange("p h (a b) -> p h a b", a=16)
            A_ = lambda t4: t4[:, :, 0, :].unsqueeze(3).to_broadcast([128, 8, 16, 16])
            B_ = lambda t4: t4[:, :, 1, :].unsqueeze(2).to_broadcast([128, 8, 16, 16])
            pg.tt("dve", cs4, A_(tv4), B_(tv4), ALU.add, reads=allt, writes=["cs"])
            yield
            for h in range(8):
                pg.op("dve", lambda e, h=h: e.max(bs[:, h, 0:8], cs[:, h, :]), reads=["cs"], writes=[("bs", h)])
                yield
                pg.op("dve", lambda e, h=h: e.max_index(posu[:, h, 0:8], bs[:, h, 0:8], cs[:, h, :]), reads=["cs", ("bs", h)], writes=[("posu", h)])
                yield
                pg.op("dve", lambda e, h=h: e.match_replace(work[:, :], bs[:, h, 0:8], cs[:, h, :], -1e30), reads=["cs", ("bs", h), "work"], writes=["work"])
                yield
                pg.op("dve", lambda e, h=h: e.max(bs[:, h, 8:16], work[:, :]), reads=["work"], writes=[("bs2", h)])
                yield
                pg.op("dve", lambda e, h=h: e.max_index(posu[:, h, 8:16], bs[:, h, 8:16], cs[:, h, :]), reads=["cs", ("bs2", h)], writes=[("posu2", h)])
                yield
            allb = [("bs", h) for h in range(8)] + [("bs2", h) for h in range(8)]
            allp = [("posu", h) for h in range(8)] + [("posu2", h) for h in range(8)]
            G = gate[s][:].rearrange("p (h j) -> p h j", h=8)
            pg.tt("dve", G, bs[:], bs[:, :, 0:1].to_broadcast([128, 8, 16]), ALU.subtract, reads=allb, writes=[("gate", s)])
            yield
            pg.act(G, G, AF.Exp, reads=[("gate", s)], writes=[("gate", s)])
            yield
            pg.red("dve", zz[:, 0:8], G, ALU.add, reads=[("gate", s)], writes=["zz0"])
            yield
            pg.op("dve", lambda e: e.reciprocal(zz[:, 8:16], zz[:, 0:8]), reads=["zz0"], writes=["zz1"])
            yield
            pg.tt("dve", G, G, zz[:, 8:16].unsqueeze(2).to_broadcast([128, 8, 16]), ALU.mult, reads=[("gate", s), "zz1"], writes=[("gate", s)])
            yield
            pg.ts("dve", pa_u[:], posu[:], 4, None, ALU.logical_shift_right, reads=allp, writes=["pau"])
            yield
            pg.ts("dve", pb_u[:], posu[:], 15, None, ALU.bitwise_and, reads=allp, writes=["pbu"])
            yield
            pg.cp("dve", pa[:], pa_u[:], reads=["pau"], writes=["pa"])
            yield
            pg.cp("dve", pb[:], pb_u[:], reads=["pbu"], writes=["pb"])
            yield
            iob = io16[:, :].unsqueeze(1).unsqueeze(1).to_broadcast([128, 8, 16, 16])
            for (pp, key, half, dst, dk_) in ((pa, "pa", 0, ia, "ia"), (pb, "pb", 1, ib, "ib")):
                pg.tt("dve", oh[:], pp[:].unsqueeze(3).to_broadcast([128, 8, 16, 16]), iob, ALU.is_equal, reads=[key, "io16", "oh"], writes=["oh"])
                yield
                tsel = ti4[:, :, half, :].unsqueeze(2).to_broadcast([128, 8, 16, 16])
                pg.tt("dve", oh[:], oh[:], tsel, ALU.mult, reads=["oh", "ti"], writes=["oh"])
                yield
                pg.red("dve", dst[:], oh[:], ALU.add, reads=["oh"], writes=[dk_])
                yield
            pg.stt("dve", eidf[:].rearrange("p (h j) -> p h j", h=8), ia[:], 128.0, ib[:], ALU.mult, ALU.add, reads=["ia", "ib"], writes=["eidf"])
            yield
            pg.cp("dve", eidi[i % 3][:], eidf[:], reads=["eidf"], writes=[("eidi", i % 3)])
            yield

        GS = 2
        SK = 1

        def gstep(i, e_):
            s = i % 2
            b_ = e_ % NB
            pg.dma("pool", lambda e, e_=e_, b_=b_, i=i: e.indirect_dma_start(
                out=uv[b_][:, :], out_offset=None, in_=dr["UV"][:, :],
                in_offset=bass.IndirectOffsetOnAxis(ap=eidi[i % 3][:, e_:e_ + 1], axis=0)),
                reads=[("eidi", i % 3)], writes=[("uv", b_)])
            pg.op("dve", lambda e, e_=e_, b_=b_, s=s: e.scalar_tensor_tensor(junk[:], uv[b_][:, 0:D], 1.0, h2b[s][:], ALU.mult, ALU.mult,
                                                                              accum_out=actv[s][:, e_:e_ + 1]),
                  reads=[("uv", b_), ("h2b", s)], writes=[("act", s, e_)])

        def gelu_grp(i, k):
            s = i % 2
            sl = slice(k * GS, (k + 1) * GS)
            pg.act(ga[s][:, sl], actv[s][:, sl], AF.Gelu, reads=[("act", s, e_) for e_ in range(k * GS, (k + 1) * GS)], writes=[("ga", s, k)])

        def fin_grp(i, k):
            s = i % 2
            sl = slice(k * GS, (k + 1) * GS)
            pg.tt("dve", ga[s][:, sl], ga[s][:, sl], gate[s][:, sl], ALU.mult, reads=[("ga", s, k), ("gate", s)], writes=[("ga", s, k)])
            for e_ in range(k * GS, (k + 1) * GS):
                b_ = e_ % NB
                d_ = e_ % 4
                pg.act(dg[d_][:], idb[:], AF.Copy, reads=[("ga", s, k), "idb"], writes=[("dg", d_)], scale=ga[s][:, e_:e_ + 1])
                for n_ in range(2):
                    pg.mm(psY[n_][:], dg[d_][:], uv[b_][:, D + n_ * 512:D + (n_ + 1) * 512], start=(e_ == 0), stop=(e_ == 127),
                          reads=[("dg", d_), ("uv", b_)], writes=[("psY", n_)])

        def tail(i):
            s = i % 2
            t0 = i * 128
            for n_ in range(2):
                pg.tt("dve", yo[0][:, n_ * 512:(n_ + 1) * 512], psY[n_][:], x1[i % 3][:, n_ * 512:(n_ + 1) * 512], ALU.add,
                      reads=[("psY", n_), ("x1", i % 3)], writes=[("yo", 0, n_)])
            pg.ld(dr["out"][t0:t0 + 128, :], yo[0][:], reads=[("yo", 0, 0), ("yo", 0, 1)], writes=[("out", i)])

        def drain(g, n=None):
            k = 0
            while g is not None and (n is None or k < n):
                try:
                    next(g)
                except StopIteration:
                    return None
                k += 1
            return g

        drain(front(0))
        for i in range(ntiles):
            gen2 = front(i + 1) if i + 1 < ntiles else None
            for k in range(128 // GS):
                for e_ in range(k * GS, (k + 1) * GS):
                    gstep(i, e_)
                    gen2 = drain(gen2, 3)
                gelu_grp(i, k)
                if k >= SK:
                    fin_grp(i, k - SK)
            for k in range(128 // GS - SK, 128 // GS):
                fin_grp(i, k)
            drain(gen2)
            tail(i)
        pg.barrier()
        pg.emit()


_NC_CACHE = {}


def kernel(**inputs):
    inputs = {k: np.asarray(v) for k, v in inputs.items()}
    ntl = NT // 2
    if "nc" not in _NC_CACHE:
        _NC_CACHE["nc"] = build([stage1, stage2a, stage2x, stage2c, stage3, stage4, stage5], peer_tiles=ntl)
    nc = _NC_CACHE["nc"]
    base = {}
    in_maps = []
    for c in range(8):
        b, hh = c % 4, c // 4
        if hh not in base:
            base[hh] = host_inputs(inputs, b, hh, ntl)
            m = base[hh]
        else:
            m = dict(base[hh])
            m["x"] = core_x(inputs, b, hh)
        in_maps.append(m)
    res = run_bass_kernel_spmd(nc, in_maps, core_ids=list(range(8)))
    out = np.zeros((4, T, D), np.float32)
    for c in range(8):
        b, hh = c % 4, c // 4
        out[b, hh * ntl * 128:(hh + 1) * ntl * 128, :] = res.results[c]["out"]
    return out


def stage2x(C):
    nc, pg, dr = C.nc, C.pg, C.dr
    with ExitStack() as st:
        sb, ps = _mk(C, st)
        idf = sb("x_idf", [128, 128], F32)
        tri = sb("x_tri", [128, 128], F32)
        msk = sb("x_msk", [128, 3, 128], F32)
        ones = sb("x_ones", [128, 1], F32)
        inp = [[sb(f"x_in{s}_{j}", [128, 512], F32) for j in range(6)] for s in range(2)]
        Pt = sb("x_P", [128, 512], F32)
        iP = sb("x_iP", [128, 512], F32)
        Pp = sb("x_Pp", [128, 512], F32)
        tm = [[sb(f"x_tm{s}_{j}", [128, 512], F32) for j in range(4)] for s in range(2)]
        fm = [[sb(f"x_fm{s}_{j}", [64, 8, 128], F32) for j in range(4)] for s in range(2)]
        M = [[sb(f"x_M{s}_{j}", [128, 8, 128], (BF16 if j in (0, 4) else F32)) for j in range(5)] for s in range(2)]
        Xb = sb("x_Xb", [128, 8, 128], BF16)
        idb = sb("x_idb", [128, 128], BF16)
        X = [sb(f"x_X{s}", [128, 8, 128], F32) for s in range(2)]
        PC = [sb(f"x_PC{s}", [64, 8], F32) for s in range(2)]
        N2 = [sb(f"x_N2_{j}", [128, 8, 128], BF16) for j in range(2)]
        N2T = [sb(f"x_N2T_{j}", [128, 8, 128], BF16) for j in range(2)]
        Z = [sb(f"x_Z{j}", [64, 512], F32) for j in range(2)]
        rhs_sb = sb("x_rhs", [128, 512], F32)
        U_sb = sb("x_U", [128, 512], F32)
        Y_sb = [sb(f"x_Y{j}", [128, 512], F32) for j in range(2)]
        bank = [ps(f"x_bank{j}", [128, 512], F32) for j in range(8)]

        pg.ld(idf[:], dr["ident"][:, :], writes=["idf"])
        pg.ld(tri[:], dr["c_tri"][:, :], writes=["tri"])
        pg.ld(msk[:], dr["c_msk"][:, :, :], writes=["msk"])
        pg.memset("dve", ones[:], 1.0, writes=["ones"])
        pg.cp("dve", idb[:], idf[:], reads=["idf"], writes=["idb"])
        pg.memset("dve", Z[0][:], 0.0, writes=[("Z", 0)])
        names = ("RR", "RKK", "RLW", "RB", "RKp", "RV")
        bk = [0]

        def nb():
            v = bk[0]
            bk[0] = (v + 1) % 8
            return v

        def pre(c):
            s = c % 2
            t0 = c * 128
            I = inp[s]
            for j, nm in enumerate(names):
                pg.ld(I[j][:], dr[nm][t0:t0 + 128, :], reads=[(nm, c)], writes=[("in", s, j)])
            r_, kkn, lw, b_, kp, v_ = [t_[:] for t_ in I]
            bL = nb()
            pg.mm(bank[bL][:], tri[:], lw, reads=["tri", ("in", s, 2)], writes=[("bank", bL)])
            bC = nb()
            for h in range(8):
                pg.mm(bank[bC][0:64, h:h + 1], I[2][:, h * 64:(h + 1) * 64], ones[:, 0:1], reads=[("in", s, 2), "ones"], writes=[("bank", bC)])
            pg.act(PC[s][:], bank[bC][0:64, 0:8], AF.Exp, reads=[("bank", bC)], writes=[("PC", s)])
            pg.act(Pt[:], bank[bL][:], AF.Exp, reads=[("bank", bL)], writes=["P"])
            pg.act(iP[:], bank[bL][:], AF.Exp, reads=[("bank", bL)], writes=["iP"], scale=-1.0)
            pg.tt("dve", Pp[:], bank[bL][:], lw, ALU.subtract, reads=[("bank", bL), ("in", s, 2)], writes=["Pp"])
            pg.act(Pp[:], Pp[:], AF.Exp, reads=["Pp"], writes=["Pp"])
            TM = tm[s]
            pg.tt("pool", TM[0][:], r_, Pt[:], ALU.mult, reads=[("in", s, 0), "P"], writes=[("tm", s, 0)])
            pg.stt("dve", TM[1][:], kkn, -1.0, Pp[:], ALU.mult, ALU.mult, reads=[("in", s, 1), "Pp"], writes=[("tm", s, 1)])
            pg.tt("pool", TM[2][:], b_, iP[:], ALU.mult, reads=[("in", s, 3), "iP"], writes=[("tm", s, 2)])
            pg.tt("dve", TM[3][:], kp, iP[:], ALU.mult, reads=[("in", s, 4), "iP"], writes=[("tm", s, 3)])
            for j in range(4):
                if j == 0 and c < NT // 2:
                    continue
                for hg in range(2):
                    bT = nb()
                    for hh in range(4):
                        h = hg * 4 + hh
                        pg.tr(bank[bT][0:64, hh * 128:(hh + 1) * 128], TM[j][:, h * 64:(h + 1) * 64], idf[:], reads=[("tm", s, j), "idf"], writes=[("bank", bT)])
                    pg.cp("act" if (j + hg) % 2 == 0 else "dve", fm[s][j][:, hg * 4:(hg + 1) * 4, :].rearrange("p a b -> p (a b)"), bank[bT][0:64, :],
                          reads=[("bank", bT)], writes=[("fm", s, j, hg)])
            FR, FKK, FB, FK = fm[s]
            combos = ((0, FB, 2, FKK, 1, 0), (1, FK, 3, FKK, 1, 0), (2, FB, 2, FR, 0, 1), (3, FK, 3, FR, 0, 1), (4, FKK, 1, FB, 2, 2))
            for hg in range(2):
                for (mi, L_, lj, R_, rj, mk) in combos:
                    if mi in (2, 3) and c < NT // 2:
                        continue
                    bM = nb()
                    for hh in range(4):
                        h = hg * 4 + hh
                        pg.mm(bank[bM][:, hh * 128:(hh + 1) * 128], L_[:, h, :], R_[:, h, :], reads=[("fm", s, lj, hg), ("fm", s, rj, hg)], writes=[("bank", bM)])
                    pg.tt("dve", M[s][mi][:, hg * 4:(hg + 1) * 4, :], bank[bM][:].rearrange("p (a b) -> p a b", a=4),
                          msk[:, mk, :].unsqueeze(1).to_broadcast([128, 4, 128]), ALU.mult, reads=[("bank", bM), "msk"], writes=[("M", s, mi, hg)])
                pg.tt("pool", Xb[:, hg * 4:(hg + 1) * 4, :], idb[:, :].unsqueeze(1).to_broadcast([128, 4, 128]), M[s][0][:, hg * 4:(hg + 1) * 4, :], ALU.subtract,
                      reads=[("M", s, 0, hg), "idb"], writes=[("Xb", hg)])
            curN = [M[s][0], M[s][0]]
            curNT = [M[s][4], M[s][4]]
            kN = [("M", s, 0, 0), ("M", s, 0, 1)]
            kNT = [("M", s, 4, 0), ("M", s, 4, 1)]
            for j in range(6):
                dst = j % 2
                for hg in range(2):
                    b1, b2 = nb(), nb()
                    for hh in range(4):
                        h = hg * 4 + hh
                        pg.mm(bank[b1][:, hh * 128:(hh + 1) * 128], curNT[hg][:, h, :], curN[hg][:, h, :], reads=[kN[hg], kNT[hg]], writes=[("bank", b1)])
                    for hh in range(4):
                        h = hg * 4 + hh
                        pg.mm(bank[b2][:, hh * 128:(hh + 1) * 128], curN[hg][:, h, :], curNT[hg][:, h, :], reads=[kN[hg], kNT[hg]], writes=[("bank", b2)])
                    pg.cp("act", N2[dst][:, hg * 4:(hg + 1) * 4, :].rearrange("p a b -> p (a b)"), bank[b1][:], reads=[("bank", b1)], writes=[("N2", dst, hg)])
                    pg.cp("dve", N2T[dst][:, hg * 4:(hg + 1) * 4, :].rearrange("p a b -> p (a b)"), bank[b2][:], reads=[("bank", b2)], writes=[("N2T", dst, hg)])
                for hg in range(2):
                    curN[hg], curNT[hg] = N2[dst], N2T[dst]
                    kN[hg], kNT[hg] = ("N2", dst, hg), ("N2T", dst, hg)
                for hg in range(2):
                    b3 = nb()
                    for hh in range(4):
                        h = hg * 4 + hh
                        pg.mm(bank[b3][:, hh * 128:(hh + 1) * 128], curNT[hg][:, h, :], Xb[:, h, :], reads=[kNT[hg], ("Xb", hg)], writes=[("bank", b3)])
                    xo = (X[s] if j == 5 else Xb)
                    pg.tt("dve", xo[:, hg * 4:(hg + 1) * 4, :].rearrange("p a b -> p (a b)"), Xb[:, hg * 4:(hg + 1) * 4, :].rearrange("p a b -> p (a b)"), bank[b3][:], ALU.add,
                          reads=[("bank", b3), ("Xb", hg)], writes=[("X", s, hg)] if j == 5 else [("Xb", hg)])

        def seq(c):
            s = c % 2
            t0 = c * 128
            zc, zn = Z[c % 2], Z[(c + 1) % 2]
            kz, kzn = ("Z", c % 2), ("Z", (c + 1) % 2)
            FR, FKK, FB, FK = fm[s]
            V = inp[s][5]
            hsl = lambda h: slice(h * 64, (h + 1) * 64)
            Mk = lambda mi: [("M", s, mi, 0), ("M", s, mi, 1)]
            fk = lambda j: [("fm", s, j, 0), ("fm", s, j, 1)]
            Xk = [("X", s, 0), ("X", s, 1)]
            bG = nb()
            for h in range(8):
                pg.mm(bank[bG][:, hsl(h)], M[s][1][:, h, :], V[:, hsl(h)], start=True, stop=False, reads=Mk(1) + [("in", s, 5)], writes=[("bank", bG)])
                pg.mm(bank[bG][:, hsl(h)], FKK[:, h, :], zc[:, hsl(h)], start=False, stop=True, reads=fk(1) + [kz], writes=[("bank", bG)])
            pg.ts("dve", rhs_sb[:], bank[bG][:], -1.0, None, ALU.mult, reads=[("bank", bG)], writes=["rhs"])
            bU = nb()
            for h in range(8):
                pg.mm(bank[bU][:, hsl(h)], X[s][:, h, :], rhs_sb[:, hsl(h)], reads=Xk + ["rhs"], writes=[("bank", bU)])
            pg.cp("act", U_sb[:], bank[bU][:], reads=[("bank", bU)], writes=["U"])
            bZ = nb()
            for h in range(8):
                pg.mm(bank[bZ][0:64, hsl(h)], tm[s][3][:, hsl(h)], V[:, hsl(h)], start=True, stop=False, reads=[("tm", s, 3), ("in", s, 5)], writes=[("bank", bZ)])
                pg.mm(bank[bZ][0:64, hsl(h)], idf[0:64, 0:64], zc[:, hsl(h)], start=False, stop=False, reads=["idf", kz], writes=[("bank", bZ)])
                pg.mm(bank[bZ][0:64, hsl(h)], tm[s][2][:, hsl(h)], U_sb[:, hsl(h)], start=False, stop=True, reads=[("tm", s, 2), "U"], writes=[("bank", bZ)])
            pg.tt("dve", zn[:].rearrange("p (h v) -> p h v", h=8), bank[bZ][0:64, :].rearrange("p (h v) -> p h v", h=8),
                  PC[s][:, :].unsqueeze(2).to_broadcast([64, 8, 64]), ALU.mult, reads=[("bank", bZ), ("PC", s)], writes=[kzn])
            if c < NT // 2:
                return
            bY = nb()
            for h in range(8):
                pg.mm(bank[bY][:, hsl(h)], M[s][3][:, h, :], V[:, hsl(h)], start=True, stop=False, reads=Mk(3) + [("in", s, 5)], writes=[("bank", bY)])
                pg.mm(bank[bY][:, hsl(h)], FR[:, h, :], zc[:, hsl(h)], start=False, stop=False, reads=fk(0) + [kz], writes=[("bank", bY)])
                pg.mm(bank[bY][:, hsl(h)], M[s][2][:, h, :], U_sb[:, hsl(h)], start=False, stop=True, reads=Mk(2) + ["U"], writes=[("bank", bY)])
            pg.cp("act", Y_sb[s][:], bank[bY][:], reads=[("bank", bY)], writes=[("Y", s)])
            pg.ld(dr["YS"][t0:t0 + 128, :], Y_sb[s][:], reads=[("Y", s)], writes=[("YS", c)])

        tcg = table_conv_gen(C, sb)
        pre(0)
        for c in range(NT):
            if c + 1 < NT:
                pre(c + 1)
            for _ in range(8):
                next(tcg, None)
            seq(c)
        for _ in tcg:
            pass
        pg.barrier()
        pg.emit()
```

```python
import numpy as np
import concourse.bass as bass
import concourse.mybir as mybir

F32 = mybir.dt.float32
BF16 = mybir.dt.bfloat16
I32 = mybir.dt.int32
U32 = mybir.dt.uint32
ALU = mybir.AluOpType
AF = mybir.ActivationFunctionType
AX = mybir.AxisListType

EPOCH = 20000
ENGS = ("pe", "act", "dve", "pool", "sp")
NDMASEM = 16


class Prog:
    def __init__(self, nc, stack):
        self.nc = nc
        self.stack = stack
        self.ops = {e: [] for e in ENGS}
        self.cnt = {e: 0 for e in ENGS}
        self.esems = {e: [] for e in ENGS}
        self.waited = {e: {} for e in ENGS}
        self.lastw = {}
        self.readers = {}
        self.dsems = {}
        self.dcount = {}
        self.dtarget = {}
        self.semobjs = {}
        self.alltokens = {}
        for q in ("sp", "act", "pool"):
            self.dsems[q] = [self._newsem(f"d_{q}_{i}") for i in range(NDMASEM)]
            self.dcount[q] = 0
            self.dtarget[q] = [0] * NDMASEM

    def _newsem(self, name):
        s = self.stack.enter_context(self.nc.semaphore(name))
        self.semobjs[name] = s
        return name

    def _esem(self, e, idx):
        ep = idx // EPOCH
        while len(self.esems[e]) <= ep:
            self.esems[e].append(self._newsem(f"e_{e}_{len(self.esems[e])}"))
        return self.esems[e][ep], (idx % EPOCH) + 1

    def _deps(self, reads, writes):
        toks = []
        for k in reads:
            t = self.lastw.get(k)
            if t is not None:
                toks.append(t)
        for k in writes:
            t = self.lastw.get(k)
            if t is not None:
                toks.append(t)
            toks.extend(self.readers.get(k, ()))
        return toks

    def _commit(self, tok, reads, writes):
        for k in reads:
            self.readers.setdefault(k, []).append(tok)
        for k in writes:
            self.lastw[k] = tok
            self.readers[k] = []
        self.alltokens[tok[0]] = max(self.alltokens.get(tok[0], 0), tok[1])

    def _waits(self, e, toks):
        need = {}
        for (s, v) in toks:
            if v > need.get(s, 0):
                need[s] = v
        out = []
        w = self.waited[e]
        for s, v in need.items():
            if w.get(s, 0) < v:
                w[s] = v
                out.append((s, v))
        return out

    def op(self, e, fn, reads=(), writes=()):
        toks = self._deps(reads, writes)
        if e == "pe":
            toks = [t for t in toks if not t[0].startswith("e_pe_")]
        waits = self._waits(e, toks)
        idx = self.cnt[e]
        self.cnt[e] += 1
        tok = self._esem(e, idx)
        self.ops[e].append((waits, fn, (tok[0], 1)))
        self._commit(tok, reads, writes)

    def dma(self, q, fn, reads=(), writes=()):
        toks = self._deps(reads, writes)
        n = self.dcount[q]
        self.dcount[q] += 1
        slot = n % NDMASEM
        sname = self.dsems[q][slot]
        prev = self.dtarget[q][slot]
        if prev > 0:
            toks.append((sname, prev))
        tgt = prev + 16
        self.dtarget[q][slot] = tgt
        waits = self._waits(q, toks)
        tok = (sname, tgt)
        self.ops[q].append((waits, fn, (sname, 16)))
        self._commit(tok, reads, writes)

    def mm(self, out, lhsT, rhs, start=True, stop=True, reads=(), writes=()):
        self.op("pe", lambda e: e.matmul(out, lhsT, rhs, start=start, stop=stop), reads, writes)

    def tr(self, out, in_, ident, reads=(), writes=()):
        self.op("pe", lambda e: e.transpose(out, in_, ident), reads, writes)

    def act(self, out, in_, func, reads=(), writes=(), bias=None, scale=None, eng="act"):
        kw = {}
        if bias is not None:
            kw["bias"] = bias
        if scale is not None:
            kw["scale"] = scale
        self.op(eng, lambda e: e.activation(out, in_, func, **kw), reads, writes)

    def tt(self, eng, out, in0, in1, op, reads=(), writes=()):
        self.op(eng, lambda e: e.tensor_tensor(out, in0, in1, op), reads, writes)

    def ts(self, eng, out, in0, s1, s2, op0, op1=None, reads=(), writes=()):
        if op1 is None:
            self.op(eng, lambda e: e.tensor_scalar(out, in0, s1, s2, op0), reads, writes)
        else:
            self.op(eng, lambda e: e.tensor_scalar(out, in0, s1, s2, op0, op1), reads, writes)

    def stt(self, eng, out, in0, scalar, in1, op0, op1, reads=(), writes=()):
        self.op(eng, lambda e: e.scalar_tensor_tensor(out, in0, scalar, in1, op0, op1), reads, writes)

    def cp(self, eng, out, in_, reads=(), writes=()):
        if eng == "act":
            self.op(eng, lambda e: e.copy(out, in_), reads, writes)
        else:
            self.op(eng, lambda e: e.tensor_copy(out, in_), reads, writes)

    def red(self, eng, out, in_, op, reads=(), writes=(), axis=None):
        ax = AX.X if axis is None else axis
        self.op(eng, lambda e: e.tensor_reduce(out, in_, ax, op), reads, writes)

    def memset(self, eng, ap, val, writes=()):
        self.op(eng, lambda e: e.memset(ap, val), (), writes)

    def ld(self, out, in_, reads=(), writes=(), q="sp"):
        self.dma(q, lambda e: e.dma_start(out, in_), reads, writes)

    def barrier(self):
        toks = list(self.alltokens.items())
        for e in ENGS:
            waits = self._waits(e, toks)
            if waits:
                self.ops[e].append((waits, None, None))
        self.lastw = {}
        self.readers = {}

    def emit(self):
        nc = self.nc
        so = self.semobjs
        with nc.Block() as block:
            def mk(e):
                def body(eng):
                    for waits, fn, inc in self.ops[e]:
                        for (s, v) in waits:
                            eng.wait_ge(so[s], v)
                        if fn is not None:
                            ins = fn(eng)
                            ins.then_inc(so[inc[0]], inc[1])
                return body
            block.tensor(mk("pe"))
            block.scalar(mk("act"))
            block.vector(mk("dve"))
            block.gpsimd(mk("pool"))
            block.sync(mk("sp"))
        self.ops = {e: [] for e in ENGS}
from contextlib import ExitStack
from concourse.bass_utils import run_bass_kernel_spmd

T = 4096
D = 1024
NT = T // 128
INW = 5144
RWC = 1792
O_RW = 0
O_Q = 1792
O_KC = 2304
O_VC = 2432
O_KS = 2560
O_VS = 2688
O_KW = 2816
O_VW = 2944
O_BG = 3072
O_GA = 3096
O_GB = 4120


class Ctx:
    pass


def _mk(C, st):
    nc = C.nc
    sb = lambda name, shape, dt: st.enter_context(nc.sbuf_tensor(name, shape, dt))
    ps = lambda name, shape, dt: st.enter_context(nc.psum_tensor(name, shape, dt))
    return sb, ps


def stage1(C):
    nc, pg, dr = C.nc, C.pg, C.dr
    with ExitStack() as st:
        sb, ps = _mk(C, st)
        win = sb("s1_win", [128, 8, INW], BF16)
        pj = [sb(f"s1_pj{i}", [128, INW], F32) for i in range(2)]
        xt = [sb(f"s1_xt{i}", [128, D], F32) for i in range(2)]
        junk = sb("s1_junk", [128, D], F32)
        hb = [sb(f"s1_h{i}", [128, D], BF16) for i in range(2)]
        hT = [sb(f"s1_hT{i}", [128, 8, 128], BF16) for i in range(2)]
        gt = sb("s1_g", [128, D], F32)
        idf = sb("s1_idf", [128, 128], F32)
        idb = sb("s1_idb", [128, 128], BF16)
        ss = [sb(f"s1_ss{i}", [128, 4], F32) for i in range(2)]
        psT = [ps(f"s1_psT{i}", [128, 8, 128], BF16) for i in range(2)]
        psm = [ps(f"s1_psm{i}", [128, 512], F32) for i in range(4)]

        pg.ld(gt[:], dr["norm1_g_b"][:, :], writes=["gt"])
        pg.ld(idf[:], dr["ident"][:, :], writes=["idf"])
        pg.cp("dve", idb[:], idf[:], reads=["idf"], writes=["idb"])
        engs = ["act", "dve", "pool"]
        for kc in range(8):
            b = pj[kc % 2]
            pg.ld(b[:], dr["w_in"][kc * 128:(kc + 1) * 128, :], writes=[("pjall", kc % 2)])
            pg.cp(engs[kc % 3], win[:, kc, :], b[:], reads=[("pjall", kc % 2)], writes=[("win", kc)])
        winkeys = [("win", kc) for kc in range(8)]
        chunks = []
        c0 = 0
        while c0 < INW:
            w = min(512, INW - c0)
            chunks.append((c0, w))
            c0 += w
        def A1(i):
            s = i % 2
            pg.ld(xt[s][:], dr["x"][i * 128:(i + 1) * 128, :], writes=[("xt", s)])
            pg.tt("dve", junk[:], xt[s][:], xt[s][:], ALU.mult, reads=[("xt", s)], writes=["junk"])
            pg.red("dve", ss[s][:, 0:1], junk[:], ALU.add, reads=["junk"], writes=[("ss", s)])
            pg.act(ss[s][:, 1:2], ss[s][:, 0:1], AF.Sqrt, reads=[("ss", s)], writes=[("ss1", s)],
                   scale=1.0 / D, bias=C.eps6[:, 0:1])
            pg.op("dve", lambda e, o=ss[s][:, 2:3], a=ss[s][:, 1:2]: e.reciprocal(o, a),
                  reads=[("ss1", s)], writes=[("ss2", s)])
            pg.stt("dve", hb[s][:], xt[s][:], ss[s][:, 2:3], gt[:], ALU.mult, ALU.mult,
                   reads=[("xt", s), ("ss2", s), "gt"], writes=[("hb", s)])

        def A2(i):
            s = i % 2
            for j in range(8):
                pg.tr(psT[s][:, j, :], hb[s][:, j * 128:(j + 1) * 128], idb[:],
                      reads=[("hb", s), "idb"], writes=[("psT", s)])
            pg.cp("act", hT[s][:], psT[s][:], reads=[("psT", s)], writes=[("hT", s)])

        chunks_lo = [(512, 512), (1024, 512), (1536, 128), (2304, 512), (2816, 256)]

        def B(i, lo, hi):
            s = i % 2
            if i < NT // 2 - 1:
                if lo != 0:
                    return
                for ci, (c0, w) in enumerate(chunks_lo):
                    pb = psm[ci % 4]
                    for kc in range(8):
                        pg.mm(pb[:, :w], hT[s][:, kc, :], win[:, kc, c0:c0 + w], start=(kc == 0), stop=(kc == 7),
                              reads=[("hT", s), ("win", kc)], writes=[("psm", ci % 4)])
                    pg.cp("act" if ci % 2 == 0 else "dve", pj[s][:, c0:c0 + w], pb[:, :w],
                          reads=[("psm", ci % 4)], writes=[("pj", s, k_) for k_ in range(len(chunks))] + ([("pjall", s)] if i < 8 else []))
                return
            for ci in range(lo, hi):
                c0, w = chunks[ci]
                pb = psm[ci % 4]
                for kc in range(8):
                    pg.mm(pb[:, :w], hT[s][:, kc, :], win[:, kc, c0:c0 + w], start=(kc == 0), stop=(kc == 7),
                          reads=[("hT", s), ("win", kc)], writes=[("psm", ci % 4)])
                pg.cp("act" if ci % 2 == 0 else "dve", pj[s][:, c0:c0 + w], pb[:, :w],
                      reads=[("psm", ci % 4)], writes=[("pj", s, ci), ("pjall", s)] if i < 8 else [("pj", s, ci)])

        def S(i):
            s = i % 2
            pg.ld(dr["P"][i * 128:(i + 1) * 128, 0:3096], pj[s][:, 0:3096],
                  reads=[("pj", s, ci) for ci in range(len(chunks))], writes=[("P", i)])
            pg.ld(dr["PG"][i * 128:(i + 1) * 128, :], pj[s][:, 3096:5144],
                  reads=[("pj", s, ci) for ci in range(len(chunks))], writes=[("PG", i)], q="act")

        A1(0)
        A2(0)
        A1(1)
        for i in range(NT):
            B(i, 0, 6)
            if i + 1 < NT:
                A2(i + 1)
            if i + 2 < NT:
                A1(i + 2)
            B(i, 6, len(chunks))
            S(i)
        pg.barrier()
        pg.emit()


def build(stages, dbg_out=(), dbg_in=(), lvl=9, sub=9, peer_tiles=NT):
    nc = bass.Bass("TRN2", target_bir_lowering=False)
    C = Ctx()
    C.peer_tiles = peer_tiles
    C.lvl = lvl
    C.sub = sub
    C.nc = nc
    dr = {}
    C.dr = dr

    def din(name, shape, dt=F32):
        dr[name] = nc.dram_tensor(name, list(shape), dt, kind="ExternalInput").ap()

    def dscr(name, shape, dt=F32):
        kind = "ExternalOutput" if name in dbg_out else ("ExternalInput" if name in dbg_in else "Internal")
        dr[name] = nc.dram_tensor(name, list(shape), dt, kind=kind).ap()

    din("x", [T, D])
    din("norm1_g_b", [128, D])
    din("ident", [128, 128])
    din("w_in", [D, INW])
    dscr("P", [T, INW])
    dscr("PG", [T, 2048])
    for nm in ("rw_mu_b",):
        din(nm, [128, RWC])
    for nm in ("rw_w0_b", "rw_a0_b", "rw_k_k_b", "rw_k_a_b", "rw_r_k_b", "rw_ln_w_b", "rw_ln_b_b", "rw_g_up"):
        din(nm, [128, 512])
    din("rw_w_up", [64, 512])
    din("rw_a_up", [64, 512])
    for nm in ("RB", "RKp", "RV", "RG", "YA", "RR", "RKK", "RLW", "YS"):
        dscr(nm, [T, 512])
    dscr("RBON", [T, 8])
    din("blkmask", [8, 512])
    din("c_tri", [128, 128])
    din("c_msk", [128, 3, 128])
    din("nsa_gains_b", [128, 768])
    din("nsa_kc_g_b", [128, 64])
    din("ovl", [128, 2, 64])
    din("posT", [128, 2, 32])
    din("cmp_w2", [128, 2, 2, 64])
    din("cmp_w1", [2, 128, 32, 256])
    din("c_cmpb", [128, 2, T], BF16)
    din("c_esel", [128, 32, 128], BF16)
    din("c_causb", [128, 4, 512], BF16)
    din("c_winb", [128, 8, 512], BF16)
    din("c_winb4", [128, 8, 512], BF16)
    din("c_vmfb", [NT, 128, 2, 64])
    dscr("YB", [T, 512])
    din("w_branch_a", [512, D])
    din("w_branch_b", [512, D])
    din("w_out", [D, D])
    dscr("X1L", [peer_tiles * 128, D])
    dscr("YAL", [peer_tiles * 128, 512])
    din("norm2_g_b", [128, D])
    din("iota16", [128, 16])
    din("rowidx", [128, NT], I32)
    din("peer_wq", [D, 2048])
    din("peer_k1", [128, 128])
    din("peer_k2", [128, 128])
    din("peer_u", [16384, D])
    din("peer_v", [16384, D])
    dscr("UV", [16384, 2 * D], BF16)
    dr["out"] = nc.dram_tensor("out", [peer_tiles * 128, D], F32, kind="ExternalOutput").ap()
    with ExitStack() as top:
        pg = Prog(nc, top)
        C.pg = pg
        C.eps6 = top.enter_context(nc.sbuf_tensor("c_eps6", [128, 1], F32))
        pg.memset("dve", C.eps6[:], 1e-6, writes=["eps6"])
        pg.barrier()
        for s in stages:
            s(C)
        pg.barrier()
        pg.emit()
    return nc


def core_x(inputs, b, hh):
    xb = np.asarray(inputs["x"][b])
    if hh == 0:
        return np.ascontiguousarray(np.concatenate([np.zeros((T // 2, D), np.float32), xb[0:T // 2]], 0))
    return np.ascontiguousarray(xb)


def host_inputs(inputs, b, hh=1, ntl=NT):
    g = lambda k: np.ascontiguousarray(inputs[k][0])
    m = {}
    m["x"] = core_x(inputs, b, hh)
    m["norm1_g_b"] = np.ascontiguousarray(np.broadcast_to(g("norm1_g")[None, :], (128, D)))
    m["ident"] = np.eye(128, dtype=np.float32)
    m["w_in"] = g("w_in")
    bc = lambda a: np.ascontiguousarray(np.broadcast_to(np.asarray(a).reshape(1, -1), (128, a.size)))
    m["rw_mu_b"] = bc(g("rw_mu"))
    for nm in ("rw_w0", "rw_a0", "rw_k_k", "rw_k_a", "rw_r_k", "rw_ln_w", "rw_ln_b"):
        m[nm + "_b"] = bc(g(nm))
    for nm in ("rw_g_up", "rw_w_up", "rw_a_up"):
        m[nm] = g(nm)
    bmk = np.zeros((8, 512), np.float32)
    for h in range(8):
        bmk[h, h * 64:(h + 1) * 64] = 1.0
    m["blkmask"] = bmk
    ii = np.arange(128)
    m["c_tri"] = (ii[:, None] <= ii[None, :]).astype(np.float32)
    m["c_msk"] = np.ascontiguousarray(np.stack([(ii[:, None] < ii[None, :]), (ii[:, None] <= ii[None, :]), (ii[:, None] > ii[None, :])], 1).astype(np.float32))
    m.update(nsa_consts(hh))
    for nm in ("w_branch_a", "w_branch_b", "w_out", "peer_wq", "peer_k1", "peer_k2", "peer_u", "peer_v"):
        m[nm] = g(nm)
    m["norm2_g_b"] = bc(g("norm2_g"))
    ri = np.zeros((128, NT), np.int32)
    ri[:, :ntl] = ((NT - ntl) * 128 + np.arange(ntl)[None, :] * 128 + np.arange(128)[:, None]).astype(np.int32)
    m["rowidx"] = ri
    m["iota16"] = np.ascontiguousarray(np.broadcast_to(np.arange(16, dtype=np.float32)[None, :], (128, 16)))
    m["nsa_gains_b"] = bc(np.concatenate([np.tile(g("nsa_q_g"), 8), np.tile(g("nsa_ks_g"), 2), np.tile(g("nsa_kw_g"), 2)]))
    m["nsa_kc_g_b"] = bc(g("nsa_kc_g"))
    posT = np.zeros((128, 2, 32), np.float32)
    posT[0:64, 0, :] = g("cmp_pos_k").T
    posT[0:64, 1, :] = g("cmp_pos_v").T
    m["posT"] = posT
    w2 = np.stack([g("cmp_k_w2").reshape(2, 128, 64), g("cmp_v_w2").reshape(2, 128, 64)], 0)
    m["cmp_w2"] = np.ascontiguousarray(w2.transpose(2, 0, 1, 3))
    w1 = []
    for nm in ("cmp_k_w1", "cmp_v_w1"):
        a = g(nm).reshape(32, 64, 256).transpose(1, 0, 2)
        w1.append(np.concatenate([a, a], 0))
    m["cmp_w1"] = np.ascontiguousarray(np.stack(w1, 0))
    return m


def dap(ap, offset, pattern):
    return bass.AP(ap.tensor, offset, [list(p) for p in pattern])


def stage2a(C):
    nc, pg, dr = C.nc, C.pg, C.dr
    with ExitStack() as st:
        sb, ps = _mk(C, st)
        mu = sb("a_mu", [128, RWC], F32)
        w0 = sb("a_w0", [128, 512], F32)
        a0 = sb("a_a0", [128, 512], F32)
        kkc = sb("a_kk", [128, 512], F32)
        kac = sb("a_ka", [128, 512], F32)
        rkc = sb("a_rk", [128, 512], F32)
        wup = sb("a_wup", [128, 512], F32)
        gup = sb("a_gup", [128, 512], F32)
        idf = sb("a_idf", [128, 128], F32)
        p_2 = [sb(f"a_p{i_}", [128, RWC], F32) for i_ in range(2)]
        pv_2 = [sb(f"a_pv{i_}", [128, RWC], F32) for i_ in range(2)]
        pm_2 = [sb(f"a_pm{i_}", [128, RWC], F32) for i_ in range(2)]
        lor_2 = [sb(f"a_lor{i_}", [128, 256], F32) for i_ in range(2)]
        lorT_2 = [sb(f"a_lorT{i_}", [128, 256], F32) for i_ in range(2)]
        wt_2 = [sb(f"a_wt{i_}", [128, 512], F32) for i_ in range(2)]
        lwt_2 = [sb(f"a_lwt{i_}", [128, 512], F32) for i_ in range(2)]
        at_2 = [sb(f"a_at{i_}", [128, 512], F32) for i_ in range(2)]
        gt_2 = [sb(f"a_gt{i_}", [128, 512], F32) for i_ in range(2)]
        kk_2 = [sb(f"a_kkt{i_}", [128, 512], F32) for i_ in range(2)]
        sq_2 = [sb(f"a_sq{i_}", [128, 512], F32) for i_ in range(2)]
        nrm_2 = [sb(f"a_nrm{i_}", [128, 32], F32) for i_ in range(2)]
        kkn_2 = [sb(f"a_kkn{i_}", [128, 512], F32) for i_ in range(2)]
        bt_2 = [sb(f"a_bt{i_}", [128, 512], F32) for i_ in range(2)]
        t1_2 = [sb(f"a_t1{i_}", [128, 512], F32) for i_ in range(2)]
        kp_2 = [sb(f"a_kp{i_}", [128, 512], F32) for i_ in range(2)]
        bon_2 = [sb(f"a_bon{i_}", [128, 8], F32) for i_ in range(2)]
        psl_2 = [ps(f"a_psl{i_}", [128, 512], F32) for i_ in range(2)]
        psw_2 = [ps(f"a_psw{i_}", [128, 512], F32) for i_ in range(2)]
        psa_2 = [ps(f"a_psa{i_}", [128, 512], F32) for i_ in range(2)]
        psg_2 = [ps(f"a_psg{i_}", [128, 512], F32) for i_ in range(2)]

        for (tile, name) in ((mu, "rw_mu_b"), (w0, "rw_w0_b"), (a0, "rw_a0_b"), (kkc, "rw_k_k_b"),
                             (kac, "rw_k_a_b"), (rkc, "rw_r_k_b"), (gup, "rw_g_up"), (idf, "ident")):
            pg.ld(tile[:], dr[name][:, :], writes=[name])
        pg.ld(wup[0:64, :], dr["rw_w_up"][:, :], writes=["wup0"])
        pg.ld(wup[64:128, :], dr["rw_a_up"][:, :], writes=["wup1"])
        P = dr["P"]
        def E_(i):
            t0 = i * 128
            s = i % 2
            p, pv, pm, lor, lorT, wt, lwt, at, gt, kk, sq, nrm, kkn, bt, t1, kp, bon = [t_[s] for t_ in (
                p_2, pv_2, pm_2, lor_2, lorT_2, wt_2, lwt_2, at_2, gt_2, kk_2, sq_2, nrm_2, kkn_2, bt_2, t1_2, kp_2, bon_2)]
            psl, psw, psa, psg = psl_2[s], psw_2[s], psa_2[s], psg_2[s]
            pg.ld(p[:], P[t0:t0 + 128, 0:RWC], reads=[("P", i)], writes=[("p", s)])
            if i == 0:
                pg.memset("dve", pv[0:1, :], 0.0, writes=[("pv0", s)])
                pg.ld(pv[1:128, :], P[0:127, 0:RWC], reads=[("P", 0)], writes=[("pv", s)])
                pvk = [("pv", s), ("pv0", s)]
            else:
                pg.ld(pv[:], P[t0 - 1:t0 + 127, 0:RWC], reads=[("P", i), ("P", i - 1)], writes=[("pv", s), ("pv0", s)])
                pvk = [("pv", s), ("pv0", s)]
            pg.tt("dve", pv[:], pv[:], p[:], ALU.subtract, reads=pvk + [("p", s)], writes=[("pv", s)])
            pg.tt("dve", pv[:], pv[:], mu[:], ALU.mult, reads=[("pv", s), "rw_mu_b"], writes=[("pv", s)])
            pg.tt("dve", pm[:], pv[:], p[:], ALU.add, reads=[("pv", s), ("p", s)], writes=[("pm", s)])
            r_ = pm[:, 0:512]
            k_ = pm[:, 512:1024]
            v_ = pm[:, 1024:1536]
            pg.act(lor[:, 0:64], pm[:, 1536:1600], AF.Tanh, reads=[("pm", s)], writes=[("lor0", s)])
            pg.cp("pool", lor[:, 64:128], pm[:, 1600:1664], reads=[("pm", s)], writes=[("lor1", s)])
            pg.act(lor[:, 128:256], pm[:, 1664:1792], AF.Sigmoid, reads=[("pm", s)], writes=[("lor2", s)])
            pg.tr(psl[:, 0:128], lor[:, 0:128], idf[:], reads=[("lor0", s), ("lor1", s), "ident"], writes=[("psl", s)])
            pg.tr(psl[:, 128:256], lor[:, 128:256], idf[:], reads=[("lor2", s), "ident"], writes=[("psl", s)])
            pg.cp("act", lorT[:], psl[:, 0:256], reads=[("psl", s)], writes=[("lorT", s)])
            pg.mm(psw[:], lorT[0:64, 0:128], wup[0:64, :], reads=[("lorT", s), "wup0"], writes=[("psw", s)])
            pg.mm(psa[:], lorT[64:128, 0:128], wup[64:128, :], reads=[("lorT", s), "wup1"], writes=[("psa", s)])
            pg.mm(psg[:], lorT[:, 128:256], gup[:], reads=[("lorT", s), "rw_g_up"], writes=[("psg", s)])
        def L_(i):
            t0 = i * 128
            s = i % 2
            p, pv, pm, lor, lorT, wt, lwt, at, gt, kk, sq, nrm, kkn, bt, t1, kp, bon = [t_[s] for t_ in (
                p_2, pv_2, pm_2, lor_2, lorT_2, wt_2, lwt_2, at_2, gt_2, kk_2, sq_2, nrm_2, kkn_2, bt_2, t1_2, kp_2, bon_2)]
            psl, psw, psa, psg = psl_2[s], psw_2[s], psa_2[s], psg_2[s]
            r_ = pm[:, 0:512]
            k_ = pm[:, 512:1024]
            v_ = pm[:, 1024:1536]
            pg.tt("dve", wt[:], psw[:], w0[:], ALU.add, reads=[("psw", s), "rw_w0_b"], writes=[("wt", s)])
            pg.act(wt[:], wt[:], AF.Sigmoid, reads=[("wt", s)], writes=[("wt", s)])
            pg.ts("dve", lwt[:], wt[:], -0.6065306597126334, None, ALU.mult, reads=[("wt", s)], writes=[("lwt", s)])
            pg.tt("dve", at[:], psa[:], a0[:], ALU.add, reads=[("psa", s), "rw_a0_b"], writes=[("at", s)])
            pg.act(at[:], at[:], AF.Sigmoid, reads=[("at", s)], writes=[("at", s)])
            pg.cp("act", gt[:], psg[:], reads=[("psg", s)], writes=[("gt", s)])
            pg.tt("dve", kk[:], k_, kkc[:], ALU.mult, reads=[("pm", s), "rw_k_k_b"], writes=[("kk", s)])
            pg.tt("pool", sq[:], kk[:], kk[:], ALU.mult, reads=[("kk", s)], writes=[("sq", s)])
            pg.red("dve", nrm[:, 0:8], sq[:].rearrange("p (h k) -> p h k", h=8), ALU.add, reads=[("sq", s)], writes=[("nrm0", s)])
            pg.act(nrm[:, 8:16], nrm[:, 0:8], AF.Sqrt, reads=[("nrm0", s)], writes=[("nrm1", s)])
            pg.ts("dve", nrm[:, 16:24], nrm[:, 8:16], 1e-12, None, ALU.max, reads=[("nrm1", s)], writes=[("nrm2", s)])
            pg.op("dve", lambda e, nrm=nrm: e.reciprocal(nrm[:, 24:32], nrm[:, 16:24]), reads=[("nrm2", s)], writes=[("nrm3", s)])
            rinv_b = nrm[:, 24:32].unsqueeze(2).to_broadcast([128, 8, 64])
            v3 = lambda tl: tl[:].rearrange("p (h k) -> p h k", h=8)
            pg.stt("dve", v3(kkn), v3(kk), -1.0, rinv_b, ALU.mult, ALU.mult, reads=[("kk", s), ("nrm3", s)], writes=[("kkn", s)])
            pg.stt("dve", bt[:], kkn[:], -1.0, at[:], ALU.mult, ALU.mult, reads=[("kkn", s), ("at", s)], writes=[("bt", s)])
            pg.stt("dve", t1[:], at[:], -1.0, kac[:], ALU.add, ALU.mult, reads=[("at", s), "rw_k_a_b"], writes=[("t1", s)])
            pg.stt("dve", kp[:], t1[:], 1.0, k_, ALU.add, ALU.mult, reads=[("t1", s), ("pm", s)], writes=[("kp", s)])
            pg.tt("pool", sq[:], r_, kp[:], ALU.mult, reads=[("pm", s), ("kp", s), ("sq", s)], writes=[("sq", s)])
            pg.tt("pool", sq[:], sq[:], rkc[:], ALU.mult, reads=[("sq", s), "rw_r_k_b"], writes=[("sq", s)])
            pg.red("dve", bon[:], sq[:].rearrange("p (h k) -> p h k", h=8), ALU.add, reads=[("sq", s)], writes=[("bon", s)])
            pg.ld(dr["RR"][t0:t0 + 128, :], r_, reads=[("pm", s)], writes=[("RR", i)], q="act")
            pg.ld(dr["RKK"][t0:t0 + 128, :], kkn[:], reads=[("kkn", s)], writes=[("RKK", i)], q="act")
            pg.ld(dr["RLW"][t0:t0 + 128, :], lwt[:], reads=[("lwt", s)], writes=[("RLW", i)], q="act")
            pg.ld(dr["RB"][t0:t0 + 128, :], bt[:], reads=[("bt", s)], writes=[("RB", i)], q="act")
            pg.ld(dr["RKp"][t0:t0 + 128, :], kp[:], reads=[("kp", s)], writes=[("RKp", i)], q="act")
            pg.ld(dr["RV"][t0:t0 + 128, :], v_, reads=[("pm", s)], writes=[("RV", i)], q="act")
            pg.ld(dr["RG"][t0:t0 + 128, :], gt[:], reads=[("gt", s)], writes=[("RG", i)], q="act")
            pg.ld(dr["RBON"][t0:t0 + 128, :], bon[:], reads=[("bon", s)], writes=[("RBON", i)], q="act")
        E_(0)
        for i in range(NT):
            if i + 1 < NT:
                E_(i + 1)
            L_(i)
        pg.barrier()
        pg.emit()


def igather(pg, out_ap, table_ap, idx_ap, reads, writes):
    pg.dma("pool", lambda e: e.indirect_dma_start(out=out_ap, out_offset=None, in_=table_ap,
                                                   in_offset=bass.IndirectOffsetOnAxis(ap=idx_ap, axis=0)), reads, writes)


def stage2c(C):
    nc, pg, dr = C.nc, C.pg, C.dr
    with ExitStack() as st:
        sb, ps = _mk(C, st)
        lnw = sb("c_lnw", [128, 512], F32)
        lnb = sb("c_lnb", [128, 512], F32)
        eps = sb("c_eps", [128, 1], F32)
        y = [sb(f"c_y{i}", [128, 8, 64], F32) for i in range(2)]
        v = [sb(f"c_v{i}", [128, 8, 64], F32) for i in range(2)]
        g = [sb(f"c_g{i}", [128, 512], F32) for i in range(2)]
        bon = [sb(f"c_bon{i}", [128, 8], F32) for i in range(2)]
        stt_ = [sb(f"c_st{i}", [128, 32], F32) for i in range(2)]
        sq = sb("c_sq", [128, 8, 64], F32)
        pg.ld(lnw[:], dr["rw_ln_w_b"][:, :], writes=["lnw"])
        pg.ld(lnb[:], dr["rw_ln_b_b"][:, :], writes=["lnb"])
        pg.memset("dve", eps[:], 64e-5, writes=["eps"])
        f2 = lambda tl: tl[:].rearrange("p h k -> p (h k)")
        rowidx = sb("c_rowidx", [128, NT], I32)
        pg.ld(rowidx[:], dr["rowidx"][:, :], writes=["rowidx"])
        allk = lambda nm: [(nm, k) for k in range(NT)]
        for i in range(C.peer_tiles):
            s = i % 2
            t0 = i * 128
            yk, vk, gk, bk, sk = ("y", s), ("v", s), ("g", s), ("bon", s), ("st", s)
            ix = rowidx[:, i:i + 1]
            igather(pg, f2(y[s]), dr["YS"][:, :], ix, allk("YS") + ["rowidx"], [yk])
            igather(pg, f2(v[s]), dr["RV"][:, :], ix, allk("RV") + ["rowidx"], [vk])
            igather(pg, g[s][:], dr["RG"][:, :], ix, allk("RG") + ["rowidx"], [gk])
            igather(pg, bon[s][:], dr["RBON"][:, :], ix, allk("RBON") + ["rowidx"], [bk])
            S_ = stt_[s]
            bc = lambda ap: ap.unsqueeze(2).to_broadcast([128, 8, 64])
            pg.red("dve", S_[:, 0:8], y[s][:], ALU.add, reads=[yk], writes=[(sk, 0)])
            pg.ts("dve", S_[:, 8:16], S_[:, 0:8], -1.0 / 64, None, ALU.mult, reads=[(sk, 0)], writes=[(sk, 1)])
            pg.tt("dve", y[s][:], y[s][:], bc(S_[:, 8:16]), ALU.add, reads=[yk, (sk, 1)], writes=[yk])
            pg.tt("pool", sq[:], y[s][:], y[s][:], ALU.mult, reads=[yk], writes=["sq"])
            pg.red("dve", S_[:, 16:24], sq[:], ALU.add, reads=["sq"], writes=[(sk, 2)])
            pg.act(S_[:, 24:32], S_[:, 16:24], AF.Sqrt, reads=[(sk, 2), "eps"], writes=[(sk, 3)], scale=1.0 / 64, bias=eps[:, 0:1])
            pg.op("dve", lambda e, o=S_[:, 16:24], a=S_[:, 24:32]: e.reciprocal(o, a), reads=[(sk, 3)], writes=[(sk, 2)])
            pg.tt("dve", y[s][:], y[s][:], bc(S_[:, 16:24]), ALU.mult, reads=[yk, (sk, 2)], writes=[yk])
            pg.tt("dve", f2(y[s]), f2(y[s]), lnw[:], ALU.mult, reads=[yk, "lnw"], writes=[yk])
            pg.tt("pool", f2(y[s]), f2(y[s]), lnb[:], ALU.add, reads=[yk, "lnb"], writes=[yk])
            pg.tt("pool", v[s][:], v[s][:], bc(bon[s][:, 0:8]), ALU.mult, reads=[vk, bk], writes=[vk])
            pg.tt("dve", y[s][:], y[s][:], v[s][:], ALU.add, reads=[yk, vk], writes=[yk])
            pg.tt("dve", f2(y[s]), f2(y[s]), g[s][:], ALU.mult, reads=[yk, gk], writes=[yk])
            pg.ld(dr["YAL"][t0:t0 + 128, :], f2(y[s]), reads=[yk], writes=[("YAL", i)])
        pg.barrier()
        pg.emit()


NEG = -30000.0


def stage3(C):
    nc, pg, dr = C.nc, C.pg, C.dr
    with ExitStack() as st:
        sb, ps = _mk(C, st)
        qT = sb("n_qT", [128, 4, T], BF16)
        KsT = sb("n_KsT", [128, 2, T], BF16)
        KwT = sb("n_KwT", [128, 2, T], BF16)
        Vs = sb("n_Vs", [128, NT, 2, 65], BF16)
        Vw = sb("n_Vw", [128, NT, 2, 65], BF16)
        KcT = sb("n_KcT", [128, 2, 256], BF16)
        Vc = sb("n_Vc", [128, 2, 2, 129], BF16)
        GT = sb("n_GT", [128, NT, 24], F32)
        idf = sb("n_idf", [128, 128], F32)
        idb = sb("n_idb", [128, 128], BF16)
        eps = sb("n_eps", [128, 1], F32)
        pg.ld(idf[:], dr["ident"][:, :], writes=["idf"])
        pg.cp("dve", idb[:], idf[:], reads=["idf"], writes=["idb"])
        pg.memset("dve", eps[:], 1e-6, writes=["eps"])
        pg.memset("pool", Vs[:], 1.0, writes=["Vs"])
        pg.memset("pool", Vw[:], 1.0, writes=["Vw"])
        pg.memset("pool", Vc[:], 0.0, writes=["Vc"])
        with ExitStack() as sa_:
            sb, ps = _mk(C, sa_)
            kcT2 = sb("n_kcT2", [128, T], BF16)
            vcT2 = sb("n_vcT2", [128, T], BF16)
            w1 = [sb(f"n_w1{i}", [128, 32, 256], BF16) for i in range(2)]
            w1s = sb("n_w1s", [128, 16, 256], F32)
            w2s = sb("n_w2s", [128, 2, 2, 64], F32)
            w2 = sb("n_w2", [128, 2, 2, 64], BF16)
            posf = sb("n_posf", [128, 2, 32], F32)
            posb = sb("n_posb", [128, 2, 32], BF16)
            gains = sb("n_gains", [128, 768], F32)
            kcg = sb("n_kcg", [128, 64], F32)
            ovl = sb("n_ovl", [128, 2, 64], F32)
            R = [sb(f"n_R{i}", [128, 1304], F32) for i in range(2)]
            sq = sb("n_sq", [128, 1280], F32)
            tmp = sb("n_tmp", [128, 768], F32)
            stat = sb("n_stat", [128, 64], F32)
            Xb = sb("n_Xb", [128, 10, 128], BF16)
            biasS = sb("n_biasS", [128, 4], F32)
            xb_ = sb("n_xb", [128, 256], F32)
            x2_ = sb("n_x2", [128, 256], F32)
            hT = sb("n_hT", [128, 2, 256], BF16)
            kcn2 = sb("n_kcn2", [128, 128], BF16)
            st2 = sb("n_st2", [128, 8], F32)
            ksq = sb("n_ksq", [128, 64], F32)
            psX_ = [ps(f"n_psX{i}", [128, 1024], BF16) for i in range(3)]
            psX = [t_[:, 0:512].rearrange("p (a b) -> p a b", a=4) for t_ in psX_]
            psh = ps("n_psh", [128, 512], F32)
            psb = ps("n_psb", [128, 512], F32)
            pso = ps("n_pso", [128, 512], F32)
            psk = ps("n_psk", [128, 1024], BF16)

            pg.ld(gains[:], dr["nsa_gains_b"][:, :], writes=["gains"])
            pg.ts("dve", gains[:, 0:512], gains[:, 0:512], 0.125, None, ALU.mult, reads=["gains"], writes=["gains"])
            pg.ld(kcg[:], dr["nsa_kc_g_b"][:, :], writes=["kcg"])
            pg.ld(ovl[:], dr["ovl"][:, :, :], writes=["ovl"])
            pg.ld(posf[:], dr["posT"][:, :, :], writes=["posf"])
            pg.cp("dve", posb[:], posf[:], reads=["posf"], writes=["posb"])
            pg.ld(w2s[:], dr["cmp_w2"][:, :, :, :], writes=["w2s"])
            pg.cp("dve", w2[:], w2s[:], reads=["w2s"], writes=["w2"])
            for x in range(2):
                for hf in range(2):
                    pg.ld(w1s[:], dr["cmp_w1"][x, :, hf * 16:(hf + 1) * 16, :], writes=["w1s"])
                    pg.cp("pool", w1[x][:, hf * 16:(hf + 1) * 16, :], w1s[:], reads=["w1s"], writes=[("w1", x)])
            for i in range(NT):
                s = i % 2
                t0 = i * 128
                Rk = ("R", s)
                pg.ld(R[s][:], dr["P"][t0:t0 + 128, 1792:3096], reads=[("P", i)], writes=[Rk])
                Rs = R[s]
                pg.tt("pool", sq[:], Rs[:, 0:1280], Rs[:, 0:1280], ALU.mult, reads=[Rk], writes=["sq"])
                pg.red("dve", stat[:, 0:20], sq[:].rearrange("p (a k) -> p a k", k=64), ALU.add, reads=["sq"], writes=["stat0"])
                pg.act(stat[:, 20:40], stat[:, 0:20], AF.Sqrt, reads=["stat0", "eps"], writes=["stat1"], scale=1.0 / 64, bias=eps[:, 0:1])
                pg.op("dve", lambda e: e.reciprocal(stat[:, 40:60], stat[:, 20:40]), reads=["stat1"], writes=["stat2"])
                b3 = lambda ap, n: ap.unsqueeze(2).to_broadcast([128, n, 64])
                v3 = lambda ap: ap.rearrange("p (a k) -> p a k", k=64)
                pg.tt("dve", v3(tmp[:, 0:512]), v3(Rs[:, 0:512]), b3(stat[:, 40:48], 8), ALU.mult, reads=[Rk, "stat2"], writes=["tmp"])
                pg.tt("dve", v3(tmp[:, 512:640]), v3(Rs[:, 768:896]), b3(stat[:, 52:54], 2), ALU.mult, reads=[Rk, "stat2"], writes=["tmp"])
                pg.tt("dve", v3(tmp[:, 640:768]), v3(Rs[:, 1024:1152]), b3(stat[:, 56:58], 2), ALU.mult, reads=[Rk, "stat2"], writes=["tmp"])
                pg.tt("pool", tmp[:], tmp[:], gains[:], ALU.mult, reads=["tmp", "gains"], writes=["tmp"])
                pg.cp("pool", Xb[:, 0:4, :].rearrange("p a b -> p (a b)"), tmp[:, 0:512], reads=["tmp"], writes=["Xb"])
                for (blk, c0) in ((4, 512), (6, 640)):
                    src = tmp[:, c0:c0 + 128].rearrange("p (g k) -> p g k", g=2).unsqueeze(2).to_broadcast([128, 2, 2, 64])
                    dst = Xb[:, blk:blk + 2, :].rearrange("p g (d k) -> p g d k", d=2)
                    pg.cp("dve", dst, src, reads=["tmp"], writes=["Xb"])
                pg.cp("pool", Xb[:, 8, :], Rs[:, 512:640], reads=[Rk], writes=["Xb"])
                pg.cp("pool", Xb[:, 9, :], Rs[:, 640:768], reads=[Rk], writes=["Xb"])
                for blk in range(10):
                    pg.tr(psX[blk // 4][:, blk % 4, :], Xb[:, blk, :], idb[:], reads=["Xb", "idb"], writes=[("psX", blk // 4)])
                pg.cp("act", qT[:, :, t0:t0 + 128], psX[0], reads=[("psX", 0)], writes=["qT"])
                pg.cp("dve", KsT[:, :, t0:t0 + 128], psX[1][:, 0:2, :], reads=[("psX", 1)], writes=["KsT"])
                pg.cp("dve", KwT[:, :, t0:t0 + 128], psX[1][:, 2:4, :], reads=[("psX", 1)], writes=["KwT"])
                pg.cp("act", kcT2[:, t0:t0 + 128], psX[2][:, 0, :], reads=[("psX", 2)], writes=["kcT2"])
                pg.cp("act", vcT2[:, t0:t0 + 128], psX[2][:, 1, :], reads=[("psX", 2)], writes=["vcT2"])
                pg.cp("pool", Vs[:, i, :, 0:64], Rs[:, 896:1024].rearrange("p (g k) -> p g k", g=2), reads=[Rk, "Vs"], writes=["Vs"])
                pg.cp("pool", Vw[:, i, :, 0:64], Rs[:, 1152:1280].rearrange("p (g k) -> p g k", g=2), reads=[Rk, "Vw"], writes=["Vw"])
                pg.act(GT[:, i, :], Rs[:, 1280:1304], AF.Sigmoid, reads=[Rk], writes=["GT"])
            if getattr(C, "lvl", 9) < 2:
                pg.barrier()
                pg.emit()
                return
            pg.memset("dve", hT[:], 0.0, writes=["hT"])
            pg.memset("dve", kcn2[:], 0.0, writes=["kcn2"])
            for x in range(2):
                for hf in range(2):
                    for l in range(32):
                        pg.mm(psb[:, x * 2 + hf:x * 2 + hf + 1], w1[x][0:64, l, hf * 128:(hf + 1) * 128], posb[0:64, x, l:l + 1],
                              start=(l == 0), stop=(l == 31), reads=[("w1", x), "posb"], writes=["psb"])
            pg.cp("dve", biasS[:], psb[:, 0:4], reads=["psb"], writes=["biasS"])
            for x in range(2):
                srcT = kcT2 if x == 0 else vcT2
                skey = "kcT2" if x == 0 else "vcT2"
                for g in range(2):
                    for hf in range(2):
                        for l in range(32):
                            rhs = dap(srcT[:], g * 64 * T + l, [[T, 64], [16, 255]])
                            pg.mm(psh[:, 0:255], w1[x][g * 64:(g + 1) * 64, l, hf * 128:(hf + 1) * 128], rhs,
                                  start=(l == 0), stop=(l == 31), reads=[("w1", x), skey], writes=["psh"])
                        c = slice(0, 255)
                        pg.act(xb_[:, c], psh[:, c], AF.Identity, reads=["psh", "biasS"], writes=["xb"], bias=biasS[:, x * 2 + hf:x * 2 + hf + 1])
                        pg.tt("pool", x2_[:, c], xb_[:, c], xb_[:, c], ALU.mult, reads=["xb"], writes=["x2"])
                        pg.ts("dve", x2_[:, c], x2_[:, c], 0.044715, 1.0, ALU.mult, ALU.add, reads=["x2"], writes=["x2"])
                        pg.tt("dve", x2_[:, c], x2_[:, c], xb_[:, c], ALU.mult, reads=["x2", "xb"], writes=["x2"])
                        pg.act(x2_[:, c], x2_[:, c], AF.Tanh, reads=["x2"], writes=["x2"], scale=0.7978845608028654)
                        pg.stt("dve", x2_[:, c], x2_[:, c], 1.0, xb_[:, c], ALU.add, ALU.mult, reads=["x2", "xb"], writes=["x2"])
                        pg.ts("dve", hT[:, hf, c], x2_[:, c], 0.5, None, ALU.mult, reads=["x2"], writes=["hT"])
                    for m in range(2):
                        rows = 128 if m == 0 else 127
                        for hf in range(2):
                            pg.mm(pso[0:rows, 0:64], hT[:, hf, m * 128:m * 128 + rows], w2[:, x, hf, :], start=(hf == 0), stop=(hf == 1),
                                  reads=["hT", "w2"], writes=["pso"])
                        if x == 0:
                            pg.cp("act", ksq[0:rows, :], pso[0:rows, 0:64], reads=["pso"], writes=["ksq"])
                            pg.tt("pool", x2_[0:rows, 0:64], ksq[0:rows, :], ksq[0:rows, :], ALU.mult, reads=["ksq", "x2"], writes=["x2"])
                            pg.red("dve", st2[0:rows, 0:1], x2_[0:rows, 0:64], ALU.add, reads=["x2"], writes=["st2a"])
                            pg.act(st2[0:rows, 1:2], st2[0:rows, 0:1], AF.Sqrt, reads=["st2a", "eps"], writes=["st2b"], scale=1.0 / 64, bias=eps[0:rows, 0:1])
                            pg.op("dve", lambda e, rows=rows: e.reciprocal(st2[0:rows, 2:3], st2[0:rows, 1:2]), reads=["st2b"], writes=["st2c"])
                            pg.stt("dve", ksq[0:rows, :], ksq[0:rows, :], st2[0:rows, 2:3], kcg[0:rows, :], ALU.mult, ALU.mult,
                                   reads=["ksq", "st2c", "kcg"], writes=["ksq"])
                            src = ksq[0:rows, :].unsqueeze(1).to_broadcast([rows, 2, 64])
                            pg.cp("dve", kcn2[0:rows, :].rearrange("p (d k) -> p d k", d=2), src, reads=["ksq"], writes=["kcn2"])
                            pg.tr(psk[:, 0:128], kcn2[:, :], idb[:], reads=["kcn2", "idb"], writes=["psk"])
                            pg.cp("act", KcT[:, g, m * 128:(m + 1) * 128], psk[:, 0:128], reads=["psk"], writes=["KcT"])
                        else:
                            pg.cp("act", Vc[0:rows, m, g, 0:64], pso[0:rows, 0:64], reads=["pso", "Vc"], writes=["Vc"])
            for m in range(2):
                for g in range(2):
                    pg.memset("dve", Vc[:, m, g, 64:65], 1.0, writes=["Vc"])
                    pg.cp("dve", Vc[:, m, g, 65:129], ovl[:, m, :], reads=["ovl", "Vc"], writes=["Vc"])
            pg.barrier()
            pg.emit()
        if getattr(C, "lvl", 9) < 3:
            return
        stage3_attn(C, st, qT, KsT, KwT, Vs, Vw, KcT, Vc, GT, idf, idb)


def stage3_attn(C, st, qT, KsT, KwT, Vs, Vw, KcT, Vc, GT, idf, idb):
    nc, pg, dr = C.nc, C.pg, C.dr
    with ExitStack() as sb_:
        sb, ps = _mk(C, sb_)
        cmpb = sb("n_cmpb", [128, 2, T], BF16)
        Esel = sb("n_Esel", [128, 32, 128], BF16)
        causb = sb("n_causb", [128, 4, 512], BF16)
        winb = sb("n_winb", [128, 8, 512], BF16)
        selbT = sb("n_selbT", [128, 2, T], BF16)
        eT = [sb(f"n_eT{i}", [128, 512], BF16) for i in range(4)]
        eT2 = [sb(f"n_eT2{i}", [128, 512], BF16) for i in range(4)]
        Mt = [sb(f"n_Mt{i}", [128, 512], BF16) for i in range(2)]
        rm = [0]
        dq = []
        ocmp = sb("n_ocmp", [128, 4, 8, 64], F32)
        osel = sb("n_osel", [128, 4, 8, 64], F32)
        owin = sb("n_owin", [128, 4, 8, 64], F32)
        den = sb("n_den", [128, 16], F32)
        impw = sb("n_impw", [128, 2, 4, 64], F32)
        score = sb("n_score", [128, 2, 64], F32)
        VM = [sb(f"n_VM{i}", [128, 2, 64], F32) for i in range(2)]
        work = sb("n_work", [128, 2, 64], F32)
        m8 = sb("n_m8", [128, 2, 16], F32)
        thr = sb("n_thr", [128, 2], F32)
        msel = sb("n_msel", [128, 2, 64], F32)
        selb = sb("n_selb", [128, 2, 2, 64], BF16)
        osT = [sb(f"n_osT{i}", [65, 512], F32) for i in range(2)]
        dn2 = sb("n_dn2", [128, 8], F32)
        yb = sb("n_yb", [128, 8, 64], F32)
        yb2 = sb("n_yb2", [128, 8, 64], F32)
        psS = [ps(f"n_psS{i}", [128, 512], F32) for i in range(3)]
        psA = [ps(f"n_psA{i}", [128, 512], F32) for i in range(2)]
        psB = [ps(f"n_psB{i}", [128, 512], F32) for i in range(2)]
        psZ_ = ps("n_psZ", [128, 1024], BF16)
        psZ = psZ_[:, 0:256].rearrange("p (g q) -> p g q", g=2)

        pg.ld(cmpb[:], dr["c_cmpb"][:, :, :], writes=["cmpb"])
        pg.ld(Esel[:], dr["c_esel"][:, :, :], writes=["Esel"])
        pg.ld(causb[:], dr["c_causb"][:, :, :], writes=["causb"])
        pg.ld(winb[:], dr["c_winb"][:, :, :], writes=["winb"])
        winb4 = sb("n_winb4", [128, 8, 512], BF16)
        pg.ld(winb4[:], dr["c_winb4"][:, :, :], writes=["winb4"])
        rs = [0]
        re = [0]

        def nxt(lst, n):
            v = lst[0]
            lst[0] = (v + 1) % n
            return v

        def qk(h):
            return (h % 2) * 64, h // 2, h // 4

        for Q in range(4, 8):
            tq0 = Q * 512
            for ii in range(4):
                i = Q * 4 + ii
                t0 = i * 128
                s = i % 2
                pg.ld(VM[s][:], dr["c_vmfb"][i, :, :, :], writes=[("VM", s)])
                nm = 2 if i >= 16 else 1
                for h in range(8):
                    base, hp, g = qk(h)
                    h4 = h % 4
                    for m in range(nm):
                        r = nxt(rs, 3)
                        pS = psS[r]
                        pg.mm(pS[:, 0:128], KcT[base:base + 64, g, m * 128:(m + 1) * 128], qT[base:base + 64, hp, t0:t0 + 128],
                              start=True, stop=False, reads=["KcT", "qT"], writes=[("psS", r)])
                        pg.mm(pS[:, 0:128], idb[:, :], cmpb[:, m, t0:t0 + 128], start=False, stop=True,
                              reads=["idb", "cmpb"], writes=[("psS", r)])
                        k = nxt(re, 4)
                        pg.act(eT[k][:, 0:128], pS[:, 0:128], AF.Exp, reads=[("psS", r)], writes=[("eT", k)])
                        def pv(g=g, h4=h4, k=k, m=m, nm=nm):
                            pg.mm(psA[g][:, h4 * 65:h4 * 65 + 65], eT[k][:, 0:128], Vc[:, m, g, 0:65], start=(m == 0), stop=(m == nm - 1),
                                  reads=[("eT", k), "Vc"], writes=[("psA", g)])
                            pg.mm(psB[g][:, h4 * 64:h4 * 64 + 64], eT[k][:, 0:128], Vc[:, m, g, 65:129], start=(m == 0), stop=(m == nm - 1),
                                  reads=[("eT", k), "Vc"], writes=[("psB", g)])
                        dq.append(pv)
                        if len(dq) > 2:
                            dq.pop(0)()
                while dq:
                    dq.pop(0)()
                for g in range(2):
                    A3 = psA[g][:, 0:260].rearrange("p (h c) -> p h c", c=65)
                    B3 = psB[g][:, 0:256].rearrange("p (h c) -> p h c", c=64)
                    dsl = den[:, g * 4:(g + 1) * 4]
                    rsl = den[:, 8 + g * 4:8 + (g + 1) * 4]
                    pg.ts("dve", dsl, A3[:, :, 64], 1e-30, None, ALU.max, reads=[("psA", g)], writes=[("den", g)])
                    pg.op("dve", lambda e, o=rsl, a=dsl: e.reciprocal(o, a), reads=[("den", g)], writes=[("rden", g)])
                    rb = rsl.unsqueeze(2).to_broadcast([128, 4, 64])
                    pg.tt("dve", ocmp[:, ii, g * 4:(g + 1) * 4, :], A3[:, :, 0:64], rb, ALU.mult, reads=[("psA", g), ("rden", g)], writes=["ocmp"])
                    pg.tt("dve", impw[:, g, :, :], B3, rb, ALU.mult, reads=[("psB", g), ("rden", g)], writes=[("impw", g)])
                    pg.red("dve", score[:, g, :], impw[:, g, :, :].rearrange("p h j -> p j h"), ALU.add, reads=[("impw", g)], writes=[("score", g)])
                    vm = dr
                    pg.tt("dve", score[:, g, :], score[:, g, :], VM[s][:, 0, :], ALU.mult, reads=[("score", g), ("VM", s)], writes=[("score", g)])
                    pg.tt("dve", score[:, g, :], score[:, g, :], VM[s][:, 1, :], ALU.add, reads=[("score", g), ("VM", s)], writes=[("score", g)])
                    pg.op("dve", lambda e, g=g: e.max(m8[:, g, 0:8], score[:, g, :]), reads=[("score", g)], writes=[("m8a", g)])
                    pg.op("dve", lambda e, g=g: e.match_replace(work[:, g, :], m8[:, g, 0:8], score[:, g, :], -1e9),
                          reads=[("score", g), ("m8a", g)], writes=[("work", g)])
                    pg.op("dve", lambda e, g=g: e.max(m8[:, g, 8:16], work[:, g, :]), reads=[("work", g)], writes=[("m8b", g)])
                    pg.ts("dve", thr[:, g:g + 1], m8[:, g, 15:16], -0.5, None, ALU.max, reads=[("m8b", g)], writes=[("thr", g)])
                    pg.ts("dve", msel[:, g, :], score[:, g, :], thr[:, g:g + 1], None, ALU.is_ge, reads=[("score", g), ("thr", g)], writes=[("msel", g)])
                    pg.cp("dve", selb[:, g, :, :], msel[:, g, :].unsqueeze(1).to_broadcast([128, 2, 64]), reads=[("msel", g)], writes=[("selb", g)])
                    pg.tr(psZ[:, g, :], selb[:, g, :, :].rearrange("p d j -> p (d j)"), idb[:], reads=[("selb", g), "idb"], writes=["psZ"])
                pg.cp("act", selbT[:, :, t0:t0 + 128], psZ, reads=["psZ"], writes=["selbT"])
            for br in range(2):
                if getattr(C, "lvl", 9) < 4 + br:
                    continue
                dest = osel if br == 0 else owin
                dkey = "osel" if br == 0 else "owin"
                KT = KsT if br == 0 else KwT
                Vv = Vs if br == 0 else Vw
                kts = list(range(0, 4 * Q + 4)) if br == 0 else list(range(max(0, 4 * Q - 4), 4 * Q + 4))
                for g in range(2):
                    O = [psA[0], psA[1], psB[0], psB[1]]
                    okeys = [("psA", 0), ("psA", 1), ("psB", 0), ("psB", 1)]
                    for n_, kt in enumerate(kts):
                        if br == 0:
                            r = nxt(rs, 3)
                            pg.mm(psS[r][:, :], Esel[0:64, kt, :], selbT[0:64, g, tq0:tq0 + 512], reads=["Esel", "selbT"], writes=[("psS", r)])
                            mi = nxt(rm, 2)
                            if kt >= 4 * Q:
                                pg.tt("dve", Mt[mi][:], psS[r][:, :], causb[:, kt - 4 * Q, :], ALU.mult, reads=[("psS", r), "causb"], writes=[("Mt", mi)])
                            else:
                                pg.cp("dve", Mt[mi][:], psS[r][:, :], reads=[("psS", r)], writes=[("Mt", mi)])
                            mask, mkeys = Mt[mi][:], [("Mt", mi)]
                        else:
                            wsrc = winb4 if Q == 4 else winb
                            mask, mkeys = wsrc[:, kt - 4 * Q + 4, :], ["winb", "winb4"]
                        for h4 in range(4):
                            h = g * 4 + h4
                            base, hp, _g = qk(h)
                            r2 = nxt(rs, 3)
                            pg.mm(psS[r2][:, :], KT[base:base + 64, g, kt * 128:(kt + 1) * 128], qT[base:base + 64, hp, tq0:tq0 + 512],
                                  reads=["qT"], writes=[("psS", r2)])
                            k = nxt(re, 4)
                            pg.act(eT[k][:, :], psS[r2][:, :], AF.Exp, reads=[("psS", r2)], writes=[("eT", k)])
                            pg.tt("dve", eT2[k][:, :], eT[k][:, :], mask, ALU.mult, reads=[("eT", k)] + mkeys, writes=[("eT2", k)])
                            dq.append(lambda h4=h4, kt=kt, k=k, n_=n_, O=O, okeys=okeys, Vv=Vv, g=g, kts=kts: pg.mm(
                                O[h4][0:65, :], Vv[:, kt, g, :], eT2[k][:, :], start=(n_ == 0), stop=(n_ == len(kts) - 1),
                                reads=[("eT2", k)], writes=[okeys[h4]]))
                            if len(dq) > 2:
                                dq.pop(0)()
                    while dq:
                        dq.pop(0)()
                    for h4 in range(4):
                        h = g * 4 + h4
                        o = h4 % 2
                        pg.cp("act", osT[o][:, :], O[h4][0:65, :], reads=[okeys[h4]], writes=[("osT", o)])
                        r3 = nxt(rs, 3)
                        Tp = psS[r3]
                        for qq in range(4):
                            pg.tr(Tp[:, qq * 65:(qq + 1) * 65], osT[o][0:65, qq * 128:(qq + 1) * 128], idf[0:65, 0:65],
                                  reads=[("osT", o), "idf"], writes=[("psS", r3)])
                        T3 = Tp[:, 0:260].rearrange("p (q c) -> p q c", c=65)
                        pg.ts("dve", dn2[:, 0:4], T3[:, :, 64], 1e-30, None, ALU.max, reads=[("psS", r3)], writes=["dn2a"])
                        pg.op("dve", lambda e: e.reciprocal(dn2[:, 4:8], dn2[:, 0:4]), reads=["dn2a"], writes=["dn2b"])
                        pg.tt("dve", dest[:, :, h, :], T3[:, :, 0:64], dn2[:, 4:8].unsqueeze(2).to_broadcast([128, 4, 64]), ALU.mult,
                              reads=[("psS", r3), "dn2b"], writes=[dkey])
            for ii in range(4):
                i = Q * 4 + ii
                t0 = i * 128
                G3 = GT[:, i, :].rearrange("p (h c) -> p h c", c=3)
                gb = lambda c: G3[:, :, c].unsqueeze(2).to_broadcast([128, 8, 64])
                pg.tt("dve", yb[:], ocmp[:, ii, :, :], gb(0), ALU.mult, reads=["ocmp", "GT"], writes=["yb"])
                pg.tt("pool", yb2[:], osel[:, ii, :, :], gb(1), ALU.mult, reads=["osel", "GT"], writes=["yb2"])
                pg.tt("dve", yb[:], yb[:], yb2[:], ALU.add, reads=["yb", "yb2"], writes=["yb"])
                pg.tt("pool", yb2[:], owin[:, ii, :, :], gb(2), ALU.mult, reads=["owin", "GT", "yb2"], writes=["yb2"])
                pg.tt("dve", yb[:], yb[:], yb2[:], ALU.add, reads=["yb", "yb2"], writes=["yb"])
                pg.ld(dr["YB"][t0:t0 + 128, :], yb[:].rearrange("p h k -> p (h k)"), reads=["yb"], writes=[("YB", i)])
        pg.barrier()
        pg.emit()


_NSA_CONSTS = {}


def nsa_consts(hh=1):
    if hh in _NSA_CONSTS:
        return _NSA_CONSTS[hh]
    import ml_dtypes
    bf = ml_dtypes.bfloat16
    c = {}
    n = np.arange(256)
    t = np.arange(T)
    nlo = 128 if hh == 0 else 0
    cm = np.where((16 * n[:, None] + 31 <= t[None, :]) & (n[:, None] < 255) & (n[:, None] >= nlo), 0.0, NEG).astype(np.float32)
    c["c_cmpb"] = np.ascontiguousarray(cm.reshape(2, 128, T).transpose(1, 0, 2)).astype(bf)
    es = np.zeros((64, 32, 128), np.float32)
    for kt in range(32):
        for key in range(128):
            es[2 * kt + key // 64, kt, key] = 1.0
    c["c_esel"] = np.concatenate([es, es], 0).astype(bf)
    key = np.arange(128)
    q = np.arange(512)
    cb = np.zeros((128, 4, 512), np.float32)
    for d in range(4):
        cb[:, d, :] = np.where((d * 128 + key[:, None]) <= q[None, :], 1.0, 0.0)
    c["c_causb"] = cb.astype(bf)
    wb = np.zeros((128, 8, 512), np.float32)
    for r in range(8):
        ka = (r - 4) * 128 + key[:, None]
        wb[:, r, :] = np.where((ka <= q[None, :]) & (ka > q[None, :] - 512), 1.0, 0.0)
    c["c_winb"] = wb.astype(bf)
    wb4 = wb.copy()
    if hh == 0:
        wb4[:, 0:4, :] = 0.0
    c["c_winb4"] = wb4.astype(bf)
    cs = np.arange(256) * 16
    ss = np.arange(64) * 64
    ov = np.clip(np.minimum(cs[:, None] + 32, ss[None, :] + 64) - np.maximum(cs[:, None], ss[None, :]), 0, None) / 32.0
    ov[255, :] = 0.0
    c["ovl"] = np.ascontiguousarray(ov.reshape(2, 128, 64).transpose(1, 0, 2)).astype(np.float32)
    cur = t // 64
    j = np.arange(64)
    jlo = 32 if hh == 0 else 0
    valid = (j[None, :] <= cur[:, None]) & (j[None, :] >= jlo)
    forced = (j[None, :] == jlo) | (j[None, :] == cur[:, None]) | (j[None, :] == cur[:, None] - 1)
    vm = valid.astype(np.float32)
    fb = np.where(valid, 1000.0 * forced, -1.0).astype(np.float32)
    c["c_vmfb"] = np.ascontiguousarray(np.stack([vm, fb], 1).reshape(NT, 128, 2, 64))
    _NSA_CONSTS[hh] = c
    return c


def stage4(C):
    nc, pg, dr = C.nc, C.pg, C.dr
    with ExitStack() as st:
        sb, ps = _mk(C, st)
        wa = sb("m_wa", [128, 4, D], BF16)
        wb = sb("m_wb", [128, 4, D], BF16)
        wo = sb("m_wo", [128, 8, D], BF16)
        stg = sb("m_stg", [128, D], F32)
        idf = sb("m_idf", [128, 128], F32)
        idb = sb("m_idb", [128, 128], BF16)
        yab = [sb(f"m_yab{i}", [128, 1024], F32) for i in range(2)]
        yabb = sb("m_yabb", [128, 1024], BF16)
        yT = sb("m_yT", [128, 8, 128], BF16)
        gts = [sb(f"m_g{i}", [128, 2048], F32) for i in range(2)]
        xt = [sb(f"m_x{i}", [128, D], F32) for i in range(2)]
        mix = sb("m_mix", [128, D], F32)
        mix2 = sb("m_mix2", [128, D], F32)
        mixb = sb("m_mixb", [128, D], BF16)
        mT = sb("m_mT", [128, 8, 128], BF16)
        x1 = [sb(f"m_x1{i}", [128, D], F32) for i in range(2)]
        psT = ps("m_psT", [128, 1024], BF16)
        psm = [ps(f"m_psm{i}", [128, 512], F32) for i in range(4)]
        psT2 = ps("m_psT2", [128, 1024], BF16)
        pso = [ps(f"m_pso{i}", [128, 512], F32) for i in range(2)]

        pg.ld(idf[:], dr["ident"][:, :], writes=["idf"])
        pg.cp("dve", idb[:], idf[:], reads=["idf"], writes=["idb"])
        n = 0
        for (wt, nm, kcs) in ((wa, "w_branch_a", 4), (wb, "w_branch_b", 4), (wo, "w_out", 8)):
            for kc in range(kcs):
                pg.ld(stg[:], dr[nm][kc * 128:(kc + 1) * 128, :], writes=["stg"])
                pg.cp(("act", "dve", "pool")[n % 3], wt[:, kc, :], stg[:], reads=["stg"], writes=[nm])
                n += 1
        rowidx = sb("m_rowidx", [128, NT], I32)
        pg.ld(rowidx[:], dr["rowidx"][:, :], writes=["rowidx"])
        allk = lambda nm: [(nm, k) for k in range(NT)]
        for i in range(C.peer_tiles):
            s = i % 2
            t0 = i * 128
            ix = rowidx[:, i:i + 1]
            pg.ld(yab[s][:, 0:512], dr["YAL"][t0:t0 + 128, :], reads=[("YAL", i)], writes=[("yab", s)])
            igather(pg, yab[s][:, 512:1024], dr["YB"][:, :], ix, allk("YB") + ["rowidx"], [("yab2", s)])
            igather(pg, gts[s][:], dr["PG"][:, :], ix, allk("PG") + ["rowidx"], [("gts", s)])
            igather(pg, xt[s][:], dr["x"][:, :], ix, ["rowidx"], [("xt", s)])
            pg.cp("pool", yabb[:], yab[s][:], reads=[("yab", s), ("yab2", s)], writes=["yabb"])
            for j in range(8):
                pg.tr(psT[:, j * 128:(j + 1) * 128], yabb[:, j * 128:(j + 1) * 128], idb[:], reads=["yabb", "idb"], writes=["psT"])
            pg.cp("act", yT[:].rearrange("p a b -> p (a b)"), psT[:], reads=["psT"], writes=["yT"])
            for br in range(2):
                wt = wa if br == 0 else wb
                for nchunk in range(2):
                    pb = psm[br * 2 + nchunk]
                    for kc in range(4):
                        pg.mm(pb[:], yT[:, br * 4 + kc, :], wt[:, kc, nchunk * 512:(nchunk + 1) * 512], start=(kc == 0), stop=(kc == 3),
                              reads=["yT", "w_branch_a", "w_branch_b"], writes=[("psm", br * 2 + nchunk)])
            pg.act(gts[s][:], gts[s][:], AF.Sigmoid, reads=[("gts", s)], writes=[("gts", s)])
            for nchunk in range(2):
                c = slice(nchunk * 512, (nchunk + 1) * 512)
                pg.tt("dve", mix[:, c], psm[nchunk][:], gts[s][:, nchunk * 512:(nchunk + 1) * 512], ALU.mult,
                      reads=[("psm", nchunk), ("gts", s)], writes=[("mix", nchunk)])
                pg.tt("dve", mix2[:, c], psm[2 + nchunk][:], gts[s][:, 1024 + nchunk * 512:1024 + (nchunk + 1) * 512], ALU.mult,
                      reads=[("psm", 2 + nchunk), ("gts", s)], writes=[("mix2", nchunk)])
                pg.tt("pool", mixb[:, c], mix[:, c], mix2[:, c], ALU.add, reads=[("mix", nchunk), ("mix2", nchunk)], writes=[("mixb", nchunk)])
            for j in range(8):
                pg.tr(psT2[:, j * 128:(j + 1) * 128], mixb[:, j * 128:(j + 1) * 128], idb[:], reads=[("mixb", 0), ("mixb", 1), "idb"], writes=["psT2"])
            pg.cp("act", mT[:].rearrange("p a b -> p (a b)"), psT2[:], reads=["psT2"], writes=["mT"])
            for nchunk in range(2):
                for kc in range(8):
                    pg.mm(pso[nchunk][:], mT[:, kc, :], wo[:, kc, nchunk * 512:(nchunk + 1) * 512], start=(kc == 0), stop=(kc == 7),
                          reads=["mT", "w_out"], writes=[("pso", nchunk)])
                pg.tt("dve", x1[s][:, nchunk * 512:(nchunk + 1) * 512], pso[nchunk][:], xt[s][:, nchunk * 512:(nchunk + 1) * 512], ALU.add,
                      reads=[("pso", nchunk), ("xt", s)], writes=[("x1", s, nchunk)])
            pg.ld(dr["X1L"][t0:t0 + 128, :], x1[s][:], reads=[("x1", s, 0), ("x1", s, 1)], writes=[("X1L", i)])
        pg.barrier()
        pg.emit()


def table_conv_gen(C, sb):
    pg, dr = C.pg, C.dr
    NBUF = 4
    src = [sb(f"z_src{i}", [128, D], F32) for i in range(NBUF)]
    dst = [sb(f"z_dst{i}", [128, D], BF16) for i in range(NBUF)]
    n = 0
    for (tab, co) in (("peer_u", 0), ("peer_v", D)):
        for a in range(16384 // 128):
            b_ = n % NBUF
            pg.ld(src[b_][:], dr[tab][a * 128:(a + 1) * 128, :], writes=[("zsrc", b_)], q="sp")
            pg.cp("act", dst[b_][:], src[b_][:], reads=[("zsrc", b_)], writes=[("zdst", b_)])
            pg.ld(dr["UV"][a * 128:(a + 1) * 128, co:co + D], dst[b_][:], reads=[("zdst", b_)], writes=[("UV", co, a)], q="act")
            n += 1
            yield


def stage5(C):
    nc, pg, dr = C.nc, C.pg, C.dr
    NB = 12
    with ExitStack() as st:
        sb, ps = _mk(C, st)
        wq = sb("p_wq", [128, 8, 2048], F32)
        kT = sb("p_kT", [128, 2, 128], F32)
        kraw = sb("p_kraw", [128, 2, 128], F32)
        g2 = sb("p_g2", [128, D], F32)
        idf = sb("p_idf", [128, 128], F32)
        io16 = sb("p_io16", [128, 16], F32)
        eps = sb("p_eps", [128, 1], F32)
        x1 = [sb(f"p_x1{i}", [128, D], F32) for i in range(3)]
        h2 = [sb(f"p_h2{i}", [128, D], F32) for i in range(2)]
        junk = sb("p_junk", [128, D], BF16)

        ss = sb("p_ss", [128, 4], F32)
        h2T = sb("p_h2T", [128, 8, 128], F32)
        qT = sb("p_qT", [128, 16, 128], F32)
        sc = sb("p_sc", [128, 16, 128], F32)
        work = sb("p_work", [128, 256], F32)
        tv = sb("p_tv", [128, 16, 16], F32)
        tiu = sb("p_tiu", [128, 16, 16], U32)
        ti = sb("p_ti", [128, 16, 16], F32)
        cs = sb("p_cs", [128, 8, 256], F32)
        bs = sb("p_bs", [128, 8, 16], F32)
        posu = sb("p_posu", [128, 8, 16], U32)
        pa_u = sb("p_pau", [128, 8, 16], U32)
        pb_u = sb("p_pbu", [128, 8, 16], U32)
        pa = sb("p_pa", [128, 8, 16], F32)
        pb = sb("p_pb", [128, 8, 16], F32)
        oh = sb("p_oh", [128, 8, 16, 16], F32)
        ia = sb("p_ia", [128, 8, 16], F32)
        ib = sb("p_ib", [128, 8, 16], F32)
        eidf = sb("p_eidf", [128, 128], F32)
        eidi = [sb(f"p_eidi{i}", [128, 128], I32) for i in range(3)]
        gate = [sb(f"p_gate{i}", [128, 128], F32) for i in range(2)]
        zz = sb("p_zz", [128, 16], F32)
        actv = [sb(f"p_act{i}", [128, 128], F32) for i in range(2)]
        ga = [sb(f"p_ga{i}", [128, 128], F32) for i in range(2)]
        uv = [sb(f"p_uv{i}", [128, 2 * D], BF16) for i in range(NB)]
        h2b = [sb(f"p_h2b{i}", [128, D], BF16) for i in range(2)]
        idb = sb("p_idb", [128, 128], BF16)
        junk2 = sb("p_junk2", [128, D], F32)
        dg = [sb(f"p_dg{i}", [128, 128], BF16) for i in range(4)]
        yo = [sb(f"p_yo{i}", [128, D], F32) for i in range(1)]
        psT = ps("p_psT", [128, 8, 128], F32)
        psQ = [ps(f"p_psQ{i}", [128, 512], F32) for i in range(2)]
        psY = [ps(f"p_psY{i}", [128, 512], F32) for i in range(2)]

        pg.ld(idf[:], dr["ident"][:, :], writes=["idf"])
        pg.ld(g2[:], dr["norm2_g_b"][:, :], writes=["g2"])
        pg.cp("dve", idb[:], idf[:], reads=["idf"], writes=["idb"])
        pg.ld(io16[:], dr["iota16"][:, :], writes=["io16"])
        rowidx = sb("p_rowidx", [128, NT], I32)
        pg.ld(rowidx[:], dr["rowidx"][:, :], writes=["rowidx"])
        pg.memset("dve", eps[:], 1e-6, writes=["eps"])
        for kc in range(8):
            pg.ld(wq[:, kc, :], dr["peer_wq"][kc * 128:(kc + 1) * 128, :], writes=["wq"])
        pg.ld(kraw[:, 0, :], dr["peer_k1"][:, :], writes=["kraw"])
        pg.ld(kraw[:, 1, :], dr["peer_k2"][:, :], writes=["kraw"])
        for hf in range(2):
            pg.tr(psQ[0][:, hf * 128:(hf + 1) * 128], kraw[:, hf, :], idf[:], reads=["kraw", "idf"], writes=[("psQ", 0)])
        pg.cp("dve", kT[:].rearrange("p a b -> p (a b)"), psQ[0][:, 0:256], reads=[("psQ", 0)], writes=["kT"])
        ntiles = getattr(C, "peer_tiles", NT)

        def front(i):
            s = i % 2
            t0 = i * 128
            pg.ld(x1[i % 3][:, :], dr["X1L"][t0:t0 + 128, :], reads=[("X1L", i)], writes=[("x1", i % 3)])
            yield
            pg.tt("pool", junk2[:], x1[i % 3][:], x1[i % 3][:], ALU.mult, reads=[("x1", i % 3), "junk2"], writes=["junk2"])
            yield
            pg.red("dve", ss[:, 0:1], junk2[:], ALU.add, reads=["junk2"], writes=["ss0"])
            yield
            pg.act(ss[:, 1:2], ss[:, 0:1], AF.Sqrt, reads=["ss0", "eps"], writes=["ss1"], scale=1.0 / D, bias=eps[:, 0:1])
            yield
            pg.op("dve", lambda e: e.reciprocal(ss[:, 2:3], ss[:, 1:2]), reads=["ss1"], writes=["ss2"])
            yield
            pg.stt("dve", h2[s][:], x1[i % 3][:], ss[:, 2:3], g2[:], ALU.mult, ALU.mult, reads=[("x1", i % 3), "ss2", "g2"], writes=[("h2", s)])
            yield
            pg.cp("pool", h2b[s][:], h2[s][:], reads=[("h2", s)], writes=[("h2b", s)])
            yield
            for j in range(8):
                pg.tr(psT[:, j, :], h2[s][:, j * 128:(j + 1) * 128], idf[:], reads=[("h2", s), "idf"], writes=["psT"])
                yield
            pg.cp("act", h2T[:], psT[:], reads=["psT"], writes=["h2T"])
            yield
            for cg in range(4):
                bk = psQ[cg % 2]
                for cc in range(4):
                    c = cg * 4 + cc
                    for kc in range(8):
                        pg.mm(bk[:, cc * 128:(cc + 1) * 128], wq[:, kc, c * 128:(c + 1) * 128], h2T[:, kc, :], start=(kc == 0), stop=(kc == 7),
                              reads=["wq", "h2T"], writes=[("psQ", cg % 2)])
                        yield
                pg.cp("act" if cg % 2 == 0 else "dve", qT[:, cg * 4:(cg + 1) * 4, :].rearrange("p a b -> p (a b)"), bk[:],
                      reads=[("psQ", cg % 2)], writes=[("qT", cg)])
                yield
            for cg in range(4):
                bk = psQ[cg % 2]
                for cc in range(4):
                    c = cg * 4 + cc
                    pg.mm(bk[:, cc * 128:(cc + 1) * 128], qT[:, c, :], kT[:, c % 2, :], reads=[("qT", cg), "kT"], writes=[("psQ", cg % 2)])
                    yield
                pg.cp("act" if cg % 2 == 0 else "dve", sc[:, cg * 4:(cg + 1) * 4, :].rearrange("p a b -> p (a b)"), bk[:],
                      reads=[("psQ", cg % 2)], writes=[("sc", cg)])
                yield
            for c in range(16):
                k_ = ("sc", c // 4)
                pg.op("dve", lambda e, c=c: e.max(tv[:, c, 0:8], sc[:, c, :]), reads=[k_], writes=[("tv", c)])
                yield
                pg.op("dve", lambda e, c=c: e.max_index(tiu[:, c, 0:8], tv[:, c, 0:8], sc[:, c, :]), reads=[k_, ("tv", c)], writes=[("tiu", c)])
                yield
                pg.op("dve", lambda e, c=c: e.match_replace(work[:, 0:128], tv[:, c, 0:8], sc[:, c, :], -1e30), reads=[k_, ("tv", c), "work"], writes=["work"])
                yield
                pg.op("dve", lambda e, c=c: e.max(tv[:, c, 8:16], work[:, 0:128]), reads=["work"], writes=[("tv2", c)])
                yield
                pg.op("dve", lambda e, c=c: e.max_index(tiu[:, c, 8:16], tv[:, c, 8:16], sc[:, c, :]), reads=[k_, ("tv2", c)], writes=[("tiu2", c)])
                yield
            allt = [("tv", c) for c in range(16)] + [("tv2", c) for c in range(16)]
            alli = [("tiu", c) for c in range(16)] + [("tiu2", c) for c in range(16)]
            pg.cp("dve", ti[:], tiu[:], reads=alli, writes=["ti"])
            yield
            tv4 = tv[:].rearrange("p (h f) a -> p h f a", f=2)
            ti4 = ti[:].rearrange("p (h f) a -> p h f a", f=2)
            cs4 = cs[:].rearrange("p h (a b) -> p h a b", a=16)
            A_ = lambda t4: t4[:, :, 0, :].unsqueeze(3).to_broadcast([128, 8, 16, 16])
            B_ = lambda t4: t4[:, :, 1, :].unsqueeze(2).to_broadcast([128, 8, 16, 16])
            pg.tt("dve", cs4, A_(tv4), B_(tv4), ALU.add, reads=allt, writes=["cs"])
            yield
            for h in range(8):
                pg.op("dve", lambda e, h=h: e.max(bs[:, h, 0:8], cs[:, h, :]), reads=["cs"], writes=[("bs", h)])
                yield
                pg.op("dve", lambda e, h=h: e.max_index(posu[:, h, 0:8], bs[:, h, 0:8], cs[:, h, :]), reads=["cs", ("bs", h)], writes=[("posu", h)])
                yield
                pg.op("dve", lambda e, h=h: e.match_replace(work[:, :], bs[:, h, 0:8], cs[:, h, :], -1e30), reads=["cs", ("bs", h), "work"], writes=["work"])
                yield
                pg.op("dve", lambda e, h=h: e.max(bs[:, h, 8:16], work[:, :]), reads=["work"], writes=[("bs2", h)])
                yield
                pg.op("dve", lambda e, h=h: e.max_index(posu[:, h, 8:16], bs[:, h, 8:16], cs[:, h, :]), reads=["cs", ("bs2", h)], writes=[("posu2", h)])
                yield
            allb = [("bs", h) for h in range(8)] + [("bs2", h) for h in range(8)]
            allp = [("posu", h) for h in range(8)] + [("posu2", h) for h in range(8)]
            G = gate[s][:].rearrange("p (h j) -> p h j", h=8)
            pg.tt("dve", G, bs[:], bs[:, :, 0:1].to_broadcast([128, 8, 16]), ALU.subtract, reads=allb, writes=[("gate", s)])
            yield
            pg.act(G, G, AF.Exp, reads=[("gate", s)], writes=[("gate", s)])
            yield
            pg.red("dve", zz[:, 0:8], G, ALU.add, reads=[("gate", s)], writes=["zz0"])
            yield
            pg.op("dve", lambda e: e.reciprocal(zz[:, 8:16], zz[:, 0:8]), reads=["zz0"], writes=["zz1"])
            yield
            pg.tt("dve", G, G, zz[:, 8:16].unsqueeze(2).to_broadcast([128, 8, 16]), ALU.mult, reads=[("gate", s), "zz1"], writes=[("gate", s)])
            yield
            pg.ts("dve", pa_u[:], posu[:], 4, None, ALU.logical_shift_right, reads=allp, writes=["pau"])
            yield
            pg.ts("dve", pb_u[:], posu[:], 15, None, ALU.bitwise_and, reads=allp, writes=["pbu"])
            yield
            pg.cp("dve", pa[:], pa_u[:], reads=["pau"], writes=["pa"])
            yield
            pg.cp("dve", pb[:], pb_u[:], reads=["pbu"], writes=["pb"])
            yield
            iob = io16[:, :].unsqueeze(1).unsqueeze(1).to_broadcast([128, 8, 16, 16])
            for (pp, key, half, dst, dk_) in ((pa, "pa", 0, ia, "ia"), (pb, "pb", 1, ib, "ib")):
                pg.tt("dve", oh[:], pp[:].unsqueeze(3).to_broadcast([128, 8, 16, 16]), iob, ALU.is_equal, reads=[key, "io16", "oh"], writes=["oh"])
                yield
                tsel = ti4[:, :, half, :].unsqueeze(2).to_broadcast([128, 8, 16, 16])
                pg.tt("dve", oh[:], oh[:], tsel, ALU.mult, reads=["oh", "ti"], writes=["oh"])
                yield
                pg.red("dve", dst[:], oh[:], ALU.add, reads=["oh"], writes=[dk_])
                yield
            pg.stt("dve", eidf[:].rearrange("p (h j) -> p h j", h=8), ia[:], 128.0, ib[:], ALU.mult, ALU.add, reads=["ia", "ib"], writes=["eidf"])
            yield
            pg.cp("dve", eidi[i % 3][:], eidf[:], reads=["eidf"], writes=[("eidi", i % 3)])
            yield

        GS = 2
        SK = 1

        def gstep(i, e_):
            s = i % 2
            b_ = e_ % NB
            pg.dma("pool", lambda e, e_=e_, b_=b_, i=i: e.indirect_dma_start(
                out=uv[b_][:, :], out_offset=None, in_=dr["UV"][:, :],
                in_offset=bass.IndirectOffsetOnAxis(ap=eidi[i % 3][:, e_:e_ + 1], axis=0)),
                reads=[("eidi", i % 3)], writes=[("uv", b_)])
            pg.op("dve", lambda e, e_=e_, b_=b_, s=s: e.scalar_tensor_tensor(junk[:], uv[b_][:, 0:D], 1.0, h2b[s][:], ALU.mult, ALU.mult,
                                                                              accum_out=actv[s][:, e_:e_ + 1]),
                  reads=[("uv", b_), ("h2b", s)], writes=[("act", s, e_)])

        def gelu_grp(i, k):
            s = i % 2
            sl = slice(k * GS, (k + 1) * GS)
            pg.act(ga[s][:, sl], actv[s][:, sl], AF.Gelu, reads=[("act", s, e_) for e_ in range(k * GS, (k + 1) * GS)], writes=[("ga", s, k)])

        def fin_grp(i, k):
            s = i % 2
            sl = slice(k * GS, (k + 1) * GS)
            pg.tt("dve", ga[s][:, sl], ga[s][:, sl], gate[s][:, sl], ALU.mult, reads=[("ga", s, k), ("gate", s)], writes=[("ga", s, k)])
            for e_ in range(k * GS, (k + 1) * GS):
                b_ = e_ % NB
                d_ = e_ % 4
                pg.act(dg[d_][:], idb[:], AF.Copy, reads=[("ga", s, k), "idb"], writes=[("dg", d_)], scale=ga[s][:, e_:e_ + 1])
                for n_ in range(2):
                    pg.mm(psY[n_][:], dg[d_][:], uv[b_][:, D + n_ * 512:D + (n_ + 1) * 512], start=(e_ == 0), stop=(e_ == 127),
                          reads=[("dg", d_), ("uv", b_)], writes=[("psY", n_)])

        def tail(i):
            s = i % 2
            t0 = i * 128
            for n_ in range(2):
                pg.tt("dve", yo[0][:, n_ * 512:(n_ + 1) * 512], psY[n_][:], x1[i % 3][:, n_ * 512:(n_ + 1) * 512], ALU.add,
                      reads=[("psY", n_), ("x1", i % 3)], writes=[("yo", 0, n_)])
            pg.ld(dr["out"][t0:t0 + 128, :], yo[0][:], reads=[("yo", 0, 0), ("yo", 0, 1)], writes=[("out", i)])

        def drain(g, n=None):
            k = 0
            while g is not None and (n is None or k < n):
                try:
                    next(g)
                except StopIteration:
                    return None
                k += 1
            return g

        drain(front(0))
        for i in range(ntiles):
            gen2 = front(i + 1) if i + 1 < ntiles else None
            for k in range(128 // GS):
                for e_ in range(k * GS, (k + 1) * GS):
                    gstep(i, e_)
                    gen2 = drain(gen2, 3)
                gelu_grp(i, k)
                if k >= SK:
                    fin_grp(i, k - SK)
            for k in range(128 // GS - SK, 128 // GS):
                fin_grp(i, k)
            drain(gen2)
            tail(i)
        pg.barrier()
        pg.emit()


_NC_CACHE = {}


def kernel(**inputs):
    inputs = {k: np.asarray(v) for k, v in inputs.items()}
    ntl = NT // 2
    if "nc" not in _NC_CACHE:
        _NC_CACHE["nc"] = build([stage1, stage2a, stage2x, stage2c, stage3, stage4, stage5], peer_tiles=ntl)
    nc = _NC_CACHE["nc"]
    base = {}
    in_maps = []
    for c in range(8):
        b, hh = c % 4, c // 4
        if hh not in base:
            base[hh] = host_inputs(inputs, b, hh, ntl)
            m = base[hh]
        else:
            m = dict(base[hh])
            m["x"] = core_x(inputs, b, hh)
        in_maps.append(m)
    res = run_bass_kernel_spmd(nc, in_maps, core_ids=list(range(8)))
    out = np.zeros((4, T, D), np.float32)
    for c in range(8):
        b, hh = c % 4, c // 4
        out[b, hh * ntl * 128:(hh + 1) * ntl * 128, :] = res.results[c]["out"]
    return out


def stage2x(C):
    nc, pg, dr = C.nc, C.pg, C.dr
    with ExitStack() as st:
        sb, ps = _mk(C, st)
        idf = sb("x_idf", [128, 128], F32)
        tri = sb("x_tri", [128, 128], F32)
        msk = sb("x_msk", [128, 3, 128], F32)
        ones = sb("x_ones", [128, 1], F32)
        inp = [[sb(f"x_in{s}_{j}", [128, 512], F32) for j in range(6)] for s in range(2)]
        Pt = sb("x_P", [128, 512], F32)
        iP = sb("x_iP", [128, 512], F32)
        Pp = sb("x_Pp", [128, 512], F32)
        tm = [[sb(f"x_tm{s}_{j}", [128, 512], F32) for j in range(4)] for s in range(2)]
        fm = [[sb(f"x_fm{s}_{j}", [64, 8, 128], F32) for j in range(4)] for s in range(2)]
        M = [[sb(f"x_M{s}_{j}", [128, 8, 128], (BF16 if j in (0, 4) else F32)) for j in range(5)] for s in range(2)]
        Xb = sb("x_Xb", [128, 8, 128], BF16)
        idb = sb("x_idb", [128, 128], BF16)
        X = [sb(f"x_X{s}", [128, 8, 128], F32) for s in range(2)]
        PC = [sb(f"x_PC{s}", [64, 8], F32) for s in range(2)]
        N2 = [sb(f"x_N2_{j}", [128, 8, 128], BF16) for j in range(2)]
        N2T = [sb(f"x_N2T_{j}", [128, 8, 128], BF16) for j in range(2)]
        Z = [sb(f"x_Z{j}", [64, 512], F32) for j in range(2)]
        rhs_sb = sb("x_rhs", [128, 512], F32)
        U_sb = sb("x_U", [128, 512], F32)
        Y_sb = [sb(f"x_Y{j}", [128, 512], F32) for j in range(2)]
        bank = [ps(f"x_bank{j}", [128, 512], F32) for j in range(8)]

        pg.ld(idf[:], dr["ident"][:, :], writes=["idf"])
        pg.ld(tri[:], dr["c_tri"][:, :], writes=["tri"])
        pg.ld(msk[:], dr["c_msk"][:, :, :], writes=["msk"])
        pg.memset("dve", ones[:], 1.0, writes=["ones"])
        pg.cp("dve", idb[:], idf[:], reads=["idf"], writes=["idb"])
        pg.memset("dve", Z[0][:], 0.0, writes=[("Z", 0)])
        names = ("RR", "RKK", "RLW", "RB", "RKp", "RV")
        bk = [0]

        def nb():
            v = bk[0]
            bk[0] = (v + 1) % 8
            return v

        def pre(c):
            s = c % 2
            t0 = c * 128
            I = inp[s]
            for j, nm in enumerate(names):
                pg.ld(I[j][:], dr[nm][t0:t0 + 128, :], reads=[(nm, c)], writes=[("in", s, j)], q=("pool" if j % 2 == 0 else "sp"))
            r_, kkn, lw, b_, kp, v_ = [t_[:] for t_ in I]
            bL = nb()
            pg.mm(bank[bL][:], tri[:], lw, reads=["tri", ("in", s, 2)], writes=[("bank", bL)])
            bC = nb()
            for h in range(8):
                pg.mm(bank[bC][0:64, h:h + 1], I[2][:, h * 64:(h + 1) * 64], ones[:, 0:1], reads=[("in", s, 2), "ones"], writes=[("bank", bC)])
            pg.act(PC[s][:], bank[bC][0:64, 0:8], AF.Exp, reads=[("bank", bC)], writes=[("PC", s)])
            pg.act(Pt[:], bank[bL][:], AF.Exp, reads=[("bank", bL)], writes=["P"])
            pg.act(iP[:], bank[bL][:], AF.Exp, reads=[("bank", bL)], writes=["iP"], scale=-1.0)
            pg.tt("dve", Pp[:], bank[bL][:], lw, ALU.subtract, reads=[("bank", bL), ("in", s, 2)], writes=["Pp"])
            pg.act(Pp[:], Pp[:], AF.Exp, reads=["Pp"], writes=["Pp"])
            TM = tm[s]
            pg.tt("pool", TM[0][:], r_, Pt[:], ALU.mult, reads=[("in", s, 0), "P"], writes=[("tm", s, 0)])
            pg.stt("dve", TM[1][:], kkn, -1.0, Pp[:], ALU.mult, ALU.mult, reads=[("in", s, 1), "Pp"], writes=[("tm", s, 1)])
            pg.tt("pool", TM[2][:], b_, iP[:], ALU.mult, reads=[("in", s, 3), "iP"], writes=[("tm", s, 2)])
            pg.tt("dve", TM[3][:], kp, iP[:], ALU.mult, reads=[("in", s, 4), "iP"], writes=[("tm", s, 3)])
            for j in range(4):
                if j == 0 and c < NT // 2:
                    continue
                for hg in range(2):
                    bT = nb()
                    for hh in range(4):
                        h = hg * 4 + hh
                        pg.tr(bank[bT][0:64, hh * 128:(hh + 1) * 128], TM[j][:, h * 64:(h + 1) * 64], idf[:], reads=[("tm", s, j), "idf"], writes=[("bank", bT)])
                    pg.cp("act" if (j + hg) % 2 == 0 else "dve", fm[s][j][:, hg * 4:(hg + 1) * 4, :].rearrange("p a b -> p (a b)"), bank[bT][0:64, :],
                          reads=[("bank", bT)], writes=[("fm", s, j, hg)])
            FR, FKK, FB, FK = fm[s]
            combos = ((0, FB, 2, FKK, 1, 0), (1, FK, 3, FKK, 1, 0), (2, FB, 2, FR, 0, 1), (3, FK, 3, FR, 0, 1), (4, FKK, 1, FB, 2, 2))
            for hg in range(2):
                for (mi, L_, lj, R_, rj, mk) in combos:
                    if mi in (2, 3) and c < NT // 2:
                        continue
                    bM = nb()
                    for hh in range(4):
                        h = hg * 4 + hh
                        pg.mm(bank[bM][:, hh * 128:(hh + 1) * 128], L_[:, h, :], R_[:, h, :], reads=[("fm", s, lj, hg), ("fm", s, rj, hg)], writes=[("bank", bM)])
                    pg.tt("dve", M[s][mi][:, hg * 4:(hg + 1) * 4, :], bank[bM][:].rearrange("p (a b) -> p a b", a=4),
                          msk[:, mk, :].unsqueeze(1).to_broadcast([128, 4, 128]), ALU.mult, reads=[("bank", bM), "msk"], writes=[("M", s, mi, hg)])
                pg.tt("pool", Xb[:, hg * 4:(hg + 1) * 4, :], idb[:, :].unsqueeze(1).to_broadcast([128, 4, 128]), M[s][0][:, hg * 4:(hg + 1) * 4, :], ALU.subtract,
                      reads=[("M", s, 0, hg), "idb"], writes=[("Xb", hg)])
            curN = [M[s][0], M[s][0]]
            curNT = [M[s][4], M[s][4]]
            kN = [("M", s, 0, 0), ("M", s, 0, 1)]
            kNT = [("M", s, 4, 0), ("M", s, 4, 1)]
            for j in range(6):
                dst = j % 2
                for hg in range(2):
                    b1, b2 = nb(), nb()
                    for hh in range(4):
                        h = hg * 4 + hh
                        pg.mm(bank[b1][:, hh * 128:(hh + 1) * 128], curNT[hg][:, h, :], curN[hg][:, h, :], reads=[kN[hg], kNT[hg]], writes=[("bank", b1)])
                    for hh in range(4):
                        h = hg * 4 + hh
                        pg.mm(bank[b2][:, hh * 128:(hh + 1) * 128], curN[hg][:, h, :], curNT[hg][:, h, :], reads=[kN[hg], kNT[hg]], writes=[("bank", b2)])
                    pg.cp("act", N2[dst][:, hg * 4:(hg + 1) * 4, :].rearrange("p a b -> p (a b)"), bank[b1][:], reads=[("bank", b1)], writes=[("N2", dst, hg)])
                    pg.cp("dve", N2T[dst][:, hg * 4:(hg + 1) * 4, :].rearrange("p a b -> p (a b)"), bank[b2][:], reads=[("bank", b2)], writes=[("N2T", dst, hg)])
                for hg in range(2):
                    curN[hg], curNT[hg] = N2[dst], N2T[dst]
                    kN[hg], kNT[hg] = ("N2", dst, hg), ("N2T", dst, hg)
                for hg in range(2):
                    b3 = nb()
                    for hh in range(4):
                        h = hg * 4 + hh
                        pg.mm(bank[b3][:, hh * 128:(hh + 1) * 128], curNT[hg][:, h, :], Xb[:, h, :], reads=[kNT[hg], ("Xb", hg)], writes=[("bank", b3)])
                    xo = (X[s] if j == 5 else Xb)
                    pg.tt("dve", xo[:, hg * 4:(hg + 1) * 4, :].rearrange("p a b -> p (a b)"), Xb[:, hg * 4:(hg + 1) * 4, :].rearrange("p a b -> p (a b)"), bank[b3][:], ALU.add,
                          reads=[("bank", b3), ("Xb", hg)], writes=[("X", s, hg)] if j == 5 else [("Xb", hg)])

        def seq(c):
            s = c % 2
            t0 = c * 128
            zc, zn = Z[c % 2], Z[(c + 1) % 2]
            kz, kzn = ("Z", c % 2), ("Z", (c + 1) % 2)
            FR, FKK, FB, FK = fm[s]
            V = inp[s][5]
            hsl = lambda h: slice(h * 64, (h + 1) * 64)
            Mk = lambda mi: [("M", s, mi, 0), ("M", s, mi, 1)]
            fk = lambda j: [("fm", s, j, 0), ("fm", s, j, 1)]
            Xk = [("X", s, 0), ("X", s, 1)]
            bG = nb()
            for h in range(8):
                pg.mm(bank[bG][:, hsl(h)], M[s][1][:, h, :], V[:, hsl(h)], start=True, stop=False, reads=Mk(1) + [("in", s, 5)], writes=[("bank", bG)])
                pg.mm(bank[bG][:, hsl(h)], FKK[:, h, :], zc[:, hsl(h)], start=False, stop=True, reads=fk(1) + [kz], writes=[("bank", bG)])
            pg.ts("dve", rhs_sb[:], bank[bG][:], -1.0, None, ALU.mult, reads=[("bank", bG)], writes=["rhs"])
            bU = nb()
            for h in range(8):
                pg.mm(bank[bU][:, hsl(h)], X[s][:, h, :], rhs_sb[:, hsl(h)], reads=Xk + ["rhs"], writes=[("bank", bU)])
            pg.cp("act", U_sb[:], bank[bU][:], reads=[("bank", bU)], writes=["U"])
            bZ = nb()
            for h in range(8):
                pg.mm(bank[bZ][0:64, hsl(h)], tm[s][3][:, hsl(h)], V[:, hsl(h)], start=True, stop=False, reads=[("tm", s, 3), ("in", s, 5)], writes=[("bank", bZ)])
                pg.mm(bank[bZ][0:64, hsl(h)], idf[0:64, 0:64], zc[:, hsl(h)], start=False, stop=False, reads=["idf", kz], writes=[("bank", bZ)])
                pg.mm(bank[bZ][0:64, hsl(h)], tm[s][2][:, hsl(h)], U_sb[:, hsl(h)], start=False, stop=True, reads=[("tm", s, 2), "U"], writes=[("bank", bZ)])
            pg.tt("dve", zn[:].rearrange("p (h v) -> p h v", h=8), bank[bZ][0:64, :].rearrange("p (h v) -> p h v", h=8),
                  PC[s][:, :].unsqueeze(2).to_broadcast([64, 8, 64]), ALU.mult, reads=[("bank", bZ), ("PC", s)], writes=[kzn])
            if c < NT // 2:
                return
            bY = nb()
            for h in range(8):
                pg.mm(bank[bY][:, hsl(h)], M[s][3][:, h, :], V[:, hsl(h)], start=True, stop=False, reads=Mk(3) + [("in", s, 5)], writes=[("bank", bY)])
                pg.mm(bank[bY][:, hsl(h)], FR[:, h, :], zc[:, hsl(h)], start=False, stop=False, reads=fk(0) + [kz], writes=[("bank", bY)])
                pg.mm(bank[bY][:, hsl(h)], M[s][2][:, h, :], U_sb[:, hsl(h)], start=False, stop=True, reads=Mk(2) + ["U"], writes=[("bank", bY)])
            pg.cp("act", Y_sb[s][:], bank[bY][:], reads=[("bank", bY)], writes=[("Y", s)])
            pg.ld(dr["YS"][t0:t0 + 128, :], Y_sb[s][:], reads=[("Y", s)], writes=[("YS", c)])

        tcg = table_conv_gen(C, sb)
        pre(0)
        for c in range(NT):
            if c + 1 < NT:
                pre(c + 1)
            for _ in range(8):
                next(tcg, None)
            seq(c)
        for _ in tcg:
            pass
        pg.barrier()
        pg.emit()
```

```python
import numpy as np
import concourse.bass as bass
import concourse.mybir as mybir

F32 = mybir.dt.float32
BF16 = mybir.dt.bfloat16
I32 = mybir.dt.int32
U32 = mybir.dt.uint32
ALU = mybir.AluOpType
AF = mybir.ActivationFunctionType
AX = mybir.AxisListType

EPOCH = 20000
ENGS = ("pe", "act", "dve", "pool", "sp")
NDMASEM = 16


class Prog:
    def __init__(self, nc, stack):
        self.nc = nc
        self.stack = stack
        self.ops = {e: [] for e in ENGS}
        self.cnt = {e: 0 for e in ENGS}
        self.esems = {e: [] for e in ENGS}
        self.waited = {e: {} for e in ENGS}
        self.lastw = {}
        self.readers = {}
        self.dsems = {}
        self.dcount = {}
        self.dtarget = {}
        self.semobjs = {}
        self.alltokens = {}
        for q in ("sp", "act", "pool"):
            self.dsems[q] = [self._newsem(f"d_{q}_{i}") for i in range(NDMASEM)]
            self.dcount[q] = 0
            self.dtarget[q] = [0] * NDMASEM

    def _newsem(self, name):
        s = self.stack.enter_context(self.nc.semaphore(name))
        self.semobjs[name] = s
        return name

    def _esem(self, e, idx):
        ep = idx // EPOCH
        while len(self.esems[e]) <= ep:
            self.esems[e].append(self._newsem(f"e_{e}_{len(self.esems[e])}"))
        return self.esems[e][ep], (idx % EPOCH) + 1

    def _deps(self, reads, writes):
        toks = []
        for k in reads:
            t = self.lastw.get(k)
            if t is not None:
                toks.append(t)
        for k in writes:
            t = self.lastw.get(k)
            if t is not None:
                toks.append(t)
            toks.extend(self.readers.get(k, ()))
        return toks

    def _commit(self, tok, reads, writes):
        for k in reads:
            self.readers.setdefault(k, []).append(tok)
        for k in writes:
            self.lastw[k] = tok
            self.readers[k] = []
        self.alltokens[tok[0]] = max(self.alltokens.get(tok[0], 0), tok[1])

    def _waits(self, e, toks):
        need = {}
        for (s, v) in toks:
            if v > need.get(s, 0):
                need[s] = v
        out = []
        w = self.waited[e]
        for s, v in need.items():
            if w.get(s, 0) < v:
                w[s] = v
                out.append((s, v))
        return out

    def op(self, e, fn, reads=(), writes=()):
        toks = self._deps(reads, writes)
        if e == "pe":
            toks = [t for t in toks if not t[0].startswith("e_pe_")]
        waits = self._waits(e, toks)
        idx = self.cnt[e]
        self.cnt[e] += 1
        tok = self._esem(e, idx)
        self.ops[e].append((waits, fn, (tok[0], 1)))
        self._commit(tok, reads, writes)

    def dma(self, q, fn, reads=(), writes=()):
        toks = self._deps(reads, writes)
        n = self.dcount[q]
        self.dcount[q] += 1
        slot = n % NDMASEM
        sname = self.dsems[q][slot]
        prev = self.dtarget[q][slot]
        if prev > 0:
            toks.append((sname, prev))
        tgt = prev + 16
        self.dtarget[q][slot] = tgt
        waits = self._waits(q, toks)
        tok = (sname, tgt)
        self.ops[q].append((waits, fn, (sname, 16)))
        self._commit(tok, reads, writes)

    def mm(self, out, lhsT, rhs, start=True, stop=True, reads=(), writes=()):
        self.op("pe", lambda e: e.matmul(out, lhsT, rhs, start=start, stop=stop), reads, writes)

    def tr(self, out, in_, ident, reads=(), writes=()):
        self.op("pe", lambda e: e.transpose(out, in_, ident), reads, writes)

    def act(self, out, in_, func, reads=(), writes=(), bias=None, scale=None, eng="act"):
        kw = {}
        if bias is not None:
            kw["bias"] = bias
        if scale is not None:
            kw["scale"] = scale
        self.op(eng, lambda e: e.activation(out, in_, func, **kw), reads, writes)

    def tt(self, eng, out, in0, in1, op, reads=(), writes=()):
        self.op(eng, lambda e: e.tensor_tensor(out, in0, in1, op), reads, writes)

    def ts(self, eng, out, in0, s1, s2, op0, op1=None, reads=(), writes=()):
        if op1 is None:
            self.op(eng, lambda e: e.tensor_scalar(out, in0, s1, s2, op0), reads, writes)
        else:
            self.op(eng, lambda e: e.tensor_scalar(out, in0, s1, s2, op0, op1), reads, writes)

    def stt(self, eng, out, in0, scalar, in1, op0, op1, reads=(), writes=()):
        self.op(eng, lambda e: e.scalar_tensor_tensor(out, in0, scalar, in1, op0, op1), reads, writes)

    def cp(self, eng, out, in_, reads=(), writes=()):
        if eng == "act":
            self.op(eng, lambda e: e.copy(out, in_), reads, writes)
        else:
            self.op(eng, lambda e: e.tensor_copy(out, in_), reads, writes)

    def red(self, eng, out, in_, op, reads=(), writes=(), axis=None):
        ax = AX.X if axis is None else axis
        self.op(eng, lambda e: e.tensor_reduce(out, in_, ax, op), reads, writes)

    def memset(self, eng, ap, val, writes=()):
        self.op(eng, lambda e: e.memset(ap, val), (), writes)

    def ld(self, out, in_, reads=(), writes=(), q="sp"):
        self.dma(q, lambda e: e.dma_start(out, in_), reads, writes)

    def barrier(self):
        toks = list(self.alltokens.items())
        for e in ENGS:
            waits = self._waits(e, toks)
            if waits:
                self.ops[e].append((waits, None, None))
        self.lastw = {}
        self.readers = {}

    def emit(self):
        nc = self.nc
        so = self.semobjs
        with nc.Block() as block:
            def mk(e):
                def body(eng):
                    for waits, fn, inc in self.ops[e]:
                        for (s, v) in waits:
                            eng.wait_ge(so[s], v)
                        if fn is not None:
                            ins = fn(eng)
                            ins.then_inc(so[inc[0]], inc[1])
                return body
            block.tensor(mk("pe"))
            block.scalar(mk("act"))
            block.vector(mk("dve"))
            block.gpsimd(mk("pool"))
            block.sync(mk("sp"))
        self.ops = {e: [] for e in ENGS}
from contextlib import ExitStack
from concourse.bass_utils import run_bass_kernel_spmd

T = 4096
D = 1024
NT = T // 128
INW = 5144
RWC = 1792
O_RW = 0
O_Q = 1792
O_KC = 2304
O_VC = 2432
O_KS = 2560
O_VS = 2688
O_KW = 2816
O_VW = 2944
O_BG = 3072
O_GA = 3096
O_GB = 4120


class Ctx:
    pass


def _mk(C, st):
    nc = C.nc
    sb = lambda name, shape, dt: st.enter_context(nc.sbuf_tensor(name, shape, dt))
    ps = lambda name, shape, dt: st.enter_context(nc.psum_tensor(name, shape, dt))
    return sb, ps


def stage1(C):
    nc, pg, dr = C.nc, C.pg, C.dr
    with ExitStack() as st:
        sb, ps = _mk(C, st)
        win = sb("s1_win", [128, 8, INW], BF16)
        pj = [sb(f"s1_pj{i}", [128, INW], F32) for i in range(2)]
        xt = [sb(f"s1_xt{i}", [128, D], F32) for i in range(2)]
        junk = sb("s1_junk", [128, D], F32)
        hb = [sb(f"s1_h{i}", [128, D], BF16) for i in range(2)]
        hT = [sb(f"s1_hT{i}", [128, 8, 128], BF16) for i in range(2)]
        gt = sb("s1_g", [128, D], F32)
        idf = sb("s1_idf", [128, 128], F32)
        idb = sb("s1_idb", [128, 128], BF16)
        ss = [sb(f"s1_ss{i}", [128, 4], F32) for i in range(2)]
        psT = [ps(f"s1_psT{i}", [128, 8, 128], BF16) for i in range(2)]
        psm = [ps(f"s1_psm{i}", [128, 512], F32) for i in range(4)]

        pg.ld(gt[:], dr["norm1_g_b"][:, :], writes=["gt"])
        pg.ld(idf[:], dr["ident"][:, :], writes=["idf"])
        pg.cp("dve", idb[:], idf[:], reads=["idf"], writes=["idb"])
        engs = ["act", "dve", "pool"]
        for kc in range(8):
            b = pj[kc % 2]
            pg.ld(b[:], dr["w_in"][kc * 128:(kc + 1) * 128, :], writes=[("pjall", kc % 2)])
            pg.cp(engs[kc % 3], win[:, kc, :], b[:], reads=[("pjall", kc % 2)], writes=[("win", kc)])
        winkeys = [("win", kc) for kc in range(8)]
        chunks = []
        c0 = 0
        while c0 < INW:
            w = min(512, INW - c0)
            chunks.append((c0, w))
            c0 += w
        def A1(i):
            s = i % 2
            pg.ld(xt[s][:], dr["x"][i * 128:(i + 1) * 128, :], writes=[("xt", s)])
            pg.tt("dve", junk[:], xt[s][:], xt[s][:], ALU.mult, reads=[("xt", s)], writes=["junk"])
            pg.red("dve", ss[s][:, 0:1], junk[:], ALU.add, reads=["junk"], writes=[("ss", s)])
            pg.act(ss[s][:, 1:2], ss[s][:, 0:1], AF.Sqrt, reads=[("ss", s)], writes=[("ss1", s)],
                   scale=1.0 / D, bias=C.eps6[:, 0:1])
            pg.op("dve", lambda e, o=ss[s][:, 2:3], a=ss[s][:, 1:2]: e.reciprocal(o, a),
                  reads=[("ss1", s)], writes=[("ss2", s)])
            pg.stt("dve", hb[s][:], xt[s][:], ss[s][:, 2:3], gt[:], ALU.mult, ALU.mult,
                   reads=[("xt", s), ("ss2", s), "gt"], writes=[("hb", s)])

        def A2(i):
            s = i % 2
            for j in range(8):
                pg.tr(psT[s][:, j, :], hb[s][:, j * 128:(j + 1) * 128], idb[:],
                      reads=[("hb", s), "idb"], writes=[("psT", s)])
            pg.cp("act", hT[s][:], psT[s][:], reads=[("psT", s)], writes=[("hT", s)])

        chunks_lo = [(512, 512), (1024, 512), (1536, 128), (2304, 512), (2816, 256)]

        def B(i, lo, hi):
            s = i % 2
            if i < NT // 2 - 1:
                if lo != 0:
                    return
                for ci, (c0, w) in enumerate(chunks_lo):
                    pb = psm[ci % 4]
                    for kc in range(8):
                        pg.mm(pb[:, :w], hT[s][:, kc, :], win[:, kc, c0:c0 + w], start=(kc == 0), stop=(kc == 7),
                              reads=[("hT", s), ("win", kc)], writes=[("psm", ci % 4)])
                    pg.cp("act" if ci % 2 == 0 else "dve", pj[s][:, c0:c0 + w], pb[:, :w],
                          reads=[("psm", ci % 4)], writes=[("pj", s, k_) for k_ in range(len(chunks))] + ([("pjall", s)] if i < 8 else []))
                return
            for ci in range(lo, hi):
                c0, w = chunks[ci]
                pb = psm[ci % 4]
                for kc in range(8):
                    pg.mm(pb[:, :w], hT[s][:, kc, :], win[:, kc, c0:c0 + w], start=(kc == 0), stop=(kc == 7),
                          reads=[("hT", s), ("win", kc)], writes=[("psm", ci % 4)])
                pg.cp("act" if ci % 2 == 0 else "dve", pj[s][:, c0:c0 + w], pb[:, :w],
                      reads=[("psm", ci % 4)], writes=[("pj", s, ci), ("pjall", s)] if i < 8 else [("pj", s, ci)])

        def S(i):
            s = i % 2
            pg.ld(dr["P"][i * 128:(i + 1) * 128, 0:3096], pj[s][:, 0:3096],
                  reads=[("pj", s, ci) for ci in range(len(chunks))], writes=[("P", i)])
            pg.ld(dr["PG"][i * 128:(i + 1) * 128, :], pj[s][:, 3096:5144],
                  reads=[("pj", s, ci) for ci in range(len(chunks))], writes=[("PG", i)], q="act")

        A1(0)
        A2(0)
        A1(1)
        for i in range(NT):
            B(i, 0, 6)
            if i + 1 < NT:
                A2(i + 1)
            if i + 2 < NT:
                A1(i + 2)
            B(i, 6, len(chunks))
            S(i)
        pg.barrier()
        pg.emit()


def build(stages, dbg_out=(), dbg_in=(), lvl=9, sub=9, peer_tiles=NT):
    nc = bass.Bass("TRN2", target_bir_lowering=False)
    C = Ctx()
    C.peer_tiles = peer_tiles
    C.lvl = lvl
    C.sub = sub
    C.nc = nc
    dr = {}
    C.dr = dr

    def din(name, shape, dt=F32):
        dr[name] = nc.dram_tensor(name, list(shape), dt, kind="ExternalInput").ap()

    def dscr(name, shape, dt=F32):
        kind = "ExternalOutput" if name in dbg_out else ("ExternalInput" if name in dbg_in else "Internal")
        dr[name] = nc.dram_tensor(name, list(shape), dt, kind=kind).ap()

    din("x", [T, D])
    din("norm1_g_b", [128, D])
    din("ident", [128, 128])
    din("w_in", [D, INW])
    dscr("P", [T, INW])
    dscr("PG", [T, 2048])
    for nm in ("rw_mu_b",):
        din(nm, [128, RWC])
    for nm in ("rw_w0_b", "rw_a0_b", "rw_k_k_b", "rw_k_a_b", "rw_r_k_b", "rw_ln_w_b", "rw_ln_b_b", "rw_g_up"):
        din(nm, [128, 512])
    din("rw_w_up", [64, 512])
    din("rw_a_up", [64, 512])
    for nm in ("RB", "RKp", "RV", "RG", "YA", "RR", "RKK", "RLW", "YS"):
        dscr(nm, [T, 512])
    dscr("RBON", [T, 8])
    din("blkmask", [8, 512])
    din("c_tri", [128, 128])
    din("c_msk", [128, 3, 128])
    din("nsa_gains_b", [128, 768])
    din("nsa_kc_g_b", [128, 64])
    din("ovl", [128, 2, 64])
    din("posT", [128, 2, 32])
    din("cmp_w2", [128, 2, 2, 64])
    din("cmp_w1", [2, 128, 32, 256])
    din("c_cmpb", [128, 2, T], BF16)
    din("c_esel", [128, 32, 128], BF16)
    din("c_causb", [128, 4, 512], BF16)
    din("c_winb", [128, 8, 512], BF16)
    din("c_winb4", [128, 8, 512], BF16)
    din("c_vmfb", [NT, 128, 2, 64])
    dscr("YB", [T, 512])
    din("w_branch_a", [512, D])
    din("w_branch_b", [512, D])
    din("w_out", [D, D])
    dscr("X1L", [peer_tiles * 128, D])
    dscr("YAL", [peer_tiles * 128, 512])
    din("norm2_g_b", [128, D])
    din("iota16", [128, 16])
    din("rowidx", [128, NT], I32)
    din("peer_wq", [D, 2048])
    din("peer_k1", [128, 128])
    din("peer_k2", [128, 128])
    din("peer_u", [16384, D])
    din("peer_v", [16384, D])
    dscr("UV", [16384, 2 * D], BF16)
    dr["out"] = nc.dram_tensor("out", [peer_tiles * 128, D], F32, kind="ExternalOutput").ap()
    with ExitStack() as top:
        pg = Prog(nc, top)
        C.pg = pg
        C.eps6 = top.enter_context(nc.sbuf_tensor("c_eps6", [128, 1], F32))
        pg.memset("dve", C.eps6[:], 1e-6, writes=["eps6"])
        pg.barrier()
        for s in stages:
            s(C)
        pg.barrier()
        pg.emit()
    return nc


def core_x(inputs, b, hh):
    xb = np.asarray(inputs["x"][b])
    if hh == 0:
        return np.ascontiguousarray(np.concatenate([np.zeros((T // 2, D), np.float32), xb[0:T // 2]], 0))
    return np.ascontiguousarray(xb)


def host_inputs(inputs, b, hh=1, ntl=NT):
    g = lambda k: np.ascontiguousarray(inputs[k][0])
    m = {}
    m["x"] = core_x(inputs, b, hh)
    m["norm1_g_b"] = np.ascontiguousarray(np.broadcast_to(g("norm1_g")[None, :], (128, D)))
    m["ident"] = np.eye(128, dtype=np.float32)
    m["w_in"] = g("w_in")
    bc = lambda a: np.ascontiguousarray(np.broadcast_to(np.asarray(a).reshape(1, -1), (128, a.size)))
    m["rw_mu_b"] = bc(g("rw_mu"))
    for nm in ("rw_w0", "rw_a0", "rw_k_k", "rw_k_a", "rw_r_k", "rw_ln_w", "rw_ln_b"):
        m[nm + "_b"] = bc(g(nm))
    for nm in ("rw_g_up", "rw_w_up", "rw_a_up"):
        m[nm] = g(nm)
    bmk = np.zeros((8, 512), np.float32)
    for h in range(8):
        bmk[h, h * 64:(h + 1) * 64] = 1.0
    m["blkmask"] = bmk
    ii = np.arange(128)
    m["c_tri"] = (ii[:, None] <= ii[None, :]).astype(np.float32)
    m["c_msk"] = np.ascontiguousarray(np.stack([(ii[:, None] < ii[None, :]), (ii[:, None] <= ii[None, :]), (ii[:, None] > ii[None, :])], 1).astype(np.float32))
    m.update(nsa_consts(hh))
    for nm in ("w_branch_a", "w_branch_b", "w_out", "peer_wq", "peer_k1", "peer_k2", "peer_u", "peer_v"):
        m[nm] = g(nm)
    m["norm2_g_b"] = bc(g("norm2_g"))
    ri = np.zeros((128, NT), np.int32)
    ri[:, :ntl] = ((NT - ntl) * 128 + np.arange(ntl)[None, :] * 128 + np.arange(128)[:, None]).astype(np.int32)
    m["rowidx"] = ri
    m["iota16"] = np.ascontiguousarray(np.broadcast_to(np.arange(16, dtype=np.float32)[None, :], (128, 16)))
    m["nsa_gains_b"] = bc(np.concatenate([np.tile(g("nsa_q_g"), 8), np.tile(g("nsa_ks_g"), 2), np.tile(g("nsa_kw_g"), 2)]))
    m["nsa_kc_g_b"] = bc(g("nsa_kc_g"))
    posT = np.zeros((128, 2, 32), np.float32)
    posT[0:64, 0, :] = g("cmp_pos_k").T
    posT[0:64, 1, :] = g("cmp_pos_v").T
    m["posT"] = posT
    w2 = np.stack([g("cmp_k_w2").reshape(2, 128, 64), g("cmp_v_w2").reshape(2, 128, 64)], 0)
    m["cmp_w2"] = np.ascontiguousarray(w2.transpose(2, 0, 1, 3))
    w1 = []
    for nm in ("cmp_k_w1", "cmp_v_w1"):
        a = g(nm).reshape(32, 64, 256).transpose(1, 0, 2)
        w1.append(np.concatenate([a, a], 0))
    m["cmp_w1"] = np.ascontiguousarray(np.stack(w1, 0))
    return m


def dap(ap, offset, pattern):
    return bass.AP(ap.tensor, offset, [list(p) for p in pattern])


def stage2a(C):
    nc, pg, dr = C.nc, C.pg, C.dr
    with ExitStack() as st:
        sb, ps = _mk(C, st)
        mu = sb("a_mu", [128, RWC], F32)
        w0 = sb("a_w0", [128, 512], F32)
        a0 = sb("a_a0", [128, 512], F32)
        kkc = sb("a_kk", [128, 512], F32)
        kac = sb("a_ka", [128, 512], F32)
        rkc = sb("a_rk", [128, 512], F32)
        wup = sb("a_wup", [128, 512], F32)
        gup = sb("a_gup", [128, 512], F32)
        idf = sb("a_idf", [128, 128], F32)
        p_2 = [sb(f"a_p{i_}", [128, RWC], F32) for i_ in range(2)]
        pv_2 = [sb(f"a_pv{i_}", [128, RWC], F32) for i_ in range(2)]
        pm_2 = [sb(f"a_pm{i_}", [128, RWC], F32) for i_ in range(2)]
        lor_2 = [sb(f"a_lor{i_}", [128, 256], F32) for i_ in range(2)]
        lorT_2 = [sb(f"a_lorT{i_}", [128, 256], F32) for i_ in range(2)]
        wt_2 = [sb(f"a_wt{i_}", [128, 512], F32) for i_ in range(2)]
        lwt_2 = [sb(f"a_lwt{i_}", [128, 512], F32) for i_ in range(2)]
        at_2 = [sb(f"a_at{i_}", [128, 512], F32) for i_ in range(2)]
        gt_2 = [sb(f"a_gt{i_}", [128, 512], F32) for i_ in range(2)]
        kk_2 = [sb(f"a_kkt{i_}", [128, 512], F32) for i_ in range(2)]
        sq_2 = [sb(f"a_sq{i_}", [128, 512], F32) for i_ in range(2)]
        nrm_2 = [sb(f"a_nrm{i_}", [128, 32], F32) for i_ in range(2)]
        kkn_2 = [sb(f"a_kkn{i_}", [128, 512], F32) for i_ in range(2)]
        bt_2 = [sb(f"a_bt{i_}", [128, 512], F32) for i_ in range(2)]
        t1_2 = [sb(f"a_t1{i_}", [128, 512], F32) for i_ in range(2)]
        kp_2 = [sb(f"a_kp{i_}", [128, 512], F32) for i_ in range(2)]
        bon_2 = [sb(f"a_bon{i_}", [128, 8], F32) for i_ in range(2)]
        psl_2 = [ps(f"a_psl{i_}", [128, 512], F32) for i_ in range(2)]
        psw_2 = [ps(f"a_psw{i_}", [128, 512], F32) for i_ in range(2)]
        psa_2 = [ps(f"a_psa{i_}", [128, 512], F32) for i_ in range(2)]
        psg_2 = [ps(f"a_psg{i_}", [128, 512], F32) for i_ in range(2)]

        for (tile, name) in ((mu, "rw_mu_b"), (w0, "rw_w0_b"), (a0, "rw_a0_b"), (kkc, "rw_k_k_b"),
                             (kac, "rw_k_a_b"), (rkc, "rw_r_k_b"), (gup, "rw_g_up"), (idf, "ident")):
            pg.ld(tile[:], dr[name][:, :], writes=[name])
        pg.ld(wup[0:64, :], dr["rw_w_up"][:, :], writes=["wup0"])
        pg.ld(wup[64:128, :], dr["rw_a_up"][:, :], writes=["wup1"])
        P = dr["P"]
        def E_(i):
            t0 = i * 128
            s = i % 2
            p, pv, pm, lor, lorT, wt, lwt, at, gt, kk, sq, nrm, kkn, bt, t1, kp, bon = [t_[s] for t_ in (
                p_2, pv_2, pm_2, lor_2, lorT_2, wt_2, lwt_2, at_2, gt_2, kk_2, sq_2, nrm_2, kkn_2, bt_2, t1_2, kp_2, bon_2)]
            psl, psw, psa, psg = psl_2[s], psw_2[s], psa_2[s], psg_2[s]
            pg.ld(p[:], P[t0:t0 + 128, 0:RWC], reads=[("P", i)], writes=[("p", s)])
            if i == 0:
                pg.memset("dve", pv[0:1, :], 0.0, writes=[("pv0", s)])
                pg.ld(pv[1:128, :], P[0:127, 0:RWC], reads=[("P", 0)], writes=[("pv", s)])
                pvk = [("pv", s), ("pv0", s)]
            else:
                pg.ld(pv[:], P[t0 - 1:t0 + 127, 0:RWC], reads=[("P", i), ("P", i - 1)], writes=[("pv", s), ("pv0", s)])
                pvk = [("pv", s), ("pv0", s)]
            pg.tt("dve", pv[:], pv[:], p[:], ALU.subtract, reads=pvk + [("p", s)], writes=[("pv", s)])
            pg.tt("dve", pv[:], pv[:], mu[:], ALU.mult, reads=[("pv", s), "rw_mu_b"], writes=[("pv", s)])
            pg.tt("dve", pm[:], pv[:], p[:], ALU.add, reads=[("pv", s), ("p", s)], writes=[("pm", s)])
            r_ = pm[:, 0:512]
            k_ = pm[:, 512:1024]
            v_ = pm[:, 1024:1536]
            pg.act(lor[:, 0:64], pm[:, 1536:1600], AF.Tanh, reads=[("pm", s)], writes=[("lor0", s)])
            pg.cp("pool", lor[:, 64:128], pm[:, 1600:1664], reads=[("pm", s)], writes=[("lor1", s)])
            pg.act(lor[:, 128:256], pm[:, 1664:1792], AF.Sigmoid, reads=[("pm", s)], writes=[("lor2", s)])
            pg.tr(psl[:, 0:128], lor[:, 0:128], idf[:], reads=[("lor0", s), ("lor1", s), "ident"], writes=[("psl", s)])
            pg.tr(psl[:, 128:256], lor[:, 128:256], idf[:], reads=[("lor2", s), "ident"], writes=[("psl", s)])
            pg.cp("act", lorT[:], psl[:, 0:256], reads=[("psl", s)], writes=[("lorT", s)])
            pg.mm(psw[:], lorT[0:64, 0:128], wup[0:64, :], reads=[("lorT", s), "wup0"], writes=[("psw", s)])
            pg.mm(psa[:], lorT[64:128, 0:128], wup[64:128, :], reads=[("lorT", s), "wup1"], writes=[("psa", s)])
            pg.mm(psg[:], lorT[:, 128:256], gup[:], reads=[("lorT", s), "rw_g_up"], writes=[("psg", s)])
        def L_(i):
            t0 = i * 128
            s = i % 2
            p, pv, pm, lor, lorT, wt, lwt, at, gt, kk, sq, nrm, kkn, bt, t1, kp, bon = [t_[s] for t_ in (
                p_2, pv_2, pm_2, lor_2, lorT_2, wt_2, lwt_2, at_2, gt_2, kk_2, sq_2, nrm_2, kkn_2, bt_2, t1_2, kp_2, bon_2)]
            psl, psw, psa, psg = psl_2[s], psw_2[s], psa_2[s], psg_2[s]
            r_ = pm[:, 0:512]
            k_ = pm[:, 512:1024]
            v_ = pm[:, 1024:1536]
            pg.tt("dve", wt[:], psw[:], w0[:], ALU.add, reads=[("psw", s), "rw_w0_b"], writes=[("wt", s)])
            pg.act(wt[:], wt[:], AF.Sigmoid, reads=[("wt", s)], writes=[("wt", s)])
            pg.ts("dve", lwt[:], wt[:], -0.6065306597126334, None, ALU.mult, reads=[("wt", s)], writes=[("lwt", s)])
            pg.tt("dve", at[:], psa[:], a0[:], ALU.add, reads=[("psa", s), "rw_a0_b"], writes=[("at", s)])
            pg.act(at[:], at[:], AF.Sigmoid, reads=[("at", s)], writes=[("at", s)])
            pg.cp("act", gt[:], psg[:], reads=[("psg", s)], writes=[("gt", s)])
            pg.tt("dve", kk[:], k_, kkc[:], ALU.mult, reads=[("pm", s), "rw_k_k_b"], writes=[("kk", s)])
            pg.tt("pool", sq[:], kk[:], kk[:], ALU.mult, reads=[("kk", s)], writes=[("sq", s)])
            pg.red("dve", nrm[:, 0:8], sq[:].rearrange("p (h k) -> p h k", h=8), ALU.add, reads=[("sq", s)], writes=[("nrm0", s)])
            pg.act(nrm[:, 8:16], nrm[:, 0:8], AF.Sqrt, reads=[("nrm0", s)], writes=[("nrm1", s)])
            pg.ts("dve", nrm[:, 16:24], nrm[:, 8:16], 1e-12, None, ALU.max, reads=[("nrm1", s)], writes=[("nrm2", s)])
            pg.op("dve", lambda e, nrm=nrm: e.reciprocal(nrm[:, 24:32], nrm[:, 16:24]), reads=[("nrm2", s)], writes=[("nrm3", s)])
            rinv_b = nrm[:, 24:32].unsqueeze(2).to_broadcast([128, 8, 64])
            v3 = lambda tl: tl[:].rearrange("p (h k) -> p h k", h=8)
            pg.stt("dve", v3(kkn), v3(kk), -1.0, rinv_b, ALU.mult, ALU.mult, reads=[("kk", s), ("nrm3", s)], writes=[("kkn", s)])
            pg.stt("dve", bt[:], kkn[:], -1.0, at[:], ALU.mult, ALU.mult, reads=[("kkn", s), ("at", s)], writes=[("bt", s)])
            pg.stt("dve", t1[:], at[:], -1.0, kac[:], ALU.add, ALU.mult, reads=[("at", s), "rw_k_a_b"], writes=[("t1", s)])
            pg.stt("dve", kp[:], t1[:], 1.0, k_, ALU.add, ALU.mult, reads=[("t1", s), ("pm", s)], writes=[("kp", s)])
            pg.tt("pool", sq[:], r_, kp[:], ALU.mult, reads=[("pm", s), ("kp", s), ("sq", s)], writes=[("sq", s)])
            pg.tt("pool", sq[:], sq[:], rkc[:], ALU.mult, reads=[("sq", s), "rw_r_k_b"], writes=[("sq", s)])
            pg.red("dve", bon[:], sq[:].rearrange("p (h k) -> p h k", h=8), ALU.add, reads=[("sq", s)], writes=[("bon", s)])
            pg.ld(dr["RR"][t0:t0 + 128, :], r_, reads=[("pm", s)], writes=[("RR", i)], q="act")
            pg.ld(dr["RKK"][t0:t0 + 128, :], kkn[:], reads=[("kkn", s)], writes=[("RKK", i)], q="act")
            pg.ld(dr["RLW"][t0:t0 + 128, :], lwt[:], reads=[("lwt", s)], writes=[("RLW", i)], q="act")
            pg.ld(dr["RB"][t0:t0 + 128, :], bt[:], reads=[("bt", s)], writes=[("RB", i)], q="act")
            pg.ld(dr["RKp"][t0:t0 + 128, :], kp[:], reads=[("kp", s)], writes=[("RKp", i)], q="act")
            pg.ld(dr["RV"][t0:t0 + 128, :], v_, reads=[("pm", s)], writes=[("RV", i)], q="act")
            pg.ld(dr["RG"][t0:t0 + 128, :], gt[:], reads=[("gt", s)], writes=[("RG", i)], q="act")
            pg.ld(dr["RBON"][t0:t0 + 128, :], bon[:], reads=[("bon", s)], writes=[("RBON", i)], q="act")
        E_(0)
        for i in range(NT):
            if i + 1 < NT:
                E_(i + 1)
            L_(i)
        pg.barrier()
        pg.emit()


def igather(pg, out_ap, table_ap, idx_ap, reads, writes):
    pg.dma("pool", lambda e: e.indirect_dma_start(out=out_ap, out_offset=None, in_=table_ap,
                                                   in_offset=bass.IndirectOffsetOnAxis(ap=idx_ap, axis=0)), reads, writes)


def stage2c(C):
    nc, pg, dr = C.nc, C.pg, C.dr
    with ExitStack() as st:
        sb, ps = _mk(C, st)
        lnw = sb("c_lnw", [128, 512], F32)
        lnb = sb("c_lnb", [128, 512], F32)
        eps = sb("c_eps", [128, 1], F32)
        y = [sb(f"c_y{i}", [128, 8, 64], F32) for i in range(2)]
        v = [sb(f"c_v{i}", [128, 8, 64], F32) for i in range(2)]
        g = [sb(f"c_g{i}", [128, 512], F32) for i in range(2)]
        bon = [sb(f"c_bon{i}", [128, 8], F32) for i in range(2)]
        stt_ = [sb(f"c_st{i}", [128, 32], F32) for i in range(2)]
        sq = sb("c_sq", [128, 8, 64], F32)
        pg.ld(lnw[:], dr["rw_ln_w_b"][:, :], writes=["lnw"])
        pg.ld(lnb[:], dr["rw_ln_b_b"][:, :], writes=["lnb"])
        pg.memset("dve", eps[:], 64e-5, writes=["eps"])
        f2 = lambda tl: tl[:].rearrange("p h k -> p (h k)")
        rowidx = sb("c_rowidx", [128, NT], I32)
        pg.ld(rowidx[:], dr["rowidx"][:, :], writes=["rowidx"])
        allk = lambda nm: [(nm, k) for k in range(NT)]
        for i in range(C.peer_tiles):
            s = i % 2
            t0 = i * 128
            yk, vk, gk, bk, sk = ("y", s), ("v", s), ("g", s), ("bon", s), ("st", s)
            ix = rowidx[:, i:i + 1]
            igather(pg, f2(y[s]), dr["YS"][:, :], ix, allk("YS") + ["rowidx"], [yk])
            igather(pg, f2(v[s]), dr["RV"][:, :], ix, allk("RV") + ["rowidx"], [vk])
            igather(pg, g[s][:], dr["RG"][:, :], ix, allk("RG") + ["rowidx"], [gk])
            igather(pg, bon[s][:], dr["RBON"][:, :], ix, allk("RBON") + ["rowidx"], [bk])
            S_ = stt_[s]
            bc = lambda ap: ap.unsqueeze(2).to_broadcast([128, 8, 64])
            pg.red("dve", S_[:, 0:8], y[s][:], ALU.add, reads=[yk], writes=[(sk, 0)])
            pg.ts("dve", S_[:, 8:16], S_[:, 0:8], -1.0 / 64, None, ALU.mult, reads=[(sk, 0)], writes=[(sk, 1)])
            pg.tt("dve", y[s][:], y[s][:], bc(S_[:, 8:16]), ALU.add, reads=[yk, (sk, 1)], writes=[yk])
            pg.tt("pool", sq[:], y[s][:], y[s][:], ALU.mult, reads=[yk], writes=["sq"])
            pg.red("dve", S_[:, 16:24], sq[:], ALU.add, reads=["sq"], writes=[(sk, 2)])
            pg.act(S_[:, 24:32], S_[:, 16:24], AF.Sqrt, reads=[(sk, 2), "eps"], writes=[(sk, 3)], scale=1.0 / 64, bias=eps[:, 0:1])
            pg.op("dve", lambda e, o=S_[:, 16:24], a=S_[:, 24:32]: e.reciprocal(o, a), reads=[(sk, 3)], writes=[(sk, 2)])
            pg.tt("dve", y[s][:], y[s][:], bc(S_[:, 16:24]), ALU.mult, reads=[yk, (sk, 2)], writes=[yk])
            pg.tt("dve", f2(y[s]), f2(y[s]), lnw[:], ALU.mult, reads=[yk, "lnw"], writes=[yk])
            pg.tt("pool", f2(y[s]), f2(y[s]), lnb[:], ALU.add, reads=[yk, "lnb"], writes=[yk])
            pg.tt("pool", v[s][:], v[s][:], bc(bon[s][:, 0:8]), ALU.mult, reads=[vk, bk], writes=[vk])
            pg.tt("dve", y[s][:], y[s][:], v[s][:], ALU.add, reads=[yk, vk], writes=[yk])
            pg.tt("dve", f2(y[s]), f2(y[s]), g[s][:], ALU.mult, reads=[yk, gk], writes=[yk])
            pg.ld(dr["YAL"][t0:t0 + 128, :], f2(y[s]), reads=[yk], writes=[("YAL", i)])
        pg.barrier()
        pg.emit()


NEG = -30000.0


def stage3(C):
    nc, pg, dr = C.nc, C.pg, C.dr
    with ExitStack() as st:
        sb, ps = _mk(C, st)
        qT = sb("n_qT", [128, 4, T], BF16)
        KsT = sb("n_KsT", [128, 2, T], BF16)
        KwT = sb("n_KwT", [128, 2, T], BF16)
        Vs = sb("n_Vs", [128, NT, 2, 65], BF16)
        Vw = sb("n_Vw", [128, NT, 2, 65], BF16)
        KcT = sb("n_KcT", [128, 2, 256], BF16)
        Vc = sb("n_Vc", [128, 2, 2, 129], BF16)
        GT = sb("n_GT", [128, NT, 24], F32)
        idf = sb("n_idf", [128, 128], F32)
        idb = sb("n_idb", [128, 128], BF16)
        eps = sb("n_eps", [128, 1], F32)
        pg.ld(idf[:], dr["ident"][:, :], writes=["idf"])
        pg.cp("dve", idb[:], idf[:], reads=["idf"], writes=["idb"])
        pg.memset("dve", eps[:], 1e-6, writes=["eps"])
        pg.memset("pool", Vs[:], 1.0, writes=["Vs"])
        pg.memset("pool", Vw[:], 1.0, writes=["Vw"])
        pg.memset("pool", Vc[:], 0.0, writes=["Vc"])
        with ExitStack() as sa_:
            sb, ps = _mk(C, sa_)
            kcT2 = sb("n_kcT2", [128, T], BF16)
            vcT2 = sb("n_vcT2", [128, T], BF16)
            w1 = [sb(f"n_w1{i}", [128, 32, 256], BF16) for i in range(2)]
            w1s = sb("n_w1s", [128, 16, 256], F32)
            w2s = sb("n_w2s", [128, 2, 2, 64], F32)
            w2 = sb("n_w2", [128, 2, 2, 64], BF16)
            posf = sb("n_posf", [128, 2, 32], F32)
            posb = sb("n_posb", [128, 2, 32], BF16)
            gains = sb("n_gains", [128, 768], F32)
            kcg = sb("n_kcg", [128, 64], F32)
            ovl = sb("n_ovl", [128, 2, 64], F32)
            R = [sb(f"n_R{i}", [128, 1304], F32) for i in range(2)]
            sq = sb("n_sq", [128, 1280], F32)
            tmp = sb("n_tmp", [128, 768], F32)
            stat = sb("n_stat", [128, 64], F32)
            Xb = sb("n_Xb", [128, 10, 128], BF16)
            biasS = sb("n_biasS", [128, 4], F32)
            xb_ = sb("n_xb", [128, 256], F32)
            x2_ = sb("n_x2", [128, 256], F32)
            hT = sb("n_hT", [128, 2, 256], BF16)
            kcn2 = sb("n_kcn2", [128, 128], BF16)
            st2 = sb("n_st2", [128, 8], F32)
            ksq = sb("n_ksq", [128, 64], F32)
            psX_ = [ps(f"n_psX{i}", [128, 1024], BF16) for i in range(3)]
            psX = [t_[:, 0:512].rearrange("p (a b) -> p a b", a=4) for t_ in psX_]
            psh = ps("n_psh", [128, 512], F32)
            psb = ps("n_psb", [128, 512], F32)
            pso = ps("n_pso", [128, 512], F32)
            psk = ps("n_psk", [128, 1024], BF16)

            pg.ld(gains[:], dr["nsa_gains_b"][:, :], writes=["gains"])
            pg.ts("dve", gains[:, 0:512], gains[:, 0:512], 0.125, None, ALU.mult, reads=["gains"], writes=["gains"])
            pg.ld(kcg[:], dr["nsa_kc_g_b"][:, :], writes=["kcg"])
            pg.ld(ovl[:], dr["ovl"][:, :, :], writes=["ovl"])
            pg.ld(posf[:], dr["posT"][:, :, :], writes=["posf"])
            pg.cp("dve", posb[:], posf[:], reads=["posf"], writes=["posb"])
            pg.ld(w2s[:], dr["cmp_w2"][:, :, :, :], writes=["w2s"])
            pg.cp("dve", w2[:], w2s[:], reads=["w2s"], writes=["w2"])
            for x in range(2):
                for hf in range(2):
                    pg.ld(w1s[:], dr["cmp_w1"][x, :, hf * 16:(hf + 1) * 16, :], writes=["w1s"])
                    pg.cp("pool", w1[x][:, hf * 16:(hf + 1) * 16, :], w1s[:], reads=["w1s"], writes=[("w1", x)])
            for i in range(NT):
                s = i % 2
                t0 = i * 128
                Rk = ("R", s)
                pg.ld(R[s][:], dr["P"][t0:t0 + 128, 1792:3096], reads=[("P", i)], writes=[Rk])
                Rs = R[s]
                pg.tt("pool", sq[:], Rs[:, 0:1280], Rs[:, 0:1280], ALU.mult, reads=[Rk], writes=["sq"])
                pg.red("dve", stat[:, 0:20], sq[:].rearrange("p (a k) -> p a k", k=64), ALU.add, reads=["sq"], writes=["stat0"])
                pg.act(stat[:, 20:40], stat[:, 0:20], AF.Sqrt, reads=["stat0", "eps"], writes=["stat1"], scale=1.0 / 64, bias=eps[:, 0:1])
                pg.op("dve", lambda e: e.reciprocal(stat[:, 40:60], stat[:, 20:40]), reads=["stat1"], writes=["stat2"])
                b3 = lambda ap, n: ap.unsqueeze(2).to_broadcast([128, n, 64])
                v3 = lambda ap: ap.rearrange("p (a k) -> p a k", k=64)
                pg.tt("dve", v3(tmp[:, 0:512]), v3(Rs[:, 0:512]), b3(stat[:, 40:48], 8), ALU.mult, reads=[Rk, "stat2"], writes=["tmp"])
                pg.tt("dve", v3(tmp[:, 512:640]), v3(Rs[:, 768:896]), b3(stat[:, 52:54], 2), ALU.mult, reads=[Rk, "stat2"], writes=["tmp"])
                pg.tt("dve", v3(tmp[:, 640:768]), v3(Rs[:, 1024:1152]), b3(stat[:, 56:58], 2), ALU.mult, reads=[Rk, "stat2"], writes=["tmp"])
                pg.tt("pool", tmp[:], tmp[:], gains[:], ALU.mult, reads=["tmp", "gains"], writes=["tmp"])
                pg.cp("pool", Xb[:, 0:4, :].rearrange("p a b -> p (a b)"), tmp[:, 0:512], reads=["tmp"], writes=["Xb"])
                for (blk, c0) in ((4, 512), (6, 640)):
                    src = tmp[:, c0:c0 + 128].rearrange("p (g k) -> p g k", g=2).unsqueeze(2).to_broadcast([128, 2, 2, 64])
                    dst = Xb[:, blk:blk + 2, :].rearrange("p g (d k) -> p g d k", d=2)
                    pg.cp("dve", dst, src, reads=["tmp"], writes=["Xb"])
                pg.cp("pool", Xb[:, 8, :], Rs[:, 512:640], reads=[Rk], writes=["Xb"])
                pg.cp("pool", Xb[:, 9, :], Rs[:, 640:768], reads=[Rk], writes=["Xb"])
                for blk in range(10):
                    pg.tr(psX[blk // 4][:, blk % 4, :], Xb[:, blk, :], idb[:], reads=["Xb", "idb"], writes=[("psX", blk // 4)])
                pg.cp("act", qT[:, :, t0:t0 + 128], psX[0], reads=[("psX", 0)], writes=["qT"])
                pg.cp("dve", KsT[:, :, t0:t0 + 128], psX[1][:, 0:2, :], reads=[("psX", 1)], writes=["KsT"])
                pg.cp("dve", KwT[:, :, t0:t0 + 128], psX[1][:, 2:4, :], reads=[("psX", 1)], writes=["KwT"])
                pg.cp("act", kcT2[:, t0:t0 + 128], psX[2][:, 0, :], reads=[("psX", 2)], writes=["kcT2"])
                pg.cp("act", vcT2[:, t0:t0 + 128], psX[2][:, 1, :], reads=[("psX", 2)], writes=["vcT2"])
                pg.cp("pool", Vs[:, i, :, 0:64], Rs[:, 896:1024].rearrange("p (g k) -> p g k", g=2), reads=[Rk, "Vs"], writes=["Vs"])
                pg.cp("pool", Vw[:, i, :, 0:64], Rs[:, 1152:1280].rearrange("p (g k) -> p g k", g=2), reads=[Rk, "Vw"], writes=["Vw"])
                pg.act(GT[:, i, :], Rs[:, 1280:1304], AF.Sigmoid, reads=[Rk], writes=["GT"])
            if getattr(C, "lvl", 9) < 2:
                pg.barrier()
                pg.emit()
                return
            pg.memset("dve", hT[:], 0.0, writes=["hT"])
            pg.memset("dve", kcn2[:], 0.0, writes=["kcn2"])
            for x in range(2):
                for hf in range(2):
                    for l in range(32):
                        pg.mm(psb[:, x * 2 + hf:x * 2 + hf + 1], w1[x][0:64, l, hf * 128:(hf + 1) * 128], posb[0:64, x, l:l + 1],
                              start=(l == 0), stop=(l == 31), reads=[("w1", x), "posb"], writes=["psb"])
            pg.cp("dve", biasS[:], psb[:, 0:4], reads=["psb"], writes=["biasS"])
            for x in range(2):
                srcT = kcT2 if x == 0 else vcT2
                skey = "kcT2" if x == 0 else "vcT2"
                for g in range(2):
                    for hf in range(2):
                        for l in range(32):
                            rhs = dap(srcT[:], g * 64 * T + l, [[T, 64], [16, 255]])
                            pg.mm(psh[:, 0:255], w1[x][g * 64:(g + 1) * 64, l, hf * 128:(hf + 1) * 128], rhs,
                                  start=(l == 0), stop=(l == 31), reads=[("w1", x), skey], writes=["psh"])
                        c = slice(0, 255)
                        pg.act(xb_[:, c], psh[:, c], AF.Identity, reads=["psh", "biasS"], writes=["xb"], bias=biasS[:, x * 2 + hf:x * 2 + hf + 1])
                        pg.tt("pool", x2_[:, c], xb_[:, c], xb_[:, c], ALU.mult, reads=["xb"], writes=["x2"])
                        pg.ts("dve", x2_[:, c], x2_[:, c], 0.044715, 1.0, ALU.mult, ALU.add, reads=["x2"], writes=["x2"])
                        pg.tt("dve", x2_[:, c], x2_[:, c], xb_[:, c], ALU.mult, reads=["x2", "xb"], writes=["x2"])
                        pg.act(x2_[:, c], x2_[:, c], AF.Tanh, reads=["x2"], writes=["x2"], scale=0.7978845608028654)
                        pg.stt("dve", x2_[:, c], x2_[:, c], 1.0, xb_[:, c], ALU.add, ALU.mult, reads=["x2", "xb"], writes=["x2"])
                        pg.ts("dve", hT[:, hf, c], x2_[:, c], 0.5, None, ALU.mult, reads=["x2"], writes=["hT"])
                    for m in range(2):
                        rows = 128 if m == 0 else 127
                        for hf in range(2):
                            pg.mm(pso[0:rows, 0:64], hT[:, hf, m * 128:m * 128 + rows], w2[:, x, hf, :], start=(hf == 0), stop=(hf == 1),
                                  reads=["hT", "w2"], writes=["pso"])
                        if x == 0:
                            pg.cp("act", ksq[0:rows, :], pso[0:rows, 0:64], reads=["pso"], writes=["ksq"])
                            pg.tt("pool", x2_[0:rows, 0:64], ksq[0:rows, :], ksq[0:rows, :], ALU.mult, reads=["ksq", "x2"], writes=["x2"])
                            pg.red("dve", st2[0:rows, 0:1], x2_[0:rows, 0:64], ALU.add, reads=["x2"], writes=["st2a"])
                            pg.act(st2[0:rows, 1:2], st2[0:rows, 0:1], AF.Sqrt, reads=["st2a", "eps"], writes=["st2b"], scale=1.0 / 64, bias=eps[0:rows, 0:1])
                            pg.op("dve", lambda e, rows=rows: e.reciprocal(st2[0:rows, 2:3], st2[0:rows, 1:2]), reads=["st2b"], writes=["st2c"])
                            pg.stt("dve", ksq[0:rows, :], ksq[0:rows, :], st2[0:rows, 2:3], kcg[0:rows, :], ALU.mult, ALU.mult,
                                   reads=["ksq", "st2c", "kcg"], writes=["ksq"])
                            src = ksq[0:rows, :].unsqueeze(1).to_broadcast([rows, 2, 64])
                            pg.cp("dve", kcn2[0:rows, :].rearrange("p (d k) -> p d k", d=2), src, reads=["ksq"], writes=["kcn2"])
                            pg.tr(psk[:, 0:128], kcn2[:, :], idb[:], reads=["kcn2", "idb"], writes=["psk"])
                            pg.cp("act", KcT[:, g, m * 128:(m + 1) * 128], psk[:, 0:128], reads=["psk"], writes=["KcT"])
                        else:
                            pg.cp("act", Vc[0:rows, m, g, 0:64], pso[0:rows, 0:64], reads=["pso", "Vc"], writes=["Vc"])
            for m in range(2):
                for g in range(2):
                    pg.memset("dve", Vc[:, m, g, 64:65], 1.0, writes=["Vc"])
                    pg.cp("dve", Vc[:, m, g, 65:129], ovl[:, m, :], reads=["ovl", "Vc"], writes=["Vc"])
            pg.barrier()
            pg.emit()
        if getattr(C, "lvl", 9) < 3:
            return
        stage3_attn(C, st, qT, KsT, KwT, Vs, Vw, KcT, Vc, GT, idf, idb)


def stage3_attn(C, st, qT, KsT, KwT, Vs, Vw, KcT, Vc, GT, idf, idb):
    nc, pg, dr = C.nc, C.pg, C.dr
    with ExitStack() as sb_:
        sb, ps = _mk(C, sb_)
        cmpb = sb("n_cmpb", [128, 2, T], BF16)
        Esel = sb("n_Esel", [128, 32, 128], BF16)
        causb = sb("n_causb", [128, 4, 512], BF16)
        winb = sb("n_winb", [128, 8, 512], BF16)
        selbT = sb("n_selbT", [128, 2, T], BF16)
        eT = [sb(f"n_eT{i}", [128, 512], BF16) for i in range(4)]
        eT2 = [sb(f"n_eT2{i}", [128, 512], BF16) for i in range(4)]
        Mt = [sb(f"n_Mt{i}", [128, 512], BF16) for i in range(2)]
        rm = [0]
        dq = []
        ocmp = sb("n_ocmp", [128, 4, 8, 64], F32)
        osel = sb("n_osel", [128, 4, 8, 64], F32)
        owin = sb("n_owin", [128, 4, 8, 64], F32)
        den = sb("n_den", [128, 16], F32)
        impw = sb("n_impw", [128, 2, 4, 64], F32)
        score = sb("n_score", [128, 2, 64], F32)
        VM = [sb(f"n_VM{i}", [128, 2, 64], F32) for i in range(2)]
        work = sb("n_work", [128, 2, 64], F32)
        m8 = sb("n_m8", [128, 2, 16], F32)
        thr = sb("n_thr", [128, 2], F32)
        msel = sb("n_msel", [128, 2, 64], F32)
        selb = sb("n_selb", [128, 2, 2, 64], BF16)
        osT = [sb(f"n_osT{i}", [65, 512], F32) for i in range(2)]
        dn2 = sb("n_dn2", [128, 8], F32)
        yb = sb("n_yb", [128, 8, 64], F32)
        yb2 = sb("n_yb2", [128, 8, 64], F32)
        psS = [ps(f"n_psS{i}", [128, 512], F32) for i in range(3)]
        psA = [ps(f"n_psA{i}", [128, 512], F32) for i in range(2)]
        psB = [ps(f"n_psB{i}", [128, 512], F32) for i in range(2)]
        psZ_ = ps("n_psZ", [128, 1024], BF16)
        psZ = psZ_[:, 0:256].rearrange("p (g q) -> p g q", g=2)

        pg.ld(cmpb[:], dr["c_cmpb"][:, :, :], writes=["cmpb"])
        pg.ld(Esel[:], dr["c_esel"][:, :, :], writes=["Esel"])
        pg.ld(causb[:], dr["c_causb"][:, :, :], writes=["causb"])
        pg.ld(winb[:], dr["c_winb"][:, :, :], writes=["winb"])
        winb4 = sb("n_winb4", [128, 8, 512], BF16)
        pg.ld(winb4[:], dr["c_winb4"][:, :, :], writes=["winb4"])
        rs = [0]
        re = [0]

        def nxt(lst, n):
            v = lst[0]
            lst[0] = (v + 1) % n
            return v

        def qk(h):
            return (h % 2) * 64, h // 2, h // 4

        for Q in range(4, 8):
            tq0 = Q * 512
            for ii in range(4):
                i = Q * 4 + ii
                t0 = i * 128
                s = i % 2
                pg.ld(VM[s][:], dr["c_vmfb"][i, :, :, :], writes=[("VM", s)])
                nm = 2 if i >= 16 else 1
                for h in range(8):
                    base, hp, g = qk(h)
                    h4 = h % 4
                    for m in range(nm):
                        r = nxt(rs, 3)
                        pS = psS[r]
                        pg.mm(pS[:, 0:128], KcT[base:base + 64, g, m * 128:(m + 1) * 128], qT[base:base + 64, hp, t0:t0 + 128],
                              start=True, stop=False, reads=["KcT", "qT"], writes=[("psS", r)])
                        pg.mm(pS[:, 0:128], idb[:, :], cmpb[:, m, t0:t0 + 128], start=False, stop=True,
                              reads=["idb", "cmpb"], writes=[("psS", r)])
                        k = nxt(re, 4)
                        pg.act(eT[k][:, 0:128], pS[:, 0:128], AF.Exp, reads=[("psS", r)], writes=[("eT", k)])
                        def pv(g=g, h4=h4, k=k, m=m, nm=nm):
                            pg.mm(psA[g][:, h4 * 65:h4 * 65 + 65], eT[k][:, 0:128], Vc[:, m, g, 0:65], start=(m == 0), stop=(m == nm - 1),
                                  reads=[("eT", k), "Vc"], writes=[("psA", g)])
                            pg.mm(psB[g][:, h4 * 64:h4 * 64 + 64], eT[k][:, 0:128], Vc[:, m, g, 65:129], start=(m == 0), stop=(m == nm - 1),
                                  reads=[("eT", k), "Vc"], writes=[("psB", g)])
                        dq.append(pv)
                        if len(dq) > 2:
                            dq.pop(0)()
                while dq:
                    dq.pop(0)()
                for g in range(2):
                    A3 = psA[g][:, 0:260].rearrange("p (h c) -> p h c", c=65)
                    B3 = psB[g][:, 0:256].rearrange("p (h c) -> p h c", c=64)
                    dsl = den[:, g * 4:(g + 1) * 4]
                    rsl = den[:, 8 + g * 4:8 + (g + 1) * 4]
                    pg.ts("dve", dsl, A3[:, :, 64], 1e-30, None, ALU.max, reads=[("psA", g)], writes=[("den", g)])
                    pg.op("dve", lambda e, o=rsl, a=dsl: e.reciprocal(o, a), reads=[("den", g)], writes=[("rden", g)])
                    rb = rsl.unsqueeze(2).to_broadcast([128, 4, 64])
                    pg.tt("dve", ocmp[:, ii, g * 4:(g + 1) * 4, :], A3[:, :, 0:64], rb, ALU.mult, reads=[("psA", g), ("rden", g)], writes=["ocmp"])
                    pg.tt("dve", impw[:, g, :, :], B3, rb, ALU.mult, reads=[("psB", g), ("rden", g)], writes=[("impw", g)])
                    pg.red("dve", score[:, g, :], impw[:, g, :, :].rearrange("p h j -> p j h"), ALU.add, reads=[("impw", g)], writes=[("score", g)])
                    vm = dr
                    pg.tt("dve", score[:, g, :], score[:, g, :], VM[s][:, 0, :], ALU.mult, reads=[("score", g), ("VM", s)], writes=[("score", g)])
                    pg.tt("dve", score[:, g, :], score[:, g, :], VM[s][:, 1, :], ALU.add, reads=[("score", g), ("VM", s)], writes=[("score", g)])
                    pg.op("dve", lambda e, g=g: e.max(m8[:, g, 0:8], score[:, g, :]), reads=[("score", g)], writes=[("m8a", g)])
                    pg.op("dve", lambda e, g=g: e.match_replace(work[:, g, :], m8[:, g, 0:8], score[:, g, :], -1e9),
                          reads=[("score", g), ("m8a", g)], writes=[("work", g)])
                    pg.op("dve", lambda e, g=g: e.max(m8[:, g, 8:16], work[:, g, :]), reads=[("work", g)], writes=[("m8b", g)])
                    pg.ts("dve", thr[:, g:g + 1], m8[:, g, 15:16], -0.5, None, ALU.max, reads=[("m8b", g)], writes=[("thr", g)])
                    pg.ts("dve", msel[:, g, :], score[:, g, :], thr[:, g:g + 1], None, ALU.is_ge, reads=[("score", g), ("thr", g)], writes=[("msel", g)])
                    pg.cp("dve", selb[:, g, :, :], msel[:, g, :].unsqueeze(1).to_broadcast([128, 2, 64]), reads=[("msel", g)], writes=[("selb", g)])
                    pg.tr(psZ[:, g, :], selb[:, g, :, :].rearrange("p d j -> p (d j)"), idb[:], reads=[("selb", g), "idb"], writes=["psZ"])
                pg.cp("act", selbT[:, :, t0:t0 + 128], psZ, reads=["psZ"], writes=["selbT"])
            for br in range(2):
                if getattr(C, "lvl", 9) < 4 + br:
                    continue
                dest = osel if br == 0 else owin
                dkey = "osel" if br == 0 else "owin"
                KT = KsT if br == 0 else KwT
                Vv = Vs if br == 0 else Vw
                kts = list(range(0, 4 * Q + 4)) if br == 0 else list(range(max(0, 4 * Q - 4), 4 * Q + 4))
                for g in range(2):
                    O = [psA[0], psA[1], psB[0], psB[1]]
                    okeys = [("psA", 0), ("psA", 1), ("psB", 0), ("psB", 1)]
                    for n_, kt in enumerate(kts):
                        if br == 0:
                            r = nxt(rs, 3)
                            pg.mm(psS[r][:, :], Esel[0:64, kt, :], selbT[0:64, g, tq0:tq0 + 512], reads=["Esel", "selbT"], writes=[("psS", r)])
                            mi = nxt(rm, 2)
                            if kt >= 4 * Q:
                                pg.tt("dve", Mt[mi][:], psS[r][:, :], causb[:, kt - 4 * Q, :], ALU.mult, reads=[("psS", r), "causb"], writes=[("Mt", mi)])
                            else:
                                pg.cp("dve", Mt[mi][:], psS[r][:, :], reads=[("psS", r)], writes=[("Mt", mi)])
                            mask, mkeys = Mt[mi][:], [("Mt", mi)]
                        else:
                            wsrc = winb4 if Q == 4 else winb
                            mask, mkeys = wsrc[:, kt - 4 * Q + 4, :], ["winb", "winb4"]
                        for h4 in range(4):
                            h = g * 4 + h4
                            base, hp, _g = qk(h)
                            r2 = nxt(rs, 3)
                            pg.mm(psS[r2][:, :], KT[base:base + 64, g, kt * 128:(kt + 1) * 128], qT[base:base + 64, hp, tq0:tq0 + 512],
                                  reads=["qT"], writes=[("psS", r2)])
                            k = nxt(re, 4)
                            pg.act(eT[k][:, :], psS[r2][:, :], AF.Exp, reads=[("psS", r2)], writes=[("eT", k)])
                            pg.tt("dve", eT2[k][:, :], eT[k][:, :], mask, ALU.mult, reads=[("eT", k)] + mkeys, writes=[("eT2", k)])
                            dq.append(lambda h4=h4, kt=kt, k=k, n_=n_, O=O, okeys=okeys, Vv=Vv, g=g, kts=kts: pg.mm(
                                O[h4][0:65, :], Vv[:, kt, g, :], eT2[k][:, :], start=(n_ == 0), stop=(n_ == len(kts) - 1),
                                reads=[("eT2", k)], writes=[okeys[h4]]))
                            if len(dq) > 2:
                                dq.pop(0)()
                    while dq:
                        dq.pop(0)()
                    for h4 in range(4):
                        h = g * 4 + h4
                        o = h4 % 2
                        pg.cp("act", osT[o][:, :], O[h4][0:65, :], reads=[okeys[h4]], writes=[("osT", o)])
                        r3 = nxt(rs, 3)
                        Tp = psS[r3]
                        for qq in range(4):
                            pg.tr(Tp[:, qq * 65:(qq + 1) * 65], osT[o][0:65, qq * 128:(qq + 1) * 128], idf[0:65, 0:65],
                                  reads=[("osT", o), "idf"], writes=[("psS", r3)])
                        T3 = Tp[:, 0:260].rearrange("p (q c) -> p q c", c=65)
                        pg.ts("dve", dn2[:, 0:4], T3[:, :, 64], 1e-30, None, ALU.max, reads=[("psS", r3)], writes=["dn2a"])
                        pg.op("dve", lambda e: e.reciprocal(dn2[:, 4:8], dn2[:, 0:4]), reads=["dn2a"], writes=["dn2b"])
                        pg.tt("dve", dest[:, :, h, :], T3[:, :, 0:64], dn2[:, 4:8].unsqueeze(2).to_broadcast([128, 4, 64]), ALU.mult,
                              reads=[("psS", r3), "dn2b"], writes=[dkey])
            for ii in range(4):
                i = Q * 4 + ii
                t0 = i * 128
                G3 = GT[:, i, :].rearrange("p (h c) -> p h c", c=3)
                gb = lambda c: G3[:, :, c].unsqueeze(2).to_broadcast([128, 8, 64])
                pg.tt("dve", yb[:], ocmp[:, ii, :, :], gb(0), ALU.mult, reads=["ocmp", "GT"], writes=["yb"])
                pg.tt("pool", yb2[:], osel[:, ii, :, :], gb(1), ALU.mult, reads=["osel", "GT"], writes=["yb2"])
                pg.tt("dve", yb[:], yb[:], yb2[:], ALU.add, reads=["yb", "yb2"], writes=["yb"])
                pg.tt("pool", yb2[:], owin[:, ii, :, :], gb(2), ALU.mult, reads=["owin", "GT", "yb2"], writes=["yb2"])
                pg.tt("dve", yb[:], yb[:], yb2[:], ALU.add, reads=["yb", "yb2"], writes=["yb"])
                pg.ld(dr["YB"][t0:t0 + 128, :], yb[:].rearrange("p h k -> p (h k)"), reads=["yb"], writes=[("YB", i)])
        pg.barrier()
        pg.emit()


_NSA_CONSTS = {}


def nsa_consts(hh=1):
    if hh in _NSA_CONSTS:
        return _NSA_CONSTS[hh]
    import ml_dtypes
    bf = ml_dtypes.bfloat16
    c = {}
    n = np.arange(256)
    t = np.arange(T)
    nlo = 128 if hh == 0 else 0
    cm = np.where((16 * n[:, None] + 31 <= t[None, :]) & (n[:, None] < 255) & (n[:, None] >= nlo), 0.0, NEG).astype(np.float32)
    c["c_cmpb"] = np.ascontiguousarray(cm.reshape(2, 128, T).transpose(1, 0, 2)).astype(bf)
    es = np.zeros((64, 32, 128), np.float32)
    for kt in range(32):
        for key in range(128):
            es[2 * kt + key // 64, kt, key] = 1.0
    c["c_esel"] = np.concatenate([es, es], 0).astype(bf)
    key = np.arange(128)
    q = np.arange(512)
    cb = np.zeros((128, 4, 512), np.float32)
    for d in range(4):
        cb[:, d, :] = np.where((d * 128 + key[:, None]) <= q[None, :], 1.0, 0.0)
    c["c_causb"] = cb.astype(bf)
    wb = np.zeros((128, 8, 512), np.float32)
    for r in range(8):
        ka = (r - 4) * 128 + key[:, None]
        wb[:, r, :] = np.where((ka <= q[None, :]) & (ka > q[None, :] - 512), 1.0, 0.0)
    c["c_winb"] = wb.astype(bf)
    wb4 = wb.copy()
    if hh == 0:
        wb4[:, 0:4, :] = 0.0
    c["c_winb4"] = wb4.astype(bf)
    cs = np.arange(256) * 16
    ss = np.arange(64) * 64
    ov = np.clip(np.minimum(cs[:, None] + 32, ss[None, :] + 64) - np.maximum(cs[:, None], ss[None, :]), 0, None) / 32.0
    ov[255, :] = 0.0
    c["ovl"] = np.ascontiguousarray(ov.reshape(2, 128, 64).transpose(1, 0, 2)).astype(np.float32)
    cur = t // 64
    j = np.arange(64)
    jlo = 32 if hh == 0 else 0
    valid = (j[None, :] <= cur[:, None]) & (j[None, :] >= jlo)
    forced = (j[None, :] == jlo) | (j[None, :] == cur[:, None]) | (j[None, :] == cur[:, None] - 1)
    vm = valid.astype(np.float32)
    fb = np.where(valid, 1000.0 * forced, -1.0).astype(np.float32)
    c["c_vmfb"] = np.ascontiguousarray(np.stack([vm, fb], 1).reshape(NT, 128, 2, 64))
    _NSA_CONSTS[hh] = c
    return c


def stage4(C):
    nc, pg, dr = C.nc, C.pg, C.dr
    with ExitStack() as st:
        sb, ps = _mk(C, st)
        wa = sb("m_wa", [128, 4, D], BF16)
        wb = sb("m_wb", [128, 4, D], BF16)
        wo = sb("m_wo", [128, 8, D], BF16)
        stg = sb("m_stg", [128, D], F32)
        idf = sb("m_idf", [128, 128], F32)
        idb = sb("m_idb", [128, 128], BF16)
        yab = [sb(f"m_yab{i}", [128, 1024], F32) for i in range(2)]
        yabb = sb("m_yabb", [128, 1024], BF16)
        yT = sb("m_yT", [128, 8, 128], BF16)
        gts = [sb(f"m_g{i}", [128, 2048], F32) for i in range(2)]
        xt = [sb(f"m_x{i}", [128, D], F32) for i in range(2)]
        mix = sb("m_mix", [128, D], F32)
        mix2 = sb("m_mix2", [128, D], F32)
        mixb = sb("m_mixb", [128, D], BF16)
        mT = sb("m_mT", [128, 8, 128], BF16)
        x1 = [sb(f"m_x1{i}", [128, D], F32) for i in range(2)]
        psT = ps("m_psT", [128, 1024], BF16)
        psm = [ps(f"m_psm{i}", [128, 512], F32) for i in range(4)]
        psT2 = ps("m_psT2", [128, 1024], BF16)
        pso = [ps(f"m_pso{i}", [128, 512], F32) for i in range(2)]

        pg.ld(idf[:], dr["ident"][:, :], writes=["idf"])
        pg.cp("dve", idb[:], idf[:], reads=["idf"], writes=["idb"])
        n = 0
        for (wt, nm, kcs) in ((wa, "w_branch_a", 4), (wb, "w_branch_b", 4), (wo, "w_out", 8)):
            for kc in range(kcs):
                pg.ld(stg[:], dr[nm][kc * 128:(kc + 1) * 128, :], writes=["stg"])
                pg.cp(("act", "dve", "pool")[n % 3], wt[:, kc, :], stg[:], reads=["stg"], writes=[nm])
                n += 1
        rowidx = sb("m_rowidx", [128, NT], I32)
        pg.ld(rowidx[:], dr["rowidx"][:, :], writes=["rowidx"])
        allk = lambda nm: [(nm, k) for k in range(NT)]
        for i in range(C.peer_tiles):
            s = i % 2
            t0 = i * 128
            ix = rowidx[:, i:i + 1]
            pg.ld(yab[s][:, 0:512], dr["YAL"][t0:t0 + 128, :], reads=[("YAL", i)], writes=[("yab", s)])
            igather(pg, yab[s][:, 512:1024], dr["YB"][:, :], ix, allk("YB") + ["rowidx"], [("yab2", s)])
            igather(pg, gts[s][:], dr["PG"][:, :], ix, allk("PG") + ["rowidx"], [("gts", s)])
            igather(pg, xt[s][:], dr["x"][:, :], ix, ["rowidx"], [("xt", s)])
            pg.cp("pool", yabb[:], yab[s][:], reads=[("yab", s), ("yab2", s)], writes=["yabb"])
            for j in range(8):
                pg.tr(psT[:, j * 128:(j + 1) * 128], yabb[:, j * 128:(j + 1) * 128], idb[:], reads=["yabb", "idb"], writes=["psT"])
            pg.cp("act", yT[:].rearrange("p a b -> p (a b)"), psT[:], reads=["psT"], writes=["yT"])
            for br in range(2):
                wt = wa if br == 0 else wb
                for nchunk in range(2):
                    pb = psm[br * 2 + nchunk]
                    for kc in range(4):
                        pg.mm(pb[:], yT[:, br * 4 + kc, :], wt[:, kc, nchunk * 512:(nchunk + 1) * 512], start=(kc == 0), stop=(kc == 3),
                              reads=["yT", "w_branch_a", "w_branch_b"], writes=[("psm", br * 2 + nchunk)])
            pg.act(gts[s][:], gts[s][:], AF.Sigmoid, reads=[("gts", s)], writes=[("gts", s)])
            for nchunk in range(2):
                c = slice(nchunk * 512, (nchunk + 1) * 512)
                pg.tt("dve", mix[:, c], psm[nchunk][:], gts[s][:, nchunk * 512:(nchunk + 1) * 512], ALU.mult,
                      reads=[("psm", nchunk), ("gts", s)], writes=[("mix", nchunk)])
                pg.tt("dve", mix2[:, c], psm[2 + nchunk][:], gts[s][:, 1024 + nchunk * 512:1024 + (nchunk + 1) * 512], ALU.mult,
                      reads=[("psm", 2 + nchunk), ("gts", s)], writes=[("mix2", nchunk)])
                pg.tt("pool", mixb[:, c], mix[:, c], mix2[:, c], ALU.add, reads=[("mix", nchunk), ("mix2", nchunk)], writes=[("mixb", nchunk)])
            for j in range(8):
                pg.tr(psT2[:, j * 128:(j + 1) * 128], mixb[:, j * 128:(j + 1) * 128], idb[:], reads=[("mixb", 0), ("mixb", 1), "idb"], writes=["psT2"])
            pg.cp("act", mT[:].rearrange("p a b -> p (a b)"), psT2[:], reads=["psT2"], writes=["mT"])
            for nchunk in range(2):
                for kc in range(8):
                    pg.mm(pso[nchunk][:], mT[:, kc, :], wo[:, kc, nchunk * 512:(nchunk + 1) * 512], start=(kc == 0), stop=(kc == 7),
                          reads=["mT", "w_out"], writes=[("pso", nchunk)])
                pg.tt("dve", x1[s][:, nchunk * 512:(nchunk + 1) * 512], pso[nchunk][:], xt[s][:, nchunk * 512:(nchunk + 1) * 512], ALU.add,
                      reads=[("pso", nchunk), ("xt", s)], writes=[("x1", s, nchunk)])
            pg.ld(dr["X1L"][t0:t0 + 128, :], x1[s][:], reads=[("x1", s, 0), ("x1", s, 1)], writes=[("X1L", i)])
        pg.barrier()
        pg.emit()


def table_conv_gen(C, sb):
    pg, dr = C.pg, C.dr
    NBUF = 4
    src = [sb(f"z_src{i}", [128, D], F32) for i in range(NBUF)]
    dst = [sb(f"z_dst{i}", [128, D], BF16) for i in range(NBUF)]
    n = 0
    for (tab, co) in (("peer_u", 0), ("peer_v", D)):
        for a in range(16384 // 128):
            b_ = n % NBUF
            pg.ld(src[b_][:], dr[tab][a * 128:(a + 1) * 128, :], writes=[("zsrc", b_)], q="sp")
            pg.cp("act", dst[b_][:], src[b_][:], reads=[("zsrc", b_)], writes=[("zdst", b_)])
            pg.ld(dr["UV"][a * 128:(a + 1) * 128, co:co + D], dst[b_][:], reads=[("zdst", b_)], writes=[("UV", co, a)], q="act")
            n += 1
            yield


def stage5(C):
    nc, pg, dr = C.nc, C.pg, C.dr
    NB = 12
    with ExitStack() as st:
        sb, ps = _mk(C, st)
        wq = sb("p_wq", [128, 8, 2048], F32)
        kT = sb("p_kT", [128, 2, 128], F32)
        kraw = sb("p_kraw", [128, 2, 128], F32)
        g2 = sb("p_g2", [128, D], F32)
        idf = sb("p_idf", [128, 128], F32)
        io16 = sb("p_io16", [128, 16], F32)
        eps = sb("p_eps", [128, 1], F32)
        x1 = [sb(f"p_x1{i}", [128, D], F32) for i in range(3)]
        h2 = [sb(f"p_h2{i}", [128, D], F32) for i in range(2)]
        junk = sb("p_junk", [128, D], BF16)

        ss = sb("p_ss", [128, 4], F32)
        h2T = sb("p_h2T", [128, 8, 128], F32)
        qT = sb("p_qT", [128, 16, 128], F32)
        sc = sb("p_sc", [128, 16, 128], F32)
        work = sb("p_work", [128, 256], F32)
        tv = sb("p_tv", [128, 16, 16], F32)
        tiu = sb("p_tiu", [128, 16, 16], U32)
        ti = sb("p_ti", [128, 16, 16], F32)
        cs = sb("p_cs", [128, 8, 256], F32)
        bs = sb("p_bs", [128, 8, 16], F32)
        posu = sb("p_posu", [128, 8, 16], U32)
        pa_u = sb("p_pau", [128, 8, 16], U32)
        pb_u = sb("p_pbu", [128, 8, 16], U32)
        pa = sb("p_pa", [128, 8, 16], F32)
        pb = sb("p_pb", [128, 8, 16], F32)
        oh = sb("p_oh", [128, 8, 16, 16], F32)
        ia = sb("p_ia", [128, 8, 16], F32)
        ib = sb("p_ib", [128, 8, 16], F32)
        eidf = sb("p_eidf", [128, 128], F32)
        eidi = [sb(f"p_eidi{i}", [128, 128], I32) for i in range(3)]
        gate = [sb(f"p_gate{i}", [128, 128], F32) for i in range(2)]
        zz = sb("p_zz", [128, 16], F32)
        actv = [sb(f"p_act{i}", [128, 128], F32) for i in range(2)]
        ga = [sb(f"p_ga{i}", [128, 128], F32) for i in range(2)]
        uv = [sb(f"p_uv{i}", [128, 2 * D], BF16) for i in range(NB)]
        h2b = [sb(f"p_h2b{i}", [128, D], BF16) for i in range(2)]
        idb = sb("p_idb", [128, 128], BF16)
        junk2 = sb("p_junk2", [128, D], F32)
        dg = [sb(f"p_dg{i}", [128, 128], BF16) for i in range(4)]
        yo = [sb(f"p_yo{i}", [128, D], F32) for i in range(1)]
        psT = ps("p_psT", [128, 8, 128], F32)
        psQ = [ps(f"p_psQ{i}", [128, 512], F32) for i in range(2)]
        psY = [ps(f"p_psY{i}", [128, 512], F32) for i in range(2)]

        pg.ld(idf[:], dr["ident"][:, :], writes=["idf"])
        pg.ld(g2[:], dr["norm2_g_b"][:, :], writes=["g2"])
        pg.cp("dve", idb[:], idf[:], reads=["idf"], writes=["idb"])
        pg.ld(io16[:], dr["iota16"][:, :], writes=["io16"])
        rowidx = sb("p_rowidx", [128, NT], I32)
        pg.ld(rowidx[:], dr["rowidx"][:, :], writes=["rowidx"])
        pg.memset("dve", eps[:], 1e-6, writes=["eps"])
        for kc in range(8):
            pg.ld(wq[:, kc, :], dr["peer_wq"][kc * 128:(kc + 1) * 128, :], writes=["wq"])
        pg.ld(kraw[:, 0, :], dr["peer_k1"][:, :], writes=["kraw"])
        pg.ld(kraw[:, 1, :], dr["peer_k2"][:, :], writes=["kraw"])
        for hf in range(2):
            pg.tr(psQ[0][:, hf * 128:(hf + 1) * 128], kraw[:, hf, :], idf[:], reads=["kraw", "idf"], writes=[("psQ", 0)])
        pg.cp("dve", kT[:].rearrange("p a b -> p (a b)"), psQ[0][:, 0:256], reads=[("psQ", 0)], writes=["kT"])
        ntiles = getattr(C, "peer_tiles", NT)

        def front(i):
            s = i % 2
            t0 = i * 128
            pg.ld(x1[i % 3][:, :], dr["X1L"][t0:t0 + 128, :], reads=[("X1L", i)], writes=[("x1", i % 3)])
            yield
            pg.tt("pool", junk2[:], x1[i % 3][:], x1[i % 3][:], ALU.mult, reads=[("x1", i % 3), "junk2"], writes=["junk2"])
            yield
            pg.red("dve", ss[:, 0:1], junk2[:], ALU.add, reads=["junk2"], writes=["ss0"])
            yield
            pg.act(ss[:, 1:2], ss[:, 0:1], AF.Sqrt, reads=["ss0", "eps"], writes=["ss1"], scale=1.0 / D, bias=eps[:, 0:1])
            yield
            pg.op("dve", lambda e: e.reciprocal(ss[:, 2:3], ss[:, 1:2]), reads=["ss1"], writes=["ss2"])
            yield
            pg.stt("dve", h2[s][:], x1[i % 3][:], ss[:, 2:3], g2[:], ALU.mult, ALU.mult, reads=[("x1", i % 3), "ss2", "g2"], writes=[("h2", s)])
            yield
            pg.cp("act", h2b[s][:], h2[s][:], reads=[("h2", s)], writes=[("h2b", s)])
            yield
            for j in range(8):
                pg.tr(psT[:, j, :], h2[s][:, j * 128:(j + 1) * 128], idf[:], reads=[("h2", s), "idf"], writes=["psT"])
                yield
            pg.cp("act", h2T[:], psT[:], reads=["psT"], writes=["h2T"])
            yield
            for cg in range(4):
                bk = psQ[cg % 2]
                for cc in range(4):
                    c = cg * 4 + cc
                    for kc in range(8):
                        pg.mm(bk[:, cc * 128:(cc + 1) * 128], wq[:, kc, c * 128:(c + 1) * 128], h2T[:, kc, :], start=(kc == 0), stop=(kc == 7),
                              reads=["wq", "h2T"], writes=[("psQ", cg % 2)])
                        yield
                pg.cp("act" if cg % 2 == 0 else "dve", qT[:, cg * 4:(cg + 1) * 4, :].rearrange("p a b -> p (a b)"), bk[:],
                      reads=[("psQ", cg % 2)], writes=[("qT", cg)])
                yield
            for cg in range(4):
                bk = psQ[cg % 2]
                for cc in range(4):
                    c = cg * 4 + cc
                    pg.mm(bk[:, cc * 128:(cc + 1) * 128], qT[:, c, :], kT[:, c % 2, :], reads=[("qT", cg), "kT"], writes=[("psQ", cg % 2)])
                    yield
                pg.cp("act" if cg % 2 == 0 else "dve", sc[:, cg * 4:(cg + 1) * 4, :].rearrange("p a b -> p (a b)"), bk[:],
                      reads=[("psQ", cg % 2)], writes=[("sc", cg)])
                yield
            for c in range(16):
                k_ = ("sc", c // 4)
                pg.op("dve", lambda e, c=c: e.max(tv[:, c, 0:8], sc[:, c, :]), reads=[k_], writes=[("tv", c)])
                yield
                pg.op("dve", lambda e, c=c: e.max_index(tiu[:, c, 0:8], tv[:, c, 0:8], sc[:, c, :]), reads=[k_, ("tv", c)], writes=[("tiu", c)])
                yield
                pg.op("dve", lambda e, c=c: e.match_replace(work[:, 0:128], tv[:, c, 0:8], sc[:, c, :], -1e30), reads=[k_, ("tv", c), "work"], writes=["work"])
                yield
                pg.op("dve", lambda e, c=c: e.max(tv[:, c, 8:16], work[:, 0:128]), reads=["work"], writes=[("tv2", c)])
                yield
                pg.op("dve", lambda e, c=c: e.max_index(tiu[:, c, 8:16], tv[:, c, 8:16], sc[:, c, :]), reads=[k_, ("tv2", c)], writes=[("tiu2", c)])
                yield
            allt = [("tv", c) for c in range(16)] + [("tv2", c) for c in range(16)]
            alli = [("tiu", c) for c in range(16)] + [("tiu2", c) for c in range(16)]
            pg.cp("dve", ti[:], tiu[:], reads=alli, writes=["ti"])
            yield
            tv4 = tv[:].rearrange("p (h f) a -> p h f a", f=2)
            ti4 = ti[:].rearrange("p (h f) a -> p h f a", f=2)
            cs4 = cs[:].rearrange("p h (a b) -> p h a b", a=16)
            A_ = lambda t4: t4[:, :, 0, :].unsqueeze(3).to_broadcast([128, 8, 16, 16])
            B_ = lambda t4: t4[:, :, 1, :].unsqueeze(2).to_broadcast([128, 8, 16, 16])
            pg.tt("dve", cs4, A_(tv4), B_(tv4), ALU.add, reads=allt, writes=["cs"])
            yield
            for h in range(8):
                pg.op("dve", lambda e, h=h: e.max(bs[:, h, 0:8], cs[:, h, :]), reads=["cs"], writes=[("bs", h)])
                yield
                pg.op("dve", lambda e, h=h: e.max_index(posu[:, h, 0:8], bs[:, h, 0:8], cs[:, h, :]), reads=["cs", ("bs", h)], writes=[("posu", h)])
                yield
                pg.op("dve", lambda e, h=h: e.match_replace(work[:, :], bs[:, h, 0:8], cs[:, h, :], -1e30), reads=["cs", ("bs", h), "work"], writes=["work"])
                yield
                pg.op("dve", lambda e, h=h: e.max(bs[:, h, 8:16], work[:, :]), reads=["work"], writes=[("bs2", h)])
                yield
                pg.op("dve", lambda e, h=h: e.max_index(posu[:, h, 8:16], bs[:, h, 8:16], cs[:, h, :]), reads=["cs", ("bs2", h)], writes=[("posu2", h)])
                yield
            allb = [("bs", h) for h in range(8)] + [("bs2", h) for h in range(8)]
            allp = [("posu", h) for h in range(8)] + [("posu2", h) for h in range(8)]
            G = gate[s][:].rearrange("p (h j) -> p h j", h=8)
            pg.tt("dve", G, bs[:], bs[:, :, 0:1].to_broadcast([128, 8, 16]), ALU.subtract, reads=allb, writes=[("gate", s)])
            yield
            pg.act(G, G, AF.Exp, reads=[("gate", s)], writes=[("gate", s)])
            yield
            pg.red("dve", zz[:, 0:8], G, ALU.add, reads=[("gate", s)], writes=["zz0"])
            yield
            pg.op("dve", lambda e: e.reciprocal(zz[:, 8:16], zz[:, 0:8]), reads=["zz0"], writes=["zz1"])
            yield
            pg.tt("dve", G, G, zz[:, 8:16].unsqueeze(2).to_broadcast([128, 8, 16]), ALU.mult, reads=[("gate", s), "zz1"], writes=[("gate", s)])
            yield
            pg.ts("dve", pa_u[:], posu[:], 4, None, ALU.logical_shift_right, reads=allp, writes=["pau"])
            yield
            pg.ts("dve", pb_u[:], posu[:], 15, None, ALU.bitwise_and, reads=allp, writes=["pbu"])
            yield
            pg.cp("dve", pa[:], pa_u[:], reads=["pau"], writes=["pa"])
            yield
            pg.cp("dve", pb[:], pb_u[:], reads=["pbu"], writes=["pb"])
            yield
            iob = io16[:, :].unsqueeze(1).unsqueeze(1).to_broadcast([128, 8, 16, 16])
            for (pp, key, half, dst, dk_) in ((pa, "pa", 0, ia, "ia"), (pb, "pb", 1, ib, "ib")):
                pg.tt("dve", oh[:], pp[:].unsqueeze(3).to_broadcast([128, 8, 16, 16]), iob, ALU.is_equal, reads=[key, "io16", "oh"], writes=["oh"])
                yield
                tsel = ti4[:, :, half, :].unsqueeze(2).to_broadcast([128, 8, 16, 16])
                pg.tt("dve", oh[:], oh[:], tsel, ALU.mult, reads=["oh", "ti"], writes=["oh"])
                yield
                pg.red("dve", dst[:], oh[:], ALU.add, reads=["oh"], writes=[dk_])
                yield
            pg.stt("dve", eidf[:].rearrange("p (h j) -> p h j", h=8), ia[:], 128.0, ib[:], ALU.mult, ALU.add, reads=["ia", "ib"], writes=["eidf"])
            yield
            pg.cp("dve", eidi[i % 3][:], eidf[:], reads=["eidf"], writes=[("eidi", i % 3)])
            yield

        GS = 2
        SK = 1

        def gstep(i, e_):
            s = i % 2
            b_ = e_ % NB
            pg.dma("pool", lambda e, e_=e_, b_=b_, i=i: e.indirect_dma_start(
                out=uv[b_][:, :], out_offset=None, in_=dr["UV"][:, :],
                in_offset=bass.IndirectOffsetOnAxis(ap=eidi[i % 3][:, e_:e_ + 1], axis=0)),
                reads=[("eidi", i % 3)], writes=[("uv", b_)])
            pg.op("dve", lambda e, e_=e_, b_=b_, s=s: e.scalar_tensor_tensor(junk[:], uv[b_][:, 0:D], 1.0, h2b[s][:], ALU.mult, ALU.mult,
                                                                              accum_out=actv[s][:, e_:e_ + 1]),
                  reads=[("uv", b_), ("h2b", s)], writes=[("act", s, e_)])

        def gelu_grp(i, k):
            s = i % 2
            sl = slice(k * GS, (k + 1) * GS)
            pg.act(ga[s][:, sl], actv[s][:, sl], AF.Gelu, reads=[("act", s, e_) for e_ in range(k * GS, (k + 1) * GS)], writes=[("ga", s, k)])

        def fin_grp(i, k):
            s = i % 2
            sl = slice(k * GS, (k + 1) * GS)
            pg.tt("dve", ga[s][:, sl], ga[s][:, sl], gate[s][:, sl], ALU.mult, reads=[("ga", s, k), ("gate", s)], writes=[("ga", s, k)])
            for e_ in range(k * GS, (k + 1) * GS):
                b_ = e_ % NB
                d_ = e_ % 4
                pg.act(dg[d_][:], idb[:], AF.Copy, reads=[("ga", s, k), "idb"], writes=[("dg", d_)], scale=ga[s][:, e_:e_ + 1])
                for n_ in range(2):
                    pg.mm(psY[n_][:], dg[d_][:], uv[b_][:, D + n_ * 512:D + (n_ + 1) * 512], start=(e_ == 0), stop=(e_ == 127),
                          reads=[("dg", d_), ("uv", b_)], writes=[("psY", n_)])

        def tail(i):
            s = i % 2
            t0 = i * 128
            for n_ in range(2):
                pg.tt("dve", yo[0][:, n_ * 512:(n_ + 1) * 512], psY[n_][:], x1[i % 3][:, n_ * 512:(n_ + 1) * 512], ALU.add,
                      reads=[("psY", n_), ("x1", i % 3)], writes=[("yo", 0, n_)])
            pg.ld(dr["out"][t0:t0 + 128, :], yo[0][:], reads=[("yo", 0, 0), ("yo", 0, 1)], writes=[("out", i)])

        def drain(g, n=None):
            k = 0
            while g is not None and (n is None or k < n):
                try:
                    next(g)
                except StopIteration:
                    return None
                k += 1
            return g

        drain(front(0))
        for i in range(ntiles):
            gen2 = front(i + 1) if i + 1 < ntiles else None
            for k in range(128 // GS):
                for e_ in range(k * GS, (k + 1) * GS):
                    gstep(i, e_)
                    gen2 = drain(gen2, 3)
                gelu_grp(i, k)
                if k >= SK:
                    fin_grp(i, k - SK)
            for k in range(128 // GS - SK, 128 // GS):
                fin_grp(i, k)
            drain(gen2)
            tail(i)
        pg.barrier()
        pg.emit()


_NC_CACHE = {}


def kernel(**inputs):
    inputs = {k: np.asarray(v) for k, v in inputs.items()}
    ntl = NT // 2
    if "nc" not in _NC_CACHE:
        _NC_CACHE["nc"] = build([stage1, stage2a, stage2x, stage2c, stage3, stage4, stage5], peer_tiles=ntl)
    nc = _NC_CACHE["nc"]
    base = {}
    in_maps = []
    for c in range(8):
        b, hh = c % 4, c // 4
        if hh not in base:
            base[hh] = host_inputs(inputs, b, hh, ntl)
            m = base[hh]
        else:
            m = dict(base[hh])
            m["x"] = core_x(inputs, b, hh)
        in_maps.append(m)
    res = run_bass_kernel_spmd(nc, in_maps, core_ids=list(range(8)))
    out = np.zeros((4, T, D), np.float32)
    for c in range(8):
        b, hh = c % 4, c // 4
        out[b, hh * ntl * 128:(hh + 1) * ntl * 128, :] = res.results[c]["out"]
    return out


def stage2x(C):
    nc, pg, dr = C.nc, C.pg, C.dr
    with ExitStack() as st:
        sb, ps = _mk(C, st)
        idf = sb("x_idf", [128, 128], F32)
        tri = sb("x_tri", [128, 128], F32)
        msk = sb("x_msk", [128, 3, 128], F32)
        ones = sb("x_ones", [128, 1], F32)
        inp = [[sb(f"x_in{s}_{j}", [128, 512], F32) for j in range(6)] for s in range(2)]
        Pt = sb("x_P", [128, 512], F32)
        iP = sb("x_iP", [128, 512], F32)
        Pp = sb("x_Pp", [128, 512], F32)
        tm = [[sb(f"x_tm{s}_{j}", [128, 512], F32) for j in range(4)] for s in range(2)]
        fm = [[sb(f"x_fm{s}_{j}", [64, 8, 128], F32) for j in range(4)] for s in range(2)]
        M = [[sb(f"x_M{s}_{j}", [128, 8, 128], (BF16 if j in (0, 4) else F32)) for j in range(5)] for s in range(2)]
        Xb = sb("x_Xb", [128, 8, 128], BF16)
        idb = sb("x_idb", [128, 128], BF16)
        X = [sb(f"x_X{s}", [128, 8, 128], F32) for s in range(2)]
        PC = [sb(f"x_PC{s}", [64, 8], F32) for s in range(2)]
        N2 = [sb(f"x_N2_{j}", [128, 8, 128], BF16) for j in range(2)]
        N2T = [sb(f"x_N2T_{j}", [128, 8, 128], BF16) for j in range(2)]
        Z = [sb(f"x_Z{j}", [64, 512], F32) for j in range(2)]
        rhs_sb = sb("x_rhs", [128, 512], F32)
        U_sb = sb("x_U", [128, 512], F32)
        Y_sb = [sb(f"x_Y{j}", [128, 512], F32) for j in range(2)]
        bank = [ps(f"x_bank{j}", [128, 512], F32) for j in range(8)]

        pg.ld(idf[:], dr["ident"][:, :], writes=["idf"])
        pg.ld(tri[:], dr["c_tri"][:, :], writes=["tri"])
        pg.ld(msk[:], dr["c_msk"][:, :, :], writes=["msk"])
        pg.memset("dve", ones[:], 1.0, writes=["ones"])
        pg.cp("dve", idb[:], idf[:], reads=["idf"], writes=["idb"])
        pg.memset("dve", Z[0][:], 0.0, writes=[("Z", 0)])
        names = ("RR", "RKK", "RLW", "RB", "RKp", "RV")
        bk = [0]

        def nb():
            v = bk[0]
            bk[0] = (v + 1) % 8
            return v

        def pre(c):
            s = c % 2
            t0 = c * 128
            I = inp[s]
            for j, nm in enumerate(names):
                pg.ld(I[j][:], dr[nm][t0:t0 + 128, :], reads=[(nm, c)], writes=[("in", s, j)], q=("pool" if j % 2 == 0 else "sp"))
            r_, kkn, lw, b_, kp, v_ = [t_[:] for t_ in I]
            bL = nb()
            pg.mm(bank[bL][:], tri[:], lw, reads=["tri", ("in", s, 2)], writes=[("bank", bL)])
            bC = nb()
            for h in range(8):
                pg.mm(bank[bC][0:64, h:h + 1], I[2][:, h * 64:(h + 1) * 64], ones[:, 0:1], reads=[("in", s, 2), "ones"], writes=[("bank", bC)])
            pg.act(PC[s][:], bank[bC][0:64, 0:8], AF.Exp, reads=[("bank", bC)], writes=[("PC", s)])
            pg.act(Pt[:], bank[bL][:], AF.Exp, reads=[("bank", bL)], writes=["P"])
            pg.act(iP[:], bank[bL][:], AF.Exp, reads=[("bank", bL)], writes=["iP"], scale=-1.0)
            pg.tt("dve", Pp[:], bank[bL][:], lw, ALU.subtract, reads=[("bank", bL), ("in", s, 2)], writes=["Pp"])
            pg.act(Pp[:], Pp[:], AF.Exp, reads=["Pp"], writes=["Pp"])
            TM = tm[s]
            pg.tt("pool", TM[0][:], r_, Pt[:], ALU.mult, reads=[("in", s, 0), "P"], writes=[("tm", s, 0)])
            pg.stt("dve", TM[1][:], kkn, -1.0, Pp[:], ALU.mult, ALU.mult, reads=[("in", s, 1), "Pp"], writes=[("tm", s, 1)])
            pg.tt("pool", TM[2][:], b_, iP[:], ALU.mult, reads=[("in", s, 3), "iP"], writes=[("tm", s, 2)])
            pg.tt("dve", TM[3][:], kp, iP[:], ALU.mult, reads=[("in", s, 4), "iP"], writes=[("tm", s, 3)])
            for j in range(4):
                if j == 0 and c < NT // 2:
                    continue
                for hg in range(2):
                    bT = nb()
                    for hh in range(4):
                        h = hg * 4 + hh
                        pg.tr(bank[bT][0:64, hh * 128:(hh + 1) * 128], TM[j][:, h * 64:(h + 1) * 64], idf[:], reads=[("tm", s, j), "idf"], writes=[("bank", bT)])
                    pg.cp("act" if (j + hg) % 2 == 0 else "dve", fm[s][j][:, hg * 4:(hg + 1) * 4, :].rearrange("p a b -> p (a b)"), bank[bT][0:64, :],
                          reads=[("bank", bT)], writes=[("fm", s, j, hg)])
            FR, FKK, FB, FK = fm[s]
            combos = ((0, FB, 2, FKK, 1, 0), (1, FK, 3, FKK, 1, 0), (2, FB, 2, FR, 0, 1), (3, FK, 3, FR, 0, 1), (4, FKK, 1, FB, 2, 2))
            for hg in range(2):
                for (mi, L_, lj, R_, rj, mk) in combos:
                    if mi in (2, 3) and c < NT // 2:
                        continue
                    bM = nb()
                    for hh in range(4):
                        h = hg * 4 + hh
                        pg.mm(bank[bM][:, hh * 128:(hh + 1) * 128], L_[:, h, :], R_[:, h, :], reads=[("fm", s, lj, hg), ("fm", s, rj, hg)], writes=[("bank", bM)])
                    pg.tt("dve", M[s][mi][:, hg * 4:(hg + 1) * 4, :], bank[bM][:].rearrange("p (a b) -> p a b", a=4),
                          msk[:, mk, :].unsqueeze(1).to_broadcast([128, 4, 128]), ALU.mult, reads=[("bank", bM), "msk"], writes=[("M", s, mi, hg)])
                pg.tt("pool", Xb[:, hg * 4:(hg + 1) * 4, :], idb[:, :].unsqueeze(1).to_broadcast([128, 4, 128]), M[s][0][:, hg * 4:(hg + 1) * 4, :], ALU.subtract,
                      reads=[("M", s, 0, hg), "idb"], writes=[("Xb", hg)])
            curN = [M[s][0], M[s][0]]
            curNT = [M[s][4], M[s][4]]
            kN = [("M", s, 0, 0), ("M", s, 0, 1)]
            kNT = [("M", s, 4, 0), ("M", s, 4, 1)]
            for j in range(6):
                dst = j % 2
                for hg in range(2):
                    b1, b2 = nb(), nb()
                    for hh in range(4):
                        h = hg * 4 + hh
                        pg.mm(bank[b1][:, hh * 128:(hh + 1) * 128], curNT[hg][:, h, :], curN[hg][:, h, :], reads=[kN[hg], kNT[hg]], writes=[("bank", b1)])
                    for hh in range(4):
                        h = hg * 4 + hh
                        pg.mm(bank[b2][:, hh * 128:(hh + 1) * 128], curN[hg][:, h, :], curNT[hg][:, h, :], reads=[kN[hg], kNT[hg]], writes=[("bank", b2)])
                    pg.cp("act", N2[dst][:, hg * 4:(hg + 1) * 4, :].rearrange("p a b -> p (a b)"), bank[b1][:], reads=[("bank", b1)], writes=[("N2", dst, hg)])
                    pg.cp("dve", N2T[dst][:, hg * 4:(hg + 1) * 4, :].rearrange("p a b -> p (a b)"), bank[b2][:], reads=[("bank", b2)], writes=[("N2T", dst, hg)])
                for hg in range(2):
                    curN[hg], curNT[hg] = N2[dst], N2T[dst]
                    kN[hg], kNT[hg] = ("N2", dst, hg), ("N2T", dst, hg)
                for hg in range(2):
                    b3 = nb()
                    for hh in range(4):
                        h = hg * 4 + hh
                        pg.mm(bank[b3][:, hh * 128:(hh + 1) * 128], curNT[hg][:, h, :], Xb[:, h, :], reads=[kNT[hg], ("Xb", hg)], writes=[("bank", b3)])
                    xo = (X[s] if j == 5 else Xb)
                    pg.tt("dve", xo[:, hg * 4:(hg + 1) * 4, :].rearrange("p a b -> p (a b)"), Xb[:, hg * 4:(hg + 1) * 4, :].rearrange("p a b -> p (a b)"), bank[b3][:], ALU.add,
                          reads=[("bank", b3), ("Xb", hg)], writes=[("X", s, hg)] if j == 5 else [("Xb", hg)])

        def seq(c):
            s = c % 2
            t0 = c * 128
            zc, zn = Z[c % 2], Z[(c + 1) % 2]
            kz, kzn = ("Z", c % 2), ("Z", (c + 1) % 2)
            FR, FKK, FB, FK = fm[s]
            V = inp[s][5]
            hsl = lambda h: slice(h * 64, (h + 1) * 64)
            Mk = lambda mi: [("M", s, mi, 0), ("M", s, mi, 1)]
            fk = lambda j: [("fm", s, j, 0), ("fm", s, j, 1)]
            Xk = [("X", s, 0), ("X", s, 1)]
            bG = nb()
            for h in range(8):
                pg.mm(bank[bG][:, hsl(h)], M[s][1][:, h, :], V[:, hsl(h)], start=True, stop=False, reads=Mk(1) + [("in", s, 5)], writes=[("bank", bG)])
                pg.mm(bank[bG][:, hsl(h)], FKK[:, h, :], zc[:, hsl(h)], start=False, stop=True, reads=fk(1) + [kz], writes=[("bank", bG)])
            pg.ts("dve", rhs_sb[:], bank[bG][:], -1.0, None, ALU.mult, reads=[("bank", bG)], writes=["rhs"])
            bU = nb()
            for h in range(8):
                pg.mm(bank[bU][:, hsl(h)], X[s][:, h, :], rhs_sb[:, hsl(h)], reads=Xk + ["rhs"], writes=[("bank", bU)])
            pg.cp("act", U_sb[:], bank[bU][:], reads=[("bank", bU)], writes=["U"])
            bZ = nb()
            for h in range(8):
                pg.mm(bank[bZ][0:64, hsl(h)], tm[s][3][:, hsl(h)], V[:, hsl(h)], start=True, stop=False, reads=[("tm", s, 3), ("in", s, 5)], writes=[("bank", bZ)])
                pg.mm(bank[bZ][0:64, hsl(h)], idf[0:64, 0:64], zc[:, hsl(h)], start=False, stop=False, reads=["idf", kz], writes=[("bank", bZ)])
                pg.mm(bank[bZ][0:64, hsl(h)], tm[s][2][:, hsl(h)], U_sb[:, hsl(h)], start=False, stop=True, reads=[("tm", s, 2), "U"], writes=[("bank", bZ)])
            pg.tt("dve", zn[:].rearrange("p (h v) -> p h v", h=8), bank[bZ][0:64, :].rearrange("p (h v) -> p h v", h=8),
                  PC[s][:, :].unsqueeze(2).to_broadcast([64, 8, 64]), ALU.mult, reads=[("bank", bZ), ("PC", s)], writes=[kzn])
            if c < NT // 2:
                return
            bY = nb()
            for h in range(8):
                pg.mm(bank[bY][:, hsl(h)], M[s][3][:, h, :], V[:, hsl(h)], start=True, stop=False, reads=Mk(3) + [("in", s, 5)], writes=[("bank", bY)])
                pg.mm(bank[bY][:, hsl(h)], FR[:, h, :], zc[:, hsl(h)], start=False, stop=False, reads=fk(0) + [kz], writes=[("bank", bY)])
                pg.mm(bank[bY][:, hsl(h)], M[s][2][:, h, :], U_sb[:, hsl(h)], start=False, stop=True, reads=Mk(2) + ["U"], writes=[("bank", bY)])
            pg.cp("act", Y_sb[s][:], bank[bY][:], reads=[("bank", bY)], writes=[("Y", s)])
            pg.ld(dr["YS"][t0:t0 + 128, :], Y_sb[s][:], reads=[("Y", s)], writes=[("YS", c)])

        tcg = table_conv_gen(C, sb)
        pre(0)
        for c in range(NT):
            if c + 1 < NT:
                pre(c + 1)
            for _ in range(8):
                next(tcg, None)
            seq(c)
        for _ in tcg:
            pass
        pg.barrier()
        pg.emit()
```

```python
import numpy as np
import concourse.bass as bass
import concourse.mybir as mybir

F32 = mybir.dt.float32
BF16 = mybir.dt.bfloat16
I32 = mybir.dt.int32
U32 = mybir.dt.uint32
ALU = mybir.AluOpType
AF = mybir.ActivationFunctionType
AX = mybir.AxisListType

EPOCH = 20000
ENGS = ("pe", "act", "dve", "pool", "sp")
NDMASEM = 16


class Prog:
    def __init__(self, nc, stack):
        self.nc = nc
        self.stack = stack
        self.ops = {e: [] for e in ENGS}
        self.cnt = {e: 0 for e in ENGS}
        self.esems = {e: [] for e in ENGS}
        self.waited = {e: {} for e in ENGS}
        self.lastw = {}
        self.readers = {}
        self.dsems = {}
        self.dcount = {}
        self.dtarget = {}
        self.semobjs = {}
        self.alltokens = {}
        for q in ("sp", "act", "pool"):
            self.dsems[q] = [self._newsem(f"d_{q}_{i}") for i in range(NDMASEM)]
            self.dcount[q] = 0
            self.dtarget[q] = [0] * NDMASEM

    def _newsem(self, name):
        s = self.stack.enter_context(self.nc.semaphore(name))
        self.semobjs[name] = s
        return name

    def _esem(self, e, idx):
        ep = idx // EPOCH
        while len(self.esems[e]) <= ep:
            self.esems[e].append(self._newsem(f"e_{e}_{len(self.esems[e])}"))
        return self.esems[e][ep], (idx % EPOCH) + 1

    def _deps(self, reads, writes):
        toks = []
        for k in reads:
            t = self.lastw.get(k)
            if t is not None:
                toks.append(t)
        for k in writes:
            t = self.lastw.get(k)
            if t is not None:
                toks.append(t)
            toks.extend(self.readers.get(k, ()))
        return toks

    def _commit(self, tok, reads, writes):
        for k in reads:
            self.readers.setdefault(k, []).append(tok)
        for k in writes:
            self.lastw[k] = tok
            self.readers[k] = []
        self.alltokens[tok[0]] = max(self.alltokens.get(tok[0], 0), tok[1])

    def _waits(self, e, toks):
        need = {}
        for (s, v) in toks:
            if v > need.get(s, 0):
                need[s] = v
        out = []
        w = self.waited[e]
        for s, v in need.items():
            if w.get(s, 0) < v:
                w[s] = v
                out.append((s, v))
        return out

    def op(self, e, fn, reads=(), writes=()):
        toks = self._deps(reads, writes)
        if e == "pe":
            toks = [t for t in toks if not t[0].startswith("e_pe_")]
        waits = self._waits(e, toks)
        idx = self.cnt[e]
        self.cnt[e] += 1
        tok = self._esem(e, idx)
        self.ops[e].append((waits, fn, (tok[0], 1)))
        self._commit(tok, reads, writes)

    def dma(self, q, fn, reads=(), writes=()):
        toks = self._deps(reads, writes)
        n = self.dcount[q]
        self.dcount[q] += 1
        slot = n % NDMASEM
        sname = self.dsems[q][slot]
        prev = self.dtarget[q][slot]
        if prev > 0:
            toks.append((sname, prev))
        tgt = prev + 16
        self.dtarget[q][slot] = tgt
        waits = self._waits(q, toks)
        tok = (sname, tgt)
        self.ops[q].append((waits, fn, (sname, 16)))
        self._commit(tok, reads, writes)

    def mm(self, out, lhsT, rhs, start=True, stop=True, reads=(), writes=()):
        self.op("pe", lambda e: e.matmul(out, lhsT, rhs, start=start, stop=stop), reads, writes)

    def tr(self, out, in_, ident, reads=(), writes=()):
        self.op("pe", lambda e: e.transpose(out, in_, ident), reads, writes)

    def act(self, out, in_, func, reads=(), writes=(), bias=None, scale=None, eng="act"):
        kw = {}
        if bias is not None:
            kw["bias"] = bias
        if scale is not None:
            kw["scale"] = scale
        self.op(eng, lambda e: e.activation(out, in_, func, **kw), reads, writes)

    def tt(self, eng, out, in0, in1, op, reads=(), writes=()):
        self.op(eng, lambda e: e.tensor_tensor(out, in0, in1, op), reads, writes)

    def ts(self, eng, out, in0, s1, s2, op0, op1=None, reads=(), writes=()):
        if op1 is None:
            self.op(eng, lambda e: e.tensor_scalar(out, in0, s1, s2, op0), reads, writes)
        else:
            self.op(eng, lambda e: e.tensor_scalar(out, in0, s1, s2, op0, op1), reads, writes)

    def stt(self, eng, out, in0, scalar, in1, op0, op1, reads=(), writes=()):
        self.op(eng, lambda e: e.scalar_tensor_tensor(out, in0, scalar, in1, op0, op1), reads, writes)

    def cp(self, eng, out, in_, reads=(), writes=()):
        if eng == "act":
            self.op(eng, lambda e: e.copy(out, in_), reads, writes)
        else:
            self.op(eng, lambda e: e.tensor_copy(out, in_), reads, writes)

    def red(self, eng, out, in_, op, reads=(), writes=(), axis=None):
        ax = AX.X if axis is None else axis
        self.op(eng, lambda e: e.tensor_reduce(out, in_, ax, op), reads, writes)

    def memset(self, eng, ap, val, writes=()):
        self.op(eng, lambda e: e.memset(ap, val), (), writes)

    def ld(self, out, in_, reads=(), writes=(), q="sp"):
        self.dma(q, lambda e: e.dma_start(out, in_), reads, writes)

    def barrier(self):
        toks = list(self.alltokens.items())
        for e in ENGS:
            waits = self._waits(e, toks)
            if waits:
                self.ops[e].append((waits, None, None))
        self.lastw = {}
        self.readers = {}

    def emit(self):
        nc = self.nc
        so = self.semobjs
        with nc.Block() as block:
            def mk(e):
                def body(eng):
                    for waits, fn, inc in self.ops[e]:
                        for (s, v) in waits:
                            eng.wait_ge(so[s], v)
                        if fn is not None:
                            ins = fn(eng)
                            ins.then_inc(so[inc[0]], inc[1])
                return body
            block.tensor(mk("pe"))
            block.scalar(mk("act"))
            block.vector(mk("dve"))
            block.gpsimd(mk("pool"))
            block.sync(mk("sp"))
        self.ops = {e: [] for e in ENGS}
from contextlib import ExitStack
from concourse.bass_utils import run_bass_kernel_spmd

T = 4096
D = 1024
NT = T // 128
INW = 5144
RWC = 1792
O_RW = 0
O_Q = 1792
O_KC = 2304
O_VC = 2432
O_KS = 2560
O_VS = 2688
O_KW = 2816
O_VW = 2944
O_BG = 3072
O_GA = 3096
O_GB = 4120


class Ctx:
    pass


def _mk(C, st):
    nc = C.nc
    sb = lambda name, shape, dt: st.enter_context(nc.sbuf_tensor(name, shape, dt))
    ps = lambda name, shape, dt: st.enter_context(nc.psum_tensor(name, shape, dt))
    return sb, ps


def stage1(C):
    nc, pg, dr = C.nc, C.pg, C.dr
    with ExitStack() as st:
        sb, ps = _mk(C, st)
        win = sb("s1_win", [128, 8, INW], BF16)
        pj = [sb(f"s1_pj{i}", [128, INW], F32) for i in range(2)]
        xt = [sb(f"s1_xt{i}", [128, D], F32) for i in range(2)]
        junk = sb("s1_junk", [128, D], F32)
        hb = [sb(f"s1_h{i}", [128, D], BF16) for i in range(2)]
        hT = [sb(f"s1_hT{i}", [128, 8, 128], BF16) for i in range(2)]
        gt = sb("s1_g", [128, D], F32)
        idf = sb("s1_idf", [128, 128], F32)
        idb = sb("s1_idb", [128, 128], BF16)
        ss = [sb(f"s1_ss{i}", [128, 4], F32) for i in range(2)]
        psT = [ps(f"s1_psT{i}", [128, 8, 128], BF16) for i in range(2)]
        psm = [ps(f"s1_psm{i}", [128, 512], F32) for i in range(4)]

        pg.ld(gt[:], dr["norm1_g_b"][:, :], writes=["gt"])
        pg.ld(idf[:], dr["ident"][:, :], writes=["idf"])
        pg.cp("dve", idb[:], idf[:], reads=["idf"], writes=["idb"])
        engs = ["act", "dve", "pool"]
        for kc in range(8):
            b = pj[kc % 2]
            pg.ld(b[:], dr["w_in"][kc * 128:(kc + 1) * 128, :], writes=[("pjall", kc % 2)])
            pg.cp(engs[kc % 3], win[:, kc, :], b[:], reads=[("pjall", kc % 2)], writes=[("win", kc)])
        winkeys = [("win", kc) for kc in range(8)]
        chunks = []
        c0 = 0
        while c0 < INW:
            w = min(512, INW - c0)
            chunks.append((c0, w))
            c0 += w
        def A1(i):
            s = i % 2
            pg.ld(xt[s][:], dr["x"][i * 128:(i + 1) * 128, :], writes=[("xt", s)])
            pg.tt("dve", junk[:], xt[s][:], xt[s][:], ALU.mult, reads=[("xt", s)], writes=["junk"])
            pg.red("dve", ss[s][:, 0:1], junk[:], ALU.add, reads=["junk"], writes=[("ss", s)])
            pg.act(ss[s][:, 1:2], ss[s][:, 0:1], AF.Sqrt, reads=[("ss", s)], writes=[("ss1", s)],
                   scale=1.0 / D, bias=C.eps6[:, 0:1])
            pg.op("dve", lambda e, o=ss[s][:, 2:3], a=ss[s][:, 1:2]: e.reciprocal(o, a),
                  reads=[("ss1", s)], writes=[("ss2", s)])
            pg.stt("dve", hb[s][:], xt[s][:], ss[s][:, 2:3], gt[:], ALU.mult, ALU.mult,
                   reads=[("xt", s), ("ss2", s), "gt"], writes=[("hb", s)])

        def A2(i):
            s = i % 2
            for j in range(8):
                pg.tr(psT[s][:, j, :], hb[s][:, j * 128:(j + 1) * 128], idb[:],
                      reads=[("hb", s), "idb"], writes=[("psT", s)])
            pg.cp("act", hT[s][:], psT[s][:], reads=[("psT", s)], writes=[("hT", s)])

        chunks_lo = [(512, 512), (1024, 512), (1536, 128), (2304, 512), (2816, 256)]

        def B(i, lo, hi):
            s = i % 2
            if i < NT // 2 - 1:
                if lo != 0:
                    return
                for ci, (c0, w) in enumerate(chunks_lo):
                    pb = psm[ci % 4]
                    for kc in range(8):
                        pg.mm(pb[:, :w], hT[s][:, kc, :], win[:, kc, c0:c0 + w], start=(kc == 0), stop=(kc == 7),
                              reads=[("hT", s), ("win", kc)], writes=[("psm", ci % 4)])
                    pg.cp("act" if ci % 2 == 0 else "dve", pj[s][:, c0:c0 + w], pb[:, :w],
                          reads=[("psm", ci % 4)], writes=[("pj", s, k_) for k_ in range(len(chunks))] + ([("pjall", s)] if i < 8 else []))
                return
            for ci in range(lo, hi):
                c0, w = chunks[ci]
                pb = psm[ci % 4]
                for kc in range(8):
                    pg.mm(pb[:, :w], hT[s][:, kc, :], win[:, kc, c0:c0 + w], start=(kc == 0), stop=(kc == 7),
                          reads=[("hT", s), ("win", kc)], writes=[("psm", ci % 4)])
                pg.cp("act" if ci % 2 == 0 else "dve", pj[s][:, c0:c0 + w], pb[:, :w],
                      reads=[("psm", ci % 4)], writes=[("pj", s, ci), ("pjall", s)] if i < 8 else [("pj", s, ci)])

        def S(i):
            s = i % 2
            pg.ld(dr["P"][i * 128:(i + 1) * 128, 0:3096], pj[s][:, 0:3096],
                  reads=[("pj", s, ci) for ci in range(len(chunks))], writes=[("P", i)])
            pg.ld(dr["PG"][i * 128:(i + 1) * 128, :], pj[s][:, 3096:5144],
                  reads=[("pj", s, ci) for ci in range(len(chunks))], writes=[("PG", i)], q="act")

        A1(0)
        A2(0)
        A1(1)
        for i in range(NT):
            B(i, 0, 6)
            if i + 1 < NT:
                A2(i + 1)
            if i + 2 < NT:
                A1(i + 2)
            B(i, 6, len(chunks))
            S(i)
        pg.barrier()
        pg.emit()


def build(stages, dbg_out=(), dbg_in=(), lvl=9, sub=9, peer_tiles=NT):
    nc = bass.Bass("TRN2", target_bir_lowering=False)
    C = Ctx()
    C.peer_tiles = peer_tiles
    C.lvl = lvl
    C.sub = sub
    C.nc = nc
    dr = {}
    C.dr = dr

    def din(name, shape, dt=F32):
        dr[name] = nc.dram_tensor(name, list(shape), dt, kind="ExternalInput").ap()

    def dscr(name, shape, dt=F32):
        kind = "ExternalOutput" if name in dbg_out else ("ExternalInput" if name in dbg_in else "Internal")
        dr[name] = nc.dram_tensor(name, list(shape), dt, kind=kind).ap()

    din("x", [T, D])
    din("norm1_g_b", [128, D])
    din("ident", [128, 128])
    din("w_in", [D, INW])
    dscr("P", [T, INW])
    dscr("PG", [T, 2048])
    for nm in ("rw_mu_b",):
        din(nm, [128, RWC])
    for nm in ("rw_w0_b", "rw_a0_b", "rw_k_k_b", "rw_k_a_b", "rw_r_k_b", "rw_ln_w_b", "rw_ln_b_b", "rw_g_up"):
        din(nm, [128, 512])
    din("rw_w_up", [64, 512])
    din("rw_a_up", [64, 512])
    for nm in ("RB", "RKp", "RV", "RG", "YA", "RR", "RKK", "RLW", "YS"):
        dscr(nm, [T, 512])
    dscr("RBON", [T, 8])
    din("blkmask", [8, 512])
    din("c_tri", [128, 128])
    din("c_msk", [128, 3, 128])
    din("nsa_gains_b", [128, 768])
    din("nsa_kc_g_b", [128, 64])
    din("ovl", [128, 2, 64])
    din("posT", [128, 2, 32])
    din("cmp_w2", [128, 2, 2, 64])
    din("cmp_w1", [2, 128, 32, 256])
    din("c_cmpb", [128, 2, T], BF16)
    din("c_esel", [128, 32, 128], BF16)
    din("c_causb", [128, 4, 512], BF16)
    din("c_winb", [128, 8, 512], BF16)
    din("c_winb4", [128, 8, 512], BF16)
    din("c_vmfb", [NT, 128, 2, 64])
    dscr("YB", [T, 512])
    din("w_branch_a", [512, D])
    din("w_branch_b", [512, D])
    din("w_out", [D, D])
    dscr("X1L", [peer_tiles * 128, D])
    dscr("YAL", [peer_tiles * 128, 512])
    din("norm2_g_b", [128, D])
    din("iota16", [128, 16])
    din("rowidx", [128, NT], I32)
    din("peer_wq", [D, 2048])
    din("peer_k1", [128, 128])
    din("peer_k2", [128, 128])
    din("peer_u", [16384, D])
    din("peer_v", [16384, D])
    dscr("UV", [16384, 2 * D], BF16)
    dr["out"] = nc.dram_tensor("out", [peer_tiles * 128, D], F32, kind="ExternalOutput").ap()
    with ExitStack() as top:
        pg = Prog(nc, top)
        C.pg = pg
        C.eps6 = top.enter_context(nc.sbuf_tensor("c_eps6", [128, 1], F32))
        pg.memset("dve", C.eps6[:], 1e-6, writes=["eps6"])
        pg.barrier()
        for s in stages:
            s(C)
        pg.barrier()
        pg.emit()
    return nc


def core_x(inputs, b, hh):
    xb = np.asarray(inputs["x"][b])
    if hh == 0:
        return np.ascontiguousarray(np.concatenate([np.zeros((T // 2, D), np.float32), xb[0:T // 2]], 0))
    return np.ascontiguousarray(xb)


def host_inputs(inputs, b, hh=1, ntl=NT):
    g = lambda k: np.ascontiguousarray(inputs[k][0])
    m = {}
    m["x"] = core_x(inputs, b, hh)
    m["norm1_g_b"] = np.ascontiguousarray(np.broadcast_to(g("norm1_g")[None, :], (128, D)))
    m["ident"] = np.eye(128, dtype=np.float32)
    m["w_in"] = g("w_in")
    bc = lambda a: np.ascontiguousarray(np.broadcast_to(np.asarray(a).reshape(1, -1), (128, a.size)))
    m["rw_mu_b"] = bc(g("rw_mu"))
    for nm in ("rw_w0", "rw_a0", "rw_k_k", "rw_k_a", "rw_r_k", "rw_ln_w", "rw_ln_b"):
        m[nm + "_b"] = bc(g(nm))
    for nm in ("rw_g_up", "rw_w_up", "rw_a_up"):
        m[nm] = g(nm)
    bmk = np.zeros((8, 512), np.float32)
    for h in range(8):
        bmk[h, h * 64:(h + 1) * 64] = 1.0
    m["blkmask"] = bmk
    ii = np.arange(128)
    m["c_tri"] = (ii[:, None] <= ii[None, :]).astype(np.float32)
    m["c_msk"] = np.ascontiguousarray(np.stack([(ii[:, None] < ii[None, :]), (ii[:, None] <= ii[None, :]), (ii[:, None] > ii[None, :])], 1).astype(np.float32))
    m.update(nsa_consts(hh))
    for nm in ("w_branch_a", "w_branch_b", "w_out", "peer_wq", "peer_k1", "peer_k2", "peer_u", "peer_v"):
        m[nm] = g(nm)
    m["norm2_g_b"] = bc(g("norm2_g"))
    ri = np.zeros((128, NT), np.int32)
    ri[:, :ntl] = ((NT - ntl) * 128 + np.arange(ntl)[None, :] * 128 + np.arange(128)[:, None]).astype(np.int32)
    m["rowidx"] = ri
    m["iota16"] = np.ascontiguousarray(np.broadcast_to(np.arange(16, dtype=np.float32)[None, :], (128, 16)))
    m["nsa_gains_b"] = bc(np.concatenate([np.tile(g("nsa_q_g"), 8), np.tile(g("nsa_ks_g"), 2), np.tile(g("nsa_kw_g"), 2)]))
    m["nsa_kc_g_b"] = bc(g("nsa_kc_g"))
    posT = np.zeros((128, 2, 32), np.float32)
    posT[0:64, 0, :] = g("cmp_pos_k").T
    posT[0:64, 1, :] = g("cmp_pos_v").T
    m["posT"] = posT
    w2 = np.stack([g("cmp_k_w2").reshape(2, 128, 64), g("cmp_v_w2").reshape(2, 128, 64)], 0)
    m["cmp_w2"] = np.ascontiguousarray(w2.transpose(2, 0, 1, 3))
    w1 = []
    for nm in ("cmp_k_w1", "cmp_v_w1"):
        a = g(nm).reshape(32, 64, 256).transpose(1, 0, 2)
        w1.append(np.concatenate([a, a], 0))
    m["cmp_w1"] = np.ascontiguousarray(np.stack(w1, 0))
    return m


def dap(ap, offset, pattern):
    return bass.AP(ap.tensor, offset, [list(p) for p in pattern])


def stage2a(C):
    nc, pg, dr = C.nc, C.pg, C.dr
    with ExitStack() as st:
        sb, ps = _mk(C, st)
        mu = sb("a_mu", [128, RWC], F32)
        w0 = sb("a_w0", [128, 512], F32)
        a0 = sb("a_a0", [128, 512], F32)
        kkc = sb("a_kk", [128, 512], F32)
        kac = sb("a_ka", [128, 512], F32)
        rkc = sb("a_rk", [128, 512], F32)
        wup = sb("a_wup", [128, 512], F32)
        gup = sb("a_gup", [128, 512], F32)
        idf = sb("a_idf", [128, 128], F32)
        p_2 = [sb(f"a_p{i_}", [128, RWC], F32) for i_ in range(2)]
        pv_2 = [sb(f"a_pv{i_}", [128, RWC], F32) for i_ in range(2)]
        pm_2 = [sb(f"a_pm{i_}", [128, RWC], F32) for i_ in range(2)]
        lor_2 = [sb(f"a_lor{i_}", [128, 256], F32) for i_ in range(2)]
        lorT_2 = [sb(f"a_lorT{i_}", [128, 256], F32) for i_ in range(2)]
        wt_2 = [sb(f"a_wt{i_}", [128, 512], F32) for i_ in range(2)]
        lwt_2 = [sb(f"a_lwt{i_}", [128, 512], F32) for i_ in range(2)]
        at_2 = [sb(f"a_at{i_}", [128, 512], F32) for i_ in range(2)]
        gt_2 = [sb(f"a_gt{i_}", [128, 512], F32) for i_ in range(2)]
        kk_2 = [sb(f"a_kkt{i_}", [128, 512], F32) for i_ in range(2)]
        sq_2 = [sb(f"a_sq{i_}", [128, 512], F32) for i_ in range(2)]
        nrm_2 = [sb(f"a_nrm{i_}", [128, 32], F32) for i_ in range(2)]
        kkn_2 = [sb(f"a_kkn{i_}", [128, 512], F32) for i_ in range(2)]
        bt_2 = [sb(f"a_bt{i_}", [128, 512], F32) for i_ in range(2)]
        t1_2 = [sb(f"a_t1{i_}", [128, 512], F32) for i_ in range(2)]
        kp_2 = [sb(f"a_kp{i_}", [128, 512], F32) for i_ in range(2)]
        bon_2 = [sb(f"a_bon{i_}", [128, 8], F32) for i_ in range(2)]
        psl_2 = [ps(f"a_psl{i_}", [128, 512], F32) for i_ in range(2)]
        psw_2 = [ps(f"a_psw{i_}", [128, 512], F32) for i_ in range(2)]
        psa_2 = [ps(f"a_psa{i_}", [128, 512], F32) for i_ in range(2)]
        psg_2 = [ps(f"a_psg{i_}", [128, 512], F32) for i_ in range(2)]

        for (tile, name) in ((mu, "rw_mu_b"), (w0, "rw_w0_b"), (a0, "rw_a0_b"), (kkc, "rw_k_k_b"),
                             (kac, "rw_k_a_b"), (rkc, "rw_r_k_b"), (gup, "rw_g_up"), (idf, "ident")):
            pg.ld(tile[:], dr[name][:, :], writes=[name])
        pg.ld(wup[0:64, :], dr["rw_w_up"][:, :], writes=["wup0"])
        pg.ld(wup[64:128, :], dr["rw_a_up"][:, :], writes=["wup1"])
        P = dr["P"]
        def E_(i):
            t0 = i * 128
            s = i % 2
            p, pv, pm, lor, lorT, wt, lwt, at, gt, kk, sq, nrm, kkn, bt, t1, kp, bon = [t_[s] for t_ in (
                p_2, pv_2, pm_2, lor_2, lorT_2, wt_2, lwt_2, at_2, gt_2, kk_2, sq_2, nrm_2, kkn_2, bt_2, t1_2, kp_2, bon_2)]
            psl, psw, psa, psg = psl_2[s], psw_2[s], psa_2[s], psg_2[s]
            pg.ld(p[:], P[t0:t0 + 128, 0:RWC], reads=[("P", i)], writes=[("p", s)])
            if i == 0:
                pg.memset("dve", pv[0:1, :], 0.0, writes=[("pv0", s)])
                pg.ld(pv[1:128, :], P[0:127, 0:RWC], reads=[("P", 0)], writes=[("pv", s)])
                pvk = [("pv", s), ("pv0", s)]
            else:
                pg.ld(pv[:], P[t0 - 1:t0 + 127, 0:RWC], reads=[("P", i), ("P", i - 1)], writes=[("pv", s), ("pv0", s)])
                pvk = [("pv", s), ("pv0", s)]
            pg.tt("dve", pv[:], pv[:], p[:], ALU.subtract, reads=pvk + [("p", s)], writes=[("pv", s)])
            pg.tt("dve", pv[:], pv[:], mu[:], ALU.mult, reads=[("pv", s), "rw_mu_b"], writes=[("pv", s)])
            pg.tt("dve", pm[:], pv[:], p[:], ALU.add, reads=[("pv", s), ("p", s)], writes=[("pm", s)])
            r_ = pm[:, 0:512]
            k_ = pm[:, 512:1024]
            v_ = pm[:, 1024:1536]
            pg.act(lor[:, 0:64], pm[:, 1536:1600], AF.Tanh, reads=[("pm", s)], writes=[("lor0", s)])
            pg.cp("pool", lor[:, 64:128], pm[:, 1600:1664], reads=[("pm", s)], writes=[("lor1", s)])
            pg.act(lor[:, 128:256], pm[:, 1664:1792], AF.Sigmoid, reads=[("pm", s)], writes=[("lor2", s)])
            pg.tr(psl[:, 0:128], lor[:, 0:128], idf[:], reads=[("lor0", s), ("lor1", s), "ident"], writes=[("psl", s)])
            pg.tr(psl[:, 128:256], lor[:, 128:256], idf[:], reads=[("lor2", s), "ident"], writes=[("psl", s)])
            pg.cp("act", lorT[:], psl[:, 0:256], reads=[("psl", s)], writes=[("lorT", s)])
            pg.mm(psw[:], lorT[0:64, 0:128], wup[0:64, :], reads=[("lorT", s), "wup0"], writes=[("psw", s)])
            pg.mm(psa[:], lorT[64:128, 0:128], wup[64:128, :], reads=[("lorT", s), "wup1"], writes=[("psa", s)])
            pg.mm(psg[:], lorT[:, 128:256], gup[:], reads=[("lorT", s), "rw_g_up"], writes=[("psg", s)])
        def L_(i):
            t0 = i * 128
            s = i % 2
            p, pv, pm, lor, lorT, wt, lwt, at, gt, kk, sq, nrm, kkn, bt, t1, kp, bon = [t_[s] for t_ in (
                p_2, pv_2, pm_2, lor_2, lorT_2, wt_2, lwt_2, at_2, gt_2, kk_2, sq_2, nrm_2, kkn_2, bt_2, t1_2, kp_2, bon_2)]
            psl, psw, psa, psg = psl_2[s], psw_2[s], psa_2[s], psg_2[s]
            r_ = pm[:, 0:512]
            k_ = pm[:, 512:1024]
            v_ = pm[:, 1024:1536]
            pg.tt("dve", wt[:], psw[:], w0[:], ALU.add, reads=[("psw", s), "rw_w0_b"], writes=[("wt", s)])
            pg.act(wt[:], wt[:], AF.Sigmoid, reads=[("wt", s)], writes=[("wt", s)])
            pg.ts("dve", lwt[:], wt[:], -0.6065306597126334, None, ALU.mult, reads=[("wt", s)], writes=[("lwt", s)])
            pg.tt("dve", at[:], psa[:], a0[:], ALU.add, reads=[("psa", s), "rw_a0_b"], writes=[("at", s)])
            pg.act(at[:], at[:], AF.Sigmoid, reads=[("at", s)], writes=[("at", s)])
            pg.cp("act", gt[:], psg[:], reads=[("psg", s)], writes=[("gt", s)])
            pg.tt("dve", kk[:], k_, kkc[:], ALU.mult, reads=[("pm", s), "rw_k_k_b"], writes=[("kk", s)])
            pg.tt("pool", sq[:], kk[:], kk[:], ALU.mult, reads=[("kk", s)], writes=[("sq", s)])
            pg.red("dve", nrm[:, 0:8], sq[:].rearrange("p (h k) -> p h k", h=8), ALU.add, reads=[("sq", s)], writes=[("nrm0", s)])
            pg.act(nrm[:, 8:16], nrm[:, 0:8], AF.Sqrt, reads=[("nrm0", s)], writes=[("nrm1", s)])
            pg.ts("dve", nrm[:, 16:24], nrm[:, 8:16], 1e-12, None, ALU.max, reads=[("nrm1", s)], writes=[("nrm2", s)])
            pg.op("dve", lambda e, nrm=nrm: e.reciprocal(nrm[:, 24:32], nrm[:, 16:24]), reads=[("nrm2", s)], writes=[("nrm3", s)])
            rinv_b = nrm[:, 24:32].unsqueeze(2).to_broadcast([128, 8, 64])
            v3 = lambda tl: tl[:].rearrange("p (h k) -> p h k", h=8)
            pg.stt("dve", v3(kkn), v3(kk), -1.0, rinv_b, ALU.mult, ALU.mult, reads=[("kk", s), ("nrm3", s)], writes=[("kkn", s)])
            pg.stt("dve", bt[:], kkn[:], -1.0, at[:], ALU.mult, ALU.mult, reads=[("kkn", s), ("at", s)], writes=[("bt", s)])
            pg.stt("dve", t1[:], at[:], -1.0, kac[:], ALU.add, ALU.mult, reads=[("at", s), "rw_k_a_b"], writes=[("t1", s)])
            pg.stt("dve", kp[:], t1[:], 1.0, k_, ALU.add, ALU.mult, reads=[("t1", s), ("pm", s)], writes=[("kp", s)])
            pg.tt("pool", sq[:], r_, kp[:], ALU.mult, reads=[("pm", s), ("kp", s), ("sq", s)], writes=[("sq", s)])
            pg.tt("pool", sq[:], sq[:], rkc[:], ALU.mult, reads=[("sq", s), "rw_r_k_b"], writes=[("sq", s)])
            pg.red("dve", bon[:], sq[:].rearrange("p (h k) -> p h k", h=8), ALU.add, reads=[("sq", s)], writes=[("bon", s)])
            pg.ld(dr["RR"][t0:t0 + 128, :], r_, reads=[("pm", s)], writes=[("RR", i)], q="act")
            pg.ld(dr["RKK"][t0:t0 + 128, :], kkn[:], reads=[("kkn", s)], writes=[("RKK", i)], q="act")
            pg.ld(dr["RLW"][t0:t0 + 128, :], lwt[:], reads=[("lwt", s)], writes=[("RLW", i)], q="act")
            pg.ld(dr["RB"][t0:t0 + 128, :], bt[:], reads=[("bt", s)], writes=[("RB", i)], q="act")
            pg.ld(dr["RKp"][t0:t0 + 128, :], kp[:], reads=[("kp", s)], writes=[("RKp", i)], q="act")
            pg.ld(dr["RV"][t0:t0 + 128, :], v_, reads=[("pm", s)], writes=[("RV", i)], q="act")
            pg.ld(dr["RG"][t0:t0 + 128, :], gt[:], reads=[("gt", s)], writes=[("RG", i)], q="act")
            pg.ld(dr["RBON"][t0:t0 + 128, :], bon[:], reads=[("bon", s)], writes=[("RBON", i)], q="act")
        E_(0)
        for i in range(NT):
            if i + 1 < NT:
                E_(i + 1)
            L_(i)
        pg.barrier()
        pg.emit()


_IG = [0]


def igather(pg, out_ap, table_ap, idx_ap, reads, writes, row0=None):
    if row0 is not None:
        _IG[0] += 1
        pg.ld(out_ap, table_ap[row0:row0 + 128, :], reads=reads, writes=writes, q=("sp" if _IG[0] % 2 == 0 else "act"))
        return
    pg.dma("pool", lambda e: e.indirect_dma_start(out=out_ap, out_offset=None, in_=table_ap,
                                                   in_offset=bass.IndirectOffsetOnAxis(ap=idx_ap, axis=0)), reads, writes)


def stage2c(C):
    nc, pg, dr = C.nc, C.pg, C.dr
    with ExitStack() as st:
        sb, ps = _mk(C, st)
        lnw = sb("c_lnw", [128, 512], F32)
        lnb = sb("c_lnb", [128, 512], F32)
        eps = sb("c_eps", [128, 1], F32)
        y = [sb(f"c_y{i}", [128, 8, 64], F32) for i in range(2)]
        v = [sb(f"c_v{i}", [128, 8, 64], F32) for i in range(2)]
        g = [sb(f"c_g{i}", [128, 512], F32) for i in range(2)]
        bon = [sb(f"c_bon{i}", [128, 8], F32) for i in range(2)]
        stt_ = [sb(f"c_st{i}", [128, 32], F32) for i in range(2)]
        sq = sb("c_sq", [128, 8, 64], F32)
        pg.ld(lnw[:], dr["rw_ln_w_b"][:, :], writes=["lnw"])
        pg.ld(lnb[:], dr["rw_ln_b_b"][:, :], writes=["lnb"])
        pg.memset("dve", eps[:], 64e-5, writes=["eps"])
        f2 = lambda tl: tl[:].rearrange("p h k -> p (h k)")
        rowidx = sb("c_rowidx", [128, NT], I32)
        pg.ld(rowidx[:], dr["rowidx"][:, :], writes=["rowidx"])
        allk = lambda nm: [(nm, k) for k in range(NT)]
        for i in range(C.peer_tiles):
            s = i % 2
            t0 = i * 128
            yk, vk, gk, bk, sk = ("y", s), ("v", s), ("g", s), ("bon", s), ("st", s)
            ix = rowidx[:, i:i + 1]
            igather(pg, f2(y[s]), dr["YS"][:, :], ix, allk("YS") + ["rowidx"], [yk], row0=(NT - C.peer_tiles) * 128 + t0)
            igather(pg, f2(v[s]), dr["RV"][:, :], ix, allk("RV") + ["rowidx"], [vk], row0=(NT - C.peer_tiles) * 128 + t0)
            igather(pg, g[s][:], dr["RG"][:, :], ix, allk("RG") + ["rowidx"], [gk], row0=(NT - C.peer_tiles) * 128 + t0)
            igather(pg, bon[s][:], dr["RBON"][:, :], ix, allk("RBON") + ["rowidx"], [bk], row0=(NT - C.peer_tiles) * 128 + t0)
            S_ = stt_[s]
            bc = lambda ap: ap.unsqueeze(2).to_broadcast([128, 8, 64])
            pg.red("dve", S_[:, 0:8], y[s][:], ALU.add, reads=[yk], writes=[(sk, 0)])
            pg.ts("dve", S_[:, 8:16], S_[:, 0:8], -1.0 / 64, None, ALU.mult, reads=[(sk, 0)], writes=[(sk, 1)])
            pg.tt("dve", y[s][:], y[s][:], bc(S_[:, 8:16]), ALU.add, reads=[yk, (sk, 1)], writes=[yk])
            pg.tt("pool", sq[:], y[s][:], y[s][:], ALU.mult, reads=[yk], writes=["sq"])
            pg.red("dve", S_[:, 16:24], sq[:], ALU.add, reads=["sq"], writes=[(sk, 2)])
            pg.act(S_[:, 24:32], S_[:, 16:24], AF.Sqrt, reads=[(sk, 2), "eps"], writes=[(sk, 3)], scale=1.0 / 64, bias=eps[:, 0:1])
            pg.op("dve", lambda e, o=S_[:, 16:24], a=S_[:, 24:32]: e.reciprocal(o, a), reads=[(sk, 3)], writes=[(sk, 2)])
            pg.tt("dve", y[s][:], y[s][:], bc(S_[:, 16:24]), ALU.mult, reads=[yk, (sk, 2)], writes=[yk])
            pg.tt("dve", f2(y[s]), f2(y[s]), lnw[:], ALU.mult, reads=[yk, "lnw"], writes=[yk])
            pg.tt("pool", f2(y[s]), f2(y[s]), lnb[:], ALU.add, reads=[yk, "lnb"], writes=[yk])
            pg.tt("pool", v[s][:], v[s][:], bc(bon[s][:, 0:8]), ALU.mult, reads=[vk, bk], writes=[vk])
            pg.tt("dve", y[s][:], y[s][:], v[s][:], ALU.add, reads=[yk, vk], writes=[yk])
            pg.tt("dve", f2(y[s]), f2(y[s]), g[s][:], ALU.mult, reads=[yk, gk], writes=[yk])
            pg.ld(dr["YAL"][t0:t0 + 128, :], f2(y[s]), reads=[yk], writes=[("YAL", i)])
        pg.barrier()
        pg.emit()


NEG = -30000.0


def stage3(C):
    nc, pg, dr = C.nc, C.pg, C.dr
    with ExitStack() as st:
        sb, ps = _mk(C, st)
        qT = sb("n_qT", [128, 4, T], BF16)
        KsT = sb("n_KsT", [128, 2, T], BF16)
        KwT = sb("n_KwT", [128, 2, T], BF16)
        Vs = sb("n_Vs", [128, NT, 2, 65], BF16)
        Vw = sb("n_Vw", [128, NT, 2, 65], BF16)
        KcT = sb("n_KcT", [128, 2, 256], BF16)
        Vc = sb("n_Vc", [128, 2, 2, 129], BF16)
        GT = sb("n_GT", [128, NT, 24], F32)
        idf = sb("n_idf", [128, 128], F32)
        idb = sb("n_idb", [128, 128], BF16)
        eps = sb("n_eps", [128, 1], F32)
        pg.ld(idf[:], dr["ident"][:, :], writes=["idf"])
        pg.cp("dve", idb[:], idf[:], reads=["idf"], writes=["idb"])
        pg.memset("dve", eps[:], 1e-6, writes=["eps"])
        pg.memset("pool", Vs[:], 1.0, writes=["Vs"])
        pg.memset("pool", Vw[:], 1.0, writes=["Vw"])
        pg.memset("pool", Vc[:], 0.0, writes=["Vc"])
        with ExitStack() as sa_:
            sb, ps = _mk(C, sa_)
            kcT2 = sb("n_kcT2", [128, T], BF16)
            vcT2 = sb("n_vcT2", [128, T], BF16)
            w1 = [sb(f"n_w1{i}", [128, 32, 256], BF16) for i in range(2)]
            w1s = sb("n_w1s", [128, 16, 256], F32)
            w2s = sb("n_w2s", [128, 2, 2, 64], F32)
            w2 = sb("n_w2", [128, 2, 2, 64], BF16)
            posf = sb("n_posf", [128, 2, 32], F32)
            posb = sb("n_posb", [128, 2, 32], BF16)
            gains = sb("n_gains", [128, 768], F32)
            kcg = sb("n_kcg", [128, 64], F32)
            ovl = sb("n_ovl", [128, 2, 64], F32)
            R = [sb(f"n_R{i}", [128, 1304], F32) for i in range(2)]
            sq = sb("n_sq", [128, 1280], F32)
            tmp = sb("n_tmp", [128, 768], F32)
            stat = sb("n_stat", [128, 64], F32)
            Xb = sb("n_Xb", [128, 10, 128], BF16)
            biasS = sb("n_biasS", [128, 4], F32)
            xb_ = sb("n_xb", [128, 256], F32)
            x2_ = sb("n_x2", [128, 256], F32)
            hT = sb("n_hT", [128, 2, 256], BF16)
            kcn2 = sb("n_kcn2", [128, 128], BF16)
            st2 = sb("n_st2", [128, 8], F32)
            ksq = sb("n_ksq", [128, 64], F32)
            psX_ = [ps(f"n_psX{i}", [128, 1024], BF16) for i in range(3)]
            psX = [t_[:, 0:512].rearrange("p (a b) -> p a b", a=4) for t_ in psX_]
            psh = ps("n_psh", [128, 512], F32)
            psb = ps("n_psb", [128, 512], F32)
            pso = ps("n_pso", [128, 512], F32)
            psk = ps("n_psk", [128, 1024], BF16)

            pg.ld(gains[:], dr["nsa_gains_b"][:, :], writes=["gains"])
            pg.ts("dve", gains[:, 0:512], gains[:, 0:512], 0.125, None, ALU.mult, reads=["gains"], writes=["gains"])
            pg.ld(kcg[:], dr["nsa_kc_g_b"][:, :], writes=["kcg"])
            pg.ld(ovl[:], dr["ovl"][:, :, :], writes=["ovl"])
            pg.ld(posf[:], dr["posT"][:, :, :], writes=["posf"])
            pg.cp("dve", posb[:], posf[:], reads=["posf"], writes=["posb"])
            pg.ld(w2s[:], dr["cmp_w2"][:, :, :, :], writes=["w2s"])
            pg.cp("dve", w2[:], w2s[:], reads=["w2s"], writes=["w2"])
            for x in range(2):
                for hf in range(2):
                    pg.ld(w1s[:], dr["cmp_w1"][x, :, hf * 16:(hf + 1) * 16, :], writes=["w1s"])
                    pg.cp("pool", w1[x][:, hf * 16:(hf + 1) * 16, :], w1s[:], reads=["w1s"], writes=[("w1", x)])
            for i in range(NT):
                s = i % 2
                t0 = i * 128
                Rk = ("R", s)
                pg.ld(R[s][:], dr["P"][t0:t0 + 128, 1792:3096], reads=[("P", i)], writes=[Rk])
                Rs = R[s]
                pg.tt("pool", sq[:], Rs[:, 0:1280], Rs[:, 0:1280], ALU.mult, reads=[Rk], writes=["sq"])
                pg.red("dve", stat[:, 0:20], sq[:].rearrange("p (a k) -> p a k", k=64), ALU.add, reads=["sq"], writes=["stat0"])
                pg.act(stat[:, 20:40], stat[:, 0:20], AF.Sqrt, reads=["stat0", "eps"], writes=["stat1"], scale=1.0 / 64, bias=eps[:, 0:1])
                pg.op("dve", lambda e: e.reciprocal(stat[:, 40:60], stat[:, 20:40]), reads=["stat1"], writes=["stat2"])
                b3 = lambda ap, n: ap.unsqueeze(2).to_broadcast([128, n, 64])
                v3 = lambda ap: ap.rearrange("p (a k) -> p a k", k=64)
                pg.tt("dve", v3(tmp[:, 0:512]), v3(Rs[:, 0:512]), b3(stat[:, 40:48], 8), ALU.mult, reads=[Rk, "stat2"], writes=["tmp"])
                pg.tt("dve", v3(tmp[:, 512:640]), v3(Rs[:, 768:896]), b3(stat[:, 52:54], 2), ALU.mult, reads=[Rk, "stat2"], writes=["tmp"])
                pg.tt("dve", v3(tmp[:, 640:768]), v3(Rs[:, 1024:1152]), b3(stat[:, 56:58], 2), ALU.mult, reads=[Rk, "stat2"], writes=["tmp"])
                pg.tt("pool", tmp[:], tmp[:], gains[:], ALU.mult, reads=["tmp", "gains"], writes=["tmp"])
                pg.cp("pool", Xb[:, 0:4, :].rearrange("p a b -> p (a b)"), tmp[:, 0:512], reads=["tmp"], writes=["Xb"])
                for (blk, c0) in ((4, 512), (6, 640)):
                    src = tmp[:, c0:c0 + 128].rearrange("p (g k) -> p g k", g=2).unsqueeze(2).to_broadcast([128, 2, 2, 64])
                    dst = Xb[:, blk:blk + 2, :].rearrange("p g (d k) -> p g d k", d=2)
                    pg.cp("dve", dst, src, reads=["tmp"], writes=["Xb"])
                pg.cp("pool", Xb[:, 8, :], Rs[:, 512:640], reads=[Rk], writes=["Xb"])
                pg.cp("pool", Xb[:, 9, :], Rs[:, 640:768], reads=[Rk], writes=["Xb"])
                for blk in range(10):
                    pg.tr(psX[blk // 4][:, blk % 4, :], Xb[:, blk, :], idb[:], reads=["Xb", "idb"], writes=[("psX", blk // 4)])
                pg.cp("act", qT[:, :, t0:t0 + 128], psX[0], reads=[("psX", 0)], writes=["qT"])
                pg.cp("dve", KsT[:, :, t0:t0 + 128], psX[1][:, 0:2, :], reads=[("psX", 1)], writes=["KsT"])
                pg.cp("dve", KwT[:, :, t0:t0 + 128], psX[1][:, 2:4, :], reads=[("psX", 1)], writes=["KwT"])
                pg.cp("act", kcT2[:, t0:t0 + 128], psX[2][:, 0, :], reads=[("psX", 2)], writes=["kcT2"])
                pg.cp("act", vcT2[:, t0:t0 + 128], psX[2][:, 1, :], reads=[("psX", 2)], writes=["vcT2"])
                pg.cp("pool", Vs[:, i, :, 0:64], Rs[:, 896:1024].rearrange("p (g k) -> p g k", g=2), reads=[Rk, "Vs"], writes=["Vs"])
                pg.cp("pool", Vw[:, i, :, 0:64], Rs[:, 1152:1280].rearrange("p (g k) -> p g k", g=2), reads=[Rk, "Vw"], writes=["Vw"])
                pg.act(GT[:, i, :], Rs[:, 1280:1304], AF.Sigmoid, reads=[Rk], writes=["GT"])
            if getattr(C, "lvl", 9) < 2:
                pg.barrier()
                pg.emit()
                return
            pg.memset("dve", hT[:], 0.0, writes=["hT"])
            pg.memset("dve", kcn2[:], 0.0, writes=["kcn2"])
            for x in range(2):
                for hf in range(2):
                    for l in range(32):
                        pg.mm(psb[:, x * 2 + hf:x * 2 + hf + 1], w1[x][0:64, l, hf * 128:(hf + 1) * 128], posb[0:64, x, l:l + 1],
                              start=(l == 0), stop=(l == 31), reads=[("w1", x), "posb"], writes=["psb"])
            pg.cp("dve", biasS[:], psb[:, 0:4], reads=["psb"], writes=["biasS"])
            for x in range(2):
                srcT = kcT2 if x == 0 else vcT2
                skey = "kcT2" if x == 0 else "vcT2"
                for g in range(2):
                    for hf in range(2):
                        for l in range(32):
                            rhs = dap(srcT[:], g * 64 * T + l, [[T, 64], [16, 255]])
                            pg.mm(psh[:, 0:255], w1[x][g * 64:(g + 1) * 64, l, hf * 128:(hf + 1) * 128], rhs,
                                  start=(l == 0), stop=(l == 31), reads=[("w1", x), skey], writes=["psh"])
                        c = slice(0, 255)
                        pg.act(xb_[:, c], psh[:, c], AF.Identity, reads=["psh", "biasS"], writes=["xb"], bias=biasS[:, x * 2 + hf:x * 2 + hf + 1])
                        pg.tt("pool", x2_[:, c], xb_[:, c], xb_[:, c], ALU.mult, reads=["xb"], writes=["x2"])
                        pg.ts("dve", x2_[:, c], x2_[:, c], 0.044715, 1.0, ALU.mult, ALU.add, reads=["x2"], writes=["x2"])
                        pg.tt("dve", x2_[:, c], x2_[:, c], xb_[:, c], ALU.mult, reads=["x2", "xb"], writes=["x2"])
                        pg.act(x2_[:, c], x2_[:, c], AF.Tanh, reads=["x2"], writes=["x2"], scale=0.7978845608028654)
                        pg.stt("dve", x2_[:, c], x2_[:, c], 1.0, xb_[:, c], ALU.add, ALU.mult, reads=["x2", "xb"], writes=["x2"])
                        pg.ts("dve", hT[:, hf, c], x2_[:, c], 0.5, None, ALU.mult, reads=["x2"], writes=["hT"])
                    for m in range(2):
                        rows = 128 if m == 0 else 127
                        for hf in range(2):
                            pg.mm(pso[0:rows, 0:64], hT[:, hf, m * 128:m * 128 + rows], w2[:, x, hf, :], start=(hf == 0), stop=(hf == 1),
                                  reads=["hT", "w2"], writes=["pso"])
                        if x == 0:
                            pg.cp("act", ksq[0:rows, :], pso[0:rows, 0:64], reads=["pso"], writes=["ksq"])
                            pg.tt("pool", x2_[0:rows, 0:64], ksq[0:rows, :], ksq[0:rows, :], ALU.mult, reads=["ksq", "x2"], writes=["x2"])
                            pg.red("dve", st2[0:rows, 0:1], x2_[0:rows, 0:64], ALU.add, reads=["x2"], writes=["st2a"])
                            pg.act(st2[0:rows, 1:2], st2[0:rows, 0:1], AF.Sqrt, reads=["st2a", "eps"], writes=["st2b"], scale=1.0 / 64, bias=eps[0:rows, 0:1])
                            pg.op("dve", lambda e, rows=rows: e.reciprocal(st2[0:rows, 2:3], st2[0:rows, 1:2]), reads=["st2b"], writes=["st2c"])
                            pg.stt("dve", ksq[0:rows, :], ksq[0:rows, :], st2[0:rows, 2:3], kcg[0:rows, :], ALU.mult, ALU.mult,
                                   reads=["ksq", "st2c", "kcg"], writes=["ksq"])
                            src = ksq[0:rows, :].unsqueeze(1).to_broadcast([rows, 2, 64])
                            pg.cp("dve", kcn2[0:rows, :].rearrange("p (d k) -> p d k", d=2), src, reads=["ksq"], writes=["kcn2"])
                            pg.tr(psk[:, 0:128], kcn2[:, :], idb[:], reads=["kcn2", "idb"], writes=["psk"])
                            pg.cp("act", KcT[:, g, m * 128:(m + 1) * 128], psk[:, 0:128], reads=["psk"], writes=["KcT"])
                        else:
                            pg.cp("act", Vc[0:rows, m, g, 0:64], pso[0:rows, 0:64], reads=["pso", "Vc"], writes=["Vc"])
            for m in range(2):
                for g in range(2):
                    pg.memset("dve", Vc[:, m, g, 64:65], 1.0, writes=["Vc"])
                    pg.cp("dve", Vc[:, m, g, 65:129], ovl[:, m, :], reads=["ovl", "Vc"], writes=["Vc"])
            pg.barrier()
            pg.emit()
        if getattr(C, "lvl", 9) < 3:
            return
        stage3_attn(C, st, qT, KsT, KwT, Vs, Vw, KcT, Vc, GT, idf, idb)


def stage3_attn(C, st, qT, KsT, KwT, Vs, Vw, KcT, Vc, GT, idf, idb):
    nc, pg, dr = C.nc, C.pg, C.dr
    with ExitStack() as sb_:
        sb, ps = _mk(C, sb_)
        cmpb = sb("n_cmpb", [128, 2, T], BF16)
        Esel = sb("n_Esel", [128, 32, 128], BF16)
        causb = sb("n_causb", [128, 4, 512], BF16)
        winb = sb("n_winb", [128, 8, 512], BF16)
        selbT = sb("n_selbT", [128, 2, T], BF16)
        eT = [sb(f"n_eT{i}", [128, 512], BF16) for i in range(4)]
        eT2 = [sb(f"n_eT2{i}", [128, 512], BF16) for i in range(4)]
        Mt = [sb(f"n_Mt{i}", [128, 512], BF16) for i in range(2)]
        rm = [0]
        dq = []
        ocmp = sb("n_ocmp", [128, 4, 8, 64], F32)
        osel = sb("n_osel", [128, 4, 8, 64], F32)
        owin = sb("n_owin", [128, 4, 8, 64], F32)
        den = sb("n_den", [128, 16], F32)
        impw = sb("n_impw", [128, 2, 4, 64], F32)
        score = sb("n_score", [128, 2, 64], F32)
        VM = [sb(f"n_VM{i}", [128, 2, 64], F32) for i in range(2)]
        work = sb("n_work", [128, 2, 64], F32)
        m8 = sb("n_m8", [128, 2, 16], F32)
        thr = sb("n_thr", [128, 2], F32)
        msel = sb("n_msel", [128, 2, 64], F32)
        selb = sb("n_selb", [128, 2, 2, 64], BF16)
        osT = [sb(f"n_osT{i}", [65, 512], F32) for i in range(2)]
        dn2 = sb("n_dn2", [128, 8], F32)
        yb = sb("n_yb", [128, 8, 64], F32)
        yb2 = sb("n_yb2", [128, 8, 64], F32)
        psS = [ps(f"n_psS{i}", [128, 512], F32) for i in range(3)]
        psA = [ps(f"n_psA{i}", [128, 512], F32) for i in range(2)]
        psB = [ps(f"n_psB{i}", [128, 512], F32) for i in range(2)]
        psZ_ = ps("n_psZ", [128, 1024], BF16)
        psZ = psZ_[:, 0:256].rearrange("p (g q) -> p g q", g=2)

        pg.ld(cmpb[:], dr["c_cmpb"][:, :, :], writes=["cmpb"])
        pg.ld(Esel[:], dr["c_esel"][:, :, :], writes=["Esel"])
        pg.ld(causb[:], dr["c_causb"][:, :, :], writes=["causb"])
        pg.ld(winb[:], dr["c_winb"][:, :, :], writes=["winb"])
        winb4 = sb("n_winb4", [128, 8, 512], BF16)
        pg.ld(winb4[:], dr["c_winb4"][:, :, :], writes=["winb4"])
        rs = [0]
        re = [0]

        def nxt(lst, n):
            v = lst[0]
            lst[0] = (v + 1) % n
            return v

        def qk(h):
            return (h % 2) * 64, h // 2, h // 4

        for Q in range(4, 8):
            tq0 = Q * 512
            for ii in range(4):
                i = Q * 4 + ii
                t0 = i * 128
                s = i % 2
                pg.ld(VM[s][:], dr["c_vmfb"][i, :, :, :], writes=[("VM", s)])
                nm = 2 if i >= 16 else 1
                for h in range(8):
                    base, hp, g = qk(h)
                    h4 = h % 4
                    for m in range(nm):
                        r = nxt(rs, 3)
                        pS = psS[r]
                        pg.mm(pS[:, 0:128], KcT[base:base + 64, g, m * 128:(m + 1) * 128], qT[base:base + 64, hp, t0:t0 + 128],
                              start=True, stop=False, reads=["KcT", "qT"], writes=[("psS", r)])
                        pg.mm(pS[:, 0:128], idb[:, :], cmpb[:, m, t0:t0 + 128], start=False, stop=True,
                              reads=["idb", "cmpb"], writes=[("psS", r)])
                        k = nxt(re, 4)
                        pg.act(eT[k][:, 0:128], pS[:, 0:128], AF.Exp, reads=[("psS", r)], writes=[("eT", k)])
                        def pv(g=g, h4=h4, k=k, m=m, nm=nm):
                            pg.mm(psA[g][:, h4 * 65:h4 * 65 + 65], eT[k][:, 0:128], Vc[:, m, g, 0:65], start=(m == 0), stop=(m == nm - 1),
                                  reads=[("eT", k), "Vc"], writes=[("psA", g)])
                            pg.mm(psB[g][:, h4 * 64:h4 * 64 + 64], eT[k][:, 0:128], Vc[:, m, g, 65:129], start=(m == 0), stop=(m == nm - 1),
                                  reads=[("eT", k), "Vc"], writes=[("psB", g)])
                        dq.append(pv)
                        if len(dq) > 2:
                            dq.pop(0)()
                while dq:
                    dq.pop(0)()
                for g in range(2):
                    A3 = psA[g][:, 0:260].rearrange("p (h c) -> p h c", c=65)
                    B3 = psB[g][:, 0:256].rearrange("p (h c) -> p h c", c=64)
                    dsl = den[:, g * 4:(g + 1) * 4]
                    rsl = den[:, 8 + g * 4:8 + (g + 1) * 4]
                    pg.ts("dve", dsl, A3[:, :, 64], 1e-30, None, ALU.max, reads=[("psA", g)], writes=[("den", g)])
                    pg.op("dve", lambda e, o=rsl, a=dsl: e.reciprocal(o, a), reads=[("den", g)], writes=[("rden", g)])
                    rb = rsl.unsqueeze(2).to_broadcast([128, 4, 64])
                    pg.tt("dve", ocmp[:, ii, g * 4:(g + 1) * 4, :], A3[:, :, 0:64], rb, ALU.mult, reads=[("psA", g), ("rden", g)], writes=["ocmp"])
                    pg.tt("dve", impw[:, g, :, :], B3, rb, ALU.mult, reads=[("psB", g), ("rden", g)], writes=[("impw", g)])
                    pg.red("dve", score[:, g, :], impw[:, g, :, :].rearrange("p h j -> p j h"), ALU.add, reads=[("impw", g)], writes=[("score", g)])
                    vm = dr
                    pg.tt("dve", score[:, g, :], score[:, g, :], VM[s][:, 0, :], ALU.mult, reads=[("score", g), ("VM", s)], writes=[("score", g)])
                    pg.tt("dve", score[:, g, :], score[:, g, :], VM[s][:, 1, :], ALU.add, reads=[("score", g), ("VM", s)], writes=[("score", g)])
                    pg.op("dve", lambda e, g=g: e.max(m8[:, g, 0:8], score[:, g, :]), reads=[("score", g)], writes=[("m8a", g)])
                    pg.op("dve", lambda e, g=g: e.match_replace(work[:, g, :], m8[:, g, 0:8], score[:, g, :], -1e9),
                          reads=[("score", g), ("m8a", g)], writes=[("work", g)])
                    pg.op("dve", lambda e, g=g: e.max(m8[:, g, 8:16], work[:, g, :]), reads=[("work", g)], writes=[("m8b", g)])
                    pg.ts("dve", thr[:, g:g + 1], m8[:, g, 15:16], -0.5, None, ALU.max, reads=[("m8b", g)], writes=[("thr", g)])
                    pg.ts("dve", msel[:, g, :], score[:, g, :], thr[:, g:g + 1], None, ALU.is_ge, reads=[("score", g), ("thr", g)], writes=[("msel", g)])
                    pg.cp("dve", selb[:, g, :, :], msel[:, g, :].unsqueeze(1).to_broadcast([128, 2, 64]), reads=[("msel", g)], writes=[("selb", g)])
                    pg.tr(psZ[:, g, :], selb[:, g, :, :].rearrange("p d j -> p (d j)"), idb[:], reads=[("selb", g), "idb"], writes=["psZ"])
                pg.cp("act", selbT[:, :, t0:t0 + 128], psZ, reads=["psZ"], writes=["selbT"])
            for br in range(2):
                if getattr(C, "lvl", 9) < 4 + br:
                    continue
                dest = osel if br == 0 else owin
                dkey = "osel" if br == 0 else "owin"
                KT = KsT if br == 0 else KwT
                Vv = Vs if br == 0 else Vw
                kts = list(range(0, 4 * Q + 4)) if br == 0 else list(range(max(0, 4 * Q - 4), 4 * Q + 4))
                for g in range(2):
                    O = [psA[0], psA[1], psB[0], psB[1]]
                    okeys = [("psA", 0), ("psA", 1), ("psB", 0), ("psB", 1)]
                    for n_, kt in enumerate(kts):
                        if br == 0:
                            r = nxt(rs, 3)
                            pg.mm(psS[r][:, :], Esel[0:64, kt, :], selbT[0:64, g, tq0:tq0 + 512], reads=["Esel", "selbT"], writes=[("psS", r)])
                            mi = nxt(rm, 2)
                            if kt >= 4 * Q:
                                pg.tt("dve", Mt[mi][:], psS[r][:, :], causb[:, kt - 4 * Q, :], ALU.mult, reads=[("psS", r), "causb"], writes=[("Mt", mi)])
                            else:
                                pg.cp("dve", Mt[mi][:], psS[r][:, :], reads=[("psS", r)], writes=[("Mt", mi)])
                            mask, mkeys = Mt[mi][:], [("Mt", mi)]
                        else:
                            wsrc = winb4 if Q == 4 else winb
                            mask, mkeys = wsrc[:, kt - 4 * Q + 4, :], ["winb", "winb4"]
                        for h4 in range(4):
                            h = g * 4 + h4
                            base, hp, _g = qk(h)
                            r2 = nxt(rs, 3)
                            pg.mm(psS[r2][:, :], KT[base:base + 64, g, kt * 128:(kt + 1) * 128], qT[base:base + 64, hp, tq0:tq0 + 512],
                                  reads=["qT"], writes=[("psS", r2)])
                            k = nxt(re, 4)
                            pg.act(eT[k][:, :], psS[r2][:, :], AF.Exp, reads=[("psS", r2)], writes=[("eT", k)])
                            pg.tt("dve", eT2[k][:, :], eT[k][:, :], mask, ALU.mult, reads=[("eT", k)] + mkeys, writes=[("eT2", k)])
                            dq.append(lambda h4=h4, kt=kt, k=k, n_=n_, O=O, okeys=okeys, Vv=Vv, g=g, kts=kts: pg.mm(
                                O[h4][0:65, :], Vv[:, kt, g, :], eT2[k][:, :], start=(n_ == 0), stop=(n_ == len(kts) - 1),
                                reads=[("eT2", k)], writes=[okeys[h4]]))
                            if len(dq) > 2:
                                dq.pop(0)()
                    while dq:
                        dq.pop(0)()
                    for h4 in range(4):
                        h = g * 4 + h4
                        o = h4 % 2
                        pg.cp("act", osT[o][:, :], O[h4][0:65, :], reads=[okeys[h4]], writes=[("osT", o)])
                        r3 = nxt(rs, 3)
                        Tp = psS[r3]
                        for qq in range(4):
                            pg.tr(Tp[:, qq * 65:(qq + 1) * 65], osT[o][0:65, qq * 128:(qq + 1) * 128], idf[0:65, 0:65],
                                  reads=[("osT", o), "idf"], writes=[("psS", r3)])
                        T3 = Tp[:, 0:260].rearrange("p (q c) -> p q c", c=65)
                        pg.ts("dve", dn2[:, 0:4], T3[:, :, 64], 1e-30, None, ALU.max, reads=[("psS", r3)], writes=["dn2a"])
                        pg.op("dve", lambda e: e.reciprocal(dn2[:, 4:8], dn2[:, 0:4]), reads=["dn2a"], writes=["dn2b"])
                        pg.tt("dve", dest[:, :, h, :], T3[:, :, 0:64], dn2[:, 4:8].unsqueeze(2).to_broadcast([128, 4, 64]), ALU.mult,
                              reads=[("psS", r3), "dn2b"], writes=[dkey])
            for ii in range(4):
                i = Q * 4 + ii
                t0 = i * 128
                G3 = GT[:, i, :].rearrange("p (h c) -> p h c", c=3)
                gb = lambda c: G3[:, :, c].unsqueeze(2).to_broadcast([128, 8, 64])
                pg.tt("dve", yb[:], ocmp[:, ii, :, :], gb(0), ALU.mult, reads=["ocmp", "GT"], writes=["yb"])
                pg.tt("pool", yb2[:], osel[:, ii, :, :], gb(1), ALU.mult, reads=["osel", "GT"], writes=["yb2"])
                pg.tt("dve", yb[:], yb[:], yb2[:], ALU.add, reads=["yb", "yb2"], writes=["yb"])
                pg.tt("pool", yb2[:], owin[:, ii, :, :], gb(2), ALU.mult, reads=["owin", "GT", "yb2"], writes=["yb2"])
                pg.tt("dve", yb[:], yb[:], yb2[:], ALU.add, reads=["yb", "yb2"], writes=["yb"])
                pg.ld(dr["YB"][t0:t0 + 128, :], yb[:].rearrange("p h k -> p (h k)"), reads=["yb"], writes=[("YB", i)])
        pg.barrier()
        pg.emit()


_NSA_CONSTS = {}


def nsa_consts(hh=1):
    if hh in _NSA_CONSTS:
        return _NSA_CONSTS[hh]
    import ml_dtypes
    bf = ml_dtypes.bfloat16
    c = {}
    n = np.arange(256)
    t = np.arange(T)
    nlo = 128 if hh == 0 else 0
    cm = np.where((16 * n[:, None] + 31 <= t[None, :]) & (n[:, None] < 255) & (n[:, None] >= nlo), 0.0, NEG).astype(np.float32)
    c["c_cmpb"] = np.ascontiguousarray(cm.reshape(2, 128, T).transpose(1, 0, 2)).astype(bf)
    es = np.zeros((64, 32, 128), np.float32)
    for kt in range(32):
        for key in range(128):
            es[2 * kt + key // 64, kt, key] = 1.0
    c["c_esel"] = np.concatenate([es, es], 0).astype(bf)
    key = np.arange(128)
    q = np.arange(512)
    cb = np.zeros((128, 4, 512), np.float32)
    for d in range(4):
        cb[:, d, :] = np.where((d * 128 + key[:, None]) <= q[None, :], 1.0, 0.0)
    c["c_causb"] = cb.astype(bf)
    wb = np.zeros((128, 8, 512), np.float32)
    for r in range(8):
        ka = (r - 4) * 128 + key[:, None]
        wb[:, r, :] = np.where((ka <= q[None, :]) & (ka > q[None, :] - 512), 1.0, 0.0)
    c["c_winb"] = wb.astype(bf)
    wb4 = wb.copy()
    if hh == 0:
        wb4[:, 0:4, :] = 0.0
    c["c_winb4"] = wb4.astype(bf)
    cs = np.arange(256) * 16
    ss = np.arange(64) * 64
    ov = np.clip(np.minimum(cs[:, None] + 32, ss[None, :] + 64) - np.maximum(cs[:, None], ss[None, :]), 0, None) / 32.0
    ov[255, :] = 0.0
    c["ovl"] = np.ascontiguousarray(ov.reshape(2, 128, 64).transpose(1, 0, 2)).astype(np.float32)
    cur = t // 64
    j = np.arange(64)
    jlo = 32 if hh == 0 else 0
    valid = (j[None, :] <= cur[:, None]) & (j[None, :] >= jlo)
    forced = (j[None, :] == jlo) | (j[None, :] == cur[:, None]) | (j[None, :] == cur[:, None] - 1)
    vm = valid.astype(np.float32)
    fb = np.where(valid, 1000.0 * forced, -1.0).astype(np.float32)
    c["c_vmfb"] = np.ascontiguousarray(np.stack([vm, fb], 1).reshape(NT, 128, 2, 64))
    _NSA_CONSTS[hh] = c
    return c


def stage4(C):
    nc, pg, dr = C.nc, C.pg, C.dr
    with ExitStack() as st:
        sb, ps = _mk(C, st)
        wa = sb("m_wa", [128, 4, D], BF16)
        wb = sb("m_wb", [128, 4, D], BF16)
        wo = sb("m_wo", [128, 8, D], BF16)
        stg = sb("m_stg", [128, D], F32)
        idf = sb("m_idf", [128, 128], F32)
        idb = sb("m_idb", [128, 128], BF16)
        yab = [sb(f"m_yab{i}", [128, 1024], F32) for i in range(2)]
        yabb = sb("m_yabb", [128, 1024], BF16)
        yT = sb("m_yT", [128, 8, 128], BF16)
        gts = [sb(f"m_g{i}", [128, 2048], F32) for i in range(2)]
        xt = [sb(f"m_x{i}", [128, D], F32) for i in range(2)]
        mix = sb("m_mix", [128, D], F32)
        mix2 = sb("m_mix2", [128, D], F32)
        mixb = sb("m_mixb", [128, D], BF16)
        mT = sb("m_mT", [128, 8, 128], BF16)
        x1 = [sb(f"m_x1{i}", [128, D], F32) for i in range(2)]
        psT = ps("m_psT", [128, 1024], BF16)
        psm = [ps(f"m_psm{i}", [128, 512], F32) for i in range(4)]
        psT2 = ps("m_psT2", [128, 1024], BF16)
        pso = [ps(f"m_pso{i}", [128, 512], F32) for i in range(2)]

        pg.ld(idf[:], dr["ident"][:, :], writes=["idf"])
        pg.cp("dve", idb[:], idf[:], reads=["idf"], writes=["idb"])
        n = 0
        for (wt, nm, kcs) in ((wa, "w_branch_a", 4), (wb, "w_branch_b", 4), (wo, "w_out", 8)):
            for kc in range(kcs):
                pg.ld(stg[:], dr[nm][kc * 128:(kc + 1) * 128, :], writes=["stg"])
                pg.cp(("act", "dve", "pool")[n % 3], wt[:, kc, :], stg[:], reads=["stg"], writes=[nm])
                n += 1
        rowidx = sb("m_rowidx", [128, NT], I32)
        pg.ld(rowidx[:], dr["rowidx"][:, :], writes=["rowidx"])
        allk = lambda nm: [(nm, k) for k in range(NT)]
        for i in range(C.peer_tiles):
            s = i % 2
            t0 = i * 128
            ix = rowidx[:, i:i + 1]
            pg.ld(yab[s][:, 0:512], dr["YAL"][t0:t0 + 128, :], reads=[("YAL", i)], writes=[("yab", s)])
            igather(pg, yab[s][:, 512:1024], dr["YB"][:, :], ix, allk("YB") + ["rowidx"], [("yab2", s)], row0=(NT - C.peer_tiles) * 128 + t0)
            igather(pg, gts[s][:], dr["PG"][:, :], ix, allk("PG") + ["rowidx"], [("gts", s)], row0=(NT - C.peer_tiles) * 128 + t0)
            igather(pg, xt[s][:], dr["x"][:, :], ix, ["rowidx"], [("xt", s)], row0=(NT - C.peer_tiles) * 128 + t0)
            pg.cp("pool", yabb[:], yab[s][:], reads=[("yab", s), ("yab2", s)], writes=["yabb"])
            for j in range(8):
                pg.tr(psT[:, j * 128:(j + 1) * 128], yabb[:, j * 128:(j + 1) * 128], idb[:], reads=["yabb", "idb"], writes=["psT"])
            pg.cp("act", yT[:].rearrange("p a b -> p (a b)"), psT[:], reads=["psT"], writes=["yT"])
            for br in range(2):
                wt = wa if br == 0 else wb
                for nchunk in range(2):
                    pb = psm[br * 2 + nchunk]
                    for kc in range(4):
                        pg.mm(pb[:], yT[:, br * 4 + kc, :], wt[:, kc, nchunk * 512:(nchunk + 1) * 512], start=(kc == 0), stop=(kc == 3),
                              reads=["yT", "w_branch_a", "w_branch_b"], writes=[("psm", br * 2 + nchunk)])
            pg.act(gts[s][:], gts[s][:], AF.Sigmoid, reads=[("gts", s)], writes=[("gts", s)])
            for nchunk in range(2):
                c = slice(nchunk * 512, (nchunk + 1) * 512)
                pg.tt("dve", mix[:, c], psm[nchunk][:], gts[s][:, nchunk * 512:(nchunk + 1) * 512], ALU.mult,
                      reads=[("psm", nchunk), ("gts", s)], writes=[("mix", nchunk)])
                pg.tt("dve", mix2[:, c], psm[2 + nchunk][:], gts[s][:, 1024 + nchunk * 512:1024 + (nchunk + 1) * 512], ALU.mult,
                      reads=[("psm", 2 + nchunk), ("gts", s)], writes=[("mix2", nchunk)])
                pg.tt("pool", mixb[:, c], mix[:, c], mix2[:, c], ALU.add, reads=[("mix", nchunk), ("mix2", nchunk)], writes=[("mixb", nchunk)])
            for j in range(8):
                pg.tr(psT2[:, j * 128:(j + 1) * 128], mixb[:, j * 128:(j + 1) * 128], idb[:], reads=[("mixb", 0), ("mixb", 1), "idb"], writes=["psT2"])
            pg.cp("act", mT[:].rearrange("p a b -> p (a b)"), psT2[:], reads=["psT2"], writes=["mT"])
            for nchunk in range(2):
                for kc in range(8):
                    pg.mm(pso[nchunk][:], mT[:, kc, :], wo[:, kc, nchunk * 512:(nchunk + 1) * 512], start=(kc == 0), stop=(kc == 7),
                          reads=["mT", "w_out"], writes=[("pso", nchunk)])
                pg.tt("dve", x1[s][:, nchunk * 512:(nchunk + 1) * 512], pso[nchunk][:], xt[s][:, nchunk * 512:(nchunk + 1) * 512], ALU.add,
                      reads=[("pso", nchunk), ("xt", s)], writes=[("x1", s, nchunk)])
            pg.ld(dr["X1L"][t0:t0 + 128, :], x1[s][:], reads=[("x1", s, 0), ("x1", s, 1)], writes=[("X1L", i)])
        pg.barrier()
        pg.emit()


def table_conv_gen(C, sb):
    pg, dr = C.pg, C.dr
    NBUF = 4
    src = [sb(f"z_src{i}", [128, D], F32) for i in range(NBUF)]
    dst = [sb(f"z_dst{i}", [128, D], BF16) for i in range(NBUF)]
    n = 0
    for (tab, co) in (("peer_u", 0), ("peer_v", D)):
        for a in range(16384 // 128):
            b_ = n % NBUF
            pg.ld(src[b_][:], dr[tab][a * 128:(a + 1) * 128, :], writes=[("zsrc", b_)], q="sp")
            pg.cp("act", dst[b_][:], src[b_][:], reads=[("zsrc", b_)], writes=[("zdst", b_)])
            pg.ld(dr["UV"][a * 128:(a + 1) * 128, co:co + D], dst[b_][:], reads=[("zdst", b_)], writes=[("UV", co, a)], q="act")
            n += 1
            yield


def stage5(C):
    nc, pg, dr = C.nc, C.pg, C.dr
    NB = 12
    with ExitStack() as st:
        sb, ps = _mk(C, st)
        wq = sb("p_wq", [128, 8, 2048], F32)
        kT = sb("p_kT", [128, 2, 128], F32)
        kraw = sb("p_kraw", [128, 2, 128], F32)
        g2 = sb("p_g2", [128, D], F32)
        idf = sb("p_idf", [128, 128], F32)
        io16 = sb("p_io16", [128, 16], F32)
        eps = sb("p_eps", [128, 1], F32)
        x1 = [sb(f"p_x1{i}", [128, D], F32) for i in range(3)]
        h2 = [sb(f"p_h2{i}", [128, D], F32) for i in range(2)]
        junk = sb("p_junk", [128, D], BF16)

        ss = sb("p_ss", [128, 4], F32)
        h2T = sb("p_h2T", [128, 8, 128], F32)
        qT = sb("p_qT", [128, 16, 128], F32)
        sc = sb("p_sc", [128, 16, 128], F32)
        work = sb("p_work", [128, 256], F32)
        tv = sb("p_tv", [128, 16, 16], F32)
        tiu = sb("p_tiu", [128, 16, 16], U32)
        ti = sb("p_ti", [128, 16, 16], F32)
        cs = sb("p_cs", [128, 8, 256], F32)
        bs = sb("p_bs", [128, 8, 16], F32)
        posu = sb("p_posu", [128, 8, 16], U32)
        pa_u = sb("p_pau", [128, 8, 16], U32)
        pb_u = sb("p_pbu", [128, 8, 16], U32)
        pa = sb("p_pa", [128, 8, 16], F32)
        pb = sb("p_pb", [128, 8, 16], F32)
        oh = sb("p_oh", [128, 8, 16, 16], F32)
        ia = sb("p_ia", [128, 8, 16], F32)
        ib = sb("p_ib", [128, 8, 16], F32)
        eidf = sb("p_eidf", [128, 128], F32)
        eidi = [sb(f"p_eidi{i}", [128, 128], I32) for i in range(3)]
        gate = [sb(f"p_gate{i}", [128, 128], F32) for i in range(2)]
        zz = sb("p_zz", [128, 16], F32)
        actv = [sb(f"p_act{i}", [128, 128], F32) for i in range(2)]
        ga = [sb(f"p_ga{i}", [128, 128], F32) for i in range(2)]
        uv = [sb(f"p_uv{i}", [128, 2 * D], BF16) for i in range(NB)]
        h2b = [sb(f"p_h2b{i}", [128, D], BF16) for i in range(2)]
        idb = sb("p_idb", [128, 128], BF16)
        junk2 = sb("p_junk2", [128, D], F32)
        dg = [sb(f"p_dg{i}", [128, 128], BF16) for i in range(4)]
        yo = [sb(f"p_yo{i}", [128, D], F32) for i in range(1)]
        psT = ps("p_psT", [128, 8, 128], F32)
        psQ = [ps(f"p_psQ{i}", [128, 512], F32) for i in range(2)]
        psY = [ps(f"p_psY{i}", [128, 512], F32) for i in range(2)]

        pg.ld(idf[:], dr["ident"][:, :], writes=["idf"])
        pg.ld(g2[:], dr["norm2_g_b"][:, :], writes=["g2"])
        pg.cp("dve", idb[:], idf[:], reads=["idf"], writes=["idb"])
        pg.ld(io16[:], dr["iota16"][:, :], writes=["io16"])
        rowidx = sb("p_rowidx", [128, NT], I32)
        pg.ld(rowidx[:], dr["rowidx"][:, :], writes=["rowidx"])
        pg.memset("dve", eps[:], 1e-6, writes=["eps"])
        for kc in range(8):
            pg.ld(wq[:, kc, :], dr["peer_wq"][kc * 128:(kc + 1) * 128, :], writes=["wq"])
        pg.ld(kraw[:, 0, :], dr["peer_k1"][:, :], writes=["kraw"])
        pg.ld(kraw[:, 1, :], dr["peer_k2"][:, :], writes=["kraw"])
        for hf in range(2):
            pg.tr(psQ[0][:, hf * 128:(hf + 1) * 128], kraw[:, hf, :], idf[:], reads=["kraw", "idf"], writes=[("psQ", 0)])
        pg.cp("dve", kT[:].rearrange("p a b -> p (a b)"), psQ[0][:, 0:256], reads=[("psQ", 0)], writes=["kT"])
        ntiles = getattr(C, "peer_tiles", NT)

        def front(i):
            s = i % 2
            t0 = i * 128
            pg.ld(x1[i % 3][:, :], dr["X1L"][t0:t0 + 128, :], reads=[("X1L", i)], writes=[("x1", i % 3)])
            yield
            pg.tt("pool", junk2[:], x1[i % 3][:], x1[i % 3][:], ALU.mult, reads=[("x1", i % 3), "junk2"], writes=["junk2"])
            yield
            pg.red("dve", ss[:, 0:1], junk2[:], ALU.add, reads=["junk2"], writes=["ss0"])
            yield
            pg.act(ss[:, 1:2], ss[:, 0:1], AF.Sqrt, reads=["ss0", "eps"], writes=["ss1"], scale=1.0 / D, bias=eps[:, 0:1])
            yield
            pg.op("dve", lambda e: e.reciprocal(ss[:, 2:3], ss[:, 1:2]), reads=["ss1"], writes=["ss2"])
            yield
            pg.stt("dve", h2[s][:], x1[i % 3][:], ss[:, 2:3], g2[:], ALU.mult, ALU.mult, reads=[("x1", i % 3), "ss2", "g2"], writes=[("h2", s)])
            yield
            pg.cp("act", h2b[s][:], h2[s][:], reads=[("h2", s)], writes=[("h2b", s)])
            yield
            for j in range(8):
                pg.tr(psT[:, j, :], h2[s][:, j * 128:(j + 1) * 128], idf[:], reads=[("h2", s), "idf"], writes=["psT"])
                yield
            pg.cp("act", h2T[:], psT[:], reads=["psT"], writes=["h2T"])
            yield
            for cg in range(4):
                bk = psQ[cg % 2]
                for cc in range(4):
                    c = cg * 4 + cc
                    for kc in range(8):
                        pg.mm(bk[:, cc * 128:(cc + 1) * 128], wq[:, kc, c * 128:(c + 1) * 128], h2T[:, kc, :], start=(kc == 0), stop=(kc == 7),
                              reads=["wq", "h2T"], writes=[("psQ", cg % 2)])
                        yield
                pg.cp("act" if cg % 2 == 0 else "dve", qT[:, cg * 4:(cg + 1) * 4, :].rearrange("p a b -> p (a b)"), bk[:],
                      reads=[("psQ", cg % 2)], writes=[("qT", cg)])
                yield
            for cg in range(4):
                bk = psQ[cg % 2]
                for cc in range(4):
                    c = cg * 4 + cc
                    pg.mm(bk[:, cc * 128:(cc + 1) * 128], qT[:, c, :], kT[:, c % 2, :], reads=[("qT", cg), "kT"], writes=[("psQ", cg % 2)])
                    yield
                pg.cp("act" if cg % 2 == 0 else "dve", sc[:, cg * 4:(cg + 1) * 4, :].rearrange("p a b -> p (a b)"), bk[:],
                      reads=[("psQ", cg % 2)], writes=[("sc", cg)])
                yield
            for c in range(16):
                k_ = ("sc", c // 4)
                pg.op("dve", lambda e, c=c: e.max(tv[:, c, 0:8], sc[:, c, :]), reads=[k_], writes=[("tv", c)])
                yield
                pg.op("dve", lambda e, c=c: e.max_index(tiu[:, c, 0:8], tv[:, c, 0:8], sc[:, c, :]), reads=[k_, ("tv", c)], writes=[("tiu", c)])
                yield
                pg.op("dve", lambda e, c=c: e.match_replace(work[:, 0:128], tv[:, c, 0:8], sc[:, c, :], -1e30), reads=[k_, ("tv", c), "work"], writes=["work"])
                yield
                pg.op("dve", lambda e, c=c: e.max(tv[:, c, 8:16], work[:, 0:128]), reads=["work"], writes=[("tv2", c)])
                yield
                pg.op("dve", lambda e, c=c: e.max_index(tiu[:, c, 8:16], tv[:, c, 8:16], sc[:, c, :]), reads=[k_, ("tv2", c)], writes=[("tiu2", c)])
                yield
            allt = [("tv", c) for c in range(16)] + [("tv2", c) for c in range(16)]
            alli = [("tiu", c) for c in range(16)] + [("tiu2", c) for c in range(16)]
            pg.cp("dve", ti[:], tiu[:], reads=alli, writes=["ti"])
            yield
            tv4 = tv[:].rearrange("p (h f) a -> p h f a", f=2)
            ti4 = ti[:].rearrange("p (h f) a -> p h f a", f=2)
            cs4 = cs[:].rearrange("p h (a b) -> p h a b", a=16)
            A_ = lambda t4: t4[:, :, 0, :].unsqueeze(3).to_broadcast([128, 8, 16, 16])
            B_ = lambda t4: t4[:, :, 1, :].unsqueeze(2).to_broadcast([128, 8, 16, 16])
            pg.tt("dve", cs4, A_(tv4), B_(tv4), ALU.add, reads=allt, writes=["cs"])
            yield
            for h in range(8):
                pg.op("dve", lambda e, h=h: e.max(bs[:, h, 0:8], cs[:, h, :]), reads=["cs"], writes=[("bs", h)])
                yield
                pg.op("dve", lambda e, h=h: e.max_index(posu[:, h, 0:8], bs[:, h, 0:8], cs[:, h, :]), reads=["cs", ("bs", h)], writes=[("posu", h)])
                yield
                pg.op("dve", lambda e, h=h: e.match_replace(work[:, :], bs[:, h, 0:8], cs[:, h, :], -1e30), reads=["cs", ("bs", h), "work"], writes=["work"])
                yield
                pg.op("dve", lambda e, h=h: e.max(bs[:, h, 8:16], work[:, :]), reads=["work"], writes=[("bs2", h)])
                yield
                pg.op("dve", lambda e, h=h: e.max_index(posu[:, h, 8:16], bs[:, h, 8:16], cs[:, h, :]), reads=["cs", ("bs2", h)], writes=[("posu2", h)])
                yield
            allb = [("bs", h) for h in range(8)] + [("bs2", h) for h in range(8)]
            allp = [("posu", h) for h in range(8)] + [("posu2", h) for h in range(8)]
            G = gate[s][:].rearrange("p (h j) -> p h j", h=8)
            pg.tt("dve", G, bs[:], bs[:, :, 0:1].to_broadcast([128, 8, 16]), ALU.subtract, reads=allb, writes=[("gate", s)])
            yield
            pg.act(G, G, AF.Exp, reads=[("gate", s)], writes=[("gate", s)])
            yield
            pg.red("dve", zz[:, 0:8], G, ALU.add, reads=[("gate", s)], writes=["zz0"])
            yield
            pg.op("dve", lambda e: e.reciprocal(zz[:, 8:16], zz[:, 0:8]), reads=["zz0"], writes=["zz1"])
            yield
            pg.tt("dve", G, G, zz[:, 8:16].unsqueeze(2).to_broadcast([128, 8, 16]), ALU.mult, reads=[("gate", s), "zz1"], writes=[("gate", s)])
            yield
            pg.ts("dve", pa_u[:], posu[:], 4, None, ALU.logical_shift_right, reads=allp, writes=["pau"])
            yield
            pg.ts("dve", pb_u[:], posu[:], 15, None, ALU.bitwise_and, reads=allp, writes=["pbu"])
            yield
            pg.cp("dve", pa[:], pa_u[:], reads=["pau"], writes=["pa"])
            yield
            pg.cp("dve", pb[:], pb_u[:], reads=["pbu"], writes=["pb"])
            yield
            iob = io16[:, :].unsqueeze(1).unsqueeze(1).to_broadcast([128, 8, 16, 16])
            for (pp, key, half, dst, dk_) in ((pa, "pa", 0, ia, "ia"), (pb, "pb", 1, ib, "ib")):
                pg.tt("dve", oh[:], pp[:].unsqueeze(3).to_broadcast([128, 8, 16, 16]), iob, ALU.is_equal, reads=[key, "io16", "oh"], writes=["oh"])
                yield
                tsel = ti4[:, :, half, :].unsqueeze(2).to_broadcast([128, 8, 16, 16])
                pg.tt("dve", oh[:], oh[:], tsel, ALU.mult, reads=["oh", "ti"], writes=["oh"])
                yield
                pg.red("dve", dst[:], oh[:], ALU.add, reads=["oh"], writes=[dk_])
                yield
            pg.stt("dve", eidf[:].rearrange("p (h j) -> p h j", h=8), ia[:], 128.0, ib[:], ALU.mult, ALU.add, reads=["ia", "ib"], writes=["eidf"])
            yield
            pg.cp("dve", eidi[i % 3][:], eidf[:], reads=["eidf"], writes=[("eidi", i % 3)])
            yield

        GS = 2
        SK = 1

        def gstep(i, e_):
            s = i % 2
            b_ = e_ % NB
            pg.dma("pool", lambda e, e_=e_, b_=b_, i=i: e.indirect_dma_start(
                out=uv[b_][:, :], out_offset=None, in_=dr["UV"][:, :],
                in_offset=bass.IndirectOffsetOnAxis(ap=eidi[i % 3][:, e_:e_ + 1], axis=0)),
                reads=[("eidi", i % 3)], writes=[("uv", b_)])
            pg.op("dve", lambda e, e_=e_, b_=b_, s=s: e.scalar_tensor_tensor(junk[:], uv[b_][:, 0:D], 1.0, h2b[s][:], ALU.mult, ALU.mult,
                                                                              accum_out=actv[s][:, e_:e_ + 1]),
                  reads=[("uv", b_), ("h2b", s)], writes=[("act", s, e_)])

        def gelu_grp(i, k):
            s = i % 2
            sl = slice(k * GS, (k + 1) * GS)
            pg.act(ga[s][:, sl], actv[s][:, sl], AF.Gelu, reads=[("act", s, e_) for e_ in range(k * GS, (k + 1) * GS)], writes=[("ga", s, k)])

        def fin_grp(i, k):
            s = i % 2
            sl = slice(k * GS, (k + 1) * GS)
            pg.tt("dve", ga[s][:, sl], ga[s][:, sl], gate[s][:, sl], ALU.mult, reads=[("ga", s, k), ("gate", s)], writes=[("ga", s, k)])
            for e_ in range(k * GS, (k + 1) * GS):
                b_ = e_ % NB
                d_ = e_ % 4
                pg.act(dg[d_][:], idb[:], AF.Copy, reads=[("ga", s, k), "idb"], writes=[("dg", d_)], scale=ga[s][:, e_:e_ + 1])
                for n_ in range(2):
                    pg.mm(psY[n_][:], dg[d_][:], uv[b_][:, D + n_ * 512:D + (n_ + 1) * 512], start=(e_ == 0), stop=(e_ == 127),
                          reads=[("dg", d_), ("uv", b_)], writes=[("psY", n_)])

        def tail(i):
            s = i % 2
            t0 = i * 128
            for n_ in range(2):
                pg.tt("dve", yo[0][:, n_ * 512:(n_ + 1) * 512], psY[n_][:], x1[i % 3][:, n_ * 512:(n_ + 1) * 512], ALU.add,
                      reads=[("psY", n_), ("x1", i % 3)], writes=[("yo", 0, n_)])
            pg.ld(dr["out"][t0:t0 + 128, :], yo[0][:], reads=[("yo", 0, 0), ("yo", 0, 1)], writes=[("out", i)])

        def drain(g, n=None):
            k = 0
            while g is not None and (n is None or k < n):
                try:
                    next(g)
                except StopIteration:
                    return None
                k += 1
            return g

        drain(front(0))
        for i in range(ntiles):
            gen2 = front(i + 1) if i + 1 < ntiles else None
            for k in range(128 // GS):
                for e_ in range(k * GS, (k + 1) * GS):
                    gstep(i, e_)
                    gen2 = drain(gen2, 3)
                gelu_grp(i, k)
                if k >= SK:
                    fin_grp(i, k - SK)
            for k in range(128 // GS - SK, 128 // GS):
                fin_grp(i, k)
            drain(gen2)
            tail(i)
        pg.barrier()
        pg.emit()


_NC_CACHE = {}


def kernel(**inputs):
    inputs = {k: np.asarray(v) for k, v in inputs.items()}
    ntl = NT // 2
    if "nc" not in _NC_CACHE:
        _NC_CACHE["nc"] = build([stage1, stage2a, stage2x, stage2c, stage3, stage4, stage5], peer_tiles=ntl)
    nc = _NC_CACHE["nc"]
    base = {}
    in_maps = []
    for c in range(8):
        b, hh = c % 4, c // 4
        if hh not in base:
            base[hh] = host_inputs(inputs, b, hh, ntl)
            m = base[hh]
        else:
            m = dict(base[hh])
            m["x"] = core_x(inputs, b, hh)
        in_maps.append(m)
    res = run_bass_kernel_spmd(nc, in_maps, core_ids=list(range(8)))
    out = np.zeros((4, T, D), np.float32)
    for c in range(8):
        b, hh = c % 4, c // 4
        out[b, hh * ntl * 128:(hh + 1) * ntl * 128, :] = res.results[c]["out"]
    return out


def stage2x(C):
    nc, pg, dr = C.nc, C.pg, C.dr
    with ExitStack() as st:
        sb, ps = _mk(C, st)
        idf = sb("x_idf", [128, 128], F32)
        tri = sb("x_tri", [128, 128], F32)
        msk = sb("x_msk", [128, 3, 128], F32)
        ones = sb("x_ones", [128, 1], F32)
        inp = [[sb(f"x_in{s}_{j}", [128, 512], F32) for j in range(6)] for s in range(2)]
        Pt = sb("x_P", [128, 512], F32)
        iP = sb("x_iP", [128, 512], F32)
        Pp = sb("x_Pp", [128, 512], F32)
        tm = [[sb(f"x_tm{s}_{j}", [128, 512], F32) for j in range(4)] for s in range(2)]
        fm = [[sb(f"x_fm{s}_{j}", [64, 8, 128], F32) for j in range(4)] for s in range(2)]
        M = [[sb(f"x_M{s}_{j}", [128, 8, 128], (BF16 if j in (0, 4) else F32)) for j in range(5)] for s in range(2)]
        Xb = sb("x_Xb", [128, 8, 128], BF16)
        idb = sb("x_idb", [128, 128], BF16)
        X = [sb(f"x_X{s}", [128, 8, 128], F32) for s in range(2)]
        PC = [sb(f"x_PC{s}", [64, 8], F32) for s in range(2)]
        N2 = [sb(f"x_N2_{j}", [128, 8, 128], BF16) for j in range(2)]
        N2T = [sb(f"x_N2T_{j}", [128, 8, 128], BF16) for j in range(2)]
        Z = [sb(f"x_Z{j}", [64, 512], F32) for j in range(2)]
        rhs_sb = sb("x_rhs", [128, 512], F32)
        U_sb = sb("x_U", [128, 512], F32)
        Y_sb = [sb(f"x_Y{j}", [128, 512], F32) for j in range(2)]
        bank = [ps(f"x_bank{j}", [128, 512], F32) for j in range(8)]

        pg.ld(idf[:], dr["ident"][:, :], writes=["idf"])
        pg.ld(tri[:], dr["c_tri"][:, :], writes=["tri"])
        pg.ld(msk[:], dr["c_msk"][:, :, :], writes=["msk"])
        pg.memset("dve", ones[:], 1.0, writes=["ones"])
        pg.cp("dve", idb[:], idf[:], reads=["idf"], writes=["idb"])
        pg.memset("dve", Z[0][:], 0.0, writes=[("Z", 0)])
        names = ("RR", "RKK", "RLW", "RB", "RKp", "RV")
        bk = [0]

        def nb():
            v = bk[0]
            bk[0] = (v + 1) % 8
            return v

        def pre(c):
            s = c % 2
            t0 = c * 128
            I = inp[s]
            for j, nm in enumerate(names):
                pg.ld(I[j][:], dr[nm][t0:t0 + 128, :], reads=[(nm, c)], writes=[("in", s, j)], q=("pool" if j % 2 == 0 else "sp"))
            r_, kkn, lw, b_, kp, v_ = [t_[:] for t_ in I]
            bL = nb()
            pg.mm(bank[bL][:], tri[:], lw, reads=["tri", ("in", s, 2)], writes=[("bank", bL)])
            bC = nb()
            for h in range(8):
                pg.mm(bank[bC][0:64, h:h + 1], I[2][:, h * 64:(h + 1) * 64], ones[:, 0:1], reads=[("in", s, 2), "ones"], writes=[("bank", bC)])
            pg.act(PC[s][:], bank[bC][0:64, 0:8], AF.Exp, reads=[("bank", bC)], writes=[("PC", s)])
            pg.act(Pt[:], bank[bL][:], AF.Exp, reads=[("bank", bL)], writes=["P"])
            pg.act(iP[:], bank[bL][:], AF.Exp, reads=[("bank", bL)], writes=["iP"], scale=-1.0)
            pg.tt("dve", Pp[:], bank[bL][:], lw, ALU.subtract, reads=[("bank", bL), ("in", s, 2)], writes=["Pp"])
            pg.act(Pp[:], Pp[:], AF.Exp, reads=["Pp"], writes=["Pp"])
            TM = tm[s]
            pg.tt("pool", TM[0][:], r_, Pt[:], ALU.mult, reads=[("in", s, 0), "P"], writes=[("tm", s, 0)])
            pg.stt("dve", TM[1][:], kkn, -1.0, Pp[:], ALU.mult, ALU.mult, reads=[("in", s, 1), "Pp"], writes=[("tm", s, 1)])
            pg.tt("pool", TM[2][:], b_, iP[:], ALU.mult, reads=[("in", s, 3), "iP"], writes=[("tm", s, 2)])
            pg.tt("dve", TM[3][:], kp, iP[:], ALU.mult, reads=[("in", s, 4), "iP"], writes=[("tm", s, 3)])
            for j in range(4):
                if j == 0 and c < NT // 2:
                    continue
                for hg in range(2):
                    bT = nb()
                    for hh in range(4):
                        h = hg * 4 + hh
                        pg.tr(bank[bT][0:64, hh * 128:(hh + 1) * 128], TM[j][:, h * 64:(h + 1) * 64], idf[:], reads=[("tm", s, j), "idf"], writes=[("bank", bT)])
                    pg.cp("act" if (j + hg) % 2 == 0 else "dve", fm[s][j][:, hg * 4:(hg + 1) * 4, :].rearrange("p a b -> p (a b)"), bank[bT][0:64, :],
                          reads=[("bank", bT)], writes=[("fm", s, j, hg)])
            FR, FKK, FB, FK = fm[s]
            combos = ((0, FB, 2, FKK, 1, 0), (1, FK, 3, FKK, 1, 0), (2, FB, 2, FR, 0, 1), (3, FK, 3, FR, 0, 1), (4, FKK, 1, FB, 2, 2))
            for hg in range(2):
                for (mi, L_, lj, R_, rj, mk) in combos:
                    if mi in (2, 3) and c < NT // 2:
                        continue
                    bM = nb()
                    for hh in range(4):
                        h = hg * 4 + hh
                        pg.mm(bank[bM][:, hh * 128:(hh + 1) * 128], L_[:, h, :], R_[:, h, :], reads=[("fm", s, lj, hg), ("fm", s, rj, hg)], writes=[("bank", bM)])
                    pg.tt("dve", M[s][mi][:, hg * 4:(hg + 1) * 4, :], bank[bM][:].rearrange("p (a b) -> p a b", a=4),
                          msk[:, mk, :].unsqueeze(1).to_broadcast([128, 4, 128]), ALU.mult, reads=[("bank", bM), "msk"], writes=[("M", s, mi, hg)])
                pg.tt("pool", Xb[:, hg * 4:(hg + 1) * 4, :], idb[:, :].unsqueeze(1).to_broadcast([128, 4, 128]), M[s][0][:, hg * 4:(hg + 1) * 4, :], ALU.subtract,
                      reads=[("M", s, 0, hg), "idb"], writes=[("Xb", hg)])
            curN = [M[s][0], M[s][0]]
            curNT = [M[s][4], M[s][4]]
            kN = [("M", s, 0, 0), ("M", s, 0, 1)]
            kNT = [("M", s, 4, 0), ("M", s, 4, 1)]
            for j in range(6):
                dst = j % 2
                for hg in range(2):
                    b1, b2 = nb(), nb()
                    for hh in range(4):
                        h = hg * 4 + hh
                        pg.mm(bank[b1][:, hh * 128:(hh + 1) * 128], curNT[hg][:, h, :], curN[hg][:, h, :], reads=[kN[hg], kNT[hg]], writes=[("bank", b1)])
                    for hh in range(4):
                        h = hg * 4 + hh
                        pg.mm(bank[b2][:, hh * 128:(hh + 1) * 128], curN[hg][:, h, :], curNT[hg][:, h, :], reads=[kN[hg], kNT[hg]], writes=[("bank", b2)])
                    pg.cp("act", N2[dst][:, hg * 4:(hg + 1) * 4, :].rearrange("p a b -> p (a b)"), bank[b1][:], reads=[("bank", b1)], writes=[("N2", dst, hg)])
                    pg.cp("dve", N2T[dst][:, hg * 4:(hg + 1) * 4, :].rearrange("p a b -> p (a b)"), bank[b2][:], reads=[("bank", b2)], writes=[("N2T", dst, hg)])
                for hg in range(2):
                    curN[hg], curNT[hg] = N2[dst], N2T[dst]
                    kN[hg], kNT[hg] = ("N2", dst, hg), ("N2T", dst, hg)
                for hg in range(2):
                    b3 = nb()
                    for hh in range(4):
                        h = hg * 4 + hh
                        pg.mm(bank[b3][:, hh * 128:(hh + 1) * 128], curNT[hg][:, h, :], Xb[:, h, :], reads=[kNT[hg], ("Xb", hg)], writes=[("bank", b3)])
                    xo = (X[s] if j == 5 else Xb)
                    pg.tt("dve", xo[:, hg * 4:(hg + 1) * 4, :].rearrange("p a b -> p (a b)"), Xb[:, hg * 4:(hg + 1) * 4, :].rearrange("p a b -> p (a b)"), bank[b3][:], ALU.add,
                          reads=[("bank", b3), ("Xb", hg)], writes=[("X", s, hg)] if j == 5 else [("Xb", hg)])

        def seq(c):
            s = c % 2
            t0 = c * 128
            zc, zn = Z[c % 2], Z[(c + 1) % 2]
            kz, kzn = ("Z", c % 2), ("Z", (c + 1) % 2)
            FR, FKK, FB, FK = fm[s]
            V = inp[s][5]
            hsl = lambda h: slice(h * 64, (h + 1) * 64)
            Mk = lambda mi: [("M", s, mi, 0), ("M", s, mi, 1)]
            fk = lambda j: [("fm", s, j, 0), ("fm", s, j, 1)]
            Xk = [("X", s, 0), ("X", s, 1)]
            bG = nb()
            for h in range(8):
                pg.mm(bank[bG][:, hsl(h)], M[s][1][:, h, :], V[:, hsl(h)], start=True, stop=False, reads=Mk(1) + [("in", s, 5)], writes=[("bank", bG)])
                pg.mm(bank[bG][:, hsl(h)], FKK[:, h, :], zc[:, hsl(h)], start=False, stop=True, reads=fk(1) + [kz], writes=[("bank", bG)])
            pg.ts("dve", rhs_sb[:], bank[bG][:], -1.0, None, ALU.mult, reads=[("bank", bG)], writes=["rhs"])
            bU = nb()
            for h in range(8):
                pg.mm(bank[bU][:, hsl(h)], X[s][:, h, :], rhs_sb[:, hsl(h)], reads=Xk + ["rhs"], writes=[("bank", bU)])
            pg.cp("act", U_sb[:], bank[bU][:], reads=[("bank", bU)], writes=["U"])
            bZ = nb()
            for h in range(8):
                pg.mm(bank[bZ][0:64, hsl(h)], tm[s][3][:, hsl(h)], V[:, hsl(h)], start=True, stop=False, reads=[("tm", s, 3), ("in", s, 5)], writes=[("bank", bZ)])
                pg.mm(bank[bZ][0:64, hsl(h)], idf[0:64, 0:64], zc[:, hsl(h)], start=False, stop=False, reads=["idf", kz], writes=[("bank", bZ)])
                pg.mm(bank[bZ][0:64, hsl(h)], tm[s][2][:, hsl(h)], U_sb[:, hsl(h)], start=False, stop=True, reads=[("tm", s, 2), "U"], writes=[("bank", bZ)])
            pg.tt("dve", zn[:].rearrange("p (h v) -> p h v", h=8), bank[bZ][0:64, :].rearrange("p (h v) -> p h v", h=8),
                  PC[s][:, :].unsqueeze(2).to_broadcast([64, 8, 64]), ALU.mult, reads=[("bank", bZ), ("PC", s)], writes=[kzn])
            if c < NT // 2:
                return
            bY = nb()
            for h in range(8):
                pg.mm(bank[bY][:, hsl(h)], M[s][3][:, h, :], V[:, hsl(h)], start=True, stop=False, reads=Mk(3) + [("in", s, 5)], writes=[("bank", bY)])
                pg.mm(bank[bY][:, hsl(h)], FR[:, h, :], zc[:, hsl(h)], start=False, stop=False, reads=fk(0) + [kz], writes=[("bank", bY)])
                pg.mm(bank[bY][:, hsl(h)], M[s][2][:, h, :], U_sb[:, hsl(h)], start=False, stop=True, reads=Mk(2) + ["U"], writes=[("bank", bY)])
            pg.cp("act", Y_sb[s][:], bank[bY][:], reads=[("bank", bY)], writes=[("Y", s)])
            pg.ld(dr["YS"][t0:t0 + 128, :], Y_sb[s][:], reads=[("Y", s)], writes=[("YS", c)])

        tcg = table_conv_gen(C, sb)
        pre(0)
        for c in range(NT):
            if c + 1 < NT:
                pre(c + 1)
            for _ in range(8):
                next(tcg, None)
            seq(c)
        for _ in tcg:
            pass
        pg.barrier()
        pg.emit()
```
